# Optimizing a Trainium2 kernel written in Bass

```python
import math
import jax
import jax.numpy as jnp
from jax import lax
import numpy as np

D_MODEL = 1024
BATCH = 8
SEQ = 2048
DEPTH = 2

GRID_W = 64
CTX_LEN = 256
N_BRANCH = 4
D_BRANCH = D_MODEL // N_BRANCH
S5_GROUP = 16
S5_GROUPS = D_BRANCH // S5_GROUP
S5_STATE = 64
HG_HEAD = 64
HG_HEADS = D_BRANCH // HG_HEAD
HG_CHUNK = 16
RET_HEAD = 64
RET_HEADS = D_BRANCH // RET_HEAD
RET_CHUNK = 64
ROPE_BASE = 10000.0
RW_HEAD = 64
RW_HEADS = D_BRANCH // RW_HEAD
RW_LORA = 16
RW_SHIFT_W = 3 * D_BRANCH + 4 * RW_LORA
LN_EPS = 1e-5
RMS_EPS = 1e-6
RW_GN_EPS = 64e-5
DEEPNORM_ALPHA = (2 * DEPTH) ** 0.25
DEEPNORM_BETA = (8 * DEPTH) ** -0.25
IN_WIDTHS = (D_BRANCH, D_BRANCH,
             D_BRANCH, 2 * D_BRANCH, D_BRANCH, D_BRANCH,
             D_BRANCH, D_BRANCH, D_BRANCH, D_BRANCH,
             RW_SHIFT_W, D_BRANCH,
             N_BRANCH * D_MODEL)
D_IN = sum(IN_WIDTHS)

kernel_name = 'hybrid_s5_hgrn2_retnet_rwkv7_diffusion_block'


def layer_norm(x):
    xf = x.astype(jnp.float32)
    mu = xf.mean(-1, keepdims=True)
    var = jnp.square(xf - mu).mean(-1, keepdims=True)
    return ((xf - mu) * lax.rsqrt(var + LN_EPS)).astype(x.dtype)


def rms_norm(x):
    xf = x.astype(jnp.float32)
    return xf * lax.rsqrt(jnp.square(xf).mean(-1, keepdims=True) + RMS_EPS)


def split_heads(a, head_dim):
    return a.reshape(*a.shape[:-1], a.shape[-1] // head_dim, head_dim)


def to_dirs(a):
    return jnp.stack([a, jnp.flip(a, axis=1)])


def flip_bwd(a):
    return jnp.stack([a[0], jnp.flip(a[1], axis=1)])


def from_dirs(a):
    return a[0] + jnp.flip(a[1], axis=1)


def _rope_half(x, pos):
    n = x.shape[-1] // 2
    freqs = ROPE_BASE ** (-jnp.arange(n, dtype=jnp.float32) / n)
    ang = pos.astype(jnp.float32)[:, None] * freqs
    cos = jnp.cos(ang)[:, None, :]
    sin = jnp.sin(ang)[:, None, :]
    x1, x2 = x[..., :n], x[..., n:]
    return jnp.concatenate([x1 * cos - x2 * sin, x1 * sin + x2 * cos], axis=-1)


def rotary_2d(x, rows, cols):
    half = x.shape[-1] // 2
    return jnp.concatenate([_rope_half(x[..., :half], rows), _rope_half(x[..., half:], cols)], axis=-1).astype(x.dtype)


def token_shift_centred(x, mu):
    x_prev = jnp.pad(x[:, :-1], ((0, 0), (1, 0), (0, 0)))
    x_next = jnp.pad(x[:, 1:], ((0, 0), (0, 1), (0, 0)))
    return x + mu[0] * (x_prev - x) + mu[1] * (x_next - x)


def _linear_combine(e1, e2):
    a1, b1 = e1
    a2, b2 = e2
    return a1 * a2, a2 * b1 + b2


def s5_mixer(u, lam_re, lam_im, log_dt, b_re, b_im, c_re, c_im, d_skip, h0):
    bsz, t_len, _ = u.shape
    f32 = jnp.float32
    lam = lax.complex(lam_re.astype(f32), lam_im.astype(f32))
    a_bar = jnp.exp(lam * jnp.exp(log_dt.astype(f32))[..., None])
    b_bar = ((a_bar - 1.0) / lam)[..., None] * lax.complex(b_re.astype(f32), b_im.astype(f32))
    ug = to_dirs(u.astype(f32).reshape(bsz, t_len, S5_GROUPS, S5_GROUP)).astype(jnp.complex64)
    bu = jnp.einsum('dgnp,dbtgp->dbtgn', b_bar, ug)
    bu = bu.at[:, :, 0].add(a_bar[:, None] * h0)
    a_seq = jnp.broadcast_to(a_bar[:, None, None], bu.shape)
    _, h = lax.associative_scan(_linear_combine, (a_seq, bu), axis=2)
    c_mat = lax.complex(c_re.astype(f32), c_im.astype(f32))
    y = jnp.einsum('gpn,btgn->btgp', c_mat, from_dirs(h)).real.reshape(bsz, t_len, D_BRANCH)
    return y + d_skip.astype(f32) * u.astype(f32), h[:, :, -1]


def chunk_state_scan(s0, decay, u):
    def step(s, inp):
        dec, uu = inp
        return dec * s + uu, s
    s_last, starts = lax.scan(step, s0, (jnp.moveaxis(decay, 2, 0), jnp.moveaxis(u, 2, 0)))
    return jnp.moveaxis(starts, 0, 2), s_last


def gla_chunkwise(q, k, v, log_f, s0, chunk):
    nd, bsz, t_len, n_h, _ = q.shape
    n = t_len // chunk
    rs = lambda a: a.reshape(nd, bsz, n, chunk, *a.shape[3:])
    q, k, v, log_f = rs(q), rs(k), rs(v), rs(log_f)
    b = jnp.cumsum(log_f, axis=3)
    causal = jnp.tril(jnp.ones((chunk, chunk), dtype=bool))
    diff = b[:, :, :, :, None] - b[:, :, :, None]
    decay = jnp.exp(jnp.where(causal[:, :, None, None], diff, -jnp.inf))
    attn = jnp.einsum('dbntshk,dbnthk,dbnshk->dbntsh', decay, q, k)
    o = jnp.einsum('dbntsh,dbnshv->dbnthv', attn, v)
    b_last = b[:, :, :, -1]
    u = jnp.einsum('dbnshk,dbnshv->dbnhkv', k * jnp.exp(b_last[:, :, :, None] - b), v)
    starts, s_last = chunk_state_scan(s0, jnp.exp(b_last)[..., None], u)
    o = o + jnp.einsum('dbnthk,dbnhkv->dbnthv', q * jnp.exp(b), starts)
    return o.reshape(nd, bsz, t_len, n_h, v.shape[-1]), s_last


def retention_chunkwise(q, k, v, log_gamma, s0):
    nd, bsz, t_len, n_h, _ = q.shape
    L = RET_CHUNK
    n = t_len // L
    rs = lambda a: a.reshape(nd, bsz, n, L, *a.shape[3:])
    q, k, v = rs(q), rs(k), rs(v)
    pos = jnp.arange(L, dtype=jnp.float32)
    rel = pos[:, None] - pos[None, :]
    dmat = jnp.exp(jnp.where((rel >= 0)[None, :, :, None], rel[None, :, :, None] * log_gamma[:, None, None, :], -jnp.inf))
    attn = jnp.einsum('dbnthk,dbnshk->dbntsh', q, k) * dmat[:, None, None]
    o = jnp.einsum('dbntsh,dbnshv->dbnthv', attn, v)
    q_dec = jnp.exp((pos + 1.0)[None, :, None] * log_gamma[:, None, :])
    k_dec = jnp.exp((L - 1.0 - pos)[None, :, None] * log_gamma[:, None, :])
    u = jnp.einsum('dbnshk,dsh,dbnshv->dbnhkv', k, k_dec, v)
    chunk_dec = jnp.broadcast_to(jnp.exp(L * log_gamma)[:, None, None, :, None, None], (nd, 1, n, n_h, 1, 1))
    starts, s_last = chunk_state_scan(s0, chunk_dec, u)
    o = o + jnp.einsum('dbnthk,dth,dbnhkv->dbnthv', q, q_dec, starts)
    return o.reshape(nd, bsz, t_len, n_h, v.shape[-1]), s_last


def _rwkv7_step(s, inp):
    r, w, k, v, kk, a = inp
    sa = jnp.einsum('dbhvk,dbhk->dbhv', s, -kk)
    s = s * w[..., None, :] + sa[..., None] * (kk * a)[..., None, :] + v[..., None] * k[..., None, :]
    return s, jnp.einsum('dbhvk,dbhk->dbhv', s, r)


def rwkv7_mixer(xs, w0, w2, a0, a2, k_k, k_a, r_k, gn_w, gn_b, s0):
    bsz, t_len, _ = xs.shape
    f32 = jnp.float32
    r, k, v, w_lo, a_lo = jnp.split(xs.astype(f32), [D_BRANCH, 2 * D_BRANCH, 3 * D_BRANCH, 3 * D_BRANCH + 2 * RW_LORA], axis=-1)
    w_lo = w_lo.reshape(bsz, t_len, 2, RW_LORA)
    a_lo = a_lo.reshape(bsz, t_len, 2, RW_LORA)
    lw = -jax.nn.softplus(-(w0[:, None, None] + jnp.einsum('btdr,drc->dbtc', jnp.tanh(w_lo), w2))) - 0.5
    decay = jnp.exp(-jnp.exp(lw))
    iclr = jax.nn.sigmoid(a0[:, None, None] + jnp.einsum('btdr,drc->dbtc', a_lo, a2))
    kk = split_heads(k * k_k, RW_HEAD)
    kk = kk / jnp.maximum(jnp.linalg.norm(kk, axis=-1, keepdims=True), 1e-12)
    k_dir = k[None] * (1.0 + (iclr - 1.0) * k_a)
    hs = lambda a: split_heads(a, RW_HEAD)
    seq = (to_dirs(hs(r)), flip_bwd(hs(decay)), flip_bwd(hs(k_dir)), to_dirs(hs(v)), to_dirs(kk), flip_bwd(hs(iclr)))
    s_last, o = lax.scan(_rwkv7_step, s0, tuple(jnp.moveaxis(a, 2, 0) for a in seq))
    o = from_dirs(jnp.moveaxis(o, 0, 2))
    mu = o.mean(-1, keepdims=True)
    var = jnp.square(o - mu).mean(-1, keepdims=True)
    o = ((o - mu) * lax.rsqrt(var + RW_GN_EPS)).reshape(bsz, t_len, D_BRANCH) * gn_w + gn_b
    bonus = (hs(r[None] * k_dir * r_k).sum(-1, keepdims=True) * hs(v)[None]).sum(0)
    return o + bonus.reshape(bsz, t_len, D_BRANCH), s_last


def zero_states(bsz):
    f32 = jnp.float32
    return (jnp.zeros((2, bsz, S5_GROUPS, S5_STATE), jnp.complex64),
            jnp.zeros((2, bsz, HG_HEADS, HG_HEAD, HG_HEAD), f32),
            jnp.zeros((2, bsz, RET_HEADS, RET_HEAD, RET_HEAD), f32),
            jnp.zeros((2, bsz, RW_HEADS, RW_HEAD, RW_HEAD), f32))


def mixer_layer(h, mod, states, grid, lp):
    f32 = jnp.float32
    bsz, t_len, _ = h.shape
    shift, scale, gate = jnp.split(mod, 3, axis=-1)
    u = layer_norm(h) * (1.0 + scale) + shift
    proj = u @ lp['w_in'] + lp['b_in']
    (s5_u, s5_z, hg_q, hg_f, hg_i, hg_z, rt_q, rt_k, rt_v, rt_z, rw_x, rw_z, gate_logits) = jnp.split(
        proj, np.cumsum(IN_WIDTHS)[:-1].tolist(), axis=-1)
    s5_h0, hg_s0, rt_s0, rw_s0 = states

    y, s5_h = s5_mixer(s5_u, lp['s5_lam_re'], lp['s5_lam_im'], lp['s5_log_dt'], lp['s5_b_re'], lp['s5_b_im'],
                       lp['s5_c_re'], lp['s5_c_im'], lp['s5_d'], s5_h0)
    y = jax.nn.gelu(y)
    y_a = y * jax.nn.sigmoid(y @ lp['s5_glu_w'] + lp['s5_glu_b']) * jax.nn.silu(s5_z)

    lb = lp['hg_lb'][:, None, None]
    fgate = lb + (1.0 - lb) * jax.nn.sigmoid(jnp.moveaxis(hg_f.astype(f32).reshape(bsz, t_len, 2, D_BRANCH), 2, 0))
    hh = lambda a: split_heads(a, HG_HEAD)
    o, hg_s = gla_chunkwise(to_dirs(hh(jax.nn.silu(hg_q))), flip_bwd(hh(1.0 - fgate)), to_dirs(hh(hg_i)),
                            flip_bwd(hh(jnp.log(fgate))), hg_s0, HG_CHUNK)
    y_b = rms_norm(from_dirs(o)).reshape(bsz, t_len, D_BRANCH) * lp['hg_norm_w'] * jax.nn.silu(hg_z)

    q = split_heads(rt_q, RET_HEAD)
    k = split_heads(rt_k, RET_HEAD) * RET_HEAD ** -0.5
    if grid is not None:
        q = rotary_2d(q, grid[0], grid[1])
        k = rotary_2d(k, grid[0], grid[1])
    log_gamma = -jnp.exp(lp['ret_decay'].astype(f32))
    o, rt_s = retention_chunkwise(to_dirs(q), to_dirs(k), to_dirs(split_heads(rt_v, RET_HEAD)), log_gamma, rt_s0)
    y_c = rms_norm(from_dirs(o)).reshape(bsz, t_len, D_BRANCH) * jax.nn.silu(rt_z)

    y_d, rw_s = rwkv7_mixer(token_shift_centred(rw_x, lp['rw_mu']), lp['rw_w0'], lp['rw_w2'], lp['rw_a0'],
                            lp['rw_a2'], lp['rw_kk'], lp['rw_ka'], lp['rw_rk'], lp['rw_gn_w'], lp['rw_gn_b'], rw_s0)
    y_d = y_d * jax.nn.silu(rw_z)

    ys = jnp.stack([y_a, y_b, y_c, y_d], axis=-2).astype(h.dtype)
    branch = jnp.einsum('btkw,kwd->btkd', ys, lp['w_branch'])
    merged = jnp.sum(jax.nn.sigmoid(gate_logits.reshape(bsz, t_len, N_BRANCH, D_MODEL)) * branch, axis=-2)
    out = merged @ lp['w_out'] + lp['b_out']
    h_new = layer_norm(DEEPNORM_ALPHA * h + gate * out) * lp['ln_w'] + lp['ln_b']
    return h_new.astype(h.dtype), (s5_h, hg_s, rt_s, rw_s)


def setup_inputs(seed: int = 0) -> dict:
    key = jax.random.key(seed)
    ks = iter(jax.random.split(key, 48))
    f32 = jnp.float32
    W = D_BRANCH

    def nrm(shape, scale=1.0):
        return scale * jax.random.normal(next(ks), shape, f32)

    n_idx = jnp.arange(S5_STATE, dtype=f32)
    gamma = 1.0 - 2.0 ** (-5.0 - jnp.arange(RET_HEADS, dtype=f32))
    decay_speed = -6.0 + 5.0 * (jnp.arange(W, dtype=f32) / (W - 1)) ** 0.85
    return {
        'x': nrm((BATCH, SEQ, D_MODEL)),
        'c': nrm((BATCH, D_MODEL)),
        'ctx': nrm((BATCH, CTX_LEN, D_MODEL)),
        'c_ctx': nrm((D_MODEL,)),
        'ada_w': nrm((DEPTH, D_MODEL, 3 * D_MODEL), D_MODEL ** -0.5),
        'ada_b': nrm((DEPTH, 3 * D_MODEL), 0.01),
        'w_in': nrm((DEPTH, D_MODEL, D_IN), D_MODEL ** -0.5),
        'b_in': nrm((DEPTH, D_IN), 0.01),
        's5_lam_re': -0.5 + nrm((DEPTH, 2, S5_GROUPS, S5_STATE), 0.01),
        's5_lam_im': jnp.pi * n_idx + nrm((DEPTH, 2, S5_GROUPS, S5_STATE), 0.01),
        's5_log_dt': jax.random.uniform(next(ks), (DEPTH, 2, S5_GROUPS), f32, math.log(1e-3), math.log(1e-1)),
        's5_b_re': nrm((DEPTH, S5_GROUPS, S5_STATE, S5_GROUP), (2 * S5_GROUP) ** -0.5),
        's5_b_im': nrm((DEPTH, S5_GROUPS, S5_STATE, S5_GROUP), (2 * S5_GROUP) ** -0.5),
        's5_c_re': nrm((DEPTH, S5_GROUPS, S5_GROUP, S5_STATE), S5_STATE ** -0.5),
        's5_c_im': nrm((DEPTH, S5_GROUPS, S5_GROUP, S5_STATE), S5_STATE ** -0.5),
        's5_d': nrm((DEPTH, W)),
        's5_glu_w': nrm((DEPTH, W, W), W ** -0.5),
        's5_glu_b': nrm((DEPTH, W), 0.01),
        'hg_lb': 1.0 + nrm((DEPTH, 2, W), 0.1),
        'hg_norm_w': 1.0 + nrm((DEPTH, W), 0.01),
        'ret_decay': jnp.log(-jnp.log(gamma)) + nrm((DEPTH, 2, RET_HEADS), 0.01),
        'rw_mu': jax.random.uniform(next(ks), (DEPTH, 2, RW_SHIFT_W), f32, 0.0, 0.5),
        'rw_w0': decay_speed + nrm((DEPTH, 2, W), 0.01),
        'rw_w2': nrm((DEPTH, 2, RW_LORA, W), 0.1),
        'rw_a0': nrm((DEPTH, 2, W), 0.1),
        'rw_a2': nrm((DEPTH, 2, RW_LORA, W), 0.1),
        'rw_kk': 0.85 + nrm((DEPTH, W), 0.01),
        'rw_ka': 1.0 + nrm((DEPTH, W), 0.01),
        'rw_rk': nrm((DEPTH, W), 0.1),
        'rw_gn_w': 1.0 + nrm((DEPTH, W), 0.01),
        'rw_gn_b': nrm((DEPTH, W), 0.01),
        'w_branch': nrm((DEPTH, N_BRANCH, W, D_MODEL), W ** -0.5 * DEEPNORM_BETA),
        'w_out': nrm((DEPTH, D_MODEL, D_MODEL), D_MODEL ** -0.5 * DEEPNORM_BETA),
        'b_out': nrm((DEPTH, D_MODEL), 0.01),
        'ln_w': 1.0 + nrm((DEPTH, D_MODEL), 0.01),
        'ln_b': nrm((DEPTH, D_MODEL), 0.01),
    }


def reference(x, c, ctx, c_ctx, ada_w, ada_b, w_in, b_in, s5_lam_re, s5_lam_im, s5_log_dt, s5_b_re, s5_b_im,
              s5_c_re, s5_c_im, s5_d, s5_glu_w, s5_glu_b, hg_lb, hg_norm_w, ret_decay, rw_mu, rw_w0, rw_w2,
              rw_a0, rw_a2, rw_kk, rw_ka, rw_rk, rw_gn_w, rw_gn_b, w_branch, w_out, b_out, ln_w, ln_b):
    seq_len = x.shape[1]
    ROWS = seq_len // GRID_W
    rows = jnp.repeat(jnp.arange(ROWS, dtype=jnp.int32), GRID_W)
    cols = jnp.tile(jnp.arange(GRID_W, dtype=jnp.int32), ROWS)
    grid = (rows, cols)
    lb_soft = jax.nn.softmax(hg_lb.astype(jnp.float32), axis=0)
    lower_bounds = jnp.cumsum(lb_soft, axis=0) - lb_soft[0]
    h_lat, h_ctx = x, ctx
    for l in range(DEPTH):
        lp = {
            'w_in': w_in[l], 'b_in': b_in[l],
            's5_lam_re': s5_lam_re[l], 's5_lam_im': s5_lam_im[l], 's5_log_dt': s5_log_dt[l],
            's5_b_re': s5_b_re[l], 's5_b_im': s5_b_im[l], 's5_c_re': s5_c_re[l], 's5_c_im': s5_c_im[l],
            's5_d': s5_d[l], 's5_glu_w': s5_glu_w[l], 's5_glu_b': s5_glu_b[l],
            'hg_lb': lower_bounds[l], 'hg_norm_w': hg_norm_w[l], 'ret_decay': ret_decay[l],
            'rw_mu': rw_mu[l], 'rw_w0': rw_w0[l], 'rw_w2': rw_w2[l], 'rw_a0': rw_a0[l], 'rw_a2': rw_a2[l],
            'rw_kk': rw_kk[l], 'rw_ka': rw_ka[l], 'rw_rk': rw_rk[l], 'rw_gn_w': rw_gn_w[l], 'rw_gn_b': rw_gn_b[l],
            'w_branch': w_branch[l], 'w_out': w_out[l], 'b_out': b_out[l], 'ln_w': ln_w[l], 'ln_b': ln_b[l],
        }
        mod_lat = (jax.nn.silu(c) @ ada_w[l] + ada_b[l])[:, None, :]
        mod_ctx = jax.nn.silu(c_ctx) @ ada_w[l] + ada_b[l]
        h_ctx_next, ctx_states = mixer_layer(h_ctx, mod_ctx, zero_states(ctx.shape[0]), None, lp)
        h_lat, _ = mixer_layer(h_lat, mod_lat, ctx_states, grid, lp)
        h_ctx = h_ctx_next
    return h_lat
```

```python
import contextlib
import numpy as np
import concourse.bass as bass
import concourse.mybir as mybir
from concourse.bass_utils import run_bass_kernel_spmd

F32 = mybir.dt.float32
BF16 = mybir.dt.bfloat16
AF = mybir.ActivationFunctionType
ALU = mybir.AluOpType
AX = mybir.AxisListType

NT = 2304
NTL = 18
DM = 1024
NCOL = 8064
BLOCKS = [(0, 256), (256, 512), (768, 512), (1280, 512), (1792, 512)]
LN_EPS = 1e-5
RMS_EPS = 1e-6
RW_GN_EPS = 64e-5
ALPHA = (2 * 2) ** 0.25
PI = float(np.pi)
MEMDBG = False


class Sched:
    NDMA = 16

    def __init__(self, nc, same_engine_waits=True):
        self.nc = nc
        self.same = same_engine_waits
        self.eng = dict(pe=nc.tensor, act=nc.scalar, dve=nc.vector, pool=nc.gpsimd, sp=nc.sync)
        self.E = {n: dict(cnt=0, known={}) for n in self.eng}
        self.dq = {'sp': ['dsp%d' % i for i in range(8)], 'act': ['dac%d' % i for i in range(4)],
                   'pool': ['dpl%d' % i for i in range(8)]}
        self.dmas = {n: dict(cnt=0) for q in self.dq.values() for n in q}
        self.dma_rr = {'sp': 0, 'act': 0, 'pool': 0}
        self.lastw = {}
        self.readers = {}
        self.sems = None
        self.nins = 0

    def sem_names(self):
        return list(self.E.keys()) + list(self.dmas.keys())

    def _deps(self, reads, writes):
        deps = {}

        def add(w):
            if w is not None:
                deps[w[0]] = max(deps.get(w[0], 0), w[1])
        for k in reads:
            add(self.lastw.get(k))
        for k in writes:
            add(self.lastw.get(k))
            for r in self.readers.get(k, ()):
                add(r)
        return deps

    def _waits(self, en, deps):
        E = self.E[en]
        waits = []
        for d, v in deps.items():
            if d == en and (en == 'pe' or not self.same):
                continue
            if E['known'].get(d, 0) < v:
                waits.append((d, v))
                E['known'][d] = v
        return waits

    def _record(self, ident, reads, writes):
        for k in writes:
            self.lastw[k] = ident
            self.readers[k] = []
        for k in reads:
            self.readers.setdefault(k, []).append(ident)

    def _emit(self, en, waits, fn, inc):
        eng = self.eng[en]
        for d, v in waits:
            eng.wait_ge(self.sems[d], v)
        if fn is not None:
            fn(eng).then_inc(self.sems[inc[0]], inc[1])
            self.nins += 1

    def op(self, en, fn, reads=(), writes=()):
        E = self.E[en]
        waits = self._waits(en, self._deps(reads, writes))
        E['cnt'] += 1
        self._emit(en, waits, fn, (en, 1))
        self._record((en, E['cnt']), reads, writes)

    def dma(self, en, out, in_, reads=(), writes=(), **kw):
        dn = self.dq[en][self.dma_rr[en]]
        self.dma_rr[en] = (self.dma_rr[en] + 1) % len(self.dq[en])
        Dq = self.dmas[dn]
        deps = self._deps(reads, writes)
        if Dq['cnt'] > 0:
            deps[dn] = max(deps.get(dn, 0), Dq['cnt'])
        waits = self._waits(en, deps)
        Dq['cnt'] += 16
        self._emit(en, waits, (lambda e: e.dma_start(out=out, in_=in_, **kw)), (dn, 16))
        self._record((dn, Dq['cnt']), reads, writes)

    def barrier(self):
        cur = {n: self.E[n]['cnt'] for n in self.E}
        cur.update({n: self.dmas[n]['cnt'] for n in self.dmas})
        for en in self.E:
            waits = self._waits(en, {d: v for d, v in cur.items() if v > 0})
            self._emit(en, waits, None, None)

    def final_wait(self, en, keys):
        self._emit(en, self._waits(en, self._deps(keys, ())), None, None)


PV = {}


def _pv_layout():
    off = 0
    for name, n in [('bin', 63), ('s5d', 2), ('glub', 2), ('hglb', 8), ('hgnw', 2), ('rdec', 4), ('mu', 14),
                    ('w0', 4), ('a0', 4), ('kk', 2), ('ka', 2), ('rk', 2), ('gnw', 2), ('gnb', 2), ('adab', 16),
                    ('lamre', 16), ('lamim', 16), ('ldt', 16), ('rdech', 8)]:
        PV[name] = (off, n)
        off += n
    return off


NPV = _pv_layout()


def _colmap():
    cm = list(range(0, 3584))
    lora = [-1] * 128
    for r in range(16):
        lora[r] = 3584 + r
        lora[32 + r] = 3600 + r
        lora[64 + r] = 3616 + r
        lora[80 + r] = 3632 + r
    cm += lora
    cm += list(range(3648, 3904))
    cm += list(range(3904, 8000))
    return np.array(cm)


CMAP = _colmap()


def _fm(v):
    return np.ascontiguousarray(v.reshape(-1, 128).T)


def _masks():
    t = np.arange(128)
    s_, t_ = t[:, None], t[None, :]
    m = []
    b32 = (s_ // 32) == (t_ // 32)
    b64 = (s_ // 64) == (t_ // 64)
    m.append(b32 & (t_ >= s_))
    m.append(b32 & (t_ <= s_))
    m.append(b64 & (t_ > s_))
    m.append(b64 & (t_ < s_))
    m.append(b64 & (t_ >= s_))
    m.append(b64 & (t_ <= s_))
    for d in range(2):
        for lv in range(6):
            sz = 1 << lv
            blk = (s_ // (2 * sz)) == (t_ // (2 * sz))
            hs, ht = (s_ // sz) % 2, (t_ // sz) % 2
            if d == 0:
                m.append(blk & (ht == 1) & (hs == 0))
            else:
                m.append(blk & (ht == 0) & (hs == 1))
    return np.stack([x.astype(np.float32) for x in m], 1)


def _rot_tables():
    n = 16
    freqs = 10000.0 ** (-np.arange(n, dtype=np.float32) / n)
    tt = np.arange(2048)
    rows = (tt // 64).astype(np.float32)
    cols = (tt % 64).astype(np.float32)
    cos = np.zeros((128, 2048), np.float32)
    sins = np.zeros((128, 2048), np.float32)
    pm = np.zeros((128, 128), np.float32)
    for p in range(128):
        i = p % 64
        pos = rows if i < 32 else cols
        ii = i % 32
        ang = pos * freqs[ii % 16]
        cos[p] = np.cos(ang)
        if ii < 16:
            sins[p] = -np.sin(ang)
            partner = p + 16
        else:
            sins[p] = np.sin(ang)
            partner = p - 16
        pm[partner, p] = 1.0
    return cos, sins, pm


def prep_shared(inp):
    sh = {}
    L = 2
    w_in = inp['w_in']
    wn = np.zeros((L, 1024, NCOL), np.float32)
    valid = CMAP >= 0
    wn[:, :, valid] = w_in[:, :, CMAP[valid]]
    sh['w_in'] = np.ascontiguousarray(wn.reshape(L, 8, 128, NCOL).transpose(0, 2, 1, 3))
    bn = np.zeros((L, NCOL), np.float32)
    bn[:, valid] = inp['b_in'][:, CMAP[valid]]
    sh['ada_w'] = np.ascontiguousarray(inp['ada_w'].reshape(L, 8, 128, 3072).transpose(0, 2, 1, 3))
    pv = np.zeros((L, 128, NPV), np.float32)

    def put(l, name, arr):
        o, n = PV[name]
        assert arr.shape == (128, n), (name, arr.shape)
        pv[l, :, o:o + n] = arr
    for l in range(L):
        put(l, 'bin', _fm(bn[l]))
        put(l, 's5d', _fm(inp['s5_d'][l]))
        put(l, 'glub', _fm(inp['s5_glu_b'][l]))
        put(l, 'hglb', np.concatenate([_fm(inp['hg_lb'][ll, d]) for ll in range(2) for d in range(2)], 1))
        put(l, 'hgnw', _fm(inp['hg_norm_w'][l]))
        rd = np.zeros((128, 4), np.float32)
        for d in range(2):
            for j in range(2):
                rd[:64, d * 2 + j] = inp['ret_decay'][l, d, 2 * j]
                rd[64:, d * 2 + j] = inp['ret_decay'][l, d, 2 * j + 1]
        put(l, 'rdec', rd)
        put(l, 'rdech', np.ascontiguousarray(np.broadcast_to(inp['ret_decay'][l].reshape(1, 8), (128, 8))))
        mu = np.zeros((2, 7 * 128), np.float32)
        mu[:, :768] = inp['rw_mu'][l][:, :768]
        lv = CMAP[3584:3712]
        ok = lv >= 0
        mu[:, 768:896][:, ok] = inp['rw_mu'][l][:, lv[ok] - 2816]
        put(l, 'mu', np.concatenate([_fm(mu[0]), _fm(mu[1])], 1))
        put(l, 'w0', np.concatenate([_fm(inp['rw_w0'][l, d]) for d in range(2)], 1))
        put(l, 'a0', np.concatenate([_fm(inp['rw_a0'][l, d]) for d in range(2)], 1))
        for nm, key in [('kk', 'rw_kk'), ('ka', 'rw_ka'), ('rk', 'rw_rk'), ('gnw', 'rw_gn_w'), ('gnb', 'rw_gn_b')]:
            put(l, nm, _fm(inp[key][l]))
        put(l, 'adab', _fm(inp['ada_b'][l][:2048]))
        for nm, key in [('lamre', 's5_lam_re'), ('lamim', 's5_lam_im')]:
            a = inp[key][l].reshape(2, 8, 2, 64)
            put(l, nm, np.ascontiguousarray(a.transpose(2, 3, 0, 1).reshape(128, 16)))
        a = np.broadcast_to(inp['s5_log_dt'][l].reshape(2, 8, 2, 1), (2, 8, 2, 64))
        put(l, 'ldt', np.ascontiguousarray(a.transpose(2, 3, 0, 1).reshape(128, 16)))
    sh['pv'] = pv
    bt = np.zeros((L, 128, 2, 4, 2, 128), np.float32)
    ct = np.zeros((L, 128, 8, 2, 128), np.float32)
    for l in range(L):
        for g in range(16):
            i, g2 = g // 2, g % 2
            for q in range(16):
                c = g * 16 + q
                j, p = c // 128, c % 128
                bt[l, p, j, i % 4, 0, g2 * 64:(g2 + 1) * 64] = inp['s5_b_re'][l, g, :, q]
                bt[l, p, j, i % 4, 1, g2 * 64:(g2 + 1) * 64] = inp['s5_b_im'][l, g, :, q]
            m0 = (i % 4) * 32 + g2 * 16
            ct[l, g2 * 64:(g2 + 1) * 64, i, 0, m0:m0 + 16] = inp['s5_c_re'][l, g].T
            ct[l, g2 * 64:(g2 + 1) * 64, i, 1, m0:m0 + 16] = inp['s5_c_im'][l, g].T
    sh['s5bt'] = bt
    sh['s5ct'] = ct
    sh['gluw'] = np.ascontiguousarray(inp['s5_glu_w'].reshape(L, 2, 128, 256).transpose(0, 2, 1, 3))
    lw2 = np.zeros((L, 128, 2, 256), np.float32)
    for l in range(L):
        lw2[l, 0:16, 0] = inp['rw_w2'][l, 0]
        lw2[l, 32:48, 1] = inp['rw_w2'][l, 1]
        lw2[l, 64:80, 0] = inp['rw_a2'][l, 0]
        lw2[l, 80:96, 1] = inp['rw_a2'][l, 1]
    sh['lw2'] = lw2
    sh['wbr'] = np.ascontiguousarray(inp['w_branch'].reshape(L, 4, 2, 128, 1024).transpose(0, 3, 1, 2, 4))
    sh['wout'] = np.ascontiguousarray(inp['w_out'].reshape(L, 8, 128, 1024).transpose(0, 2, 1, 3))
    rows = np.zeros((L, 128, 4096 + 512), np.float32)
    for l in range(L):
        rows[l, :, 0:1024] = inp['b_out'][l][None]
        rows[l, :, 1024:2048] = inp['ln_w'][l][None]
        rows[l, :, 2048:3072] = inp['ln_b'][l][None]
        rows[l, :, 3072:4096] = inp['ada_b'][l][None, 2048:3072]
        rows[l, :, 4096:4352] = inp['b_in'][l][None, 1280:1536]
        rows[l, :, 4352:4608] = inp['b_in'][l][None, 2304:2560]
    sh['rows'] = rows
    sh['masks'] = _masks()
    cos, sins, pm = _rot_tables()
    sh['rcos'] = cos
    sh['rsin'] = sins
    t = np.arange(128)
    cst = np.zeros((128, 9, 128), np.float32)
    cst[:, 0] = pm
    cst[:, 1] = ((t[:, None] // 64) == (t[None, :] // 64))
    cst[:, 2] = np.maximum(t[None, :] - t[:, None], 0)
    cst[:, 3] = np.maximum(t[:, None] - t[None, :], 0)
    cst[:, 4] = (t[None, :] >= t[:, None])
    cst[:, 5] = (t[None, :] <= t[:, None])
    cst[:, 6, :64] = ((t[:, None] % 64) == np.arange(64)[None, :])
    cst[:, 6, 64:68] = ((t[:, None] // 32) == np.arange(4)[None, :])
    cst[:, 6, 68] = 127 - t
    cst[:, 6, 69] = t
    cst[:, 7] = t[None, :] + 1.0
    cst[:, 8] = 128.0 - t[None, :]
    sh['cst'] = cst
    return sh


def prep_core(inp, b):
    pc = {}
    pc['hin'] = np.ascontiguousarray(np.concatenate([inp['ctx'][b], inp['x'][b]], 0))
    cv = np.stack([inp['c'][b], inp['c_ctx']], -1)
    pc['cvec'] = np.ascontiguousarray(cv.reshape(8, 128, 2).transpose(1, 0, 2))
    return pc


SHAPES = dict(hin=[NT, DM], cvec=[128, 8, 2], w_in=[2, 128, 8, NCOL], ada_w=[2, 128, 8, 3072], pv=[2, 128, NPV],
              s5bt=[2, 128, 2, 4, 2, 128], s5ct=[2, 128, 8, 2, 128], gluw=[2, 128, 2, 256], lw2=[2, 128, 2, 256],
              wbr=[2, 128, 4, 2, 1024], wout=[2, 128, 8, 1024], rows=[2, 128, 4608], masks=[128, 18, 128],
              rcos=[128, 2048], rsin=[128, 2048], cst=[128, 9, 128])


def build(debug=(), nlayers=2, phases=('s5', 'hg', 'ret', 'rw', 'merge'), stop=None):
    nc = bass.Bass("TRN2", target_bir_lowering=False)
    S = Sched(nc)
    dr = {k: nc.dram_tensor(k, list(v), F32, kind="ExternalInput").ap() for k, v in SHAPES.items()}
    out_d = nc.dram_tensor("out", [2048, DM], F32, kind="ExternalOutput").ap()
    h1_d = nc.dram_tensor("h1", [NT, DM], F32, kind="Internal").ap()
    dbg_d = {}

    def dbg_out(name, shape):
        dbg_d[name] = nc.dram_tensor("dbg_" + name, list(shape), F32, kind="ExternalOutput").ap()
        return dbg_d[name]

    uid = [0]

    def key(p='k'):
        uid[0] += 1
        return '%s%d' % (p, uid[0])

    with contextlib.ExitStack() as top:
        S.sems = {n: top.enter_context(nc.semaphore(n)) for n in S.sem_names()}

        minrem = {}

        def sb(st, name, shape, dt=F32):
            uid[0] += 1
            t_ = st.enter_context(nc.sbuf_tensor("%s_%d" % (name, uid[0]), list(shape), dt))
            if MEMDBG:
                pre = name[:2]
                minrem[pre] = min(minrem.get(pre, 1 << 30), nc.sbuf_bytes_remaining)
            return t_

        def ps(st, name, shape, dt=F32):
            uid[0] += 1
            return st.enter_context(nc.psum_tensor("%s_%d" % (name, uid[0]), list(shape), dt))

        def mm(out, lhsT, rhs, r, w, start=True, stop=True):
            S.op('pe', lambda e: e.matmul(out, lhsT=lhsT, rhs=rhs, start=start, stop=stop), reads=r, writes=w)

        def tr(out, in_, ident, r, w):
            S.op('pe', lambda e: e.transpose(out, in_, ident), reads=r, writes=w)

        def act(out, in_, func, r, w, bias=0.0, scale=1.0):
            S.op('act', lambda e: e.activation(out=out, in_=in_, func=func, bias=bias, scale=scale), reads=r, writes=w)

        def tt(en, out, in0, in1, op, r, w):
            S.op(en, lambda e: e.tensor_tensor(out=out, in0=in0, in1=in1, op=op), reads=r, writes=w)

        def ts(en, out, in0, s1, s2, op0, op1, r, w):
            if s2 is None:
                S.op(en, lambda e: e.tensor_scalar(out=out, in0=in0, scalar1=s1, scalar2=None, op0=op0), reads=r, writes=w)
            else:
                S.op(en, lambda e: e.tensor_scalar(out=out, in0=in0, scalar1=s1, scalar2=s2, op0=op0, op1=op1),
                     reads=r, writes=w)

        def stt(out, in0, sc, in1, op0, op1, r, w):
            S.op('dve', lambda e: e.scalar_tensor_tensor(out=out, in0=in0, scalar=sc, in1=in1, op0=op0, op1=op1),
                 reads=r, writes=w)

        def cp(en, out, in_, r, w):
            if en == 'act':
                S.op('act', lambda e: e.copy(out=out, in_=in_), reads=r, writes=w)
            else:
                S.op(en, lambda e: e.tensor_copy(out=out, in_=in_), reads=r, writes=w)

        def memset(en, ap, val, w):
            S.op(en, lambda e: e.memset(ap, val), writes=w)

        def run_pipelined(gens, stagger):
            it = iter(gens)
            active, pending, rounds = [], True, 0
            while pending or active:
                if pending and rounds % stagger == 0:
                    try:
                        active.append(next(it))
                    except StopIteration:
                        pending = False
                for g in list(active):
                    try:
                        next(g)
                    except StopIteration:
                        active.remove(g)
                rounds += 1

        def mkbanks(st_, n, prefix):
            bl = [ps(st_, "%s%d" % (prefix, i), [128, 512], F32) for i in range(n)]
            cnt = [0]

            def bank():
                i = cnt[0] % n
                cnt[0] += 1
                return bl[i], '%s%d' % (prefix, i)
            return bank

        def dbg_dump(name, ap, shape, r):
            if name in debug:
                d = dbg_out(name, shape)
                S.dma('sp', d, ap, reads=r, writes=['dbgout_' + name])

        cstb = sb(top, "cstb", [128, 3, 128], BF16)
        cstf = sb(top, "cstf", [128, 7, 128], F32)
        maskb = sb(top, "maskb", [128, 18, 128], BF16)
        silc = sb(top, "silc", [128, 8, 2], F32)
        S.dma('pool', cstb[:, 0:2, :], dr['cst'][:, 0:2, :], writes=['cstb'])
        S.dma('sp', cstf[:], dr['cst'][:, 2:9, :], writes=['cstf'])
        S.dma('pool', maskb[:], dr['masks'], writes=['maskb'])
        S.dma('sp', silc[:], dr['cvec'], writes=['silc'])
        memset('pool', cstb[:, 2, :], 0.0, ['cstb'])
        S.op('pool', lambda e: e.affine_select(out=cstb[:, 2, :], in_=cstb[:, 2, :], pattern=[[-1, 128]],
                                               compare_op=ALU.not_equal, fill=1.0, base=0, channel_multiplier=1),
             reads=['cstb'], writes=['cstb'])
        act(silc[:], silc[:], AF.Silu, ['silc'], ['silc'])
        identb = cstb[:, 2, :]
        bonesb = cstb[:, 1, :]

        uT = sb(top, "uT", [128, 8, NT], BF16)
        Y = sb(top, "Y", [128, 4, 2, NT], BF16)
        pvt = sb(top, "pvt", [128, NPV], F32)
        if debug:
            memset('pool', Y[:], 0.0, ['Y0', 'Y1', 'Y2', 'Y3'])
        modfm = sb(top, "modfm", [128, 16, 2], F32)
        gatebc = sb(top, "gatebc", [128, 2, DM], F32)

        def pv(name, j=None, n=1):
            o, cnt = PV[name]
            if j is None:
                return pvt[:, o:o + cnt]
            return pvt[:, o + j:o + j + n]

        PHASES = {}
        def proj_fm(st, wt, wk, mlist, evac, pp, ppk):
            cnt = 0
            for (n0, nn) in BLOCKS:
                for mi, m in enumerate(mlist):
                    p_, pk_ = pp[cnt % len(pp)], ppk[cnt % len(pp)]
                    cnt += 1
                    for j in range(8):
                        mm(p_[:, 0:nn], wt[:, j, m * 128:(m + 1) * 128], uT[:, j, n0:n0 + nn],
                           [wk] + uTk[n0 // 128:(n0 + nn) // 128], [pk_], start=(j == 0), stop=(j == 7))
                    evac(mi, m, n0, nn, p_, pk_)

        def phase_s5(l, h_src, last):
            L = 128
            with contextlib.ExitStack() as st:
                btb = sb(st, "btb", [128, 2, 4, 2, 128], BF16)
                ctb = sb(st, "ctb", [128, 8, 2, 128], BF16)
                glub = sb(st, "glub", [128, 2, 256], BF16)
                S.dma('pool', btb[:], dr['s5bt'][l], writes=['btb'])
                S.dma('pool', ctb[:], dr['s5ct'][l], writes=['ctb'])
                S.dma('pool', glub[:], dr['gluw'][l], writes=['glub'])
                ts('pool', ctb[:, :, 1, :], ctb[:, :, 1, :], -1.0, 0.0, ALU.mult, ALU.add, ['ctb'], ['ctb'])
                ub = sb(st, "s5u", [128, 2, NT], BF16)
                zs = sb(st, "s5z", [128, 2, NT], BF16)
                yacc = sb(st, "yacc", [128, 2, NT], F32)
                PT = sb(st, "s5PT", [128, 16, 2, L], F32)
                QT = sb(st, "s5QT", [128, 16, 2, L], F32)
                sst = sb(st, "s5st", [128, 16, 2], F32)
                ones = sb(st, "s5ones", [128, L], F32)
                memset('pool', yacc[:], 0.0, ['yacc'])
                memset('pool', sst[:], 0.0, ['sst'])
                memset('pool', ones[:], 1.0, ['s5ones'])
                with contextlib.ExitStack() as st2:
                    wsu = sb(st2, "wsu", [128, 8, 512], BF16)
                    S.dma('pool', wsu[:], dr['w_in'][l][:, :, 0:512], writes=['wsu'])
                    pp = [ps(st2, "s5pp%d" % i, [128, 512], F32) for i in range(2)]

                    def evac(mi, m, n0, nn, p_, pk_):
                        if m < 2:
                            act(ub[:, m, n0:n0 + nn], p_[:, 0:nn], AF.Identity, [pk_, 'pvt'], ['s5u'], bias=pv('bin', m))
                        else:
                            act(zs[:, m - 2, n0:n0 + nn], p_[:, 0:nn], AF.Silu, [pk_, 'pvt'], ['s5z'], bias=pv('bin', m))
                    proj_fm(st2, wsu, 'wsu', [0, 1, 2, 3], evac, pp, ['s5pp0', 's5pp1'])
                    sm = sb(st2, "s5sm", [128, 20, 16], F32)
                    K_ = 's5sm'

                    def Sm(i):
                        return sm[:, i, :]

                    def T2(o, a, b, op):
                        tt('dve', Sm(o), a if not isinstance(a, int) else Sm(a), b if not isinstance(b, int) else Sm(b), op,
                           [K_, 'pvt'], [K_])
                    lamre, lamim = pv('lamre'), pv('lamim')
                    act(Sm(0), pv('ldt'), AF.Exp, ['pvt'], [K_])
                    T2(1, lamre, 0, ALU.mult)
                    act(Sm(2), Sm(1), AF.Exp, [K_], [K_])
                    act(Sm(3), Sm(1), AF.Exp, [K_], [K_], scale=-1.0)
                    T2(4, lamim, 0, ALU.mult)
                    ts('dve', Sm(5), Sm(4), PI / 2, None, ALU.add, None, [K_], [K_])
                    for x in (4, 5):
                        for _ in range(4):
                            ts('dve', Sm(16), Sm(x), PI, 2 * PI, ALU.is_gt, ALU.mult, [K_], [K_])
                            T2(x, x, 16, ALU.subtract)
                    act(Sm(6), Sm(4), AF.Sin, [K_], [K_])
                    act(Sm(7), Sm(5), AF.Sin, [K_], [K_])
                    T2(8, 2, 7, ALU.mult)
                    T2(9, 2, 6, ALU.mult)
                    T2(10, 3, 7, ALU.mult)
                    stt(Sm(11), Sm(3), -1.0, Sm(6), ALU.mult, ALU.mult, [K_], [K_])
                    ts('dve', Sm(12), Sm(8), -1.0, None, ALU.add, None, [K_], [K_])
                    T2(16, lamre, lamre, ALU.mult)
                    T2(17, lamim, lamim, ALU.mult)
                    T2(13, 16, 17, ALU.add)
                    S.op('dve', lambda e: e.reciprocal(out=Sm(13), in_=Sm(13)), reads=[K_], writes=[K_])
                    T2(16, 12, lamre, ALU.mult)
                    T2(17, 9, lamim, ALU.mult)
                    T2(16, 16, 17, ALU.add)
                    T2(14, 16, 13, ALU.mult)
                    T2(16, 9, lamre, ALU.mult)
                    T2(17, 12, lamim, ALU.mult)
                    T2(16, 16, 17, ALU.subtract)
                    T2(15, 16, 13, ALU.mult)
                    tmpa = sb(st2, "s5ta", [128, 16, L], F32)
                    tmpb = sb(st2, "s5tb", [128, 16, L], F32)

                    def cmul_bc(dst_re, dst_im, src_re, src_im, s_re, s_im, m):
                        sr = s_re.unsqueeze(2).broadcast_to([128, 16, m])
                        si = s_im.unsqueeze(2).broadcast_to([128, 16, m])
                        ta, tb = tmpa[:, :, 0:m], tmpb[:, :, 0:m]
                        kk_ = ['s5tab', 's5ta', 's5tb', 's5tc', K_]
                        tt('dve', ta, src_re, sr, ALU.mult, kk_, ['s5ta'])
                        tt('dve', tb, src_im, si, ALU.mult, kk_, ['s5tb'])
                        tt('dve', dst_re, ta, tb, ALU.subtract, kk_, ['s5tab'])
                        tt('dve', ta, src_re, si, ALU.mult, kk_, ['s5ta'])
                        tt('dve', tb, src_im, sr, ALU.mult, kk_, ['s5tb'])
                        tt('dve', dst_im, ta, tb, ALU.add, kk_, ['s5tab'])
                    for (TB, a_re, a_im) in ((PT, 8, 9), (QT, 10, 11)):
                        cp('dve', TB[:, :, 0, 0], Sm(a_re), [K_], ['s5tab'])
                        cp('dve', TB[:, :, 1, 0], Sm(a_im), [K_], ['s5tab'])
                        m = 1
                        while m < L:
                            cmul_bc(TB[:, :, 0, m:2 * m], TB[:, :, 1, m:2 * m], TB[:, :, 0, 0:m], TB[:, :, 1, 0:m],
                                    TB[:, :, 0, m - 1], TB[:, :, 1, m - 1], m)
                            m *= 2
                    tmpc = sb(st2, "s5tc", [128, 16, L], F32)
                    cp('dve', tmpc[:], QT[:, :, 0, :], ['s5tab'], ['s5tc'])
                    cmul_bc(QT[:, :, 0, :], QT[:, :, 1, :], tmpc[:], QT[:, :, 1, :], Sm(14), Sm(15), L)
                    S.barrier()
                with contextlib.ExitStack() as st2:
                    NB = 8
                    xa = [sb(st2, "s5xa%d" % i, [128, 2, L], F32) for i in range(NB)]
                    xb_ = [sb(st2, "s5xb%d" % i, [128, 2, L], F32) for i in range(NB)]
                    cw = [sb(st2, "s5cw%d" % i, [128, 2, L], F32) for i in range(NB)]
                    hb = [sb(st2, "s5hb%d" % i, [128, 2, L], BF16) for i in range(NB)]
                    pbu = [ps(st2, "s5pb%d" % i, [128, 2, 2, L], F32) for i in range(4)]
                    py = [ps(st2, "s5py%d" % i, [128, 512], F32) for i in range(2)]
                    orders = [list(range(NTL)), [1, 0] + list(range(NTL - 1, 1, -1))]
                    def s5group(gi, step, d, j):
                        c = orders[d][step]
                        n0 = c * L
                        rev = (d == 1)
                        U = []
                        for ii in range(4):
                            un = gi * 4 + ii
                            bnk = (un // 2) % 4
                            U.append(dict(ii=ii, i=j * 4 + ii, q=d * 8 + j * 4 + ii, pb=pbu[bnk][:, un % 2], pbk='s5pb%d' % bnk,
                                          A=xa[un % NB], Ak='s5xa%d' % (un % NB), B=xb_[un % NB], Bk='s5xb%d' % (un % NB),
                                          C=cw[un % NB], Ck='s5cw%d' % (un % NB), H=hb[un % NB], Hk='s5hb%d' % (un % NB)))
                        for u in U:
                            for ri in range(2):
                                mm(u['pb'][:, ri, :], btb[:, j, u['ii'], ri, :], ub[:, j, n0:n0 + L], ['btb', 's5u'], [u['pbk']])
                        yield
                        for u in U:
                            src = u['pb'][:, :, ::-1] if rev else u['pb'][:, :, :]
                            tt('dve', u['A'][:], src, QT[:, u['q'], 0:1, :].broadcast_to([128, 2, L]), ALU.mult,
                               [u['pbk'], 's5tab'], [u['Ak']])
                        yield
                        for u in U:
                            src = u['pb'][:, ::-1, ::-1] if rev else u['pb'][:, ::-1, :]
                            tt('dve', u['B'][:], src, QT[:, u['q'], 1:2, :].broadcast_to([128, 2, L]), ALU.mult,
                               [u['pbk'], 's5tab'], [u['Bk']])
                        yield
                        for u in U:
                            tt('dve', u['A'][:, 0, :], u['A'][:, 0, :], u['B'][:, 0, :], ALU.subtract, [u['Ak'], u['Bk']], [u['Ak']])
                        yield
                        for u in U:
                            tt('dve', u['A'][:, 1, :], u['A'][:, 1, :], u['B'][:, 1, :], ALU.add, [u['Ak'], u['Bk']], [u['Ak']])
                        yield
                        for ri in range(2):
                            for u in U:
                                q = u['q']
                                S.op('dve', lambda e, u=u, ri=ri, q=q: e.tensor_tensor_scan(
                                    out=u['C'][:, ri, :], data0=ones[:], data1=u['A'][:, ri, :], initial=sst[:, q, ri:ri + 1],
                                    op0=ALU.mult, op1=ALU.add), reads=[u['Ak'], 's5ones', 'sst%d' % q, 'sst'], writes=[u['Ck']])
                            yield
                        for u in U:
                            tt('pool', u['A'][:], u['C'][:], PT[:, u['q'], 0:1, :].broadcast_to([128, 2, L]), ALU.mult,
                               [u['Ck'], 's5tab', u['Ak']], [u['Ak']])
                        yield
                        for u in U:
                            tt('pool', u['B'][:], u['C'][:, ::-1, :], PT[:, u['q'], 1:2, :].broadcast_to([128, 2, L]), ALU.mult,
                               [u['Ck'], 's5tab', u['Bk']], [u['Bk']])
                        yield
                        for u in U:
                            tt('pool', u['A'][:, 0, :], u['A'][:, 0, :], u['B'][:, 0, :], ALU.subtract, [u['Ak'], u['Bk']], [u['Ak']])
                        yield
                        for u in U:
                            tt('pool', u['A'][:, 1, :], u['A'][:, 1, :], u['B'][:, 1, :], ALU.add, [u['Ak'], u['Bk']], [u['Ak']])
                        yield
                        for u in U:
                            cp('pool', sst[:, u['q'], :], u['A'][:, :, L - 1], [u['Ak']], ['sst%d' % u['q']])
                        yield
                        for u in U:
                            hsrc = u['A'][:, :, ::-1] if rev else u['A'][:]
                            cp('act', u['H'][:], hsrc, [u['Ak']], [u['Hk']])
                        yield
                        pyr = py[gi % 2][:, 0:L]
                        pyk = 's5py%d' % (gi % 2)
                        for k_, u in enumerate(U):
                            for ri in range(2):
                                mm(pyr, ctb[:, u['i'], ri, :], u['H'][:, ri, :], ['ctb', u['Hk']], [pyk],
                                   start=(k_ == 0 and ri == 0), stop=(k_ == 3 and ri == 1))
                        yield
                        yield
                        yield
                        tt('dve', yacc[:, j, n0:n0 + L], yacc[:, j, n0:n0 + L], pyr, ALU.add, [pyk, 'yacc'], ['yacc'])

                    glist = [(step, d, j) for step in range(NTL) for d in range(2) for j in range(2)]
                    run_pipelined((s5group(gi, *g) for gi, g in enumerate(glist)), 11)
                    S.barrier()
                for j in range(2):
                    stt(yacc[:, j, :], ub[:, j, :], pv('s5d', j), yacc[:, j, :], ALU.mult, ALU.add, ['s5u', 'yacc', 'pvt'],
                        ['yacc'])
                dbg_dump('ya%d' % l, yacc[:], [128, 2, NT], ['yacc'])
                with contextlib.ExitStack() as st2:
                    t1 = [sb(st2, "s5g1_%d" % i, [128, 512], F32) for i in range(2)]
                    t2 = [sb(st2, "s5g2_%d" % i, [128, 512], BF16) for i in range(2)]
                    pg = [ps(st2, "s5pg%d" % i, [128, 512], F32) for i in range(2)]
                    cnt = 0
                    for (n0, nn) in BLOCKS:
                        for j in range(2):
                            a, ak = t1[cnt % 2], 's5g1_%d' % (cnt % 2)
                            cnt += 1
                            ysl = yacc[:, j, n0:n0 + nn]
                            act(a[:, 0:nn], ysl, AF.Square, ['yacc'], [ak])
                            ts('dve', a[:, 0:nn], a[:, 0:nn], 0.044715, 1.0, ALU.mult, ALU.add, [ak], [ak])
                            tt('dve', a[:, 0:nn], a[:, 0:nn], ysl, ALU.mult, [ak, 'yacc'], [ak])
                            act(a[:, 0:nn], a[:, 0:nn], AF.Sigmoid, [ak], [ak], scale=1.5957691216057308)
                            tt('dve', ub[:, j, n0:n0 + nn], a[:, 0:nn], ysl, ALU.mult, [ak, 'yacc'], ['s5u'])
                    cnt = 0
                    for (n0, nn) in BLOCKS:
                        for m in range(2):
                            p_, pk_ = pg[cnt % 2], 's5pg%d' % (cnt % 2)
                            b_, bk_ = t2[cnt % 2], 's5g2_%d' % (cnt % 2)
                            cnt += 1
                            for jc in range(2):
                                mm(p_[:, 0:nn], glub[:, jc, m * 128:(m + 1) * 128], ub[:, jc, n0:n0 + nn], ['glub', 's5u'], [pk_],
                                   start=(jc == 0), stop=(jc == 1))
                            act(b_[:, 0:nn], p_[:, 0:nn], AF.Sigmoid, [pk_, 'pvt'], [bk_], bias=pv('glub', m))
                            tt('dve', b_[:, 0:nn], b_[:, 0:nn], ub[:, m, n0:n0 + nn], ALU.mult, [bk_, 's5u'], [bk_])
                            tt('pool', Y[:, 0, m, n0:n0 + nn], b_[:, 0:nn], zs[:, m, n0:n0 + nn], ALU.mult, [bk_, 's5z'], ['Y0'])
                    S.barrier()
                S.barrier()
        PHASES['s5'] = phase_s5
        def phase_hg(l, h_src, last):
            with contextlib.ExitStack() as st:
                QP = [sb(st, "hgQP%d" % d, [128, 2, NT], BF16) for d in range(2)]
                KP = [sb(st, "hgKP%d" % d, [128, 2, NT], BF16) for d in range(2)]
                G = sb(st, "hgG", [128, 2, 72, 2], F32)
                VT = sb(st, "hgVT", [128, NTL, 256], BF16)
                zs = sb(st, "hgzs", [128, 2, NT], BF16)
                lbt = sb(st, "hglbt", [128, 2, 4], F32)
                if l == 0:
                    memset('pool', lbt[:, 0, :], 0.0, ['hglbt'])
                    memset('pool', lbt[:, 1, :], 1.0, ['hglbt'])
                else:
                    o_, _ = PV['hglb']
                    tt('dve', lbt[:, 0, :], pvt[:, o_ + 4:o_ + 8], pvt[:, o_:o_ + 4], ALU.subtract, ['pvt'], ['hglbt'])
                    act(lbt[:, 0, :], lbt[:, 0, :], AF.Sigmoid, ['hglbt'], ['hglbt'])
                    ts('dve', lbt[:, 1, :], lbt[:, 0, :], -1.0, 1.0, ALU.mult, ALU.add, ['hglbt'], ['hglbt'])
                with contextlib.ExitStack() as st2:
                    wh = sb(st2, "hgw", [128, 8, 1280], BF16)
                    S.dma('pool', wh[:, :, 0:768], dr['w_in'][l][:, :, 512:1280], writes=['hgw'])
                    S.dma('pool', wh[:, :, 768:1280], dr['w_in'][l][:, :, 1280:1792], writes=['hgw'])
                    brow = sb(st2, "hgbrow", [128, 256], F32)
                    S.dma('sp', brow[:], dr['rows'][l][:, 4096:4352], writes=['hgbrow'])
                    R32 = sb(st2, "hgR32", [128, 512], F32)
                    memset('pool', R32[:], 1.0, ['hgR32'])
                    memset('pool', R32[:, 0:512:32], 0.0, ['hgR32'])
                    QS = sb(st2, "hgQS", [128, 2, 512], BF16)
                    T = [[sb(st2, "hgT%d_%d" % (i, k), [128, 512], F32) for k in range(4)] for i in range(1)]
                    pp = [ps(st2, "hgpp%d" % i, [128, 512], F32) for i in range(3)]
                    pt = [ps(st2, "hgpt%d" % i, [128, 512], F32) for i in range(2)]
                    cnt = 0
                    ic = 0
                    for (n0, nn) in BLOCKS:
                        ukeys = uTk[n0 // 128:(n0 + nn) // 128]
                        for m in (0, 1, 8, 9, 2, 3, 4, 5):
                            p_, pk_ = pp[cnt % 3], 'hgpp%d' % (cnt % 3)
                            cnt += 1
                            for jj in range(8):
                                mm(p_[:, 0:nn], wh[:, jj, m * 128:(m + 1) * 128], uT[:, jj, n0:n0 + nn], ['hgw'] + ukeys, [pk_],
                                   start=(jj == 0), stop=(jj == 7))
                            bias = pv('bin', 4 + m)
                            if m < 2:
                                act(QS[:, m, 0:nn], p_[:, 0:nn], AF.Silu, [pk_, 'pvt'], ['hgQS'], bias=bias)
                            elif m >= 8:
                                act(zs[:, m - 8, n0:n0 + nn], p_[:, 0:nn], AF.Silu, [pk_, 'pvt'], ['hgzs'], bias=bias)
                            else:
                                d, j = (m - 2) // 2, (m - 2) % 2
                                Ts = T[0]
                                Tk = ['hgT%d_%d' % (0, k) for k in range(4)]
                                ic += 1
                                t1, t2, t3, t4 = [x[:, 0:nn] for x in Ts]
                                act(t1, p_[:, 0:nn], AF.Sigmoid, [pk_, 'pvt'], [Tk[0]], bias=bias)
                                ts('dve', t1, t1, lbt[:, 1, d * 2 + j:d * 2 + j + 1], lbt[:, 0, d * 2 + j:d * 2 + j + 1], ALU.mult, ALU.add,
                                   [Tk[0], 'hglbt'], [Tk[0]])
                                act(t2, t1, AF.Ln, [Tk[0]], [Tk[1]])
                                if d == 0:
                                    S.op('dve', lambda e: e.tensor_tensor_scan(out=t3, data0=R32[:, 0:nn], data1=t2, initial=0.0,
                                                                               op0=ALU.mult, op1=ALU.add),
                                         reads=[Tk[1], 'hgR32'], writes=[Tk[2]])
                                else:
                                    S.op('dve', lambda e: e.tensor_tensor_scan(out=t3[:, ::-1],
                                                                               data0=R32[:, 0:nn], data1=t2[:, ::-1], initial=0.0,
                                                                               op0=ALU.mult, op1=ALU.add),
                                         reads=[Tk[1], 'hgR32'], writes=[Tk[2]])
                                ts('dve', t3, t3, -80.0, None, ALU.max, None, [Tk[2]], [Tk[2]])
                                ts('dve', t1, t1, -1.0, 1.0, ALU.mult, ALU.add, [Tk[0]], [Tk[0]])
                                act(t4, t3, AF.Exp, [Tk[2]], [Tk[3]])
                                act(t2, t3, AF.Exp, [Tk[2]], [Tk[1]], scale=-1.0)
                                tt('pool', KP[d][:, j, n0:n0 + nn], t1, t2, ALU.mult, [Tk[0], Tk[1]], ['hgKP%d' % d])
                                tt('pool', QP[d][:, j, n0:n0 + nn], QS[:, j, 0:nn], t4, ALU.mult, ['hgQS', Tk[3]], ['hgQP%d' % d])
                                c0 = n0 // 32
                                gsrc = t4[:, 31::32] if d == 0 else t4[:, 0::32]
                                cp('act', G[:, d, c0:c0 + nn // 32, j], gsrc, [Tk[3]], ['hgG'])
                    for t in range(NTL):
                        p_, pk_ = pt[t % 2], 'hgpt%d' % (t % 2)
                        for jj in range(8):
                            mm(p_[:, 0:256], uT[:, jj, t * 128:(t + 1) * 128], wh[:, jj, 768:1024], ['hgw', uTk[t]], [pk_],
                               start=(jj == 0), stop=(jj == 7))
                        tt('dve', VT[:, t, :], p_[:, 0:256], brow[:], ALU.add, [pk_, 'hgbrow'], ['hgVT'])
                    S.barrier()
                Sall = [sb(st, "hgSall%d" % d, [128, 2, 72, 64], BF16) for d in range(2)]
                with contextlib.ExitStack() as st2:
                    Sst = [sb(st2, "hgS%d" % d, [128, 2, 64], F32) for d in range(2)]
                    kTm = [sb(st2, "hgkTm%d" % i, [128, 4, 256], BF16) for i in range(3)]
                    Ug = [sb(st2, "hgUg%d" % i, [128, 4, 2, 64], F32) for i in range(3)]
                    ptr = [ps(st2, "hgptr%d" % i, [128, 8, 128], BF16) for i in range(2)]
                    pU = [ps(st2, "hgpU%d" % i, [128, 4, 2, 64], F32) for i in range(3)]
                    orders = [list(range(NTL)), [1, 0] + list(range(NTL - 1, 1, -1))]
                    for d in range(2):
                        memset('pool', Sst[d][:], 0.0, ['hgS%d' % d])
                    def hgchain(it, step, d):
                        t = orders[d][step]
                        pr, prk = ptr[it % 2], 'hgptr%d' % (it % 2)
                        km, kmk = kTm[it % 3], 'hgkTm%d' % (it % 3)
                        pu, puk = pU[it % 3], 'hgpU%d' % (it % 3)
                        ug, ugk = Ug[it % 3], 'hgUg%d' % (it % 3)
                        for j in range(2):
                            tr(pr[:, j, :], KP[d][:, j, t * 128:(t + 1) * 128], identb, ['hgKP%d' % d, 'cstb'], [prk])
                        yield
                        for cc in range(4):
                            prf = pr[:, 0:2, :].rearrange("p a b -> p (a b)")
                            if cc % 2 == 0:
                                ts('dve', km[:, cc, :], prf, cstf[:, 4, 64 + cc:64 + cc + 1], None, ALU.mult, None, [prk, 'cstf'], [kmk])
                            else:
                                act(km[:, cc, :], prf, AF.Identity, [prk, 'cstf'], [kmk], scale=cstf[:, 4, 64 + cc:64 + cc + 1])
                        yield
                        for cc in range(4):
                            for h in range(4):
                                hp = (h % 2) * 64
                                mm(pu[hp:hp + 64, cc, h // 2, :], km[:, cc, h * 64:(h + 1) * 64], VT[:, t, h * 64:(h + 1) * 64],
                                   [kmk, 'hgVT'], [puk])
                        yield
                        tt('dve', ug[:], pu[:], G[:, d, t * 4:(t + 1) * 4, :].unsqueeze(3).broadcast_to([128, 4, 2, 64]), ALU.mult,
                           [puk, 'hgG'], [ugk])
                        yield
                        ccs = range(4) if d == 0 else range(3, -1, -1)
                        for cc in ccs:
                            c = t * 4 + cc
                            cp('act', Sall[d][:, :, c, :], Sst[d][:], ['hgS%d' % d], ['hgSall%d_%d' % (d, t)])
                            for j in range(2):
                                stt(Sst[d][:, j, :], Sst[d][:, j, :], G[:, d, c, j:j + 1], ug[:, cc, j, :], ALU.mult, ALU.add,
                                    ['hgS%d' % d, 'hgG', ugk], ['hgS%d' % d])
                            yield

                    run_pipelined((hgchain(i_, sd[0], sd[1]) for i_, sd in enumerate([(s_, d_) for s_ in range(NTL) for d_ in range(2)])), 3)
                    S.barrier()
                with contextlib.ExitStack() as st2:
                    if ('yb%d' % l) in debug:
                        dbgbuf = sb(st2, "dbgbuf", [128, 2, NT], F32)
                    AT = [[sb(st2, "hgAT%d_%d" % (i, d), [128, 4, 128], BF16) for d in range(2)] for i in range(2)]
                    sq = [sb(st2, "hgsq%d" % i, [128, 2, 128], BF16) for i in range(2)]
                    rr = [sb(st2, "hgrr%d" % i, [128, 2, 128], F32) for i in range(2)]
                    ob = [sb(st2, "hgob%d" % i, [128, 2, 128], F32) for i in range(2)]
                    bank = mkbanks(st2, 8, "hgbk")

                    def hgout(t):
                        i2 = t % 2
                        tsl = slice(t * 128, (t + 1) * 128)
                        pas = {}
                        for d in range(2):
                            for par in range(2):
                                pas[(d, par)] = bank()
                            for h in range(4):
                                hp = (h % 2) * 64
                                pa, pak = pas[(d, h % 2)]
                                pav = pa[:, 0:256].rearrange("p (a b) -> p a b", a=2)
                                mm(pav[:, h // 2, :], KP[d][hp:hp + 64, h // 2, tsl], QP[d][hp:hp + 64, h // 2, tsl],
                                   ['hgKP%d' % d, 'hgQP%d' % d], [pak])
                        yield
                        for d in range(2):
                            for par in range(2):
                                pa, pak = pas[(d, par)]
                                pav = pa[:, 0:256].rearrange("p (a b) -> p a b", a=2)
                                tt('dve', AT[i2][d][:, par::2, :], pav, maskb[:, d, :].unsqueeze(1).broadcast_to([128, 2, 128]), ALU.mult,
                                   [pak, 'maskb'], ['hgAT%d_%d' % (i2, d)])
                        yield
                        pos = [bank() for _ in range(2)]
                        povs = [pos[par][0][:, 0:256].rearrange("p (a b) -> p a b", a=2) for par in range(2)]
                        for h in range(4):
                            hp = (h % 2) * 64
                            pok = pos[h % 2][1]
                            reg = povs[h % 2][hp:hp + 64, h // 2, :]
                            first = True
                            for d in range(2):
                                mm(reg, VT[:, t, h * 64:(h + 1) * 64], AT[i2][d][:, h, :], ['hgVT', 'hgAT%d_%d' % (i2, d)], [pok],
                                   start=first, stop=False)
                                first = False
                                for cc in range(4):
                                    c = t * 4 + cc
                                    mm(reg[:, cc * 32:(cc + 1) * 32], Sall[d][hp:hp + 64, h // 2, c, :],
                                       QP[d][hp:hp + 64, h // 2, t * 128 + cc * 32:t * 128 + (cc + 1) * 32],
                                       ['hgSall%d_%d' % (d, t), 'hgQP%d' % d], [pok], start=False, stop=(d == 1 and cc == 3))
                        yield
                        obk = 'hgob%d' % i2
                        cp('act', ob[i2][0:64], povs[0][0:64], [pos[0][1]], [obk])
                        cp('dve', ob[i2][64:128], povs[1][64:128], [pos[1][1]], [obk])
                        yield
                        pov = ob[i2][:]
                        pok = obk
                        if ('yb%d' % l) in debug:
                            cp('pool', dbgbuf[:, :, tsl], pov, [pok], ['dbgbuf'])
                        act(sq[i2][:], pov, AF.Square, [pok], ['hgsq%d' % i2])
                        yield
                        pss_, psk = bank()
                        psv = pss_[:, 0:256].rearrange("p (a b) -> p a b", a=2)
                        for j in range(2):
                            mm(psv[:, j, :], bonesb, sq[i2][:, j, :], ['cstb', 'hgsq%d' % i2], [psk])
                        yield
                        act(rr[i2][:], psv, AF.Sqrt, [psk], ['hgrr%d' % i2], bias=RMS_EPS, scale=1.0 / 64)
                        yield
                        S.op('dve', lambda e: e.reciprocal(out=rr[i2][:], in_=rr[i2][:]), reads=['hgrr%d' % i2], writes=['hgrr%d' % i2])
                        yield
                        tt('dve', rr[i2][:], pov, rr[i2][:], ALU.mult, [pok, 'hgrr%d' % i2], ['hgrr%d' % i2])
                        yield
                        for j in range(2):
                            stt(Y[:, 1, j, tsl], rr[i2][:, j, :], pv('hgnw', j), zs[:, j, tsl], ALU.mult, ALU.mult,
                                ['hgrr%d' % i2, 'pvt', 'hgzs'], ['Y1'])

                    run_pipelined((hgout(t) for t in range(NTL) if not (last and t < 2 and not debug)), 5)
                    if ('yb%d' % l) in debug:
                        dbg_dump('yb%d' % l, dbgbuf[:], [128, 2, NT], ['dbgbuf'])
                    S.barrier()
                S.barrier()
        PHASES['hg'] = phase_hg
        def phase_ret(l, h_src, last):
            with contextlib.ExitStack() as st:
                QR = sb(st, "rtQR", [128, 2, NT], BF16)
                KR = sb(st, "rtKR", [128, 2, NT], BF16)
                VT = sb(st, "rtVT", [128, NTL, 256], BF16)
                zs = sb(st, "rtzs", [128, 2, NT], BF16)
                Sall = [sb(st, "rtSall%d" % d, [128, 2, NTL, 64], BF16) for d in range(2)]
                LG = sb(st, "rtLG", [128, 4], F32)
                GL = sb(st, "rtGL", [128, 4], F32)
                LGH = sb(st, "rtLGH", [128, 8], F32)
                QDEC = sb(st, "rtQDEC", [128, 2, 2, 128], F32)
                KDEC = sb(st, "rtKDEC", [128, 2, 4], F32)
                DS = sb(st, "rtDS", [128, 4, 128], F32)
                tb8 = sb(st, "rtb8", [128, 2], F32)
                K_ = 'rttab'
                act(LG[:], pv('rdec'), AF.Exp, ['pvt'], [K_])
                ts('dve', LG[:], LG[:], -1.0, None, ALU.mult, None, [K_], [K_])
                act(GL[:], LG[:], AF.Exp, [K_], [K_], scale=128.0)
                act(LGH[:], pv('rdech'), AF.Exp, ['pvt'], [K_])
                ts('dve', LGH[:], LGH[:], -1.0, None, ALU.mult, None, [K_], [K_])
                for d in range(2):
                    for j in range(2):
                        act(QDEC[:, d, j, :], cstf[:, 5 + d, :], AF.Exp, ['cstf', K_], [K_], scale=LG[:, d * 2 + j:d * 2 + j + 1])
                    act(KDEC[:, d, :], LGH[:, d * 4:(d + 1) * 4], AF.Exp, ['cstf', K_], [K_], scale=cstf[:, 4, 68 + d:69 + d])
                with contextlib.ExitStack() as st2:
                    ta = sb(st2, "rtta", [128, 128], F32)
                    tb = sb(st2, "rttb", [128, 128], F32)
                    for h in range(4):
                        act(ta[:], cstf[:, 0, :], AF.Exp, ['cstf', K_], ['rtta'], scale=LGH[:, h:h + 1])
                        tt('dve', ta[:], ta[:], cstf[:, 2, :], ALU.mult, ['rtta', 'cstf'], ['rtta'])
                        act(tb[:], cstf[:, 1, :], AF.Exp, ['cstf', K_], ['rttb'], scale=LGH[:, 4 + h:5 + h])
                        tt('dve', tb[:], tb[:], cstf[:, 3, :], ALU.mult, ['rttb', 'cstf'], ['rttb'])
                        tt('dve', DS[:, h, :], ta[:], tb[:], ALU.add, ['rtta', 'rttb'], [K_])
                    ts('dve', tb8[:], pv('bin', 16, 2), 0.125, None, ALU.mult, None, ['pvt'], [K_])
                    S.barrier()
                if stop == 'ret_tab':
                    return
                with contextlib.ExitStack() as st2:
                    wr = sb(st2, "rtw", [128, 8, 1024], BF16)
                    S.dma('pool', wr[:], dr['w_in'][l][:, :, 1792:2816], writes=['rtw'])
                    brow = sb(st2, "rtbrow", [128, 256], F32)
                    S.dma('sp', brow[:], dr['rows'][l][:, 4352:4608], writes=['rtbrow'])
                    COS = sb(st2, "rtcos", [128, 2048], F32)
                    SIN = sb(st2, "rtsin", [128, 2048], F32)
                    permf = sb(st2, "rtperm", [128, 128], F32)
                    S.dma('sp', COS[:], dr['rcos'], writes=['rtcos'])
                    S.dma('act', SIN[:], dr['rsin'], writes=['rtsin'])
                    S.dma('sp', permf[:], dr['cst'][:, 0, :], writes=['rtperm'])
                    qf = [sb(st2, "rtqf%d" % i, [128, 512], F32) for i in range(2)]
                    t1 = [sb(st2, "rtt1_%d" % i, [128, 512], F32) for i in range(2)]
                    pp = [ps(st2, "rtpp%d" % i, [128, 512], F32) for i in range(2)]
                    pq = [ps(st2, "rtpq%d" % i, [128, 512], F32) for i in range(2)]
                    pt = [ps(st2, "rtpt%d" % i, [128, 512], F32) for i in range(2)]
                    cnt = 0
                    rc = 0
                    for (n0, nn) in BLOCKS:
                        ukeys = uTk[n0 // 128:(n0 + nn) // 128]
                        for m in (0, 1, 2, 3, 6, 7):
                            p_, pk_ = pp[cnt % 2], 'rtpp%d' % (cnt % 2)
                            cnt += 1
                            for jj in range(8):
                                mm(p_[:, 0:nn], wr[:, jj, m * 128:(m + 1) * 128], uT[:, jj, n0:n0 + nn], ['rtw'] + ukeys, [pk_],
                                   start=(jj == 0), stop=(jj == 7))
                            if m >= 6:
                                act(zs[:, m - 6, n0:n0 + nn], p_[:, 0:nn], AF.Silu, [pk_, 'pvt'], ['rtzs'], bias=pv('bin', 14 + m))
                                continue
                            isk = m >= 2
                            j = m % 2
                            dst = (KR if isk else QR)[:, j, n0:n0 + nn]
                            dk = 'rtKR' if isk else 'rtQR'
                            if n0 < 256:
                                if isk:
                                    act(dst, p_[:, 0:nn], AF.Identity, [pk_, K_], [dk], bias=tb8[:, j:j + 1], scale=0.125)
                                else:
                                    act(dst, p_[:, 0:nn], AF.Identity, [pk_, 'pvt'], [dk], bias=pv('bin', 14 + m))
                                continue
                            q_, qk_ = qf[rc % 2], 'rtqf%d' % (rc % 2)
                            a_, ak_ = t1[rc % 2], 'rtt1_%d' % (rc % 2)
                            r_, rk_ = pq[rc % 2], 'rtpq%d' % (rc % 2)
                            rc += 1
                            if isk:
                                act(q_[:, 0:nn], p_[:, 0:nn], AF.Identity, [pk_, K_], [qk_], bias=tb8[:, j:j + 1], scale=0.125)
                            else:
                                act(q_[:, 0:nn], p_[:, 0:nn], AF.Identity, [pk_, 'pvt'], [qk_], bias=pv('bin', 14 + m))
                            mm(r_[:, 0:nn], permf[:], q_[:, 0:nn], ['rtperm', qk_], [rk_])
                            tsl = slice(n0 - 256, n0 - 256 + nn)
                            tt('dve', a_[:, 0:nn], r_[:, 0:nn], SIN[:, tsl], ALU.mult, [rk_, 'rtsin'], [ak_])
                            tt('pool', q_[:, 0:nn], q_[:, 0:nn], COS[:, tsl], ALU.mult, [qk_, 'rtcos'], [qk_])
                            tt('dve', dst, a_[:, 0:nn], q_[:, 0:nn], ALU.add, [ak_, qk_], [dk])
                    for t in range(NTL):
                        p_, pk_ = pt[t % 2], 'rtpt%d' % (t % 2)
                        for jj in range(8):
                            mm(p_[:, 0:256], uT[:, jj, t * 128:(t + 1) * 128], wr[:, jj, 512:768], ['rtw', uTk[t]], [pk_],
                               start=(jj == 0), stop=(jj == 7))
                        tt('dve', VT[:, t, :], p_[:, 0:256], brow[:], ALU.add, [pk_, 'rtbrow'], ['rtVT'])
                    S.barrier()
                if stop == 'ret_proj':
                    return
                with contextlib.ExitStack() as st2:
                    Sst = [sb(st2, "rtS%d" % d, [128, 2, 64], F32) for d in range(2)]
                    kT = [sb(st2, "rtkT%d" % i, [128, 256], BF16) for i in range(3)]
                    ptr = [ps(st2, "rtptr%d" % i, [128, 8, 128], BF16) for i in range(2)]
                    pU = [ps(st2, "rtpU%d" % i, [128, 512], F32) for i in range(3)]
                    orders = [list(range(NTL)), [1, 0] + list(range(NTL - 1, 1, -1))]
                    for d in range(2):
                        memset('pool', Sst[d][:], 0.0, ['rtS%d' % d])
                    def rtchain(it, step, d):
                        t = orders[d][step]
                        pr, prk = ptr[it % 2], 'rtptr%d' % (it % 2)
                        kt, ktk = kT[it % 3], 'rtkT%d' % (it % 3)
                        pu, puk = pU[it % 3], 'rtpU%d' % (it % 3)
                        puv = pu[:, 0:128].rearrange("p (a b) -> p a b", a=2)
                        for j in range(2):
                            tr(pr[:, j, :], KR[:, j, t * 128:(t + 1) * 128], identb, ['rtKR', 'cstb'], [prk])
                        yield
                        tt('dve', kt[:].rearrange("p (h k) -> p h k", h=4), pr[:, 0:2, :].rearrange("p a (b k) -> p (a b) k", b=2),
                           KDEC[:, d, :].unsqueeze(2).broadcast_to([128, 4, 64]), ALU.mult, [prk, K_], [ktk])
                        yield
                        for h in range(4):
                            hp = (h % 2) * 64
                            mm(puv[hp:hp + 64, h // 2, :], kt[:, h * 64:(h + 1) * 64], VT[:, t, h * 64:(h + 1) * 64], [ktk, 'rtVT'], [puk])
                        yield
                        cp('act', Sall[d][:, :, t, :], Sst[d][:], ['rtS%d' % d], ['rtSall%d_%d' % (d, t)])
                        for j in range(2):
                            stt(Sst[d][:, j, :], Sst[d][:, j, :], GL[:, d * 2 + j:d * 2 + j + 1], puv[:, j, :], ALU.mult, ALU.add,
                                ['rtS%d' % d, K_, puk], ['rtS%d' % d])

                    run_pipelined((rtchain(i_, sd[0], sd[1]) for i_, sd in enumerate([(s_, d_) for s_ in range(NTL) for d_ in range(2)])), 2)
                    S.barrier()
                if stop == 'ret_chain':
                    return
                with contextlib.ExitStack() as st2:
                    if ('yc%d' % l) in debug:
                        dbgbuf = sb(st2, "dbgbuf", [128, 2, NT], F32)
                    AT = [sb(st2, "rtAT%d" % i, [128, 4, 128], BF16) for i in range(2)]
                    qd = [[sb(st2, "rtqd%d_%d" % (i, d), [128, 2, 128], BF16) for d in range(2)] for i in range(2)]
                    sq = [sb(st2, "rtsq%d" % i, [128, 2, 128], BF16) for i in range(2)]
                    rr = [sb(st2, "rtrr%d" % i, [128, 2, 128], F32) for i in range(2)]
                    ob = [sb(st2, "rtob%d" % i, [128, 2, 128], F32) for i in range(2)]
                    bank = mkbanks(st2, 8, "rtbk")

                    def rtout(t):
                        i2 = t % 2
                        tsl = slice(t * 128, (t + 1) * 128)
                        pas = [bank() for _ in range(2)]
                        for h in range(4):
                            hp = (h % 2) * 64
                            pav = pas[h % 2][0][:, 0:256].rearrange("p (a b) -> p a b", a=2)
                            mm(pav[:, h // 2, :], KR[hp:hp + 64, h // 2, tsl], QR[hp:hp + 64, h // 2, tsl], ['rtKR', 'rtQR'], [pas[h % 2][1]])
                        for d in range(2):
                            tt('pool', qd[i2][d][:], QR[:, :, tsl], QDEC[:, d, :, :], ALU.mult, ['rtQR', K_], ['rtqd%d_%d' % (i2, d)])
                        yield
                        for par in range(2):
                            pav = pas[par][0][:, 0:256].rearrange("p (a b) -> p a b", a=2)
                            tt('dve', AT[i2][:, par::2, :], pav, DS[:, par::2, :], ALU.mult, [pas[par][1], K_], ['rtAT%d' % i2])
                        yield
                        pos = [bank() for _ in range(2)]
                        povs = [pos[par][0][:, 0:256].rearrange("p (a b) -> p a b", a=2) for par in range(2)]
                        for h in range(4):
                            hp = (h % 2) * 64
                            pok = pos[h % 2][1]
                            reg = povs[h % 2][hp:hp + 64, h // 2, :]
                            mm(reg, VT[:, t, h * 64:(h + 1) * 64], AT[i2][:, h, :], ['rtVT', 'rtAT%d' % i2], [pok], start=True, stop=False)
                            for d in range(2):
                                mm(reg, Sall[d][hp:hp + 64, h // 2, t, :], qd[i2][d][hp:hp + 64, h // 2, :],
                                   ['rtSall%d_%d' % (d, t), 'rtqd%d_%d' % (i2, d)], [pok], start=False, stop=(d == 1))
                        yield
                        obk = 'rtob%d' % i2
                        cp('act', ob[i2][0:64], povs[0][0:64], [pos[0][1]], [obk])
                        cp('dve', ob[i2][64:128], povs[1][64:128], [pos[1][1]], [obk])
                        yield
                        pov = ob[i2][:]
                        pok = obk
                        if ('yc%d' % l) in debug:
                            cp('pool', dbgbuf[:, :, tsl], pov, [pok], ['dbgbuf'])
                        act(sq[i2][:], pov, AF.Square, [pok], ['rtsq%d' % i2])
                        yield
                        pss_, psk = bank()
                        psv = pss_[:, 0:256].rearrange("p (a b) -> p a b", a=2)
                        for j in range(2):
                            mm(psv[:, j, :], bonesb, sq[i2][:, j, :], ['cstb', 'rtsq%d' % i2], [psk])
                        yield
                        act(rr[i2][:], psv, AF.Sqrt, [psk], ['rtrr%d' % i2], bias=RMS_EPS, scale=1.0 / 64)
                        yield
                        S.op('dve', lambda e: e.reciprocal(out=rr[i2][:], in_=rr[i2][:]), reads=['rtrr%d' % i2], writes=['rtrr%d' % i2])
                        yield
                        tt('dve', rr[i2][:], pov, rr[i2][:], ALU.mult, [pok, 'rtrr%d' % i2], ['rtrr%d' % i2])
                        yield
                        tt('pool', Y[:, 2, :, tsl], rr[i2][:], zs[:, :, tsl], ALU.mult, ['rtrr%d' % i2, 'rtzs'], ['Y2'])

                    run_pipelined((rtout(t) for t in range(NTL) if not (last and t < 2 and not debug)), 5)
                    if ('yc%d' % l) in debug:
                        dbg_dump('yc%d' % l, dbgbuf[:], [128, 2, NT], ['dbgbuf'])
                    S.barrier()
                S.barrier()
        PHASES['ret'] = phase_ret
        def phase_rw(l, h_src, last):
            with contextlib.ExitStack() as st:
                RB = sb(st, "rwRB", [128, 2, NT], BF16)
                KB = sb(st, "rwKB", [128, 2, NT], BF16)
                VB = sb(st, "rwVB", [128, 2, NT], BF16)
                LB = sb(st, "rwLB", [128, NT], BF16)
                zs = sb(st, "rwzs", [128, 2, NT], BF16)
                vT = sb(st, "rwvT", [128, NTL, 256], BF16)
                lw2b = sb(st, "rwlw2", [128, 2, 256], BF16)
                S.dma('pool', lw2b[:], dr['lw2'][l], writes=['rwlw2'])
                oka = sb(st, "rwoka", [128, 2], F32)
                ts('dve', oka[:], pv('ka'), -1.0, 1.0, ALU.mult, ALU.add, ['pvt'], ['rwoka'])
                seen_b, seen_o = set(), set()
                with contextlib.ExitStack() as st2:
                    ww = sb(st2, "rww", [128, 8, 1152], BF16)
                    S.dma('pool', ww[:, :, 0:640], dr['w_in'][l][:, :, 2816:3456], writes=['rww'])
                    S.dma('pool', ww[:, :, 640:1152], dr['w_in'][l][:, :, 3456:3968], writes=['rww'])
                    XR = sb(st2, "rwXR", [128, NT + 4], F32)
                    XS = sb(st2, "rwXS", [128, NT], F32)
                    c0 = sb(st2, "rwc0", [128, 7], F32)
                    pp = [ps(st2, "rwpp%d" % i, [128, 512], F32) for i in range(3)]
                    ptr = [ps(st2, "rwptr%d" % i, [128, 8, 128], BF16) for i in range(2)]
                    o_mu, _ = PV['mu']
                    mu0, mu1 = pvt[:, o_mu:o_mu + 7], pvt[:, o_mu + 7:o_mu + 14]
                    tt('dve', c0[:], mu0, mu1, ALU.add, ['pvt'], ['rwc0'])
                    ts('dve', c0[:], c0[:], -1.0, 1.0, ALU.mult, ALU.add, ['rwc0'], ['rwc0'])
                    memset('pool', XR[:], 0.0, ['rwXR'])
                    cnt = 0
                    for m in range(9):
                        for (n0, nn) in BLOCKS:
                            p_, pk_ = pp[cnt % 3], 'rwpp%d' % (cnt % 3)
                            cnt += 1
                            for jj in range(8):
                                mm(p_[:, 0:nn], ww[:, jj, m * 128:(m + 1) * 128], uT[:, jj, n0:n0 + nn],
                                   ['rww'] + uTk[n0 // 128:(n0 + nn) // 128], [pk_], start=(jj == 0), stop=(jj == 7))
                            if m >= 7:
                                act(zs[:, m - 7, n0:n0 + nn], p_[:, 0:nn], AF.Silu, [pk_, 'pvt'], ['rwzs'], bias=pv('bin', 22 + m))
                            else:
                                xo = 1 if n0 < 256 else 3
                                act(XR[:, n0 + xo:n0 + xo + nn], p_[:, 0:nn], AF.Identity, [pk_, 'pvt'], ['rwXR'], bias=pv('bin', 22 + m))
                        if m >= 7:
                            continue
                        for (b0, ln, o0) in ((1, 256, 0), (259, 2048, 256)):
                            ts('dve', XS[:, o0:o0 + ln], XR[:, b0:b0 + ln], c0[:, m:m + 1], None, ALU.mult, None, ['rwXR', 'rwc0'], ['rwXS'])
                            stt(XS[:, o0:o0 + ln], XR[:, b0 - 1:b0 - 1 + ln], mu0[:, m:m + 1], XS[:, o0:o0 + ln], ALU.mult, ALU.add,
                                ['rwXR', 'pvt', 'rwXS'], ['rwXS'])
                            stt(XS[:, o0:o0 + ln], XR[:, b0 + 1:b0 + 1 + ln], mu1[:, m:m + 1], XS[:, o0:o0 + ln], ALU.mult, ALU.add,
                                ['rwXR', 'pvt', 'rwXS'], ['rwXS'])
                        if m < 6:
                            dstT, dk = [(RB, 'rwRB'), (KB, 'rwKB'), (VB, 'rwVB')][m // 2]
                            cp('act', dstT[:, m % 2, :], XS[:], ['rwXS'], [dk])
                        else:
                            act(LB[0:64, :], XS[0:64, :], AF.Tanh, ['rwXS'], ['rwLB'])
                            cp('pool', LB[64:128, :], XS[64:128, :], ['rwXS'], ['rwLB'])
                    for t in range(NTL):
                        pr, prk = ptr[t % 2], 'rwptr%d' % (t % 2)
                        for j in range(2):
                            tr(pr[:, j, :], VB[:, j, t * 128:(t + 1) * 128], identb, ['rwVB', 'cstb'], [prk])
                        cp('dve' if t % 2 == 0 else 'act', vT[:, t, :], pr[:, 0:2, :].rearrange("p a b -> p (a b)"), [prk], ['rwvT'])
                    S.barrier()
                if stop == 'rw_proj':
                    return
                OS = sb(st, "rwOS", [128, 2, NT], F32)
                with contextlib.ExitStack() as st2:
                    def B(name, shape, dt=BF16):
                        return sb(st2, "rw_" + name, shape, dt), "rw_" + name
                    R64, R64k = B("R64", [128, 256], F32)
                    memset('pool', R64[:], 1.0, [R64k])
                    memset('pool', R64[:, 0:256:64], 0.0, [R64k])
                    LW, LWk = B("LW", [128, 2, 128], F32)
                    SA, SAk = B("SA", [128, 2, 128], F32)
                    LGm, LGk = B("LG", [128, 2, 128], F32)
                    EG, EGk = B("EG", [128, 2, 128], F32)
                    ENG, ENGk = B("ENG", [128, 2, 128], F32)
                    EGM, EGMk = B("EGM", [128, 2, 128], F32)
                    U0, U0k = B("U0", [128, 2, 128], F32)
                    TA, TAk = B("TA", [128, 2, 128], F32)
                    TB_, TBk = B("TB", [128, 2, 128], F32)
                    SQ, SQk = B("SQ", [128, 2, 128])
                    RKD, RKDk = B("RKD", [128, 2, 128])
                    OBt = (None, None)
                    Zst = [B("Z%d" % d, [128, 2, 64], F32) for d in range(2)]
                    BUF = [dict() for _ in range(2)]
                    for d_ in range(2):
                        BUF[d_]['KKN'] = B("KKN_%d" % d_, [128, 2, 128])
                        BUF[d_]['KT'] = B("KT_%d" % d_, [128, 3, 2, 128])
                        BUF[d_]['RT'] = B("RT_%d" % d_, [128, 2, 128])
                        for j_ in range(2):
                            sfx = "_%d_%d" % (d_, j_)
                            SB = dict()
                            SB['TM'] = B("TM" + sfx, [128, 3, 128])
                            for nm_ in ('A1T', 'A2T', 'A3T', 'A4T', 'ALT', 'Tm', 'TTm', 'Xb', 'RHS', 'BYb'):
                                SB[nm_] = B(nm_ + sfx, [128, 2, 128])
                            SB['NY'] = B("NY" + sfx, [128, 2, 64])
                            SB['RH'] = B("RH" + sfx, [128, 128])
                            SB['GTb'] = B("GTb" + sfx, [128, 2, 128])
                            SB['ZLG'] = B("ZLG" + sfx, [128, 2, 64], F32)
                            SB['Z0b'] = B("Z0b" + sfx, [128, 2, 64])
                            BUF[d_][j_] = SB
                        BUF[d_]['GLt'] = B("GLt_%d" % d_, [128, 2, 2], F32)
                    banks = [ps(st2, "rwbank%d" % i, [128, 512], F32) for i in range(8)]
                    bcnt = [0]

                    def bank():
                        i = bcnt[0] % 8
                        bcnt[0] += 1
                        return banks[i], 'rwbank%d' % i
                    for d in range(2):
                        memset('pool', Zst[d][0][:], 0.0, [Zst[d][1], 'rw_Zs_%d_0' % d, 'rw_Zs_%d_1' % d])
                    for d_ in range(2):
                        for j_ in range(2):
                            memset('pool', BUF[d_][j_]['GTb'][0][:], 0.0, [BUF[d_][j_]['GTb'][1]])
                    orders = [list(range(NTL)), [1, 0] + list(range(NTL - 1, 1, -1))]
                    bc3 = lambda ap: ap.unsqueeze(2).broadcast_to([128, 2, 128])
                    def unit(d, t):
                        KKN, KKNk = BUF[d]['KKN']
                        KT, KTk = BUF[d]['KT']
                        RTb, RTk = BUF[d]['RT']
                        GLt, GLk = BUF[d]['GLt']
                        tsl = slice(t * 128, (t + 1) * 128)
                        rev = (d == 1)
                        Z, Zk = Zst[d]
                        plw, plwk = bank()
                        pla, plak = bank()
                        plwv = plw[:, 0:256].rearrange("p (j t) -> p j t", j=2)
                        plav = pla[:, 0:256].rearrange("p (j t) -> p j t", j=2)
                        wb_ = 32 * d
                        for j in range(2):
                            mm(plwv[:, j, :], lw2b[wb_:wb_ + 16, d, j * 128:(j + 1) * 128], LB[wb_:wb_ + 16, tsl], ['rwlw2', 'rwLB'], [plwk])
                        for j in range(2):
                            mm(plav[:, j, :], lw2b[64:96, d, j * 128:(j + 1) * 128], LB[64:96, tsl], ['rwlw2', 'rwLB'], [plak])
                        for j in range(2):
                            act(LW[:, j, :], plwv[:, j, :], AF.Sigmoid, [plwk, 'pvt'], [LWk], bias=pv('w0', d * 2 + j))
                            act(SA[:, j, :], plav[:, j, :], AF.Sigmoid, [plak, 'pvt'], [SAk], bias=pv('a0', d * 2 + j))
                        ts('dve', LW[:], LW[:], -0.6065306597126334, None, ALU.mult, None, [LWk], [LWk])
                        lwf = LW[:].rearrange("p a b -> p (a b)")
                        lgf = LGm[:].rearrange("p a b -> p (a b)")
                        if not rev:
                            S.op('dve', lambda e: e.tensor_tensor_scan(out=lgf, data0=R64[:], data1=lwf, initial=0.0, op0=ALU.mult, op1=ALU.add),
                                 reads=[LWk, R64k], writes=[LGk])
                        else:
                            S.op('dve', lambda e: e.tensor_tensor_scan(out=lgf[:, ::-1], data0=R64[:], data1=lwf[:, ::-1], initial=0.0,
                                                                       op0=ALU.mult, op1=ALU.add), reads=[LWk, R64k], writes=[LGk])
                        act(EG[:], LGm[:], AF.Exp, [LGk], [EGk])
                        act(ENG[:], LGm[:], AF.Exp, [LGk], [ENGk], scale=-1.0)
                        tt('pool', TA[:], LGm[:], LW[:], ALU.subtract, [LGk, LWk], [TAk])
                        act(EGM[:], TA[:], AF.Exp, [TAk], [EGMk])
                        gsrc = EG[:, :, 63::64] if not rev else EG[:, :, 0::64]
                        cp('pool', GLt[:], gsrc, [EGk], [GLk])
                        if stop == 'rw_u1':
                            return
                        tt('dve', TA[:], KB[:, :, tsl], bc3(pv('kk')), ALU.mult, ['rwKB', 'pvt', TAk], [TAk])
                        act(SQ[:], TA[:], AF.Square, [TAk], [SQk])
                        pss_, pssk = bank()
                        pssv = pss_[:, 0:256].rearrange("p (a b) -> p a b", a=2)
                        for j in range(2):
                            mm(pssv[:, j, :], bonesb, SQ[:, j, :], ['cstb', SQk], [pssk])
                        act(TB_[:], pssv, AF.Sqrt, [pssk], [TBk])
                        ts('dve', TB_[:], TB_[:], 1e-12, None, ALU.max, None, [TBk], [TBk])
                        S.op('dve', lambda e: e.reciprocal(out=TB_[:], in_=TB_[:]), reads=[TBk], writes=[TBk])
                        tt('dve', KKN[:], TA[:], TB_[:], ALU.mult, [TAk, TBk], [KKNk])
                        if stop == 'rw_u2':
                            return
                        tt('pool', KT[:, 0], KKN[:], EGM[:], ALU.mult, [KKNk, EGMk], [KTk])
                        tt('dve', TA[:], SA[:], ENG[:], ALU.mult, [SAk, ENGk, TAk], [TAk])
                        tt('pool', KT[:, 1], KKN[:], TA[:], ALU.mult, [KKNk, TAk], [KTk])
                        tt('dve', U0[:], SA[:], bc3(pv('ka')), ALU.mult, [SAk, 'pvt'], [U0k])
                        tt('dve', U0[:], U0[:], bc3(oka[:]), ALU.add, [U0k, 'rwoka'], [U0k])
                        tt('pool', TB_[:], U0[:], ENG[:], ALU.mult, [U0k, ENGk, TBk], [TBk])
                        tt('pool', KT[:, 2], KB[:, :, tsl], TB_[:], ALU.mult, ['rwKB', TBk], [KTk])
                        tt('dve', RTb[:], RB[:, :, tsl], EG[:], ALU.mult, ['rwRB', EGk], [RTk])
                        tt('dve', U0[:], U0[:], KB[:, :, tsl], ALU.mult, [U0k, 'rwKB'], [U0k])
                        tt('dve', U0[:], U0[:], bc3(pv('rk')), ALU.mult, [U0k, 'pvt'], [U0k])
                        tt('pool', RKD[:], U0[:], RB[:, :, tsl], ALU.mult, [U0k, 'rwRB'], [RKDk])
                        pbn, pbnk = bank()
                        pbnv = pbn[:, 0:256].rearrange("p (a b) -> p a b", a=2)
                        for j in range(2):
                            mm(pbnv[:, j, :], bonesb, RKD[:, j, :], ['cstb', RKDk], [pbnk])
                        if t not in seen_b:
                            seen_b.add(t)
                            tt('dve', Y[:, 3, :, tsl], pbnv, VB[:, :, tsl], ALU.mult, [pbnk, 'rwVB'], ['Y3'])
                        else:
                            tt('dve', TA[:], pbnv, VB[:, :, tsl], ALU.mult, [pbnk, 'rwVB', TAk], [TAk])
                            tt('pool', Y[:, 3, :, tsl], Y[:, 3, :, tsl], TA[:], ALU.add, ['Y3', TAk], ['Y3'])
                        if stop == 'rw_u3':
                            return
                        subs = [stream(d, j, t, rev, tsl, KT, KTk, RTb, RTk, GLt, GLk) for j in range(2)]
                        while subs:
                            for g in list(subs):
                                try:
                                    next(g)
                                except StopIteration:
                                    subs.remove(g)
                                yield

                    def stream(d, j, t, rev, tsl, KT, KTk, RTb, RTk, GLt, GLk):
                        SB = BUF[d][j]
                        TM, TMk = SB['TM']
                        A1T, A1k = SB['A1T']
                        A2T, A2k = SB['A2T']
                        A3T, A3k = SB['A3T']
                        A4T, A4k = SB['A4T']
                        ALT, ALk = SB['ALT']
                        Tm, Tmk = SB['Tm']
                        TTm, TTk = SB['TTm']
                        Xb, Xbk = SB['Xb']
                        RHS, RHSk = SB['RHS']
                        BYb, BYk = SB['BYb']
                        NY, NYk = SB['NY']
                        RH, RHk = SB['RH']
                        GTb, GTk = SB['GTb']
                        ZLG, ZLGk = SB['ZLG']
                        Z0b, Z0k = SB['Z0b']
                        Z, _zk = Zst[d]
                        Zk = 'rw_Zs_%d_%d' % (d, j)
                        ptb, ptbk = bank()
                        ptv = ptb[:].bitcast(BF16).rearrange("p (a b) -> p a b", a=8)
                        for x in range(3):
                            tr(ptv[:, x, :], KT[:, x, j, :], identb, [KTk, 'cstb'], [ptbk])
                        yield
                        cp('act', TM[:], ptv[:, 0:3, :], [ptbk], [TMk])
                        yield

                        def amat(dst, dstk, li, ri_src, ri_k, mslot):
                            pas = []
                            for par in range(2):
                                hp = par * 64
                                pa, pak = bank()
                                rhs = (RTb[hp:hp + 64, j, :] if ri_src is None else KT[hp:hp + 64, ri_src, j, :])
                                mm(pa[:, 0:128], KT[hp:hp + 64, li, j, :], rhs, [KTk, ri_k], [pak])
                                pas.append((pa, pak))
                            return pas

                        def aevac(pas, dst, dstk, mslot):
                            for par, (pa, pak) in enumerate(pas):
                                if mslot is None:
                                    cp('act', dst[:, par, :], pa[:, 0:128], [pak], [dstk])
                                else:
                                    tt('dve', dst[:, par, :], pa[:, 0:128], maskb[:, mslot, :], ALU.mult, [pak, 'maskb'], [dstk])
                        for (dst, dstk, li, rs, rk, ms) in ((A1T, A1k, 1, 0, KTk, None), (A2T, A2k, 2, 0, KTk, 2 + d),
                                                            (A3T, A3k, 1, None, RTk, 4 + d), (A4T, A4k, 2, None, RTk, 4 + d)):
                            pas = amat(dst, dstk, li, rs, rk, ms)
                            yield
                            aevac(pas, dst, dstk, ms)
                            yield
                        idb2 = identb.unsqueeze(1).broadcast_to([128, 2, 128])
                        cp('pool', Tm[:], idb2, ['cstb'], [Tmk])
                        cp('pool', TTm[:], idb2, ['cstb'], [TTk])
                        for lv in range(6):
                            tt('pool', ALT[:], A1T[:], maskb[:, 6 + d * 6 + lv, :].unsqueeze(1).broadcast_to([128, 2, 128]), ALU.mult,
                               [A1k, 'maskb'], [ALk])
                            yield
                            px, pxk = bank()
                            pxv = px[:, 0:256].rearrange("p (h t) -> p h t", h=2)
                            for par in range(2):
                                mm(pxv[:, par, :], ALT[:, par, :], Tm[:, par, :], [ALk, Tmk], [pxk])
                            yield
                            cp('act', Xb[:], pxv, [pxk], [Xbk])
                            yield
                            py_, pyk = bank()
                            pyv = py_[:].rearrange("p (x h t) -> p x h t", x=2, h=2)
                            for par in range(2):
                                mm(pyv[:, 0, par, :], Xb[:, par, :], TTm[:, par, :], [Xbk, TTk], [pyk])
                            if lv < 5:
                                for par in range(2):
                                    mm(pyv[:, 1, par, :], TTm[:, par, :], Xb[:, par, :], [Xbk, TTk], [pyk])
                            yield
                            if lv < 5:
                                tt('dve', Tm[:], Tm[:], pyv[:, 1], ALU.subtract, [Tmk, pyk], [Tmk])
                            tt('dve', TTm[:], TTm[:], pyv[:, 0], ALU.subtract, [TTk, pyk], [TTk])
                            yield
                        pw, pwk = bank()
                        pwv = pw[:, 0:128].rearrange("p (h v) -> p h v", h=2)
                        for par in range(2):
                            h = 2 * j + par
                            mm(pwv[:, par, :], A2T[:, par, :], vT[:, t, h * 64:(h + 1) * 64], [A2k, 'rwvT'], [pwk])
                        cp('pool', RHS[:, :, 0:64], TM[:, 0, :].rearrange("p (h k) -> p h k", h=2), [TMk], [RHSk])
                        yield
                        cp('act', RHS[:, :, 64:128], pwv, [pwk], [RHSk])
                        yield
                        pby, pbyk = bank()
                        pbyv = pby[:, 0:256].rearrange("p (h t) -> p h t", h=2)
                        for par in range(2):
                            mm(pbyv[:, par, :], TTm[:, par, :], RHS[:, par, :], [TTk, RHSk], [pbyk])
                        yield
                        cp('act', BYb[:], pbyv, [pbyk], [BYk])
                        yield
                        ts('pool', NY[:], BYb[:, :, 64:128], -1.0, 0.0, ALU.mult, ALU.add, [BYk], [NYk])
                        pr_, prk = bank()
                        for par in range(2):
                            hp = par * 64
                            mm(pr_[hp:hp + 64, 0:128], BYb[:, par, 0:64], A3T[:, par, :], [BYk, A3k], [prk])
                        yield
                        tt('dve', RH[:], RTb[:, j, :], pr_[:, 0:128], ALU.subtract, [RTk, prk], [RHk])
                        yield
                        for c in range(2):
                            cs = slice(c * 64, (c + 1) * 64)
                            pg_, pgk = bank()
                            pgv = pg_[:, 0:128].rearrange("p (x v) -> p x v", x=2)
                            for par in range(2):
                                hp = par * 64
                                h = 2 * j + par
                                hc = slice(h * 64, (h + 1) * 64)
                                pc = slice(par * 64, (par + 1) * 64)
                                mm(pgv[hp:hp + 64, 0, :], BYb[cs, par, 0:64], TM[cs, 1, pc], [BYk, TMk], [pgk])
                                mm(pgv[hp:hp + 64, 1, :], TM[cs, 2, pc], vT[cs, t, hc], [TMk, 'rwvT'], [pgk], start=True, stop=False)
                                mm(pgv[hp:hp + 64, 1, :], TM[cs, 1, pc], NY[cs, par, :], [TMk, NYk], [pgk], start=False, stop=True)
                            yield
                            for par in range(2):
                                hp = par * 64
                                tt('dve', GTb[hp:hp + 64, c, hp:hp + 64], cstf[hp:hp + 64, 4, 0:64], pgv[hp:hp + 64, 0, :], ALU.subtract,
                                   ['cstf', pgk], [GTk])
                            ts('dve', ZLG[:, c, :], pgv[:, 1, :], GLt[:, j, c:c + 1], None, ALU.mult, None, [pgk, GLk], [ZLGk])
                            yield
                        for c in ((0, 1) if not rev else (1, 0)):
                            cp('act', Z0b[:, c, :], Z[:, j, :], [Zk], [Z0k])
                            yield
                            pn, pnk = bank()
                            mm(pn[:, 0:64], GTb[:, c, :], Z0b[:, c, :], [GTk, Z0k], [pnk])
                            yield
                            stt(Z[:, j, :], pn[:, 0:64], GLt[:, j, c:c + 1], ZLG[:, c, :], ALU.mult, ALU.add, [pnk, GLk, ZLGk, Zk], [Zk])
                            yield
                        for par in range(2):
                            hp = par * 64
                            h = 2 * j + par
                            hc = slice(h * 64, (h + 1) * 64)
                            po_, pok = bank()
                            reg = po_[hp:hp + 64, 0:128]
                            mm(reg, vT[:, t, hc], A4T[:, par, :], ['rwvT', A4k], [pok], start=True, stop=False)
                            mm(reg, NY[:, par, :], A3T[:, par, :], [NYk, A3k], [pok], start=False, stop=False)
                            for c in range(2):
                                mm(reg[:, c * 64:(c + 1) * 64], Z0b[hp:hp + 64, c, :], RH[hp:hp + 64, c * 64:(c + 1) * 64],
                                   [Z0k, RHk], [pok], start=False, stop=(c == 1))
                            yield
                            osl = OS[hp:hp + 64, j, tsl]
                            osk = 'rwOS%d_%d' % (t, j)
                            if (t, j, par) not in seen_o:
                                seen_o.add((t, j, par))
                                cp('dve' if par == 0 else 'act', osl, reg, [pok], [osk])
                            else:
                                tt('dve', osl, osl, reg, ALU.add, [pok, osk], [osk])
                            yield

                    for step in range(NTL):
                        if stop is not None and stop.startswith('rw_u') and step >= 1:
                            break
                        gens = [unit(d, orders[d][step]) for d in range(2)]
                        while gens:
                            for g in list(gens):
                                try:
                                    next(g)
                                except StopIteration:
                                    gens.remove(g)
                    S.barrier()
                if stop is not None and stop.startswith('rw_'):
                    return
                with contextlib.ExitStack() as st2:
                    ob = [sb(st2, "rwob%d" % i, [128, 2, 128], BF16) for i in range(2)]
                    cen = [sb(st2, "rwcen%d" % i, [128, 2, 128], F32) for i in range(2)]
                    rs = [sb(st2, "rwrs%d" % i, [128, 2, 128], F32) for i in range(2)]
                    pm_ = [ps(st2, "rwpm%d" % i, [128, 512], F32) for i in range(2)]
                    pv_ = [ps(st2, "rwpv%d" % i, [128, 512], F32) for i in range(2)]
                    for t in range(NTL):
                        i2 = t % 2
                        tsl = slice(t * 128, (t + 1) * 128)
                        osk = 'rwOS%d_0' % t
                        osk1 = 'rwOS%d_1' % t
                        cp('act', ob[i2][:], OS[:, :, tsl], [osk, osk1], ['rwob%d' % i2])
                        pmv = pm_[i2][:, 0:256].rearrange("p (a b) -> p a b", a=2)
                        for j in range(2):
                            mm(pmv[:, j, :], bonesb, ob[i2][:, j, :], ['cstb', 'rwob%d' % i2], ['rwpm%d' % i2])
                        stt(cen[i2][:], pmv, -1.0 / 64, OS[:, :, tsl], ALU.mult, ALU.add, ['rwpm%d' % i2, osk, osk1], ['rwcen%d' % i2])
                        act(ob[i2][:], cen[i2][:], AF.Square, ['rwcen%d' % i2], ['rwob%d' % i2])
                        pvv = pv_[i2][:, 0:256].rearrange("p (a b) -> p a b", a=2)
                        for j in range(2):
                            mm(pvv[:, j, :], bonesb, ob[i2][:, j, :], ['cstb', 'rwob%d' % i2], ['rwpv%d' % i2])
                        act(rs[i2][:], pvv, AF.Sqrt, ['rwpv%d' % i2], ['rwrs%d' % i2], bias=RW_GN_EPS, scale=1.0 / 64)
                        S.op('dve', lambda e: e.reciprocal(out=rs[i2][:], in_=rs[i2][:]), reads=['rwrs%d' % i2], writes=['rwrs%d' % i2])
                        tt('dve', cen[i2][:], cen[i2][:], rs[i2][:], ALU.mult, ['rwcen%d' % i2, 'rwrs%d' % i2], ['rwcen%d' % i2])
                        tt('pool', cen[i2][:], cen[i2][:], bc3(pv('gnw')), ALU.mult, ['rwcen%d' % i2, 'pvt'], ['rwcen%d' % i2])
                        tt('pool', cen[i2][:], cen[i2][:], bc3(pv('gnb')), ALU.add, ['rwcen%d' % i2, 'pvt'], ['rwcen%d' % i2])
                        tt('dve', cen[i2][:], cen[i2][:], Y[:, 3, :, tsl], ALU.add, ['rwcen%d' % i2, 'Y3'], ['rwcen%d' % i2])
                        if ('yd%d' % l) in debug:
                            cp('act', OS[:, :, tsl], cen[i2][:], ['rwcen%d' % i2], [osk, osk1])
                        tt('dve', Y[:, 3, :, tsl], cen[i2][:], zs[:, :, tsl], ALU.mult, ['rwcen%d' % i2, 'rwzs'], ['Y3'])
                    if ('yd%d' % l) in debug:
                        dbg_dump('yd%d' % l, OS[:], [128, 2, NT], ['rwOS%d_%d' % (t, j_) for t in range(NTL) for j_ in range(2)])
                    S.barrier()
                S.barrier()
        PHASES['rw'] = phase_rw
        def phase_merge(l, h_src, last):
            h_dst = out_d if last else h1_d
            with contextlib.ExitStack() as st:
                MG = sb(st, "mgMG", [128, 8, NT], BF16)
                wbr = sb(st, "mgwbr", [128, 4, 2, DM], BF16)
                S.dma('pool', wbr[:], dr['wbr'][l], writes=['mgwbr'])
                with contextlib.ExitStack() as st2:
                    wg = [sb(st2, "mgwg%d" % i, [128, 8, 4, 128], BF16) for i in range(2)]
                    sg = [sb(st2, "mgsg%d" % i, [128, 512], BF16) for i in range(3)]
                    ac = [sb(st2, "mgac%d" % i, [128, 512], F32) for i in range(2)]
                    tm = [sb(st2, "mgtm%d" % i, [128, 512], F32) for i in range(2)]
                    pgl = [ps(st2, "mgpg%d" % i, [128, 512], F32) for i in range(3)]
                    pbr = [ps(st2, "mgpb%d" % i, [128, 512], F32) for i in range(3)]
                    cg = 0
                    ca = 0
                    for dt_ in range(8):
                        w_, wk_ = wg[dt_ % 2], 'mgwg%d' % (dt_ % 2)
                        for k in range(4):
                            c0 = 3968 + k * 1024 + dt_ * 128
                            S.dma('pool', w_[:, :, k, :], dr['w_in'][l][:, :, c0:c0 + 128], writes=[wk_])
                        for (n0, nn) in BLOCKS:
                            if last and n0 < 256:
                                continue
                            a_, ak_ = ac[ca % 2], 'mgac%d' % (ca % 2)
                            t_, tk_ = tm[ca % 2], 'mgtm%d' % (ca % 2)
                            ca += 1
                            for k in range(4):
                                pg_, pgk_ = pgl[cg % 3], 'mgpg%d' % (cg % 3)
                                pb_, pbk_ = pbr[cg % 3], 'mgpb%d' % (cg % 3)
                                s_, sk_ = sg[cg % 3], 'mgsg%d' % (cg % 3)
                                cg += 1
                                for jj in range(8):
                                    mm(pg_[:, 0:nn], w_[:, jj, k, :], uT[:, jj, n0:n0 + nn], [wk_] + uTk[n0 // 128:(n0 + nn) // 128], [pgk_],
                                       start=(jj == 0), stop=(jj == 7))
                                act(s_[:, 0:nn], pg_[:, 0:nn], AF.Sigmoid, [pgk_, 'pvt'], [sk_], bias=pv('bin', 31 + k * 8 + dt_))
                                for jc in range(2):
                                    mm(pb_[:, 0:nn], wbr[:, k, jc, dt_ * 128:(dt_ + 1) * 128], Y[:, k, jc, n0:n0 + nn], ['mgwbr', 'Y%d' % k], [pbk_],
                                       start=(jc == 0), stop=(jc == 1))
                                if k == 0:
                                    tt('dve', a_[:, 0:nn], pb_[:, 0:nn], s_[:, 0:nn], ALU.mult, [pbk_, sk_], [ak_])
                                else:
                                    tt('dve', t_[:, 0:nn], pb_[:, 0:nn], s_[:, 0:nn], ALU.mult, [pbk_, sk_], [tk_])
                                    if k < 3:
                                        tt('pool', a_[:, 0:nn], a_[:, 0:nn], t_[:, 0:nn], ALU.add, [ak_, tk_], [ak_])
                                    else:
                                        tt('pool', MG[:, dt_, n0:n0 + nn], a_[:, 0:nn], t_[:, 0:nn], ALU.add, [ak_, tk_], ['mgMG%d' % (n0 // 512 if n0 else 9)])
                    S.barrier()
                if ('merged%d' % l) in debug:
                    with contextlib.ExitStack() as st2:
                        mf = sb(st2, "mgf", [128, 8, NT], F32)
                        cp('dve', mf[:], MG[:], ['mgMG%d' % i for i in (9, 0, 1, 2, 3)], ['mgf'])
                        dbg_dump('merged%d' % l, mf[:], [128, 8, NT], ['mgf'])
                        S.barrier()
                with contextlib.ExitStack() as st2:
                    wo = sb(st2, "mgwo", [128, 8, DM], BF16)
                    S.dma('pool', wo[:], dr['wout'][l], writes=['mgwo'])
                    rows = sb(st2, "mgrows", [128, 3, DM], F32)
                    S.dma('sp', rows[:], dr['rows'][l][:, 0:3072].rearrange("p (a b) -> p a b", a=3), writes=['mgrows'])
                    hin_ = [sb(st2, "mghin%d" % i, [128, DM], F32) for i in range(2)]
                    ot = [sb(st2, "mgot%d" % i, [128, DM], F32) for i in range(2)]
                    stat = [sb(st2, "mgst%d" % i, [128, 16], F32) for i in range(2)]
                    po = [[ps(st2, "mgpo%d_%d" % (i, hh), [128, 512], F32) for hh in range(2)] for i in range(2)]
                    it = 0
                    for t in range(NTL):
                        if last and t < 2:
                            continue
                        i2 = it % 2
                        it += 1
                        ci = 1 if t < 2 else 0
                        tsl = slice(t * 128, (t + 1) * 128)
                        mgk = 'mgMG%d' % (9 if t < 2 else (t - 2) // 4)
                        hk_, ok_, sk_ = 'mghin%d' % i2, 'mgot%d' % i2, 'mgst%d' % i2
                        hi, o_, sti = hin_[i2], ot[i2], stat[i2]
                        S.dma('sp', hi[:], h_src[t * 128:(t + 1) * 128, :], writes=[hk_])
                        for hh in range(2):
                            pk_ = 'mgpo%d_%d' % (i2, hh)
                            for jj in range(8):
                                mm(po[i2][hh][:], MG[:, jj, tsl], wo[:, jj, hh * 512:(hh + 1) * 512], [mgk, 'mgwo'], [pk_], start=(jj == 0), stop=(jj == 7))
                            tt('dve', o_[:, hh * 512:(hh + 1) * 512], po[i2][hh][:], rows[:, 0, hh * 512:(hh + 1) * 512], ALU.add, [pk_, 'mgrows'], [ok_])
                        tt('pool', o_[:], o_[:], gatebc[:, ci, :], ALU.mult, [ok_, 'gatebc'], [ok_])
                        stt(o_[:], hi[:], ALPHA, o_[:], ALU.mult, ALU.add, [hk_, ok_], [ok_])
                        S.op('dve', lambda e: e.bn_stats(out=sti[:, 0:6], in_=o_[:, 0:512]), reads=[ok_], writes=[sk_])
                        S.op('dve', lambda e: e.bn_stats(out=sti[:, 6:12], in_=o_[:, 512:1024]), reads=[ok_], writes=[sk_])
                        S.op('dve', lambda e: e.bn_aggr(out=sti[:, 12:14], in_=sti[:, 0:12]), reads=[sk_], writes=[sk_])
                        act(sti[:, 14:15], sti[:, 13:14], AF.Sqrt, [sk_], [sk_], bias=LN_EPS)
                        S.op('dve', lambda e: e.reciprocal(out=sti[:, 14:15], in_=sti[:, 14:15]), reads=[sk_], writes=[sk_])
                        stt(sti[:, 15:16], sti[:, 12:13], -1.0, sti[:, 14:15], ALU.mult, ALU.mult, [sk_], [sk_])
                        act(o_[:], o_[:], AF.Identity, [ok_, sk_], [ok_], bias=sti[:, 15:16], scale=sti[:, 14:15])
                        tt('pool', o_[:], o_[:], rows[:, 1, :], ALU.mult, [ok_, 'mgrows'], [ok_])
                        tt('dve', o_[:], o_[:], rows[:, 2, :], ALU.add, [ok_, 'mgrows'], [ok_])
                        if last:
                            S.dma('sp', out_d[(t - 2) * 128:(t - 1) * 128, :], o_[:], reads=[ok_], writes=['outfinal'])
                        else:
                            S.dma('sp', h1_d[t * 128:(t + 1) * 128, :], o_[:], reads=[ok_], writes=['h1'])
                    S.barrier()
                S.barrier()
        PHASES['merge'] = phase_merge
        for l in range(nlayers):
            last = (l == nlayers - 1)
            h_src = dr['hin'] if l == 0 else h1_d
            S.dma('sp', pvt[:], dr['pv'][l], writes=['pvt'])
            with contextlib.ExitStack() as st:
                adw = [sb(st, "adw%d" % i, [128, 8, 512], F32) for i in range(2)]
                scb = sb(st, "scb", [128, 2, 8, 128], F32)
                grow = sb(st, "grow", [128, DM], F32)
                pm0 = ps(st, "pm0", [128, 16, 2], F32)
                pg = [ps(st, "pg%d" % i, [128, 512], F32) for i in range(2)]
                for i in range(2):
                    cp('dve', scb[:, i], silc[:, :, i:i + 1].broadcast_to([128, 8, 128]), ['silc'], ['scb'])
                S.dma('sp', grow[:], dr['rows'][l][:, 3072:4096], writes=['grow'])
                for ch in range(6):
                    buf = adw[ch % 2]
                    bk = 'adw%d' % (ch % 2)
                    S.dma('sp' if ch % 2 == 0 else 'act', buf[:], dr['ada_w'][l][:, :, ch * 512:(ch + 1) * 512], writes=[bk])
                    if ch < 4:
                        for mloc in range(4):
                            m = ch * 4 + mloc
                            for j in range(8):
                                mm(pm0[:, m, :], buf[:, j, mloc * 128:(mloc + 1) * 128], silc[:, j, :], [bk, 'silc'],
                                   ['pm0'], start=(j == 0), stop=(j == 7))
                    else:
                        for i in range(2):
                            for j in range(8):
                                mm(pg[i][:], scb[:, i, j, :], buf[:, j, :], [bk, 'scb'], ['pg%d' % i],
                                   start=(j == 0), stop=(j == 7))
                            tt('dve', gatebc[:, i, (ch - 4) * 512:(ch - 3) * 512], pg[i][:],
                               grow[:, (ch - 4) * 512:(ch - 3) * 512], ALU.add, ['pg%d' % i, 'grow'], ['gatebc'])
                tt('dve', modfm[:], pm0[:], pv('adab').unsqueeze(2).broadcast_to([128, 16, 2]), ALU.add,
                   ['pm0', 'pvt'], ['modfm'])
                ts('dve', modfm[:, 8:16, :], modfm[:, 8:16, :], 1.0, None, ALU.add, None, ['modfm'], ['modfm'])
                dbg_dump('modfm%d' % l, modfm[:], [128, 16, 2], ['modfm'])
                dbg_dump('gatebc%d' % l, gatebc[:], [128, 2, DM], ['gatebc'])
                S.barrier()
            with contextlib.ExitStack() as st:
                xin = [sb(st, "xin%d" % i, [128, DM], F32) for i in range(3)]
                xn = [sb(st, "xn%d" % i, [128, DM], BF16) for i in range(2)]
                stat = [sb(st, "stat%d" % i, [128, 16], F32) for i in range(2)]
                ptr = [ps(st, "ptr%d" % i, [128, 8, 128], BF16) for i in range(2)]
                for t in range(NTL):
                    xi, xk = xin[t % 3], 'xin%d' % (t % 3)
                    sti, sk = stat[t % 2], 'stat%d' % (t % 2)
                    xo, xok = xn[t % 2], 'xn%d' % (t % 2)
                    pt, ptk = ptr[t % 2], 'ptr%d' % (t % 2)
                    ci = 1 if t < 2 else 0
                    S.dma('sp' if t % 2 == 0 else 'act', xi[:], h_src[t * 128:(t + 1) * 128, :], writes=[xk])
                    S.op('dve', lambda e: e.bn_stats(out=sti[:, 0:6], in_=xi[:, 0:512]), reads=[xk], writes=[sk])
                    S.op('dve', lambda e: e.bn_stats(out=sti[:, 6:12], in_=xi[:, 512:1024]), reads=[xk], writes=[sk])
                    S.op('dve', lambda e: e.bn_aggr(out=sti[:, 12:14], in_=sti[:, 0:12]), reads=[sk], writes=[sk])
                    act(sti[:, 14:15], sti[:, 13:14], AF.Sqrt, [sk], [sk], bias=LN_EPS)
                    S.op('dve', lambda e: e.reciprocal(out=sti[:, 14:15], in_=sti[:, 14:15]), reads=[sk], writes=[sk])
                    stt(sti[:, 15:16], sti[:, 12:13], -1.0, sti[:, 14:15], ALU.mult, ALU.mult, [sk], [sk])
                    act(xo[:], xi[:], AF.Identity, [xk, sk], [xok], bias=sti[:, 15:16], scale=sti[:, 14:15])
                    for j in range(8):
                        tr(pt[:, j, :], xo[:, j * 128:(j + 1) * 128], identb, [xok, 'cstb'], [ptk])
                    for j in range(8):
                        if j % 2 == 0:
                            act(uT[:, j, t * 128:(t + 1) * 128], pt[:, j, :], AF.Identity, [ptk, 'modfm'], ['uT%d' % t],
                                bias=modfm[:, j, ci:ci + 1], scale=modfm[:, 8 + j, ci:ci + 1])
                        else:
                            ts('dve', uT[:, j, t * 128:(t + 1) * 128], pt[:, j, :], modfm[:, 8 + j, ci:ci + 1],
                               modfm[:, j, ci:ci + 1], ALU.mult, ALU.add, [ptk, 'modfm'], ['uT%d' % t])
                if ('uT%d' % l) in debug:
                    utf = sb(st, "utf", [128, 8, NT], F32)
                    cp('dve', utf[:], uT[:], ['uT%d' % t for t in range(NTL)], ['utf'])
                    dbg_dump('uT%d' % l, utf[:], [128, 8, NT], ['utf'])
                S.barrier()
            uTk = ['uT%d' % t for t in range(NTL)]

            for ph in list(PHASES):
                if ph in phases:
                    PHASES[ph](l, h_src, last)
            if ('h%d' % l) in debug and not last:
                d_ = dbg_out('h%d' % l, [NT, DM])
                S.dma('sp', d_, h1_d, writes=['dbgout_h%d' % l])
                S.barrier()
            if ('Y%d' % l) in debug:
                with contextlib.ExitStack() as st:
                    yf = sb(st, "yf", [128, 4, 2, NT], F32)
                    cp('dve', yf[:], Y[:], ['Y0', 'Y1', 'Y2', 'Y3'], ['yf'])
                    dbg_dump('Y%d' % l, yf[:], [128, 4, 2, NT], ['yf'])
                    S.barrier()

        S.final_wait('sp', ['outfinal'] + ['dbgout_' + n for n in dbg_d])
    if MEMDBG:
        print('SBUF min remaining by prefix:', minrem)
    return nc, dbg_d


def kernel(**inputs):
    inp = {k: np.asarray(v) for k, v in inputs.items()}
    sh = prep_shared(inp)
    nc, _ = build()
    in_maps = []
    for b in range(8):
        m = dict(sh)
        m.update(prep_core(inp, b))
        in_maps.append(m)
    res = run_bass_kernel_spmd(nc, in_maps, core_ids=list(range(8)))
    return np.stack([np.asarray(res.results[b]['out'], dtype=np.float32) for b in range(8)], 0)
```

```python
import contextlib
import numpy as np
import concourse.bass as bass
import concourse.mybir as mybir
from concourse.bass_utils import run_bass_kernel_spmd

F32 = mybir.dt.float32
BF16 = mybir.dt.bfloat16
AF = mybir.ActivationFunctionType
ALU = mybir.AluOpType
AX = mybir.AxisListType

NT = 2304
NTL = 18
DM = 1024
NCOL = 8064
BLOCKS = [(0, 256), (256, 512), (768, 512), (1280, 512), (1792, 512)]
LN_EPS = 1e-5
RMS_EPS = 1e-6
RW_GN_EPS = 64e-5
ALPHA = (2 * 2) ** 0.25
PI = float(np.pi)
MEMDBG = False


class Sched:
    NDMA = 16

    def __init__(self, nc, same_engine_waits=True):
        self.nc = nc
        self.same = same_engine_waits
        self.eng = dict(pe=nc.tensor, act=nc.scalar, dve=nc.vector, pool=nc.gpsimd, sp=nc.sync)
        self.E = {n: dict(cnt=0, known={}) for n in self.eng}
        self.dq = {'sp': ['dsp%d' % i for i in range(8)], 'act': ['dac%d' % i for i in range(4)],
                   'pool': ['dpl%d' % i for i in range(8)]}
        self.dmas = {n: dict(cnt=0) for q in self.dq.values() for n in q}
        self.dma_rr = {'sp': 0, 'act': 0, 'pool': 0}
        self.lastw = {}
        self.readers = {}
        self.sems = None
        self.nins = 0

    def sem_names(self):
        return list(self.E.keys()) + list(self.dmas.keys())

    def _deps(self, reads, writes):
        deps = {}

        def add(w):
            if w is not None:
                deps[w[0]] = max(deps.get(w[0], 0), w[1])
        for k in reads:
            add(self.lastw.get(k))
        for k in writes:
            add(self.lastw.get(k))
            for r in self.readers.get(k, ()):
                add(r)
        return deps

    def _waits(self, en, deps):
        E = self.E[en]
        waits = []
        for d, v in deps.items():
            if d == en and (en == 'pe' or not self.same):
                continue
            if E['known'].get(d, 0) < v:
                waits.append((d, v))
                E['known'][d] = v
        return waits

    def _record(self, ident, reads, writes):
        for k in writes:
            self.lastw[k] = ident
            self.readers[k] = []
        for k in reads:
            self.readers.setdefault(k, []).append(ident)

    def _emit(self, en, waits, fn, inc):
        eng = self.eng[en]
        for d, v in waits:
            eng.wait_ge(self.sems[d], v)
        if fn is not None:
            fn(eng).then_inc(self.sems[inc[0]], inc[1])
            self.nins += 1

    def op(self, en, fn, reads=(), writes=()):
        E = self.E[en]
        waits = self._waits(en, self._deps(reads, writes))
        E['cnt'] += 1
        self._emit(en, waits, fn, (en, 1))
        self._record((en, E['cnt']), reads, writes)

    def dma(self, en, out, in_, reads=(), writes=(), **kw):
        dn = self.dq[en][self.dma_rr[en]]
        self.dma_rr[en] = (self.dma_rr[en] + 1) % len(self.dq[en])
        Dq = self.dmas[dn]
        deps = self._deps(reads, writes)
        if Dq['cnt'] > 0:
            deps[dn] = max(deps.get(dn, 0), Dq['cnt'])
        waits = self._waits(en, deps)
        Dq['cnt'] += 16
        self._emit(en, waits, (lambda e: e.dma_start(out=out, in_=in_, **kw)), (dn, 16))
        self._record((dn, Dq['cnt']), reads, writes)

    def barrier(self):
        cur = {n: self.E[n]['cnt'] for n in self.E}
        cur.update({n: self.dmas[n]['cnt'] for n in self.dmas})
        for en in self.E:
            waits = self._waits(en, {d: v for d, v in cur.items() if v > 0})
            self._emit(en, waits, None, None)

    def final_wait(self, en, keys):
        self._emit(en, self._waits(en, self._deps(keys, ())), None, None)


PV = {}


def _pv_layout():
    off = 0
    for name, n in [('bin', 63), ('s5d', 2), ('glub', 2), ('hglb', 8), ('hgnw', 2), ('rdec', 4), ('mu', 14),
                    ('w0', 4), ('a0', 4), ('kk', 2), ('ka', 2), ('rk', 2), ('gnw', 2), ('gnb', 2), ('adab', 16),
                    ('lamre', 16), ('lamim', 16), ('ldt', 16), ('rdech', 8)]:
        PV[name] = (off, n)
        off += n
    return off


NPV = _pv_layout()


def _colmap():
    cm = list(range(0, 3584))
    lora = [-1] * 128
    for r in range(16):
        lora[r] = 3584 + r
        lora[32 + r] = 3600 + r
        lora[64 + r] = 3616 + r
        lora[80 + r] = 3632 + r
    cm += lora
    cm += list(range(3648, 3904))
    cm += list(range(3904, 8000))
    return np.array(cm)


CMAP = _colmap()


def _fm(v):
    return np.ascontiguousarray(v.reshape(-1, 128).T)


def _masks():
    t = np.arange(128)
    s_, t_ = t[:, None], t[None, :]
    m = []
    b32 = (s_ // 32) == (t_ // 32)
    b64 = (s_ // 64) == (t_ // 64)
    m.append(b32 & (t_ >= s_))
    m.append(b32 & (t_ <= s_))
    m.append(b64 & (t_ > s_))
    m.append(b64 & (t_ < s_))
    m.append(b64 & (t_ >= s_))
    m.append(b64 & (t_ <= s_))
    for d in range(2):
        for lv in range(6):
            sz = 1 << lv
            blk = (s_ // (2 * sz)) == (t_ // (2 * sz))
            hs, ht = (s_ // sz) % 2, (t_ // sz) % 2
            if d == 0:
                m.append(blk & (ht == 1) & (hs == 0))
            else:
                m.append(blk & (ht == 0) & (hs == 1))
    return np.stack([x.astype(np.float32) for x in m], 1)


def _rot_tables():
    n = 16
    freqs = 10000.0 ** (-np.arange(n, dtype=np.float32) / n)
    tt = np.arange(2048)
    rows = (tt // 64).astype(np.float32)
    cols = (tt % 64).astype(np.float32)
    cos = np.zeros((128, 2048), np.float32)
    sins = np.zeros((128, 2048), np.float32)
    pm = np.zeros((128, 128), np.float32)
    for p in range(128):
        i = p % 64
        pos = rows if i < 32 else cols
        ii = i % 32
        ang = pos * freqs[ii % 16]
        cos[p] = np.cos(ang)
        if ii < 16:
            sins[p] = -np.sin(ang)
            partner = p + 16
        else:
            sins[p] = np.sin(ang)
            partner = p - 16
        pm[partner, p] = 1.0
    return cos, sins, pm


def prep_shared(inp):
    sh = {}
    L = 2
    w_in = inp['w_in']
    wn = np.zeros((L, 1024, NCOL), np.float32)
    valid = CMAP >= 0
    wn[:, :, valid] = w_in[:, :, CMAP[valid]]
    sh['w_in'] = np.ascontiguousarray(wn.reshape(L, 8, 128, NCOL).transpose(0, 2, 1, 3))
    bn = np.zeros((L, NCOL), np.float32)
    bn[:, valid] = inp['b_in'][:, CMAP[valid]]
    sh['ada_w'] = np.ascontiguousarray(inp['ada_w'].reshape(L, 8, 128, 3072).transpose(0, 2, 1, 3))
    pv = np.zeros((L, 128, NPV), np.float32)

    def put(l, name, arr):
        o, n = PV[name]
        assert arr.shape == (128, n), (name, arr.shape)
        pv[l, :, o:o + n] = arr
    for l in range(L):
        put(l, 'bin', _fm(bn[l]))
        put(l, 's5d', _fm(inp['s5_d'][l]))
        put(l, 'glub', _fm(inp['s5_glu_b'][l]))
        put(l, 'hglb', np.concatenate([_fm(inp['hg_lb'][ll, d]) for ll in range(2) for d in range(2)], 1))
        put(l, 'hgnw', _fm(inp['hg_norm_w'][l]))
        rd = np.zeros((128, 4), np.float32)
        for d in range(2):
            for j in range(2):
                rd[:64, d * 2 + j] = inp['ret_decay'][l, d, 2 * j]
                rd[64:, d * 2 + j] = inp['ret_decay'][l, d, 2 * j + 1]
        put(l, 'rdec', rd)
        put(l, 'rdech', np.ascontiguousarray(np.broadcast_to(inp['ret_decay'][l].reshape(1, 8), (128, 8))))
        mu = np.zeros((2, 7 * 128), np.float32)
        mu[:, :768] = inp['rw_mu'][l][:, :768]
        lv = CMAP[3584:3712]
        ok = lv >= 0
        mu[:, 768:896][:, ok] = inp['rw_mu'][l][:, lv[ok] - 2816]
        put(l, 'mu', np.concatenate([_fm(mu[0]), _fm(mu[1])], 1))
        put(l, 'w0', np.concatenate([_fm(inp['rw_w0'][l, d]) for d in range(2)], 1))
        put(l, 'a0', np.concatenate([_fm(inp['rw_a0'][l, d]) for d in range(2)], 1))
        for nm, key in [('kk', 'rw_kk'), ('ka', 'rw_ka'), ('rk', 'rw_rk'), ('gnw', 'rw_gn_w'), ('gnb', 'rw_gn_b')]:
            put(l, nm, _fm(inp[key][l]))
        put(l, 'adab', _fm(inp['ada_b'][l][:2048]))
        for nm, key in [('lamre', 's5_lam_re'), ('lamim', 's5_lam_im')]:
            a = inp[key][l].reshape(2, 8, 2, 64)
            put(l, nm, np.ascontiguousarray(a.transpose(2, 3, 0, 1).reshape(128, 16)))
        a = np.broadcast_to(inp['s5_log_dt'][l].reshape(2, 8, 2, 1), (2, 8, 2, 64))
        put(l, 'ldt', np.ascontiguousarray(a.transpose(2, 3, 0, 1).reshape(128, 16)))
    sh['pv'] = pv
    bt = np.zeros((L, 128, 2, 4, 2, 128), np.float32)
    ct = np.zeros((L, 128, 8, 2, 128), np.float32)
    for l in range(L):
        for g in range(16):
            i, g2 = g // 2, g % 2
            for q in range(16):
                c = g * 16 + q
                j, p = c // 128, c % 128
                bt[l, p, j, i % 4, 0, g2 * 64:(g2 + 1) * 64] = inp['s5_b_re'][l, g, :, q]
                bt[l, p, j, i % 4, 1, g2 * 64:(g2 + 1) * 64] = inp['s5_b_im'][l, g, :, q]
            m0 = (i % 4) * 32 + g2 * 16
            ct[l, g2 * 64:(g2 + 1) * 64, i, 0, m0:m0 + 16] = inp['s5_c_re'][l, g].T
            ct[l, g2 * 64:(g2 + 1) * 64, i, 1, m0:m0 + 16] = inp['s5_c_im'][l, g].T
    sh['s5bt'] = bt
    sh['s5ct'] = ct
    sh['gluw'] = np.ascontiguousarray(inp['s5_glu_w'].reshape(L, 2, 128, 256).transpose(0, 2, 1, 3))
    lw2 = np.zeros((L, 128, 2, 256), np.float32)
    for l in range(L):
        lw2[l, 0:16, 0] = inp['rw_w2'][l, 0]
        lw2[l, 32:48, 1] = inp['rw_w2'][l, 1]
        lw2[l, 64:80, 0] = inp['rw_a2'][l, 0]
        lw2[l, 80:96, 1] = inp['rw_a2'][l, 1]
    sh['lw2'] = lw2
    sh['wbr'] = np.ascontiguousarray(inp['w_branch'].reshape(L, 4, 2, 128, 1024).transpose(0, 3, 1, 2, 4))
    sh['wout'] = np.ascontiguousarray(inp['w_out'].reshape(L, 8, 128, 1024).transpose(0, 2, 1, 3))
    rows = np.zeros((L, 128, 4096 + 512), np.float32)
    for l in range(L):
        rows[l, :, 0:1024] = inp['b_out'][l][None]
        rows[l, :, 1024:2048] = inp['ln_w'][l][None]
        rows[l, :, 2048:3072] = inp['ln_b'][l][None]
        rows[l, :, 3072:4096] = inp['ada_b'][l][None, 2048:3072]
        rows[l, :, 4096:4352] = inp['b_in'][l][None, 1280:1536]
        rows[l, :, 4352:4608] = inp['b_in'][l][None, 2304:2560]
    sh['rows'] = rows
    sh['masks'] = _masks()
    cos, sins, pm = _rot_tables()
    sh['rcos'] = cos
    sh['rsin'] = sins
    t = np.arange(128)
    cst = np.zeros((128, 9, 128), np.float32)
    cst[:, 0] = pm
    cst[:, 1] = ((t[:, None] // 64) == (t[None, :] // 64))
    cst[:, 2] = np.maximum(t[None, :] - t[:, None], 0)
    cst[:, 3] = np.maximum(t[:, None] - t[None, :], 0)
    cst[:, 4] = (t[None, :] >= t[:, None])
    cst[:, 5] = (t[None, :] <= t[:, None])
    cst[:, 6, :64] = ((t[:, None] % 64) == np.arange(64)[None, :])
    cst[:, 6, 64:68] = ((t[:, None] // 32) == np.arange(4)[None, :])
    cst[:, 6, 68] = 127 - t
    cst[:, 6, 69] = t
    cst[:, 7] = t[None, :] + 1.0
    cst[:, 8] = 128.0 - t[None, :]
    sh['cst'] = cst
    return sh


def prep_core(inp, b):
    pc = {}
    pc['hin'] = np.ascontiguousarray(np.concatenate([inp['ctx'][b], inp['x'][b]], 0))
    cv = np.stack([inp['c'][b], inp['c_ctx']], -1)
    pc['cvec'] = np.ascontiguousarray(cv.reshape(8, 128, 2).transpose(1, 0, 2))
    return pc


SHAPES = dict(hin=[NT, DM], cvec=[128, 8, 2], w_in=[2, 128, 8, NCOL], ada_w=[2, 128, 8, 3072], pv=[2, 128, NPV],
              s5bt=[2, 128, 2, 4, 2, 128], s5ct=[2, 128, 8, 2, 128], gluw=[2, 128, 2, 256], lw2=[2, 128, 2, 256],
              wbr=[2, 128, 4, 2, 1024], wout=[2, 128, 8, 1024], rows=[2, 128, 4608], masks=[128, 18, 128],
              rcos=[128, 2048], rsin=[128, 2048], cst=[128, 9, 128])


def build(debug=(), nlayers=2, phases=('s5', 'hg', 'ret', 'rw', 'merge'), stop=None):
    nc = bass.Bass("TRN2", target_bir_lowering=False)
    S = Sched(nc)
    dr = {k: nc.dram_tensor(k, list(v), F32, kind="ExternalInput").ap() for k, v in SHAPES.items()}
    out_d = nc.dram_tensor("out", [2048, DM], F32, kind="ExternalOutput").ap()
    h1_d = nc.dram_tensor("h1", [NT, DM], F32, kind="Internal").ap()
    dbg_d = {}

    def dbg_out(name, shape):
        dbg_d[name] = nc.dram_tensor("dbg_" + name, list(shape), F32, kind="ExternalOutput").ap()
        return dbg_d[name]

    uid = [0]

    def key(p='k'):
        uid[0] += 1
        return '%s%d' % (p, uid[0])

    with contextlib.ExitStack() as top:
        S.sems = {n: top.enter_context(nc.semaphore(n)) for n in S.sem_names()}

        minrem = {}

        def sb(st, name, shape, dt=F32):
            uid[0] += 1
            t_ = st.enter_context(nc.sbuf_tensor("%s_%d" % (name, uid[0]), list(shape), dt))
            if MEMDBG:
                pre = name[:2]
                minrem[pre] = min(minrem.get(pre, 1 << 30), nc.sbuf_bytes_remaining)
            return t_

        def ps(st, name, shape, dt=F32):
            uid[0] += 1
            return st.enter_context(nc.psum_tensor("%s_%d" % (name, uid[0]), list(shape), dt))

        def mm(out, lhsT, rhs, r, w, start=True, stop=True):
            S.op('pe', lambda e: e.matmul(out, lhsT=lhsT, rhs=rhs, start=start, stop=stop), reads=r, writes=w)

        def tr(out, in_, ident, r, w):
            S.op('pe', lambda e: e.transpose(out, in_, ident), reads=r, writes=w)

        def act(out, in_, func, r, w, bias=0.0, scale=1.0):
            S.op('act', lambda e: e.activation(out=out, in_=in_, func=func, bias=bias, scale=scale), reads=r, writes=w)

        def tt(en, out, in0, in1, op, r, w):
            S.op(en, lambda e: e.tensor_tensor(out=out, in0=in0, in1=in1, op=op), reads=r, writes=w)

        def ts(en, out, in0, s1, s2, op0, op1, r, w):
            if s2 is None:
                S.op(en, lambda e: e.tensor_scalar(out=out, in0=in0, scalar1=s1, scalar2=None, op0=op0), reads=r, writes=w)
            else:
                S.op(en, lambda e: e.tensor_scalar(out=out, in0=in0, scalar1=s1, scalar2=s2, op0=op0, op1=op1),
                     reads=r, writes=w)

        def stt(out, in0, sc, in1, op0, op1, r, w):
            S.op('dve', lambda e: e.scalar_tensor_tensor(out=out, in0=in0, scalar=sc, in1=in1, op0=op0, op1=op1),
                 reads=r, writes=w)

        def cp(en, out, in_, r, w):
            if en == 'act':
                S.op('act', lambda e: e.copy(out=out, in_=in_), reads=r, writes=w)
            else:
                S.op(en, lambda e: e.tensor_copy(out=out, in_=in_), reads=r, writes=w)

        def memset(en, ap, val, w):
            S.op(en, lambda e: e.memset(ap, val), writes=w)

        def run_pipelined(gens, stagger):
            it = iter(gens)
            active, pending, rounds = [], True, 0
            while pending or active:
                if pending and rounds % stagger == 0:
                    try:
                        active.append(next(it))
                    except StopIteration:
                        pending = False
                for g in list(active):
                    try:
                        next(g)
                    except StopIteration:
                        active.remove(g)
                rounds += 1

        def mkbanks(st_, n, prefix):
            bl = [ps(st_, "%s%d" % (prefix, i), [128, 512], F32) for i in range(n)]
            cnt = [0]

            def bank():
                i = cnt[0] % n
                cnt[0] += 1
                return bl[i], '%s%d' % (prefix, i)
            return bank

        def dbg_dump(name, ap, shape, r):
            if name in debug:
                d = dbg_out(name, shape)
                S.dma('sp', d, ap, reads=r, writes=['dbgout_' + name])

        cstb = sb(top, "cstb", [128, 3, 128], BF16)
        cstf = sb(top, "cstf", [128, 7, 128], F32)
        maskb = sb(top, "maskb", [128, 18, 128], BF16)
        silc = sb(top, "silc", [128, 8, 2], F32)
        S.dma('pool', cstb[:, 0:2, :], dr['cst'][:, 0:2, :], writes=['cstb'])
        S.dma('sp', cstf[:], dr['cst'][:, 2:9, :], writes=['cstf'])
        S.dma('pool', maskb[:], dr['masks'], writes=['maskb'])
        S.dma('sp', silc[:], dr['cvec'], writes=['silc'])
        memset('pool', cstb[:, 2, :], 0.0, ['cstb'])
        S.op('pool', lambda e: e.affine_select(out=cstb[:, 2, :], in_=cstb[:, 2, :], pattern=[[-1, 128]],
                                               compare_op=ALU.not_equal, fill=1.0, base=0, channel_multiplier=1),
             reads=['cstb'], writes=['cstb'])
        act(silc[:], silc[:], AF.Silu, ['silc'], ['silc'])
        identb = cstb[:, 2, :]
        bonesb = cstb[:, 1, :]

        uT = sb(top, "uT", [128, 8, NT], BF16)
        Y = sb(top, "Y", [128, 4, 2, NT], BF16)
        pvt = sb(top, "pvt", [128, NPV], F32)
        if debug:
            memset('pool', Y[:], 0.0, ['Y0', 'Y1', 'Y2', 'Y3'])
        modfm = sb(top, "modfm", [128, 16, 2], F32)
        gatebc = sb(top, "gatebc", [128, 2, DM], F32)

        def pv(name, j=None, n=1):
            o, cnt = PV[name]
            if j is None:
                return pvt[:, o:o + cnt]
            return pvt[:, o + j:o + j + n]

        PHASES = {}
        def proj_fm(st, wt, wk, mlist, evac, pp, ppk):
            cnt = 0
            for (n0, nn) in BLOCKS:
                for mi, m in enumerate(mlist):
                    p_, pk_ = pp[cnt % len(pp)], ppk[cnt % len(pp)]
                    cnt += 1
                    for j in range(8):
                        mm(p_[:, 0:nn], wt[:, j, m * 128:(m + 1) * 128], uT[:, j, n0:n0 + nn],
                           [wk] + uTk[n0 // 128:(n0 + nn) // 128], [pk_], start=(j == 0), stop=(j == 7))
                    evac(mi, m, n0, nn, p_, pk_)

        def phase_s5(l, h_src, last):
            L = 128
            with contextlib.ExitStack() as st:
                btb = sb(st, "btb", [128, 2, 4, 2, 128], BF16)
                ctb = sb(st, "ctb", [128, 8, 2, 128], BF16)
                glub = sb(st, "glub", [128, 2, 256], BF16)
                S.dma('pool', btb[:], dr['s5bt'][l], writes=['btb'])
                S.dma('pool', ctb[:], dr['s5ct'][l], writes=['ctb'])
                S.dma('pool', glub[:], dr['gluw'][l], writes=['glub'])
                ts('pool', ctb[:, :, 1, :], ctb[:, :, 1, :], -1.0, 0.0, ALU.mult, ALU.add, ['ctb'], ['ctb'])
                ub = sb(st, "s5u", [128, 2, NT], BF16)
                zs = sb(st, "s5z", [128, 2, NT], BF16)
                yacc = sb(st, "yacc", [128, 2, NT], F32)
                PT = sb(st, "s5PT", [128, 16, 2, L], F32)
                QT = sb(st, "s5QT", [128, 16, 2, L], F32)
                sst = sb(st, "s5st", [128, 16, 2], F32)
                ones = sb(st, "s5ones", [128, L], F32)
                memset('pool', yacc[:], 0.0, ['yacc'])
                memset('pool', sst[:], 0.0, ['sst'])
                memset('pool', ones[:], 1.0, ['s5ones'])
                with contextlib.ExitStack() as st2:
                    wsu = sb(st2, "wsu", [128, 8, 512], BF16)
                    S.dma('pool', wsu[:], dr['w_in'][l][:, :, 0:512], writes=['wsu'])
                    pp = [ps(st2, "s5pp%d" % i, [128, 512], F32) for i in range(2)]

                    def evac(mi, m, n0, nn, p_, pk_):
                        if m < 2:
                            act(ub[:, m, n0:n0 + nn], p_[:, 0:nn], AF.Identity, [pk_, 'pvt'], ['s5u'], bias=pv('bin', m))
                        else:
                            act(zs[:, m - 2, n0:n0 + nn], p_[:, 0:nn], AF.Silu, [pk_, 'pvt'], ['s5z'], bias=pv('bin', m))
                    proj_fm(st2, wsu, 'wsu', [0, 1, 2, 3], evac, pp, ['s5pp0', 's5pp1'])
                    sm = sb(st2, "s5sm", [128, 20, 16], F32)
                    K_ = 's5sm'

                    def Sm(i):
                        return sm[:, i, :]

                    def T2(o, a, b, op):
                        tt('dve', Sm(o), a if not isinstance(a, int) else Sm(a), b if not isinstance(b, int) else Sm(b), op,
                           [K_, 'pvt'], [K_])
                    lamre, lamim = pv('lamre'), pv('lamim')
                    act(Sm(0), pv('ldt'), AF.Exp, ['pvt'], [K_])
                    T2(1, lamre, 0, ALU.mult)
                    act(Sm(2), Sm(1), AF.Exp, [K_], [K_])
                    act(Sm(3), Sm(1), AF.Exp, [K_], [K_], scale=-1.0)
                    T2(4, lamim, 0, ALU.mult)
                    ts('dve', Sm(5), Sm(4), PI / 2, None, ALU.add, None, [K_], [K_])
                    for x in (4, 5):
                        for _ in range(4):
                            ts('dve', Sm(16), Sm(x), PI, 2 * PI, ALU.is_gt, ALU.mult, [K_], [K_])
                            T2(x, x, 16, ALU.subtract)
                    act(Sm(6), Sm(4), AF.Sin, [K_], [K_])
                    act(Sm(7), Sm(5), AF.Sin, [K_], [K_])
                    T2(8, 2, 7, ALU.mult)
                    T2(9, 2, 6, ALU.mult)
                    T2(10, 3, 7, ALU.mult)
                    stt(Sm(11), Sm(3), -1.0, Sm(6), ALU.mult, ALU.mult, [K_], [K_])
                    ts('dve', Sm(12), Sm(8), -1.0, None, ALU.add, None, [K_], [K_])
                    T2(16, lamre, lamre, ALU.mult)
                    T2(17, lamim, lamim, ALU.mult)
                    T2(13, 16, 17, ALU.add)
                    S.op('dve', lambda e: e.reciprocal(out=Sm(13), in_=Sm(13)), reads=[K_], writes=[K_])
                    T2(16, 12, lamre, ALU.mult)
                    T2(17, 9, lamim, ALU.mult)
                    T2(16, 16, 17, ALU.add)
                    T2(14, 16, 13, ALU.mult)
                    T2(16, 9, lamre, ALU.mult)
                    T2(17, 12, lamim, ALU.mult)
                    T2(16, 16, 17, ALU.subtract)
                    T2(15, 16, 13, ALU.mult)
                    tmpa = sb(st2, "s5ta", [128, 16, L], F32)
                    tmpb = sb(st2, "s5tb", [128, 16, L], F32)

                    def cmul_bc(dst_re, dst_im, src_re, src_im, s_re, s_im, m):
                        sr = s_re.unsqueeze(2).broadcast_to([128, 16, m])
                        si = s_im.unsqueeze(2).broadcast_to([128, 16, m])
                        ta, tb = tmpa[:, :, 0:m], tmpb[:, :, 0:m]
                        kk_ = ['s5tab', 's5ta', 's5tb', 's5tc', K_]
                        tt('dve', ta, src_re, sr, ALU.mult, kk_, ['s5ta'])
                        tt('dve', tb, src_im, si, ALU.mult, kk_, ['s5tb'])
                        tt('dve', dst_re, ta, tb, ALU.subtract, kk_, ['s5tab'])
                        tt('dve', ta, src_re, si, ALU.mult, kk_, ['s5ta'])
                        tt('dve', tb, src_im, sr, ALU.mult, kk_, ['s5tb'])
                        tt('dve', dst_im, ta, tb, ALU.add, kk_, ['s5tab'])
                    for (TB, a_re, a_im) in ((PT, 8, 9), (QT, 10, 11)):
                        cp('dve', TB[:, :, 0, 0], Sm(a_re), [K_], ['s5tab'])
                        cp('dve', TB[:, :, 1, 0], Sm(a_im), [K_], ['s5tab'])
                        m = 1
                        while m < L:
                            cmul_bc(TB[:, :, 0, m:2 * m], TB[:, :, 1, m:2 * m], TB[:, :, 0, 0:m], TB[:, :, 1, 0:m],
                                    TB[:, :, 0, m - 1], TB[:, :, 1, m - 1], m)
                            m *= 2
                    tmpc = sb(st2, "s5tc", [128, 16, L], F32)
                    cp('dve', tmpc[:], QT[:, :, 0, :], ['s5tab'], ['s5tc'])
                    cmul_bc(QT[:, :, 0, :], QT[:, :, 1, :], tmpc[:], QT[:, :, 1, :], Sm(14), Sm(15), L)
                    S.barrier()
                with contextlib.ExitStack() as st2:
                    NB = 8
                    xa = [sb(st2, "s5xa%d" % i, [128, 2, L], F32) for i in range(NB)]
                    xb_ = [sb(st2, "s5xb%d" % i, [128, 2, L], F32) for i in range(NB)]
                    cw = [sb(st2, "s5cw%d" % i, [128, 2, L], F32) for i in range(NB)]
                    hb = [sb(st2, "s5hb%d" % i, [128, 2, L], BF16) for i in range(NB)]
                    pbu = [ps(st2, "s5pb%d" % i, [128, 2, 2, L], F32) for i in range(4)]
                    py = [ps(st2, "s5py%d" % i, [128, 512], F32) for i in range(2)]
                    orders = [list(range(NTL)), [1, 0] + list(range(NTL - 1, 1, -1))]
                    def s5group(gi, step, d, j):
                        c = orders[d][step]
                        n0 = c * L
                        rev = (d == 1)
                        U = []
                        for ii in range(4):
                            un = gi * 4 + ii
                            bnk = (un // 2) % 4
                            U.append(dict(ii=ii, i=j * 4 + ii, q=d * 8 + j * 4 + ii, pb=pbu[bnk][:, un % 2], pbk='s5pb%d' % bnk,
                                          A=xa[un % NB], Ak='s5xa%d' % (un % NB), B=xb_[un % NB], Bk='s5xb%d' % (un % NB),
                                          C=cw[un % NB], Ck='s5cw%d' % (un % NB), H=hb[un % NB], Hk='s5hb%d' % (un % NB)))
                        for u in U:
                            for ri in range(2):
                                mm(u['pb'][:, ri, :], btb[:, j, u['ii'], ri, :], ub[:, j, n0:n0 + L], ['btb', 's5u'], [u['pbk']])
                        yield
                        for u in U:
                            src = u['pb'][:, :, ::-1] if rev else u['pb'][:, :, :]
                            tt('dve', u['A'][:], src, QT[:, u['q'], 0:1, :].broadcast_to([128, 2, L]), ALU.mult,
                               [u['pbk'], 's5tab'], [u['Ak']])
                        yield
                        for u in U:
                            src = u['pb'][:, ::-1, ::-1] if rev else u['pb'][:, ::-1, :]
                            tt('dve', u['B'][:], src, QT[:, u['q'], 1:2, :].broadcast_to([128, 2, L]), ALU.mult,
                               [u['pbk'], 's5tab'], [u['Bk']])
                        yield
                        for u in U:
                            tt('dve', u['A'][:, 0, :], u['A'][:, 0, :], u['B'][:, 0, :], ALU.subtract, [u['Ak'], u['Bk']], [u['Ak']])
                        yield
                        for u in U:
                            tt('dve', u['A'][:, 1, :], u['A'][:, 1, :], u['B'][:, 1, :], ALU.add, [u['Ak'], u['Bk']], [u['Ak']])
                        yield
                        for ri in range(2):
                            for u in U:
                                q = u['q']
                                S.op('dve', lambda e, u=u, ri=ri, q=q: e.tensor_tensor_scan(
                                    out=u['C'][:, ri, :], data0=ones[:], data1=u['A'][:, ri, :], initial=sst[:, q, ri:ri + 1],
                                    op0=ALU.mult, op1=ALU.add), reads=[u['Ak'], 's5ones', 'sst%d' % q, 'sst'], writes=[u['Ck']])
                            yield
                        for u in U:
                            tt('pool', u['A'][:], u['C'][:], PT[:, u['q'], 0:1, :].broadcast_to([128, 2, L]), ALU.mult,
                               [u['Ck'], 's5tab', u['Ak']], [u['Ak']])
                        yield
                        for u in U:
                            tt('pool', u['B'][:], u['C'][:, ::-1, :], PT[:, u['q'], 1:2, :].broadcast_to([128, 2, L]), ALU.mult,
                               [u['Ck'], 's5tab', u['Bk']], [u['Bk']])
                        yield
                        for u in U:
                            tt('pool', u['A'][:, 0, :], u['A'][:, 0, :], u['B'][:, 0, :], ALU.subtract, [u['Ak'], u['Bk']], [u['Ak']])
                        yield
                        for u in U:
                            tt('pool', u['A'][:, 1, :], u['A'][:, 1, :], u['B'][:, 1, :], ALU.add, [u['Ak'], u['Bk']], [u['Ak']])
                        yield
                        for u in U:
                            cp('pool', sst[:, u['q'], :], u['A'][:, :, L - 1], [u['Ak']], ['sst%d' % u['q']])
                        yield
                        for u in U:
                            hsrc = u['A'][:, :, ::-1] if rev else u['A'][:]
                            cp('act', u['H'][:], hsrc, [u['Ak']], [u['Hk']])
                        yield
                        pyr = py[gi % 2][:, 0:L]
                        pyk = 's5py%d' % (gi % 2)
                        for k_, u in enumerate(U):
                            for ri in range(2):
                                mm(pyr, ctb[:, u['i'], ri, :], u['H'][:, ri, :], ['ctb', u['Hk']], [pyk],
                                   start=(k_ == 0 and ri == 0), stop=(k_ == 3 and ri == 1))
                        yield
                        yield
                        yield
                        tt('dve', yacc[:, j, n0:n0 + L], yacc[:, j, n0:n0 + L], pyr, ALU.add, [pyk, 'yacc'], ['yacc'])

                    glist = [(step, d, j) for step in range(NTL) for d in range(2) for j in range(2)]
                    run_pipelined((s5group(gi, *g) for gi, g in enumerate(glist)), 11)
                    S.barrier()
                for j in range(2):
                    stt(yacc[:, j, :], ub[:, j, :], pv('s5d', j), yacc[:, j, :], ALU.mult, ALU.add, ['s5u', 'yacc', 'pvt'],
                        ['yacc'])
                dbg_dump('ya%d' % l, yacc[:], [128, 2, NT], ['yacc'])
                with contextlib.ExitStack() as st2:
                    t1 = [sb(st2, "s5g1_%d" % i, [128, 512], F32) for i in range(2)]
                    t2 = [sb(st2, "s5g2_%d" % i, [128, 512], BF16) for i in range(2)]
                    pg = [ps(st2, "s5pg%d" % i, [128, 512], F32) for i in range(2)]
                    cnt = 0
                    for (n0, nn) in BLOCKS:
                        for j in range(2):
                            a, ak = t1[cnt % 2], 's5g1_%d' % (cnt % 2)
                            cnt += 1
                            ysl = yacc[:, j, n0:n0 + nn]
                            act(a[:, 0:nn], ysl, AF.Square, ['yacc'], [ak])
                            ts('dve', a[:, 0:nn], a[:, 0:nn], 0.044715, 1.0, ALU.mult, ALU.add, [ak], [ak])
                            tt('dve', a[:, 0:nn], a[:, 0:nn], ysl, ALU.mult, [ak, 'yacc'], [ak])
                            act(a[:, 0:nn], a[:, 0:nn], AF.Sigmoid, [ak], [ak], scale=1.5957691216057308)
                            tt('dve', ub[:, j, n0:n0 + nn], a[:, 0:nn], ysl, ALU.mult, [ak, 'yacc'], ['s5u'])
                    cnt = 0
                    for (n0, nn) in BLOCKS:
                        for m in range(2):
                            p_, pk_ = pg[cnt % 2], 's5pg%d' % (cnt % 2)
                            b_, bk_ = t2[cnt % 2], 's5g2_%d' % (cnt % 2)
                            cnt += 1
                            for jc in range(2):
                                mm(p_[:, 0:nn], glub[:, jc, m * 128:(m + 1) * 128], ub[:, jc, n0:n0 + nn], ['glub', 's5u'], [pk_],
                                   start=(jc == 0), stop=(jc == 1))
                            act(b_[:, 0:nn], p_[:, 0:nn], AF.Sigmoid, [pk_, 'pvt'], [bk_], bias=pv('glub', m))
                            tt('dve', b_[:, 0:nn], b_[:, 0:nn], ub[:, m, n0:n0 + nn], ALU.mult, [bk_, 's5u'], [bk_])
                            tt('pool', Y[:, 0, m, n0:n0 + nn], b_[:, 0:nn], zs[:, m, n0:n0 + nn], ALU.mult, [bk_, 's5z'], ['Y0'])
                    S.barrier()
                S.barrier()
        PHASES['s5'] = phase_s5
        def phase_hg(l, h_src, last):
            with contextlib.ExitStack() as st:
                QP = [sb(st, "hgQP%d" % d, [128, 2, NT], BF16) for d in range(2)]
                KP = [sb(st, "hgKP%d" % d, [128, 2, NT], BF16) for d in range(2)]
                G = sb(st, "hgG", [128, 2, 72, 2], F32)
                VT = sb(st, "hgVT", [128, NTL, 256], BF16)
                zs = sb(st, "hgzs", [128, 2, NT], BF16)
                lbt = sb(st, "hglbt", [128, 2, 4], F32)
                if l == 0:
                    memset('pool', lbt[:, 0, :], 0.0, ['hglbt'])
                    memset('pool', lbt[:, 1, :], 1.0, ['hglbt'])
                else:
                    o_, _ = PV['hglb']
                    tt('dve', lbt[:, 0, :], pvt[:, o_ + 4:o_ + 8], pvt[:, o_:o_ + 4], ALU.subtract, ['pvt'], ['hglbt'])
                    act(lbt[:, 0, :], lbt[:, 0, :], AF.Sigmoid, ['hglbt'], ['hglbt'])
                    ts('dve', lbt[:, 1, :], lbt[:, 0, :], -1.0, 1.0, ALU.mult, ALU.add, ['hglbt'], ['hglbt'])
                with contextlib.ExitStack() as st2:
                    wh = sb(st2, "hgw", [128, 8, 1280], BF16)
                    S.dma('pool', wh[:, :, 0:768], dr['w_in'][l][:, :, 512:1280], writes=['hgw'])
                    S.dma('pool', wh[:, :, 768:1280], dr['w_in'][l][:, :, 1280:1792], writes=['hgw'])
                    brow = sb(st2, "hgbrow", [128, 256], F32)
                    S.dma('sp', brow[:], dr['rows'][l][:, 4096:4352], writes=['hgbrow'])
                    R32 = sb(st2, "hgR32", [128, 512], F32)
                    memset('pool', R32[:], 1.0, ['hgR32'])
                    memset('pool', R32[:, 0:512:32], 0.0, ['hgR32'])
                    QS = [sb(st2, "hgQS%d" % i, [128, 2, 512], BF16) for i in range(2)]
                    T = [[sb(st2, "hgT%d_%d" % (i, k), [128, 512], F32) for k in range(4)] for i in range(2)]
                    pp = [ps(st2, "hgpp%d" % i, [128, 512], F32) for i in range(3)]
                    pt = [ps(st2, "hgpt%d" % i, [128, 512], F32) for i in range(2)]
                    def hgproj(cnt, ic, m, n0, nn):
                        ukeys = uTk[n0 // 128:(n0 + nn) // 128]
                        p_, pk_ = pp[cnt % 3], 'hgpp%d' % (cnt % 3)
                        bi = (n0 // 512) % 2 if n0 else 0
                        for jj in range(8):
                            mm(p_[:, 0:nn], wh[:, jj, m * 128:(m + 1) * 128], uT[:, jj, n0:n0 + nn], ['hgw'] + ukeys, [pk_],
                               start=(jj == 0), stop=(jj == 7))
                        yield
                        bias = pv('bin', 4 + m)
                        if m < 2:
                            act(QS[bi][:, m, 0:nn], p_[:, 0:nn], AF.Silu, [pk_, 'pvt'], ['hgQS%d' % bi], bias=bias)
                            return
                        if m >= 8:
                            act(zs[:, m - 8, n0:n0 + nn], p_[:, 0:nn], AF.Silu, [pk_, 'pvt'], ['hgzs'], bias=bias)
                            return
                        d, j = (m - 2) // 2, (m - 2) % 2
                        Ts = T[ic % 2]
                        Tk = ['hgT%d_%d' % (ic % 2, k) for k in range(4)]
                        t1, t2, t3, t4 = [x[:, 0:nn] for x in Ts]
                        act(t1, p_[:, 0:nn], AF.Sigmoid, [pk_, 'pvt'], [Tk[0]], bias=bias)
                        yield
                        ts('dve', t1, t1, lbt[:, 1, d * 2 + j:d * 2 + j + 1], lbt[:, 0, d * 2 + j:d * 2 + j + 1], ALU.mult, ALU.add,
                           [Tk[0], 'hglbt'], [Tk[0]])
                        yield
                        act(t2, t1, AF.Ln, [Tk[0]], [Tk[1]])
                        yield
                        if d == 0:
                            S.op('dve', lambda e: e.tensor_tensor_scan(out=t3, data0=R32[:, 0:nn], data1=t2, initial=0.0,
                                                                       op0=ALU.mult, op1=ALU.add),
                                 reads=[Tk[1], 'hgR32'], writes=[Tk[2]])
                        else:
                            S.op('dve', lambda e: e.tensor_tensor_scan(out=t3[:, ::-1],
                                                                       data0=R32[:, 0:nn], data1=t2[:, ::-1], initial=0.0,
                                                                       op0=ALU.mult, op1=ALU.add),
                                 reads=[Tk[1], 'hgR32'], writes=[Tk[2]])
                        yield
                        ts('dve', t3, t3, -80.0, None, ALU.max, None, [Tk[2]], [Tk[2]])
                        ts('dve', t1, t1, -1.0, 1.0, ALU.mult, ALU.add, [Tk[0]], [Tk[0]])
                        yield
                        act(t4, t3, AF.Exp, [Tk[2]], [Tk[3]])
                        act(t2, t3, AF.Exp, [Tk[2]], [Tk[1]], scale=-1.0)
                        yield
                        tt('pool', KP[d][:, j, n0:n0 + nn], t1, t2, ALU.mult, [Tk[0], Tk[1]], ['hgKP%d' % d])
                        tt('pool', QP[d][:, j, n0:n0 + nn], QS[bi][:, j, 0:nn], t4, ALU.mult, ['hgQS%d' % bi, Tk[3]], ['hgQP%d' % d])
                        c0 = n0 // 32
                        gsrc = t4[:, 31::32] if d == 0 else t4[:, 0::32]
                        cp('act', G[:, d, c0:c0 + nn // 32, j], gsrc, [Tk[3]], ['hgG'])

                    plist = []
                    cnt = 0
                    ic = 0
                    for (n0, nn) in BLOCKS:
                        for m in (0, 1, 8, 9, 2, 3, 4, 5):
                            plist.append((cnt, ic, m, n0, nn))
                            cnt += 1
                            if 2 <= m < 8:
                                ic += 1
                    run_pipelined((hgproj(*p) for p in plist), 4)
                    for t in range(NTL):
                        p_, pk_ = pt[t % 2], 'hgpt%d' % (t % 2)
                        for jj in range(8):
                            mm(p_[:, 0:256], uT[:, jj, t * 128:(t + 1) * 128], wh[:, jj, 768:1024], ['hgw', uTk[t]], [pk_],
                               start=(jj == 0), stop=(jj == 7))
                        tt('dve', VT[:, t, :], p_[:, 0:256], brow[:], ALU.add, [pk_, 'hgbrow'], ['hgVT'])
                    S.barrier()
                Sall = [sb(st, "hgSall%d" % d, [128, 2, 72, 64], BF16) for d in range(2)]
                with contextlib.ExitStack() as st2:
                    Sst = [sb(st2, "hgS%d" % d, [128, 2, 64], F32) for d in range(2)]
                    kTm = [sb(st2, "hgkTm%d" % i, [128, 4, 256], BF16) for i in range(3)]
                    Ug = [sb(st2, "hgUg%d" % i, [128, 4, 2, 64], F32) for i in range(3)]
                    ptr = [ps(st2, "hgptr%d" % i, [128, 8, 128], BF16) for i in range(2)]
                    pU = [ps(st2, "hgpU%d" % i, [128, 4, 2, 64], F32) for i in range(3)]
                    orders = [list(range(NTL)), [1, 0] + list(range(NTL - 1, 1, -1))]
                    for d in range(2):
                        memset('pool', Sst[d][:], 0.0, ['hgS%d' % d])
                    def hgchain(it, step, d):
                        t = orders[d][step]
                        pr, prk = ptr[it % 2], 'hgptr%d' % (it % 2)
                        km, kmk = kTm[it % 3], 'hgkTm%d' % (it % 3)
                        pu, puk = pU[it % 3], 'hgpU%d' % (it % 3)
                        ug, ugk = Ug[it % 3], 'hgUg%d' % (it % 3)
                        for j in range(2):
                            tr(pr[:, j, :], KP[d][:, j, t * 128:(t + 1) * 128], identb, ['hgKP%d' % d, 'cstb'], [prk])
                        yield
                        for cc in range(4):
                            prf = pr[:, 0:2, :].rearrange("p a b -> p (a b)")
                            if cc % 2 == 0:
                                ts('dve', km[:, cc, :], prf, cstf[:, 4, 64 + cc:64 + cc + 1], None, ALU.mult, None, [prk, 'cstf'], [kmk])
                            else:
                                act(km[:, cc, :], prf, AF.Identity, [prk, 'cstf'], [kmk], scale=cstf[:, 4, 64 + cc:64 + cc + 1])
                        yield
                        for cc in range(4):
                            for h in range(4):
                                hp = (h % 2) * 64
                                mm(pu[hp:hp + 64, cc, h // 2, :], km[:, cc, h * 64:(h + 1) * 64], VT[:, t, h * 64:(h + 1) * 64],
                                   [kmk, 'hgVT'], [puk])
                        yield
                        tt('dve', ug[:], pu[:], G[:, d, t * 4:(t + 1) * 4, :].unsqueeze(3).broadcast_to([128, 4, 2, 64]), ALU.mult,
                           [puk, 'hgG'], [ugk])
                        yield
                        ccs = range(4) if d == 0 else range(3, -1, -1)
                        for cc in ccs:
                            c = t * 4 + cc
                            cp('act', Sall[d][:, :, c, :], Sst[d][:], ['hgS%d' % d], ['hgSall%d_%d' % (d, t)])
                            for j in range(2):
                                stt(Sst[d][:, j, :], Sst[d][:, j, :], G[:, d, c, j:j + 1], ug[:, cc, j, :], ALU.mult, ALU.add,
                                    ['hgS%d' % d, 'hgG', ugk], ['hgS%d' % d])
                            yield

                    run_pipelined((hgchain(i_, sd[0], sd[1]) for i_, sd in enumerate([(s_, d_) for s_ in range(NTL) for d_ in range(2)])), 3)
                    S.barrier()
                with contextlib.ExitStack() as st2:
                    if ('yb%d' % l) in debug:
                        dbgbuf = sb(st2, "dbgbuf", [128, 2, NT], F32)
                    AT = [[sb(st2, "hgAT%d_%d" % (i, d), [128, 4, 128], BF16) for d in range(2)] for i in range(2)]
                    sq = [sb(st2, "hgsq%d" % i, [128, 2, 128], BF16) for i in range(2)]
                    rr = [sb(st2, "hgrr%d" % i, [128, 2, 128], F32) for i in range(2)]
                    ob = [sb(st2, "hgob%d" % i, [128, 2, 128], F32) for i in range(2)]
                    bank = mkbanks(st2, 8, "hgbk")

                    def hgout(t):
                        i2 = t % 2
                        tsl = slice(t * 128, (t + 1) * 128)
                        pas = {}
                        for d in range(2):
                            for par in range(2):
                                pas[(d, par)] = bank()
                            for h in range(4):
                                hp = (h % 2) * 64
                                pa, pak = pas[(d, h % 2)]
                                pav = pa[:, 0:256].rearrange("p (a b) -> p a b", a=2)
                                mm(pav[:, h // 2, :], KP[d][hp:hp + 64, h // 2, tsl], QP[d][hp:hp + 64, h // 2, tsl],
                                   ['hgKP%d' % d, 'hgQP%d' % d], [pak])
                        yield
                        for d in range(2):
                            for par in range(2):
                                pa, pak = pas[(d, par)]
                                pav = pa[:, 0:256].rearrange("p (a b) -> p a b", a=2)
                                tt('dve', AT[i2][d][:, par::2, :], pav, maskb[:, d, :].unsqueeze(1).broadcast_to([128, 2, 128]), ALU.mult,
                                   [pak, 'maskb'], ['hgAT%d_%d' % (i2, d)])
                        yield
                        pos = [bank() for _ in range(2)]
                        povs = [pos[par][0][:, 0:256].rearrange("p (a b) -> p a b", a=2) for par in range(2)]
                        for h in range(4):
                            hp = (h % 2) * 64
                            pok = pos[h % 2][1]
                            reg = povs[h % 2][hp:hp + 64, h // 2, :]
                            first = True
                            for d in range(2):
                                mm(reg, VT[:, t, h * 64:(h + 1) * 64], AT[i2][d][:, h, :], ['hgVT', 'hgAT%d_%d' % (i2, d)], [pok],
                                   start=first, stop=False)
                                first = False
                                for cc in range(4):
                                    c = t * 4 + cc
                                    mm(reg[:, cc * 32:(cc + 1) * 32], Sall[d][hp:hp + 64, h // 2, c, :],
                                       QP[d][hp:hp + 64, h // 2, t * 128 + cc * 32:t * 128 + (cc + 1) * 32],
                                       ['hgSall%d_%d' % (d, t), 'hgQP%d' % d], [pok], start=False, stop=(d == 1 and cc == 3))
                        yield
                        obk = 'hgob%d' % i2
                        cp('act', ob[i2][0:64], povs[0][0:64], [pos[0][1]], [obk])
                        cp('dve', ob[i2][64:128], povs[1][64:128], [pos[1][1]], [obk])
                        yield
                        pov = ob[i2][:]
                        pok = obk
                        if ('yb%d' % l) in debug:
                            cp('pool', dbgbuf[:, :, tsl], pov, [pok], ['dbgbuf'])
                        act(sq[i2][:], pov, AF.Square, [pok], ['hgsq%d' % i2])
                        yield
                        pss_, psk = bank()
                        psv = pss_[:, 0:256].rearrange("p (a b) -> p a b", a=2)
                        for j in range(2):
                            mm(psv[:, j, :], bonesb, sq[i2][:, j, :], ['cstb', 'hgsq%d' % i2], [psk])
                        yield
                        act(rr[i2][:], psv, AF.Sqrt, [psk], ['hgrr%d' % i2], bias=RMS_EPS, scale=1.0 / 64)
                        yield
                        S.op('dve', lambda e: e.reciprocal(out=rr[i2][:], in_=rr[i2][:]), reads=['hgrr%d' % i2], writes=['hgrr%d' % i2])
                        yield
                        tt('dve', rr[i2][:], pov, rr[i2][:], ALU.mult, [pok, 'hgrr%d' % i2], ['hgrr%d' % i2])
                        yield
                        for j in range(2):
                            stt(Y[:, 1, j, tsl], rr[i2][:, j, :], pv('hgnw', j), zs[:, j, tsl], ALU.mult, ALU.mult,
                                ['hgrr%d' % i2, 'pvt', 'hgzs'], ['Y1'])

                    run_pipelined((hgout(t) for t in range(NTL) if not (last and t < 2 and not debug)), 5)
                    if ('yb%d' % l) in debug:
                        dbg_dump('yb%d' % l, dbgbuf[:], [128, 2, NT], ['dbgbuf'])
                    S.barrier()
                S.barrier()
        PHASES['hg'] = phase_hg
        def phase_ret(l, h_src, last):
            with contextlib.ExitStack() as st:
                QR = sb(st, "rtQR", [128, 2, NT], BF16)
                KR = sb(st, "rtKR", [128, 2, NT], BF16)
                VT = sb(st, "rtVT", [128, NTL, 256], BF16)
                zs = sb(st, "rtzs", [128, 2, NT], BF16)
                Sall = [sb(st, "rtSall%d" % d, [128, 2, NTL, 64], BF16) for d in range(2)]
                LG = sb(st, "rtLG", [128, 4], F32)
                GL = sb(st, "rtGL", [128, 4], F32)
                LGH = sb(st, "rtLGH", [128, 8], F32)
                QDEC = sb(st, "rtQDEC", [128, 2, 2, 128], F32)
                KDEC = sb(st, "rtKDEC", [128, 2, 4], F32)
                DS = sb(st, "rtDS", [128, 4, 128], F32)
                tb8 = sb(st, "rtb8", [128, 2], F32)
                K_ = 'rttab'
                act(LG[:], pv('rdec'), AF.Exp, ['pvt'], [K_])
                ts('dve', LG[:], LG[:], -1.0, None, ALU.mult, None, [K_], [K_])
                act(GL[:], LG[:], AF.Exp, [K_], [K_], scale=128.0)
                act(LGH[:], pv('rdech'), AF.Exp, ['pvt'], [K_])
                ts('dve', LGH[:], LGH[:], -1.0, None, ALU.mult, None, [K_], [K_])
                for d in range(2):
                    for j in range(2):
                        act(QDEC[:, d, j, :], cstf[:, 5 + d, :], AF.Exp, ['cstf', K_], [K_], scale=LG[:, d * 2 + j:d * 2 + j + 1])
                    act(KDEC[:, d, :], LGH[:, d * 4:(d + 1) * 4], AF.Exp, ['cstf', K_], [K_], scale=cstf[:, 4, 68 + d:69 + d])
                with contextlib.ExitStack() as st2:
                    ta = sb(st2, "rtta", [128, 128], F32)
                    tb = sb(st2, "rttb", [128, 128], F32)
                    for h in range(4):
                        act(ta[:], cstf[:, 0, :], AF.Exp, ['cstf', K_], ['rtta'], scale=LGH[:, h:h + 1])
                        tt('dve', ta[:], ta[:], cstf[:, 2, :], ALU.mult, ['rtta', 'cstf'], ['rtta'])
                        act(tb[:], cstf[:, 1, :], AF.Exp, ['cstf', K_], ['rttb'], scale=LGH[:, 4 + h:5 + h])
                        tt('dve', tb[:], tb[:], cstf[:, 3, :], ALU.mult, ['rttb', 'cstf'], ['rttb'])
                        tt('dve', DS[:, h, :], ta[:], tb[:], ALU.add, ['rtta', 'rttb'], [K_])
                    ts('dve', tb8[:], pv('bin', 16, 2), 0.125, None, ALU.mult, None, ['pvt'], [K_])
                    S.barrier()
                if stop == 'ret_tab':
                    return
                with contextlib.ExitStack() as st2:
                    wr = sb(st2, "rtw", [128, 8, 1024], BF16)
                    S.dma('pool', wr[:], dr['w_in'][l][:, :, 1792:2816], writes=['rtw'])
                    brow = sb(st2, "rtbrow", [128, 256], F32)
                    S.dma('sp', brow[:], dr['rows'][l][:, 4352:4608], writes=['rtbrow'])
                    COS = sb(st2, "rtcos", [128, 2048], F32)
                    SIN = sb(st2, "rtsin", [128, 2048], F32)
                    permf = sb(st2, "rtperm", [128, 128], F32)
                    S.dma('sp', COS[:], dr['rcos'], writes=['rtcos'])
                    S.dma('act', SIN[:], dr['rsin'], writes=['rtsin'])
                    S.dma('sp', permf[:], dr['cst'][:, 0, :], writes=['rtperm'])
                    qf = [sb(st2, "rtqf%d" % i, [128, 512], F32) for i in range(2)]
                    t1 = [sb(st2, "rtt1_%d" % i, [128, 512], F32) for i in range(2)]
                    pp = [ps(st2, "rtpp%d" % i, [128, 512], F32) for i in range(2)]
                    pq = [ps(st2, "rtpq%d" % i, [128, 512], F32) for i in range(2)]
                    pt = [ps(st2, "rtpt%d" % i, [128, 512], F32) for i in range(2)]
                    def rtproj(cnt, rc, m, n0, nn):
                        ukeys = uTk[n0 // 128:(n0 + nn) // 128]
                        p_, pk_ = pp[cnt % 2], 'rtpp%d' % (cnt % 2)
                        for jj in range(8):
                            mm(p_[:, 0:nn], wr[:, jj, m * 128:(m + 1) * 128], uT[:, jj, n0:n0 + nn], ['rtw'] + ukeys, [pk_],
                               start=(jj == 0), stop=(jj == 7))
                        yield
                        if m >= 6:
                            act(zs[:, m - 6, n0:n0 + nn], p_[:, 0:nn], AF.Silu, [pk_, 'pvt'], ['rtzs'], bias=pv('bin', 14 + m))
                            return
                        isk = m >= 2
                        j = m % 2
                        dst = (KR if isk else QR)[:, j, n0:n0 + nn]
                        dk = 'rtKR' if isk else 'rtQR'
                        if n0 < 256:
                            if isk:
                                act(dst, p_[:, 0:nn], AF.Identity, [pk_, K_], [dk], bias=tb8[:, j:j + 1], scale=0.125)
                            else:
                                act(dst, p_[:, 0:nn], AF.Identity, [pk_, 'pvt'], [dk], bias=pv('bin', 14 + m))
                            return
                        q_, qk_ = qf[rc % 2], 'rtqf%d' % (rc % 2)
                        a_, ak_ = t1[rc % 2], 'rtt1_%d' % (rc % 2)
                        r_, rk_ = pq[rc % 2], 'rtpq%d' % (rc % 2)
                        if isk:
                            act(q_[:, 0:nn], p_[:, 0:nn], AF.Identity, [pk_, K_], [qk_], bias=tb8[:, j:j + 1], scale=0.125)
                        else:
                            act(q_[:, 0:nn], p_[:, 0:nn], AF.Identity, [pk_, 'pvt'], [qk_], bias=pv('bin', 14 + m))
                        yield
                        mm(r_[:, 0:nn], permf[:], q_[:, 0:nn], ['rtperm', qk_], [rk_])
                        yield
                        tsl = slice(n0 - 256, n0 - 256 + nn)
                        tt('dve', a_[:, 0:nn], r_[:, 0:nn], SIN[:, tsl], ALU.mult, [rk_, 'rtsin'], [ak_])
                        tt('pool', q_[:, 0:nn], q_[:, 0:nn], COS[:, tsl], ALU.mult, [qk_, 'rtcos'], [qk_])
                        yield
                        tt('dve', dst, a_[:, 0:nn], q_[:, 0:nn], ALU.add, [ak_, qk_], [dk])

                    plist = []
                    cnt = 0
                    rc = 0
                    for (n0, nn) in BLOCKS:
                        for m in (0, 1, 2, 3, 6, 7):
                            plist.append((cnt, rc, m, n0, nn))
                            cnt += 1
                            if m < 6 and n0 >= 256:
                                rc += 1
                    run_pipelined((rtproj(*p) for p in plist), 2)
                    for t in range(NTL):
                        p_, pk_ = pt[t % 2], 'rtpt%d' % (t % 2)
                        for jj in range(8):
                            mm(p_[:, 0:256], uT[:, jj, t * 128:(t + 1) * 128], wr[:, jj, 512:768], ['rtw', uTk[t]], [pk_],
                               start=(jj == 0), stop=(jj == 7))
                        tt('dve', VT[:, t, :], p_[:, 0:256], brow[:], ALU.add, [pk_, 'rtbrow'], ['rtVT'])
                    S.barrier()
                if stop == 'ret_proj':
                    return
                with contextlib.ExitStack() as st2:
                    Sst = [sb(st2, "rtS%d" % d, [128, 2, 64], F32) for d in range(2)]
                    kT = [sb(st2, "rtkT%d" % i, [128, 256], BF16) for i in range(3)]
                    ptr = [ps(st2, "rtptr%d" % i, [128, 8, 128], BF16) for i in range(2)]
                    pU = [ps(st2, "rtpU%d" % i, [128, 512], F32) for i in range(3)]
                    orders = [list(range(NTL)), [1, 0] + list(range(NTL - 1, 1, -1))]
                    for d in range(2):
                        memset('pool', Sst[d][:], 0.0, ['rtS%d' % d])
                    def rtchain(it, step, d):
                        t = orders[d][step]
                        pr, prk = ptr[it % 2], 'rtptr%d' % (it % 2)
                        kt, ktk = kT[it % 3], 'rtkT%d' % (it % 3)
                        pu, puk = pU[it % 3], 'rtpU%d' % (it % 3)
                        puv = pu[:, 0:128].rearrange("p (a b) -> p a b", a=2)
                        for j in range(2):
                            tr(pr[:, j, :], KR[:, j, t * 128:(t + 1) * 128], identb, ['rtKR', 'cstb'], [prk])
                        yield
                        tt('dve', kt[:].rearrange("p (h k) -> p h k", h=4), pr[:, 0:2, :].rearrange("p a (b k) -> p (a b) k", b=2),
                           KDEC[:, d, :].unsqueeze(2).broadcast_to([128, 4, 64]), ALU.mult, [prk, K_], [ktk])
                        yield
                        for h in range(4):
                            hp = (h % 2) * 64
                            mm(puv[hp:hp + 64, h // 2, :], kt[:, h * 64:(h + 1) * 64], VT[:, t, h * 64:(h + 1) * 64], [ktk, 'rtVT'], [puk])
                        yield
                        cp('act', Sall[d][:, :, t, :], Sst[d][:], ['rtS%d' % d], ['rtSall%d_%d' % (d, t)])
                        for j in range(2):
                            stt(Sst[d][:, j, :], Sst[d][:, j, :], GL[:, d * 2 + j:d * 2 + j + 1], puv[:, j, :], ALU.mult, ALU.add,
                                ['rtS%d' % d, K_, puk], ['rtS%d' % d])

                    run_pipelined((rtchain(i_, sd[0], sd[1]) for i_, sd in enumerate([(s_, d_) for s_ in range(NTL) for d_ in range(2)])), 2)
                    S.barrier()
                if stop == 'ret_chain':
                    return
                with contextlib.ExitStack() as st2:
                    if ('yc%d' % l) in debug:
                        dbgbuf = sb(st2, "dbgbuf", [128, 2, NT], F32)
                    AT = [sb(st2, "rtAT%d" % i, [128, 4, 128], BF16) for i in range(2)]
                    qd = [[sb(st2, "rtqd%d_%d" % (i, d), [128, 2, 128], BF16) for d in range(2)] for i in range(2)]
                    sq = [sb(st2, "rtsq%d" % i, [128, 2, 128], BF16) for i in range(2)]
                    rr = [sb(st2, "rtrr%d" % i, [128, 2, 128], F32) for i in range(2)]
                    ob = [sb(st2, "rtob%d" % i, [128, 2, 128], F32) for i in range(2)]
                    bank = mkbanks(st2, 8, "rtbk")

                    def rtout(t):
                        i2 = t % 2
                        tsl = slice(t * 128, (t + 1) * 128)
                        pas = [bank() for _ in range(2)]
                        for h in range(4):
                            hp = (h % 2) * 64
                            pav = pas[h % 2][0][:, 0:256].rearrange("p (a b) -> p a b", a=2)
                            mm(pav[:, h // 2, :], KR[hp:hp + 64, h // 2, tsl], QR[hp:hp + 64, h // 2, tsl], ['rtKR', 'rtQR'], [pas[h % 2][1]])
                        for d in range(2):
                            tt('pool', qd[i2][d][:], QR[:, :, tsl], QDEC[:, d, :, :], ALU.mult, ['rtQR', K_], ['rtqd%d_%d' % (i2, d)])
                        yield
                        for par in range(2):
                            pav = pas[par][0][:, 0:256].rearrange("p (a b) -> p a b", a=2)
                            tt('dve', AT[i2][:, par::2, :], pav, DS[:, par::2, :], ALU.mult, [pas[par][1], K_], ['rtAT%d' % i2])
                        yield
                        pos = [bank() for _ in range(2)]
                        povs = [pos[par][0][:, 0:256].rearrange("p (a b) -> p a b", a=2) for par in range(2)]
                        for h in range(4):
                            hp = (h % 2) * 64
                            pok = pos[h % 2][1]
                            reg = povs[h % 2][hp:hp + 64, h // 2, :]
                            mm(reg, VT[:, t, h * 64:(h + 1) * 64], AT[i2][:, h, :], ['rtVT', 'rtAT%d' % i2], [pok], start=True, stop=False)
                            for d in range(2):
                                mm(reg, Sall[d][hp:hp + 64, h // 2, t, :], qd[i2][d][hp:hp + 64, h // 2, :],
                                   ['rtSall%d_%d' % (d, t), 'rtqd%d_%d' % (i2, d)], [pok], start=False, stop=(d == 1))
                        yield
                        obk = 'rtob%d' % i2
                        cp('act', ob[i2][0:64], povs[0][0:64], [pos[0][1]], [obk])
                        cp('dve', ob[i2][64:128], povs[1][64:128], [pos[1][1]], [obk])
                        yield
                        pov = ob[i2][:]
                        pok = obk
                        if ('yc%d' % l) in debug:
                            cp('pool', dbgbuf[:, :, tsl], pov, [pok], ['dbgbuf'])
                        act(sq[i2][:], pov, AF.Square, [pok], ['rtsq%d' % i2])
                        yield
                        pss_, psk = bank()
                        psv = pss_[:, 0:256].rearrange("p (a b) -> p a b", a=2)
                        for j in range(2):
                            mm(psv[:, j, :], bonesb, sq[i2][:, j, :], ['cstb', 'rtsq%d' % i2], [psk])
                        yield
                        act(rr[i2][:], psv, AF.Sqrt, [psk], ['rtrr%d' % i2], bias=RMS_EPS, scale=1.0 / 64)
                        yield
                        S.op('dve', lambda e: e.reciprocal(out=rr[i2][:], in_=rr[i2][:]), reads=['rtrr%d' % i2], writes=['rtrr%d' % i2])
                        yield
                        tt('dve', rr[i2][:], pov, rr[i2][:], ALU.mult, [pok, 'rtrr%d' % i2], ['rtrr%d' % i2])
                        yield
                        tt('pool', Y[:, 2, :, tsl], rr[i2][:], zs[:, :, tsl], ALU.mult, ['rtrr%d' % i2, 'rtzs'], ['Y2'])

                    run_pipelined((rtout(t) for t in range(NTL) if not (last and t < 2 and not debug)), 5)
                    if ('yc%d' % l) in debug:
                        dbg_dump('yc%d' % l, dbgbuf[:], [128, 2, NT], ['dbgbuf'])
                    S.barrier()
                S.barrier()
        PHASES['ret'] = phase_ret
        def phase_rw(l, h_src, last):
            with contextlib.ExitStack() as st:
                RB = sb(st, "rwRB", [128, 2, NT], BF16)
                KB = sb(st, "rwKB", [128, 2, NT], BF16)
                VB = sb(st, "rwVB", [128, 2, NT], BF16)
                LB = sb(st, "rwLB", [128, NT], BF16)
                zs = sb(st, "rwzs", [128, 2, NT], BF16)
                vT = sb(st, "rwvT", [128, NTL, 256], BF16)
                lw2b = sb(st, "rwlw2", [128, 2, 256], BF16)
                S.dma('pool', lw2b[:], dr['lw2'][l], writes=['rwlw2'])
                oka = sb(st, "rwoka", [128, 2], F32)
                ts('dve', oka[:], pv('ka'), -1.0, 1.0, ALU.mult, ALU.add, ['pvt'], ['rwoka'])
                seen_b, seen_o = set(), set()
                with contextlib.ExitStack() as st2:
                    ww = sb(st2, "rww", [128, 8, 1152], BF16)
                    S.dma('pool', ww[:, :, 0:640], dr['w_in'][l][:, :, 2816:3456], writes=['rww'])
                    S.dma('pool', ww[:, :, 640:1152], dr['w_in'][l][:, :, 3456:3968], writes=['rww'])
                    XR = sb(st2, "rwXR", [128, NT + 4], F32)
                    XS = sb(st2, "rwXS", [128, NT], F32)
                    c0 = sb(st2, "rwc0", [128, 7], F32)
                    pp = [ps(st2, "rwpp%d" % i, [128, 512], F32) for i in range(3)]
                    ptr = [ps(st2, "rwptr%d" % i, [128, 8, 128], BF16) for i in range(2)]
                    o_mu, _ = PV['mu']
                    mu0, mu1 = pvt[:, o_mu:o_mu + 7], pvt[:, o_mu + 7:o_mu + 14]
                    tt('dve', c0[:], mu0, mu1, ALU.add, ['pvt'], ['rwc0'])
                    ts('dve', c0[:], c0[:], -1.0, 1.0, ALU.mult, ALU.add, ['rwc0'], ['rwc0'])
                    memset('pool', XR[:], 0.0, ['rwXR'])
                    cnt = 0
                    for m in range(9):
                        for (n0, nn) in BLOCKS:
                            p_, pk_ = pp[cnt % 3], 'rwpp%d' % (cnt % 3)
                            cnt += 1
                            for jj in range(8):
                                mm(p_[:, 0:nn], ww[:, jj, m * 128:(m + 1) * 128], uT[:, jj, n0:n0 + nn],
                                   ['rww'] + uTk[n0 // 128:(n0 + nn) // 128], [pk_], start=(jj == 0), stop=(jj == 7))
                            if m >= 7:
                                act(zs[:, m - 7, n0:n0 + nn], p_[:, 0:nn], AF.Silu, [pk_, 'pvt'], ['rwzs'], bias=pv('bin', 22 + m))
                            else:
                                xo = 1 if n0 < 256 else 3
                                act(XR[:, n0 + xo:n0 + xo + nn], p_[:, 0:nn], AF.Identity, [pk_, 'pvt'], ['rwXR'], bias=pv('bin', 22 + m))
                        if m >= 7:
                            continue
                        for (b0, ln, o0) in ((1, 256, 0), (259, 2048, 256)):
                            ts('dve', XS[:, o0:o0 + ln], XR[:, b0:b0 + ln], c0[:, m:m + 1], None, ALU.mult, None, ['rwXR', 'rwc0'], ['rwXS'])
                            stt(XS[:, o0:o0 + ln], XR[:, b0 - 1:b0 - 1 + ln], mu0[:, m:m + 1], XS[:, o0:o0 + ln], ALU.mult, ALU.add,
                                ['rwXR', 'pvt', 'rwXS'], ['rwXS'])
                            stt(XS[:, o0:o0 + ln], XR[:, b0 + 1:b0 + 1 + ln], mu1[:, m:m + 1], XS[:, o0:o0 + ln], ALU.mult, ALU.add,
                                ['rwXR', 'pvt', 'rwXS'], ['rwXS'])
                        if m < 6:
                            dstT, dk = [(RB, 'rwRB'), (KB, 'rwKB'), (VB, 'rwVB')][m // 2]
                            cp('act', dstT[:, m % 2, :], XS[:], ['rwXS'], [dk])
                        else:
                            act(LB[0:64, :], XS[0:64, :], AF.Tanh, ['rwXS'], ['rwLB'])
                            cp('pool', LB[64:128, :], XS[64:128, :], ['rwXS'], ['rwLB'])
                    for t in range(NTL):
                        pr, prk = ptr[t % 2], 'rwptr%d' % (t % 2)
                        for j in range(2):
                            tr(pr[:, j, :], VB[:, j, t * 128:(t + 1) * 128], identb, ['rwVB', 'cstb'], [prk])
                        cp('dve' if t % 2 == 0 else 'act', vT[:, t, :], pr[:, 0:2, :].rearrange("p a b -> p (a b)"), [prk], ['rwvT'])
                    S.barrier()
                if stop == 'rw_proj':
                    return
                OS = sb(st, "rwOS", [128, 2, NT], F32)
                with contextlib.ExitStack() as st2:
                    def B(name, shape, dt=BF16):
                        return sb(st2, "rw_" + name, shape, dt), "rw_" + name
                    R64, R64k = B("R64", [128, 256], F32)
                    memset('pool', R64[:], 1.0, [R64k])
                    memset('pool', R64[:, 0:256:64], 0.0, [R64k])
                    LW, LWk = B("LW", [128, 2, 128], F32)
                    SA, SAk = B("SA", [128, 2, 128], F32)
                    LGm, LGk = B("LG", [128, 2, 128], F32)
                    EG, EGk = B("EG", [128, 2, 128], F32)
                    ENG, ENGk = B("ENG", [128, 2, 128], F32)
                    EGM, EGMk = B("EGM", [128, 2, 128], F32)
                    U0, U0k = B("U0", [128, 2, 128], F32)
                    TA, TAk = B("TA", [128, 2, 128], F32)
                    TB_, TBk = B("TB", [128, 2, 128], F32)
                    SQ, SQk = B("SQ", [128, 2, 128])
                    RKD, RKDk = B("RKD", [128, 2, 128])
                    OBt = (None, None)
                    Zst = [B("Z%d" % d, [128, 2, 64], F32) for d in range(2)]
                    BUF = [dict() for _ in range(2)]
                    for d_ in range(2):
                        BUF[d_]['KKN'] = B("KKN_%d" % d_, [128, 2, 128])
                        BUF[d_]['KT'] = B("KT_%d" % d_, [128, 3, 2, 128])
                        BUF[d_]['RT'] = B("RT_%d" % d_, [128, 2, 128])
                        for j_ in range(2):
                            sfx = "_%d_%d" % (d_, j_)
                            SB = dict()
                            SB['TM'] = B("TM" + sfx, [128, 3, 128])
                            for nm_ in ('A1T', 'A2T', 'A3T', 'A4T', 'ALT', 'Tm', 'TTm', 'Xb', 'RHS', 'BYb'):
                                SB[nm_] = B(nm_ + sfx, [128, 2, 128])
                            SB['NY'] = B("NY" + sfx, [128, 2, 64])
                            SB['RH'] = B("RH" + sfx, [128, 128])
                            SB['GTb'] = B("GTb" + sfx, [128, 2, 128])
                            SB['ZLG'] = B("ZLG" + sfx, [128, 2, 64], F32)
                            SB['Z0b'] = B("Z0b" + sfx, [128, 2, 64])
                            BUF[d_][j_] = SB
                        BUF[d_]['GLt'] = B("GLt_%d" % d_, [128, 2, 2], F32)
                    banks = [ps(st2, "rwbank%d" % i, [128, 512], F32) for i in range(8)]
                    bcnt = [0]

                    def bank():
                        i = bcnt[0] % 8
                        bcnt[0] += 1
                        return banks[i], 'rwbank%d' % i
                    for d in range(2):
                        memset('pool', Zst[d][0][:], 0.0, [Zst[d][1], 'rw_Zs_%d_0' % d, 'rw_Zs_%d_1' % d])
                    for d_ in range(2):
                        for j_ in range(2):
                            memset('pool', BUF[d_][j_]['GTb'][0][:], 0.0, [BUF[d_][j_]['GTb'][1]])
                    orders = [list(range(NTL)), [1, 0] + list(range(NTL - 1, 1, -1))]
                    bc3 = lambda ap: ap.unsqueeze(2).broadcast_to([128, 2, 128])
                    def unit(d, t):
                        KKN, KKNk = BUF[d]['KKN']
                        KT, KTk = BUF[d]['KT']
                        RTb, RTk = BUF[d]['RT']
                        GLt, GLk = BUF[d]['GLt']
                        tsl = slice(t * 128, (t + 1) * 128)
                        rev = (d == 1)
                        Z, Zk = Zst[d]
                        plw, plwk = bank()
                        pla, plak = bank()
                        plwv = plw[:, 0:256].rearrange("p (j t) -> p j t", j=2)
                        plav = pla[:, 0:256].rearrange("p (j t) -> p j t", j=2)
                        wb_ = 32 * d
                        for j in range(2):
                            mm(plwv[:, j, :], lw2b[wb_:wb_ + 16, d, j * 128:(j + 1) * 128], LB[wb_:wb_ + 16, tsl], ['rwlw2', 'rwLB'], [plwk])
                        for j in range(2):
                            mm(plav[:, j, :], lw2b[64:96, d, j * 128:(j + 1) * 128], LB[64:96, tsl], ['rwlw2', 'rwLB'], [plak])
                        for j in range(2):
                            act(LW[:, j, :], plwv[:, j, :], AF.Sigmoid, [plwk, 'pvt'], [LWk], bias=pv('w0', d * 2 + j))
                            act(SA[:, j, :], plav[:, j, :], AF.Sigmoid, [plak, 'pvt'], [SAk], bias=pv('a0', d * 2 + j))
                        ts('dve', LW[:], LW[:], -0.6065306597126334, None, ALU.mult, None, [LWk], [LWk])
                        lwf = LW[:].rearrange("p a b -> p (a b)")
                        lgf = LGm[:].rearrange("p a b -> p (a b)")
                        if not rev:
                            S.op('dve', lambda e: e.tensor_tensor_scan(out=lgf, data0=R64[:], data1=lwf, initial=0.0, op0=ALU.mult, op1=ALU.add),
                                 reads=[LWk, R64k], writes=[LGk])
                        else:
                            S.op('dve', lambda e: e.tensor_tensor_scan(out=lgf[:, ::-1], data0=R64[:], data1=lwf[:, ::-1], initial=0.0,
                                                                       op0=ALU.mult, op1=ALU.add), reads=[LWk, R64k], writes=[LGk])
                        act(EG[:], LGm[:], AF.Exp, [LGk], [EGk])
                        act(ENG[:], LGm[:], AF.Exp, [LGk], [ENGk], scale=-1.0)
                        tt('pool', TA[:], LGm[:], LW[:], ALU.subtract, [LGk, LWk], [TAk])
                        act(EGM[:], TA[:], AF.Exp, [TAk], [EGMk])
                        gsrc = EG[:, :, 63::64] if not rev else EG[:, :, 0::64]
                        cp('pool', GLt[:], gsrc, [EGk], [GLk])
                        if stop == 'rw_u1':
                            return
                        tt('dve', TA[:], KB[:, :, tsl], bc3(pv('kk')), ALU.mult, ['rwKB', 'pvt', TAk], [TAk])
                        act(SQ[:], TA[:], AF.Square, [TAk], [SQk])
                        pss_, pssk = bank()
                        pssv = pss_[:, 0:256].rearrange("p (a b) -> p a b", a=2)
                        for j in range(2):
                            mm(pssv[:, j, :], bonesb, SQ[:, j, :], ['cstb', SQk], [pssk])
                        act(TB_[:], pssv, AF.Sqrt, [pssk], [TBk])
                        ts('dve', TB_[:], TB_[:], 1e-12, None, ALU.max, None, [TBk], [TBk])
                        S.op('dve', lambda e: e.reciprocal(out=TB_[:], in_=TB_[:]), reads=[TBk], writes=[TBk])
                        tt('dve', KKN[:], TA[:], TB_[:], ALU.mult, [TAk, TBk], [KKNk])
                        if stop == 'rw_u2':
                            return
                        tt('pool', KT[:, 0], KKN[:], EGM[:], ALU.mult, [KKNk, EGMk], [KTk])
                        tt('dve', TA[:], SA[:], ENG[:], ALU.mult, [SAk, ENGk, TAk], [TAk])
                        tt('pool', KT[:, 1], KKN[:], TA[:], ALU.mult, [KKNk, TAk], [KTk])
                        tt('dve', U0[:], SA[:], bc3(pv('ka')), ALU.mult, [SAk, 'pvt'], [U0k])
                        tt('dve', U0[:], U0[:], bc3(oka[:]), ALU.add, [U0k, 'rwoka'], [U0k])
                        tt('pool', TB_[:], U0[:], ENG[:], ALU.mult, [U0k, ENGk, TBk], [TBk])
                        tt('pool', KT[:, 2], KB[:, :, tsl], TB_[:], ALU.mult, ['rwKB', TBk], [KTk])
                        tt('dve', RTb[:], RB[:, :, tsl], EG[:], ALU.mult, ['rwRB', EGk], [RTk])
                        tt('dve', U0[:], U0[:], KB[:, :, tsl], ALU.mult, [U0k, 'rwKB'], [U0k])
                        tt('dve', U0[:], U0[:], bc3(pv('rk')), ALU.mult, [U0k, 'pvt'], [U0k])
                        tt('pool', RKD[:], U0[:], RB[:, :, tsl], ALU.mult, [U0k, 'rwRB'], [RKDk])
                        pbn, pbnk = bank()
                        pbnv = pbn[:, 0:256].rearrange("p (a b) -> p a b", a=2)
                        for j in range(2):
                            mm(pbnv[:, j, :], bonesb, RKD[:, j, :], ['cstb', RKDk], [pbnk])
                        if t not in seen_b:
                            seen_b.add(t)
                            tt('dve', Y[:, 3, :, tsl], pbnv, VB[:, :, tsl], ALU.mult, [pbnk, 'rwVB'], ['Y3'])
                        else:
                            tt('dve', TA[:], pbnv, VB[:, :, tsl], ALU.mult, [pbnk, 'rwVB', TAk], [TAk])
                            tt('pool', Y[:, 3, :, tsl], Y[:, 3, :, tsl], TA[:], ALU.add, ['Y3', TAk], ['Y3'])
                        if stop == 'rw_u3':
                            return
                        subs = [stream(d, j, t, rev, tsl, KT, KTk, RTb, RTk, GLt, GLk) for j in range(2)]
                        while subs:
                            for g in list(subs):
                                try:
                                    next(g)
                                except StopIteration:
                                    subs.remove(g)
                                yield

                    def stream(d, j, t, rev, tsl, KT, KTk, RTb, RTk, GLt, GLk):
                        SB = BUF[d][j]
                        TM, TMk = SB['TM']
                        A1T, A1k = SB['A1T']
                        A2T, A2k = SB['A2T']
                        A3T, A3k = SB['A3T']
                        A4T, A4k = SB['A4T']
                        ALT, ALk = SB['ALT']
                        Tm, Tmk = SB['Tm']
                        TTm, TTk = SB['TTm']
                        Xb, Xbk = SB['Xb']
                        RHS, RHSk = SB['RHS']
                        BYb, BYk = SB['BYb']
                        NY, NYk = SB['NY']
                        RH, RHk = SB['RH']
                        GTb, GTk = SB['GTb']
                        ZLG, ZLGk = SB['ZLG']
                        Z0b, Z0k = SB['Z0b']
                        Z, _zk = Zst[d]
                        Zk = 'rw_Zs_%d_%d' % (d, j)
                        ptb, ptbk = bank()
                        ptv = ptb[:].bitcast(BF16).rearrange("p (a b) -> p a b", a=8)
                        for x in range(3):
                            tr(ptv[:, x, :], KT[:, x, j, :], identb, [KTk, 'cstb'], [ptbk])
                        yield
                        cp('act', TM[:], ptv[:, 0:3, :], [ptbk], [TMk])
                        yield

                        def amat(dst, dstk, li, ri_src, ri_k, mslot):
                            pas = []
                            for par in range(2):
                                hp = par * 64
                                pa, pak = bank()
                                rhs = (RTb[hp:hp + 64, j, :] if ri_src is None else KT[hp:hp + 64, ri_src, j, :])
                                mm(pa[:, 0:128], KT[hp:hp + 64, li, j, :], rhs, [KTk, ri_k], [pak])
                                pas.append((pa, pak))
                            return pas

                        def aevac(pas, dst, dstk, mslot):
                            for par, (pa, pak) in enumerate(pas):
                                if mslot is None:
                                    cp('act', dst[:, par, :], pa[:, 0:128], [pak], [dstk])
                                else:
                                    tt('dve', dst[:, par, :], pa[:, 0:128], maskb[:, mslot, :], ALU.mult, [pak, 'maskb'], [dstk])
                        for (dst, dstk, li, rs, rk, ms) in ((A1T, A1k, 1, 0, KTk, None), (A2T, A2k, 2, 0, KTk, 2 + d),
                                                            (A3T, A3k, 1, None, RTk, 4 + d), (A4T, A4k, 2, None, RTk, 4 + d)):
                            pas = amat(dst, dstk, li, rs, rk, ms)
                            yield
                            aevac(pas, dst, dstk, ms)
                            yield
                        idb2 = identb.unsqueeze(1).broadcast_to([128, 2, 128])
                        cp('pool', Tm[:], idb2, ['cstb'], [Tmk])
                        cp('pool', TTm[:], idb2, ['cstb'], [TTk])
                        for lv in range(6):
                            tt('pool', ALT[:], A1T[:], maskb[:, 6 + d * 6 + lv, :].unsqueeze(1).broadcast_to([128, 2, 128]), ALU.mult,
                               [A1k, 'maskb'], [ALk])
                            yield
                            px, pxk = bank()
                            pxv = px[:, 0:256].rearrange("p (h t) -> p h t", h=2)
                            for par in range(2):
                                mm(pxv[:, par, :], ALT[:, par, :], Tm[:, par, :], [ALk, Tmk], [pxk])
                            yield
                            cp('act', Xb[:], pxv, [pxk], [Xbk])
                            yield
                            py_, pyk = bank()
                            pyv = py_[:].rearrange("p (x h t) -> p x h t", x=2, h=2)
                            for par in range(2):
                                mm(pyv[:, 0, par, :], Xb[:, par, :], TTm[:, par, :], [Xbk, TTk], [pyk])
                            if lv < 5:
                                for par in range(2):
                                    mm(pyv[:, 1, par, :], TTm[:, par, :], Xb[:, par, :], [Xbk, TTk], [pyk])
                            yield
                            if lv < 5:
                                tt('dve', Tm[:], Tm[:], pyv[:, 1], ALU.subtract, [Tmk, pyk], [Tmk])
                            tt('dve', TTm[:], TTm[:], pyv[:, 0], ALU.subtract, [TTk, pyk], [TTk])
                            yield
                        pw, pwk = bank()
                        pwv = pw[:, 0:128].rearrange("p (h v) -> p h v", h=2)
                        for par in range(2):
                            h = 2 * j + par
                            mm(pwv[:, par, :], A2T[:, par, :], vT[:, t, h * 64:(h + 1) * 64], [A2k, 'rwvT'], [pwk])
                        cp('pool', RHS[:, :, 0:64], TM[:, 0, :].rearrange("p (h k) -> p h k", h=2), [TMk], [RHSk])
                        yield
                        cp('act', RHS[:, :, 64:128], pwv, [pwk], [RHSk])
                        yield
                        pby, pbyk = bank()
                        pbyv = pby[:, 0:256].rearrange("p (h t) -> p h t", h=2)
                        for par in range(2):
                            mm(pbyv[:, par, :], TTm[:, par, :], RHS[:, par, :], [TTk, RHSk], [pbyk])
                        yield
                        cp('act', BYb[:], pbyv, [pbyk], [BYk])
                        yield
                        ts('pool', NY[:], BYb[:, :, 64:128], -1.0, 0.0, ALU.mult, ALU.add, [BYk], [NYk])
                        pr_, prk = bank()
                        for par in range(2):
                            hp = par * 64
                            mm(pr_[hp:hp + 64, 0:128], BYb[:, par, 0:64], A3T[:, par, :], [BYk, A3k], [prk])
                        yield
                        tt('dve', RH[:], RTb[:, j, :], pr_[:, 0:128], ALU.subtract, [RTk, prk], [RHk])
                        yield
                        for c in range(2):
                            cs = slice(c * 64, (c + 1) * 64)
                            pg_, pgk = bank()
                            pgv = pg_[:, 0:128].rearrange("p (x v) -> p x v", x=2)
                            for par in range(2):
                                hp = par * 64
                                h = 2 * j + par
                                hc = slice(h * 64, (h + 1) * 64)
                                pc = slice(par * 64, (par + 1) * 64)
                                mm(pgv[hp:hp + 64, 0, :], BYb[cs, par, 0:64], TM[cs, 1, pc], [BYk, TMk], [pgk])
                                mm(pgv[hp:hp + 64, 1, :], TM[cs, 2, pc], vT[cs, t, hc], [TMk, 'rwvT'], [pgk], start=True, stop=False)
                                mm(pgv[hp:hp + 64, 1, :], TM[cs, 1, pc], NY[cs, par, :], [TMk, NYk], [pgk], start=False, stop=True)
                            yield
                            for par in range(2):
                                hp = par * 64
                                tt('dve', GTb[hp:hp + 64, c, hp:hp + 64], cstf[hp:hp + 64, 4, 0:64], pgv[hp:hp + 64, 0, :], ALU.subtract,
                                   ['cstf', pgk], [GTk])
                            ts('dve', ZLG[:, c, :], pgv[:, 1, :], GLt[:, j, c:c + 1], None, ALU.mult, None, [pgk, GLk], [ZLGk])
                            yield
                        for c in ((0, 1) if not rev else (1, 0)):
                            cp('act', Z0b[:, c, :], Z[:, j, :], [Zk], [Z0k])
                            yield
                            pn, pnk = bank()
                            mm(pn[:, 0:64], GTb[:, c, :], Z0b[:, c, :], [GTk, Z0k], [pnk])
                            yield
                            stt(Z[:, j, :], pn[:, 0:64], GLt[:, j, c:c + 1], ZLG[:, c, :], ALU.mult, ALU.add, [pnk, GLk, ZLGk, Zk], [Zk])
                            yield
                        for par in range(2):
                            hp = par * 64
                            h = 2 * j + par
                            hc = slice(h * 64, (h + 1) * 64)
                            po_, pok = bank()
                            reg = po_[hp:hp + 64, 0:128]
                            mm(reg, vT[:, t, hc], A4T[:, par, :], ['rwvT', A4k], [pok], start=True, stop=False)
                            mm(reg, NY[:, par, :], A3T[:, par, :], [NYk, A3k], [pok], start=False, stop=False)
                            for c in range(2):
                                mm(reg[:, c * 64:(c + 1) * 64], Z0b[hp:hp + 64, c, :], RH[hp:hp + 64, c * 64:(c + 1) * 64],
                                   [Z0k, RHk], [pok], start=False, stop=(c == 1))
                            yield
                            osl = OS[hp:hp + 64, j, tsl]
                            osk = 'rwOS%d_%d' % (t, j)
                            if (t, j, par) not in seen_o:
                                seen_o.add((t, j, par))
                                cp('dve' if par == 0 else 'act', osl, reg, [pok], [osk])
                            else:
                                tt('dve', osl, osl, reg, ALU.add, [pok, osk], [osk])
                            yield

                    for step in range(NTL):
                        if stop is not None and stop.startswith('rw_u') and step >= 1:
                            break
                        gens = [unit(d, orders[d][step]) for d in range(2)]
                        while gens:
                            for g in list(gens):
                                try:
                                    next(g)
                                except StopIteration:
                                    gens.remove(g)
                    S.barrier()
                if stop is not None and stop.startswith('rw_'):
                    return
                with contextlib.ExitStack() as st2:
                    ob = [sb(st2, "rwob%d" % i, [128, 2, 128], BF16) for i in range(2)]
                    cen = [sb(st2, "rwcen%d" % i, [128, 2, 128], F32) for i in range(2)]
                    rs = [sb(st2, "rwrs%d" % i, [128, 2, 128], F32) for i in range(2)]
                    pm_ = [ps(st2, "rwpm%d" % i, [128, 512], F32) for i in range(2)]
                    pv_ = [ps(st2, "rwpv%d" % i, [128, 512], F32) for i in range(2)]
                    for t in range(NTL):
                        i2 = t % 2
                        tsl = slice(t * 128, (t + 1) * 128)
                        osk = 'rwOS%d_0' % t
                        osk1 = 'rwOS%d_1' % t
                        cp('act', ob[i2][:], OS[:, :, tsl], [osk, osk1], ['rwob%d' % i2])
                        pmv = pm_[i2][:, 0:256].rearrange("p (a b) -> p a b", a=2)
                        for j in range(2):
                            mm(pmv[:, j, :], bonesb, ob[i2][:, j, :], ['cstb', 'rwob%d' % i2], ['rwpm%d' % i2])
                        stt(cen[i2][:], pmv, -1.0 / 64, OS[:, :, tsl], ALU.mult, ALU.add, ['rwpm%d' % i2, osk, osk1], ['rwcen%d' % i2])
                        act(ob[i2][:], cen[i2][:], AF.Square, ['rwcen%d' % i2], ['rwob%d' % i2])
                        pvv = pv_[i2][:, 0:256].rearrange("p (a b) -> p a b", a=2)
                        for j in range(2):
                            mm(pvv[:, j, :], bonesb, ob[i2][:, j, :], ['cstb', 'rwob%d' % i2], ['rwpv%d' % i2])
                        act(rs[i2][:], pvv, AF.Sqrt, ['rwpv%d' % i2], ['rwrs%d' % i2], bias=RW_GN_EPS, scale=1.0 / 64)
                        S.op('dve', lambda e: e.reciprocal(out=rs[i2][:], in_=rs[i2][:]), reads=['rwrs%d' % i2], writes=['rwrs%d' % i2])
                        tt('dve', cen[i2][:], cen[i2][:], rs[i2][:], ALU.mult, ['rwcen%d' % i2, 'rwrs%d' % i2], ['rwcen%d' % i2])
                        tt('pool', cen[i2][:], cen[i2][:], bc3(pv('gnw')), ALU.mult, ['rwcen%d' % i2, 'pvt'], ['rwcen%d' % i2])
                        tt('pool', cen[i2][:], cen[i2][:], bc3(pv('gnb')), ALU.add, ['rwcen%d' % i2, 'pvt'], ['rwcen%d' % i2])
                        tt('dve', cen[i2][:], cen[i2][:], Y[:, 3, :, tsl], ALU.add, ['rwcen%d' % i2, 'Y3'], ['rwcen%d' % i2])
                        if ('yd%d' % l) in debug:
                            cp('act', OS[:, :, tsl], cen[i2][:], ['rwcen%d' % i2], [osk, osk1])
                        tt('dve', Y[:, 3, :, tsl], cen[i2][:], zs[:, :, tsl], ALU.mult, ['rwcen%d' % i2, 'rwzs'], ['Y3'])
                    if ('yd%d' % l) in debug:
                        dbg_dump('yd%d' % l, OS[:], [128, 2, NT], ['rwOS%d_%d' % (t, j_) for t in range(NTL) for j_ in range(2)])
                    S.barrier()
                S.barrier()
        PHASES['rw'] = phase_rw
        def phase_merge(l, h_src, last):
            h_dst = out_d if last else h1_d
            with contextlib.ExitStack() as st:
                MG = sb(st, "mgMG", [128, 8, NT], BF16)
                wbr = sb(st, "mgwbr", [128, 4, 2, DM], BF16)
                S.dma('pool', wbr[:], dr['wbr'][l], writes=['mgwbr'])
                with contextlib.ExitStack() as st2:
                    wg = [sb(st2, "mgwg%d" % i, [128, 8, 4, 128], BF16) for i in range(2)]
                    sg = [sb(st2, "mgsg%d" % i, [128, 512], BF16) for i in range(3)]
                    ac = [sb(st2, "mgac%d" % i, [128, 512], F32) for i in range(2)]
                    tm = [sb(st2, "mgtm%d" % i, [128, 512], F32) for i in range(2)]
                    pgl = [ps(st2, "mgpg%d" % i, [128, 512], F32) for i in range(3)]
                    pbr = [ps(st2, "mgpb%d" % i, [128, 512], F32) for i in range(3)]
                    cg = 0
                    ca = 0
                    for dt_ in range(8):
                        w_, wk_ = wg[dt_ % 2], 'mgwg%d' % (dt_ % 2)
                        for k in range(4):
                            c0 = 3968 + k * 1024 + dt_ * 128
                            S.dma('pool', w_[:, :, k, :], dr['w_in'][l][:, :, c0:c0 + 128], writes=[wk_])
                        for (n0, nn) in BLOCKS:
                            if last and n0 < 256:
                                continue
                            a_, ak_ = ac[ca % 2], 'mgac%d' % (ca % 2)
                            t_, tk_ = tm[ca % 2], 'mgtm%d' % (ca % 2)
                            ca += 1
                            for k in range(4):
                                pg_, pgk_ = pgl[cg % 3], 'mgpg%d' % (cg % 3)
                                pb_, pbk_ = pbr[cg % 3], 'mgpb%d' % (cg % 3)
                                s_, sk_ = sg[cg % 3], 'mgsg%d' % (cg % 3)
                                cg += 1
                                for jj in range(8):
                                    mm(pg_[:, 0:nn], w_[:, jj, k, :], uT[:, jj, n0:n0 + nn], [wk_] + uTk[n0 // 128:(n0 + nn) // 128], [pgk_],
                                       start=(jj == 0), stop=(jj == 7))
                                act(s_[:, 0:nn], pg_[:, 0:nn], AF.Sigmoid, [pgk_, 'pvt'], [sk_], bias=pv('bin', 31 + k * 8 + dt_))
                                for jc in range(2):
                                    mm(pb_[:, 0:nn], wbr[:, k, jc, dt_ * 128:(dt_ + 1) * 128], Y[:, k, jc, n0:n0 + nn], ['mgwbr', 'Y%d' % k], [pbk_],
                                       start=(jc == 0), stop=(jc == 1))
                                if k == 0:
                                    tt('dve', a_[:, 0:nn], pb_[:, 0:nn], s_[:, 0:nn], ALU.mult, [pbk_, sk_], [ak_])
                                else:
                                    tt('dve', t_[:, 0:nn], pb_[:, 0:nn], s_[:, 0:nn], ALU.mult, [pbk_, sk_], [tk_])
                                    if k < 3:
                                        tt('pool', a_[:, 0:nn], a_[:, 0:nn], t_[:, 0:nn], ALU.add, [ak_, tk_], [ak_])
                                    else:
                                        tt('pool', MG[:, dt_, n0:n0 + nn], a_[:, 0:nn], t_[:, 0:nn], ALU.add, [ak_, tk_], ['mgMG%d' % (n0 // 512 if n0 else 9)])
                    S.barrier()
                if ('merged%d' % l) in debug:
                    with contextlib.ExitStack() as st2:
                        mf = sb(st2, "mgf", [128, 8, NT], F32)
                        cp('dve', mf[:], MG[:], ['mgMG%d' % i for i in (9, 0, 1, 2, 3)], ['mgf'])
                        dbg_dump('merged%d' % l, mf[:], [128, 8, NT], ['mgf'])
                        S.barrier()
                with contextlib.ExitStack() as st2:
                    wo = sb(st2, "mgwo", [128, 8, DM], BF16)
                    S.dma('pool', wo[:], dr['wout'][l], writes=['mgwo'])
                    rows = sb(st2, "mgrows", [128, 3, DM], F32)
                    S.dma('sp', rows[:], dr['rows'][l][:, 0:3072].rearrange("p (a b) -> p a b", a=3), writes=['mgrows'])
                    hin_ = [sb(st2, "mghin%d" % i, [128, DM], F32) for i in range(2)]
                    ot = [sb(st2, "mgot%d" % i, [128, DM], F32) for i in range(2)]
                    stat = [sb(st2, "mgst%d" % i, [128, 16], F32) for i in range(2)]
                    po = [[ps(st2, "mgpo%d_%d" % (i, hh), [128, 512], F32) for hh in range(2)] for i in range(2)]
                    it = 0
                    for t in range(NTL):
                        if last and t < 2:
                            continue
                        i2 = it % 2
                        it += 1
                        ci = 1 if t < 2 else 0
                        tsl = slice(t * 128, (t + 1) * 128)
                        mgk = 'mgMG%d' % (9 if t < 2 else (t - 2) // 4)
                        hk_, ok_, sk_ = 'mghin%d' % i2, 'mgot%d' % i2, 'mgst%d' % i2
                        hi, o_, sti = hin_[i2], ot[i2], stat[i2]
                        S.dma('sp', hi[:], h_src[t * 128:(t + 1) * 128, :], writes=[hk_])
                        for hh in range(2):
                            pk_ = 'mgpo%d_%d' % (i2, hh)
                            for jj in range(8):
                                mm(po[i2][hh][:], MG[:, jj, tsl], wo[:, jj, hh * 512:(hh + 1) * 512], [mgk, 'mgwo'], [pk_], start=(jj == 0), stop=(jj == 7))
                            tt('dve', o_[:, hh * 512:(hh + 1) * 512], po[i2][hh][:], rows[:, 0, hh * 512:(hh + 1) * 512], ALU.add, [pk_, 'mgrows'], [ok_])
                        tt('pool', o_[:], o_[:], gatebc[:, ci, :], ALU.mult, [ok_, 'gatebc'], [ok_])
                        stt(o_[:], hi[:], ALPHA, o_[:], ALU.mult, ALU.add, [hk_, ok_], [ok_])
                        S.op('dve', lambda e: e.bn_stats(out=sti[:, 0:6], in_=o_[:, 0:512]), reads=[ok_], writes=[sk_])
                        S.op('dve', lambda e: e.bn_stats(out=sti[:, 6:12], in_=o_[:, 512:1024]), reads=[ok_], writes=[sk_])
                        S.op('dve', lambda e: e.bn_aggr(out=sti[:, 12:14], in_=sti[:, 0:12]), reads=[sk_], writes=[sk_])
                        act(sti[:, 14:15], sti[:, 13:14], AF.Sqrt, [sk_], [sk_], bias=LN_EPS)
                        S.op('dve', lambda e: e.reciprocal(out=sti[:, 14:15], in_=sti[:, 14:15]), reads=[sk_], writes=[sk_])
                        stt(sti[:, 15:16], sti[:, 12:13], -1.0, sti[:, 14:15], ALU.mult, ALU.mult, [sk_], [sk_])
                        act(o_[:], o_[:], AF.Identity, [ok_, sk_], [ok_], bias=sti[:, 15:16], scale=sti[:, 14:15])
                        tt('pool', o_[:], o_[:], rows[:, 1, :], ALU.mult, [ok_, 'mgrows'], [ok_])
                        tt('dve', o_[:], o_[:], rows[:, 2, :], ALU.add, [ok_, 'mgrows'], [ok_])
                        if last:
                            S.dma('sp', out_d[(t - 2) * 128:(t - 1) * 128, :], o_[:], reads=[ok_], writes=['outfinal'])
                        else:
                            S.dma('sp', h1_d[t * 128:(t + 1) * 128, :], o_[:], reads=[ok_], writes=['h1'])
                    S.barrier()
                S.barrier()
        PHASES['merge'] = phase_merge
        for l in range(nlayers):
            last = (l == nlayers - 1)
            h_src = dr['hin'] if l == 0 else h1_d
            S.dma('sp', pvt[:], dr['pv'][l], writes=['pvt'])
            with contextlib.ExitStack() as st:
                adw = [sb(st, "adw%d" % i, [128, 8, 512], F32) for i in range(2)]
                scb = sb(st, "scb", [128, 2, 8, 128], F32)
                grow = sb(st, "grow", [128, DM], F32)
                pm0 = ps(st, "pm0", [128, 16, 2], F32)
                pg = [ps(st, "pg%d" % i, [128, 512], F32) for i in range(2)]
                for i in range(2):
                    cp('dve', scb[:, i], silc[:, :, i:i + 1].broadcast_to([128, 8, 128]), ['silc'], ['scb'])
                S.dma('sp', grow[:], dr['rows'][l][:, 3072:4096], writes=['grow'])
                for ch in range(6):
                    buf = adw[ch % 2]
                    bk = 'adw%d' % (ch % 2)
                    S.dma('sp' if ch % 2 == 0 else 'act', buf[:], dr['ada_w'][l][:, :, ch * 512:(ch + 1) * 512], writes=[bk])
                    if ch < 4:
                        for mloc in range(4):
                            m = ch * 4 + mloc
                            for j in range(8):
                                mm(pm0[:, m, :], buf[:, j, mloc * 128:(mloc + 1) * 128], silc[:, j, :], [bk, 'silc'],
                                   ['pm0'], start=(j == 0), stop=(j == 7))
                    else:
                        for i in range(2):
                            for j in range(8):
                                mm(pg[i][:], scb[:, i, j, :], buf[:, j, :], [bk, 'scb'], ['pg%d' % i],
                                   start=(j == 0), stop=(j == 7))
                            tt('dve', gatebc[:, i, (ch - 4) * 512:(ch - 3) * 512], pg[i][:],
                               grow[:, (ch - 4) * 512:(ch - 3) * 512], ALU.add, ['pg%d' % i, 'grow'], ['gatebc'])
                tt('dve', modfm[:], pm0[:], pv('adab').unsqueeze(2).broadcast_to([128, 16, 2]), ALU.add,
                   ['pm0', 'pvt'], ['modfm'])
                ts('dve', modfm[:, 8:16, :], modfm[:, 8:16, :], 1.0, None, ALU.add, None, ['modfm'], ['modfm'])
                dbg_dump('modfm%d' % l, modfm[:], [128, 16, 2], ['modfm'])
                dbg_dump('gatebc%d' % l, gatebc[:], [128, 2, DM], ['gatebc'])
                S.barrier()
            with contextlib.ExitStack() as st:
                xin = [sb(st, "xin%d" % i, [128, DM], F32) for i in range(3)]
                xn = [sb(st, "xn%d" % i, [128, DM], BF16) for i in range(2)]
                stat = [sb(st, "stat%d" % i, [128, 16], F32) for i in range(3)]
                ptr = [ps(st, "ptr%d" % i, [128, 8, 128], BF16) for i in range(2)]
                def p1tile(t):
                    xi, xk = xin[t % 3], 'xin%d' % (t % 3)
                    sti, sk = stat[t % 3], 'stat%d' % (t % 3)
                    xo, xok = xn[t % 2], 'xn%d' % (t % 2)
                    pt, ptk = ptr[t % 2], 'ptr%d' % (t % 2)
                    ci = 1 if t < 2 else 0
                    S.dma('sp' if t % 2 == 0 else 'act', xi[:], h_src[t * 128:(t + 1) * 128, :], writes=[xk])
                    yield
                    S.op('dve', lambda e: e.bn_stats(out=sti[:, 0:6], in_=xi[:, 0:512]), reads=[xk], writes=[sk])
                    S.op('dve', lambda e: e.bn_stats(out=sti[:, 6:12], in_=xi[:, 512:1024]), reads=[xk], writes=[sk])
                    yield
                    S.op('dve', lambda e: e.bn_aggr(out=sti[:, 12:14], in_=sti[:, 0:12]), reads=[sk], writes=[sk])
                    yield
                    act(sti[:, 14:15], sti[:, 13:14], AF.Sqrt, [sk], [sk], bias=LN_EPS)
                    yield
                    S.op('dve', lambda e: e.reciprocal(out=sti[:, 14:15], in_=sti[:, 14:15]), reads=[sk], writes=[sk])
                    yield
                    stt(sti[:, 15:16], sti[:, 12:13], -1.0, sti[:, 14:15], ALU.mult, ALU.mult, [sk], [sk])
                    yield
                    act(xo[:], xi[:], AF.Identity, [xk, sk], [xok], bias=sti[:, 15:16], scale=sti[:, 14:15])
                    yield
                    for j in range(8):
                        tr(pt[:, j, :], xo[:, j * 128:(j + 1) * 128], identb, [xok, 'cstb'], [ptk])
                    yield
                    for j in range(8):
                        if j % 2 == 0:
                            act(uT[:, j, t * 128:(t + 1) * 128], pt[:, j, :], AF.Identity, [ptk, 'modfm'], ['uT%d' % t],
                                bias=modfm[:, j, ci:ci + 1], scale=modfm[:, 8 + j, ci:ci + 1])
                        else:
                            ts('dve', uT[:, j, t * 128:(t + 1) * 128], pt[:, j, :], modfm[:, 8 + j, ci:ci + 1],
                               modfm[:, j, ci:ci + 1], ALU.mult, ALU.add, [ptk, 'modfm'], ['uT%d' % t])

                run_pipelined((p1tile(t) for t in range(NTL)), 4)
                if ('uT%d' % l) in debug:
                    utf = sb(st, "utf", [128, 8, NT], F32)
                    cp('dve', utf[:], uT[:], ['uT%d' % t for t in range(NTL)], ['utf'])
                    dbg_dump('uT%d' % l, utf[:], [128, 8, NT], ['utf'])
                S.barrier()
            uTk = ['uT%d' % t for t in range(NTL)]

            for ph in list(PHASES):
                if ph in phases:
                    PHASES[ph](l, h_src, last)
            if ('h%d' % l) in debug and not last:
                d_ = dbg_out('h%d' % l, [NT, DM])
                S.dma('sp', d_, h1_d, writes=['dbgout_h%d' % l])
                S.barrier()
            if ('Y%d' % l) in debug:
                with contextlib.ExitStack() as st:
                    yf = sb(st, "yf", [128, 4, 2, NT], F32)
                    cp('dve', yf[:], Y[:], ['Y0', 'Y1', 'Y2', 'Y3'], ['yf'])
                    dbg_dump('Y%d' % l, yf[:], [128, 4, 2, NT], ['yf'])
                    S.barrier()

        S.final_wait('sp', ['outfinal'] + ['dbgout_' + n for n in dbg_d])
    if MEMDBG:
        print('SBUF min remaining by prefix:', minrem)
    return nc, dbg_d


def kernel(**inputs):
    inp = {k: np.asarray(v) for k, v in inputs.items()}
    sh = prep_shared(inp)
    nc, _ = build()
    in_maps = []
    for b in range(8):
        m = dict(sh)
        m.update(prep_core(inp, b))
        in_maps.append(m)
    res = run_bass_kernel_spmd(nc, in_maps, core_ids=list(range(8)))
    return np.stack([np.asarray(res.results[b]['out'], dtype=np.float32) for b in range(8)], 0)
```

```python
import contextlib
import numpy as np
import concourse.bass as bass
import concourse.mybir as mybir
from concourse.bass_utils import run_bass_kernel_spmd

F32 = mybir.dt.float32
BF16 = mybir.dt.bfloat16
AF = mybir.ActivationFunctionType
ALU = mybir.AluOpType
AX = mybir.AxisListType

NT = 2304
NTL = 18
DM = 1024
NCOL = 8064
BLOCKS = [(0, 256), (256, 512), (768, 512), (1280, 512), (1792, 512)]
LN_EPS = 1e-5
RMS_EPS = 1e-6
RW_GN_EPS = 64e-5
ALPHA = (2 * 2) ** 0.25
PI = float(np.pi)
MEMDBG = False


class Sched:
    NDMA = 16

    def __init__(self, nc, same_engine_waits=True):
        self.nc = nc
        self.same = same_engine_waits
        self.eng = dict(pe=nc.tensor, act=nc.scalar, dve=nc.vector, pool=nc.gpsimd, sp=nc.sync)
        self.E = {n: dict(cnt=0, known={}) for n in self.eng}
        self.dq = {'sp': ['dsp%d' % i for i in range(8)], 'act': ['dac%d' % i for i in range(4)],
                   'pool': ['dpl%d' % i for i in range(8)]}
        self.dmas = {n: dict(cnt=0) for q in self.dq.values() for n in q}
        self.dma_rr = {'sp': 0, 'act': 0, 'pool': 0}
        self.lastw = {}
        self.readers = {}
        self.sems = None
        self.nins = 0

    def sem_names(self):
        return list(self.E.keys()) + list(self.dmas.keys())

    def _deps(self, reads, writes):
        deps = {}

        def add(w):
            if w is not None:
                deps[w[0]] = max(deps.get(w[0], 0), w[1])
        for k in reads:
            add(self.lastw.get(k))
        for k in writes:
            add(self.lastw.get(k))
            for r in self.readers.get(k, ()):
                add(r)
        return deps

    def _waits(self, en, deps):
        E = self.E[en]
        waits = []
        for d, v in deps.items():
            if d == en and (en == 'pe' or not self.same):
                continue
            if E['known'].get(d, 0) < v:
                waits.append((d, v))
                E['known'][d] = v
        return waits

    def _record(self, ident, reads, writes):
        for k in writes:
            self.lastw[k] = ident
            self.readers[k] = []
        for k in reads:
            self.readers.setdefault(k, []).append(ident)

    def _emit(self, en, waits, fn, inc):
        eng = self.eng[en]
        for d, v in waits:
            eng.wait_ge(self.sems[d], v)
        if fn is not None:
            fn(eng).then_inc(self.sems[inc[0]], inc[1])
            self.nins += 1

    def op(self, en, fn, reads=(), writes=()):
        E = self.E[en]
        waits = self._waits(en, self._deps(reads, writes))
        E['cnt'] += 1
        self._emit(en, waits, fn, (en, 1))
        self._record((en, E['cnt']), reads, writes)

    def dma(self, en, out, in_, reads=(), writes=(), **kw):
        dn = self.dq[en][self.dma_rr[en]]
        self.dma_rr[en] = (self.dma_rr[en] + 1) % len(self.dq[en])
        Dq = self.dmas[dn]
        deps = self._deps(reads, writes)
        if Dq['cnt'] > 0:
            deps[dn] = max(deps.get(dn, 0), Dq['cnt'])
        waits = self._waits(en, deps)
        Dq['cnt'] += 16
        self._emit(en, waits, (lambda e: e.dma_start(out=out, in_=in_, **kw)), (dn, 16))
        self._record((dn, Dq['cnt']), reads, writes)

    def barrier(self):
        cur = {n: self.E[n]['cnt'] for n in self.E}
        cur.update({n: self.dmas[n]['cnt'] for n in self.dmas})
        for en in self.E:
            waits = self._waits(en, {d: v for d, v in cur.items() if v > 0})
            self._emit(en, waits, None, None)

    def final_wait(self, en, keys):
        self._emit(en, self._waits(en, self._deps(keys, ())), None, None)


PV = {}


def _pv_layout():
    off = 0
    for name, n in [('bin', 63), ('s5d', 2), ('glub', 2), ('hglb', 8), ('hgnw', 2), ('rdec', 4), ('mu', 14),
                    ('w0', 4), ('a0', 4), ('kk', 2), ('ka', 2), ('rk', 2), ('gnw', 2), ('gnb', 2), ('adab', 16),
                    ('lamre', 16), ('lamim', 16), ('ldt', 16), ('rdech', 8)]:
        PV[name] = (off, n)
        off += n
    return off


NPV = _pv_layout()


def _colmap():
    cm = list(range(0, 3584))
    lora = [-1] * 128
    for r in range(16):
        lora[r] = 3584 + r
        lora[32 + r] = 3600 + r
        lora[64 + r] = 3616 + r
        lora[80 + r] = 3632 + r
    cm += lora
    cm += list(range(3648, 3904))
    cm += list(range(3904, 8000))
    return np.array(cm)


CMAP = _colmap()


def _fm(v):
    return np.ascontiguousarray(v.reshape(-1, 128).T)


def _masks():
    t = np.arange(128)
    s_, t_ = t[:, None], t[None, :]
    m = []
    b32 = (s_ // 32) == (t_ // 32)
    b64 = (s_ // 64) == (t_ // 64)
    m.append(b32 & (t_ >= s_))
    m.append(b32 & (t_ <= s_))
    m.append(b64 & (t_ > s_))
    m.append(b64 & (t_ < s_))
    m.append(b64 & (t_ >= s_))
    m.append(b64 & (t_ <= s_))
    for d in range(2):
        for lv in range(6):
            sz = 1 << lv
            blk = (s_ // (2 * sz)) == (t_ // (2 * sz))
            hs, ht = (s_ // sz) % 2, (t_ // sz) % 2
            if d == 0:
                m.append(blk & (ht == 1) & (hs == 0))
            else:
                m.append(blk & (ht == 0) & (hs == 1))
    return np.stack([x.astype(np.float32) for x in m], 1)


def _rot_tables():
    n = 16
    freqs = 10000.0 ** (-np.arange(n, dtype=np.float32) / n)
    tt = np.arange(2048)
    rows = (tt // 64).astype(np.float32)
    cols = (tt % 64).astype(np.float32)
    cos = np.zeros((128, 2048), np.float32)
    sins = np.zeros((128, 2048), np.float32)
    pm = np.zeros((128, 128), np.float32)
    for p in range(128):
        i = p % 64
        pos = rows if i < 32 else cols
        ii = i % 32
        ang = pos * freqs[ii % 16]
        cos[p] = np.cos(ang)
        if ii < 16:
            sins[p] = -np.sin(ang)
            partner = p + 16
        else:
            sins[p] = np.sin(ang)
            partner = p - 16
        pm[partner, p] = 1.0
    return cos, sins, pm


def prep_shared(inp):
    sh = {}
    L = 2
    w_in = inp['w_in']
    wn = np.zeros((L, 1024, NCOL), np.float32)
    valid = CMAP >= 0
    wn[:, :, valid] = w_in[:, :, CMAP[valid]]
    sh['w_in'] = np.ascontiguousarray(wn.reshape(L, 8, 128, NCOL).transpose(0, 2, 1, 3))
    bn = np.zeros((L, NCOL), np.float32)
    bn[:, valid] = inp['b_in'][:, CMAP[valid]]
    sh['ada_w'] = np.ascontiguousarray(inp['ada_w'].reshape(L, 8, 128, 3072).transpose(0, 2, 1, 3))
    pv = np.zeros((L, 128, NPV), np.float32)

    def put(l, name, arr):
        o, n = PV[name]
        assert arr.shape == (128, n), (name, arr.shape)
        pv[l, :, o:o + n] = arr
    for l in range(L):
        put(l, 'bin', _fm(bn[l]))
        put(l, 's5d', _fm(inp['s5_d'][l]))
        put(l, 'glub', _fm(inp['s5_glu_b'][l]))
        put(l, 'hglb', np.concatenate([_fm(inp['hg_lb'][ll, d]) for ll in range(2) for d in range(2)], 1))
        put(l, 'hgnw', _fm(inp['hg_norm_w'][l]))
        rd = np.zeros((128, 4), np.float32)
        for d in range(2):
            for j in range(2):
                rd[:64, d * 2 + j] = inp['ret_decay'][l, d, 2 * j]
                rd[64:, d * 2 + j] = inp['ret_decay'][l, d, 2 * j + 1]
        put(l, 'rdec', rd)
        put(l, 'rdech', np.ascontiguousarray(np.broadcast_to(inp['ret_decay'][l].reshape(1, 8), (128, 8))))
        mu = np.zeros((2, 7 * 128), np.float32)
        mu[:, :768] = inp['rw_mu'][l][:, :768]
        lv = CMAP[3584:3712]
        ok = lv >= 0
        mu[:, 768:896][:, ok] = inp['rw_mu'][l][:, lv[ok] - 2816]
        put(l, 'mu', np.concatenate([_fm(mu[0]), _fm(mu[1])], 1))
        put(l, 'w0', np.concatenate([_fm(inp['rw_w0'][l, d]) for d in range(2)], 1))
        put(l, 'a0', np.concatenate([_fm(inp['rw_a0'][l, d]) for d in range(2)], 1))
        for nm, key in [('kk', 'rw_kk'), ('ka', 'rw_ka'), ('rk', 'rw_rk'), ('gnw', 'rw_gn_w'), ('gnb', 'rw_gn_b')]:
            put(l, nm, _fm(inp[key][l]))
        put(l, 'adab', _fm(inp['ada_b'][l][:2048]))
        for nm, key in [('lamre', 's5_lam_re'), ('lamim', 's5_lam_im')]:
            a = inp[key][l].reshape(2, 8, 2, 64)
            put(l, nm, np.ascontiguousarray(a.transpose(2, 3, 0, 1).reshape(128, 16)))
        a = np.broadcast_to(inp['s5_log_dt'][l].reshape(2, 8, 2, 1), (2, 8, 2, 64))
        put(l, 'ldt', np.ascontiguousarray(a.transpose(2, 3, 0, 1).reshape(128, 16)))
    sh['pv'] = pv
    bt = np.zeros((L, 128, 2, 4, 2, 128), np.float32)
    ct = np.zeros((L, 128, 8, 2, 128), np.float32)
    for l in range(L):
        for g in range(16):
            i, g2 = g // 2, g % 2
            for q in range(16):
                c = g * 16 + q
                j, p = c // 128, c % 128
                bt[l, p, j, i % 4, 0, g2 * 64:(g2 + 1) * 64] = inp['s5_b_re'][l, g, :, q]
                bt[l, p, j, i % 4, 1, g2 * 64:(g2 + 1) * 64] = inp['s5_b_im'][l, g, :, q]
            m0 = (i % 4) * 32 + g2 * 16
            ct[l, g2 * 64:(g2 + 1) * 64, i, 0, m0:m0 + 16] = inp['s5_c_re'][l, g].T
            ct[l, g2 * 64:(g2 + 1) * 64, i, 1, m0:m0 + 16] = inp['s5_c_im'][l, g].T
    sh['s5bt'] = bt
    sh['s5ct'] = ct
    sh['gluw'] = np.ascontiguousarray(inp['s5_glu_w'].reshape(L, 2, 128, 256).transpose(0, 2, 1, 3))
    lw2 = np.zeros((L, 128, 2, 256), np.float32)
    for l in range(L):
        lw2[l, 0:16, 0] = inp['rw_w2'][l, 0]
        lw2[l, 32:48, 1] = inp['rw_w2'][l, 1]
        lw2[l, 64:80, 0] = inp['rw_a2'][l, 0]
        lw2[l, 80:96, 1] = inp['rw_a2'][l, 1]
    sh['lw2'] = lw2
    sh['wbr'] = np.ascontiguousarray(inp['w_branch'].reshape(L, 4, 2, 128, 1024).transpose(0, 3, 1, 2, 4))
    sh['wout'] = np.ascontiguousarray(inp['w_out'].reshape(L, 8, 128, 1024).transpose(0, 2, 1, 3))
    rows = np.zeros((L, 128, 4096 + 512), np.float32)
    for l in range(L):
        rows[l, :, 0:1024] = inp['b_out'][l][None]
        rows[l, :, 1024:2048] = inp['ln_w'][l][None]
        rows[l, :, 2048:3072] = inp['ln_b'][l][None]
        rows[l, :, 3072:4096] = inp['ada_b'][l][None, 2048:3072]
        rows[l, :, 4096:4352] = inp['b_in'][l][None, 1280:1536]
        rows[l, :, 4352:4608] = inp['b_in'][l][None, 2304:2560]
    sh['rows'] = rows
    sh['masks'] = _masks()
    cos, sins, pm = _rot_tables()
    sh['rcos'] = cos
    sh['rsin'] = sins
    t = np.arange(128)
    cst = np.zeros((128, 9, 128), np.float32)
    cst[:, 0] = pm
    cst[:, 1] = ((t[:, None] // 64) == (t[None, :] // 64))
    cst[:, 2] = np.maximum(t[None, :] - t[:, None], 0)
    cst[:, 3] = np.maximum(t[:, None] - t[None, :], 0)
    cst[:, 4] = (t[None, :] >= t[:, None])
    cst[:, 5] = (t[None, :] <= t[:, None])
    cst[:, 6, :64] = ((t[:, None] % 64) == np.arange(64)[None, :])
    cst[:, 6, 64:68] = ((t[:, None] // 32) == np.arange(4)[None, :])
    cst[:, 6, 68] = 127 - t
    cst[:, 6, 69] = t
    cst[:, 7] = t[None, :] + 1.0
    cst[:, 8] = 128.0 - t[None, :]
    sh['cst'] = cst
    return sh


def prep_core(inp, b):
    pc = {}
    pc['hin'] = np.ascontiguousarray(np.concatenate([inp['ctx'][b], inp['x'][b]], 0))
    cv = np.stack([inp['c'][b], inp['c_ctx']], -1)
    pc['cvec'] = np.ascontiguousarray(cv.reshape(8, 128, 2).transpose(1, 0, 2))
    return pc


SHAPES = dict(hin=[NT, DM], cvec=[128, 8, 2], w_in=[2, 128, 8, NCOL], ada_w=[2, 128, 8, 3072], pv=[2, 128, NPV],
              s5bt=[2, 128, 2, 4, 2, 128], s5ct=[2, 128, 8, 2, 128], gluw=[2, 128, 2, 256], lw2=[2, 128, 2, 256],
              wbr=[2, 128, 4, 2, 1024], wout=[2, 128, 8, 1024], rows=[2, 128, 4608], masks=[128, 18, 128],
              rcos=[128, 2048], rsin=[128, 2048], cst=[128, 9, 128])


def build(debug=(), nlayers=2, phases=('s5', 'hg', 'ret', 'rw', 'merge'), stop=None):
    nc = bass.Bass("TRN2", target_bir_lowering=False)
    S = Sched(nc)
    dr = {k: nc.dram_tensor(k, list(v), F32, kind="ExternalInput").ap() for k, v in SHAPES.items()}
    out_d = nc.dram_tensor("out", [2048, DM], F32, kind="ExternalOutput").ap()
    h1_d = nc.dram_tensor("h1", [NT, DM], F32, kind="Internal").ap()
    dbg_d = {}

    def dbg_out(name, shape):
        dbg_d[name] = nc.dram_tensor("dbg_" + name, list(shape), F32, kind="ExternalOutput").ap()
        return dbg_d[name]

    uid = [0]

    def key(p='k'):
        uid[0] += 1
        return '%s%d' % (p, uid[0])

    with contextlib.ExitStack() as top:
        S.sems = {n: top.enter_context(nc.semaphore(n)) for n in S.sem_names()}

        minrem = {}

        def sb(st, name, shape, dt=F32):
            uid[0] += 1
            t_ = st.enter_context(nc.sbuf_tensor("%s_%d" % (name, uid[0]), list(shape), dt))
            if MEMDBG:
                pre = name[:2]
                minrem[pre] = min(minrem.get(pre, 1 << 30), nc.sbuf_bytes_remaining)
            return t_

        def ps(st, name, shape, dt=F32):
            uid[0] += 1
            return st.enter_context(nc.psum_tensor("%s_%d" % (name, uid[0]), list(shape), dt))

        def mm(out, lhsT, rhs, r, w, start=True, stop=True):
            S.op('pe', lambda e: e.matmul(out, lhsT=lhsT, rhs=rhs, start=start, stop=stop), reads=r, writes=w)

        def tr(out, in_, ident, r, w):
            S.op('pe', lambda e: e.transpose(out, in_, ident), reads=r, writes=w)

        def act(out, in_, func, r, w, bias=0.0, scale=1.0):
            S.op('act', lambda e: e.activation(out=out, in_=in_, func=func, bias=bias, scale=scale), reads=r, writes=w)

        def tt(en, out, in0, in1, op, r, w):
            S.op(en, lambda e: e.tensor_tensor(out=out, in0=in0, in1=in1, op=op), reads=r, writes=w)

        def ts(en, out, in0, s1, s2, op0, op1, r, w):
            if s2 is None:
                S.op(en, lambda e: e.tensor_scalar(out=out, in0=in0, scalar1=s1, scalar2=None, op0=op0), reads=r, writes=w)
            else:
                S.op(en, lambda e: e.tensor_scalar(out=out, in0=in0, scalar1=s1, scalar2=s2, op0=op0, op1=op1),
                     reads=r, writes=w)

        def stt(out, in0, sc, in1, op0, op1, r, w):
            S.op('dve', lambda e: e.scalar_tensor_tensor(out=out, in0=in0, scalar=sc, in1=in1, op0=op0, op1=op1),
                 reads=r, writes=w)

        def cp(en, out, in_, r, w):
            if en == 'act':
                S.op('act', lambda e: e.copy(out=out, in_=in_), reads=r, writes=w)
            else:
                S.op(en, lambda e: e.tensor_copy(out=out, in_=in_), reads=r, writes=w)

        def memset(en, ap, val, w):
            S.op(en, lambda e: e.memset(ap, val), writes=w)

        def run_pipelined(gens, stagger):
            it = iter(gens)
            active, pending, rounds = [], True, 0
            while pending or active:
                if pending and rounds % stagger == 0:
                    try:
                        active.append(next(it))
                    except StopIteration:
                        pending = False
                for g in list(active):
                    try:
                        next(g)
                    except StopIteration:
                        active.remove(g)
                rounds += 1

        def mkbanks(st_, n, prefix):
            bl = [ps(st_, "%s%d" % (prefix, i), [128, 512], F32) for i in range(n)]
            cnt = [0]

            def bank():
                i = cnt[0] % n
                cnt[0] += 1
                return bl[i], '%s%d' % (prefix, i)
            return bank

        def dbg_dump(name, ap, shape, r):
            if name in debug:
                d = dbg_out(name, shape)
                S.dma('sp', d, ap, reads=r, writes=['dbgout_' + name])

        cstb = sb(top, "cstb", [128, 3, 128], BF16)
        cstf = sb(top, "cstf", [128, 7, 128], F32)
        maskb = sb(top, "maskb", [128, 18, 128], BF16)
        silc = sb(top, "silc", [128, 8, 2], F32)
        S.dma('pool', cstb[:, 0:2, :], dr['cst'][:, 0:2, :], writes=['cstb'])
        S.dma('sp', cstf[:], dr['cst'][:, 2:9, :], writes=['cstf'])
        S.dma('pool', maskb[:], dr['masks'], writes=['maskb'])
        S.dma('sp', silc[:], dr['cvec'], writes=['silc'])
        memset('pool', cstb[:, 2, :], 0.0, ['cstb'])
        S.op('pool', lambda e: e.affine_select(out=cstb[:, 2, :], in_=cstb[:, 2, :], pattern=[[-1, 128]],
                                               compare_op=ALU.not_equal, fill=1.0, base=0, channel_multiplier=1),
             reads=['cstb'], writes=['cstb'])
        act(silc[:], silc[:], AF.Silu, ['silc'], ['silc'])
        identb = cstb[:, 2, :]
        bonesb = cstb[:, 1, :]

        uT = sb(top, "uT", [128, 8, NT], BF16)
        Y = sb(top, "Y", [128, 4, 2, NT], BF16)
        pvt = sb(top, "pvt", [128, NPV], F32)
        if debug:
            memset('pool', Y[:], 0.0, ['Y0', 'Y1', 'Y2', 'Y3'])
        modfm = sb(top, "modfm", [128, 16, 2], F32)
        gatebc = sb(top, "gatebc", [128, 2, DM], F32)

        def pv(name, j=None, n=1):
            o, cnt = PV[name]
            if j is None:
                return pvt[:, o:o + cnt]
            return pvt[:, o + j:o + j + n]

        PHASES = {}
        def proj_fm(st, wt, wk, mlist, evac, pp, ppk):
            cnt = 0
            for (n0, nn) in BLOCKS:
                for mi, m in enumerate(mlist):
                    p_, pk_ = pp[cnt % len(pp)], ppk[cnt % len(pp)]
                    cnt += 1
                    for j in range(8):
                        mm(p_[:, 0:nn], wt[:, j, m * 128:(m + 1) * 128], uT[:, j, n0:n0 + nn],
                           [wk] + uTk[n0 // 128:(n0 + nn) // 128], [pk_], start=(j == 0), stop=(j == 7))
                    evac(mi, m, n0, nn, p_, pk_)

        def phase_s5(l, h_src, last):
            L = 128
            with contextlib.ExitStack() as st:
                btb = sb(st, "btb", [128, 2, 4, 2, 128], BF16)
                ctb = sb(st, "ctb", [128, 8, 2, 128], BF16)
                glub = sb(st, "glub", [128, 2, 256], BF16)
                S.dma('pool', btb[:], dr['s5bt'][l], writes=['btb'])
                S.dma('pool', ctb[:], dr['s5ct'][l], writes=['ctb'])
                S.dma('pool', glub[:], dr['gluw'][l], writes=['glub'])
                ts('pool', ctb[:, :, 1, :], ctb[:, :, 1, :], -1.0, 0.0, ALU.mult, ALU.add, ['ctb'], ['ctb'])
                ub = sb(st, "s5u", [128, 2, NT], BF16)
                zs = sb(st, "s5z", [128, 2, NT], BF16)
                yacc = sb(st, "yacc", [128, 2, NT], F32)
                PT = sb(st, "s5PT", [128, 16, 2, L], F32)
                QT = sb(st, "s5QT", [128, 16, 2, L], F32)
                sst = sb(st, "s5st", [128, 16, 2], F32)
                ones = sb(st, "s5ones", [128, L], F32)
                memset('pool', yacc[:], 0.0, ['yacc'])
                memset('pool', sst[:], 0.0, ['sst'])
                memset('pool', ones[:], 1.0, ['s5ones'])
                with contextlib.ExitStack() as st2:
                    wsu = sb(st2, "wsu", [128, 8, 512], BF16)
                    S.dma('pool', wsu[:], dr['w_in'][l][:, :, 0:512], writes=['wsu'])
                    pp = [ps(st2, "s5pp%d" % i, [128, 512], F32) for i in range(2)]

                    def evac(mi, m, n0, nn, p_, pk_):
                        if m < 2:
                            act(ub[:, m, n0:n0 + nn], p_[:, 0:nn], AF.Identity, [pk_, 'pvt'], ['s5u'], bias=pv('bin', m))
                        else:
                            act(zs[:, m - 2, n0:n0 + nn], p_[:, 0:nn], AF.Silu, [pk_, 'pvt'], ['s5z'], bias=pv('bin', m))
                    proj_fm(st2, wsu, 'wsu', [0, 1, 2, 3], evac, pp, ['s5pp0', 's5pp1'])
                    sm = sb(st2, "s5sm", [128, 20, 16], F32)
                    K_ = 's5sm'

                    def Sm(i):
                        return sm[:, i, :]

                    def T2(o, a, b, op):
                        tt('dve', Sm(o), a if not isinstance(a, int) else Sm(a), b if not isinstance(b, int) else Sm(b), op,
                           [K_, 'pvt'], [K_])
                    lamre, lamim = pv('lamre'), pv('lamim')
                    act(Sm(0), pv('ldt'), AF.Exp, ['pvt'], [K_])
                    T2(1, lamre, 0, ALU.mult)
                    act(Sm(2), Sm(1), AF.Exp, [K_], [K_])
                    act(Sm(3), Sm(1), AF.Exp, [K_], [K_], scale=-1.0)
                    T2(4, lamim, 0, ALU.mult)
                    ts('dve', Sm(5), Sm(4), PI / 2, None, ALU.add, None, [K_], [K_])
                    for x in (4, 5):
                        for _ in range(4):
                            ts('dve', Sm(16), Sm(x), PI, 2 * PI, ALU.is_gt, ALU.mult, [K_], [K_])
                            T2(x, x, 16, ALU.subtract)
                    act(Sm(6), Sm(4), AF.Sin, [K_], [K_])
                    act(Sm(7), Sm(5), AF.Sin, [K_], [K_])
                    T2(8, 2, 7, ALU.mult)
                    T2(9, 2, 6, ALU.mult)
                    T2(10, 3, 7, ALU.mult)
                    stt(Sm(11), Sm(3), -1.0, Sm(6), ALU.mult, ALU.mult, [K_], [K_])
                    ts('dve', Sm(12), Sm(8), -1.0, None, ALU.add, None, [K_], [K_])
                    T2(16, lamre, lamre, ALU.mult)
                    T2(17, lamim, lamim, ALU.mult)
                    T2(13, 16, 17, ALU.add)
                    S.op('dve', lambda e: e.reciprocal(out=Sm(13), in_=Sm(13)), reads=[K_], writes=[K_])
                    T2(16, 12, lamre, ALU.mult)
                    T2(17, 9, lamim, ALU.mult)
                    T2(16, 16, 17, ALU.add)
                    T2(14, 16, 13, ALU.mult)
                    T2(16, 9, lamre, ALU.mult)
                    T2(17, 12, lamim, ALU.mult)
                    T2(16, 16, 17, ALU.subtract)
                    T2(15, 16, 13, ALU.mult)
                    tmpa = sb(st2, "s5ta", [128, 16, L], F32)
                    tmpb = sb(st2, "s5tb", [128, 16, L], F32)

                    def cmul_bc(dst_re, dst_im, src_re, src_im, s_re, s_im, m):
                        sr = s_re.unsqueeze(2).broadcast_to([128, 16, m])
                        si = s_im.unsqueeze(2).broadcast_to([128, 16, m])
                        ta, tb = tmpa[:, :, 0:m], tmpb[:, :, 0:m]
                        kk_ = ['s5tab', 's5ta', 's5tb', 's5tc', K_]
                        tt('dve', ta, src_re, sr, ALU.mult, kk_, ['s5ta'])
                        tt('dve', tb, src_im, si, ALU.mult, kk_, ['s5tb'])
                        tt('dve', dst_re, ta, tb, ALU.subtract, kk_, ['s5tab'])
                        tt('dve', ta, src_re, si, ALU.mult, kk_, ['s5ta'])
                        tt('dve', tb, src_im, sr, ALU.mult, kk_, ['s5tb'])
                        tt('dve', dst_im, ta, tb, ALU.add, kk_, ['s5tab'])
                    for (TB, a_re, a_im) in ((PT, 8, 9), (QT, 10, 11)):
                        cp('dve', TB[:, :, 0, 0], Sm(a_re), [K_], ['s5tab'])
                        cp('dve', TB[:, :, 1, 0], Sm(a_im), [K_], ['s5tab'])
                        m = 1
                        while m < L:
                            cmul_bc(TB[:, :, 0, m:2 * m], TB[:, :, 1, m:2 * m], TB[:, :, 0, 0:m], TB[:, :, 1, 0:m],
                                    TB[:, :, 0, m - 1], TB[:, :, 1, m - 1], m)
                            m *= 2
                    tmpc = sb(st2, "s5tc", [128, 16, L], F32)
                    cp('dve', tmpc[:], QT[:, :, 0, :], ['s5tab'], ['s5tc'])
                    cmul_bc(QT[:, :, 0, :], QT[:, :, 1, :], tmpc[:], QT[:, :, 1, :], Sm(14), Sm(15), L)
                    S.barrier()
                with contextlib.ExitStack() as st2:
                    NB = 8
                    xa = [sb(st2, "s5xa%d" % i, [128, 2, L], F32) for i in range(NB)]
                    xb_ = [sb(st2, "s5xb%d" % i, [128, 2, L], F32) for i in range(NB)]
                    cw = [sb(st2, "s5cw%d" % i, [128, 2, L], F32) for i in range(NB)]
                    hb = [sb(st2, "s5hb%d" % i, [128, 2, L], BF16) for i in range(NB)]
                    pbu = [ps(st2, "s5pb%d" % i, [128, 2, 2, L], F32) for i in range(4)]
                    py = [ps(st2, "s5py%d" % i, [128, 512], F32) for i in range(2)]
                    orders = [list(range(NTL)), [1, 0] + list(range(NTL - 1, 1, -1))]
                    def s5group(gi, step, d, j):
                        c = orders[d][step]
                        n0 = c * L
                        rev = (d == 1)
                        U = []
                        for ii in range(4):
                            un = gi * 4 + ii
                            bnk = (un // 2) % 4
                            U.append(dict(ii=ii, i=j * 4 + ii, q=d * 8 + j * 4 + ii, pb=pbu[bnk][:, un % 2], pbk='s5pb%d' % bnk,
                                          A=xa[un % NB], Ak='s5xa%d' % (un % NB), B=xb_[un % NB], Bk='s5xb%d' % (un % NB),
                                          C=cw[un % NB], Ck='s5cw%d' % (un % NB), H=hb[un % NB], Hk='s5hb%d' % (un % NB)))
                        for u in U:
                            for ri in range(2):
                                mm(u['pb'][:, ri, :], btb[:, j, u['ii'], ri, :], ub[:, j, n0:n0 + L], ['btb', 's5u'], [u['pbk']])
                        yield
                        for u in U:
                            src = u['pb'][:, :, ::-1] if rev else u['pb'][:, :, :]
                            tt('dve', u['A'][:], src, QT[:, u['q'], 0:1, :].broadcast_to([128, 2, L]), ALU.mult,
                               [u['pbk'], 's5tab'], [u['Ak']])
                        yield
                        for u in U:
                            src = u['pb'][:, ::-1, ::-1] if rev else u['pb'][:, ::-1, :]
                            tt('dve', u['B'][:], src, QT[:, u['q'], 1:2, :].broadcast_to([128, 2, L]), ALU.mult,
                               [u['pbk'], 's5tab'], [u['Bk']])
                        yield
                        for u in U:
                            tt('dve', u['A'][:, 0, :], u['A'][:, 0, :], u['B'][:, 0, :], ALU.subtract, [u['Ak'], u['Bk']], [u['Ak']])
                        yield
                        for u in U:
                            tt('dve', u['A'][:, 1, :], u['A'][:, 1, :], u['B'][:, 1, :], ALU.add, [u['Ak'], u['Bk']], [u['Ak']])
                        yield
                        for ri in range(2):
                            for u in U:
                                q = u['q']
                                S.op('dve', lambda e, u=u, ri=ri, q=q: e.tensor_tensor_scan(
                                    out=u['C'][:, ri, :], data0=ones[:], data1=u['A'][:, ri, :], initial=sst[:, q, ri:ri + 1],
                                    op0=ALU.mult, op1=ALU.add), reads=[u['Ak'], 's5ones', 'sst%d' % q, 'sst'], writes=[u['Ck']])
                            yield
                        for u in U:
                            tt('pool', u['A'][:], u['C'][:], PT[:, u['q'], 0:1, :].broadcast_to([128, 2, L]), ALU.mult,
                               [u['Ck'], 's5tab', u['Ak']], [u['Ak']])
                        yield
                        for u in U:
                            tt('pool', u['B'][:], u['C'][:, ::-1, :], PT[:, u['q'], 1:2, :].broadcast_to([128, 2, L]), ALU.mult,
                               [u['Ck'], 's5tab', u['Bk']], [u['Bk']])
                        yield
                        for u in U:
                            tt('pool', u['A'][:, 0, :], u['A'][:, 0, :], u['B'][:, 0, :], ALU.subtract, [u['Ak'], u['Bk']], [u['Ak']])
                        yield
                        for u in U:
                            tt('pool', u['A'][:, 1, :], u['A'][:, 1, :], u['B'][:, 1, :], ALU.add, [u['Ak'], u['Bk']], [u['Ak']])
                        yield
                        for u in U:
                            cp('pool', sst[:, u['q'], :], u['A'][:, :, L - 1], [u['Ak']], ['sst%d' % u['q']])
                        yield
                        for u in U:
                            hsrc = u['A'][:, :, ::-1] if rev else u['A'][:]
                            cp('act', u['H'][:], hsrc, [u['Ak']], [u['Hk']])
                        yield
                        pyr = py[gi % 2][:, 0:L]
                        pyk = 's5py%d' % (gi % 2)
                        for k_, u in enumerate(U):
                            for ri in range(2):
                                mm(pyr, ctb[:, u['i'], ri, :], u['H'][:, ri, :], ['ctb', u['Hk']], [pyk],
                                   start=(k_ == 0 and ri == 0), stop=(k_ == 3 and ri == 1))
                        yield
                        yield
                        yield
                        tt('dve', yacc[:, j, n0:n0 + L], yacc[:, j, n0:n0 + L], pyr, ALU.add, [pyk, 'yacc'], ['yacc'])

                    glist = [(step, d, j) for step in range(NTL) for d in range(2) for j in range(2)]
                    run_pipelined((s5group(gi, *g) for gi, g in enumerate(glist)), 11)
                    S.barrier()
                for j in range(2):
                    stt(yacc[:, j, :], ub[:, j, :], pv('s5d', j), yacc[:, j, :], ALU.mult, ALU.add, ['s5u', 'yacc', 'pvt'],
                        ['yacc'])
                dbg_dump('ya%d' % l, yacc[:], [128, 2, NT], ['yacc'])
                with contextlib.ExitStack() as st2:
                    t1 = [sb(st2, "s5g1_%d" % i, [128, 512], F32) for i in range(2)]
                    t2 = [sb(st2, "s5g2_%d" % i, [128, 512], BF16) for i in range(2)]
                    pg = [ps(st2, "s5pg%d" % i, [128, 512], F32) for i in range(2)]
                    cnt = 0
                    for (n0, nn) in BLOCKS:
                        for j in range(2):
                            a, ak = t1[cnt % 2], 's5g1_%d' % (cnt % 2)
                            cnt += 1
                            ysl = yacc[:, j, n0:n0 + nn]
                            act(a[:, 0:nn], ysl, AF.Square, ['yacc'], [ak])
                            ts('dve', a[:, 0:nn], a[:, 0:nn], 0.044715, 1.0, ALU.mult, ALU.add, [ak], [ak])
                            tt('dve', a[:, 0:nn], a[:, 0:nn], ysl, ALU.mult, [ak, 'yacc'], [ak])
                            act(a[:, 0:nn], a[:, 0:nn], AF.Sigmoid, [ak], [ak], scale=1.5957691216057308)
                            tt('dve', ub[:, j, n0:n0 + nn], a[:, 0:nn], ysl, ALU.mult, [ak, 'yacc'], ['s5u'])
                    cnt = 0
                    for (n0, nn) in BLOCKS:
                        for m in range(2):
                            p_, pk_ = pg[cnt % 2], 's5pg%d' % (cnt % 2)
                            b_, bk_ = t2[cnt % 2], 's5g2_%d' % (cnt % 2)
                            cnt += 1
                            for jc in range(2):
                                mm(p_[:, 0:nn], glub[:, jc, m * 128:(m + 1) * 128], ub[:, jc, n0:n0 + nn], ['glub', 's5u'], [pk_],
                                   start=(jc == 0), stop=(jc == 1))
                            act(b_[:, 0:nn], p_[:, 0:nn], AF.Sigmoid, [pk_, 'pvt'], [bk_], bias=pv('glub', m))
                            tt('dve', b_[:, 0:nn], b_[:, 0:nn], ub[:, m, n0:n0 + nn], ALU.mult, [bk_, 's5u'], [bk_])
                            tt('pool', Y[:, 0, m, n0:n0 + nn], b_[:, 0:nn], zs[:, m, n0:n0 + nn], ALU.mult, [bk_, 's5z'], ['Y0'])
                    S.barrier()
                S.barrier()
        PHASES['s5'] = phase_s5
        def phase_hg(l, h_src, last):
            with contextlib.ExitStack() as st:
                QP = [sb(st, "hgQP%d" % d, [128, 2, NT], BF16) for d in range(2)]
                KP = [sb(st, "hgKP%d" % d, [128, 2, NT], BF16) for d in range(2)]
                G = sb(st, "hgG", [128, 2, 72, 2], F32)
                VT = sb(st, "hgVT", [128, NTL, 256], BF16)
                zs = sb(st, "hgzs", [128, 2, NT], BF16)
                lbt = sb(st, "hglbt", [128, 2, 4], F32)
                if l == 0:
                    memset('pool', lbt[:, 0, :], 0.0, ['hglbt'])
                    memset('pool', lbt[:, 1, :], 1.0, ['hglbt'])
                else:
                    o_, _ = PV['hglb']
                    tt('dve', lbt[:, 0, :], pvt[:, o_ + 4:o_ + 8], pvt[:, o_:o_ + 4], ALU.subtract, ['pvt'], ['hglbt'])
                    act(lbt[:, 0, :], lbt[:, 0, :], AF.Sigmoid, ['hglbt'], ['hglbt'])
                    ts('dve', lbt[:, 1, :], lbt[:, 0, :], -1.0, 1.0, ALU.mult, ALU.add, ['hglbt'], ['hglbt'])
                with contextlib.ExitStack() as st2:
                    wh = sb(st2, "hgw", [128, 8, 1280], BF16)
                    S.dma('pool', wh[:, :, 0:768], dr['w_in'][l][:, :, 512:1280], writes=['hgw'])
                    S.dma('pool', wh[:, :, 768:1280], dr['w_in'][l][:, :, 1280:1792], writes=['hgw'])
                    brow = sb(st2, "hgbrow", [128, 256], F32)
                    S.dma('sp', brow[:], dr['rows'][l][:, 4096:4352], writes=['hgbrow'])
                    R32 = sb(st2, "hgR32", [128, 512], F32)
                    memset('pool', R32[:], 1.0, ['hgR32'])
                    memset('pool', R32[:, 0:512:32], 0.0, ['hgR32'])
                    QS = [sb(st2, "hgQS%d" % i, [128, 2, 512], BF16) for i in range(2)]
                    T = [[sb(st2, "hgT%d_%d" % (i, k), [128, 512], F32) for k in range(4)] for i in range(2)]
                    pp = [ps(st2, "hgpp%d" % i, [128, 512], F32) for i in range(3)]
                    pt = [ps(st2, "hgpt%d" % i, [128, 512], F32) for i in range(2)]
                    def hgproj(cnt, ic, m, n0, nn):
                        ukeys = uTk[n0 // 128:(n0 + nn) // 128]
                        p_, pk_ = pp[cnt % 3], 'hgpp%d' % (cnt % 3)
                        bi = (n0 // 512) % 2 if n0 else 0
                        for jj in range(8):
                            mm(p_[:, 0:nn], wh[:, jj, m * 128:(m + 1) * 128], uT[:, jj, n0:n0 + nn], ['hgw'] + ukeys, [pk_],
                               start=(jj == 0), stop=(jj == 7))
                        yield
                        bias = pv('bin', 4 + m)
                        if m < 2:
                            act(QS[bi][:, m, 0:nn], p_[:, 0:nn], AF.Silu, [pk_, 'pvt'], ['hgQS%d' % bi], bias=bias)
                            return
                        if m >= 8:
                            act(zs[:, m - 8, n0:n0 + nn], p_[:, 0:nn], AF.Silu, [pk_, 'pvt'], ['hgzs'], bias=bias)
                            return
                        d, j = (m - 2) // 2, (m - 2) % 2
                        Ts = T[ic % 2]
                        Tk = ['hgT%d_%d' % (ic % 2, k) for k in range(4)]
                        t1, t2, t3, t4 = [x[:, 0:nn] for x in Ts]
                        act(t1, p_[:, 0:nn], AF.Sigmoid, [pk_, 'pvt'], [Tk[0]], bias=bias)
                        yield
                        ts('dve', t1, t1, lbt[:, 1, d * 2 + j:d * 2 + j + 1], lbt[:, 0, d * 2 + j:d * 2 + j + 1], ALU.mult, ALU.add,
                           [Tk[0], 'hglbt'], [Tk[0]])
                        yield
                        act(t2, t1, AF.Ln, [Tk[0]], [Tk[1]])
                        yield
                        if d == 0:
                            S.op('dve', lambda e: e.tensor_tensor_scan(out=t3, data0=R32[:, 0:nn], data1=t2, initial=0.0,
                                                                       op0=ALU.mult, op1=ALU.add),
                                 reads=[Tk[1], 'hgR32'], writes=[Tk[2]])
                        else:
                            S.op('dve', lambda e: e.tensor_tensor_scan(out=t3[:, ::-1],
                                                                       data0=R32[:, 0:nn], data1=t2[:, ::-1], initial=0.0,
                                                                       op0=ALU.mult, op1=ALU.add),
                                 reads=[Tk[1], 'hgR32'], writes=[Tk[2]])
                        yield
                        ts('dve', t3, t3, -80.0, None, ALU.max, None, [Tk[2]], [Tk[2]])
                        ts('dve', t1, t1, -1.0, 1.0, ALU.mult, ALU.add, [Tk[0]], [Tk[0]])
                        yield
                        act(t4, t3, AF.Exp, [Tk[2]], [Tk[3]])
                        act(t2, t3, AF.Exp, [Tk[2]], [Tk[1]], scale=-1.0)
                        yield
                        tt('pool', KP[d][:, j, n0:n0 + nn], t1, t2, ALU.mult, [Tk[0], Tk[1]], ['hgKP%d' % d])
                        tt('pool', QP[d][:, j, n0:n0 + nn], QS[bi][:, j, 0:nn], t4, ALU.mult, ['hgQS%d' % bi, Tk[3]], ['hgQP%d' % d])
                        c0 = n0 // 32
                        gsrc = t4[:, 31::32] if d == 0 else t4[:, 0::32]
                        cp('act', G[:, d, c0:c0 + nn // 32, j], gsrc, [Tk[3]], ['hgG'])

                    plist = []
                    cnt = 0
                    ic = 0
                    for (n0, nn) in BLOCKS:
                        for m in (0, 1, 8, 9, 2, 3, 4, 5):
                            plist.append((cnt, ic, m, n0, nn))
                            cnt += 1
                            if 2 <= m < 8:
                                ic += 1
                    run_pipelined((hgproj(*p) for p in plist), 4)
                    for t in range(NTL):
                        p_, pk_ = pt[t % 2], 'hgpt%d' % (t % 2)
                        for jj in range(8):
                            mm(p_[:, 0:256], uT[:, jj, t * 128:(t + 1) * 128], wh[:, jj, 768:1024], ['hgw', uTk[t]], [pk_],
                               start=(jj == 0), stop=(jj == 7))
                        tt('dve', VT[:, t, :], p_[:, 0:256], brow[:], ALU.add, [pk_, 'hgbrow'], ['hgVT'])
                    S.barrier()
                Sall = [sb(st, "hgSall%d" % d, [128, 2, 72, 64], BF16) for d in range(2)]
                with contextlib.ExitStack() as st2:
                    Sst = [sb(st2, "hgS%d" % d, [128, 2, 64], F32) for d in range(2)]
                    kTm = [sb(st2, "hgkTm%d" % i, [128, 4, 256], BF16) for i in range(3)]
                    Ug = [sb(st2, "hgUg%d" % i, [128, 4, 2, 64], F32) for i in range(3)]
                    ptr = [ps(st2, "hgptr%d" % i, [128, 8, 128], BF16) for i in range(2)]
                    pU = [ps(st2, "hgpU%d" % i, [128, 4, 2, 64], F32) for i in range(3)]
                    orders = [list(range(NTL)), [1, 0] + list(range(NTL - 1, 1, -1))]
                    for d in range(2):
                        memset('pool', Sst[d][:], 0.0, ['hgS%d' % d])
                    def hgchain(it, step, d):
                        t = orders[d][step]
                        pr, prk = ptr[it % 2], 'hgptr%d' % (it % 2)
                        km, kmk = kTm[it % 3], 'hgkTm%d' % (it % 3)
                        pu, puk = pU[it % 3], 'hgpU%d' % (it % 3)
                        ug, ugk = Ug[it % 3], 'hgUg%d' % (it % 3)
                        for j in range(2):
                            tr(pr[:, j, :], KP[d][:, j, t * 128:(t + 1) * 128], identb, ['hgKP%d' % d, 'cstb'], [prk])
                        yield
                        for cc in range(4):
                            prf = pr[:, 0:2, :].rearrange("p a b -> p (a b)")
                            if cc % 2 == 0:
                                ts('dve', km[:, cc, :], prf, cstf[:, 4, 64 + cc:64 + cc + 1], None, ALU.mult, None, [prk, 'cstf'], [kmk])
                            else:
                                act(km[:, cc, :], prf, AF.Identity, [prk, 'cstf'], [kmk], scale=cstf[:, 4, 64 + cc:64 + cc + 1])
                        yield
                        for cc in range(4):
                            for h in range(4):
                                hp = (h % 2) * 64
                                mm(pu[hp:hp + 64, cc, h // 2, :], km[:, cc, h * 64:(h + 1) * 64], VT[:, t, h * 64:(h + 1) * 64],
                                   [kmk, 'hgVT'], [puk])
                        yield
                        tt('dve', ug[:], pu[:], G[:, d, t * 4:(t + 1) * 4, :].unsqueeze(3).broadcast_to([128, 4, 2, 64]), ALU.mult,
                           [puk, 'hgG'], [ugk])
                        yield
                        ccs = range(4) if d == 0 else range(3, -1, -1)
                        for cc in ccs:
                            c = t * 4 + cc
                            cp('act', Sall[d][:, :, c, :], Sst[d][:], ['hgS%d' % d], ['hgSall%d_%d' % (d, t)])
                            for j in range(2):
                                stt(Sst[d][:, j, :], Sst[d][:, j, :], G[:, d, c, j:j + 1], ug[:, cc, j, :], ALU.mult, ALU.add,
                                    ['hgS%d' % d, 'hgG', ugk], ['hgS%d' % d])
                            yield

                    run_pipelined((hgchain(i_, sd[0], sd[1]) for i_, sd in enumerate([(s_, d_) for s_ in range(NTL) for d_ in range(2)])), 3)
                    S.barrier()
                with contextlib.ExitStack() as st2:
                    if ('yb%d' % l) in debug:
                        dbgbuf = sb(st2, "dbgbuf", [128, 2, NT], F32)
                    AT = [[sb(st2, "hgAT%d_%d" % (i, d), [128, 4, 128], BF16) for d in range(2)] for i in range(2)]
                    sq = [sb(st2, "hgsq%d" % i, [128, 2, 128], BF16) for i in range(2)]
                    rr = [sb(st2, "hgrr%d" % i, [128, 2, 128], F32) for i in range(2)]
                    ob = [sb(st2, "hgob%d" % i, [128, 2, 128], F32) for i in range(2)]
                    bank = mkbanks(st2, 8, "hgbk")

                    def hgout(t):
                        i2 = t % 2
                        tsl = slice(t * 128, (t + 1) * 128)
                        pas = {}
                        for d in range(2):
                            for par in range(2):
                                pas[(d, par)] = bank()
                            for h in range(4):
                                hp = (h % 2) * 64
                                pa, pak = pas[(d, h % 2)]
                                pav = pa[:, 0:256].rearrange("p (a b) -> p a b", a=2)
                                mm(pav[:, h // 2, :], KP[d][hp:hp + 64, h // 2, tsl], QP[d][hp:hp + 64, h // 2, tsl],
                                   ['hgKP%d' % d, 'hgQP%d' % d], [pak])
                        yield
                        for d in range(2):
                            for par in range(2):
                                pa, pak = pas[(d, par)]
                                pav = pa[:, 0:256].rearrange("p (a b) -> p a b", a=2)
                                tt('dve', AT[i2][d][:, par::2, :], pav, maskb[:, d, :].unsqueeze(1).broadcast_to([128, 2, 128]), ALU.mult,
                                   [pak, 'maskb'], ['hgAT%d_%d' % (i2, d)])
                        yield
                        pos = [bank() for _ in range(2)]
                        povs = [pos[par][0][:, 0:256].rearrange("p (a b) -> p a b", a=2) for par in range(2)]
                        for h in range(4):
                            hp = (h % 2) * 64
                            pok = pos[h % 2][1]
                            reg = povs[h % 2][hp:hp + 64, h // 2, :]
                            first = True
                            for d in range(2):
                                mm(reg, VT[:, t, h * 64:(h + 1) * 64], AT[i2][d][:, h, :], ['hgVT', 'hgAT%d_%d' % (i2, d)], [pok],
                                   start=first, stop=False)
                                first = False
                                for cc in range(4):
                                    c = t * 4 + cc
                                    mm(reg[:, cc * 32:(cc + 1) * 32], Sall[d][hp:hp + 64, h // 2, c, :],
                                       QP[d][hp:hp + 64, h // 2, t * 128 + cc * 32:t * 128 + (cc + 1) * 32],
                                       ['hgSall%d_%d' % (d, t), 'hgQP%d' % d], [pok], start=False, stop=(d == 1 and cc == 3))
                        yield
                        obk = 'hgob%d' % i2
                        cp('act', ob[i2][0:64], povs[0][0:64], [pos[0][1]], [obk])
                        cp('dve', ob[i2][64:128], povs[1][64:128], [pos[1][1]], [obk])
                        yield
                        pov = ob[i2][:]
                        pok = obk
                        if ('yb%d' % l) in debug:
                            cp('pool', dbgbuf[:, :, tsl], pov, [pok], ['dbgbuf'])
                        act(sq[i2][:], pov, AF.Square, [pok], ['hgsq%d' % i2])
                        yield
                        pss_, psk = bank()
                        psv = pss_[:, 0:256].rearrange("p (a b) -> p a b", a=2)
                        for j in range(2):
                            mm(psv[:, j, :], bonesb, sq[i2][:, j, :], ['cstb', 'hgsq%d' % i2], [psk])
                        yield
                        act(rr[i2][:], psv, AF.Sqrt, [psk], ['hgrr%d' % i2], bias=RMS_EPS, scale=1.0 / 64)
                        yield
                        S.op('dve', lambda e: e.reciprocal(out=rr[i2][:], in_=rr[i2][:]), reads=['hgrr%d' % i2], writes=['hgrr%d' % i2])
                        yield
                        tt('dve', rr[i2][:], pov, rr[i2][:], ALU.mult, [pok, 'hgrr%d' % i2], ['hgrr%d' % i2])
                        yield
                        for j in range(2):
                            stt(Y[:, 1, j, tsl], rr[i2][:, j, :], pv('hgnw', j), zs[:, j, tsl], ALU.mult, ALU.mult,
                                ['hgrr%d' % i2, 'pvt', 'hgzs'], ['Y1'])

                    run_pipelined((hgout(t) for t in range(NTL) if not (last and t < 2 and not debug)), 5)
                    if ('yb%d' % l) in debug:
                        dbg_dump('yb%d' % l, dbgbuf[:], [128, 2, NT], ['dbgbuf'])
                    S.barrier()
                S.barrier()
        PHASES['hg'] = phase_hg
        def phase_ret(l, h_src, last):
            with contextlib.ExitStack() as st:
                QR = sb(st, "rtQR", [128, 2, NT], BF16)
                KR = sb(st, "rtKR", [128, 2, NT], BF16)
                VT = sb(st, "rtVT", [128, NTL, 256], BF16)
                zs = sb(st, "rtzs", [128, 2, NT], BF16)
                Sall = [sb(st, "rtSall%d" % d, [128, 2, NTL, 64], BF16) for d in range(2)]
                LG = sb(st, "rtLG", [128, 4], F32)
                GL = sb(st, "rtGL", [128, 4], F32)
                LGH = sb(st, "rtLGH", [128, 8], F32)
                QDEC = sb(st, "rtQDEC", [128, 2, 2, 128], F32)
                KDEC = sb(st, "rtKDEC", [128, 2, 4], F32)
                DS = sb(st, "rtDS", [128, 4, 128], F32)
                tb8 = sb(st, "rtb8", [128, 2], F32)
                K_ = 'rttab'
                act(LG[:], pv('rdec'), AF.Exp, ['pvt'], [K_])
                ts('dve', LG[:], LG[:], -1.0, None, ALU.mult, None, [K_], [K_])
                act(GL[:], LG[:], AF.Exp, [K_], [K_], scale=128.0)
                act(LGH[:], pv('rdech'), AF.Exp, ['pvt'], [K_])
                ts('dve', LGH[:], LGH[:], -1.0, None, ALU.mult, None, [K_], [K_])
                for d in range(2):
                    for j in range(2):
                        act(QDEC[:, d, j, :], cstf[:, 5 + d, :], AF.Exp, ['cstf', K_], [K_], scale=LG[:, d * 2 + j:d * 2 + j + 1])
                    act(KDEC[:, d, :], LGH[:, d * 4:(d + 1) * 4], AF.Exp, ['cstf', K_], [K_], scale=cstf[:, 4, 68 + d:69 + d])
                with contextlib.ExitStack() as st2:
                    ta = sb(st2, "rtta", [128, 128], F32)
                    tb = sb(st2, "rttb", [128, 128], F32)
                    for h in range(4):
                        act(ta[:], cstf[:, 0, :], AF.Exp, ['cstf', K_], ['rtta'], scale=LGH[:, h:h + 1])
                        tt('dve', ta[:], ta[:], cstf[:, 2, :], ALU.mult, ['rtta', 'cstf'], ['rtta'])
                        act(tb[:], cstf[:, 1, :], AF.Exp, ['cstf', K_], ['rttb'], scale=LGH[:, 4 + h:5 + h])
                        tt('dve', tb[:], tb[:], cstf[:, 3, :], ALU.mult, ['rttb', 'cstf'], ['rttb'])
                        tt('dve', DS[:, h, :], ta[:], tb[:], ALU.add, ['rtta', 'rttb'], [K_])
                    ts('dve', tb8[:], pv('bin', 16, 2), 0.125, None, ALU.mult, None, ['pvt'], [K_])
                    S.barrier()
                if stop == 'ret_tab':
                    return
                with contextlib.ExitStack() as st2:
                    wr = sb(st2, "rtw", [128, 8, 1024], BF16)
                    S.dma('pool', wr[:], dr['w_in'][l][:, :, 1792:2816], writes=['rtw'])
                    brow = sb(st2, "rtbrow", [128, 256], F32)
                    S.dma('sp', brow[:], dr['rows'][l][:, 4352:4608], writes=['rtbrow'])
                    COS = sb(st2, "rtcos", [128, 2048], F32)
                    SIN = sb(st2, "rtsin", [128, 2048], F32)
                    permf = sb(st2, "rtperm", [128, 128], F32)
                    S.dma('sp', COS[:], dr['rcos'], writes=['rtcos'])
                    S.dma('act', SIN[:], dr['rsin'], writes=['rtsin'])
                    S.dma('sp', permf[:], dr['cst'][:, 0, :], writes=['rtperm'])
                    qf = [sb(st2, "rtqf%d" % i, [128, 512], F32) for i in range(2)]
                    t1 = [sb(st2, "rtt1_%d" % i, [128, 512], F32) for i in range(2)]
                    pp = [ps(st2, "rtpp%d" % i, [128, 512], F32) for i in range(2)]
                    pq = [ps(st2, "rtpq%d" % i, [128, 512], F32) for i in range(2)]
                    pt = [ps(st2, "rtpt%d" % i, [128, 512], F32) for i in range(2)]
                    def rtproj(cnt, rc, m, n0, nn):
                        ukeys = uTk[n0 // 128:(n0 + nn) // 128]
                        p_, pk_ = pp[cnt % 2], 'rtpp%d' % (cnt % 2)
                        for jj in range(8):
                            mm(p_[:, 0:nn], wr[:, jj, m * 128:(m + 1) * 128], uT[:, jj, n0:n0 + nn], ['rtw'] + ukeys, [pk_],
                               start=(jj == 0), stop=(jj == 7))
                        yield
                        if m >= 6:
                            act(zs[:, m - 6, n0:n0 + nn], p_[:, 0:nn], AF.Silu, [pk_, 'pvt'], ['rtzs'], bias=pv('bin', 14 + m))
                            return
                        isk = m >= 2
                        j = m % 2
                        dst = (KR if isk else QR)[:, j, n0:n0 + nn]
                        dk = 'rtKR' if isk else 'rtQR'
                        if n0 < 256:
                            if isk:
                                act(dst, p_[:, 0:nn], AF.Identity, [pk_, K_], [dk], bias=tb8[:, j:j + 1], scale=0.125)
                            else:
                                act(dst, p_[:, 0:nn], AF.Identity, [pk_, 'pvt'], [dk], bias=pv('bin', 14 + m))
                            return
                        q_, qk_ = qf[rc % 2], 'rtqf%d' % (rc % 2)
                        a_, ak_ = t1[rc % 2], 'rtt1_%d' % (rc % 2)
                        r_, rk_ = pq[rc % 2], 'rtpq%d' % (rc % 2)
                        if isk:
                            act(q_[:, 0:nn], p_[:, 0:nn], AF.Identity, [pk_, K_], [qk_], bias=tb8[:, j:j + 1], scale=0.125)
                        else:
                            act(q_[:, 0:nn], p_[:, 0:nn], AF.Identity, [pk_, 'pvt'], [qk_], bias=pv('bin', 14 + m))
                        yield
                        mm(r_[:, 0:nn], permf[:], q_[:, 0:nn], ['rtperm', qk_], [rk_])
                        yield
                        tsl = slice(n0 - 256, n0 - 256 + nn)
                        tt('dve', a_[:, 0:nn], r_[:, 0:nn], SIN[:, tsl], ALU.mult, [rk_, 'rtsin'], [ak_])
                        tt('pool', q_[:, 0:nn], q_[:, 0:nn], COS[:, tsl], ALU.mult, [qk_, 'rtcos'], [qk_])
                        yield
                        tt('dve', dst, a_[:, 0:nn], q_[:, 0:nn], ALU.add, [ak_, qk_], [dk])

                    plist = []
                    cnt = 0
                    rc = 0
                    for (n0, nn) in BLOCKS:
                        for m in (0, 1, 2, 3, 6, 7):
                            plist.append((cnt, rc, m, n0, nn))
                            cnt += 1
                            if m < 6 and n0 >= 256:
                                rc += 1
                    run_pipelined((rtproj(*p) for p in plist), 2)
                    for t in range(NTL):
                        p_, pk_ = pt[t % 2], 'rtpt%d' % (t % 2)
                        for jj in range(8):
                            mm(p_[:, 0:256], uT[:, jj, t * 128:(t + 1) * 128], wr[:, jj, 512:768], ['rtw', uTk[t]], [pk_],
                               start=(jj == 0), stop=(jj == 7))
                        tt('dve', VT[:, t, :], p_[:, 0:256], brow[:], ALU.add, [pk_, 'rtbrow'], ['rtVT'])
                    S.barrier()
                if stop == 'ret_proj':
                    return
                with contextlib.ExitStack() as st2:
                    Sst = [sb(st2, "rtS%d" % d, [128, 2, 64], F32) for d in range(2)]
                    kT = [sb(st2, "rtkT%d" % i, [128, 256], BF16) for i in range(3)]
                    ptr = [ps(st2, "rtptr%d" % i, [128, 8, 128], BF16) for i in range(2)]
                    pU = [ps(st2, "rtpU%d" % i, [128, 512], F32) for i in range(3)]
                    orders = [list(range(NTL)), [1, 0] + list(range(NTL - 1, 1, -1))]
                    for d in range(2):
                        memset('pool', Sst[d][:], 0.0, ['rtS%d' % d])
                    def rtchain(it, step, d):
                        t = orders[d][step]
                        pr, prk = ptr[it % 2], 'rtptr%d' % (it % 2)
                        kt, ktk = kT[it % 3], 'rtkT%d' % (it % 3)
                        pu, puk = pU[it % 3], 'rtpU%d' % (it % 3)
                        puv = pu[:, 0:128].rearrange("p (a b) -> p a b", a=2)
                        for j in range(2):
                            tr(pr[:, j, :], KR[:, j, t * 128:(t + 1) * 128], identb, ['rtKR', 'cstb'], [prk])
                        yield
                        tt('dve', kt[:].rearrange("p (h k) -> p h k", h=4), pr[:, 0:2, :].rearrange("p a (b k) -> p (a b) k", b=2),
                           KDEC[:, d, :].unsqueeze(2).broadcast_to([128, 4, 64]), ALU.mult, [prk, K_], [ktk])
                        yield
                        for h in range(4):
                            hp = (h % 2) * 64
                            mm(puv[hp:hp + 64, h // 2, :], kt[:, h * 64:(h + 1) * 64], VT[:, t, h * 64:(h + 1) * 64], [ktk, 'rtVT'], [puk])
                        yield
                        cp('act', Sall[d][:, :, t, :], Sst[d][:], ['rtS%d' % d], ['rtSall%d_%d' % (d, t)])
                        for j in range(2):
                            stt(Sst[d][:, j, :], Sst[d][:, j, :], GL[:, d * 2 + j:d * 2 + j + 1], puv[:, j, :], ALU.mult, ALU.add,
                                ['rtS%d' % d, K_, puk], ['rtS%d' % d])

                    run_pipelined((rtchain(i_, sd[0], sd[1]) for i_, sd in enumerate([(s_, d_) for s_ in range(NTL) for d_ in range(2)])), 2)
                    S.barrier()
                if stop == 'ret_chain':
                    return
                with contextlib.ExitStack() as st2:
                    if ('yc%d' % l) in debug:
                        dbgbuf = sb(st2, "dbgbuf", [128, 2, NT], F32)
                    AT = [sb(st2, "rtAT%d" % i, [128, 4, 128], BF16) for i in range(2)]
                    qd = [[sb(st2, "rtqd%d_%d" % (i, d), [128, 2, 128], BF16) for d in range(2)] for i in range(2)]
                    sq = [sb(st2, "rtsq%d" % i, [128, 2, 128], BF16) for i in range(2)]
                    rr = [sb(st2, "rtrr%d" % i, [128, 2, 128], F32) for i in range(2)]
                    ob = [sb(st2, "rtob%d" % i, [128, 2, 128], F32) for i in range(2)]
                    bank = mkbanks(st2, 8, "rtbk")

                    def rtout(t):
                        i2 = t % 2
                        tsl = slice(t * 128, (t + 1) * 128)
                        pas = [bank() for _ in range(2)]
                        for h in range(4):
                            hp = (h % 2) * 64
                            pav = pas[h % 2][0][:, 0:256].rearrange("p (a b) -> p a b", a=2)
                            mm(pav[:, h // 2, :], KR[hp:hp + 64, h // 2, tsl], QR[hp:hp + 64, h // 2, tsl], ['rtKR', 'rtQR'], [pas[h % 2][1]])
                        for d in range(2):
                            tt('pool', qd[i2][d][:], QR[:, :, tsl], QDEC[:, d, :, :], ALU.mult, ['rtQR', K_], ['rtqd%d_%d' % (i2, d)])
                        yield
                        for par in range(2):
                            pav = pas[par][0][:, 0:256].rearrange("p (a b) -> p a b", a=2)
                            tt('dve', AT[i2][:, par::2, :], pav, DS[:, par::2, :], ALU.mult, [pas[par][1], K_], ['rtAT%d' % i2])
                        yield
                        pos = [bank() for _ in range(2)]
                        povs = [pos[par][0][:, 0:256].rearrange("p (a b) -> p a b", a=2) for par in range(2)]
                        for h in range(4):
                            hp = (h % 2) * 64
                            pok = pos[h % 2][1]
                            reg = povs[h % 2][hp:hp + 64, h // 2, :]
                            mm(reg, VT[:, t, h * 64:(h + 1) * 64], AT[i2][:, h, :], ['rtVT', 'rtAT%d' % i2], [pok], start=True, stop=False)
                            for d in range(2):
                                mm(reg, Sall[d][hp:hp + 64, h // 2, t, :], qd[i2][d][hp:hp + 64, h // 2, :],
                                   ['rtSall%d_%d' % (d, t), 'rtqd%d_%d' % (i2, d)], [pok], start=False, stop=(d == 1))
                        yield
                        obk = 'rtob%d' % i2
                        cp('act', ob[i2][0:64], povs[0][0:64], [pos[0][1]], [obk])
                        cp('dve', ob[i2][64:128], povs[1][64:128], [pos[1][1]], [obk])
                        yield
                        pov = ob[i2][:]
                        pok = obk
                        if ('yc%d' % l) in debug:
                            cp('pool', dbgbuf[:, :, tsl], pov, [pok], ['dbgbuf'])
                        act(sq[i2][:], pov, AF.Square, [pok], ['rtsq%d' % i2])
                        yield
                        pss_, psk = bank()
                        psv = pss_[:, 0:256].rearrange("p (a b) -> p a b", a=2)
                        for j in range(2):
                            mm(psv[:, j, :], bonesb, sq[i2][:, j, :], ['cstb', 'rtsq%d' % i2], [psk])
                        yield
                        act(rr[i2][:], psv, AF.Sqrt, [psk], ['rtrr%d' % i2], bias=RMS_EPS, scale=1.0 / 64)
                        yield
                        S.op('dve', lambda e: e.reciprocal(out=rr[i2][:], in_=rr[i2][:]), reads=['rtrr%d' % i2], writes=['rtrr%d' % i2])
                        yield
                        tt('dve', rr[i2][:], pov, rr[i2][:], ALU.mult, [pok, 'rtrr%d' % i2], ['rtrr%d' % i2])
                        yield
                        tt('pool', Y[:, 2, :, tsl], rr[i2][:], zs[:, :, tsl], ALU.mult, ['rtrr%d' % i2, 'rtzs'], ['Y2'])

                    run_pipelined((rtout(t) for t in range(NTL) if not (last and t < 2 and not debug)), 5)
                    if ('yc%d' % l) in debug:
                        dbg_dump('yc%d' % l, dbgbuf[:], [128, 2, NT], ['dbgbuf'])
                    S.barrier()
                S.barrier()
        PHASES['ret'] = phase_ret
        def phase_rw(l, h_src, last):
            with contextlib.ExitStack() as st:
                RB = sb(st, "rwRB", [128, 2, NT], BF16)
                KB = sb(st, "rwKB", [128, 2, NT], BF16)
                VB = sb(st, "rwVB", [128, 2, NT], BF16)
                LB = sb(st, "rwLB", [128, NT], BF16)
                zs = sb(st, "rwzs", [128, 2, NT], BF16)
                vT = sb(st, "rwvT", [128, NTL, 256], BF16)
                lw2b = sb(st, "rwlw2", [128, 2, 256], BF16)
                S.dma('pool', lw2b[:], dr['lw2'][l], writes=['rwlw2'])
                oka = sb(st, "rwoka", [128, 2], F32)
                ts('dve', oka[:], pv('ka'), -1.0, 1.0, ALU.mult, ALU.add, ['pvt'], ['rwoka'])
                seen_b, seen_o = set(), set()
                with contextlib.ExitStack() as st2:
                    ww = sb(st2, "rww", [128, 8, 1152], BF16)
                    S.dma('pool', ww[:, :, 0:640], dr['w_in'][l][:, :, 2816:3456], writes=['rww'])
                    S.dma('pool', ww[:, :, 640:1152], dr['w_in'][l][:, :, 3456:3968], writes=['rww'])
                    XR = sb(st2, "rwXR", [128, NT + 4], F32)
                    XS = sb(st2, "rwXS", [128, NT], F32)
                    c0 = sb(st2, "rwc0", [128, 7], F32)
                    pp = [ps(st2, "rwpp%d" % i, [128, 512], F32) for i in range(3)]
                    ptr = [ps(st2, "rwptr%d" % i, [128, 8, 128], BF16) for i in range(2)]
                    o_mu, _ = PV['mu']
                    mu0, mu1 = pvt[:, o_mu:o_mu + 7], pvt[:, o_mu + 7:o_mu + 14]
                    tt('dve', c0[:], mu0, mu1, ALU.add, ['pvt'], ['rwc0'])
                    ts('dve', c0[:], c0[:], -1.0, 1.0, ALU.mult, ALU.add, ['rwc0'], ['rwc0'])
                    memset('pool', XR[:], 0.0, ['rwXR'])
                    cnt = 0
                    for m in range(9):
                        for (n0, nn) in BLOCKS:
                            p_, pk_ = pp[cnt % 3], 'rwpp%d' % (cnt % 3)
                            cnt += 1
                            for jj in range(8):
                                mm(p_[:, 0:nn], ww[:, jj, m * 128:(m + 1) * 128], uT[:, jj, n0:n0 + nn],
                                   ['rww'] + uTk[n0 // 128:(n0 + nn) // 128], [pk_], start=(jj == 0), stop=(jj == 7))
                            if m >= 7:
                                act(zs[:, m - 7, n0:n0 + nn], p_[:, 0:nn], AF.Silu, [pk_, 'pvt'], ['rwzs'], bias=pv('bin', 22 + m))
                            else:
                                xo = 1 if n0 < 256 else 3
                                act(XR[:, n0 + xo:n0 + xo + nn], p_[:, 0:nn], AF.Identity, [pk_, 'pvt'], ['rwXR'], bias=pv('bin', 22 + m))
                        if m >= 7:
                            continue
                        for (b0, ln, o0) in ((1, 256, 0), (259, 2048, 256)):
                            ts('dve', XS[:, o0:o0 + ln], XR[:, b0:b0 + ln], c0[:, m:m + 1], None, ALU.mult, None, ['rwXR', 'rwc0'], ['rwXS'])
                            stt(XS[:, o0:o0 + ln], XR[:, b0 - 1:b0 - 1 + ln], mu0[:, m:m + 1], XS[:, o0:o0 + ln], ALU.mult, ALU.add,
                                ['rwXR', 'pvt', 'rwXS'], ['rwXS'])
                            stt(XS[:, o0:o0 + ln], XR[:, b0 + 1:b0 + 1 + ln], mu1[:, m:m + 1], XS[:, o0:o0 + ln], ALU.mult, ALU.add,
                                ['rwXR', 'pvt', 'rwXS'], ['rwXS'])
                        if m < 6:
                            dstT, dk = [(RB, 'rwRB'), (KB, 'rwKB'), (VB, 'rwVB')][m // 2]
                            cp('act', dstT[:, m % 2, :], XS[:], ['rwXS'], [dk])
                        else:
                            act(LB[0:64, :], XS[0:64, :], AF.Tanh, ['rwXS'], ['rwLB'])
                            cp('pool', LB[64:128, :], XS[64:128, :], ['rwXS'], ['rwLB'])
                    for t in range(NTL):
                        pr, prk = ptr[t % 2], 'rwptr%d' % (t % 2)
                        for j in range(2):
                            tr(pr[:, j, :], VB[:, j, t * 128:(t + 1) * 128], identb, ['rwVB', 'cstb'], [prk])
                        cp('dve' if t % 2 == 0 else 'act', vT[:, t, :], pr[:, 0:2, :].rearrange("p a b -> p (a b)"), [prk], ['rwvT'])
                    S.barrier()
                if stop == 'rw_proj':
                    return
                OS = sb(st, "rwOS", [128, 2, NT], F32)
                with contextlib.ExitStack() as st2:
                    def B(name, shape, dt=BF16):
                        return sb(st2, "rw_" + name, shape, dt), "rw_" + name
                    R64, R64k = B("R64", [128, 256], F32)
                    memset('pool', R64[:], 1.0, [R64k])
                    memset('pool', R64[:, 0:256:64], 0.0, [R64k])
                    LW, LWk = B("LW", [128, 2, 128], F32)
                    SA, SAk = B("SA", [128, 2, 128], F32)
                    LGm, LGk = B("LG", [128, 2, 128], F32)
                    EG, EGk = B("EG", [128, 2, 128], F32)
                    ENG, ENGk = B("ENG", [128, 2, 128], F32)
                    EGM, EGMk = B("EGM", [128, 2, 128], F32)
                    U0, U0k = B("U0", [128, 2, 128], F32)
                    TA, TAk = B("TA", [128, 2, 128], F32)
                    TB_, TBk = B("TB", [128, 2, 128], F32)
                    SQ, SQk = B("SQ", [128, 2, 128])
                    RKD, RKDk = B("RKD", [128, 2, 128])
                    OBt = (None, None)
                    Zst = [B("Z%d" % d, [128, 2, 64], F32) for d in range(2)]
                    BUF = [dict() for _ in range(2)]
                    for d_ in range(2):
                        BUF[d_]['KKN'] = B("KKN_%d" % d_, [128, 2, 128])
                        BUF[d_]['KT'] = B("KT_%d" % d_, [128, 3, 2, 128])
                        BUF[d_]['RT'] = B("RT_%d" % d_, [128, 2, 128])
                        for j_ in range(2):
                            sfx = "_%d_%d" % (d_, j_)
                            SB = dict()
                            SB['TM'] = B("TM" + sfx, [128, 3, 128])
                            for nm_ in ('A1T', 'A2T', 'A3T', 'A4T', 'ALT', 'Tm', 'TTm', 'Xb', 'RHS', 'BYb'):
                                SB[nm_] = B(nm_ + sfx, [128, 2, 128])
                            SB['NY'] = B("NY" + sfx, [128, 2, 64])
                            SB['RH'] = B("RH" + sfx, [128, 128])
                            SB['GTb'] = B("GTb" + sfx, [128, 2, 128])
                            SB['ZLG'] = B("ZLG" + sfx, [128, 2, 64], F32)
                            SB['Z0b'] = B("Z0b" + sfx, [128, 2, 64])
                            BUF[d_][j_] = SB
                        BUF[d_]['GLt'] = B("GLt_%d" % d_, [128, 2, 2], F32)
                    banks = [ps(st2, "rwbank%d" % i, [128, 512], F32) for i in range(8)]
                    bcnt = [0]

                    def bank():
                        i = bcnt[0] % 8
                        bcnt[0] += 1
                        return banks[i], 'rwbank%d' % i
                    for d in range(2):
                        memset('pool', Zst[d][0][:], 0.0, [Zst[d][1], 'rw_Zs_%d_0' % d, 'rw_Zs_%d_1' % d])
                    for d_ in range(2):
                        for j_ in range(2):
                            memset('pool', BUF[d_][j_]['GTb'][0][:], 0.0, [BUF[d_][j_]['GTb'][1]])
                    orders = [list(range(NTL)), [1, 0] + list(range(NTL - 1, 1, -1))]
                    bc3 = lambda ap: ap.unsqueeze(2).broadcast_to([128, 2, 128])
                    def unit(d, t):
                        KKN, KKNk = BUF[d]['KKN']
                        KT, KTk = BUF[d]['KT']
                        RTb, RTk = BUF[d]['RT']
                        GLt, GLk = BUF[d]['GLt']
                        tsl = slice(t * 128, (t + 1) * 128)
                        rev = (d == 1)
                        Z, Zk = Zst[d]
                        plw, plwk = bank()
                        pla, plak = bank()
                        plwv = plw[:, 0:256].rearrange("p (j t) -> p j t", j=2)
                        plav = pla[:, 0:256].rearrange("p (j t) -> p j t", j=2)
                        wb_ = 32 * d
                        for j in range(2):
                            mm(plwv[:, j, :], lw2b[wb_:wb_ + 16, d, j * 128:(j + 1) * 128], LB[wb_:wb_ + 16, tsl], ['rwlw2', 'rwLB'], [plwk])
                        for j in range(2):
                            mm(plav[:, j, :], lw2b[64:96, d, j * 128:(j + 1) * 128], LB[64:96, tsl], ['rwlw2', 'rwLB'], [plak])
                        for j in range(2):
                            act(LW[:, j, :], plwv[:, j, :], AF.Sigmoid, [plwk, 'pvt'], [LWk], bias=pv('w0', d * 2 + j))
                            act(SA[:, j, :], plav[:, j, :], AF.Sigmoid, [plak, 'pvt'], [SAk], bias=pv('a0', d * 2 + j))
                        ts('dve', LW[:], LW[:], -0.6065306597126334, None, ALU.mult, None, [LWk], [LWk])
                        lwf = LW[:].rearrange("p a b -> p (a b)")
                        lgf = LGm[:].rearrange("p a b -> p (a b)")
                        if not rev:
                            S.op('dve', lambda e: e.tensor_tensor_scan(out=lgf, data0=R64[:], data1=lwf, initial=0.0, op0=ALU.mult, op1=ALU.add),
                                 reads=[LWk, R64k], writes=[LGk])
                        else:
                            S.op('dve', lambda e: e.tensor_tensor_scan(out=lgf[:, ::-1], data0=R64[:], data1=lwf[:, ::-1], initial=0.0,
                                                                       op0=ALU.mult, op1=ALU.add), reads=[LWk, R64k], writes=[LGk])
                        act(EG[:], LGm[:], AF.Exp, [LGk], [EGk])
                        act(ENG[:], LGm[:], AF.Exp, [LGk], [ENGk], scale=-1.0)
                        tt('pool', TA[:], LGm[:], LW[:], ALU.subtract, [LGk, LWk], [TAk])
                        act(EGM[:], TA[:], AF.Exp, [TAk], [EGMk])
                        gsrc = EG[:, :, 63::64] if not rev else EG[:, :, 0::64]
                        cp('pool', GLt[:], gsrc, [EGk], [GLk])
                        if stop == 'rw_u1':
                            return
                        tt('dve', TA[:], KB[:, :, tsl], bc3(pv('kk')), ALU.mult, ['rwKB', 'pvt', TAk], [TAk])
                        act(SQ[:], TA[:], AF.Square, [TAk], [SQk])
                        pss_, pssk = bank()
                        pssv = pss_[:, 0:256].rearrange("p (a b) -> p a b", a=2)
                        for j in range(2):
                            mm(pssv[:, j, :], bonesb, SQ[:, j, :], ['cstb', SQk], [pssk])
                        act(TB_[:], pssv, AF.Sqrt, [pssk], [TBk])
                        ts('dve', TB_[:], TB_[:], 1e-12, None, ALU.max, None, [TBk], [TBk])
                        S.op('dve', lambda e: e.reciprocal(out=TB_[:], in_=TB_[:]), reads=[TBk], writes=[TBk])
                        tt('dve', KKN[:], TA[:], TB_[:], ALU.mult, [TAk, TBk], [KKNk])
                        if stop == 'rw_u2':
                            return
                        tt('pool', KT[:, 0], KKN[:], EGM[:], ALU.mult, [KKNk, EGMk], [KTk])
                        tt('dve', TA[:], SA[:], ENG[:], ALU.mult, [SAk, ENGk, TAk], [TAk])
                        tt('pool', KT[:, 1], KKN[:], TA[:], ALU.mult, [KKNk, TAk], [KTk])
                        tt('dve', U0[:], SA[:], bc3(pv('ka')), ALU.mult, [SAk, 'pvt'], [U0k])
                        tt('dve', U0[:], U0[:], bc3(oka[:]), ALU.add, [U0k, 'rwoka'], [U0k])
                        tt('pool', TB_[:], U0[:], ENG[:], ALU.mult, [U0k, ENGk, TBk], [TBk])
                        tt('pool', KT[:, 2], KB[:, :, tsl], TB_[:], ALU.mult, ['rwKB', TBk], [KTk])
                        tt('dve', RTb[:], RB[:, :, tsl], EG[:], ALU.mult, ['rwRB', EGk], [RTk])
                        tt('dve', U0[:], U0[:], KB[:, :, tsl], ALU.mult, [U0k, 'rwKB'], [U0k])
                        tt('dve', U0[:], U0[:], bc3(pv('rk')), ALU.mult, [U0k, 'pvt'], [U0k])
                        tt('pool', RKD[:], U0[:], RB[:, :, tsl], ALU.mult, [U0k, 'rwRB'], [RKDk])
                        pbn, pbnk = bank()
                        pbnv = pbn[:, 0:256].rearrange("p (a b) -> p a b", a=2)
                        for j in range(2):
                            mm(pbnv[:, j, :], bonesb, RKD[:, j, :], ['cstb', RKDk], [pbnk])
                        if t not in seen_b:
                            seen_b.add(t)
                            tt('dve', Y[:, 3, :, tsl], pbnv, VB[:, :, tsl], ALU.mult, [pbnk, 'rwVB'], ['Y3'])
                        else:
                            tt('dve', TA[:], pbnv, VB[:, :, tsl], ALU.mult, [pbnk, 'rwVB', TAk], [TAk])
                            tt('pool', Y[:, 3, :, tsl], Y[:, 3, :, tsl], TA[:], ALU.add, ['Y3', TAk], ['Y3'])
                        if stop == 'rw_u3':
                            return
                        subs = [stream(d, j, t, rev, tsl, KT, KTk, RTb, RTk, GLt, GLk) for j in range(2)]
                        while subs:
                            for g in list(subs):
                                try:
                                    next(g)
                                except StopIteration:
                                    subs.remove(g)
                                yield

                    def stream(d, j, t, rev, tsl, KT, KTk, RTb, RTk, GLt, GLk):
                        SB = BUF[d][j]
                        TM, TMk = SB['TM']
                        A1T, A1k = SB['A1T']
                        A2T, A2k = SB['A2T']
                        A3T, A3k = SB['A3T']
                        A4T, A4k = SB['A4T']
                        ALT, ALk = SB['ALT']
                        Tm, Tmk = SB['Tm']
                        TTm, TTk = SB['TTm']
                        Xb, Xbk = SB['Xb']
                        RHS, RHSk = SB['RHS']
                        BYb, BYk = SB['BYb']
                        NY, NYk = SB['NY']
                        RH, RHk = SB['RH']
                        GTb, GTk = SB['GTb']
                        ZLG, ZLGk = SB['ZLG']
                        Z0b, Z0k = SB['Z0b']
                        Z, _zk = Zst[d]
                        Zk = 'rw_Zs_%d_%d' % (d, j)
                        ptb, ptbk = bank()
                        ptv = ptb[:].bitcast(BF16).rearrange("p (a b) -> p a b", a=8)
                        for x in range(3):
                            tr(ptv[:, x, :], KT[:, x, j, :], identb, [KTk, 'cstb'], [ptbk])
                        yield
                        cp('act', TM[:], ptv[:, 0:3, :], [ptbk], [TMk])
                        yield

                        def amat(dst, dstk, li, ri_src, ri_k, mslot):
                            pas = []
                            for par in range(2):
                                hp = par * 64
                                pa, pak = bank()
                                rhs = (RTb[hp:hp + 64, j, :] if ri_src is None else KT[hp:hp + 64, ri_src, j, :])
                                mm(pa[:, 0:128], KT[hp:hp + 64, li, j, :], rhs, [KTk, ri_k], [pak])
                                pas.append((pa, pak))
                            return pas

                        def aevac(pas, dst, dstk, mslot):
                            for par, (pa, pak) in enumerate(pas):
                                if mslot is None:
                                    cp('act', dst[:, par, :], pa[:, 0:128], [pak], [dstk])
                                else:
                                    tt('dve', dst[:, par, :], pa[:, 0:128], maskb[:, mslot, :], ALU.mult, [pak, 'maskb'], [dstk])
                        for (dst, dstk, li, rs, rk, ms) in ((A1T, A1k, 1, 0, KTk, None), (A2T, A2k, 2, 0, KTk, 2 + d),
                                                            (A3T, A3k, 1, None, RTk, 4 + d), (A4T, A4k, 2, None, RTk, 4 + d)):
                            pas = amat(dst, dstk, li, rs, rk, ms)
                            yield
                            aevac(pas, dst, dstk, ms)
                            yield
                        idb2 = identb.unsqueeze(1).broadcast_to([128, 2, 128])
                        cp('pool', Tm[:], idb2, ['cstb'], [Tmk])
                        cp('pool', TTm[:], idb2, ['cstb'], [TTk])
                        for lv in range(6):
                            tt('pool', ALT[:], A1T[:], maskb[:, 6 + d * 6 + lv, :].unsqueeze(1).broadcast_to([128, 2, 128]), ALU.mult,
                               [A1k, 'maskb'], [ALk])
                            yield
                            px, pxk = bank()
                            pxv = px[:, 0:256].rearrange("p (h t) -> p h t", h=2)
                            for par in range(2):
                                mm(pxv[:, par, :], ALT[:, par, :], Tm[:, par, :], [ALk, Tmk], [pxk])
                            yield
                            cp('act', Xb[:], pxv, [pxk], [Xbk])
                            yield
                            py_, pyk = bank()
                            pyv = py_[:].rearrange("p (x h t) -> p x h t", x=2, h=2)
                            for par in range(2):
                                mm(pyv[:, 0, par, :], Xb[:, par, :], TTm[:, par, :], [Xbk, TTk], [pyk])
                            if lv < 5:
                                for par in range(2):
                                    mm(pyv[:, 1, par, :], TTm[:, par, :], Xb[:, par, :], [Xbk, TTk], [pyk])
                            yield
                            if lv < 5:
                                tt('dve', Tm[:], Tm[:], pyv[:, 1], ALU.subtract, [Tmk, pyk], [Tmk])
                            tt('dve', TTm[:], TTm[:], pyv[:, 0], ALU.subtract, [TTk, pyk], [TTk])
                            yield
                        pw, pwk = bank()
                        pwv = pw[:, 0:128].rearrange("p (h v) -> p h v", h=2)
                        for par in range(2):
                            h = 2 * j + par
                            mm(pwv[:, par, :], A2T[:, par, :], vT[:, t, h * 64:(h + 1) * 64], [A2k, 'rwvT'], [pwk])
                        cp('pool', RHS[:, :, 0:64], TM[:, 0, :].rearrange("p (h k) -> p h k", h=2), [TMk], [RHSk])
                        yield
                        cp('act', RHS[:, :, 64:128], pwv, [pwk], [RHSk])
                        yield
                        pby, pbyk = bank()
                        pbyv = pby[:, 0:256].rearrange("p (h t) -> p h t", h=2)
                        for par in range(2):
                            mm(pbyv[:, par, :], TTm[:, par, :], RHS[:, par, :], [TTk, RHSk], [pbyk])
                        yield
                        cp('act', BYb[:], pbyv, [pbyk], [BYk])
                        yield
                        ts('pool', NY[:], BYb[:, :, 64:128], -1.0, 0.0, ALU.mult, ALU.add, [BYk], [NYk])
                        pr_, prk = bank()
                        for par in range(2):
                            hp = par * 64
                            mm(pr_[hp:hp + 64, 0:128], BYb[:, par, 0:64], A3T[:, par, :], [BYk, A3k], [prk])
                        yield
                        tt('dve', RH[:], RTb[:, j, :], pr_[:, 0:128], ALU.subtract, [RTk, prk], [RHk])
                        yield
                        for c in range(2):
                            cs = slice(c * 64, (c + 1) * 64)
                            pg_, pgk = bank()
                            pgv = pg_[:, 0:128].rearrange("p (x v) -> p x v", x=2)
                            for par in range(2):
                                hp = par * 64
                                h = 2 * j + par
                                hc = slice(h * 64, (h + 1) * 64)
                                pc = slice(par * 64, (par + 1) * 64)
                                mm(pgv[hp:hp + 64, 0, :], BYb[cs, par, 0:64], TM[cs, 1, pc], [BYk, TMk], [pgk])
                                mm(pgv[hp:hp + 64, 1, :], TM[cs, 2, pc], vT[cs, t, hc], [TMk, 'rwvT'], [pgk], start=True, stop=False)
                                mm(pgv[hp:hp + 64, 1, :], TM[cs, 1, pc], NY[cs, par, :], [TMk, NYk], [pgk], start=False, stop=True)
                            yield
                            for par in range(2):
                                hp = par * 64
                                tt('dve', GTb[hp:hp + 64, c, hp:hp + 64], cstf[hp:hp + 64, 4, 0:64], pgv[hp:hp + 64, 0, :], ALU.subtract,
                                   ['cstf', pgk], [GTk])
                            ts('dve', ZLG[:, c, :], pgv[:, 1, :], GLt[:, j, c:c + 1], None, ALU.mult, None, [pgk, GLk], [ZLGk])
                            yield
                        for c in ((0, 1) if not rev else (1, 0)):
                            cp('act', Z0b[:, c, :], Z[:, j, :], [Zk], [Z0k])
                            yield
                            pn, pnk = bank()
                            mm(pn[:, 0:64], GTb[:, c, :], Z0b[:, c, :], [GTk, Z0k], [pnk])
                            yield
                            stt(Z[:, j, :], pn[:, 0:64], GLt[:, j, c:c + 1], ZLG[:, c, :], ALU.mult, ALU.add, [pnk, GLk, ZLGk, Zk], [Zk])
                            yield
                        for par in range(2):
                            hp = par * 64
                            h = 2 * j + par
                            hc = slice(h * 64, (h + 1) * 64)
                            po_, pok = bank()
                            reg = po_[hp:hp + 64, 0:128]
                            mm(reg, vT[:, t, hc], A4T[:, par, :], ['rwvT', A4k], [pok], start=True, stop=False)
                            mm(reg, NY[:, par, :], A3T[:, par, :], [NYk, A3k], [pok], start=False, stop=False)
                            for c in range(2):
                                mm(reg[:, c * 64:(c + 1) * 64], Z0b[hp:hp + 64, c, :], RH[hp:hp + 64, c * 64:(c + 1) * 64],
                                   [Z0k, RHk], [pok], start=False, stop=(c == 1))
                            yield
                            osl = OS[hp:hp + 64, j, tsl]
                            osk = 'rwOS%d_%d' % (t, j)
                            if (t, j, par) not in seen_o:
                                seen_o.add((t, j, par))
                                cp('dve' if par == 0 else 'act', osl, reg, [pok], [osk])
                            else:
                                tt('dve', osl, osl, reg, ALU.add, [pok, osk], [osk])
                            yield

                    for step in range(NTL):
                        if stop is not None and stop.startswith('rw_u') and step >= 1:
                            break
                        gens = [unit(d, orders[d][step]) for d in range(2)]
                        while gens:
                            for g in list(gens):
                                try:
                                    next(g)
                                except StopIteration:
                                    gens.remove(g)
                    S.barrier()
                if stop is not None and stop.startswith('rw_'):
                    return
                with contextlib.ExitStack() as st2:
                    ob = [sb(st2, "rwob%d" % i, [128, 2, 128], BF16) for i in range(2)]
                    cen = [sb(st2, "rwcen%d" % i, [128, 2, 128], F32) for i in range(2)]
                    rs = [sb(st2, "rwrs%d" % i, [128, 2, 128], F32) for i in range(2)]
                    pm_ = [ps(st2, "rwpm%d" % i, [128, 512], F32) for i in range(2)]
                    pv_ = [ps(st2, "rwpv%d" % i, [128, 512], F32) for i in range(2)]
                    for t in range(NTL):
                        i2 = t % 2
                        tsl = slice(t * 128, (t + 1) * 128)
                        osk = 'rwOS%d_0' % t
                        osk1 = 'rwOS%d_1' % t
                        cp('act', ob[i2][:], OS[:, :, tsl], [osk, osk1], ['rwob%d' % i2])
                        pmv = pm_[i2][:, 0:256].rearrange("p (a b) -> p a b", a=2)
                        for j in range(2):
                            mm(pmv[:, j, :], bonesb, ob[i2][:, j, :], ['cstb', 'rwob%d' % i2], ['rwpm%d' % i2])
                        stt(cen[i2][:], pmv, -1.0 / 64, OS[:, :, tsl], ALU.mult, ALU.add, ['rwpm%d' % i2, osk, osk1], ['rwcen%d' % i2])
                        act(ob[i2][:], cen[i2][:], AF.Square, ['rwcen%d' % i2], ['rwob%d' % i2])
                        pvv = pv_[i2][:, 0:256].rearrange("p (a b) -> p a b", a=2)
                        for j in range(2):
                            mm(pvv[:, j, :], bonesb, ob[i2][:, j, :], ['cstb', 'rwob%d' % i2], ['rwpv%d' % i2])
                        act(rs[i2][:], pvv, AF.Sqrt, ['rwpv%d' % i2], ['rwrs%d' % i2], bias=RW_GN_EPS, scale=1.0 / 64)
                        S.op('dve', lambda e: e.reciprocal(out=rs[i2][:], in_=rs[i2][:]), reads=['rwrs%d' % i2], writes=['rwrs%d' % i2])
                        tt('dve', cen[i2][:], cen[i2][:], rs[i2][:], ALU.mult, ['rwcen%d' % i2, 'rwrs%d' % i2], ['rwcen%d' % i2])
                        tt('pool', cen[i2][:], cen[i2][:], bc3(pv('gnw')), ALU.mult, ['rwcen%d' % i2, 'pvt'], ['rwcen%d' % i2])
                        tt('pool', cen[i2][:], cen[i2][:], bc3(pv('gnb')), ALU.add, ['rwcen%d' % i2, 'pvt'], ['rwcen%d' % i2])
                        tt('dve', cen[i2][:], cen[i2][:], Y[:, 3, :, tsl], ALU.add, ['rwcen%d' % i2, 'Y3'], ['rwcen%d' % i2])
                        if ('yd%d' % l) in debug:
                            cp('act', OS[:, :, tsl], cen[i2][:], ['rwcen%d' % i2], [osk, osk1])
                        tt('dve', Y[:, 3, :, tsl], cen[i2][:], zs[:, :, tsl], ALU.mult, ['rwcen%d' % i2, 'rwzs'], ['Y3'])
                    if ('yd%d' % l) in debug:
                        dbg_dump('yd%d' % l, OS[:], [128, 2, NT], ['rwOS%d_%d' % (t, j_) for t in range(NTL) for j_ in range(2)])
                    S.barrier()
                S.barrier()
        PHASES['rw'] = phase_rw
        def phase_merge(l, h_src, last):
            h_dst = out_d if last else h1_d
            with contextlib.ExitStack() as st:
                MG = sb(st, "mgMG", [128, 8, NT], BF16)
                wbr = sb(st, "mgwbr", [128, 4, 2, DM], BF16)
                S.dma('pool', wbr[:], dr['wbr'][l], writes=['mgwbr'])
                with contextlib.ExitStack() as st2:
                    wg = [sb(st2, "mgwg%d" % i, [128, 8, 4, 128], BF16) for i in range(2)]
                    sg = [sb(st2, "mgsg%d" % i, [128, 512], BF16) for i in range(3)]
                    ac = [sb(st2, "mgac%d" % i, [128, 512], F32) for i in range(2)]
                    tm = [sb(st2, "mgtm%d" % i, [128, 512], F32) for i in range(2)]
                    pgl = [ps(st2, "mgpg%d" % i, [128, 512], F32) for i in range(3)]
                    pbr = [ps(st2, "mgpb%d" % i, [128, 512], F32) for i in range(3)]
                    cg = 0
                    ca = 0
                    def load_wg(dt_):
                        for k in range(4):
                            c0 = 3968 + k * 1024 + dt_ * 128
                            S.dma('pool', wg[dt_ % 2][:, :, k, :], dr['w_in'][l][:, :, c0:c0 + 128], writes=['mgwg%d' % (dt_ % 2)])
                    load_wg(0)
                    for dt_ in range(8):
                        w_, wk_ = wg[dt_ % 2], 'mgwg%d' % (dt_ % 2)
                        if dt_ + 1 < 8:
                            load_wg(dt_ + 1)
                        for (n0, nn) in BLOCKS:
                            if last and n0 < 256:
                                continue
                            a_, ak_ = ac[ca % 2], 'mgac%d' % (ca % 2)
                            t_, tk_ = tm[ca % 2], 'mgtm%d' % (ca % 2)
                            ca += 1
                            for k in range(4):
                                pg_, pgk_ = pgl[cg % 3], 'mgpg%d' % (cg % 3)
                                pb_, pbk_ = pbr[cg % 3], 'mgpb%d' % (cg % 3)
                                s_, sk_ = sg[cg % 3], 'mgsg%d' % (cg % 3)
                                cg += 1
                                for jj in range(8):
                                    mm(pg_[:, 0:nn], w_[:, jj, k, :], uT[:, jj, n0:n0 + nn], [wk_] + uTk[n0 // 128:(n0 + nn) // 128], [pgk_],
                                       start=(jj == 0), stop=(jj == 7))
                                act(s_[:, 0:nn], pg_[:, 0:nn], AF.Sigmoid, [pgk_, 'pvt'], [sk_], bias=pv('bin', 31 + k * 8 + dt_))
                                for jc in range(2):
                                    mm(pb_[:, 0:nn], wbr[:, k, jc, dt_ * 128:(dt_ + 1) * 128], Y[:, k, jc, n0:n0 + nn], ['mgwbr', 'Y%d' % k], [pbk_],
                                       start=(jc == 0), stop=(jc == 1))
                                if k == 0:
                                    tt('dve', a_[:, 0:nn], pb_[:, 0:nn], s_[:, 0:nn], ALU.mult, [pbk_, sk_], [ak_])
                                else:
                                    tt('dve', t_[:, 0:nn], pb_[:, 0:nn], s_[:, 0:nn], ALU.mult, [pbk_, sk_], [tk_])
                                    if k < 3:
                                        tt('pool', a_[:, 0:nn], a_[:, 0:nn], t_[:, 0:nn], ALU.add, [ak_, tk_], [ak_])
                                    else:
                                        tt('pool', MG[:, dt_, n0:n0 + nn], a_[:, 0:nn], t_[:, 0:nn], ALU.add, [ak_, tk_], ['mgMG%d' % (n0 // 512 if n0 else 9)])
                    S.barrier()
                if ('merged%d' % l) in debug:
                    with contextlib.ExitStack() as st2:
                        mf = sb(st2, "mgf", [128, 8, NT], F32)
                        cp('dve', mf[:], MG[:], ['mgMG%d' % i for i in (9, 0, 1, 2, 3)], ['mgf'])
                        dbg_dump('merged%d' % l, mf[:], [128, 8, NT], ['mgf'])
                        S.barrier()
                with contextlib.ExitStack() as st2:
                    wo = sb(st2, "mgwo", [128, 8, DM], BF16)
                    S.dma('pool', wo[:], dr['wout'][l], writes=['mgwo'])
                    rows = sb(st2, "mgrows", [128, 3, DM], F32)
                    S.dma('sp', rows[:], dr['rows'][l][:, 0:3072].rearrange("p (a b) -> p a b", a=3), writes=['mgrows'])
                    hin_ = [sb(st2, "mghin%d" % i, [128, DM], F32) for i in range(2)]
                    ot = [sb(st2, "mgot%d" % i, [128, DM], F32) for i in range(2)]
                    stat = [sb(st2, "mgst%d" % i, [128, 16], F32) for i in range(2)]
                    po = [[ps(st2, "mgpo%d_%d" % (i, hh), [128, 512], F32) for hh in range(2)] for i in range(2)]
                    def mgout(it, t):
                        i2 = it % 2
                        ci = 1 if t < 2 else 0
                        tsl = slice(t * 128, (t + 1) * 128)
                        mgk = 'mgMG%d' % (9 if t < 2 else (t - 2) // 4)
                        hk_, ok_, sk_ = 'mghin%d' % i2, 'mgot%d' % i2, 'mgst%d' % i2
                        hi, o_, sti = hin_[i2], ot[i2], stat[i2]
                        S.dma('sp', hi[:], h_src[t * 128:(t + 1) * 128, :], writes=[hk_])
                        for hh in range(2):
                            pk_ = 'mgpo%d_%d' % (i2, hh)
                            for jj in range(8):
                                mm(po[i2][hh][:], MG[:, jj, tsl], wo[:, jj, hh * 512:(hh + 1) * 512], [mgk, 'mgwo'], [pk_], start=(jj == 0), stop=(jj == 7))
                        yield
                        for hh in range(2):
                            pk_ = 'mgpo%d_%d' % (i2, hh)
                            tt('dve', o_[:, hh * 512:(hh + 1) * 512], po[i2][hh][:], rows[:, 0, hh * 512:(hh + 1) * 512], ALU.add, [pk_, 'mgrows'], [ok_])
                        yield
                        tt('dve', o_[:], o_[:], gatebc[:, ci, :], ALU.mult, [ok_, 'gatebc'], [ok_])
                        yield
                        stt(o_[:], hi[:], ALPHA, o_[:], ALU.mult, ALU.add, [hk_, ok_], [ok_])
                        yield
                        S.op('dve', lambda e: e.bn_stats(out=sti[:, 0:6], in_=o_[:, 0:512]), reads=[ok_], writes=[sk_])
                        S.op('dve', lambda e: e.bn_stats(out=sti[:, 6:12], in_=o_[:, 512:1024]), reads=[ok_], writes=[sk_])
                        yield
                        S.op('dve', lambda e: e.bn_aggr(out=sti[:, 12:14], in_=sti[:, 0:12]), reads=[sk_], writes=[sk_])
                        yield
                        act(sti[:, 14:15], sti[:, 13:14], AF.Sqrt, [sk_], [sk_], bias=LN_EPS)
                        yield
                        S.op('dve', lambda e: e.reciprocal(out=sti[:, 14:15], in_=sti[:, 14:15]), reads=[sk_], writes=[sk_])
                        yield
                        stt(sti[:, 15:16], sti[:, 12:13], -1.0, sti[:, 14:15], ALU.mult, ALU.mult, [sk_], [sk_])
                        yield
                        act(o_[:], o_[:], AF.Identity, [ok_, sk_], [ok_], bias=sti[:, 15:16], scale=sti[:, 14:15])
                        yield
                        tt('pool', o_[:, 0:512], o_[:, 0:512], rows[:, 1, 0:512], ALU.mult, [ok_, 'mgrows'], [ok_])
                        tt('dve', o_[:, 512:1024], o_[:, 512:1024], rows[:, 1, 512:1024], ALU.mult, [ok_, 'mgrows'], [ok_])
                        yield
                        tt('pool', o_[:, 0:512], o_[:, 0:512], rows[:, 2, 0:512], ALU.add, [ok_, 'mgrows'], [ok_])
                        tt('dve', o_[:, 512:1024], o_[:, 512:1024], rows[:, 2, 512:1024], ALU.add, [ok_, 'mgrows'], [ok_])
                        yield
                        if last:
                            S.dma('sp', out_d[(t - 2) * 128:(t - 1) * 128, :], o_[:], reads=[ok_], writes=['outfinal'])
                        else:
                            S.dma('sp', h1_d[t * 128:(t + 1) * 128, :], o_[:], reads=[ok_], writes=['h1'])

                    tl = [t for t in range(NTL) if not (last and t < 2)]
                    run_pipelined((mgout(i_, t) for i_, t in enumerate(tl)), 6)
                    S.barrier()
                S.barrier()
        PHASES['merge'] = phase_merge
        for l in range(nlayers):
            last = (l == nlayers - 1)
            h_src = dr['hin'] if l == 0 else h1_d
            S.dma('sp', pvt[:], dr['pv'][l], writes=['pvt'])
            with contextlib.ExitStack() as st:
                adw = [sb(st, "adw%d" % i, [128, 8, 512], F32) for i in range(2)]
                scb = sb(st, "scb", [128, 2, 8, 128], F32)
                grow = sb(st, "grow", [128, DM], F32)
                pm0 = ps(st, "pm0", [128, 16, 2], F32)
                pg = [ps(st, "pg%d" % i, [128, 512], F32) for i in range(2)]
                for i in range(2):
                    cp('dve', scb[:, i], silc[:, :, i:i + 1].broadcast_to([128, 8, 128]), ['silc'], ['scb'])
                S.dma('sp', grow[:], dr['rows'][l][:, 3072:4096], writes=['grow'])
                for ch in range(6):
                    buf = adw[ch % 2]
                    bk = 'adw%d' % (ch % 2)
                    S.dma('sp' if ch % 2 == 0 else 'act', buf[:], dr['ada_w'][l][:, :, ch * 512:(ch + 1) * 512], writes=[bk])
                    if ch < 4:
                        for mloc in range(4):
                            m = ch * 4 + mloc
                            for j in range(8):
                                mm(pm0[:, m, :], buf[:, j, mloc * 128:(mloc + 1) * 128], silc[:, j, :], [bk, 'silc'],
                                   ['pm0'], start=(j == 0), stop=(j == 7))
                    else:
                        for i in range(2):
                            for j in range(8):
                                mm(pg[i][:], scb[:, i, j, :], buf[:, j, :], [bk, 'scb'], ['pg%d' % i],
                                   start=(j == 0), stop=(j == 7))
                            tt('dve', gatebc[:, i, (ch - 4) * 512:(ch - 3) * 512], pg[i][:],
                               grow[:, (ch - 4) * 512:(ch - 3) * 512], ALU.add, ['pg%d' % i, 'grow'], ['gatebc'])
                tt('dve', modfm[:], pm0[:], pv('adab').unsqueeze(2).broadcast_to([128, 16, 2]), ALU.add,
                   ['pm0', 'pvt'], ['modfm'])
                ts('dve', modfm[:, 8:16, :], modfm[:, 8:16, :], 1.0, None, ALU.add, None, ['modfm'], ['modfm'])
                dbg_dump('modfm%d' % l, modfm[:], [128, 16, 2], ['modfm'])
                dbg_dump('gatebc%d' % l, gatebc[:], [128, 2, DM], ['gatebc'])
                S.barrier()
            with contextlib.ExitStack() as st:
                xin = [sb(st, "xin%d" % i, [128, DM], F32) for i in range(3)]
                xn = [sb(st, "xn%d" % i, [128, DM], BF16) for i in range(2)]
                stat = [sb(st, "stat%d" % i, [128, 16], F32) for i in range(3)]
                ptr = [ps(st, "ptr%d" % i, [128, 8, 128], BF16) for i in range(2)]
                def p1tile(t):
                    xi, xk = xin[t % 3], 'xin%d' % (t % 3)
                    sti, sk = stat[t % 3], 'stat%d' % (t % 3)
                    xo, xok = xn[t % 2], 'xn%d' % (t % 2)
                    pt, ptk = ptr[t % 2], 'ptr%d' % (t % 2)
                    ci = 1 if t < 2 else 0
                    S.dma('sp' if t % 2 == 0 else 'act', xi[:], h_src[t * 128:(t + 1) * 128, :], writes=[xk])
                    yield
                    S.op('dve', lambda e: e.bn_stats(out=sti[:, 0:6], in_=xi[:, 0:512]), reads=[xk], writes=[sk])
                    S.op('dve', lambda e: e.bn_stats(out=sti[:, 6:12], in_=xi[:, 512:1024]), reads=[xk], writes=[sk])
                    yield
                    S.op('dve', lambda e: e.bn_aggr(out=sti[:, 12:14], in_=sti[:, 0:12]), reads=[sk], writes=[sk])
                    yield
                    act(sti[:, 14:15], sti[:, 13:14], AF.Sqrt, [sk], [sk], bias=LN_EPS)
                    yield
                    S.op('dve', lambda e: e.reciprocal(out=sti[:, 14:15], in_=sti[:, 14:15]), reads=[sk], writes=[sk])
                    yield
                    stt(sti[:, 15:16], sti[:, 12:13], -1.0, sti[:, 14:15], ALU.mult, ALU.mult, [sk], [sk])
                    yield
                    act(xo[:], xi[:], AF.Identity, [xk, sk], [xok], bias=sti[:, 15:16], scale=sti[:, 14:15])
                    yield
                    for j in range(8):
                        tr(pt[:, j, :], xo[:, j * 128:(j + 1) * 128], identb, [xok, 'cstb'], [ptk])
                    yield
                    for j in range(8):
                        if j % 2 == 0:
                            act(uT[:, j, t * 128:(t + 1) * 128], pt[:, j, :], AF.Identity, [ptk, 'modfm'], ['uT%d' % t],
                                bias=modfm[:, j, ci:ci + 1], scale=modfm[:, 8 + j, ci:ci + 1])
                        else:
                            ts('dve', uT[:, j, t * 128:(t + 1) * 128], pt[:, j, :], modfm[:, 8 + j, ci:ci + 1],
                               modfm[:, j, ci:ci + 1], ALU.mult, ALU.add, [ptk, 'modfm'], ['uT%d' % t])

                run_pipelined((p1tile(t) for t in range(NTL)), 4)
                if ('uT%d' % l) in debug:
                    utf = sb(st, "utf", [128, 8, NT], F32)
                    cp('dve', utf[:], uT[:], ['uT%d' % t for t in range(NTL)], ['utf'])
                    dbg_dump('uT%d' % l, utf[:], [128, 8, NT], ['utf'])
                S.barrier()
            uTk = ['uT%d' % t for t in range(NTL)]

            for ph in list(PHASES):
                if ph in phases:
                    PHASES[ph](l, h_src, last)
            if ('h%d' % l) in debug and not last:
                d_ = dbg_out('h%d' % l, [NT, DM])
                S.dma('sp', d_, h1_d, writes=['dbgout_h%d' % l])
                S.barrier()
            if ('Y%d' % l) in debug:
                with contextlib.ExitStack() as st:
                    yf = sb(st, "yf", [128, 4, 2, NT], F32)
                    cp('dve', yf[:], Y[:], ['Y0', 'Y1', 'Y2', 'Y3'], ['yf'])
                    dbg_dump('Y%d' % l, yf[:], [128, 4, 2, NT], ['yf'])
                    S.barrier()

        S.final_wait('sp', ['outfinal'] + ['dbgout_' + n for n in dbg_d])
    if MEMDBG:
        print('SBUF min remaining by prefix:', minrem)
    return nc, dbg_d


def kernel(**inputs):
    inp = {k: np.asarray(v) for k, v in inputs.items()}
    sh = prep_shared(inp)
    nc, _ = build()
    in_maps = []
    for b in range(8):
        m = dict(sh)
        m.update(prep_core(inp, b))
        in_maps.append(m)
    res = run_bass_kernel_spmd(nc, in_maps, core_ids=list(range(8)))
    return np.stack([np.asarray(res.results[b]['out'], dtype=np.float32) for b in range(8)], 0)
```

```python
import contextlib
import numpy as np
import concourse.bass as bass
import concourse.mybir as mybir
from concourse.bass_utils import run_bass_kernel_spmd

F32 = mybir.dt.float32
BF16 = mybir.dt.bfloat16
AF = mybir.ActivationFunctionType
ALU = mybir.AluOpType
AX = mybir.AxisListType

NT = 2304
NTL = 18
DM = 1024
NCOL = 8064
BLOCKS = [(0, 256), (256, 512), (768, 512), (1280, 512), (1792, 512)]
LN_EPS = 1e-5
RMS_EPS = 1e-6
RW_GN_EPS = 64e-5
ALPHA = (2 * 2) ** 0.25
PI = float(np.pi)
MEMDBG = False


class Sched:
    NDMA = 16

    def __init__(self, nc, same_engine_waits=True):
        self.nc = nc
        self.same = same_engine_waits
        self.eng = dict(pe=nc.tensor, act=nc.scalar, dve=nc.vector, pool=nc.gpsimd, sp=nc.sync)
        self.E = {n: dict(cnt=0, known={}) for n in self.eng}
        self.dq = {'sp': ['dsp%d' % i for i in range(8)], 'act': ['dac%d' % i for i in range(4)],
                   'pool': ['dpl%d' % i for i in range(8)]}
        self.dmas = {n: dict(cnt=0) for q in self.dq.values() for n in q}
        self.dma_rr = {'sp': 0, 'act': 0, 'pool': 0}
        self.lastw = {}
        self.readers = {}
        self.sems = None
        self.nins = 0

    def sem_names(self):
        return list(self.E.keys()) + list(self.dmas.keys())

    def _deps(self, reads, writes):
        deps = {}

        def add(w):
            if w is not None:
                deps[w[0]] = max(deps.get(w[0], 0), w[1])
        for k in reads:
            add(self.lastw.get(k))
        for k in writes:
            add(self.lastw.get(k))
            for r in self.readers.get(k, ()):
                add(r)
        return deps

    def _waits(self, en, deps):
        E = self.E[en]
        waits = []
        for d, v in deps.items():
            if d == en and (en == 'pe' or not self.same):
                continue
            if E['known'].get(d, 0) < v:
                waits.append((d, v))
                E['known'][d] = v
        return waits

    def _record(self, ident, reads, writes):
        for k in writes:
            self.lastw[k] = ident
            self.readers[k] = []
        for k in reads:
            self.readers.setdefault(k, []).append(ident)

    def _emit(self, en, waits, fn, inc):
        eng = self.eng[en]
        for d, v in waits:
            eng.wait_ge(self.sems[d], v)
        if fn is not None:
            fn(eng).then_inc(self.sems[inc[0]], inc[1])
            self.nins += 1

    def op(self, en, fn, reads=(), writes=()):
        E = self.E[en]
        waits = self._waits(en, self._deps(reads, writes))
        E['cnt'] += 1
        self._emit(en, waits, fn, (en, 1))
        self._record((en, E['cnt']), reads, writes)

    def dma(self, en, out, in_, reads=(), writes=(), **kw):
        dn = self.dq[en][self.dma_rr[en]]
        self.dma_rr[en] = (self.dma_rr[en] + 1) % len(self.dq[en])
        Dq = self.dmas[dn]
        deps = self._deps(reads, writes)
        if Dq['cnt'] > 0:
            deps[dn] = max(deps.get(dn, 0), Dq['cnt'])
        waits = self._waits(en, deps)
        Dq['cnt'] += 16
        self._emit(en, waits, (lambda e: e.dma_start(out=out, in_=in_, **kw)), (dn, 16))
        self._record((dn, Dq['cnt']), reads, writes)

    def barrier(self):
        cur = {n: self.E[n]['cnt'] for n in self.E}
        cur.update({n: self.dmas[n]['cnt'] for n in self.dmas})
        for en in self.E:
            waits = self._waits(en, {d: v for d, v in cur.items() if v > 0})
            self._emit(en, waits, None, None)

    def final_wait(self, en, keys):
        self._emit(en, self._waits(en, self._deps(keys, ())), None, None)


PV = {}


def _pv_layout():
    off = 0
    for name, n in [('bin', 63), ('s5d', 2), ('glub', 2), ('hglb', 8), ('hgnw', 2), ('rdec', 4), ('mu', 14),
                    ('w0', 4), ('a0', 4), ('kk', 2), ('ka', 2), ('rk', 2), ('gnw', 2), ('gnb', 2), ('adab', 16),
                    ('lamre', 16), ('lamim', 16), ('ldt', 16), ('rdech', 8)]:
        PV[name] = (off, n)
        off += n
    return off


NPV = _pv_layout()


def _colmap():
    cm = list(range(0, 3584))
    lora = [-1] * 128
    for r in range(16):
        lora[r] = 3584 + r
        lora[32 + r] = 3600 + r
        lora[64 + r] = 3616 + r
        lora[80 + r] = 3632 + r
    cm += lora
    cm += list(range(3648, 3904))
    cm += list(range(3904, 8000))
    return np.array(cm)


CMAP = _colmap()


def _fm(v):
    return np.ascontiguousarray(v.reshape(-1, 128).T)


def _masks():
    t = np.arange(128)
    s_, t_ = t[:, None], t[None, :]
    m = []
    b32 = (s_ // 32) == (t_ // 32)
    b64 = (s_ // 64) == (t_ // 64)
    m.append(b32 & (t_ >= s_))
    m.append(b32 & (t_ <= s_))
    m.append(b64 & (t_ > s_))
    m.append(b64 & (t_ < s_))
    m.append(b64 & (t_ >= s_))
    m.append(b64 & (t_ <= s_))
    for d in range(2):
        for lv in range(6):
            sz = 1 << lv
            blk = (s_ // (2 * sz)) == (t_ // (2 * sz))
            hs, ht = (s_ // sz) % 2, (t_ // sz) % 2
            if d == 0:
                m.append(blk & (ht == 1) & (hs == 0))
            else:
                m.append(blk & (ht == 0) & (hs == 1))
    return np.stack([x.astype(np.float32) for x in m], 1)


def _rot_tables():
    n = 16
    freqs = 10000.0 ** (-np.arange(n, dtype=np.float32) / n)
    tt = np.arange(2048)
    rows = (tt // 64).astype(np.float32)
    cols = (tt % 64).astype(np.float32)
    cos = np.zeros((128, 2048), np.float32)
    sins = np.zeros((128, 2048), np.float32)
    pm = np.zeros((128, 128), np.float32)
    for p in range(128):
        i = p % 64
        pos = rows if i < 32 else cols
        ii = i % 32
        ang = pos * freqs[ii % 16]
        cos[p] = np.cos(ang)
        if ii < 16:
            sins[p] = -np.sin(ang)
            partner = p + 16
        else:
            sins[p] = np.sin(ang)
            partner = p - 16
        pm[partner, p] = 1.0
    return cos, sins, pm


def prep_shared(inp):
    sh = {}
    L = 2
    w_in = inp['w_in']
    wn = np.zeros((L, 1024, NCOL), np.float32)
    valid = CMAP >= 0
    wn[:, :, valid] = w_in[:, :, CMAP[valid]]
    sh['w_in'] = np.ascontiguousarray(wn.reshape(L, 8, 128, NCOL).transpose(0, 2, 1, 3))
    bn = np.zeros((L, NCOL), np.float32)
    bn[:, valid] = inp['b_in'][:, CMAP[valid]]
    sh['ada_w'] = np.ascontiguousarray(inp['ada_w'].reshape(L, 8, 128, 3072).transpose(0, 2, 1, 3))
    pv = np.zeros((L, 128, NPV), np.float32)

    def put(l, name, arr):
        o, n = PV[name]
        assert arr.shape == (128, n), (name, arr.shape)
        pv[l, :, o:o + n] = arr
    for l in range(L):
        put(l, 'bin', _fm(bn[l]))
        put(l, 's5d', _fm(inp['s5_d'][l]))
        put(l, 'glub', _fm(inp['s5_glu_b'][l]))
        put(l, 'hglb', np.concatenate([_fm(inp['hg_lb'][ll, d]) for ll in range(2) for d in range(2)], 1))
        put(l, 'hgnw', _fm(inp['hg_norm_w'][l]))
        rd = np.zeros((128, 4), np.float32)
        for d in range(2):
            for j in range(2):
                rd[:64, d * 2 + j] = inp['ret_decay'][l, d, 2 * j]
                rd[64:, d * 2 + j] = inp['ret_decay'][l, d, 2 * j + 1]
        put(l, 'rdec', rd)
        put(l, 'rdech', np.ascontiguousarray(np.broadcast_to(inp['ret_decay'][l].reshape(1, 8), (128, 8))))
        mu = np.zeros((2, 7 * 128), np.float32)
        mu[:, :768] = inp['rw_mu'][l][:, :768]
        lv = CMAP[3584:3712]
        ok = lv >= 0
        mu[:, 768:896][:, ok] = inp['rw_mu'][l][:, lv[ok] - 2816]
        put(l, 'mu', np.concatenate([_fm(mu[0]), _fm(mu[1])], 1))
        put(l, 'w0', np.concatenate([_fm(inp['rw_w0'][l, d]) for d in range(2)], 1))
        put(l, 'a0', np.concatenate([_fm(inp['rw_a0'][l, d]) for d in range(2)], 1))
        for nm, key in [('kk', 'rw_kk'), ('ka', 'rw_ka'), ('rk', 'rw_rk'), ('gnw', 'rw_gn_w'), ('gnb', 'rw_gn_b')]:
            put(l, nm, _fm(inp[key][l]))
        put(l, 'adab', _fm(inp['ada_b'][l][:2048]))
        for nm, key in [('lamre', 's5_lam_re'), ('lamim', 's5_lam_im')]:
            a = inp[key][l].reshape(2, 8, 2, 64)
            put(l, nm, np.ascontiguousarray(a.transpose(2, 3, 0, 1).reshape(128, 16)))
        a = np.broadcast_to(inp['s5_log_dt'][l].reshape(2, 8, 2, 1), (2, 8, 2, 64))
        put(l, 'ldt', np.ascontiguousarray(a.transpose(2, 3, 0, 1).reshape(128, 16)))
    sh['pv'] = pv
    bt = np.zeros((L, 128, 2, 4, 2, 128), np.float32)
    ct = np.zeros((L, 128, 8, 2, 128), np.float32)
    for l in range(L):
        for g in range(16):
            i, g2 = g // 2, g % 2
            for q in range(16):
                c = g * 16 + q
                j, p = c // 128, c % 128
                bt[l, p, j, i % 4, 0, g2 * 64:(g2 + 1) * 64] = inp['s5_b_re'][l, g, :, q]
                bt[l, p, j, i % 4, 1, g2 * 64:(g2 + 1) * 64] = inp['s5_b_im'][l, g, :, q]
            m0 = (i % 4) * 32 + g2 * 16
            ct[l, g2 * 64:(g2 + 1) * 64, i, 0, m0:m0 + 16] = inp['s5_c_re'][l, g].T
            ct[l, g2 * 64:(g2 + 1) * 64, i, 1, m0:m0 + 16] = inp['s5_c_im'][l, g].T
    sh['s5bt'] = bt
    sh['s5ct'] = ct
    sh['gluw'] = np.ascontiguousarray(inp['s5_glu_w'].reshape(L, 2, 128, 256).transpose(0, 2, 1, 3))
    lw2 = np.zeros((L, 128, 2, 256), np.float32)
    for l in range(L):
        lw2[l, 0:16, 0] = inp['rw_w2'][l, 0]
        lw2[l, 32:48, 1] = inp['rw_w2'][l, 1]
        lw2[l, 64:80, 0] = inp['rw_a2'][l, 0]
        lw2[l, 80:96, 1] = inp['rw_a2'][l, 1]
    sh['lw2'] = lw2
    sh['wbr'] = np.ascontiguousarray(inp['w_branch'].reshape(L, 4, 2, 128, 1024).transpose(0, 3, 1, 2, 4))
    sh['wout'] = np.ascontiguousarray(inp['w_out'].reshape(L, 8, 128, 1024).transpose(0, 2, 1, 3))
    rows = np.zeros((L, 128, 4096 + 512), np.float32)
    for l in range(L):
        rows[l, :, 0:1024] = inp['b_out'][l][None]
        rows[l, :, 1024:2048] = inp['ln_w'][l][None]
        rows[l, :, 2048:3072] = inp['ln_b'][l][None]
        rows[l, :, 3072:4096] = inp['ada_b'][l][None, 2048:3072]
        rows[l, :, 4096:4352] = inp['b_in'][l][None, 1280:1536]
        rows[l, :, 4352:4608] = inp['b_in'][l][None, 2304:2560]
    sh['rows'] = rows
    sh['masks'] = _masks()
    cos, sins, pm = _rot_tables()
    sh['rcos'] = cos
    sh['rsin'] = sins
    t = np.arange(128)
    cst = np.zeros((128, 9, 128), np.float32)
    cst[:, 0] = pm
    cst[:, 1] = ((t[:, None] // 64) == (t[None, :] // 64))
    cst[:, 2] = np.maximum(t[None, :] - t[:, None], 0)
    cst[:, 3] = np.maximum(t[:, None] - t[None, :], 0)
    cst[:, 4] = (t[None, :] >= t[:, None])
    cst[:, 5] = (t[None, :] <= t[:, None])
    cst[:, 6, :64] = ((t[:, None] % 64) == np.arange(64)[None, :])
    cst[:, 6, 64:68] = ((t[:, None] // 32) == np.arange(4)[None, :])
    cst[:, 6, 68] = 127 - t
    cst[:, 6, 69] = t
    cst[:, 7] = t[None, :] + 1.0
    cst[:, 8] = 128.0 - t[None, :]
    sh['cst'] = cst
    return sh


def prep_core(inp, b):
    pc = {}
    pc['hin'] = np.ascontiguousarray(np.concatenate([inp['ctx'][b], inp['x'][b]], 0))
    cv = np.stack([inp['c'][b], inp['c_ctx']], -1)
    pc['cvec'] = np.ascontiguousarray(cv.reshape(8, 128, 2).transpose(1, 0, 2))
    return pc


SHAPES = dict(hin=[NT, DM], cvec=[128, 8, 2], w_in=[2, 128, 8, NCOL], ada_w=[2, 128, 8, 3072], pv=[2, 128, NPV],
              s5bt=[2, 128, 2, 4, 2, 128], s5ct=[2, 128, 8, 2, 128], gluw=[2, 128, 2, 256], lw2=[2, 128, 2, 256],
              wbr=[2, 128, 4, 2, 1024], wout=[2, 128, 8, 1024], rows=[2, 128, 4608], masks=[128, 18, 128],
              rcos=[128, 2048], rsin=[128, 2048], cst=[128, 9, 128])


def build(debug=(), nlayers=2, phases=('s5', 'hg', 'ret', 'rw', 'merge'), stop=None):
    nc = bass.Bass("TRN2", target_bir_lowering=False)
    S = Sched(nc)
    dr = {k: nc.dram_tensor(k, list(v), F32, kind="ExternalInput").ap() for k, v in SHAPES.items()}
    out_d = nc.dram_tensor("out", [2048, DM], F32, kind="ExternalOutput").ap()
    h1_d = nc.dram_tensor("h1", [NT, DM], F32, kind="Internal").ap()
    dbg_d = {}

    def dbg_out(name, shape):
        dbg_d[name] = nc.dram_tensor("dbg_" + name, list(shape), F32, kind="ExternalOutput").ap()
        return dbg_d[name]

    uid = [0]

    def key(p='k'):
        uid[0] += 1
        return '%s%d' % (p, uid[0])

    with contextlib.ExitStack() as top:
        S.sems = {n: top.enter_context(nc.semaphore(n)) for n in S.sem_names()}

        minrem = {}

        def sb(st, name, shape, dt=F32):
            uid[0] += 1
            t_ = st.enter_context(nc.sbuf_tensor("%s_%d" % (name, uid[0]), list(shape), dt))
            if MEMDBG:
                pre = name[:2]
                minrem[pre] = min(minrem.get(pre, 1 << 30), nc.sbuf_bytes_remaining)
            return t_

        def ps(st, name, shape, dt=F32):
            uid[0] += 1
            return st.enter_context(nc.psum_tensor("%s_%d" % (name, uid[0]), list(shape), dt))

        def mm(out, lhsT, rhs, r, w, start=True, stop=True):
            S.op('pe', lambda e: e.matmul(out, lhsT=lhsT, rhs=rhs, start=start, stop=stop), reads=r, writes=w)

        def tr(out, in_, ident, r, w):
            S.op('pe', lambda e: e.transpose(out, in_, ident), reads=r, writes=w)

        def act(out, in_, func, r, w, bias=0.0, scale=1.0):
            S.op('act', lambda e: e.activation(out=out, in_=in_, func=func, bias=bias, scale=scale), reads=r, writes=w)

        def tt(en, out, in0, in1, op, r, w):
            S.op(en, lambda e: e.tensor_tensor(out=out, in0=in0, in1=in1, op=op), reads=r, writes=w)

        def ts(en, out, in0, s1, s2, op0, op1, r, w):
            if s2 is None:
                S.op(en, lambda e: e.tensor_scalar(out=out, in0=in0, scalar1=s1, scalar2=None, op0=op0), reads=r, writes=w)
            else:
                S.op(en, lambda e: e.tensor_scalar(out=out, in0=in0, scalar1=s1, scalar2=s2, op0=op0, op1=op1),
                     reads=r, writes=w)

        def stt(out, in0, sc, in1, op0, op1, r, w):
            S.op('dve', lambda e: e.scalar_tensor_tensor(out=out, in0=in0, scalar=sc, in1=in1, op0=op0, op1=op1),
                 reads=r, writes=w)

        def cp(en, out, in_, r, w):
            if en == 'act':
                S.op('act', lambda e: e.copy(out=out, in_=in_), reads=r, writes=w)
            else:
                S.op(en, lambda e: e.tensor_copy(out=out, in_=in_), reads=r, writes=w)

        def memset(en, ap, val, w):
            S.op(en, lambda e: e.memset(ap, val), writes=w)

        def run_pipelined(gens, stagger):
            it = iter(gens)
            active, pending, rounds = [], True, 0
            while pending or active:
                if pending and rounds % stagger == 0:
                    try:
                        active.append(next(it))
                    except StopIteration:
                        pending = False
                for g in list(active):
                    try:
                        next(g)
                    except StopIteration:
                        active.remove(g)
                rounds += 1

        def mkbanks(st_, n, prefix):
            bl = [ps(st_, "%s%d" % (prefix, i), [128, 512], F32) for i in range(n)]
            cnt = [0]

            def bank():
                i = cnt[0] % n
                cnt[0] += 1
                return bl[i], '%s%d' % (prefix, i)
            return bank

        def dbg_dump(name, ap, shape, r):
            if name in debug:
                d = dbg_out(name, shape)
                S.dma('sp', d, ap, reads=r, writes=['dbgout_' + name])

        cstb = sb(top, "cstb", [128, 3, 128], BF16)
        cstf = sb(top, "cstf", [128, 7, 128], F32)
        maskb = sb(top, "maskb", [128, 18, 128], BF16)
        silc = sb(top, "silc", [128, 8, 2], F32)
        S.dma('pool', cstb[:, 0:2, :], dr['cst'][:, 0:2, :], writes=['cstb'])
        S.dma('sp', cstf[:], dr['cst'][:, 2:9, :], writes=['cstf'])
        S.dma('pool', maskb[:], dr['masks'], writes=['maskb'])
        S.dma('sp', silc[:], dr['cvec'], writes=['silc'])
        memset('pool', cstb[:, 2, :], 0.0, ['cstb'])
        S.op('pool', lambda e: e.affine_select(out=cstb[:, 2, :], in_=cstb[:, 2, :], pattern=[[-1, 128]],
                                               compare_op=ALU.not_equal, fill=1.0, base=0, channel_multiplier=1),
             reads=['cstb'], writes=['cstb'])
        act(silc[:], silc[:], AF.Silu, ['silc'], ['silc'])
        identb = cstb[:, 2, :]
        bonesb = cstb[:, 1, :]

        uT = sb(top, "uT", [128, 8, NT], BF16)
        Y = sb(top, "Y", [128, 4, 2, NT], BF16)
        pvt = sb(top, "pvt", [128, NPV], F32)
        if debug:
            memset('pool', Y[:], 0.0, ['Y0', 'Y1', 'Y2', 'Y3'])
        modfm = sb(top, "modfm", [128, 16, 2], F32)
        gatebc = sb(top, "gatebc", [128, 2, DM], F32)

        def pv(name, j=None, n=1):
            o, cnt = PV[name]
            if j is None:
                return pvt[:, o:o + cnt]
            return pvt[:, o + j:o + j + n]

        PHASES = {}
        def proj_fm(st, wt, wk, mlist, evac, pp, ppk):
            cnt = 0
            for (n0, nn) in BLOCKS:
                for mi, m in enumerate(mlist):
                    p_, pk_ = pp[cnt % len(pp)], ppk[cnt % len(pp)]
                    cnt += 1
                    for j in range(8):
                        mm(p_[:, 0:nn], wt[:, j, m * 128:(m + 1) * 128], uT[:, j, n0:n0 + nn],
                           ['%s%d' % (wk, m // 2)] + uTk[n0 // 128:(n0 + nn) // 128], [pk_], start=(j == 0), stop=(j == 7))
                    evac(mi, m, n0, nn, p_, pk_)

        def phase_s5(l, h_src, last):
            L = 128
            with contextlib.ExitStack() as st:
                btb = sb(st, "btb", [128, 2, 4, 2, 128], BF16)
                ctb = sb(st, "ctb", [128, 8, 2, 128], BF16)
                glub = sb(st, "glub", [128, 2, 256], BF16)
                S.dma('pool', btb[:], dr['s5bt'][l], writes=['btb'])
                S.dma('pool', ctb[:], dr['s5ct'][l], writes=['ctb'])
                S.dma('pool', glub[:], dr['gluw'][l], writes=['glub'])
                ts('pool', ctb[:, :, 1, :], ctb[:, :, 1, :], -1.0, 0.0, ALU.mult, ALU.add, ['ctb'], ['ctb'])
                ub = sb(st, "s5u", [128, 2, NT], BF16)
                zs = sb(st, "s5z", [128, 2, NT], BF16)
                yacc = sb(st, "yacc", [128, 2, NT], F32)
                PT = sb(st, "s5PT", [128, 16, 2, L], F32)
                QT = sb(st, "s5QT", [128, 16, 2, L], F32)
                sst = sb(st, "s5st", [128, 16, 2], F32)
                ones = sb(st, "s5ones", [128, L], F32)
                memset('pool', yacc[:], 0.0, ['yacc'])
                memset('pool', sst[:], 0.0, ['sst'])
                memset('pool', ones[:], 1.0, ['s5ones'])
                with contextlib.ExitStack() as st2:
                    wsu = sb(st2, "wsu", [128, 8, 512], BF16)
                    for pc_ in range(2):
                        S.dma('pool', wsu[:, :, pc_ * 256:(pc_ + 1) * 256], dr['w_in'][l][:, :, pc_ * 256:(pc_ + 1) * 256], writes=['wsu%d' % pc_])
                    pp = [ps(st2, "s5pp%d" % i, [128, 512], F32) for i in range(2)]

                    def evac(mi, m, n0, nn, p_, pk_):
                        if m < 2:
                            act(ub[:, m, n0:n0 + nn], p_[:, 0:nn], AF.Identity, [pk_, 'pvt'], ['s5u'], bias=pv('bin', m))
                        else:
                            act(zs[:, m - 2, n0:n0 + nn], p_[:, 0:nn], AF.Silu, [pk_, 'pvt'], ['s5z'], bias=pv('bin', m))
                    proj_fm(st2, wsu, 'wsu', [0, 1, 2, 3], evac, pp, ['s5pp0', 's5pp1'])
                    sm = sb(st2, "s5sm", [128, 20, 16], F32)
                    K_ = 's5sm'

                    def Sm(i):
                        return sm[:, i, :]

                    def T2(o, a, b, op):
                        tt('dve', Sm(o), a if not isinstance(a, int) else Sm(a), b if not isinstance(b, int) else Sm(b), op,
                           [K_, 'pvt'], [K_])
                    lamre, lamim = pv('lamre'), pv('lamim')
                    act(Sm(0), pv('ldt'), AF.Exp, ['pvt'], [K_])
                    T2(1, lamre, 0, ALU.mult)
                    act(Sm(2), Sm(1), AF.Exp, [K_], [K_])
                    act(Sm(3), Sm(1), AF.Exp, [K_], [K_], scale=-1.0)
                    T2(4, lamim, 0, ALU.mult)
                    ts('dve', Sm(5), Sm(4), PI / 2, None, ALU.add, None, [K_], [K_])
                    for x in (4, 5):
                        for _ in range(4):
                            ts('dve', Sm(16), Sm(x), PI, 2 * PI, ALU.is_gt, ALU.mult, [K_], [K_])
                            T2(x, x, 16, ALU.subtract)
                    act(Sm(6), Sm(4), AF.Sin, [K_], [K_])
                    act(Sm(7), Sm(5), AF.Sin, [K_], [K_])
                    T2(8, 2, 7, ALU.mult)
                    T2(9, 2, 6, ALU.mult)
                    T2(10, 3, 7, ALU.mult)
                    stt(Sm(11), Sm(3), -1.0, Sm(6), ALU.mult, ALU.mult, [K_], [K_])
                    ts('dve', Sm(12), Sm(8), -1.0, None, ALU.add, None, [K_], [K_])
                    T2(16, lamre, lamre, ALU.mult)
                    T2(17, lamim, lamim, ALU.mult)
                    T2(13, 16, 17, ALU.add)
                    S.op('dve', lambda e: e.reciprocal(out=Sm(13), in_=Sm(13)), reads=[K_], writes=[K_])
                    T2(16, 12, lamre, ALU.mult)
                    T2(17, 9, lamim, ALU.mult)
                    T2(16, 16, 17, ALU.add)
                    T2(14, 16, 13, ALU.mult)
                    T2(16, 9, lamre, ALU.mult)
                    T2(17, 12, lamim, ALU.mult)
                    T2(16, 16, 17, ALU.subtract)
                    T2(15, 16, 13, ALU.mult)
                    tmpa = sb(st2, "s5ta", [128, 16, L], F32)
                    tmpb = sb(st2, "s5tb", [128, 16, L], F32)

                    def cmul_bc(dst_re, dst_im, src_re, src_im, s_re, s_im, m):
                        sr = s_re.unsqueeze(2).broadcast_to([128, 16, m])
                        si = s_im.unsqueeze(2).broadcast_to([128, 16, m])
                        ta, tb = tmpa[:, :, 0:m], tmpb[:, :, 0:m]
                        kk_ = ['s5tab', 's5ta', 's5tb', 's5tc', K_]
                        tt('dve', ta, src_re, sr, ALU.mult, kk_, ['s5ta'])
                        tt('dve', tb, src_im, si, ALU.mult, kk_, ['s5tb'])
                        tt('dve', dst_re, ta, tb, ALU.subtract, kk_, ['s5tab'])
                        tt('dve', ta, src_re, si, ALU.mult, kk_, ['s5ta'])
                        tt('dve', tb, src_im, sr, ALU.mult, kk_, ['s5tb'])
                        tt('dve', dst_im, ta, tb, ALU.add, kk_, ['s5tab'])
                    for (TB, a_re, a_im) in ((PT, 8, 9), (QT, 10, 11)):
                        cp('dve', TB[:, :, 0, 0], Sm(a_re), [K_], ['s5tab'])
                        cp('dve', TB[:, :, 1, 0], Sm(a_im), [K_], ['s5tab'])
                        m = 1
                        while m < L:
                            cmul_bc(TB[:, :, 0, m:2 * m], TB[:, :, 1, m:2 * m], TB[:, :, 0, 0:m], TB[:, :, 1, 0:m],
                                    TB[:, :, 0, m - 1], TB[:, :, 1, m - 1], m)
                            m *= 2
                    tmpc = sb(st2, "s5tc", [128, 16, L], F32)
                    cp('dve', tmpc[:], QT[:, :, 0, :], ['s5tab'], ['s5tc'])
                    cmul_bc(QT[:, :, 0, :], QT[:, :, 1, :], tmpc[:], QT[:, :, 1, :], Sm(14), Sm(15), L)
                    S.barrier()
                with contextlib.ExitStack() as st2:
                    NB = 8
                    xa = [sb(st2, "s5xa%d" % i, [128, 2, L], F32) for i in range(NB)]
                    xb_ = [sb(st2, "s5xb%d" % i, [128, 2, L], F32) for i in range(NB)]
                    cw = [sb(st2, "s5cw%d" % i, [128, 2, L], F32) for i in range(NB)]
                    hb = [sb(st2, "s5hb%d" % i, [128, 2, L], BF16) for i in range(NB)]
                    pbu = [ps(st2, "s5pb%d" % i, [128, 2, 2, L], F32) for i in range(4)]
                    py = [ps(st2, "s5py%d" % i, [128, 512], F32) for i in range(2)]
                    orders = [list(range(NTL)), [1, 0] + list(range(NTL - 1, 1, -1))]
                    def s5group(gi, step, d, j):
                        c = orders[d][step]
                        n0 = c * L
                        rev = (d == 1)
                        U = []
                        for ii in range(4):
                            un = gi * 4 + ii
                            bnk = (un // 2) % 4
                            U.append(dict(ii=ii, i=j * 4 + ii, q=d * 8 + j * 4 + ii, pb=pbu[bnk][:, un % 2], pbk='s5pb%d' % bnk,
                                          A=xa[un % NB], Ak='s5xa%d' % (un % NB), B=xb_[un % NB], Bk='s5xb%d' % (un % NB),
                                          C=cw[un % NB], Ck='s5cw%d' % (un % NB), H=hb[un % NB], Hk='s5hb%d' % (un % NB)))
                        for u in U:
                            for ri in range(2):
                                mm(u['pb'][:, ri, :], btb[:, j, u['ii'], ri, :], ub[:, j, n0:n0 + L], ['btb', 's5u'], [u['pbk']])
                        yield
                        for u in U:
                            src = u['pb'][:, :, ::-1] if rev else u['pb'][:, :, :]
                            tt('dve', u['A'][:], src, QT[:, u['q'], 0:1, :].broadcast_to([128, 2, L]), ALU.mult,
                               [u['pbk'], 's5tab'], [u['Ak']])
                        yield
                        for u in U:
                            src = u['pb'][:, ::-1, ::-1] if rev else u['pb'][:, ::-1, :]
                            tt('dve', u['B'][:], src, QT[:, u['q'], 1:2, :].broadcast_to([128, 2, L]), ALU.mult,
                               [u['pbk'], 's5tab'], [u['Bk']])
                        yield
                        for u in U:
                            tt('dve', u['A'][:, 0, :], u['A'][:, 0, :], u['B'][:, 0, :], ALU.subtract, [u['Ak'], u['Bk']], [u['Ak']])
                        yield
                        for u in U:
                            tt('dve', u['A'][:, 1, :], u['A'][:, 1, :], u['B'][:, 1, :], ALU.add, [u['Ak'], u['Bk']], [u['Ak']])
                        yield
                        for ri in range(2):
                            for u in U:
                                q = u['q']
                                S.op('dve', lambda e, u=u, ri=ri, q=q: e.tensor_tensor_scan(
                                    out=u['C'][:, ri, :], data0=ones[:], data1=u['A'][:, ri, :], initial=sst[:, q, ri:ri + 1],
                                    op0=ALU.mult, op1=ALU.add), reads=[u['Ak'], 's5ones', 'sst%d' % q, 'sst'], writes=[u['Ck']])
                            yield
                        for u in U:
                            tt('pool', u['A'][:], u['C'][:], PT[:, u['q'], 0:1, :].broadcast_to([128, 2, L]), ALU.mult,
                               [u['Ck'], 's5tab', u['Ak']], [u['Ak']])
                        yield
                        for u in U:
                            tt('pool', u['B'][:], u['C'][:, ::-1, :], PT[:, u['q'], 1:2, :].broadcast_to([128, 2, L]), ALU.mult,
                               [u['Ck'], 's5tab', u['Bk']], [u['Bk']])
                        yield
                        for u in U:
                            tt('pool', u['A'][:, 0, :], u['A'][:, 0, :], u['B'][:, 0, :], ALU.subtract, [u['Ak'], u['Bk']], [u['Ak']])
                        yield
                        for u in U:
                            tt('pool', u['A'][:, 1, :], u['A'][:, 1, :], u['B'][:, 1, :], ALU.add, [u['Ak'], u['Bk']], [u['Ak']])
                        yield
                        for u in U:
                            cp('pool', sst[:, u['q'], :], u['A'][:, :, L - 1], [u['Ak']], ['sst%d' % u['q']])
                        yield
                        for u in U:
                            hsrc = u['A'][:, :, ::-1] if rev else u['A'][:]
                            cp('act', u['H'][:], hsrc, [u['Ak']], [u['Hk']])
                        yield
                        pyr = py[gi % 2][:, 0:L]
                        pyk = 's5py%d' % (gi % 2)
                        for k_, u in enumerate(U):
                            for ri in range(2):
                                mm(pyr, ctb[:, u['i'], ri, :], u['H'][:, ri, :], ['ctb', u['Hk']], [pyk],
                                   start=(k_ == 0 and ri == 0), stop=(k_ == 3 and ri == 1))
                        yield
                        yield
                        yield
                        tt('dve', yacc[:, j, n0:n0 + L], yacc[:, j, n0:n0 + L], pyr, ALU.add, [pyk, 'yacc'], ['yacc'])

                    glist = [(step, d, j) for step in range(NTL) for d in range(2) for j in range(2)]
                    run_pipelined((s5group(gi, *g) for gi, g in enumerate(glist)), 11)
                    S.barrier()
                for j in range(2):
                    stt(yacc[:, j, :], ub[:, j, :], pv('s5d', j), yacc[:, j, :], ALU.mult, ALU.add, ['s5u', 'yacc', 'pvt'],
                        ['yacc'])
                dbg_dump('ya%d' % l, yacc[:], [128, 2, NT], ['yacc'])
                with contextlib.ExitStack() as st2:
                    t1 = [sb(st2, "s5g1_%d" % i, [128, 512], F32) for i in range(2)]
                    t2 = [sb(st2, "s5g2_%d" % i, [128, 512], BF16) for i in range(2)]
                    pg = [ps(st2, "s5pg%d" % i, [128, 512], F32) for i in range(2)]
                    cnt = 0
                    for (n0, nn) in BLOCKS:
                        for j in range(2):
                            a, ak = t1[cnt % 2], 's5g1_%d' % (cnt % 2)
                            cnt += 1
                            ysl = yacc[:, j, n0:n0 + nn]
                            act(a[:, 0:nn], ysl, AF.Square, ['yacc'], [ak])
                            ts('dve', a[:, 0:nn], a[:, 0:nn], 0.044715, 1.0, ALU.mult, ALU.add, [ak], [ak])
                            tt('dve', a[:, 0:nn], a[:, 0:nn], ysl, ALU.mult, [ak, 'yacc'], [ak])
                            act(a[:, 0:nn], a[:, 0:nn], AF.Sigmoid, [ak], [ak], scale=1.5957691216057308)
                            tt('dve', ub[:, j, n0:n0 + nn], a[:, 0:nn], ysl, ALU.mult, [ak, 'yacc'], ['s5u'])
                    cnt = 0
                    for (n0, nn) in BLOCKS:
                        for m in range(2):
                            p_, pk_ = pg[cnt % 2], 's5pg%d' % (cnt % 2)
                            b_, bk_ = t2[cnt % 2], 's5g2_%d' % (cnt % 2)
                            cnt += 1
                            for jc in range(2):
                                mm(p_[:, 0:nn], glub[:, jc, m * 128:(m + 1) * 128], ub[:, jc, n0:n0 + nn], ['glub', 's5u'], [pk_],
                                   start=(jc == 0), stop=(jc == 1))
                            act(b_[:, 0:nn], p_[:, 0:nn], AF.Sigmoid, [pk_, 'pvt'], [bk_], bias=pv('glub', m))
                            tt('dve', b_[:, 0:nn], b_[:, 0:nn], ub[:, m, n0:n0 + nn], ALU.mult, [bk_, 's5u'], [bk_])
                            tt('pool', Y[:, 0, m, n0:n0 + nn], b_[:, 0:nn], zs[:, m, n0:n0 + nn], ALU.mult, [bk_, 's5z'], ['Y0'])
                    S.barrier()
                S.barrier()
        PHASES['s5'] = phase_s5
        def phase_hg(l, h_src, last):
            with contextlib.ExitStack() as st:
                QP = [sb(st, "hgQP%d" % d, [128, 2, NT], BF16) for d in range(2)]
                KP = [sb(st, "hgKP%d" % d, [128, 2, NT], BF16) for d in range(2)]
                G = sb(st, "hgG", [128, 2, 72, 2], F32)
                VT = sb(st, "hgVT", [128, NTL, 256], BF16)
                zs = sb(st, "hgzs", [128, 2, NT], BF16)
                lbt = sb(st, "hglbt", [128, 2, 4], F32)
                if l == 0:
                    memset('pool', lbt[:, 0, :], 0.0, ['hglbt'])
                    memset('pool', lbt[:, 1, :], 1.0, ['hglbt'])
                else:
                    o_, _ = PV['hglb']
                    tt('dve', lbt[:, 0, :], pvt[:, o_ + 4:o_ + 8], pvt[:, o_:o_ + 4], ALU.subtract, ['pvt'], ['hglbt'])
                    act(lbt[:, 0, :], lbt[:, 0, :], AF.Sigmoid, ['hglbt'], ['hglbt'])
                    ts('dve', lbt[:, 1, :], lbt[:, 0, :], -1.0, 1.0, ALU.mult, ALU.add, ['hglbt'], ['hglbt'])
                with contextlib.ExitStack() as st2:
                    wh = sb(st2, "hgw", [128, 8, 1280], BF16)
                    for pc_ in (0, 4, 1, 2, 3):
                        S.dma('pool', wh[:, :, pc_ * 256:(pc_ + 1) * 256], dr['w_in'][l][:, :, 512 + pc_ * 256:512 + (pc_ + 1) * 256], writes=['hgw%d' % pc_])
                    brow = sb(st2, "hgbrow", [128, 256], F32)
                    S.dma('sp', brow[:], dr['rows'][l][:, 4096:4352], writes=['hgbrow'])
                    R32 = sb(st2, "hgR32", [128, 512], F32)
                    memset('pool', R32[:], 1.0, ['hgR32'])
                    memset('pool', R32[:, 0:512:32], 0.0, ['hgR32'])
                    QS = [sb(st2, "hgQS%d" % i, [128, 2, 512], BF16) for i in range(2)]
                    T = [[sb(st2, "hgT%d_%d" % (i, k), [128, 512], F32) for k in range(4)] for i in range(2)]
                    pp = [ps(st2, "hgpp%d" % i, [128, 512], F32) for i in range(3)]
                    pt = [ps(st2, "hgpt%d" % i, [128, 512], F32) for i in range(2)]
                    def hgproj(cnt, ic, m, n0, nn):
                        ukeys = uTk[n0 // 128:(n0 + nn) // 128]
                        p_, pk_ = pp[cnt % 3], 'hgpp%d' % (cnt % 3)
                        bi = (n0 // 512) % 2 if n0 else 0
                        for jj in range(8):
                            mm(p_[:, 0:nn], wh[:, jj, m * 128:(m + 1) * 128], uT[:, jj, n0:n0 + nn], ['hgw%d' % (m // 2)] + ukeys, [pk_],
                               start=(jj == 0), stop=(jj == 7))
                        yield
                        bias = pv('bin', 4 + m)
                        if m < 2:
                            act(QS[bi][:, m, 0:nn], p_[:, 0:nn], AF.Silu, [pk_, 'pvt'], ['hgQS%d' % bi], bias=bias)
                            return
                        if m >= 8:
                            act(zs[:, m - 8, n0:n0 + nn], p_[:, 0:nn], AF.Silu, [pk_, 'pvt'], ['hgzs'], bias=bias)
                            return
                        d, j = (m - 2) // 2, (m - 2) % 2
                        Ts = T[ic % 2]
                        Tk = ['hgT%d_%d' % (ic % 2, k) for k in range(4)]
                        t1, t2, t3, t4 = [x[:, 0:nn] for x in Ts]
                        act(t1, p_[:, 0:nn], AF.Sigmoid, [pk_, 'pvt'], [Tk[0]], bias=bias)
                        yield
                        ts('dve', t1, t1, lbt[:, 1, d * 2 + j:d * 2 + j + 1], lbt[:, 0, d * 2 + j:d * 2 + j + 1], ALU.mult, ALU.add,
                           [Tk[0], 'hglbt'], [Tk[0]])
                        yield
                        act(t2, t1, AF.Ln, [Tk[0]], [Tk[1]])
                        yield
                        if d == 0:
                            S.op('dve', lambda e: e.tensor_tensor_scan(out=t3, data0=R32[:, 0:nn], data1=t2, initial=0.0,
                                                                       op0=ALU.mult, op1=ALU.add),
                                 reads=[Tk[1], 'hgR32'], writes=[Tk[2]])
                        else:
                            S.op('dve', lambda e: e.tensor_tensor_scan(out=t3[:, ::-1],
                                                                       data0=R32[:, 0:nn], data1=t2[:, ::-1], initial=0.0,
                                                                       op0=ALU.mult, op1=ALU.add),
                                 reads=[Tk[1], 'hgR32'], writes=[Tk[2]])
                        yield
                        ts('dve', t3, t3, -80.0, None, ALU.max, None, [Tk[2]], [Tk[2]])
                        ts('dve', t1, t1, -1.0, 1.0, ALU.mult, ALU.add, [Tk[0]], [Tk[0]])
                        yield
                        act(t4, t3, AF.Exp, [Tk[2]], [Tk[3]])
                        act(t2, t3, AF.Exp, [Tk[2]], [Tk[1]], scale=-1.0)
                        yield
                        tt('pool', KP[d][:, j, n0:n0 + nn], t1, t2, ALU.mult, [Tk[0], Tk[1]], ['hgKP%d' % d])
                        tt('pool', QP[d][:, j, n0:n0 + nn], QS[bi][:, j, 0:nn], t4, ALU.mult, ['hgQS%d' % bi, Tk[3]], ['hgQP%d' % d])
                        c0 = n0 // 32
                        gsrc = t4[:, 31::32] if d == 0 else t4[:, 0::32]
                        cp('act', G[:, d, c0:c0 + nn // 32, j], gsrc, [Tk[3]], ['hgG'])

                    plist = []
                    cnt = 0
                    ic = 0
                    for (n0, nn) in BLOCKS:
                        for m in (0, 1, 8, 9, 2, 3, 4, 5):
                            plist.append((cnt, ic, m, n0, nn))
                            cnt += 1
                            if 2 <= m < 8:
                                ic += 1
                    run_pipelined((hgproj(*p) for p in plist), 4)
                    for t in range(NTL):
                        p_, pk_ = pt[t % 2], 'hgpt%d' % (t % 2)
                        for jj in range(8):
                            mm(p_[:, 0:256], uT[:, jj, t * 128:(t + 1) * 128], wh[:, jj, 768:1024], ['hgw3', uTk[t]], [pk_],
                               start=(jj == 0), stop=(jj == 7))
                        tt('dve', VT[:, t, :], p_[:, 0:256], brow[:], ALU.add, [pk_, 'hgbrow'], ['hgVT'])
                    S.barrier()
                Sall = [sb(st, "hgSall%d" % d, [128, 2, 72, 64], BF16) for d in range(2)]
                with contextlib.ExitStack() as st2:
                    Sst = [sb(st2, "hgS%d" % d, [128, 2, 64], F32) for d in range(2)]
                    kTm = [sb(st2, "hgkTm%d" % i, [128, 4, 256], BF16) for i in range(3)]
                    Ug = [sb(st2, "hgUg%d" % i, [128, 4, 2, 64], F32) for i in range(3)]
                    ptr = [ps(st2, "hgptr%d" % i, [128, 8, 128], BF16) for i in range(2)]
                    pU = [ps(st2, "hgpU%d" % i, [128, 4, 2, 64], F32) for i in range(3)]
                    orders = [list(range(NTL)), [1, 0] + list(range(NTL - 1, 1, -1))]
                    for d in range(2):
                        memset('pool', Sst[d][:], 0.0, ['hgS%d' % d])
                    def hgchain(it, step, d):
                        t = orders[d][step]
                        pr, prk = ptr[it % 2], 'hgptr%d' % (it % 2)
                        km, kmk = kTm[it % 3], 'hgkTm%d' % (it % 3)
                        pu, puk = pU[it % 3], 'hgpU%d' % (it % 3)
                        ug, ugk = Ug[it % 3], 'hgUg%d' % (it % 3)
                        for j in range(2):
                            tr(pr[:, j, :], KP[d][:, j, t * 128:(t + 1) * 128], identb, ['hgKP%d' % d, 'cstb'], [prk])
                        yield
                        for cc in range(4):
                            prf = pr[:, 0:2, :].rearrange("p a b -> p (a b)")
                            if cc % 2 == 0:
                                ts('dve', km[:, cc, :], prf, cstf[:, 4, 64 + cc:64 + cc + 1], None, ALU.mult, None, [prk, 'cstf'], [kmk])
                            else:
                                act(km[:, cc, :], prf, AF.Identity, [prk, 'cstf'], [kmk], scale=cstf[:, 4, 64 + cc:64 + cc + 1])
                        yield
                        for cc in range(4):
                            for h in range(4):
                                hp = (h % 2) * 64
                                mm(pu[hp:hp + 64, cc, h // 2, :], km[:, cc, h * 64:(h + 1) * 64], VT[:, t, h * 64:(h + 1) * 64],
                                   [kmk, 'hgVT'], [puk])
                        yield
                        tt('dve', ug[:], pu[:], G[:, d, t * 4:(t + 1) * 4, :].unsqueeze(3).broadcast_to([128, 4, 2, 64]), ALU.mult,
                           [puk, 'hgG'], [ugk])
                        yield
                        ccs = range(4) if d == 0 else range(3, -1, -1)
                        for cc in ccs:
                            c = t * 4 + cc
                            cp('act', Sall[d][:, :, c, :], Sst[d][:], ['hgS%d' % d], ['hgSall%d_%d' % (d, t)])
                            for j in range(2):
                                stt(Sst[d][:, j, :], Sst[d][:, j, :], G[:, d, c, j:j + 1], ug[:, cc, j, :], ALU.mult, ALU.add,
                                    ['hgS%d' % d, 'hgG', ugk], ['hgS%d' % d])
                            yield

                    run_pipelined((hgchain(i_, sd[0], sd[1]) for i_, sd in enumerate([(s_, d_) for s_ in range(NTL) for d_ in range(2)])), 3)
                    S.barrier()
                with contextlib.ExitStack() as st2:
                    if ('yb%d' % l) in debug:
                        dbgbuf = sb(st2, "dbgbuf", [128, 2, NT], F32)
                    AT = [[sb(st2, "hgAT%d_%d" % (i, d), [128, 4, 128], BF16) for d in range(2)] for i in range(2)]
                    sq = [sb(st2, "hgsq%d" % i, [128, 2, 128], BF16) for i in range(2)]
                    rr = [sb(st2, "hgrr%d" % i, [128, 2, 128], F32) for i in range(2)]
                    ob = [sb(st2, "hgob%d" % i, [128, 2, 128], F32) for i in range(2)]
                    bank = mkbanks(st2, 8, "hgbk")

                    def hgout(t):
                        i2 = t % 2
                        tsl = slice(t * 128, (t + 1) * 128)
                        pas = {}
                        for d in range(2):
                            for par in range(2):
                                pas[(d, par)] = bank()
                            for h in range(4):
                                hp = (h % 2) * 64
                                pa, pak = pas[(d, h % 2)]
                                pav = pa[:, 0:256].rearrange("p (a b) -> p a b", a=2)
                                mm(pav[:, h // 2, :], KP[d][hp:hp + 64, h // 2, tsl], QP[d][hp:hp + 64, h // 2, tsl],
                                   ['hgKP%d' % d, 'hgQP%d' % d], [pak])
                        yield
                        for d in range(2):
                            for par in range(2):
                                pa, pak = pas[(d, par)]
                                pav = pa[:, 0:256].rearrange("p (a b) -> p a b", a=2)
                                tt('dve', AT[i2][d][:, par::2, :], pav, maskb[:, d, :].unsqueeze(1).broadcast_to([128, 2, 128]), ALU.mult,
                                   [pak, 'maskb'], ['hgAT%d_%d' % (i2, d)])
                        yield
                        pos = [bank() for _ in range(2)]
                        povs = [pos[par][0][:, 0:256].rearrange("p (a b) -> p a b", a=2) for par in range(2)]
                        for h in range(4):
                            hp = (h % 2) * 64
                            pok = pos[h % 2][1]
                            reg = povs[h % 2][hp:hp + 64, h // 2, :]
                            first = True
                            for d in range(2):
                                mm(reg, VT[:, t, h * 64:(h + 1) * 64], AT[i2][d][:, h, :], ['hgVT', 'hgAT%d_%d' % (i2, d)], [pok],
                                   start=first, stop=False)
                                first = False
                                for cc in range(4):
                                    c = t * 4 + cc
                                    mm(reg[:, cc * 32:(cc + 1) * 32], Sall[d][hp:hp + 64, h // 2, c, :],
                                       QP[d][hp:hp + 64, h // 2, t * 128 + cc * 32:t * 128 + (cc + 1) * 32],
                                       ['hgSall%d_%d' % (d, t), 'hgQP%d' % d], [pok], start=False, stop=(d == 1 and cc == 3))
                        yield
                        obk = 'hgob%d' % i2
                        cp('act', ob[i2][0:64], povs[0][0:64], [pos[0][1]], [obk])
                        cp('dve', ob[i2][64:128], povs[1][64:128], [pos[1][1]], [obk])
                        yield
                        pov = ob[i2][:]
                        pok = obk
                        if ('yb%d' % l) in debug:
                            cp('pool', dbgbuf[:, :, tsl], pov, [pok], ['dbgbuf'])
                        act(sq[i2][:], pov, AF.Square, [pok], ['hgsq%d' % i2])
                        yield
                        pss_, psk = bank()
                        psv = pss_[:, 0:256].rearrange("p (a b) -> p a b", a=2)
                        for j in range(2):
                            mm(psv[:, j, :], bonesb, sq[i2][:, j, :], ['cstb', 'hgsq%d' % i2], [psk])
                        yield
                        act(rr[i2][:], psv, AF.Sqrt, [psk], ['hgrr%d' % i2], bias=RMS_EPS, scale=1.0 / 64)
                        yield
                        S.op('dve', lambda e: e.reciprocal(out=rr[i2][:], in_=rr[i2][:]), reads=['hgrr%d' % i2], writes=['hgrr%d' % i2])
                        yield
                        tt('dve', rr[i2][:], pov, rr[i2][:], ALU.mult, [pok, 'hgrr%d' % i2], ['hgrr%d' % i2])
                        yield
                        for j in range(2):
                            stt(Y[:, 1, j, tsl], rr[i2][:, j, :], pv('hgnw', j), zs[:, j, tsl], ALU.mult, ALU.mult,
                                ['hgrr%d' % i2, 'pvt', 'hgzs'], ['Y1'])

                    run_pipelined((hgout(t) for t in range(NTL) if not (last and t < 2 and not debug)), 5)
                    if ('yb%d' % l) in debug:
                        dbg_dump('yb%d' % l, dbgbuf[:], [128, 2, NT], ['dbgbuf'])
                    S.barrier()
                S.barrier()
        PHASES['hg'] = phase_hg
        def phase_ret(l, h_src, last):
            with contextlib.ExitStack() as st:
                QR = sb(st, "rtQR", [128, 2, NT], BF16)
                KR = sb(st, "rtKR", [128, 2, NT], BF16)
                VT = sb(st, "rtVT", [128, NTL, 256], BF16)
                zs = sb(st, "rtzs", [128, 2, NT], BF16)
                Sall = [sb(st, "rtSall%d" % d, [128, 2, NTL, 64], BF16) for d in range(2)]
                LG = sb(st, "rtLG", [128, 4], F32)
                GL = sb(st, "rtGL", [128, 4], F32)
                LGH = sb(st, "rtLGH", [128, 8], F32)
                QDEC = sb(st, "rtQDEC", [128, 2, 2, 128], F32)
                KDEC = sb(st, "rtKDEC", [128, 2, 4], F32)
                DS = sb(st, "rtDS", [128, 4, 128], F32)
                tb8 = sb(st, "rtb8", [128, 2], F32)
                K_ = 'rttab'
                act(LG[:], pv('rdec'), AF.Exp, ['pvt'], [K_])
                ts('dve', LG[:], LG[:], -1.0, None, ALU.mult, None, [K_], [K_])
                act(GL[:], LG[:], AF.Exp, [K_], [K_], scale=128.0)
                act(LGH[:], pv('rdech'), AF.Exp, ['pvt'], [K_])
                ts('dve', LGH[:], LGH[:], -1.0, None, ALU.mult, None, [K_], [K_])
                for d in range(2):
                    for j in range(2):
                        act(QDEC[:, d, j, :], cstf[:, 5 + d, :], AF.Exp, ['cstf', K_], [K_], scale=LG[:, d * 2 + j:d * 2 + j + 1])
                    act(KDEC[:, d, :], LGH[:, d * 4:(d + 1) * 4], AF.Exp, ['cstf', K_], [K_], scale=cstf[:, 4, 68 + d:69 + d])
                with contextlib.ExitStack() as st2:
                    ta = sb(st2, "rtta", [128, 128], F32)
                    tb = sb(st2, "rttb", [128, 128], F32)
                    for h in range(4):
                        act(ta[:], cstf[:, 0, :], AF.Exp, ['cstf', K_], ['rtta'], scale=LGH[:, h:h + 1])
                        tt('dve', ta[:], ta[:], cstf[:, 2, :], ALU.mult, ['rtta', 'cstf'], ['rtta'])
                        act(tb[:], cstf[:, 1, :], AF.Exp, ['cstf', K_], ['rttb'], scale=LGH[:, 4 + h:5 + h])
                        tt('dve', tb[:], tb[:], cstf[:, 3, :], ALU.mult, ['rttb', 'cstf'], ['rttb'])
                        tt('dve', DS[:, h, :], ta[:], tb[:], ALU.add, ['rtta', 'rttb'], [K_])
                    ts('dve', tb8[:], pv('bin', 16, 2), 0.125, None, ALU.mult, None, ['pvt'], [K_])
                    S.barrier()
                if stop == 'ret_tab':
                    return
                with contextlib.ExitStack() as st2:
                    wr = sb(st2, "rtw", [128, 8, 1024], BF16)
                    for pc_ in range(4):
                        S.dma('pool', wr[:, :, pc_ * 256:(pc_ + 1) * 256], dr['w_in'][l][:, :, 1792 + pc_ * 256:1792 + (pc_ + 1) * 256], writes=['rtw%d' % pc_])
                    brow = sb(st2, "rtbrow", [128, 256], F32)
                    S.dma('sp', brow[:], dr['rows'][l][:, 4352:4608], writes=['rtbrow'])
                    COS = sb(st2, "rtcos", [128, 2048], F32)
                    SIN = sb(st2, "rtsin", [128, 2048], F32)
                    permf = sb(st2, "rtperm", [128, 128], F32)
                    S.dma('sp', COS[:], dr['rcos'], writes=['rtcos'])
                    S.dma('act', SIN[:], dr['rsin'], writes=['rtsin'])
                    S.dma('sp', permf[:], dr['cst'][:, 0, :], writes=['rtperm'])
                    qf = [sb(st2, "rtqf%d" % i, [128, 512], F32) for i in range(2)]
                    t1 = [sb(st2, "rtt1_%d" % i, [128, 512], F32) for i in range(2)]
                    pp = [ps(st2, "rtpp%d" % i, [128, 512], F32) for i in range(2)]
                    pq = [ps(st2, "rtpq%d" % i, [128, 512], F32) for i in range(2)]
                    pt = [ps(st2, "rtpt%d" % i, [128, 512], F32) for i in range(2)]
                    def rtproj(cnt, rc, m, n0, nn):
                        ukeys = uTk[n0 // 128:(n0 + nn) // 128]
                        p_, pk_ = pp[cnt % 2], 'rtpp%d' % (cnt % 2)
                        for jj in range(8):
                            mm(p_[:, 0:nn], wr[:, jj, m * 128:(m + 1) * 128], uT[:, jj, n0:n0 + nn], ['rtw%d' % (m // 2)] + ukeys, [pk_],
                               start=(jj == 0), stop=(jj == 7))
                        yield
                        if m >= 6:
                            act(zs[:, m - 6, n0:n0 + nn], p_[:, 0:nn], AF.Silu, [pk_, 'pvt'], ['rtzs'], bias=pv('bin', 14 + m))
                            return
                        isk = m >= 2
                        j = m % 2
                        dst = (KR if isk else QR)[:, j, n0:n0 + nn]
                        dk = 'rtKR' if isk else 'rtQR'
                        if n0 < 256:
                            if isk:
                                act(dst, p_[:, 0:nn], AF.Identity, [pk_, K_], [dk], bias=tb8[:, j:j + 1], scale=0.125)
                            else:
                                act(dst, p_[:, 0:nn], AF.Identity, [pk_, 'pvt'], [dk], bias=pv('bin', 14 + m))
                            return
                        q_, qk_ = qf[rc % 2], 'rtqf%d' % (rc % 2)
                        a_, ak_ = t1[rc % 2], 'rtt1_%d' % (rc % 2)
                        r_, rk_ = pq[rc % 2], 'rtpq%d' % (rc % 2)
                        if isk:
                            act(q_[:, 0:nn], p_[:, 0:nn], AF.Identity, [pk_, K_], [qk_], bias=tb8[:, j:j + 1], scale=0.125)
                        else:
                            act(q_[:, 0:nn], p_[:, 0:nn], AF.Identity, [pk_, 'pvt'], [qk_], bias=pv('bin', 14 + m))
                        yield
                        mm(r_[:, 0:nn], permf[:], q_[:, 0:nn], ['rtperm', qk_], [rk_])
                        yield
                        tsl = slice(n0 - 256, n0 - 256 + nn)
                        tt('dve', a_[:, 0:nn], r_[:, 0:nn], SIN[:, tsl], ALU.mult, [rk_, 'rtsin'], [ak_])
                        tt('pool', q_[:, 0:nn], q_[:, 0:nn], COS[:, tsl], ALU.mult, [qk_, 'rtcos'], [qk_])
                        yield
                        tt('dve', dst, a_[:, 0:nn], q_[:, 0:nn], ALU.add, [ak_, qk_], [dk])

                    plist = []
                    cnt = 0
                    rc = 0
                    for (n0, nn) in BLOCKS:
                        for m in (0, 1, 2, 3, 6, 7):
                            plist.append((cnt, rc, m, n0, nn))
                            cnt += 1
                            if m < 6 and n0 >= 256:
                                rc += 1
                    run_pipelined((rtproj(*p) for p in plist), 2)
                    for t in range(NTL):
                        p_, pk_ = pt[t % 2], 'rtpt%d' % (t % 2)
                        for jj in range(8):
                            mm(p_[:, 0:256], uT[:, jj, t * 128:(t + 1) * 128], wr[:, jj, 512:768], ['rtw2', uTk[t]], [pk_],
                               start=(jj == 0), stop=(jj == 7))
                        tt('dve', VT[:, t, :], p_[:, 0:256], brow[:], ALU.add, [pk_, 'rtbrow'], ['rtVT'])
                    S.barrier()
                if stop == 'ret_proj':
                    return
                with contextlib.ExitStack() as st2:
                    Sst = [sb(st2, "rtS%d" % d, [128, 2, 64], F32) for d in range(2)]
                    kT = [sb(st2, "rtkT%d" % i, [128, 256], BF16) for i in range(3)]
                    ptr = [ps(st2, "rtptr%d" % i, [128, 8, 128], BF16) for i in range(2)]
                    pU = [ps(st2, "rtpU%d" % i, [128, 512], F32) for i in range(3)]
                    orders = [list(range(NTL)), [1, 0] + list(range(NTL - 1, 1, -1))]
                    for d in range(2):
                        memset('pool', Sst[d][:], 0.0, ['rtS%d' % d])
                    def rtchain(it, step, d):
                        t = orders[d][step]
                        pr, prk = ptr[it % 2], 'rtptr%d' % (it % 2)
                        kt, ktk = kT[it % 3], 'rtkT%d' % (it % 3)
                        pu, puk = pU[it % 3], 'rtpU%d' % (it % 3)
                        puv = pu[:, 0:128].rearrange("p (a b) -> p a b", a=2)
                        for j in range(2):
                            tr(pr[:, j, :], KR[:, j, t * 128:(t + 1) * 128], identb, ['rtKR', 'cstb'], [prk])
                        yield
                        tt('dve', kt[:].rearrange("p (h k) -> p h k", h=4), pr[:, 0:2, :].rearrange("p a (b k) -> p (a b) k", b=2),
                           KDEC[:, d, :].unsqueeze(2).broadcast_to([128, 4, 64]), ALU.mult, [prk, K_], [ktk])
                        yield
                        for h in range(4):
                            hp = (h % 2) * 64
                            mm(puv[hp:hp + 64, h // 2, :], kt[:, h * 64:(h + 1) * 64], VT[:, t, h * 64:(h + 1) * 64], [ktk, 'rtVT'], [puk])
                        yield
                        cp('act', Sall[d][:, :, t, :], Sst[d][:], ['rtS%d' % d], ['rtSall%d_%d' % (d, t)])
                        for j in range(2):
                            stt(Sst[d][:, j, :], Sst[d][:, j, :], GL[:, d * 2 + j:d * 2 + j + 1], puv[:, j, :], ALU.mult, ALU.add,
                                ['rtS%d' % d, K_, puk], ['rtS%d' % d])

                    run_pipelined((rtchain(i_, sd[0], sd[1]) for i_, sd in enumerate([(s_, d_) for s_ in range(NTL) for d_ in range(2)])), 2)
                    S.barrier()
                if stop == 'ret_chain':
                    return
                with contextlib.ExitStack() as st2:
                    if ('yc%d' % l) in debug:
                        dbgbuf = sb(st2, "dbgbuf", [128, 2, NT], F32)
                    AT = [sb(st2, "rtAT%d" % i, [128, 4, 128], BF16) for i in range(2)]
                    qd = [[sb(st2, "rtqd%d_%d" % (i, d), [128, 2, 128], BF16) for d in range(2)] for i in range(2)]
                    sq = [sb(st2, "rtsq%d" % i, [128, 2, 128], BF16) for i in range(2)]
                    rr = [sb(st2, "rtrr%d" % i, [128, 2, 128], F32) for i in range(2)]
                    ob = [sb(st2, "rtob%d" % i, [128, 2, 128], F32) for i in range(2)]
                    bank = mkbanks(st2, 8, "rtbk")

                    def rtout(t):
                        i2 = t % 2
                        tsl = slice(t * 128, (t + 1) * 128)
                        pas = [bank() for _ in range(2)]
                        for h in range(4):
                            hp = (h % 2) * 64
                            pav = pas[h % 2][0][:, 0:256].rearrange("p (a b) -> p a b", a=2)
                            mm(pav[:, h // 2, :], KR[hp:hp + 64, h // 2, tsl], QR[hp:hp + 64, h // 2, tsl], ['rtKR', 'rtQR'], [pas[h % 2][1]])
                        for d in range(2):
                            tt('pool', qd[i2][d][:], QR[:, :, tsl], QDEC[:, d, :, :], ALU.mult, ['rtQR', K_], ['rtqd%d_%d' % (i2, d)])
                        yield
                        for par in range(2):
                            pav = pas[par][0][:, 0:256].rearrange("p (a b) -> p a b", a=2)
                            tt('dve', AT[i2][:, par::2, :], pav, DS[:, par::2, :], ALU.mult, [pas[par][1], K_], ['rtAT%d' % i2])
                        yield
                        pos = [bank() for _ in range(2)]
                        povs = [pos[par][0][:, 0:256].rearrange("p (a b) -> p a b", a=2) for par in range(2)]
                        for h in range(4):
                            hp = (h % 2) * 64
                            pok = pos[h % 2][1]
                            reg = povs[h % 2][hp:hp + 64, h // 2, :]
                            mm(reg, VT[:, t, h * 64:(h + 1) * 64], AT[i2][:, h, :], ['rtVT', 'rtAT%d' % i2], [pok], start=True, stop=False)
                            for d in range(2):
                                mm(reg, Sall[d][hp:hp + 64, h // 2, t, :], qd[i2][d][hp:hp + 64, h // 2, :],
                                   ['rtSall%d_%d' % (d, t), 'rtqd%d_%d' % (i2, d)], [pok], start=False, stop=(d == 1))
                        yield
                        obk = 'rtob%d' % i2
                        cp('act', ob[i2][0:64], povs[0][0:64], [pos[0][1]], [obk])
                        cp('dve', ob[i2][64:128], povs[1][64:128], [pos[1][1]], [obk])
                        yield
                        pov = ob[i2][:]
                        pok = obk
                        if ('yc%d' % l) in debug:
                            cp('pool', dbgbuf[:, :, tsl], pov, [pok], ['dbgbuf'])
                        act(sq[i2][:], pov, AF.Square, [pok], ['rtsq%d' % i2])
                        yield
                        pss_, psk = bank()
                        psv = pss_[:, 0:256].rearrange("p (a b) -> p a b", a=2)
                        for j in range(2):
                            mm(psv[:, j, :], bonesb, sq[i2][:, j, :], ['cstb', 'rtsq%d' % i2], [psk])
                        yield
                        act(rr[i2][:], psv, AF.Sqrt, [psk], ['rtrr%d' % i2], bias=RMS_EPS, scale=1.0 / 64)
                        yield
                        S.op('dve', lambda e: e.reciprocal(out=rr[i2][:], in_=rr[i2][:]), reads=['rtrr%d' % i2], writes=['rtrr%d' % i2])
                        yield
                        tt('dve', rr[i2][:], pov, rr[i2][:], ALU.mult, [pok, 'rtrr%d' % i2], ['rtrr%d' % i2])
                        yield
                        tt('pool', Y[:, 2, :, tsl], rr[i2][:], zs[:, :, tsl], ALU.mult, ['rtrr%d' % i2, 'rtzs'], ['Y2'])

                    run_pipelined((rtout(t) for t in range(NTL) if not (last and t < 2 and not debug)), 5)
                    if ('yc%d' % l) in debug:
                        dbg_dump('yc%d' % l, dbgbuf[:], [128, 2, NT], ['dbgbuf'])
                    S.barrier()
                S.barrier()
        PHASES['ret'] = phase_ret
        def phase_rw(l, h_src, last):
            with contextlib.ExitStack() as st:
                RB = sb(st, "rwRB", [128, 2, NT], BF16)
                KB = sb(st, "rwKB", [128, 2, NT], BF16)
                VB = sb(st, "rwVB", [128, 2, NT], BF16)
                LB = sb(st, "rwLB", [128, NT], BF16)
                zs = sb(st, "rwzs", [128, 2, NT], BF16)
                vT = sb(st, "rwvT", [128, NTL, 256], BF16)
                lw2b = sb(st, "rwlw2", [128, 2, 256], BF16)
                S.dma('pool', lw2b[:], dr['lw2'][l], writes=['rwlw2'])
                oka = sb(st, "rwoka", [128, 2], F32)
                ts('dve', oka[:], pv('ka'), -1.0, 1.0, ALU.mult, ALU.add, ['pvt'], ['rwoka'])
                seen_b, seen_o = set(), set()
                with contextlib.ExitStack() as st2:
                    ww = sb(st2, "rww", [128, 8, 1152], BF16)
                    for pc_ in range(9):
                        S.dma('pool', ww[:, :, pc_ * 128:(pc_ + 1) * 128], dr['w_in'][l][:, :, 2816 + pc_ * 128:2816 + (pc_ + 1) * 128], writes=['rww%d' % pc_])
                    XR = sb(st2, "rwXR", [128, NT + 4], F32)
                    XS = sb(st2, "rwXS", [128, NT], F32)
                    c0 = sb(st2, "rwc0", [128, 7], F32)
                    pp = [ps(st2, "rwpp%d" % i, [128, 512], F32) for i in range(3)]
                    ptr = [ps(st2, "rwptr%d" % i, [128, 8, 128], BF16) for i in range(2)]
                    o_mu, _ = PV['mu']
                    mu0, mu1 = pvt[:, o_mu:o_mu + 7], pvt[:, o_mu + 7:o_mu + 14]
                    tt('dve', c0[:], mu0, mu1, ALU.add, ['pvt'], ['rwc0'])
                    ts('dve', c0[:], c0[:], -1.0, 1.0, ALU.mult, ALU.add, ['rwc0'], ['rwc0'])
                    memset('pool', XR[:], 0.0, ['rwXR'])
                    cnt = 0
                    for m in range(9):
                        for (n0, nn) in BLOCKS:
                            p_, pk_ = pp[cnt % 3], 'rwpp%d' % (cnt % 3)
                            cnt += 1
                            for jj in range(8):
                                mm(p_[:, 0:nn], ww[:, jj, m * 128:(m + 1) * 128], uT[:, jj, n0:n0 + nn],
                                   ['rww%d' % m] + uTk[n0 // 128:(n0 + nn) // 128], [pk_], start=(jj == 0), stop=(jj == 7))
                            if m >= 7:
                                act(zs[:, m - 7, n0:n0 + nn], p_[:, 0:nn], AF.Silu, [pk_, 'pvt'], ['rwzs'], bias=pv('bin', 22 + m))
                            else:
                                xo = 1 if n0 < 256 else 3
                                act(XR[:, n0 + xo:n0 + xo + nn], p_[:, 0:nn], AF.Identity, [pk_, 'pvt'], ['rwXR'], bias=pv('bin', 22 + m))
                        if m >= 7:
                            continue
                        for (b0, ln, o0) in ((1, 256, 0), (259, 2048, 256)):
                            ts('dve', XS[:, o0:o0 + ln], XR[:, b0:b0 + ln], c0[:, m:m + 1], None, ALU.mult, None, ['rwXR', 'rwc0'], ['rwXS'])
                            stt(XS[:, o0:o0 + ln], XR[:, b0 - 1:b0 - 1 + ln], mu0[:, m:m + 1], XS[:, o0:o0 + ln], ALU.mult, ALU.add,
                                ['rwXR', 'pvt', 'rwXS'], ['rwXS'])
                            stt(XS[:, o0:o0 + ln], XR[:, b0 + 1:b0 + 1 + ln], mu1[:, m:m + 1], XS[:, o0:o0 + ln], ALU.mult, ALU.add,
                                ['rwXR', 'pvt', 'rwXS'], ['rwXS'])
                        if m < 6:
                            dstT, dk = [(RB, 'rwRB'), (KB, 'rwKB'), (VB, 'rwVB')][m // 2]
                            cp('act', dstT[:, m % 2, :], XS[:], ['rwXS'], [dk])
                        else:
                            act(LB[0:64, :], XS[0:64, :], AF.Tanh, ['rwXS'], ['rwLB'])
                            cp('pool', LB[64:128, :], XS[64:128, :], ['rwXS'], ['rwLB'])
                    for t in range(NTL):
                        pr, prk = ptr[t % 2], 'rwptr%d' % (t % 2)
                        for j in range(2):
                            tr(pr[:, j, :], VB[:, j, t * 128:(t + 1) * 128], identb, ['rwVB', 'cstb'], [prk])
                        cp('dve' if t % 2 == 0 else 'act', vT[:, t, :], pr[:, 0:2, :].rearrange("p a b -> p (a b)"), [prk], ['rwvT'])
                    S.barrier()
                if stop == 'rw_proj':
                    return
                OS = sb(st, "rwOS", [128, 2, NT], F32)
                with contextlib.ExitStack() as st2:
                    def B(name, shape, dt=BF16):
                        return sb(st2, "rw_" + name, shape, dt), "rw_" + name
                    R64, R64k = B("R64", [128, 256], F32)
                    memset('pool', R64[:], 1.0, [R64k])
                    memset('pool', R64[:, 0:256:64], 0.0, [R64k])
                    LW, LWk = B("LW", [128, 2, 128], F32)
                    SA, SAk = B("SA", [128, 2, 128], F32)
                    LGm, LGk = B("LG", [128, 2, 128], F32)
                    EG, EGk = B("EG", [128, 2, 128], F32)
                    ENG, ENGk = B("ENG", [128, 2, 128], F32)
                    EGM, EGMk = B("EGM", [128, 2, 128], F32)
                    U0, U0k = B("U0", [128, 2, 128], F32)
                    TA, TAk = B("TA", [128, 2, 128], F32)
                    TB_, TBk = B("TB", [128, 2, 128], F32)
                    SQ, SQk = B("SQ", [128, 2, 128])
                    RKD, RKDk = B("RKD", [128, 2, 128])
                    OBt = (None, None)
                    Zst = [B("Z%d" % d, [128, 2, 64], F32) for d in range(2)]
                    BUF = [dict() for _ in range(2)]
                    for d_ in range(2):
                        BUF[d_]['KKN'] = B("KKN_%d" % d_, [128, 2, 128])
                        BUF[d_]['KT'] = B("KT_%d" % d_, [128, 3, 2, 128])
                        BUF[d_]['RT'] = B("RT_%d" % d_, [128, 2, 128])
                        for j_ in range(2):
                            sfx = "_%d_%d" % (d_, j_)
                            SB = dict()
                            SB['TM'] = B("TM" + sfx, [128, 3, 128])
                            for nm_ in ('A1T', 'A2T', 'A3T', 'A4T', 'ALT', 'Tm', 'TTm', 'Xb', 'RHS', 'BYb'):
                                SB[nm_] = B(nm_ + sfx, [128, 2, 128])
                            SB['NY'] = B("NY" + sfx, [128, 2, 64])
                            SB['RH'] = B("RH" + sfx, [128, 128])
                            SB['GTb'] = B("GTb" + sfx, [128, 2, 128])
                            SB['ZLG'] = B("ZLG" + sfx, [128, 2, 64], F32)
                            SB['Z0b'] = B("Z0b" + sfx, [128, 2, 64])
                            BUF[d_][j_] = SB
                        BUF[d_]['GLt'] = B("GLt_%d" % d_, [128, 2, 2], F32)
                    banks = [ps(st2, "rwbank%d" % i, [128, 512], F32) for i in range(8)]
                    bcnt = [0]

                    def bank():
                        i = bcnt[0] % 8
                        bcnt[0] += 1
                        return banks[i], 'rwbank%d' % i
                    for d in range(2):
                        memset('pool', Zst[d][0][:], 0.0, [Zst[d][1], 'rw_Zs_%d_0' % d, 'rw_Zs_%d_1' % d])
                    for d_ in range(2):
                        for j_ in range(2):
                            memset('pool', BUF[d_][j_]['GTb'][0][:], 0.0, [BUF[d_][j_]['GTb'][1]])
                    orders = [list(range(NTL)), [1, 0] + list(range(NTL - 1, 1, -1))]
                    bc3 = lambda ap: ap.unsqueeze(2).broadcast_to([128, 2, 128])
                    def unit(d, t):
                        KKN, KKNk = BUF[d]['KKN']
                        KT, KTk = BUF[d]['KT']
                        RTb, RTk = BUF[d]['RT']
                        GLt, GLk = BUF[d]['GLt']
                        tsl = slice(t * 128, (t + 1) * 128)
                        rev = (d == 1)
                        Z, Zk = Zst[d]
                        plw, plwk = bank()
                        pla, plak = bank()
                        plwv = plw[:, 0:256].rearrange("p (j t) -> p j t", j=2)
                        plav = pla[:, 0:256].rearrange("p (j t) -> p j t", j=2)
                        wb_ = 32 * d
                        for j in range(2):
                            mm(plwv[:, j, :], lw2b[wb_:wb_ + 16, d, j * 128:(j + 1) * 128], LB[wb_:wb_ + 16, tsl], ['rwlw2', 'rwLB'], [plwk])
                        for j in range(2):
                            mm(plav[:, j, :], lw2b[64:96, d, j * 128:(j + 1) * 128], LB[64:96, tsl], ['rwlw2', 'rwLB'], [plak])
                        for j in range(2):
                            act(LW[:, j, :], plwv[:, j, :], AF.Sigmoid, [plwk, 'pvt'], [LWk], bias=pv('w0', d * 2 + j))
                            act(SA[:, j, :], plav[:, j, :], AF.Sigmoid, [plak, 'pvt'], [SAk], bias=pv('a0', d * 2 + j))
                        ts('dve', LW[:], LW[:], -0.6065306597126334, None, ALU.mult, None, [LWk], [LWk])
                        lwf = LW[:].rearrange("p a b -> p (a b)")
                        lgf = LGm[:].rearrange("p a b -> p (a b)")
                        if not rev:
                            S.op('dve', lambda e: e.tensor_tensor_scan(out=lgf, data0=R64[:], data1=lwf, initial=0.0, op0=ALU.mult, op1=ALU.add),
                                 reads=[LWk, R64k], writes=[LGk])
                        else:
                            S.op('dve', lambda e: e.tensor_tensor_scan(out=lgf[:, ::-1], data0=R64[:], data1=lwf[:, ::-1], initial=0.0,
                                                                       op0=ALU.mult, op1=ALU.add), reads=[LWk, R64k], writes=[LGk])
                        act(EG[:], LGm[:], AF.Exp, [LGk], [EGk])
                        act(ENG[:], LGm[:], AF.Exp, [LGk], [ENGk], scale=-1.0)
                        tt('pool', TA[:], LGm[:], LW[:], ALU.subtract, [LGk, LWk], [TAk])
                        act(EGM[:], TA[:], AF.Exp, [TAk], [EGMk])
                        gsrc = EG[:, :, 63::64] if not rev else EG[:, :, 0::64]
                        cp('pool', GLt[:], gsrc, [EGk], [GLk])
                        if stop == 'rw_u1':
                            return
                        tt('dve', TA[:], KB[:, :, tsl], bc3(pv('kk')), ALU.mult, ['rwKB', 'pvt', TAk], [TAk])
                        act(SQ[:], TA[:], AF.Square, [TAk], [SQk])
                        pss_, pssk = bank()
                        pssv = pss_[:, 0:256].rearrange("p (a b) -> p a b", a=2)
                        for j in range(2):
                            mm(pssv[:, j, :], bonesb, SQ[:, j, :], ['cstb', SQk], [pssk])
                        act(TB_[:], pssv, AF.Sqrt, [pssk], [TBk])
                        ts('dve', TB_[:], TB_[:], 1e-12, None, ALU.max, None, [TBk], [TBk])
                        S.op('dve', lambda e: e.reciprocal(out=TB_[:], in_=TB_[:]), reads=[TBk], writes=[TBk])
                        tt('dve', KKN[:], TA[:], TB_[:], ALU.mult, [TAk, TBk], [KKNk])
                        if stop == 'rw_u2':
                            return
                        tt('pool', KT[:, 0], KKN[:], EGM[:], ALU.mult, [KKNk, EGMk], [KTk])
                        tt('dve', TA[:], SA[:], ENG[:], ALU.mult, [SAk, ENGk, TAk], [TAk])
                        tt('pool', KT[:, 1], KKN[:], TA[:], ALU.mult, [KKNk, TAk], [KTk])
                        tt('dve', U0[:], SA[:], bc3(pv('ka')), ALU.mult, [SAk, 'pvt'], [U0k])
                        tt('dve', U0[:], U0[:], bc3(oka[:]), ALU.add, [U0k, 'rwoka'], [U0k])
                        tt('pool', TB_[:], U0[:], ENG[:], ALU.mult, [U0k, ENGk, TBk], [TBk])
                        tt('pool', KT[:, 2], KB[:, :, tsl], TB_[:], ALU.mult, ['rwKB', TBk], [KTk])
                        tt('dve', RTb[:], RB[:, :, tsl], EG[:], ALU.mult, ['rwRB', EGk], [RTk])
                        tt('dve', U0[:], U0[:], KB[:, :, tsl], ALU.mult, [U0k, 'rwKB'], [U0k])
                        tt('dve', U0[:], U0[:], bc3(pv('rk')), ALU.mult, [U0k, 'pvt'], [U0k])
                        tt('pool', RKD[:], U0[:], RB[:, :, tsl], ALU.mult, [U0k, 'rwRB'], [RKDk])
                        pbn, pbnk = bank()
                        pbnv = pbn[:, 0:256].rearrange("p (a b) -> p a b", a=2)
                        for j in range(2):
                            mm(pbnv[:, j, :], bonesb, RKD[:, j, :], ['cstb', RKDk], [pbnk])
                        if t not in seen_b:
                            seen_b.add(t)
                            tt('dve', Y[:, 3, :, tsl], pbnv, VB[:, :, tsl], ALU.mult, [pbnk, 'rwVB'], ['Y3'])
                        else:
                            tt('dve', TA[:], pbnv, VB[:, :, tsl], ALU.mult, [pbnk, 'rwVB', TAk], [TAk])
                            tt('pool', Y[:, 3, :, tsl], Y[:, 3, :, tsl], TA[:], ALU.add, ['Y3', TAk], ['Y3'])
                        if stop == 'rw_u3':
                            return
                        subs = [stream(d, j, t, rev, tsl, KT, KTk, RTb, RTk, GLt, GLk) for j in range(2)]
                        while subs:
                            for g in list(subs):
                                try:
                                    next(g)
                                except StopIteration:
                                    subs.remove(g)
                                yield

                    def stream(d, j, t, rev, tsl, KT, KTk, RTb, RTk, GLt, GLk):
                        SB = BUF[d][j]
                        TM, TMk = SB['TM']
                        A1T, A1k = SB['A1T']
                        A2T, A2k = SB['A2T']
                        A3T, A3k = SB['A3T']
                        A4T, A4k = SB['A4T']
                        ALT, ALk = SB['ALT']
                        Tm, Tmk = SB['Tm']
                        TTm, TTk = SB['TTm']
                        Xb, Xbk = SB['Xb']
                        RHS, RHSk = SB['RHS']
                        BYb, BYk = SB['BYb']
                        NY, NYk = SB['NY']
                        RH, RHk = SB['RH']
                        GTb, GTk = SB['GTb']
                        ZLG, ZLGk = SB['ZLG']
                        Z0b, Z0k = SB['Z0b']
                        Z, _zk = Zst[d]
                        Zk = 'rw_Zs_%d_%d' % (d, j)
                        ptb, ptbk = bank()
                        ptv = ptb[:].bitcast(BF16).rearrange("p (a b) -> p a b", a=8)
                        for x in range(3):
                            tr(ptv[:, x, :], KT[:, x, j, :], identb, [KTk, 'cstb'], [ptbk])
                        yield
                        cp('act', TM[:], ptv[:, 0:3, :], [ptbk], [TMk])
                        yield

                        def amat(dst, dstk, li, ri_src, ri_k, mslot):
                            pas = []
                            for par in range(2):
                                hp = par * 64
                                pa, pak = bank()
                                rhs = (RTb[hp:hp + 64, j, :] if ri_src is None else KT[hp:hp + 64, ri_src, j, :])
                                mm(pa[:, 0:128], KT[hp:hp + 64, li, j, :], rhs, [KTk, ri_k], [pak])
                                pas.append((pa, pak))
                            return pas

                        def aevac(pas, dst, dstk, mslot):
                            for par, (pa, pak) in enumerate(pas):
                                if mslot is None:
                                    cp('act', dst[:, par, :], pa[:, 0:128], [pak], [dstk])
                                else:
                                    tt('dve', dst[:, par, :], pa[:, 0:128], maskb[:, mslot, :], ALU.mult, [pak, 'maskb'], [dstk])
                        for (dst, dstk, li, rs, rk, ms) in ((A1T, A1k, 1, 0, KTk, None), (A2T, A2k, 2, 0, KTk, 2 + d),
                                                            (A3T, A3k, 1, None, RTk, 4 + d), (A4T, A4k, 2, None, RTk, 4 + d)):
                            pas = amat(dst, dstk, li, rs, rk, ms)
                            yield
                            aevac(pas, dst, dstk, ms)
                            yield
                        idb2 = identb.unsqueeze(1).broadcast_to([128, 2, 128])
                        cp('pool', Tm[:], idb2, ['cstb'], [Tmk])
                        cp('pool', TTm[:], idb2, ['cstb'], [TTk])
                        for lv in range(6):
                            tt('pool', ALT[:], A1T[:], maskb[:, 6 + d * 6 + lv, :].unsqueeze(1).broadcast_to([128, 2, 128]), ALU.mult,
                               [A1k, 'maskb'], [ALk])
                            yield
                            px, pxk = bank()
                            pxv = px[:, 0:256].rearrange("p (h t) -> p h t", h=2)
                            for par in range(2):
                                mm(pxv[:, par, :], ALT[:, par, :], Tm[:, par, :], [ALk, Tmk], [pxk])
                            yield
                            cp('act', Xb[:], pxv, [pxk], [Xbk])
                            yield
                            py_, pyk = bank()
                            pyv = py_[:].rearrange("p (x h t) -> p x h t", x=2, h=2)
                            for par in range(2):
                                mm(pyv[:, 0, par, :], Xb[:, par, :], TTm[:, par, :], [Xbk, TTk], [pyk])
                            if lv < 5:
                                for par in range(2):
                                    mm(pyv[:, 1, par, :], TTm[:, par, :], Xb[:, par, :], [Xbk, TTk], [pyk])
                            yield
                            if lv < 5:
                                tt('dve', Tm[:], Tm[:], pyv[:, 1], ALU.subtract, [Tmk, pyk], [Tmk])
                            tt('dve', TTm[:], TTm[:], pyv[:, 0], ALU.subtract, [TTk, pyk], [TTk])
                            yield
                        pw, pwk = bank()
                        pwv = pw[:, 0:128].rearrange("p (h v) -> p h v", h=2)
                        for par in range(2):
                            h = 2 * j + par
                            mm(pwv[:, par, :], A2T[:, par, :], vT[:, t, h * 64:(h + 1) * 64], [A2k, 'rwvT'], [pwk])
                        cp('pool', RHS[:, :, 0:64], TM[:, 0, :].rearrange("p (h k) -> p h k", h=2), [TMk], [RHSk])
                        yield
                        cp('act', RHS[:, :, 64:128], pwv, [pwk], [RHSk])
                        yield
                        pby, pbyk = bank()
                        pbyv = pby[:, 0:256].rearrange("p (h t) -> p h t", h=2)
                        for par in range(2):
                            mm(pbyv[:, par, :], TTm[:, par, :], RHS[:, par, :], [TTk, RHSk], [pbyk])
                        yield
                        cp('act', BYb[:], pbyv, [pbyk], [BYk])
                        yield
                        ts('pool', NY[:], BYb[:, :, 64:128], -1.0, 0.0, ALU.mult, ALU.add, [BYk], [NYk])
                        pr_, prk = bank()
                        for par in range(2):
                            hp = par * 64
                            mm(pr_[hp:hp + 64, 0:128], BYb[:, par, 0:64], A3T[:, par, :], [BYk, A3k], [prk])
                        yield
                        tt('dve', RH[:], RTb[:, j, :], pr_[:, 0:128], ALU.subtract, [RTk, prk], [RHk])
                        yield
                        for c in range(2):
                            cs = slice(c * 64, (c + 1) * 64)
                            pg_, pgk = bank()
                            pgv = pg_[:, 0:128].rearrange("p (x v) -> p x v", x=2)
                            for par in range(2):
                                hp = par * 64
                                h = 2 * j + par
                                hc = slice(h * 64, (h + 1) * 64)
                                pc = slice(par * 64, (par + 1) * 64)
                                mm(pgv[hp:hp + 64, 0, :], BYb[cs, par, 0:64], TM[cs, 1, pc], [BYk, TMk], [pgk])
                                mm(pgv[hp:hp + 64, 1, :], TM[cs, 2, pc], vT[cs, t, hc], [TMk, 'rwvT'], [pgk], start=True, stop=False)
                                mm(pgv[hp:hp + 64, 1, :], TM[cs, 1, pc], NY[cs, par, :], [TMk, NYk], [pgk], start=False, stop=True)
                            yield
                            for par in range(2):
                                hp = par * 64
                                tt('dve', GTb[hp:hp + 64, c, hp:hp + 64], cstf[hp:hp + 64, 4, 0:64], pgv[hp:hp + 64, 0, :], ALU.subtract,
                                   ['cstf', pgk], [GTk])
                            ts('dve', ZLG[:, c, :], pgv[:, 1, :], GLt[:, j, c:c + 1], None, ALU.mult, None, [pgk, GLk], [ZLGk])
                            yield
                        for c in ((0, 1) if not rev else (1, 0)):
                            cp('act', Z0b[:, c, :], Z[:, j, :], [Zk], [Z0k])
                            yield
                            pn, pnk = bank()
                            mm(pn[:, 0:64], GTb[:, c, :], Z0b[:, c, :], [GTk, Z0k], [pnk])
                            yield
                            stt(Z[:, j, :], pn[:, 0:64], GLt[:, j, c:c + 1], ZLG[:, c, :], ALU.mult, ALU.add, [pnk, GLk, ZLGk, Zk], [Zk])
                            yield
                        for par in range(2):
                            hp = par * 64
                            h = 2 * j + par
                            hc = slice(h * 64, (h + 1) * 64)
                            po_, pok = bank()
                            reg = po_[hp:hp + 64, 0:128]
                            mm(reg, vT[:, t, hc], A4T[:, par, :], ['rwvT', A4k], [pok], start=True, stop=False)
                            mm(reg, NY[:, par, :], A3T[:, par, :], [NYk, A3k], [pok], start=False, stop=False)
                            for c in range(2):
                                mm(reg[:, c * 64:(c + 1) * 64], Z0b[hp:hp + 64, c, :], RH[hp:hp + 64, c * 64:(c + 1) * 64],
                                   [Z0k, RHk], [pok], start=False, stop=(c == 1))
                            yield
                            osl = OS[hp:hp + 64, j, tsl]
                            osk = 'rwOS%d_%d' % (t, j)
                            if (t, j, par) not in seen_o:
                                seen_o.add((t, j, par))
                                cp('dve' if par == 0 else 'act', osl, reg, [pok], [osk])
                            else:
                                tt('dve', osl, osl, reg, ALU.add, [pok, osk], [osk])
                            yield

                    for step in range(NTL):
                        if stop is not None and stop.startswith('rw_u') and step >= 1:
                            break
                        gens = [unit(d, orders[d][step]) for d in range(2)]
                        while gens:
                            for g in list(gens):
                                try:
                                    next(g)
                                except StopIteration:
                                    gens.remove(g)
                    S.barrier()
                if stop is not None and stop.startswith('rw_'):
                    return
                with contextlib.ExitStack() as st2:
                    ob = [sb(st2, "rwob%d" % i, [128, 2, 128], BF16) for i in range(2)]
                    cen = [sb(st2, "rwcen%d" % i, [128, 2, 128], F32) for i in range(2)]
                    rs = [sb(st2, "rwrs%d" % i, [128, 2, 128], F32) for i in range(2)]
                    pm_ = [ps(st2, "rwpm%d" % i, [128, 512], F32) for i in range(2)]
                    pv_ = [ps(st2, "rwpv%d" % i, [128, 512], F32) for i in range(2)]
                    for t in range(NTL):
                        i2 = t % 2
                        tsl = slice(t * 128, (t + 1) * 128)
                        osk = 'rwOS%d_0' % t
                        osk1 = 'rwOS%d_1' % t
                        cp('act', ob[i2][:], OS[:, :, tsl], [osk, osk1], ['rwob%d' % i2])
                        pmv = pm_[i2][:, 0:256].rearrange("p (a b) -> p a b", a=2)
                        for j in range(2):
                            mm(pmv[:, j, :], bonesb, ob[i2][:, j, :], ['cstb', 'rwob%d' % i2], ['rwpm%d' % i2])
                        stt(cen[i2][:], pmv, -1.0 / 64, OS[:, :, tsl], ALU.mult, ALU.add, ['rwpm%d' % i2, osk, osk1], ['rwcen%d' % i2])
                        act(ob[i2][:], cen[i2][:], AF.Square, ['rwcen%d' % i2], ['rwob%d' % i2])
                        pvv = pv_[i2][:, 0:256].rearrange("p (a b) -> p a b", a=2)
                        for j in range(2):
                            mm(pvv[:, j, :], bonesb, ob[i2][:, j, :], ['cstb', 'rwob%d' % i2], ['rwpv%d' % i2])
                        act(rs[i2][:], pvv, AF.Sqrt, ['rwpv%d' % i2], ['rwrs%d' % i2], bias=RW_GN_EPS, scale=1.0 / 64)
                        S.op('dve', lambda e: e.reciprocal(out=rs[i2][:], in_=rs[i2][:]), reads=['rwrs%d' % i2], writes=['rwrs%d' % i2])
                        tt('dve', cen[i2][:], cen[i2][:], rs[i2][:], ALU.mult, ['rwcen%d' % i2, 'rwrs%d' % i2], ['rwcen%d' % i2])
                        tt('pool', cen[i2][:], cen[i2][:], bc3(pv('gnw')), ALU.mult, ['rwcen%d' % i2, 'pvt'], ['rwcen%d' % i2])
                        tt('pool', cen[i2][:], cen[i2][:], bc3(pv('gnb')), ALU.add, ['rwcen%d' % i2, 'pvt'], ['rwcen%d' % i2])
                        tt('dve', cen[i2][:], cen[i2][:], Y[:, 3, :, tsl], ALU.add, ['rwcen%d' % i2, 'Y3'], ['rwcen%d' % i2])
                        if ('yd%d' % l) in debug:
                            cp('act', OS[:, :, tsl], cen[i2][:], ['rwcen%d' % i2], [osk, osk1])
                        tt('dve', Y[:, 3, :, tsl], cen[i2][:], zs[:, :, tsl], ALU.mult, ['rwcen%d' % i2, 'rwzs'], ['Y3'])
                    if ('yd%d' % l) in debug:
                        dbg_dump('yd%d' % l, OS[:], [128, 2, NT], ['rwOS%d_%d' % (t, j_) for t in range(NTL) for j_ in range(2)])
                    S.barrier()
                S.barrier()
        PHASES['rw'] = phase_rw
        def phase_merge(l, h_src, last):
            h_dst = out_d if last else h1_d
            with contextlib.ExitStack() as st:
                MG = sb(st, "mgMG", [128, 8, NT], BF16)
                wbr = sb(st, "mgwbr", [128, 4, 2, DM], BF16)
                S.dma('pool', wbr[:], dr['wbr'][l], writes=['mgwbr'])
                with contextlib.ExitStack() as st2:
                    wg = [sb(st2, "mgwg%d" % i, [128, 8, 4, 128], BF16) for i in range(2)]
                    sg = [sb(st2, "mgsg%d" % i, [128, 512], BF16) for i in range(3)]
                    ac = [sb(st2, "mgac%d" % i, [128, 512], F32) for i in range(2)]
                    tm = [sb(st2, "mgtm%d" % i, [128, 512], F32) for i in range(2)]
                    pgl = [ps(st2, "mgpg%d" % i, [128, 512], F32) for i in range(3)]
                    pbr = [ps(st2, "mgpb%d" % i, [128, 512], F32) for i in range(3)]
                    cg = 0
                    ca = 0
                    def load_wg(dt_):
                        for k in range(4):
                            c0 = 3968 + k * 1024 + dt_ * 128
                            S.dma('pool', wg[dt_ % 2][:, :, k, :], dr['w_in'][l][:, :, c0:c0 + 128], writes=['mgwg%d' % (dt_ % 2)])
                    load_wg(0)
                    for dt_ in range(8):
                        w_, wk_ = wg[dt_ % 2], 'mgwg%d' % (dt_ % 2)
                        if dt_ + 1 < 8:
                            load_wg(dt_ + 1)
                        for (n0, nn) in BLOCKS:
                            if last and n0 < 256:
                                continue
                            a_, ak_ = ac[ca % 2], 'mgac%d' % (ca % 2)
                            t_, tk_ = tm[ca % 2], 'mgtm%d' % (ca % 2)
                            ca += 1
                            for k in range(4):
                                pg_, pgk_ = pgl[cg % 3], 'mgpg%d' % (cg % 3)
                                pb_, pbk_ = pbr[cg % 3], 'mgpb%d' % (cg % 3)
                                s_, sk_ = sg[cg % 3], 'mgsg%d' % (cg % 3)
                                cg += 1
                                for jj in range(8):
                                    mm(pg_[:, 0:nn], w_[:, jj, k, :], uT[:, jj, n0:n0 + nn], [wk_] + uTk[n0 // 128:(n0 + nn) // 128], [pgk_],
                                       start=(jj == 0), stop=(jj == 7))
                                act(s_[:, 0:nn], pg_[:, 0:nn], AF.Sigmoid, [pgk_, 'pvt'], [sk_], bias=pv('bin', 31 + k * 8 + dt_))
                                for jc in range(2):
                                    mm(pb_[:, 0:nn], wbr[:, k, jc, dt_ * 128:(dt_ + 1) * 128], Y[:, k, jc, n0:n0 + nn], ['mgwbr', 'Y%d' % k], [pbk_],
                                       start=(jc == 0), stop=(jc == 1))
                                if k == 0:
                                    tt('dve', a_[:, 0:nn], pb_[:, 0:nn], s_[:, 0:nn], ALU.mult, [pbk_, sk_], [ak_])
                                else:
                                    tt('dve', t_[:, 0:nn], pb_[:, 0:nn], s_[:, 0:nn], ALU.mult, [pbk_, sk_], [tk_])
                                    if k < 3:
                                        tt('pool', a_[:, 0:nn], a_[:, 0:nn], t_[:, 0:nn], ALU.add, [ak_, tk_], [ak_])
                                    else:
                                        tt('pool', MG[:, dt_, n0:n0 + nn], a_[:, 0:nn], t_[:, 0:nn], ALU.add, [ak_, tk_], ['mgMG%d' % (n0 // 512 if n0 else 9)])
                    S.barrier()
                if ('merged%d' % l) in debug:
                    with contextlib.ExitStack() as st2:
                        mf = sb(st2, "mgf", [128, 8, NT], F32)
                        cp('dve', mf[:], MG[:], ['mgMG%d' % i for i in (9, 0, 1, 2, 3)], ['mgf'])
                        dbg_dump('merged%d' % l, mf[:], [128, 8, NT], ['mgf'])
                        S.barrier()
                with contextlib.ExitStack() as st2:
                    wo = sb(st2, "mgwo", [128, 8, DM], BF16)
                    S.dma('pool', wo[:], dr['wout'][l], writes=['mgwo'])
                    rows = sb(st2, "mgrows", [128, 3, DM], F32)
                    S.dma('sp', rows[:], dr['rows'][l][:, 0:3072].rearrange("p (a b) -> p a b", a=3), writes=['mgrows'])
                    hin_ = [sb(st2, "mghin%d" % i, [128, DM], F32) for i in range(2)]
                    ot = [sb(st2, "mgot%d" % i, [128, DM], F32) for i in range(2)]
                    stat = [sb(st2, "mgst%d" % i, [128, 16], F32) for i in range(2)]
                    po = [[ps(st2, "mgpo%d_%d" % (i, hh), [128, 512], F32) for hh in range(2)] for i in range(2)]
                    def mgout(it, t):
                        i2 = it % 2
                        ci = 1 if t < 2 else 0
                        tsl = slice(t * 128, (t + 1) * 128)
                        mgk = 'mgMG%d' % (9 if t < 2 else (t - 2) // 4)
                        hk_, ok_, sk_ = 'mghin%d' % i2, 'mgot%d' % i2, 'mgst%d' % i2
                        hi, o_, sti = hin_[i2], ot[i2], stat[i2]
                        S.dma('sp', hi[:], h_src[t * 128:(t + 1) * 128, :], writes=[hk_])
                        for hh in range(2):
                            pk_ = 'mgpo%d_%d' % (i2, hh)
                            for jj in range(8):
                                mm(po[i2][hh][:], MG[:, jj, tsl], wo[:, jj, hh * 512:(hh + 1) * 512], [mgk, 'mgwo'], [pk_], start=(jj == 0), stop=(jj == 7))
                        yield
                        for hh in range(2):
                            pk_ = 'mgpo%d_%d' % (i2, hh)
                            tt('dve', o_[:, hh * 512:(hh + 1) * 512], po[i2][hh][:], rows[:, 0, hh * 512:(hh + 1) * 512], ALU.add, [pk_, 'mgrows'], [ok_])
                        yield
                        tt('dve', o_[:], o_[:], gatebc[:, ci, :], ALU.mult, [ok_, 'gatebc'], [ok_])
                        yield
                        stt(o_[:], hi[:], ALPHA, o_[:], ALU.mult, ALU.add, [hk_, ok_], [ok_])
                        yield
                        S.op('dve', lambda e: e.bn_stats(out=sti[:, 0:6], in_=o_[:, 0:512]), reads=[ok_], writes=[sk_])
                        S.op('dve', lambda e: e.bn_stats(out=sti[:, 6:12], in_=o_[:, 512:1024]), reads=[ok_], writes=[sk_])
                        yield
                        S.op('dve', lambda e: e.bn_aggr(out=sti[:, 12:14], in_=sti[:, 0:12]), reads=[sk_], writes=[sk_])
                        yield
                        act(sti[:, 14:15], sti[:, 13:14], AF.Sqrt, [sk_], [sk_], bias=LN_EPS)
                        yield
                        S.op('dve', lambda e: e.reciprocal(out=sti[:, 14:15], in_=sti[:, 14:15]), reads=[sk_], writes=[sk_])
                        yield
                        stt(sti[:, 15:16], sti[:, 12:13], -1.0, sti[:, 14:15], ALU.mult, ALU.mult, [sk_], [sk_])
                        yield
                        act(o_[:], o_[:], AF.Identity, [ok_, sk_], [ok_], bias=sti[:, 15:16], scale=sti[:, 14:15])
                        yield
                        tt('pool', o_[:, 0:512], o_[:, 0:512], rows[:, 1, 0:512], ALU.mult, [ok_, 'mgrows'], [ok_])
                        tt('dve', o_[:, 512:1024], o_[:, 512:1024], rows[:, 1, 512:1024], ALU.mult, [ok_, 'mgrows'], [ok_])
                        yield
                        tt('pool', o_[:, 0:512], o_[:, 0:512], rows[:, 2, 0:512], ALU.add, [ok_, 'mgrows'], [ok_])
                        tt('dve', o_[:, 512:1024], o_[:, 512:1024], rows[:, 2, 512:1024], ALU.add, [ok_, 'mgrows'], [ok_])
                        yield
                        if last:
                            S.dma('sp', out_d[(t - 2) * 128:(t - 1) * 128, :], o_[:], reads=[ok_], writes=['outfinal'])
                        else:
                            S.dma('sp', h1_d[t * 128:(t + 1) * 128, :], o_[:], reads=[ok_], writes=['h1'])

                    tl = [t for t in range(NTL) if not (last and t < 2)]
                    run_pipelined((mgout(i_, t) for i_, t in enumerate(tl)), 6)
                    S.barrier()
                S.barrier()
        PHASES['merge'] = phase_merge
        for l in range(nlayers):
            last = (l == nlayers - 1)
            h_src = dr['hin'] if l == 0 else h1_d
            S.dma('sp', pvt[:], dr['pv'][l], writes=['pvt'])
            with contextlib.ExitStack() as st:
                adw = [sb(st, "adw%d" % i, [128, 8, 512], F32) for i in range(2)]
                scb = sb(st, "scb", [128, 2, 8, 128], F32)
                grow = sb(st, "grow", [128, DM], F32)
                pm0 = ps(st, "pm0", [128, 16, 2], F32)
                pg = [ps(st, "pg%d" % i, [128, 512], F32) for i in range(2)]
                for i in range(2):
                    cp('dve', scb[:, i], silc[:, :, i:i + 1].broadcast_to([128, 8, 128]), ['silc'], ['scb'])
                S.dma('sp', grow[:], dr['rows'][l][:, 3072:4096], writes=['grow'])
                for ch in range(6):
                    buf = adw[ch % 2]
                    bk = 'adw%d' % (ch % 2)
                    S.dma('sp' if ch % 2 == 0 else 'act', buf[:], dr['ada_w'][l][:, :, ch * 512:(ch + 1) * 512], writes=[bk])
                    if ch < 4:
                        for mloc in range(4):
                            m = ch * 4 + mloc
                            for j in range(8):
                                mm(pm0[:, m, :], buf[:, j, mloc * 128:(mloc + 1) * 128], silc[:, j, :], [bk, 'silc'],
                                   ['pm0'], start=(j == 0), stop=(j == 7))
                    else:
                        for i in range(2):
                            for j in range(8):
                                mm(pg[i][:], scb[:, i, j, :], buf[:, j, :], [bk, 'scb'], ['pg%d' % i],
                                   start=(j == 0), stop=(j == 7))
                            tt('dve', gatebc[:, i, (ch - 4) * 512:(ch - 3) * 512], pg[i][:],
                               grow[:, (ch - 4) * 512:(ch - 3) * 512], ALU.add, ['pg%d' % i, 'grow'], ['gatebc'])
                tt('dve', modfm[:], pm0[:], pv('adab').unsqueeze(2).broadcast_to([128, 16, 2]), ALU.add,
                   ['pm0', 'pvt'], ['modfm'])
                ts('dve', modfm[:, 8:16, :], modfm[:, 8:16, :], 1.0, None, ALU.add, None, ['modfm'], ['modfm'])
                dbg_dump('modfm%d' % l, modfm[:], [128, 16, 2], ['modfm'])
                dbg_dump('gatebc%d' % l, gatebc[:], [128, 2, DM], ['gatebc'])
                S.barrier()
            with contextlib.ExitStack() as st:
                xin = [sb(st, "xin%d" % i, [128, DM], F32) for i in range(3)]
                xn = [sb(st, "xn%d" % i, [128, DM], BF16) for i in range(2)]
                stat = [sb(st, "stat%d" % i, [128, 16], F32) for i in range(3)]
                ptr = [ps(st, "ptr%d" % i, [128, 8, 128], BF16) for i in range(2)]
                def p1tile(t):
                    xi, xk = xin[t % 3], 'xin%d' % (t % 3)
                    sti, sk = stat[t % 3], 'stat%d' % (t % 3)
                    xo, xok = xn[t % 2], 'xn%d' % (t % 2)
                    pt, ptk = ptr[t % 2], 'ptr%d' % (t % 2)
                    ci = 1 if t < 2 else 0
                    S.dma('sp' if t % 2 == 0 else 'act', xi[:], h_src[t * 128:(t + 1) * 128, :], writes=[xk])
                    yield
                    S.op('dve', lambda e: e.bn_stats(out=sti[:, 0:6], in_=xi[:, 0:512]), reads=[xk], writes=[sk])
                    S.op('dve', lambda e: e.bn_stats(out=sti[:, 6:12], in_=xi[:, 512:1024]), reads=[xk], writes=[sk])
                    yield
                    S.op('dve', lambda e: e.bn_aggr(out=sti[:, 12:14], in_=sti[:, 0:12]), reads=[sk], writes=[sk])
                    yield
                    act(sti[:, 14:15], sti[:, 13:14], AF.Sqrt, [sk], [sk], bias=LN_EPS)
                    yield
                    S.op('dve', lambda e: e.reciprocal(out=sti[:, 14:15], in_=sti[:, 14:15]), reads=[sk], writes=[sk])
                    yield
                    stt(sti[:, 15:16], sti[:, 12:13], -1.0, sti[:, 14:15], ALU.mult, ALU.mult, [sk], [sk])
                    yield
                    act(xo[:], xi[:], AF.Identity, [xk, sk], [xok], bias=sti[:, 15:16], scale=sti[:, 14:15])
                    yield
                    for j in range(8):
                        tr(pt[:, j, :], xo[:, j * 128:(j + 1) * 128], identb, [xok, 'cstb'], [ptk])
                    yield
                    for j in range(8):
                        if j % 2 == 0:
                            act(uT[:, j, t * 128:(t + 1) * 128], pt[:, j, :], AF.Identity, [ptk, 'modfm'], ['uT%d' % t],
                                bias=modfm[:, j, ci:ci + 1], scale=modfm[:, 8 + j, ci:ci + 1])
                        else:
                            ts('dve', uT[:, j, t * 128:(t + 1) * 128], pt[:, j, :], modfm[:, 8 + j, ci:ci + 1],
                               modfm[:, j, ci:ci + 1], ALU.mult, ALU.add, [ptk, 'modfm'], ['uT%d' % t])

                run_pipelined((p1tile(t) for t in range(NTL)), 4)
                if ('uT%d' % l) in debug:
                    utf = sb(st, "utf", [128, 8, NT], F32)
                    cp('dve', utf[:], uT[:], ['uT%d' % t for t in range(NTL)], ['utf'])
                    dbg_dump('uT%d' % l, utf[:], [128, 8, NT], ['utf'])
                S.barrier()
            uTk = ['uT%d' % t for t in range(NTL)]

            for ph in list(PHASES):
                if ph in phases:
                    PHASES[ph](l, h_src, last)
            if ('h%d' % l) in debug and not last:
                d_ = dbg_out('h%d' % l, [NT, DM])
                S.dma('sp', d_, h1_d, writes=['dbgout_h%d' % l])
                S.barrier()
            if ('Y%d' % l) in debug:
                with contextlib.ExitStack() as st:
                    yf = sb(st, "yf", [128, 4, 2, NT], F32)
                    cp('dve', yf[:], Y[:], ['Y0', 'Y1', 'Y2', 'Y3'], ['yf'])
                    dbg_dump('Y%d' % l, yf[:], [128, 4, 2, NT], ['yf'])
                    S.barrier()

        S.final_wait('sp', ['outfinal'] + ['dbgout_' + n for n in dbg_d])
    if MEMDBG:
        print('SBUF min remaining by prefix:', minrem)
    return nc, dbg_d


def kernel(**inputs):
    inp = {k: np.asarray(v) for k, v in inputs.items()}
    sh = prep_shared(inp)
    nc, _ = build()
    in_maps = []
    for b in range(8):
        m = dict(sh)
        m.update(prep_core(inp, b))
        in_maps.append(m)
    res = run_bass_kernel_spmd(nc, in_maps, core_ids=list(range(8)))
    return np.stack([np.asarray(res.results[b]['out'], dtype=np.float32) for b in range(8)], 0)
```

```python
import contextlib
import numpy as np
import concourse.bass as bass
import concourse.mybir as mybir
from concourse.bass_utils import run_bass_kernel_spmd

F32 = mybir.dt.float32
BF16 = mybir.dt.bfloat16
AF = mybir.ActivationFunctionType
ALU = mybir.AluOpType
AX = mybir.AxisListType

NT = 2304
NTL = 18
DM = 1024
NCOL = 8064
BLOCKS = [(0, 256), (256, 512), (768, 512), (1280, 512), (1792, 512)]
LN_EPS = 1e-5
RMS_EPS = 1e-6
RW_GN_EPS = 64e-5
ALPHA = (2 * 2) ** 0.25
PI = float(np.pi)
MEMDBG = False
GATE_PRE = False
S5_STAGGER = 6
GJ_SPLIT = [[], [], [], []]
GJ_S5 = list(range(32))


class Sched:
    NDMA = 16

    def __init__(self, nc, same_engine_waits=True):
        self.nc = nc
        self.same = same_engine_waits
        self.eng = dict(pe=nc.tensor, act=nc.scalar, dve=nc.vector, pool=nc.gpsimd, sp=nc.sync)
        self.E = {n: dict(cnt=0, known={}) for n in self.eng}
        self.dq = {'sp': ['dsp%d' % i for i in range(8)], 'act': ['dac%d' % i for i in range(4)],
                   'pool': ['dpl%d' % i for i in range(8)]}
        self.dmas = {n: dict(cnt=0) for q in self.dq.values() for n in q}
        self.dma_rr = {'sp': 0, 'act': 0, 'pool': 0}
        self.lastw = {}
        self.readers = {}
        self.sems = None
        self.nins = 0

    def sem_names(self):
        return list(self.E.keys()) + list(self.dmas.keys())

    def _deps(self, reads, writes):
        deps = {}

        def add(w):
            if w is not None:
                deps[w[0]] = max(deps.get(w[0], 0), w[1])
        for k in reads:
            add(self.lastw.get(k))
        for k in writes:
            add(self.lastw.get(k))
            for r in self.readers.get(k, ()):
                add(r)
        return deps

    def _waits(self, en, deps):
        E = self.E[en]
        waits = []
        for d, v in deps.items():
            if d == en and (en == 'pe' or not self.same):
                continue
            if E['known'].get(d, 0) < v:
                waits.append((d, v))
                E['known'][d] = v
        return waits

    def _record(self, ident, reads, writes):
        for k in writes:
            self.lastw[k] = ident
            self.readers[k] = []
        for k in reads:
            self.readers.setdefault(k, []).append(ident)

    def _emit(self, en, waits, fn, inc):
        eng = self.eng[en]
        for d, v in waits:
            eng.wait_ge(self.sems[d], v)
        if fn is not None:
            fn(eng).then_inc(self.sems[inc[0]], inc[1])
            self.nins += 1

    def op(self, en, fn, reads=(), writes=()):
        E = self.E[en]
        waits = self._waits(en, self._deps(reads, writes))
        E['cnt'] += 1
        self._emit(en, waits, fn, (en, 1))
        self._record((en, E['cnt']), reads, writes)

    def dma(self, en, out, in_, reads=(), writes=(), **kw):
        dn = self.dq[en][self.dma_rr[en]]
        self.dma_rr[en] = (self.dma_rr[en] + 1) % len(self.dq[en])
        Dq = self.dmas[dn]
        deps = self._deps(reads, writes)
        if Dq['cnt'] > 0:
            deps[dn] = max(deps.get(dn, 0), Dq['cnt'])
        waits = self._waits(en, deps)
        Dq['cnt'] += 16
        self._emit(en, waits, (lambda e: e.dma_start(out=out, in_=in_, **kw)), (dn, 16))
        self._record((dn, Dq['cnt']), reads, writes)

    def barrier(self):
        cur = {n: self.E[n]['cnt'] for n in self.E}
        cur.update({n: self.dmas[n]['cnt'] for n in self.dmas})
        for en in self.E:
            waits = self._waits(en, {d: v for d, v in cur.items() if v > 0})
            self._emit(en, waits, None, None)

    def final_wait(self, en, keys):
        self._emit(en, self._waits(en, self._deps(keys, ())), None, None)


PV = {}


def _pv_layout():
    off = 0
    for name, n in [('bin', 63), ('s5d', 2), ('glub', 2), ('hglb', 8), ('hgnw', 2), ('rdec', 4), ('mu', 14),
                    ('w0', 4), ('a0', 4), ('kk', 2), ('ka', 2), ('rk', 2), ('gnw', 2), ('gnb', 2), ('adab', 16),
                    ('lamre', 16), ('lamim', 16), ('ldt', 16), ('rdech', 8)]:
        PV[name] = (off, n)
        off += n
    return off


NPV = _pv_layout()


def _colmap():
    cm = list(range(0, 3584))
    lora = [-1] * 128
    for r in range(16):
        lora[r] = 3584 + r
        lora[32 + r] = 3600 + r
        lora[64 + r] = 3616 + r
        lora[80 + r] = 3632 + r
    cm += lora
    cm += list(range(3648, 3904))
    cm += list(range(3904, 8000))
    return np.array(cm)


CMAP = _colmap()


def _fm(v):
    return np.ascontiguousarray(v.reshape(-1, 128).T)


def _masks():
    t = np.arange(128)
    s_, t_ = t[:, None], t[None, :]
    m = []
    b32 = (s_ // 32) == (t_ // 32)
    b64 = (s_ // 64) == (t_ // 64)
    m.append(b32 & (t_ >= s_))
    m.append(b32 & (t_ <= s_))
    m.append(b64 & (t_ > s_))
    m.append(b64 & (t_ < s_))
    m.append(b64 & (t_ >= s_))
    m.append(b64 & (t_ <= s_))
    for d in range(2):
        for lv in range(6):
            sz = 1 << lv
            blk = (s_ // (2 * sz)) == (t_ // (2 * sz))
            hs, ht = (s_ // sz) % 2, (t_ // sz) % 2
            if d == 0:
                m.append(blk & (ht == 1) & (hs == 0))
            else:
                m.append(blk & (ht == 0) & (hs == 1))
    return np.stack([x.astype(np.float32) for x in m], 1)


def _rot_tables():
    n = 16
    freqs = 10000.0 ** (-np.arange(n, dtype=np.float32) / n)
    tt = np.arange(2048)
    rows = (tt // 64).astype(np.float32)
    cols = (tt % 64).astype(np.float32)
    cos = np.zeros((128, 2048), np.float32)
    sins = np.zeros((128, 2048), np.float32)
    pm = np.zeros((128, 128), np.float32)
    for p in range(128):
        i = p % 64
        pos = rows if i < 32 else cols
        ii = i % 32
        ang = pos * freqs[ii % 16]
        cos[p] = np.cos(ang)
        if ii < 16:
            sins[p] = -np.sin(ang)
            partner = p + 16
        else:
            sins[p] = np.sin(ang)
            partner = p - 16
        pm[partner, p] = 1.0
    return cos, sins, pm


def prep_shared(inp):
    sh = {}
    L = 2
    w_in = inp['w_in']
    wn = np.zeros((L, 1024, NCOL), np.float32)
    valid = CMAP >= 0
    wn[:, :, valid] = w_in[:, :, CMAP[valid]]
    sh['w_in'] = np.ascontiguousarray(wn.reshape(L, 8, 128, NCOL).transpose(0, 2, 1, 3))
    bn = np.zeros((L, NCOL), np.float32)
    bn[:, valid] = inp['b_in'][:, CMAP[valid]]
    sh['ada_w'] = np.ascontiguousarray(inp['ada_w'].reshape(L, 8, 128, 3072).transpose(0, 2, 1, 3))
    pv = np.zeros((L, 128, NPV), np.float32)

    def put(l, name, arr):
        o, n = PV[name]
        assert arr.shape == (128, n), (name, arr.shape)
        pv[l, :, o:o + n] = arr
    for l in range(L):
        put(l, 'bin', _fm(bn[l]))
        put(l, 's5d', _fm(inp['s5_d'][l]))
        put(l, 'glub', _fm(inp['s5_glu_b'][l]))
        put(l, 'hglb', np.concatenate([_fm(inp['hg_lb'][ll, d]) for ll in range(2) for d in range(2)], 1))
        put(l, 'hgnw', _fm(inp['hg_norm_w'][l]))
        rd = np.zeros((128, 4), np.float32)
        for d in range(2):
            for j in range(2):
                rd[:64, d * 2 + j] = inp['ret_decay'][l, d, 2 * j]
                rd[64:, d * 2 + j] = inp['ret_decay'][l, d, 2 * j + 1]
        put(l, 'rdec', rd)
        put(l, 'rdech', np.ascontiguousarray(np.broadcast_to(inp['ret_decay'][l].reshape(1, 8), (128, 8))))
        mu = np.zeros((2, 7 * 128), np.float32)
        mu[:, :768] = inp['rw_mu'][l][:, :768]
        lv = CMAP[3584:3712]
        ok = lv >= 0
        mu[:, 768:896][:, ok] = inp['rw_mu'][l][:, lv[ok] - 2816]
        put(l, 'mu', np.concatenate([_fm(mu[0]), _fm(mu[1])], 1))
        put(l, 'w0', np.concatenate([_fm(inp['rw_w0'][l, d]) for d in range(2)], 1))
        put(l, 'a0', np.concatenate([_fm(inp['rw_a0'][l, d]) for d in range(2)], 1))
        for nm, key in [('kk', 'rw_kk'), ('ka', 'rw_ka'), ('rk', 'rw_rk'), ('gnw', 'rw_gn_w'), ('gnb', 'rw_gn_b')]:
            put(l, nm, _fm(inp[key][l]))
        put(l, 'adab', _fm(inp['ada_b'][l][:2048]))
        for nm, key in [('lamre', 's5_lam_re'), ('lamim', 's5_lam_im')]:
            a = inp[key][l].reshape(2, 8, 2, 64)
            put(l, nm, np.ascontiguousarray(a.transpose(2, 3, 0, 1).reshape(128, 16)))
        a = np.broadcast_to(inp['s5_log_dt'][l].reshape(2, 8, 2, 1), (2, 8, 2, 64))
        put(l, 'ldt', np.ascontiguousarray(a.transpose(2, 3, 0, 1).reshape(128, 16)))
    sh['pv'] = pv
    bt = np.zeros((L, 128, 2, 4, 2, 128), np.float32)
    ct = np.zeros((L, 128, 8, 2, 128), np.float32)
    for l in range(L):
        for g in range(16):
            i, g2 = g // 2, g % 2
            for q in range(16):
                c = g * 16 + q
                j, p = c // 128, c % 128
                bt[l, p, j, i % 4, 0, g2 * 64:(g2 + 1) * 64] = inp['s5_b_re'][l, g, :, q]
                bt[l, p, j, i % 4, 1, g2 * 64:(g2 + 1) * 64] = inp['s5_b_im'][l, g, :, q]
            m0 = (i % 4) * 32 + g2 * 16
            ct[l, g2 * 64:(g2 + 1) * 64, i, 0, m0:m0 + 16] = inp['s5_c_re'][l, g].T
            ct[l, g2 * 64:(g2 + 1) * 64, i, 1, m0:m0 + 16] = inp['s5_c_im'][l, g].T
    sh['s5bt'] = bt
    sh['s5ct'] = ct
    sh['gluw'] = np.ascontiguousarray(inp['s5_glu_w'].reshape(L, 2, 128, 256).transpose(0, 2, 1, 3))
    lw2 = np.zeros((L, 128, 2, 256), np.float32)
    for l in range(L):
        lw2[l, 0:16, 0] = inp['rw_w2'][l, 0]
        lw2[l, 32:48, 1] = inp['rw_w2'][l, 1]
        lw2[l, 64:80, 0] = inp['rw_a2'][l, 0]
        lw2[l, 80:96, 1] = inp['rw_a2'][l, 1]
    sh['lw2'] = lw2
    sh['wbr'] = np.ascontiguousarray(inp['w_branch'].reshape(L, 4, 2, 128, 1024).transpose(0, 3, 1, 2, 4))
    sh['wout'] = np.ascontiguousarray(inp['w_out'].reshape(L, 8, 128, 1024).transpose(0, 2, 1, 3))
    rows = np.zeros((L, 128, 4096 + 512), np.float32)
    for l in range(L):
        rows[l, :, 0:1024] = inp['b_out'][l][None]
        rows[l, :, 1024:2048] = inp['ln_w'][l][None]
        rows[l, :, 2048:3072] = inp['ln_b'][l][None]
        rows[l, :, 3072:4096] = inp['ada_b'][l][None, 2048:3072]
        rows[l, :, 4096:4352] = inp['b_in'][l][None, 1280:1536]
        rows[l, :, 4352:4608] = inp['b_in'][l][None, 2304:2560]
    sh['rows'] = rows
    sh['masks'] = _masks()
    cos, sins, pm = _rot_tables()
    sh['rcos'] = cos
    sh['rsin'] = sins
    t = np.arange(128)
    cst = np.zeros((128, 9, 128), np.float32)
    cst[:, 0] = pm
    cst[:, 1] = ((t[:, None] // 64) == (t[None, :] // 64))
    cst[:, 2] = np.maximum(t[None, :] - t[:, None], 0)
    cst[:, 3] = np.maximum(t[:, None] - t[None, :], 0)
    cst[:, 4] = (t[None, :] >= t[:, None])
    cst[:, 5] = (t[None, :] <= t[:, None])
    cst[:, 6, :64] = ((t[:, None] % 64) == np.arange(64)[None, :])
    cst[:, 6, 64:68] = ((t[:, None] // 32) == np.arange(4)[None, :])
    cst[:, 6, 68] = 127 - t
    cst[:, 6, 69] = t
    cst[:, 7] = t[None, :] + 1.0
    cst[:, 8] = 128.0 - t[None, :]
    sh['cst'] = cst
    return sh


def prep_core(inp, b):
    pc = {}
    pc['hin'] = np.ascontiguousarray(np.concatenate([inp['ctx'][b], inp['x'][b]], 0))
    cv = np.stack([inp['c'][b], inp['c_ctx']], -1)
    pc['cvec'] = np.ascontiguousarray(cv.reshape(8, 128, 2).transpose(1, 0, 2))
    return pc


SHAPES = dict(hin=[NT, DM], cvec=[128, 8, 2], w_in=[2, 128, 8, NCOL], ada_w=[2, 128, 8, 3072], pv=[2, 128, NPV],
              s5bt=[2, 128, 2, 4, 2, 128], s5ct=[2, 128, 8, 2, 128], gluw=[2, 128, 2, 256], lw2=[2, 128, 2, 256],
              wbr=[2, 128, 4, 2, 1024], wout=[2, 128, 8, 1024], rows=[2, 128, 4608], masks=[128, 18, 128],
              rcos=[128, 2048], rsin=[128, 2048], cst=[128, 9, 128])


def build(debug=(), nlayers=2, phases=('s5', 'hg', 'ret', 'rw', 'merge'), stop=None):
    nc = bass.Bass("TRN2", target_bir_lowering=False)
    S = Sched(nc)
    dr = {k: nc.dram_tensor(k, list(v), F32, kind="ExternalInput").ap() for k, v in SHAPES.items()}
    out_d = nc.dram_tensor("out", [2048, DM], F32, kind="ExternalOutput").ap()
    h1_d = nc.dram_tensor("h1", [NT, DM], F32, kind="Internal").ap()
    sgd = nc.dram_tensor("sgd", [32, 128, NT], BF16, kind="Internal").ap()
    pre_sg = set()
    dbg_d = {}

    def dbg_out(name, shape):
        dbg_d[name] = nc.dram_tensor("dbg_" + name, list(shape), F32, kind="ExternalOutput").ap()
        return dbg_d[name]

    uid = [0]

    def key(p='k'):
        uid[0] += 1
        return '%s%d' % (p, uid[0])

    with contextlib.ExitStack() as top:
        S.sems = {n: top.enter_context(nc.semaphore(n)) for n in S.sem_names()}

        minrem = {}

        def sb(st, name, shape, dt=F32):
            uid[0] += 1
            t_ = st.enter_context(nc.sbuf_tensor("%s_%d" % (name, uid[0]), list(shape), dt))
            if MEMDBG:
                pre = name[:2]
                minrem[pre] = min(minrem.get(pre, 1 << 30), nc.sbuf_bytes_remaining)
            return t_

        def ps(st, name, shape, dt=F32):
            uid[0] += 1
            return st.enter_context(nc.psum_tensor("%s_%d" % (name, uid[0]), list(shape), dt))

        def mm(out, lhsT, rhs, r, w, start=True, stop=True):
            S.op('pe', lambda e: e.matmul(out, lhsT=lhsT, rhs=rhs, start=start, stop=stop), reads=r, writes=w)

        def tr(out, in_, ident, r, w):
            S.op('pe', lambda e: e.transpose(out, in_, ident), reads=r, writes=w)

        def act(out, in_, func, r, w, bias=0.0, scale=1.0):
            S.op('act', lambda e: e.activation(out=out, in_=in_, func=func, bias=bias, scale=scale), reads=r, writes=w)

        def tt(en, out, in0, in1, op, r, w):
            S.op(en, lambda e: e.tensor_tensor(out=out, in0=in0, in1=in1, op=op), reads=r, writes=w)

        def ts(en, out, in0, s1, s2, op0, op1, r, w):
            if s2 is None:
                S.op(en, lambda e: e.tensor_scalar(out=out, in0=in0, scalar1=s1, scalar2=None, op0=op0), reads=r, writes=w)
            else:
                S.op(en, lambda e: e.tensor_scalar(out=out, in0=in0, scalar1=s1, scalar2=s2, op0=op0, op1=op1),
                     reads=r, writes=w)

        def stt(out, in0, sc, in1, op0, op1, r, w):
            S.op('dve', lambda e: e.scalar_tensor_tensor(out=out, in0=in0, scalar=sc, in1=in1, op0=op0, op1=op1),
                 reads=r, writes=w)

        def cp(en, out, in_, r, w):
            if en == 'act':
                S.op('act', lambda e: e.copy(out=out, in_=in_), reads=r, writes=w)
            else:
                S.op(en, lambda e: e.tensor_copy(out=out, in_=in_), reads=r, writes=w)

        def memset(en, ap, val, w):
            S.op(en, lambda e: e.memset(ap, val), writes=w)

        def run_pipelined(gens, stagger):
            it = iter(gens)
            active, pending, rounds = [], True, 0
            while pending or active:
                if pending and rounds % stagger == 0:
                    try:
                        active.append(next(it))
                    except StopIteration:
                        pending = False
                for g in list(active):
                    try:
                        next(g)
                    except StopIteration:
                        active.remove(g)
                rounds += 1

        def mkbanks(st_, n, prefix):
            bl = [ps(st_, "%s%d" % (prefix, i), [128, 512], F32) for i in range(n)]
            cnt = [0]

            def bank():
                i = cnt[0] % n
                cnt[0] += 1
                return bl[i], '%s%d' % (prefix, i)
            return bank

        def gate_jobs(l, last, st_, bankfn, kds):
            wgt = [sb(st_, "gjw%d" % i, [128, 8, 128], BF16) for i in range(2)]
            sgs = [sb(st_, "gjs%d" % i, [128, 512], BF16) for i in range(2)]
            cnt = [0]

            def job(i, kd):
                k, dt_ = kd // 8, kd % 8
                w_, wk_ = wgt[i % 2], 'gjw%d' % (i % 2)
                c0 = 3968 + k * 1024 + dt_ * 128
                S.dma('pool', w_[:], dr['w_in'][l][:, :, c0:c0 + 128], writes=[wk_])
                yield
                for (n0, nn) in BLOCKS:
                    if last and n0 < 256:
                        continue
                    pg_, pgk_ = bankfn()
                    for jj in range(8):
                        mm(pg_[:, 0:nn], w_[:, jj, :], uT[:, jj, n0:n0 + nn], [wk_] + uTk[n0 // 128:(n0 + nn) // 128], [pgk_],
                           start=(jj == 0), stop=(jj == 7))
                    yield
                    c_ = cnt[0] % 2
                    cnt[0] += 1
                    act(sgs[c_][:, 0:nn], pg_[:, 0:nn], AF.Sigmoid, [pgk_, 'pvt'], ['gjs%d' % c_], bias=pv('bin', 31 + k * 8 + dt_))
                    yield
                    S.dma('sp', sgd[kd][:, n0:n0 + nn], sgs[c_][:, 0:nn], reads=['gjs%d' % c_], writes=['sgd'])
                    yield
                pre_sg.add((l, kd))
            return [job(i, kd) for i, kd in enumerate(kds)]

        def interleave(main, extra, every):
            out, ei = [], 0
            extra = list(extra)
            for i, g in enumerate(main):
                out.append(g)
                if (i + 1) % every == 0 and ei < len(extra):
                    out.append(extra[ei])
                    ei += 1
            out.extend(extra[ei:])
            return out

        def dbg_dump(name, ap, shape, r):
            if name in debug:
                d = dbg_out(name, shape)
                S.dma('sp', d, ap, reads=r, writes=['dbgout_' + name])

        cstb = sb(top, "cstb", [128, 3, 128], BF16)
        cstf = sb(top, "cstf", [128, 7, 128], F32)
        maskb = sb(top, "maskb", [128, 18, 128], BF16)
        silc = sb(top, "silc", [128, 8, 2], F32)
        S.dma('pool', cstb[:, 0:2, :], dr['cst'][:, 0:2, :], writes=['cstb'])
        S.dma('sp', cstf[:], dr['cst'][:, 2:9, :], writes=['cstf'])
        S.dma('pool', maskb[:], dr['masks'], writes=['maskb'])
        S.dma('sp', silc[:], dr['cvec'], writes=['silc'])
        memset('pool', cstb[:, 2, :], 0.0, ['cstb'])
        S.op('pool', lambda e: e.affine_select(out=cstb[:, 2, :], in_=cstb[:, 2, :], pattern=[[-1, 128]],
                                               compare_op=ALU.not_equal, fill=1.0, base=0, channel_multiplier=1),
             reads=['cstb'], writes=['cstb'])
        act(silc[:], silc[:], AF.Silu, ['silc'], ['silc'])
        identb = cstb[:, 2, :]
        bonesb = cstb[:, 1, :]

        uT = sb(top, "uT", [128, 8, NT], BF16)
        Y = sb(top, "Y", [128, 4, 2, NT], BF16)
        pvt = sb(top, "pvt", [128, NPV], F32)
        if debug:
            memset('pool', Y[:], 0.0, ['Y0', 'Y1', 'Y2', 'Y3'])
        modfm = sb(top, "modfm", [128, 16, 2], F32)
        gatebc = sb(top, "gatebc", [128, 2, DM], F32)

        def pv(name, j=None, n=1):
            o, cnt = PV[name]
            if j is None:
                return pvt[:, o:o + cnt]
            return pvt[:, o + j:o + j + n]

        PHASES = {}
        def proj_fm(st, wt, wk, mlist, evac, pp, ppk):
            cnt = 0
            for (n0, nn) in BLOCKS:
                for mi, m in enumerate(mlist):
                    p_, pk_ = pp[cnt % len(pp)], ppk[cnt % len(pp)]
                    cnt += 1
                    for j in range(8):
                        mm(p_[:, 0:nn], wt[:, j, m * 128:(m + 1) * 128], uT[:, j, n0:n0 + nn],
                           ['%s%d' % (wk, m // 2)] + uTk[n0 // 128:(n0 + nn) // 128], [pk_], start=(j == 0), stop=(j == 7))
                    evac(mi, m, n0, nn, p_, pk_)

        def phase_s5(l, h_src, last):
            L = 128
            with contextlib.ExitStack() as st:
                btb = sb(st, "btb", [128, 2, 4, 2, 128], BF16)
                ctb = sb(st, "ctb", [128, 8, 2, 128], BF16)
                glub = sb(st, "glub", [128, 2, 256], BF16)
                S.dma('pool', btb[:], dr['s5bt'][l], writes=['btb'])
                S.dma('pool', ctb[:], dr['s5ct'][l], writes=['ctb'])
                S.dma('pool', glub[:], dr['gluw'][l], writes=['glub'])
                ts('pool', ctb[:, :, 1, :], ctb[:, :, 1, :], -1.0, 0.0, ALU.mult, ALU.add, ['ctb'], ['ctb'])
                ub = sb(st, "s5u", [128, 2, NT], BF16)
                zs = sb(st, "s5z", [128, 2, NT], BF16)
                yacc = sb(st, "yacc", [128, 2, NT], F32)
                PT = sb(st, "s5PT", [128, 16, 2, L], F32)
                QT = sb(st, "s5QT", [128, 16, 2, L], F32)
                sst = sb(st, "s5st", [128, 16, 2], F32)
                ones = sb(st, "s5ones", [128, L], F32)
                memset('pool', yacc[:], 0.0, ['yacc'])
                memset('pool', sst[:], 0.0, ['sst'])
                memset('pool', ones[:], 1.0, ['s5ones'])
                with contextlib.ExitStack() as st2:
                    wsu = sb(st2, "wsu", [128, 8, 512], BF16)
                    for pc_ in range(2):
                        S.dma('pool', wsu[:, :, pc_ * 256:(pc_ + 1) * 256], dr['w_in'][l][:, :, pc_ * 256:(pc_ + 1) * 256], writes=['wsu%d' % pc_])
                    pp = [ps(st2, "s5pp%d" % i, [128, 512], F32) for i in range(2)]

                    def evac(mi, m, n0, nn, p_, pk_):
                        if m < 2:
                            act(ub[:, m, n0:n0 + nn], p_[:, 0:nn], AF.Identity, [pk_, 'pvt'], ['s5u'], bias=pv('bin', m))
                        else:
                            act(zs[:, m - 2, n0:n0 + nn], p_[:, 0:nn], AF.Silu, [pk_, 'pvt'], ['s5z'], bias=pv('bin', m))
                    proj_fm(st2, wsu, 'wsu', [0, 1, 2, 3], evac, pp, ['s5pp0', 's5pp1'])
                    sm = sb(st2, "s5sm", [128, 20, 16], F32)
                    K_ = 's5sm'

                    def Sm(i):
                        return sm[:, i, :]

                    def T2(o, a, b, op):
                        tt('dve', Sm(o), a if not isinstance(a, int) else Sm(a), b if not isinstance(b, int) else Sm(b), op,
                           [K_, 'pvt'], [K_])
                    lamre, lamim = pv('lamre'), pv('lamim')
                    act(Sm(0), pv('ldt'), AF.Exp, ['pvt'], [K_])
                    T2(1, lamre, 0, ALU.mult)
                    act(Sm(2), Sm(1), AF.Exp, [K_], [K_])
                    act(Sm(3), Sm(1), AF.Exp, [K_], [K_], scale=-1.0)
                    T2(4, lamim, 0, ALU.mult)
                    ts('dve', Sm(5), Sm(4), PI / 2, None, ALU.add, None, [K_], [K_])
                    for x in (4, 5):
                        for _ in range(4):
                            ts('dve', Sm(16), Sm(x), PI, 2 * PI, ALU.is_gt, ALU.mult, [K_], [K_])
                            T2(x, x, 16, ALU.subtract)
                    act(Sm(6), Sm(4), AF.Sin, [K_], [K_])
                    act(Sm(7), Sm(5), AF.Sin, [K_], [K_])
                    T2(8, 2, 7, ALU.mult)
                    T2(9, 2, 6, ALU.mult)
                    T2(10, 3, 7, ALU.mult)
                    stt(Sm(11), Sm(3), -1.0, Sm(6), ALU.mult, ALU.mult, [K_], [K_])
                    ts('dve', Sm(12), Sm(8), -1.0, None, ALU.add, None, [K_], [K_])
                    T2(16, lamre, lamre, ALU.mult)
                    T2(17, lamim, lamim, ALU.mult)
                    T2(13, 16, 17, ALU.add)
                    S.op('dve', lambda e: e.reciprocal(out=Sm(13), in_=Sm(13)), reads=[K_], writes=[K_])
                    T2(16, 12, lamre, ALU.mult)
                    T2(17, 9, lamim, ALU.mult)
                    T2(16, 16, 17, ALU.add)
                    T2(14, 16, 13, ALU.mult)
                    T2(16, 9, lamre, ALU.mult)
                    T2(17, 12, lamim, ALU.mult)
                    T2(16, 16, 17, ALU.subtract)
                    T2(15, 16, 13, ALU.mult)
                    tmpa = sb(st2, "s5ta", [128, 16, L], F32)
                    tmpb = sb(st2, "s5tb", [128, 16, L], F32)

                    def cmul_bc(dst_re, dst_im, src_re, src_im, s_re, s_im, m):
                        sr = s_re.unsqueeze(2).broadcast_to([128, 16, m])
                        si = s_im.unsqueeze(2).broadcast_to([128, 16, m])
                        ta, tb = tmpa[:, :, 0:m], tmpb[:, :, 0:m]
                        kk_ = ['s5tab', 's5ta', 's5tb', 's5tc', K_]
                        tt('dve', ta, src_re, sr, ALU.mult, kk_, ['s5ta'])
                        tt('dve', tb, src_im, si, ALU.mult, kk_, ['s5tb'])
                        tt('dve', dst_re, ta, tb, ALU.subtract, kk_, ['s5tab'])
                        tt('dve', ta, src_re, si, ALU.mult, kk_, ['s5ta'])
                        tt('dve', tb, src_im, sr, ALU.mult, kk_, ['s5tb'])
                        tt('dve', dst_im, ta, tb, ALU.add, kk_, ['s5tab'])
                    for (TB, a_re, a_im) in ((PT, 8, 9), (QT, 10, 11)):
                        cp('dve', TB[:, :, 0, 0], Sm(a_re), [K_], ['s5tab'])
                        cp('dve', TB[:, :, 1, 0], Sm(a_im), [K_], ['s5tab'])
                        m = 1
                        while m < L:
                            cmul_bc(TB[:, :, 0, m:2 * m], TB[:, :, 1, m:2 * m], TB[:, :, 0, 0:m], TB[:, :, 1, 0:m],
                                    TB[:, :, 0, m - 1], TB[:, :, 1, m - 1], m)
                            m *= 2
                    tmpc = sb(st2, "s5tc", [128, 16, L], F32)
                    cp('dve', tmpc[:], QT[:, :, 0, :], ['s5tab'], ['s5tc'])
                    cmul_bc(QT[:, :, 0, :], QT[:, :, 1, :], tmpc[:], QT[:, :, 1, :], Sm(14), Sm(15), L)
                    S.barrier()
                with contextlib.ExitStack() as st2:
                    NB = 8
                    xa = [sb(st2, "s5xa%d" % i, [128, 2, L], F32) for i in range(NB)]
                    xb_ = [sb(st2, "s5xb%d" % i, [128, 2, L], F32) for i in range(NB)]
                    cw = [sb(st2, "s5cw%d" % i, [128, 2, L], F32) for i in range(NB)]
                    hb = [sb(st2, "s5hb%d" % i, [128, 2, L], BF16) for i in range(NB)]
                    pbu = [ps(st2, "s5pb%d" % i, [128, 2, 2, L], F32) for i in range(4)]
                    py = [ps(st2, "s5py%d" % i, [128, 512], F32) for i in range(2)]
                    orders = [list(range(NTL)), [1, 0] + list(range(NTL - 1, 1, -1))]
                    def s5group(gi, step, d, j):
                        c = orders[d][step]
                        n0 = c * L
                        rev = (d == 1)
                        U = []
                        for ii in range(4):
                            un = gi * 4 + ii
                            bnk = (un // 2) % 4
                            U.append(dict(ii=ii, i=j * 4 + ii, q=d * 8 + j * 4 + ii, pb=pbu[bnk][:, un % 2], pbk='s5pb%d' % bnk,
                                          A=xa[un % NB], Ak='s5xa%d' % (un % NB), B=xb_[un % NB], Bk='s5xb%d' % (un % NB),
                                          C=cw[un % NB], Ck='s5cw%d' % (un % NB), H=hb[un % NB], Hk='s5hb%d' % (un % NB)))
                        for u in U:
                            for ri in range(2):
                                mm(u['pb'][:, ri, :], btb[:, j, u['ii'], ri, :], ub[:, j, n0:n0 + L], ['btb', 's5u'], [u['pbk']])
                        yield
                        for u in U:
                            src = u['pb'][:, :, ::-1] if rev else u['pb'][:, :, :]
                            tt('dve', u['A'][:], src, QT[:, u['q'], 0:1, :].broadcast_to([128, 2, L]), ALU.mult,
                               [u['pbk'], 's5tab'], [u['Ak']])
                        yield
                        for u in U:
                            src = u['pb'][:, ::-1, ::-1] if rev else u['pb'][:, ::-1, :]
                            tt('dve', u['B'][:], src, QT[:, u['q'], 1:2, :].broadcast_to([128, 2, L]), ALU.mult,
                               [u['pbk'], 's5tab'], [u['Bk']])
                        yield
                        for u in U:
                            tt('dve', u['A'][:, 0, :], u['A'][:, 0, :], u['B'][:, 0, :], ALU.subtract, [u['Ak'], u['Bk']], [u['Ak']])
                        yield
                        for u in U:
                            tt('dve', u['A'][:, 1, :], u['A'][:, 1, :], u['B'][:, 1, :], ALU.add, [u['Ak'], u['Bk']], [u['Ak']])
                        yield
                        for ri in range(2):
                            for u in U:
                                q = u['q']
                                S.op('dve', lambda e, u=u, ri=ri, q=q: e.tensor_tensor_scan(
                                    out=u['C'][:, ri, :], data0=ones[:], data1=u['A'][:, ri, :], initial=sst[:, q, ri:ri + 1],
                                    op0=ALU.mult, op1=ALU.add), reads=[u['Ak'], 's5ones', 'sst%d' % q, 'sst'], writes=[u['Ck']])
                            yield
                        for u in U:
                            tt('pool', u['A'][:], u['C'][:], PT[:, u['q'], 0:1, :].broadcast_to([128, 2, L]), ALU.mult,
                               [u['Ck'], 's5tab', u['Ak']], [u['Ak']])
                        yield
                        for u in U:
                            tt('pool', u['B'][:], u['C'][:, ::-1, :], PT[:, u['q'], 1:2, :].broadcast_to([128, 2, L]), ALU.mult,
                               [u['Ck'], 's5tab', u['Bk']], [u['Bk']])
                        yield
                        for u in U:
                            tt('pool', u['A'][:, 0, :], u['A'][:, 0, :], u['B'][:, 0, :], ALU.subtract, [u['Ak'], u['Bk']], [u['Ak']])
                        yield
                        for u in U:
                            tt('pool', u['A'][:, 1, :], u['A'][:, 1, :], u['B'][:, 1, :], ALU.add, [u['Ak'], u['Bk']], [u['Ak']])
                        yield
                        for u in U:
                            cp('pool', sst[:, u['q'], :], u['A'][:, :, L - 1], [u['Ak']], ['sst%d' % u['q']])
                        yield
                        for u in U:
                            hsrc = u['A'][:, :, ::-1] if rev else u['A'][:]
                            cp('act', u['H'][:], hsrc, [u['Ak']], [u['Hk']])
                        yield
                        pyr = py[gi % 2][:, 0:L]
                        pyk = 's5py%d' % (gi % 2)
                        for k_, u in enumerate(U):
                            for ri in range(2):
                                mm(pyr, ctb[:, u['i'], ri, :], u['H'][:, ri, :], ['ctb', u['Hk']], [pyk],
                                   start=(k_ == 0 and ri == 0), stop=(k_ == 3 and ri == 1))
                        yield
                        yield
                        yield
                        tt('dve', yacc[:, j, n0:n0 + L], yacc[:, j, n0:n0 + L], pyr, ALU.add, [pyk, 'yacc'], ['yacc'])

                    glist = [(step, d, j) for step in range(NTL) for d in range(2) for j in range(2)]
                    gbank = mkbanks(st2, 2, "s5gk") if (GATE_PRE and GJ_S5) else None
                    gj = gate_jobs(l, last, st2, gbank, GJ_S5) if (GATE_PRE and GJ_S5) else []
                    run_pipelined(interleave([s5group(gi, *g) for gi, g in enumerate(glist)], gj, 2), S5_STAGGER)
                    S.barrier()
                for j in range(2):
                    stt(yacc[:, j, :], ub[:, j, :], pv('s5d', j), yacc[:, j, :], ALU.mult, ALU.add, ['s5u', 'yacc', 'pvt'],
                        ['yacc'])
                dbg_dump('ya%d' % l, yacc[:], [128, 2, NT], ['yacc'])
                with contextlib.ExitStack() as st2:
                    t1 = [sb(st2, "s5g1_%d" % i, [128, 512], F32) for i in range(2)]
                    t2 = [sb(st2, "s5g2_%d" % i, [128, 512], BF16) for i in range(2)]
                    pg = [ps(st2, "s5pg%d" % i, [128, 512], F32) for i in range(2)]
                    cnt = 0
                    for (n0, nn) in BLOCKS:
                        for j in range(2):
                            a, ak = t1[cnt % 2], 's5g1_%d' % (cnt % 2)
                            cnt += 1
                            ysl = yacc[:, j, n0:n0 + nn]
                            act(a[:, 0:nn], ysl, AF.Square, ['yacc'], [ak])
                            ts('dve', a[:, 0:nn], a[:, 0:nn], 0.044715, 1.0, ALU.mult, ALU.add, [ak], [ak])
                            tt('dve', a[:, 0:nn], a[:, 0:nn], ysl, ALU.mult, [ak, 'yacc'], [ak])
                            act(a[:, 0:nn], a[:, 0:nn], AF.Sigmoid, [ak], [ak], scale=1.5957691216057308)
                            tt('dve', ub[:, j, n0:n0 + nn], a[:, 0:nn], ysl, ALU.mult, [ak, 'yacc'], ['s5u'])
                    cnt = 0
                    for (n0, nn) in BLOCKS:
                        for m in range(2):
                            p_, pk_ = pg[cnt % 2], 's5pg%d' % (cnt % 2)
                            b_, bk_ = t2[cnt % 2], 's5g2_%d' % (cnt % 2)
                            cnt += 1
                            for jc in range(2):
                                mm(p_[:, 0:nn], glub[:, jc, m * 128:(m + 1) * 128], ub[:, jc, n0:n0 + nn], ['glub', 's5u'], [pk_],
                                   start=(jc == 0), stop=(jc == 1))
                            act(b_[:, 0:nn], p_[:, 0:nn], AF.Sigmoid, [pk_, 'pvt'], [bk_], bias=pv('glub', m))
                            tt('dve', b_[:, 0:nn], b_[:, 0:nn], ub[:, m, n0:n0 + nn], ALU.mult, [bk_, 's5u'], [bk_])
                            tt('pool', Y[:, 0, m, n0:n0 + nn], b_[:, 0:nn], zs[:, m, n0:n0 + nn], ALU.mult, [bk_, 's5z'], ['Y0'])
                    S.barrier()
                S.barrier()
        PHASES['s5'] = phase_s5
        def phase_hg(l, h_src, last):
            with contextlib.ExitStack() as st:
                QP = [sb(st, "hgQP%d" % d, [128, 2, NT], BF16) for d in range(2)]
                KP = [sb(st, "hgKP%d" % d, [128, 2, NT], BF16) for d in range(2)]
                G = sb(st, "hgG", [128, 2, 72, 2], F32)
                VT = sb(st, "hgVT", [128, NTL, 256], BF16)
                zs = sb(st, "hgzs", [128, 2, NT], BF16)
                lbt = sb(st, "hglbt", [128, 2, 4], F32)
                if l == 0:
                    memset('pool', lbt[:, 0, :], 0.0, ['hglbt'])
                    memset('pool', lbt[:, 1, :], 1.0, ['hglbt'])
                else:
                    o_, _ = PV['hglb']
                    tt('dve', lbt[:, 0, :], pvt[:, o_ + 4:o_ + 8], pvt[:, o_:o_ + 4], ALU.subtract, ['pvt'], ['hglbt'])
                    act(lbt[:, 0, :], lbt[:, 0, :], AF.Sigmoid, ['hglbt'], ['hglbt'])
                    ts('dve', lbt[:, 1, :], lbt[:, 0, :], -1.0, 1.0, ALU.mult, ALU.add, ['hglbt'], ['hglbt'])
                with contextlib.ExitStack() as st2:
                    wh = sb(st2, "hgw", [128, 8, 1280], BF16)
                    for pc_ in (0, 4, 1, 2, 3):
                        S.dma('pool', wh[:, :, pc_ * 256:(pc_ + 1) * 256], dr['w_in'][l][:, :, 512 + pc_ * 256:512 + (pc_ + 1) * 256], writes=['hgw%d' % pc_])
                    brow = sb(st2, "hgbrow", [128, 256], F32)
                    S.dma('sp', brow[:], dr['rows'][l][:, 4096:4352], writes=['hgbrow'])
                    R32 = sb(st2, "hgR32", [128, 512], F32)
                    memset('pool', R32[:], 1.0, ['hgR32'])
                    memset('pool', R32[:, 0:512:32], 0.0, ['hgR32'])
                    QS = [sb(st2, "hgQS%d" % i, [128, 2, 512], BF16) for i in range(2)]
                    T = [[sb(st2, "hgT%d_%d" % (i, k), [128, 512], F32) for k in range(4)] for i in range(2)]
                    pp = [ps(st2, "hgpp%d" % i, [128, 512], F32) for i in range(3)]
                    pt = [ps(st2, "hgpt%d" % i, [128, 512], F32) for i in range(2)]
                    def hgproj(cnt, ic, m, n0, nn):
                        ukeys = uTk[n0 // 128:(n0 + nn) // 128]
                        p_, pk_ = pp[cnt % 3], 'hgpp%d' % (cnt % 3)
                        bi = (n0 // 512) % 2 if n0 else 0
                        for jj in range(8):
                            mm(p_[:, 0:nn], wh[:, jj, m * 128:(m + 1) * 128], uT[:, jj, n0:n0 + nn], ['hgw%d' % (m // 2)] + ukeys, [pk_],
                               start=(jj == 0), stop=(jj == 7))
                        yield
                        bias = pv('bin', 4 + m)
                        if m < 2:
                            act(QS[bi][:, m, 0:nn], p_[:, 0:nn], AF.Silu, [pk_, 'pvt'], ['hgQS%d' % bi], bias=bias)
                            return
                        if m >= 8:
                            act(zs[:, m - 8, n0:n0 + nn], p_[:, 0:nn], AF.Silu, [pk_, 'pvt'], ['hgzs'], bias=bias)
                            return
                        d, j = (m - 2) // 2, (m - 2) % 2
                        Ts = T[ic % 2]
                        Tk = ['hgT%d_%d' % (ic % 2, k) for k in range(4)]
                        t1, t2, t3, t4 = [x[:, 0:nn] for x in Ts]
                        act(t1, p_[:, 0:nn], AF.Sigmoid, [pk_, 'pvt'], [Tk[0]], bias=bias)
                        yield
                        ts('dve', t1, t1, lbt[:, 1, d * 2 + j:d * 2 + j + 1], lbt[:, 0, d * 2 + j:d * 2 + j + 1], ALU.mult, ALU.add,
                           [Tk[0], 'hglbt'], [Tk[0]])
                        yield
                        act(t2, t1, AF.Ln, [Tk[0]], [Tk[1]])
                        yield
                        if d == 0:
                            S.op('dve', lambda e: e.tensor_tensor_scan(out=t3, data0=R32[:, 0:nn], data1=t2, initial=0.0,
                                                                       op0=ALU.mult, op1=ALU.add),
                                 reads=[Tk[1], 'hgR32'], writes=[Tk[2]])
                        else:
                            S.op('dve', lambda e: e.tensor_tensor_scan(out=t3[:, ::-1],
                                                                       data0=R32[:, 0:nn], data1=t2[:, ::-1], initial=0.0,
                                                                       op0=ALU.mult, op1=ALU.add),
                                 reads=[Tk[1], 'hgR32'], writes=[Tk[2]])
                        yield
                        ts('dve', t3, t3, -80.0, None, ALU.max, None, [Tk[2]], [Tk[2]])
                        ts('dve', t1, t1, -1.0, 1.0, ALU.mult, ALU.add, [Tk[0]], [Tk[0]])
                        yield
                        act(t4, t3, AF.Exp, [Tk[2]], [Tk[3]])
                        act(t2, t3, AF.Exp, [Tk[2]], [Tk[1]], scale=-1.0)
                        yield
                        tt('pool', KP[d][:, j, n0:n0 + nn], t1, t2, ALU.mult, [Tk[0], Tk[1]], ['hgKP%d' % d])
                        tt('pool', QP[d][:, j, n0:n0 + nn], QS[bi][:, j, 0:nn], t4, ALU.mult, ['hgQS%d' % bi, Tk[3]], ['hgQP%d' % d])
                        c0 = n0 // 32
                        gsrc = t4[:, 31::32] if d == 0 else t4[:, 0::32]
                        cp('act', G[:, d, c0:c0 + nn // 32, j], gsrc, [Tk[3]], ['hgG'])

                    plist = []
                    cnt = 0
                    ic = 0
                    for (n0, nn) in BLOCKS:
                        for m in (0, 1, 8, 9, 2, 3, 4, 5):
                            plist.append((cnt, ic, m, n0, nn))
                            cnt += 1
                            if 2 <= m < 8:
                                ic += 1
                    run_pipelined((hgproj(*p) for p in plist), 4)
                    for t in range(NTL):
                        p_, pk_ = pt[t % 2], 'hgpt%d' % (t % 2)
                        for jj in range(8):
                            mm(p_[:, 0:256], uT[:, jj, t * 128:(t + 1) * 128], wh[:, jj, 768:1024], ['hgw3', uTk[t]], [pk_],
                               start=(jj == 0), stop=(jj == 7))
                        tt('dve', VT[:, t, :], p_[:, 0:256], brow[:], ALU.add, [pk_, 'hgbrow'], ['hgVT'])
                    S.barrier()
                Sall = [sb(st, "hgSall%d" % d, [128, 2, 72, 64], BF16) for d in range(2)]
                with contextlib.ExitStack() as st2:
                    Sst = [sb(st2, "hgS%d" % d, [128, 2, 64], F32) for d in range(2)]
                    kTm = [sb(st2, "hgkTm%d" % i, [128, 4, 256], BF16) for i in range(3)]
                    Ug = [sb(st2, "hgUg%d" % i, [128, 4, 2, 64], F32) for i in range(3)]
                    ptr = [ps(st2, "hgptr%d" % i, [128, 8, 128], BF16) for i in range(2)]
                    pU = [ps(st2, "hgpU%d" % i, [128, 4, 2, 64], F32) for i in range(3)]
                    orders = [list(range(NTL)), [1, 0] + list(range(NTL - 1, 1, -1))]
                    for d in range(2):
                        memset('pool', Sst[d][:], 0.0, ['hgS%d' % d])
                    def hgchain(it, step, d):
                        t = orders[d][step]
                        pr, prk = ptr[it % 2], 'hgptr%d' % (it % 2)
                        km, kmk = kTm[it % 3], 'hgkTm%d' % (it % 3)
                        pu, puk = pU[it % 3], 'hgpU%d' % (it % 3)
                        ug, ugk = Ug[it % 3], 'hgUg%d' % (it % 3)
                        for j in range(2):
                            tr(pr[:, j, :], KP[d][:, j, t * 128:(t + 1) * 128], identb, ['hgKP%d' % d, 'cstb'], [prk])
                        yield
                        for cc in range(4):
                            prf = pr[:, 0:2, :].rearrange("p a b -> p (a b)")
                            if cc % 2 == 0:
                                ts('dve', km[:, cc, :], prf, cstf[:, 4, 64 + cc:64 + cc + 1], None, ALU.mult, None, [prk, 'cstf'], [kmk])
                            else:
                                act(km[:, cc, :], prf, AF.Identity, [prk, 'cstf'], [kmk], scale=cstf[:, 4, 64 + cc:64 + cc + 1])
                        yield
                        for cc in range(4):
                            for h in range(4):
                                hp = (h % 2) * 64
                                mm(pu[hp:hp + 64, cc, h // 2, :], km[:, cc, h * 64:(h + 1) * 64], VT[:, t, h * 64:(h + 1) * 64],
                                   [kmk, 'hgVT'], [puk])
                        yield
                        tt('dve', ug[:], pu[:], G[:, d, t * 4:(t + 1) * 4, :].unsqueeze(3).broadcast_to([128, 4, 2, 64]), ALU.mult,
                           [puk, 'hgG'], [ugk])
                        yield
                        ccs = range(4) if d == 0 else range(3, -1, -1)
                        for cc in ccs:
                            c = t * 4 + cc
                            cp('act', Sall[d][:, :, c, :], Sst[d][:], ['hgS%d' % d], ['hgSall%d_%d' % (d, t)])
                            for j in range(2):
                                stt(Sst[d][:, j, :], Sst[d][:, j, :], G[:, d, c, j:j + 1], ug[:, cc, j, :], ALU.mult, ALU.add,
                                    ['hgS%d' % d, 'hgG', ugk], ['hgS%d' % d])
                            yield

                    gbank = mkbanks(st2, 3, "hggk") if GJ_SPLIT[0] else None
                    gj = gate_jobs(l, last, st2, gbank, GJ_SPLIT[0]) if (GATE_PRE and GJ_SPLIT[0]) else []
                    run_pipelined(interleave([hgchain(i_, sd[0], sd[1]) for i_, sd in enumerate([(s_, d_) for s_ in range(NTL) for d_ in range(2)])], gj, 3), 3)
                    S.barrier()
                with contextlib.ExitStack() as st2:
                    if ('yb%d' % l) in debug:
                        dbgbuf = sb(st2, "dbgbuf", [128, 2, NT], F32)
                    AT = [[sb(st2, "hgAT%d_%d" % (i, d), [128, 4, 128], BF16) for d in range(2)] for i in range(2)]
                    sq = [sb(st2, "hgsq%d" % i, [128, 2, 128], BF16) for i in range(2)]
                    rr = [sb(st2, "hgrr%d" % i, [128, 2, 128], F32) for i in range(2)]
                    ob = [sb(st2, "hgob%d" % i, [128, 2, 128], F32) for i in range(2)]
                    bank = mkbanks(st2, 8, "hgbk")

                    def hgout(t):
                        i2 = t % 2
                        tsl = slice(t * 128, (t + 1) * 128)
                        pas = {}
                        for d in range(2):
                            for par in range(2):
                                pas[(d, par)] = bank()
                            for h in range(4):
                                hp = (h % 2) * 64
                                pa, pak = pas[(d, h % 2)]
                                pav = pa[:, 0:256].rearrange("p (a b) -> p a b", a=2)
                                mm(pav[:, h // 2, :], KP[d][hp:hp + 64, h // 2, tsl], QP[d][hp:hp + 64, h // 2, tsl],
                                   ['hgKP%d' % d, 'hgQP%d' % d], [pak])
                        yield
                        for d in range(2):
                            for par in range(2):
                                pa, pak = pas[(d, par)]
                                pav = pa[:, 0:256].rearrange("p (a b) -> p a b", a=2)
                                tt('dve', AT[i2][d][:, par::2, :], pav, maskb[:, d, :].unsqueeze(1).broadcast_to([128, 2, 128]), ALU.mult,
                                   [pak, 'maskb'], ['hgAT%d_%d' % (i2, d)])
                        yield
                        pos = [bank() for _ in range(2)]
                        povs = [pos[par][0][:, 0:256].rearrange("p (a b) -> p a b", a=2) for par in range(2)]
                        for h in range(4):
                            hp = (h % 2) * 64
                            pok = pos[h % 2][1]
                            reg = povs[h % 2][hp:hp + 64, h // 2, :]
                            first = True
                            for d in range(2):
                                mm(reg, VT[:, t, h * 64:(h + 1) * 64], AT[i2][d][:, h, :], ['hgVT', 'hgAT%d_%d' % (i2, d)], [pok],
                                   start=first, stop=False)
                                first = False
                                for cc in range(4):
                                    c = t * 4 + cc
                                    mm(reg[:, cc * 32:(cc + 1) * 32], Sall[d][hp:hp + 64, h // 2, c, :],
                                       QP[d][hp:hp + 64, h // 2, t * 128 + cc * 32:t * 128 + (cc + 1) * 32],
                                       ['hgSall%d_%d' % (d, t), 'hgQP%d' % d], [pok], start=False, stop=(d == 1 and cc == 3))
                        yield
                        obk = 'hgob%d' % i2
                        cp('act', ob[i2][0:64], povs[0][0:64], [pos[0][1]], [obk])
                        cp('dve', ob[i2][64:128], povs[1][64:128], [pos[1][1]], [obk])
                        yield
                        pov = ob[i2][:]
                        pok = obk
                        if ('yb%d' % l) in debug:
                            cp('pool', dbgbuf[:, :, tsl], pov, [pok], ['dbgbuf'])
                        act(sq[i2][:], pov, AF.Square, [pok], ['hgsq%d' % i2])
                        yield
                        pss_, psk = bank()
                        psv = pss_[:, 0:256].rearrange("p (a b) -> p a b", a=2)
                        for j in range(2):
                            mm(psv[:, j, :], bonesb, sq[i2][:, j, :], ['cstb', 'hgsq%d' % i2], [psk])
                        yield
                        act(rr[i2][:], psv, AF.Sqrt, [psk], ['hgrr%d' % i2], bias=RMS_EPS, scale=1.0 / 64)
                        yield
                        S.op('dve', lambda e: e.reciprocal(out=rr[i2][:], in_=rr[i2][:]), reads=['hgrr%d' % i2], writes=['hgrr%d' % i2])
                        yield
                        tt('dve', rr[i2][:], pov, rr[i2][:], ALU.mult, [pok, 'hgrr%d' % i2], ['hgrr%d' % i2])
                        yield
                        for j in range(2):
                            stt(Y[:, 1, j, tsl], rr[i2][:, j, :], pv('hgnw', j), zs[:, j, tsl], ALU.mult, ALU.mult,
                                ['hgrr%d' % i2, 'pvt', 'hgzs'], ['Y1'])

                    gj = gate_jobs(l, last, st2, bank, GJ_SPLIT[1]) if (GATE_PRE and GJ_SPLIT[1]) else []
                    run_pipelined(interleave([hgout(t) for t in range(NTL) if not (last and t < 2 and not debug)], gj, 3), 5)
                    if ('yb%d' % l) in debug:
                        dbg_dump('yb%d' % l, dbgbuf[:], [128, 2, NT], ['dbgbuf'])
                    S.barrier()
                S.barrier()
        PHASES['hg'] = phase_hg
        def phase_ret(l, h_src, last):
            with contextlib.ExitStack() as st:
                QR = sb(st, "rtQR", [128, 2, NT], BF16)
                KR = sb(st, "rtKR", [128, 2, NT], BF16)
                VT = sb(st, "rtVT", [128, NTL, 256], BF16)
                zs = sb(st, "rtzs", [128, 2, NT], BF16)
                Sall = [sb(st, "rtSall%d" % d, [128, 2, NTL, 64], BF16) for d in range(2)]
                LG = sb(st, "rtLG", [128, 4], F32)
                GL = sb(st, "rtGL", [128, 4], F32)
                LGH = sb(st, "rtLGH", [128, 8], F32)
                QDEC = sb(st, "rtQDEC", [128, 2, 2, 128], F32)
                KDEC = sb(st, "rtKDEC", [128, 2, 4], F32)
                DS = sb(st, "rtDS", [128, 4, 128], F32)
                tb8 = sb(st, "rtb8", [128, 2], F32)
                K_ = 'rttab'
                act(LG[:], pv('rdec'), AF.Exp, ['pvt'], [K_])
                ts('dve', LG[:], LG[:], -1.0, None, ALU.mult, None, [K_], [K_])
                act(GL[:], LG[:], AF.Exp, [K_], [K_], scale=128.0)
                act(LGH[:], pv('rdech'), AF.Exp, ['pvt'], [K_])
                ts('dve', LGH[:], LGH[:], -1.0, None, ALU.mult, None, [K_], [K_])
                for d in range(2):
                    for j in range(2):
                        act(QDEC[:, d, j, :], cstf[:, 5 + d, :], AF.Exp, ['cstf', K_], [K_], scale=LG[:, d * 2 + j:d * 2 + j + 1])
                    act(KDEC[:, d, :], LGH[:, d * 4:(d + 1) * 4], AF.Exp, ['cstf', K_], [K_], scale=cstf[:, 4, 68 + d:69 + d])
                with contextlib.ExitStack() as st2:
                    ta = sb(st2, "rtta", [128, 128], F32)
                    tb = sb(st2, "rttb", [128, 128], F32)
                    for h in range(4):
                        act(ta[:], cstf[:, 0, :], AF.Exp, ['cstf', K_], ['rtta'], scale=LGH[:, h:h + 1])
                        tt('dve', ta[:], ta[:], cstf[:, 2, :], ALU.mult, ['rtta', 'cstf'], ['rtta'])
                        act(tb[:], cstf[:, 1, :], AF.Exp, ['cstf', K_], ['rttb'], scale=LGH[:, 4 + h:5 + h])
                        tt('dve', tb[:], tb[:], cstf[:, 3, :], ALU.mult, ['rttb', 'cstf'], ['rttb'])
                        tt('dve', DS[:, h, :], ta[:], tb[:], ALU.add, ['rtta', 'rttb'], [K_])
                    ts('dve', tb8[:], pv('bin', 16, 2), 0.125, None, ALU.mult, None, ['pvt'], [K_])
                    S.barrier()
                if stop == 'ret_tab':
                    return
                with contextlib.ExitStack() as st2:
                    wr = sb(st2, "rtw", [128, 8, 1024], BF16)
                    for pc_ in range(4):
                        S.dma('pool', wr[:, :, pc_ * 256:(pc_ + 1) * 256], dr['w_in'][l][:, :, 1792 + pc_ * 256:1792 + (pc_ + 1) * 256], writes=['rtw%d' % pc_])
                    brow = sb(st2, "rtbrow", [128, 256], F32)
                    S.dma('sp', brow[:], dr['rows'][l][:, 4352:4608], writes=['rtbrow'])
                    COS = sb(st2, "rtcos", [128, 2048], F32)
                    SIN = sb(st2, "rtsin", [128, 2048], F32)
                    permf = sb(st2, "rtperm", [128, 128], F32)
                    S.dma('sp', COS[:], dr['rcos'], writes=['rtcos'])
                    S.dma('act', SIN[:], dr['rsin'], writes=['rtsin'])
                    S.dma('sp', permf[:], dr['cst'][:, 0, :], writes=['rtperm'])
                    qf = [sb(st2, "rtqf%d" % i, [128, 512], F32) for i in range(2)]
                    t1 = [sb(st2, "rtt1_%d" % i, [128, 512], F32) for i in range(2)]
                    pp = [ps(st2, "rtpp%d" % i, [128, 512], F32) for i in range(2)]
                    pq = [ps(st2, "rtpq%d" % i, [128, 512], F32) for i in range(2)]
                    pt = [ps(st2, "rtpt%d" % i, [128, 512], F32) for i in range(2)]
                    def rtproj(cnt, rc, m, n0, nn):
                        ukeys = uTk[n0 // 128:(n0 + nn) // 128]
                        p_, pk_ = pp[cnt % 2], 'rtpp%d' % (cnt % 2)
                        for jj in range(8):
                            mm(p_[:, 0:nn], wr[:, jj, m * 128:(m + 1) * 128], uT[:, jj, n0:n0 + nn], ['rtw%d' % (m // 2)] + ukeys, [pk_],
                               start=(jj == 0), stop=(jj == 7))
                        yield
                        if m >= 6:
                            act(zs[:, m - 6, n0:n0 + nn], p_[:, 0:nn], AF.Silu, [pk_, 'pvt'], ['rtzs'], bias=pv('bin', 14 + m))
                            return
                        isk = m >= 2
                        j = m % 2
                        dst = (KR if isk else QR)[:, j, n0:n0 + nn]
                        dk = 'rtKR' if isk else 'rtQR'
                        if n0 < 256:
                            if isk:
                                act(dst, p_[:, 0:nn], AF.Identity, [pk_, K_], [dk], bias=tb8[:, j:j + 1], scale=0.125)
                            else:
                                act(dst, p_[:, 0:nn], AF.Identity, [pk_, 'pvt'], [dk], bias=pv('bin', 14 + m))
                            return
                        q_, qk_ = qf[rc % 2], 'rtqf%d' % (rc % 2)
                        a_, ak_ = t1[rc % 2], 'rtt1_%d' % (rc % 2)
                        r_, rk_ = pq[rc % 2], 'rtpq%d' % (rc % 2)
                        if isk:
                            act(q_[:, 0:nn], p_[:, 0:nn], AF.Identity, [pk_, K_], [qk_], bias=tb8[:, j:j + 1], scale=0.125)
                        else:
                            act(q_[:, 0:nn], p_[:, 0:nn], AF.Identity, [pk_, 'pvt'], [qk_], bias=pv('bin', 14 + m))
                        yield
                        mm(r_[:, 0:nn], permf[:], q_[:, 0:nn], ['rtperm', qk_], [rk_])
                        yield
                        tsl = slice(n0 - 256, n0 - 256 + nn)
                        tt('dve', a_[:, 0:nn], r_[:, 0:nn], SIN[:, tsl], ALU.mult, [rk_, 'rtsin'], [ak_])
                        tt('pool', q_[:, 0:nn], q_[:, 0:nn], COS[:, tsl], ALU.mult, [qk_, 'rtcos'], [qk_])
                        yield
                        tt('dve', dst, a_[:, 0:nn], q_[:, 0:nn], ALU.add, [ak_, qk_], [dk])

                    plist = []
                    cnt = 0
                    rc = 0
                    for (n0, nn) in BLOCKS:
                        for m in (0, 1, 2, 3, 6, 7):
                            plist.append((cnt, rc, m, n0, nn))
                            cnt += 1
                            if m < 6 and n0 >= 256:
                                rc += 1
                    run_pipelined((rtproj(*p) for p in plist), 2)
                    for t in range(NTL):
                        p_, pk_ = pt[t % 2], 'rtpt%d' % (t % 2)
                        for jj in range(8):
                            mm(p_[:, 0:256], uT[:, jj, t * 128:(t + 1) * 128], wr[:, jj, 512:768], ['rtw2', uTk[t]], [pk_],
                               start=(jj == 0), stop=(jj == 7))
                        tt('dve', VT[:, t, :], p_[:, 0:256], brow[:], ALU.add, [pk_, 'rtbrow'], ['rtVT'])
                    S.barrier()
                if stop == 'ret_proj':
                    return
                with contextlib.ExitStack() as st2:
                    Sst = [sb(st2, "rtS%d" % d, [128, 2, 64], F32) for d in range(2)]
                    kT = [sb(st2, "rtkT%d" % i, [128, 256], BF16) for i in range(3)]
                    ptr = [ps(st2, "rtptr%d" % i, [128, 8, 128], BF16) for i in range(2)]
                    pU = [ps(st2, "rtpU%d" % i, [128, 512], F32) for i in range(3)]
                    orders = [list(range(NTL)), [1, 0] + list(range(NTL - 1, 1, -1))]
                    for d in range(2):
                        memset('pool', Sst[d][:], 0.0, ['rtS%d' % d])
                    def rtchain(it, step, d):
                        t = orders[d][step]
                        pr, prk = ptr[it % 2], 'rtptr%d' % (it % 2)
                        kt, ktk = kT[it % 3], 'rtkT%d' % (it % 3)
                        pu, puk = pU[it % 3], 'rtpU%d' % (it % 3)
                        puv = pu[:, 0:128].rearrange("p (a b) -> p a b", a=2)
                        for j in range(2):
                            tr(pr[:, j, :], KR[:, j, t * 128:(t + 1) * 128], identb, ['rtKR', 'cstb'], [prk])
                        yield
                        tt('dve', kt[:].rearrange("p (h k) -> p h k", h=4), pr[:, 0:2, :].rearrange("p a (b k) -> p (a b) k", b=2),
                           KDEC[:, d, :].unsqueeze(2).broadcast_to([128, 4, 64]), ALU.mult, [prk, K_], [ktk])
                        yield
                        for h in range(4):
                            hp = (h % 2) * 64
                            mm(puv[hp:hp + 64, h // 2, :], kt[:, h * 64:(h + 1) * 64], VT[:, t, h * 64:(h + 1) * 64], [ktk, 'rtVT'], [puk])
                        yield
                        cp('act', Sall[d][:, :, t, :], Sst[d][:], ['rtS%d' % d], ['rtSall%d_%d' % (d, t)])
                        for j in range(2):
                            stt(Sst[d][:, j, :], Sst[d][:, j, :], GL[:, d * 2 + j:d * 2 + j + 1], puv[:, j, :], ALU.mult, ALU.add,
                                ['rtS%d' % d, K_, puk], ['rtS%d' % d])

                    gbank = mkbanks(st2, 3, "rtgk") if GJ_SPLIT[2] else None
                    gj = gate_jobs(l, last, st2, gbank, GJ_SPLIT[2]) if (GATE_PRE and GJ_SPLIT[2]) else []
                    run_pipelined(interleave([rtchain(i_, sd[0], sd[1]) for i_, sd in enumerate([(s_, d_) for s_ in range(NTL) for d_ in range(2)])], gj, 4), 2)
                    S.barrier()
                if stop == 'ret_chain':
                    return
                with contextlib.ExitStack() as st2:
                    if ('yc%d' % l) in debug:
                        dbgbuf = sb(st2, "dbgbuf", [128, 2, NT], F32)
                    AT = [sb(st2, "rtAT%d" % i, [128, 4, 128], BF16) for i in range(2)]
                    qd = [[sb(st2, "rtqd%d_%d" % (i, d), [128, 2, 128], BF16) for d in range(2)] for i in range(2)]
                    sq = [sb(st2, "rtsq%d" % i, [128, 2, 128], BF16) for i in range(2)]
                    rr = [sb(st2, "rtrr%d" % i, [128, 2, 128], F32) for i in range(2)]
                    ob = [sb(st2, "rtob%d" % i, [128, 2, 128], F32) for i in range(2)]
                    bank = mkbanks(st2, 8, "rtbk")

                    def rtout(t):
                        i2 = t % 2
                        tsl = slice(t * 128, (t + 1) * 128)
                        pas = [bank() for _ in range(2)]
                        for h in range(4):
                            hp = (h % 2) * 64
                            pav = pas[h % 2][0][:, 0:256].rearrange("p (a b) -> p a b", a=2)
                            mm(pav[:, h // 2, :], KR[hp:hp + 64, h // 2, tsl], QR[hp:hp + 64, h // 2, tsl], ['rtKR', 'rtQR'], [pas[h % 2][1]])
                        for d in range(2):
                            tt('pool', qd[i2][d][:], QR[:, :, tsl], QDEC[:, d, :, :], ALU.mult, ['rtQR', K_], ['rtqd%d_%d' % (i2, d)])
                        yield
                        for par in range(2):
                            pav = pas[par][0][:, 0:256].rearrange("p (a b) -> p a b", a=2)
                            tt('dve', AT[i2][:, par::2, :], pav, DS[:, par::2, :], ALU.mult, [pas[par][1], K_], ['rtAT%d' % i2])
                        yield
                        pos = [bank() for _ in range(2)]
                        povs = [pos[par][0][:, 0:256].rearrange("p (a b) -> p a b", a=2) for par in range(2)]
                        for h in range(4):
                            hp = (h % 2) * 64
                            pok = pos[h % 2][1]
                            reg = povs[h % 2][hp:hp + 64, h // 2, :]
                            mm(reg, VT[:, t, h * 64:(h + 1) * 64], AT[i2][:, h, :], ['rtVT', 'rtAT%d' % i2], [pok], start=True, stop=False)
                            for d in range(2):
                                mm(reg, Sall[d][hp:hp + 64, h // 2, t, :], qd[i2][d][hp:hp + 64, h // 2, :],
                                   ['rtSall%d_%d' % (d, t), 'rtqd%d_%d' % (i2, d)], [pok], start=False, stop=(d == 1))
                        yield
                        obk = 'rtob%d' % i2
                        cp('act', ob[i2][0:64], povs[0][0:64], [pos[0][1]], [obk])
                        cp('dve', ob[i2][64:128], povs[1][64:128], [pos[1][1]], [obk])
                        yield
                        pov = ob[i2][:]
                        pok = obk
                        if ('yc%d' % l) in debug:
                            cp('pool', dbgbuf[:, :, tsl], pov, [pok], ['dbgbuf'])
                        act(sq[i2][:], pov, AF.Square, [pok], ['rtsq%d' % i2])
                        yield
                        pss_, psk = bank()
                        psv = pss_[:, 0:256].rearrange("p (a b) -> p a b", a=2)
                        for j in range(2):
                            mm(psv[:, j, :], bonesb, sq[i2][:, j, :], ['cstb', 'rtsq%d' % i2], [psk])
                        yield
                        act(rr[i2][:], psv, AF.Sqrt, [psk], ['rtrr%d' % i2], bias=RMS_EPS, scale=1.0 / 64)
                        yield
                        S.op('dve', lambda e: e.reciprocal(out=rr[i2][:], in_=rr[i2][:]), reads=['rtrr%d' % i2], writes=['rtrr%d' % i2])
                        yield
                        tt('dve', rr[i2][:], pov, rr[i2][:], ALU.mult, [pok, 'rtrr%d' % i2], ['rtrr%d' % i2])
                        yield
                        tt('pool', Y[:, 2, :, tsl], rr[i2][:], zs[:, :, tsl], ALU.mult, ['rtrr%d' % i2, 'rtzs'], ['Y2'])

                    gj = gate_jobs(l, last, st2, bank, GJ_SPLIT[3]) if (GATE_PRE and GJ_SPLIT[3]) else []
                    run_pipelined(interleave([rtout(t) for t in range(NTL) if not (last and t < 2 and not debug)], gj, 3), 5)
                    if ('yc%d' % l) in debug:
                        dbg_dump('yc%d' % l, dbgbuf[:], [128, 2, NT], ['dbgbuf'])
                    S.barrier()
                S.barrier()
        PHASES['ret'] = phase_ret
        def phase_rw(l, h_src, last):
            with contextlib.ExitStack() as st:
                RB = sb(st, "rwRB", [128, 2, NT], BF16)
                KB = sb(st, "rwKB", [128, 2, NT], BF16)
                VB = sb(st, "rwVB", [128, 2, NT], BF16)
                LB = sb(st, "rwLB", [128, NT], BF16)
                zs = sb(st, "rwzs", [128, 2, NT], BF16)
                vT = sb(st, "rwvT", [128, NTL, 256], BF16)
                lw2b = sb(st, "rwlw2", [128, 2, 256], BF16)
                S.dma('pool', lw2b[:], dr['lw2'][l], writes=['rwlw2'])
                oka = sb(st, "rwoka", [128, 2], F32)
                ts('dve', oka[:], pv('ka'), -1.0, 1.0, ALU.mult, ALU.add, ['pvt'], ['rwoka'])
                seen_b, seen_o = set(), set()
                with contextlib.ExitStack() as st2:
                    ww = sb(st2, "rww", [128, 8, 1152], BF16)
                    for pc_ in range(9):
                        S.dma('pool', ww[:, :, pc_ * 128:(pc_ + 1) * 128], dr['w_in'][l][:, :, 2816 + pc_ * 128:2816 + (pc_ + 1) * 128], writes=['rww%d' % pc_])
                    XR = sb(st2, "rwXR", [128, NT + 4], F32)
                    XS = sb(st2, "rwXS", [128, NT], F32)
                    c0 = sb(st2, "rwc0", [128, 7], F32)
                    pp = [ps(st2, "rwpp%d" % i, [128, 512], F32) for i in range(3)]
                    ptr = [ps(st2, "rwptr%d" % i, [128, 8, 128], BF16) for i in range(2)]
                    o_mu, _ = PV['mu']
                    mu0, mu1 = pvt[:, o_mu:o_mu + 7], pvt[:, o_mu + 7:o_mu + 14]
                    tt('dve', c0[:], mu0, mu1, ALU.add, ['pvt'], ['rwc0'])
                    ts('dve', c0[:], c0[:], -1.0, 1.0, ALU.mult, ALU.add, ['rwc0'], ['rwc0'])
                    memset('pool', XR[:], 0.0, ['rwXR'])
                    cnt = 0
                    for m in range(9):
                        for (n0, nn) in BLOCKS:
                            p_, pk_ = pp[cnt % 3], 'rwpp%d' % (cnt % 3)
                            cnt += 1
                            for jj in range(8):
                                mm(p_[:, 0:nn], ww[:, jj, m * 128:(m + 1) * 128], uT[:, jj, n0:n0 + nn],
                                   ['rww%d' % m] + uTk[n0 // 128:(n0 + nn) // 128], [pk_], start=(jj == 0), stop=(jj == 7))
                            if m >= 7:
                                act(zs[:, m - 7, n0:n0 + nn], p_[:, 0:nn], AF.Silu, [pk_, 'pvt'], ['rwzs'], bias=pv('bin', 22 + m))
                            else:
                                xo = 1 if n0 < 256 else 3
                                act(XR[:, n0 + xo:n0 + xo + nn], p_[:, 0:nn], AF.Identity, [pk_, 'pvt'], ['rwXR'], bias=pv('bin', 22 + m))
                        if m >= 7:
                            continue
                        for (b0, ln, o0) in ((1, 256, 0), (259, 2048, 256)):
                            ts('dve', XS[:, o0:o0 + ln], XR[:, b0:b0 + ln], c0[:, m:m + 1], None, ALU.mult, None, ['rwXR', 'rwc0'], ['rwXS'])
                            stt(XS[:, o0:o0 + ln], XR[:, b0 - 1:b0 - 1 + ln], mu0[:, m:m + 1], XS[:, o0:o0 + ln], ALU.mult, ALU.add,
                                ['rwXR', 'pvt', 'rwXS'], ['rwXS'])
                            stt(XS[:, o0:o0 + ln], XR[:, b0 + 1:b0 + 1 + ln], mu1[:, m:m + 1], XS[:, o0:o0 + ln], ALU.mult, ALU.add,
                                ['rwXR', 'pvt', 'rwXS'], ['rwXS'])
                        if m < 6:
                            dstT, dk = [(RB, 'rwRB'), (KB, 'rwKB'), (VB, 'rwVB')][m // 2]
                            cp('act', dstT[:, m % 2, :], XS[:], ['rwXS'], [dk])
                        else:
                            act(LB[0:64, :], XS[0:64, :], AF.Tanh, ['rwXS'], ['rwLB'])
                            cp('pool', LB[64:128, :], XS[64:128, :], ['rwXS'], ['rwLB'])
                    for t in range(NTL):
                        pr, prk = ptr[t % 2], 'rwptr%d' % (t % 2)
                        for j in range(2):
                            tr(pr[:, j, :], VB[:, j, t * 128:(t + 1) * 128], identb, ['rwVB', 'cstb'], [prk])
                        cp('dve' if t % 2 == 0 else 'act', vT[:, t, :], pr[:, 0:2, :].rearrange("p a b -> p (a b)"), [prk], ['rwvT'])
                    S.barrier()
                if stop == 'rw_proj':
                    return
                OS = sb(st, "rwOS", [128, 2, NT], F32)
                with contextlib.ExitStack() as st2:
                    def B(name, shape, dt=BF16):
                        return sb(st2, "rw_" + name, shape, dt), "rw_" + name
                    R64, R64k = B("R64", [128, 256], F32)
                    memset('pool', R64[:], 1.0, [R64k])
                    memset('pool', R64[:, 0:256:64], 0.0, [R64k])
                    LW, LWk = B("LW", [128, 2, 128], F32)
                    SA, SAk = B("SA", [128, 2, 128], F32)
                    LGm, LGk = B("LG", [128, 2, 128], F32)
                    EG, EGk = B("EG", [128, 2, 128], F32)
                    ENG, ENGk = B("ENG", [128, 2, 128], F32)
                    EGM, EGMk = B("EGM", [128, 2, 128], F32)
                    U0, U0k = B("U0", [128, 2, 128], F32)
                    TA, TAk = B("TA", [128, 2, 128], F32)
                    TB_, TBk = B("TB", [128, 2, 128], F32)
                    SQ, SQk = B("SQ", [128, 2, 128])
                    RKD, RKDk = B("RKD", [128, 2, 128])
                    OBt = (None, None)
                    Zst = [B("Z%d" % d, [128, 2, 64], F32) for d in range(2)]
                    BUF = [dict() for _ in range(2)]
                    for d_ in range(2):
                        BUF[d_]['KKN'] = B("KKN_%d" % d_, [128, 2, 128])
                        BUF[d_]['KT'] = B("KT_%d" % d_, [128, 3, 2, 128])
                        BUF[d_]['RT'] = B("RT_%d" % d_, [128, 2, 128])
                        for j_ in range(2):
                            sfx = "_%d_%d" % (d_, j_)
                            SB = dict()
                            SB['TM'] = B("TM" + sfx, [128, 3, 128])
                            for nm_ in ('A1T', 'A2T', 'A3T', 'A4T', 'ALT', 'Tm', 'TTm', 'Xb', 'RHS', 'BYb'):
                                SB[nm_] = B(nm_ + sfx, [128, 2, 128])
                            SB['NY'] = B("NY" + sfx, [128, 2, 64])
                            SB['RH'] = B("RH" + sfx, [128, 128])
                            SB['GTb'] = B("GTb" + sfx, [128, 2, 128])
                            SB['ZLG'] = B("ZLG" + sfx, [128, 2, 64], F32)
                            SB['Z0b'] = B("Z0b" + sfx, [128, 2, 64])
                            BUF[d_][j_] = SB
                        BUF[d_]['GLt'] = B("GLt_%d" % d_, [128, 2, 2], F32)
                    banks = [ps(st2, "rwbank%d" % i, [128, 512], F32) for i in range(8)]
                    bcnt = [0]

                    def bank():
                        i = bcnt[0] % 8
                        bcnt[0] += 1
                        return banks[i], 'rwbank%d' % i
                    for d in range(2):
                        memset('pool', Zst[d][0][:], 0.0, [Zst[d][1], 'rw_Zs_%d_0' % d, 'rw_Zs_%d_1' % d])
                    for d_ in range(2):
                        for j_ in range(2):
                            memset('pool', BUF[d_][j_]['GTb'][0][:], 0.0, [BUF[d_][j_]['GTb'][1]])
                    orders = [list(range(NTL)), [1, 0] + list(range(NTL - 1, 1, -1))]
                    bc3 = lambda ap: ap.unsqueeze(2).broadcast_to([128, 2, 128])
                    def unit(d, t):
                        KKN, KKNk = BUF[d]['KKN']
                        KT, KTk = BUF[d]['KT']
                        RTb, RTk = BUF[d]['RT']
                        GLt, GLk = BUF[d]['GLt']
                        tsl = slice(t * 128, (t + 1) * 128)
                        rev = (d == 1)
                        Z, Zk = Zst[d]
                        plw, plwk = bank()
                        pla, plak = bank()
                        plwv = plw[:, 0:256].rearrange("p (j t) -> p j t", j=2)
                        plav = pla[:, 0:256].rearrange("p (j t) -> p j t", j=2)
                        wb_ = 32 * d
                        for j in range(2):
                            mm(plwv[:, j, :], lw2b[wb_:wb_ + 16, d, j * 128:(j + 1) * 128], LB[wb_:wb_ + 16, tsl], ['rwlw2', 'rwLB'], [plwk])
                        for j in range(2):
                            mm(plav[:, j, :], lw2b[64:96, d, j * 128:(j + 1) * 128], LB[64:96, tsl], ['rwlw2', 'rwLB'], [plak])
                        for j in range(2):
                            act(LW[:, j, :], plwv[:, j, :], AF.Sigmoid, [plwk, 'pvt'], [LWk], bias=pv('w0', d * 2 + j))
                            act(SA[:, j, :], plav[:, j, :], AF.Sigmoid, [plak, 'pvt'], [SAk], bias=pv('a0', d * 2 + j))
                        ts('dve', LW[:], LW[:], -0.6065306597126334, None, ALU.mult, None, [LWk], [LWk])
                        lwf = LW[:].rearrange("p a b -> p (a b)")
                        lgf = LGm[:].rearrange("p a b -> p (a b)")
                        if not rev:
                            S.op('dve', lambda e: e.tensor_tensor_scan(out=lgf, data0=R64[:], data1=lwf, initial=0.0, op0=ALU.mult, op1=ALU.add),
                                 reads=[LWk, R64k], writes=[LGk])
                        else:
                            S.op('dve', lambda e: e.tensor_tensor_scan(out=lgf[:, ::-1], data0=R64[:], data1=lwf[:, ::-1], initial=0.0,
                                                                       op0=ALU.mult, op1=ALU.add), reads=[LWk, R64k], writes=[LGk])
                        act(EG[:], LGm[:], AF.Exp, [LGk], [EGk])
                        act(ENG[:], LGm[:], AF.Exp, [LGk], [ENGk], scale=-1.0)
                        tt('pool', TA[:], LGm[:], LW[:], ALU.subtract, [LGk, LWk], [TAk])
                        act(EGM[:], TA[:], AF.Exp, [TAk], [EGMk])
                        gsrc = EG[:, :, 63::64] if not rev else EG[:, :, 0::64]
                        cp('pool', GLt[:], gsrc, [EGk], [GLk])
                        if stop == 'rw_u1':
                            return
                        tt('dve', TA[:], KB[:, :, tsl], bc3(pv('kk')), ALU.mult, ['rwKB', 'pvt', TAk], [TAk])
                        act(SQ[:], TA[:], AF.Square, [TAk], [SQk])
                        pss_, pssk = bank()
                        pssv = pss_[:, 0:256].rearrange("p (a b) -> p a b", a=2)
                        for j in range(2):
                            mm(pssv[:, j, :], bonesb, SQ[:, j, :], ['cstb', SQk], [pssk])
                        act(TB_[:], pssv, AF.Sqrt, [pssk], [TBk])
                        ts('dve', TB_[:], TB_[:], 1e-12, None, ALU.max, None, [TBk], [TBk])
                        S.op('dve', lambda e: e.reciprocal(out=TB_[:], in_=TB_[:]), reads=[TBk], writes=[TBk])
                        tt('dve', KKN[:], TA[:], TB_[:], ALU.mult, [TAk, TBk], [KKNk])
                        if stop == 'rw_u2':
                            return
                        tt('pool', KT[:, 0], KKN[:], EGM[:], ALU.mult, [KKNk, EGMk], [KTk])
                        tt('dve', TA[:], SA[:], ENG[:], ALU.mult, [SAk, ENGk, TAk], [TAk])
                        tt('pool', KT[:, 1], KKN[:], TA[:], ALU.mult, [KKNk, TAk], [KTk])
                        tt('dve', U0[:], SA[:], bc3(pv('ka')), ALU.mult, [SAk, 'pvt'], [U0k])
                        tt('dve', U0[:], U0[:], bc3(oka[:]), ALU.add, [U0k, 'rwoka'], [U0k])
                        tt('pool', TB_[:], U0[:], ENG[:], ALU.mult, [U0k, ENGk, TBk], [TBk])
                        tt('pool', KT[:, 2], KB[:, :, tsl], TB_[:], ALU.mult, ['rwKB', TBk], [KTk])
                        tt('dve', RTb[:], RB[:, :, tsl], EG[:], ALU.mult, ['rwRB', EGk], [RTk])
                        tt('dve', U0[:], U0[:], KB[:, :, tsl], ALU.mult, [U0k, 'rwKB'], [U0k])
                        tt('dve', U0[:], U0[:], bc3(pv('rk')), ALU.mult, [U0k, 'pvt'], [U0k])
                        tt('pool', RKD[:], U0[:], RB[:, :, tsl], ALU.mult, [U0k, 'rwRB'], [RKDk])
                        pbn, pbnk = bank()
                        pbnv = pbn[:, 0:256].rearrange("p (a b) -> p a b", a=2)
                        for j in range(2):
                            mm(pbnv[:, j, :], bonesb, RKD[:, j, :], ['cstb', RKDk], [pbnk])
                        if t not in seen_b:
                            seen_b.add(t)
                            tt('dve', Y[:, 3, :, tsl], pbnv, VB[:, :, tsl], ALU.mult, [pbnk, 'rwVB'], ['Y3'])
                        else:
                            tt('dve', TA[:], pbnv, VB[:, :, tsl], ALU.mult, [pbnk, 'rwVB', TAk], [TAk])
                            tt('pool', Y[:, 3, :, tsl], Y[:, 3, :, tsl], TA[:], ALU.add, ['Y3', TAk], ['Y3'])
                        if stop == 'rw_u3':
                            return
                        subs = [stream(d, j, t, rev, tsl, KT, KTk, RTb, RTk, GLt, GLk) for j in range(2)]
                        while subs:
                            for g in list(subs):
                                try:
                                    next(g)
                                except StopIteration:
                                    subs.remove(g)
                                yield

                    def stream(d, j, t, rev, tsl, KT, KTk, RTb, RTk, GLt, GLk):
                        SB = BUF[d][j]
                        TM, TMk = SB['TM']
                        A1T, A1k = SB['A1T']
                        A2T, A2k = SB['A2T']
                        A3T, A3k = SB['A3T']
                        A4T, A4k = SB['A4T']
                        ALT, ALk = SB['ALT']
                        Tm, Tmk = SB['Tm']
                        TTm, TTk = SB['TTm']
                        Xb, Xbk = SB['Xb']
                        RHS, RHSk = SB['RHS']
                        BYb, BYk = SB['BYb']
                        NY, NYk = SB['NY']
                        RH, RHk = SB['RH']
                        GTb, GTk = SB['GTb']
                        ZLG, ZLGk = SB['ZLG']
                        Z0b, Z0k = SB['Z0b']
                        Z, _zk = Zst[d]
                        Zk = 'rw_Zs_%d_%d' % (d, j)
                        ptb, ptbk = bank()
                        ptv = ptb[:].bitcast(BF16).rearrange("p (a b) -> p a b", a=8)
                        for x in range(3):
                            tr(ptv[:, x, :], KT[:, x, j, :], identb, [KTk, 'cstb'], [ptbk])
                        yield
                        cp('act', TM[:], ptv[:, 0:3, :], [ptbk], [TMk])
                        yield

                        def amat(dst, dstk, li, ri_src, ri_k, mslot):
                            pas = []
                            for par in range(2):
                                hp = par * 64
                                pa, pak = bank()
                                rhs = (RTb[hp:hp + 64, j, :] if ri_src is None else KT[hp:hp + 64, ri_src, j, :])
                                mm(pa[:, 0:128], KT[hp:hp + 64, li, j, :], rhs, [KTk, ri_k], [pak])
                                pas.append((pa, pak))
                            return pas

                        def aevac(pas, dst, dstk, mslot):
                            for par, (pa, pak) in enumerate(pas):
                                if mslot is None:
                                    cp('act', dst[:, par, :], pa[:, 0:128], [pak], [dstk])
                                else:
                                    tt('dve', dst[:, par, :], pa[:, 0:128], maskb[:, mslot, :], ALU.mult, [pak, 'maskb'], [dstk])
                        for (dst, dstk, li, rs, rk, ms) in ((A1T, A1k, 1, 0, KTk, None), (A2T, A2k, 2, 0, KTk, 2 + d),
                                                            (A3T, A3k, 1, None, RTk, 4 + d), (A4T, A4k, 2, None, RTk, 4 + d)):
                            pas = amat(dst, dstk, li, rs, rk, ms)
                            yield
                            aevac(pas, dst, dstk, ms)
                            yield
                        idb2 = identb.unsqueeze(1).broadcast_to([128, 2, 128])
                        cp('pool', Tm[:], idb2, ['cstb'], [Tmk])
                        cp('pool', TTm[:], idb2, ['cstb'], [TTk])
                        for lv in range(6):
                            tt('pool', ALT[:], A1T[:], maskb[:, 6 + d * 6 + lv, :].unsqueeze(1).broadcast_to([128, 2, 128]), ALU.mult,
                               [A1k, 'maskb'], [ALk])
                            yield
                            px, pxk = bank()
                            pxv = px[:, 0:256].rearrange("p (h t) -> p h t", h=2)
                            for par in range(2):
                                mm(pxv[:, par, :], ALT[:, par, :], Tm[:, par, :], [ALk, Tmk], [pxk])
                            yield
                            cp('act', Xb[:], pxv, [pxk], [Xbk])
                            yield
                            py_, pyk = bank()
                            pyv = py_[:].rearrange("p (x h t) -> p x h t", x=2, h=2)
                            for par in range(2):
                                mm(pyv[:, 0, par, :], Xb[:, par, :], TTm[:, par, :], [Xbk, TTk], [pyk])
                            if lv < 5:
                                for par in range(2):
                                    mm(pyv[:, 1, par, :], TTm[:, par, :], Xb[:, par, :], [Xbk, TTk], [pyk])
                            yield
                            if lv < 5:
                                tt('dve', Tm[:], Tm[:], pyv[:, 1], ALU.subtract, [Tmk, pyk], [Tmk])
                            tt('dve', TTm[:], TTm[:], pyv[:, 0], ALU.subtract, [TTk, pyk], [TTk])
                            yield
                        pw, pwk = bank()
                        pwv = pw[:, 0:128].rearrange("p (h v) -> p h v", h=2)
                        for par in range(2):
                            h = 2 * j + par
                            mm(pwv[:, par, :], A2T[:, par, :], vT[:, t, h * 64:(h + 1) * 64], [A2k, 'rwvT'], [pwk])
                        cp('pool', RHS[:, :, 0:64], TM[:, 0, :].rearrange("p (h k) -> p h k", h=2), [TMk], [RHSk])
                        yield
                        cp('act', RHS[:, :, 64:128], pwv, [pwk], [RHSk])
                        yield
                        pby, pbyk = bank()
                        pbyv = pby[:, 0:256].rearrange("p (h t) -> p h t", h=2)
                        for par in range(2):
                            mm(pbyv[:, par, :], TTm[:, par, :], RHS[:, par, :], [TTk, RHSk], [pbyk])
                        yield
                        cp('act', BYb[:], pbyv, [pbyk], [BYk])
                        yield
                        ts('pool', NY[:], BYb[:, :, 64:128], -1.0, 0.0, ALU.mult, ALU.add, [BYk], [NYk])
                        pr_, prk = bank()
                        for par in range(2):
                            hp = par * 64
                            mm(pr_[hp:hp + 64, 0:128], BYb[:, par, 0:64], A3T[:, par, :], [BYk, A3k], [prk])
                        yield
                        tt('dve', RH[:], RTb[:, j, :], pr_[:, 0:128], ALU.subtract, [RTk, prk], [RHk])
                        yield
                        for c in range(2):
                            cs = slice(c * 64, (c + 1) * 64)
                            pg_, pgk = bank()
                            pgv = pg_[:, 0:128].rearrange("p (x v) -> p x v", x=2)
                            for par in range(2):
                                hp = par * 64
                                h = 2 * j + par
                                hc = slice(h * 64, (h + 1) * 64)
                                pc = slice(par * 64, (par + 1) * 64)
                                mm(pgv[hp:hp + 64, 0, :], BYb[cs, par, 0:64], TM[cs, 1, pc], [BYk, TMk], [pgk])
                                mm(pgv[hp:hp + 64, 1, :], TM[cs, 2, pc], vT[cs, t, hc], [TMk, 'rwvT'], [pgk], start=True, stop=False)
                                mm(pgv[hp:hp + 64, 1, :], TM[cs, 1, pc], NY[cs, par, :], [TMk, NYk], [pgk], start=False, stop=True)
                            yield
                            for par in range(2):
                                hp = par * 64
                                tt('dve', GTb[hp:hp + 64, c, hp:hp + 64], cstf[hp:hp + 64, 4, 0:64], pgv[hp:hp + 64, 0, :], ALU.subtract,
                                   ['cstf', pgk], [GTk])
                            ts('dve', ZLG[:, c, :], pgv[:, 1, :], GLt[:, j, c:c + 1], None, ALU.mult, None, [pgk, GLk], [ZLGk])
                            yield
                        for c in ((0, 1) if not rev else (1, 0)):
                            cp('act', Z0b[:, c, :], Z[:, j, :], [Zk], [Z0k])
                            yield
                            pn, pnk = bank()
                            mm(pn[:, 0:64], GTb[:, c, :], Z0b[:, c, :], [GTk, Z0k], [pnk])
                            yield
                            stt(Z[:, j, :], pn[:, 0:64], GLt[:, j, c:c + 1], ZLG[:, c, :], ALU.mult, ALU.add, [pnk, GLk, ZLGk, Zk], [Zk])
                            yield
                        for par in range(2):
                            hp = par * 64
                            h = 2 * j + par
                            hc = slice(h * 64, (h + 1) * 64)
                            po_, pok = bank()
                            reg = po_[hp:hp + 64, 0:128]
                            mm(reg, vT[:, t, hc], A4T[:, par, :], ['rwvT', A4k], [pok], start=True, stop=False)
                            mm(reg, NY[:, par, :], A3T[:, par, :], [NYk, A3k], [pok], start=False, stop=False)
                            for c in range(2):
                                mm(reg[:, c * 64:(c + 1) * 64], Z0b[hp:hp + 64, c, :], RH[hp:hp + 64, c * 64:(c + 1) * 64],
                                   [Z0k, RHk], [pok], start=False, stop=(c == 1))
                            yield
                            osl = OS[hp:hp + 64, j, tsl]
                            osk = 'rwOS%d_%d' % (t, j)
                            if (t, j, par) not in seen_o:
                                seen_o.add((t, j, par))
                                cp('dve' if par == 0 else 'act', osl, reg, [pok], [osk])
                            else:
                                tt('dve', osl, osl, reg, ALU.add, [pok, osk], [osk])
                            yield

                    for step in range(NTL):
                        if stop is not None and stop.startswith('rw_u') and step >= 1:
                            break
                        gens = [unit(d, orders[d][step]) for d in range(2)]
                        while gens:
                            for g in list(gens):
                                try:
                                    next(g)
                                except StopIteration:
                                    gens.remove(g)
                    S.barrier()
                if stop is not None and stop.startswith('rw_'):
                    return
                with contextlib.ExitStack() as st2:
                    ob = [sb(st2, "rwob%d" % i, [128, 2, 128], BF16) for i in range(2)]
                    cen = [sb(st2, "rwcen%d" % i, [128, 2, 128], F32) for i in range(2)]
                    rs = [sb(st2, "rwrs%d" % i, [128, 2, 128], F32) for i in range(2)]
                    pm_ = [ps(st2, "rwpm%d" % i, [128, 512], F32) for i in range(2)]
                    pv_ = [ps(st2, "rwpv%d" % i, [128, 512], F32) for i in range(2)]
                    for t in range(NTL):
                        i2 = t % 2
                        tsl = slice(t * 128, (t + 1) * 128)
                        osk = 'rwOS%d_0' % t
                        osk1 = 'rwOS%d_1' % t
                        cp('act', ob[i2][:], OS[:, :, tsl], [osk, osk1], ['rwob%d' % i2])
                        pmv = pm_[i2][:, 0:256].rearrange("p (a b) -> p a b", a=2)
                        for j in range(2):
                            mm(pmv[:, j, :], bonesb, ob[i2][:, j, :], ['cstb', 'rwob%d' % i2], ['rwpm%d' % i2])
                        stt(cen[i2][:], pmv, -1.0 / 64, OS[:, :, tsl], ALU.mult, ALU.add, ['rwpm%d' % i2, osk, osk1], ['rwcen%d' % i2])
                        act(ob[i2][:], cen[i2][:], AF.Square, ['rwcen%d' % i2], ['rwob%d' % i2])
                        pvv = pv_[i2][:, 0:256].rearrange("p (a b) -> p a b", a=2)
                        for j in range(2):
                            mm(pvv[:, j, :], bonesb, ob[i2][:, j, :], ['cstb', 'rwob%d' % i2], ['rwpv%d' % i2])
                        act(rs[i2][:], pvv, AF.Sqrt, ['rwpv%d' % i2], ['rwrs%d' % i2], bias=RW_GN_EPS, scale=1.0 / 64)
                        S.op('dve', lambda e: e.reciprocal(out=rs[i2][:], in_=rs[i2][:]), reads=['rwrs%d' % i2], writes=['rwrs%d' % i2])
                        tt('dve', cen[i2][:], cen[i2][:], rs[i2][:], ALU.mult, ['rwcen%d' % i2, 'rwrs%d' % i2], ['rwcen%d' % i2])
                        tt('pool', cen[i2][:], cen[i2][:], bc3(pv('gnw')), ALU.mult, ['rwcen%d' % i2, 'pvt'], ['rwcen%d' % i2])
                        tt('pool', cen[i2][:], cen[i2][:], bc3(pv('gnb')), ALU.add, ['rwcen%d' % i2, 'pvt'], ['rwcen%d' % i2])
                        tt('dve', cen[i2][:], cen[i2][:], Y[:, 3, :, tsl], ALU.add, ['rwcen%d' % i2, 'Y3'], ['rwcen%d' % i2])
                        if ('yd%d' % l) in debug:
                            cp('act', OS[:, :, tsl], cen[i2][:], ['rwcen%d' % i2], [osk, osk1])
                        tt('dve', Y[:, 3, :, tsl], cen[i2][:], zs[:, :, tsl], ALU.mult, ['rwcen%d' % i2, 'rwzs'], ['Y3'])
                    if ('yd%d' % l) in debug:
                        dbg_dump('yd%d' % l, OS[:], [128, 2, NT], ['rwOS%d_%d' % (t, j_) for t in range(NTL) for j_ in range(2)])
                    S.barrier()
                S.barrier()
        PHASES['rw'] = phase_rw
        def phase_merge(l, h_src, last):
            h_dst = out_d if last else h1_d
            with contextlib.ExitStack() as st:
                MG = sb(st, "mgMG", [128, 8, NT], BF16)
                wbr = sb(st, "mgwbr", [128, 4, 2, DM], BF16)
                S.dma('pool', wbr[:], dr['wbr'][l], writes=['mgwbr'])
                with contextlib.ExitStack() as st2:
                    wg = [sb(st2, "mgwg%d" % i, [128, 8, 4, 128], BF16) for i in range(2)]
                    sg = [sb(st2, "mgsg%d" % i, [128, 512], BF16) for i in range(3)]
                    ac = [sb(st2, "mgac%d" % i, [128, 512], F32) for i in range(2)]
                    tm = [sb(st2, "mgtm%d" % i, [128, 512], F32) for i in range(2)]
                    pgl = [ps(st2, "mgpg%d" % i, [128, 512], F32) for i in range(3)]
                    pbr = [ps(st2, "mgpb%d" % i, [128, 512], F32) for i in range(3)]
                    cg = 0
                    ca = 0
                    def load_wg(dt_):
                        for k in range(4):
                            if (l, k * 8 + dt_) in pre_sg:
                                continue
                            c0 = 3968 + k * 1024 + dt_ * 128
                            S.dma('pool', wg[dt_ % 2][:, :, k, :], dr['w_in'][l][:, :, c0:c0 + 128], writes=['mgwg%d' % (dt_ % 2)])
                    load_wg(0)
                    for dt_ in range(8):
                        w_, wk_ = wg[dt_ % 2], 'mgwg%d' % (dt_ % 2)
                        if dt_ + 1 < 8:
                            load_wg(dt_ + 1)
                        for (n0, nn) in BLOCKS:
                            if last and n0 < 256:
                                continue
                            a_, ak_ = ac[ca % 2], 'mgac%d' % (ca % 2)
                            t_, tk_ = tm[ca % 2], 'mgtm%d' % (ca % 2)
                            ca += 1
                            for k in range(4):
                                pg_, pgk_ = pgl[cg % 3], 'mgpg%d' % (cg % 3)
                                pb_, pbk_ = pbr[cg % 3], 'mgpb%d' % (cg % 3)
                                s_, sk_ = sg[cg % 3], 'mgsg%d' % (cg % 3)
                                cg += 1
                                if (l, k * 8 + dt_) in pre_sg:
                                    S.dma('sp' if cg % 2 == 0 else 'act', s_[:, 0:nn], sgd[k * 8 + dt_][:, n0:n0 + nn], reads=['sgd'], writes=[sk_])
                                else:
                                    for jj in range(8):
                                        mm(pg_[:, 0:nn], w_[:, jj, k, :], uT[:, jj, n0:n0 + nn], [wk_] + uTk[n0 // 128:(n0 + nn) // 128], [pgk_],
                                           start=(jj == 0), stop=(jj == 7))
                                    act(s_[:, 0:nn], pg_[:, 0:nn], AF.Sigmoid, [pgk_, 'pvt'], [sk_], bias=pv('bin', 31 + k * 8 + dt_))
                                for jc in range(2):
                                    mm(pb_[:, 0:nn], wbr[:, k, jc, dt_ * 128:(dt_ + 1) * 128], Y[:, k, jc, n0:n0 + nn], ['mgwbr', 'Y%d' % k], [pbk_],
                                       start=(jc == 0), stop=(jc == 1))
                                if k == 0:
                                    tt('dve', a_[:, 0:nn], pb_[:, 0:nn], s_[:, 0:nn], ALU.mult, [pbk_, sk_], [ak_])
                                else:
                                    tt('dve', t_[:, 0:nn], pb_[:, 0:nn], s_[:, 0:nn], ALU.mult, [pbk_, sk_], [tk_])
                                    if k < 3:
                                        tt('pool', a_[:, 0:nn], a_[:, 0:nn], t_[:, 0:nn], ALU.add, [ak_, tk_], [ak_])
                                    else:
                                        tt('pool', MG[:, dt_, n0:n0 + nn], a_[:, 0:nn], t_[:, 0:nn], ALU.add, [ak_, tk_], ['mgMG%d' % (n0 // 512 if n0 else 9)])
                    S.barrier()
                if ('merged%d' % l) in debug:
                    with contextlib.ExitStack() as st2:
                        mf = sb(st2, "mgf", [128, 8, NT], F32)
                        cp('dve', mf[:], MG[:], ['mgMG%d' % i for i in (9, 0, 1, 2, 3)], ['mgf'])
                        dbg_dump('merged%d' % l, mf[:], [128, 8, NT], ['mgf'])
                        S.barrier()
                with contextlib.ExitStack() as st2:
                    wo = sb(st2, "mgwo", [128, 8, DM], BF16)
                    S.dma('pool', wo[:], dr['wout'][l], writes=['mgwo'])
                    rows = sb(st2, "mgrows", [128, 3, DM], F32)
                    S.dma('sp', rows[:], dr['rows'][l][:, 0:3072].rearrange("p (a b) -> p a b", a=3), writes=['mgrows'])
                    hin_ = [sb(st2, "mghin%d" % i, [128, DM], F32) for i in range(2)]
                    ot = [sb(st2, "mgot%d" % i, [128, DM], F32) for i in range(2)]
                    stat = [sb(st2, "mgst%d" % i, [128, 16], F32) for i in range(2)]
                    po = [[ps(st2, "mgpo%d_%d" % (i, hh), [128, 512], F32) for hh in range(2)] for i in range(2)]
                    def mgout(it, t):
                        i2 = it % 2
                        ci = 1 if t < 2 else 0
                        tsl = slice(t * 128, (t + 1) * 128)
                        mgk = 'mgMG%d' % (9 if t < 2 else (t - 2) // 4)
                        hk_, ok_, sk_ = 'mghin%d' % i2, 'mgot%d' % i2, 'mgst%d' % i2
                        hi, o_, sti = hin_[i2], ot[i2], stat[i2]
                        S.dma('sp', hi[:], h_src[t * 128:(t + 1) * 128, :], writes=[hk_])
                        for hh in range(2):
                            pk_ = 'mgpo%d_%d' % (i2, hh)
                            for jj in range(8):
                                mm(po[i2][hh][:], MG[:, jj, tsl], wo[:, jj, hh * 512:(hh + 1) * 512], [mgk, 'mgwo'], [pk_], start=(jj == 0), stop=(jj == 7))
                        yield
                        for hh in range(2):
                            pk_ = 'mgpo%d_%d' % (i2, hh)
                            tt('dve', o_[:, hh * 512:(hh + 1) * 512], po[i2][hh][:], rows[:, 0, hh * 512:(hh + 1) * 512], ALU.add, [pk_, 'mgrows'], [ok_])
                        yield
                        tt('dve', o_[:], o_[:], gatebc[:, ci, :], ALU.mult, [ok_, 'gatebc'], [ok_])
                        yield
                        stt(o_[:], hi[:], ALPHA, o_[:], ALU.mult, ALU.add, [hk_, ok_], [ok_])
                        yield
                        S.op('dve', lambda e: e.bn_stats(out=sti[:, 0:6], in_=o_[:, 0:512]), reads=[ok_], writes=[sk_])
                        S.op('dve', lambda e: e.bn_stats(out=sti[:, 6:12], in_=o_[:, 512:1024]), reads=[ok_], writes=[sk_])
                        yield
                        S.op('dve', lambda e: e.bn_aggr(out=sti[:, 12:14], in_=sti[:, 0:12]), reads=[sk_], writes=[sk_])
                        yield
                        act(sti[:, 14:15], sti[:, 13:14], AF.Sqrt, [sk_], [sk_], bias=LN_EPS)
                        yield
                        S.op('dve', lambda e: e.reciprocal(out=sti[:, 14:15], in_=sti[:, 14:15]), reads=[sk_], writes=[sk_])
                        yield
                        stt(sti[:, 15:16], sti[:, 12:13], -1.0, sti[:, 14:15], ALU.mult, ALU.mult, [sk_], [sk_])
                        yield
                        act(o_[:], o_[:], AF.Identity, [ok_, sk_], [ok_], bias=sti[:, 15:16], scale=sti[:, 14:15])
                        yield
                        tt('pool', o_[:, 0:512], o_[:, 0:512], rows[:, 1, 0:512], ALU.mult, [ok_, 'mgrows'], [ok_])
                        tt('dve', o_[:, 512:1024], o_[:, 512:1024], rows[:, 1, 512:1024], ALU.mult, [ok_, 'mgrows'], [ok_])
                        yield
                        tt('pool', o_[:, 0:512], o_[:, 0:512], rows[:, 2, 0:512], ALU.add, [ok_, 'mgrows'], [ok_])
                        tt('dve', o_[:, 512:1024], o_[:, 512:1024], rows[:, 2, 512:1024], ALU.add, [ok_, 'mgrows'], [ok_])
                        yield
                        if last:
                            S.dma('sp', out_d[(t - 2) * 128:(t - 1) * 128, :], o_[:], reads=[ok_], writes=['outfinal'])
                        else:
                            S.dma('sp', h1_d[t * 128:(t + 1) * 128, :], o_[:], reads=[ok_], writes=['h1'])

                    tl = [t for t in range(NTL) if not (last and t < 2)]
                    run_pipelined((mgout(i_, t) for i_, t in enumerate(tl)), 6)
                    S.barrier()
                S.barrier()
        PHASES['merge'] = phase_merge
        for l in range(nlayers):
            last = (l == nlayers - 1)
            h_src = dr['hin'] if l == 0 else h1_d
            S.dma('sp', pvt[:], dr['pv'][l], writes=['pvt'])
            with contextlib.ExitStack() as st:
                adw = [sb(st, "adw%d" % i, [128, 8, 512], F32) for i in range(2)]
                scb = sb(st, "scb", [128, 2, 8, 128], F32)
                grow = sb(st, "grow", [128, DM], F32)
                pm0 = ps(st, "pm0", [128, 16, 2], F32)
                pg = [ps(st, "pg%d" % i, [128, 512], F32) for i in range(2)]
                for i in range(2):
                    cp('dve', scb[:, i], silc[:, :, i:i + 1].broadcast_to([128, 8, 128]), ['silc'], ['scb'])
                S.dma('sp', grow[:], dr['rows'][l][:, 3072:4096], writes=['grow'])
                for ch in range(6):
                    buf = adw[ch % 2]
                    bk = 'adw%d' % (ch % 2)
                    S.dma('sp' if ch % 2 == 0 else 'act', buf[:], dr['ada_w'][l][:, :, ch * 512:(ch + 1) * 512], writes=[bk])
                    if ch < 4:
                        for mloc in range(4):
                            m = ch * 4 + mloc
                            for j in range(8):
                                mm(pm0[:, m, :], buf[:, j, mloc * 128:(mloc + 1) * 128], silc[:, j, :], [bk, 'silc'],
                                   ['pm0'], start=(j == 0), stop=(j == 7))
                    else:
                        for i in range(2):
                            for j in range(8):
                                mm(pg[i][:], scb[:, i, j, :], buf[:, j, :], [bk, 'scb'], ['pg%d' % i],
                                   start=(j == 0), stop=(j == 7))
                            tt('dve', gatebc[:, i, (ch - 4) * 512:(ch - 3) * 512], pg[i][:],
                               grow[:, (ch - 4) * 512:(ch - 3) * 512], ALU.add, ['pg%d' % i, 'grow'], ['gatebc'])
                tt('dve', modfm[:], pm0[:], pv('adab').unsqueeze(2).broadcast_to([128, 16, 2]), ALU.add,
                   ['pm0', 'pvt'], ['modfm'])
                ts('dve', modfm[:, 8:16, :], modfm[:, 8:16, :], 1.0, None, ALU.add, None, ['modfm'], ['modfm'])
                dbg_dump('modfm%d' % l, modfm[:], [128, 16, 2], ['modfm'])
                dbg_dump('gatebc%d' % l, gatebc[:], [128, 2, DM], ['gatebc'])
                S.barrier()
            with contextlib.ExitStack() as st:
                xin = [sb(st, "xin%d" % i, [128, DM], F32) for i in range(3)]
                xn = [sb(st, "xn%d" % i, [128, DM], BF16) for i in range(2)]
                stat = [sb(st, "stat%d" % i, [128, 16], F32) for i in range(3)]
                ptr = [ps(st, "ptr%d" % i, [128, 8, 128], BF16) for i in range(2)]
                def p1tile(t):
                    xi, xk = xin[t % 3], 'xin%d' % (t % 3)
                    sti, sk = stat[t % 3], 'stat%d' % (t % 3)
                    xo, xok = xn[t % 2], 'xn%d' % (t % 2)
                    pt, ptk = ptr[t % 2], 'ptr%d' % (t % 2)
                    ci = 1 if t < 2 else 0
                    S.dma('sp' if t % 2 == 0 else 'act', xi[:], h_src[t * 128:(t + 1) * 128, :], writes=[xk])
                    yield
                    S.op('dve', lambda e: e.bn_stats(out=sti[:, 0:6], in_=xi[:, 0:512]), reads=[xk], writes=[sk])
                    S.op('dve', lambda e: e.bn_stats(out=sti[:, 6:12], in_=xi[:, 512:1024]), reads=[xk], writes=[sk])
                    yield
                    S.op('dve', lambda e: e.bn_aggr(out=sti[:, 12:14], in_=sti[:, 0:12]), reads=[sk], writes=[sk])
                    yield
                    act(sti[:, 14:15], sti[:, 13:14], AF.Sqrt, [sk], [sk], bias=LN_EPS)
                    yield
                    S.op('dve', lambda e: e.reciprocal(out=sti[:, 14:15], in_=sti[:, 14:15]), reads=[sk], writes=[sk])
                    yield
                    stt(sti[:, 15:16], sti[:, 12:13], -1.0, sti[:, 14:15], ALU.mult, ALU.mult, [sk], [sk])
                    yield
                    act(xo[:], xi[:], AF.Identity, [xk, sk], [xok], bias=sti[:, 15:16], scale=sti[:, 14:15])
                    yield
                    for j in range(8):
                        tr(pt[:, j, :], xo[:, j * 128:(j + 1) * 128], identb, [xok, 'cstb'], [ptk])
                    yield
                    for j in range(8):
                        if j % 2 == 0:
                            act(uT[:, j, t * 128:(t + 1) * 128], pt[:, j, :], AF.Identity, [ptk, 'modfm'], ['uT%d' % t],
                                bias=modfm[:, j, ci:ci + 1], scale=modfm[:, 8 + j, ci:ci + 1])
                        else:
                            ts('dve', uT[:, j, t * 128:(t + 1) * 128], pt[:, j, :], modfm[:, 8 + j, ci:ci + 1],
                               modfm[:, j, ci:ci + 1], ALU.mult, ALU.add, [ptk, 'modfm'], ['uT%d' % t])

                run_pipelined((p1tile(t) for t in range(NTL)), 4)
                if ('uT%d' % l) in debug:
                    utf = sb(st, "utf", [128, 8, NT], F32)
                    cp('dve', utf[:], uT[:], ['uT%d' % t for t in range(NTL)], ['utf'])
                    dbg_dump('uT%d' % l, utf[:], [128, 8, NT], ['utf'])
                S.barrier()
            uTk = ['uT%d' % t for t in range(NTL)]

            for ph in list(PHASES):
                if ph in phases:
                    PHASES[ph](l, h_src, last)
            if ('h%d' % l) in debug and not last:
                d_ = dbg_out('h%d' % l, [NT, DM])
                S.dma('sp', d_, h1_d, writes=['dbgout_h%d' % l])
                S.barrier()
            if ('Y%d' % l) in debug:
                with contextlib.ExitStack() as st:
                    yf = sb(st, "yf", [128, 4, 2, NT], F32)
                    cp('dve', yf[:], Y[:], ['Y0', 'Y1', 'Y2', 'Y3'], ['yf'])
                    dbg_dump('Y%d' % l, yf[:], [128, 4, 2, NT], ['yf'])
                    S.barrier()

        S.final_wait('sp', ['outfinal'] + ['dbgout_' + n for n in dbg_d])
    if MEMDBG:
        print('SBUF min remaining by prefix:', minrem)
    return nc, dbg_d


def kernel(**inputs):
    inp = {k: np.asarray(v) for k, v in inputs.items()}
    sh = prep_shared(inp)
    nc, _ = build()
    in_maps = []
    for b in range(8):
        m = dict(sh)
        m.update(prep_core(inp, b))
        in_maps.append(m)
    res = run_bass_kernel_spmd(nc, in_maps, core_ids=list(range(8)))
    return np.stack([np.asarray(res.results[b]['out'], dtype=np.float32) for b in range(8)], 0)
```

```python
import contextlib
import numpy as np
import concourse.bass as bass
import concourse.mybir as mybir
from concourse.bass_utils import run_bass_kernel_spmd

F32 = mybir.dt.float32
BF16 = mybir.dt.bfloat16
AF = mybir.ActivationFunctionType
ALU = mybir.AluOpType
AX = mybir.AxisListType

NT = 2304
NTL = 18
DM = 1024
NCOL = 8064
BLOCKS = [(0, 256), (256, 512), (768, 512), (1280, 512), (1792, 512)]
LN_EPS = 1e-5
RMS_EPS = 1e-6
RW_GN_EPS = 64e-5
ALPHA = (2 * 2) ** 0.25
PI = float(np.pi)
MEMDBG = False
GATE_PRE = False
S5_STAGGER = 6
GJ_SPLIT = [[], [], [], []]
GJ_S5 = list(range(32))


class Sched:
    NDMA = 16

    def __init__(self, nc, same_engine_waits=True):
        self.nc = nc
        self.same = same_engine_waits
        self.eng = dict(pe=nc.tensor, act=nc.scalar, dve=nc.vector, pool=nc.gpsimd, sp=nc.sync)
        self.E = {n: dict(cnt=0, known={}) for n in self.eng}
        self.dq = {'sp': ['dsp%d' % i for i in range(8)], 'act': ['dac%d' % i for i in range(4)],
                   'pool': ['dpl%d' % i for i in range(8)]}
        self.dmas = {n: dict(cnt=0) for q in self.dq.values() for n in q}
        self.dma_rr = {'sp': 0, 'act': 0, 'pool': 0}
        self.lastw = {}
        self.readers = {}
        self.sems = None
        self.nins = 0

    def sem_names(self):
        return list(self.E.keys()) + list(self.dmas.keys())

    def _deps(self, reads, writes):
        deps = {}

        def add(w):
            if w is not None:
                deps[w[0]] = max(deps.get(w[0], 0), w[1])
        for k in reads:
            add(self.lastw.get(k))
        for k in writes:
            add(self.lastw.get(k))
            for r in self.readers.get(k, ()):
                add(r)
        return deps

    def _waits(self, en, deps):
        E = self.E[en]
        waits = []
        for d, v in deps.items():
            if d == en and (en == 'pe' or not self.same):
                continue
            if E['known'].get(d, 0) < v:
                waits.append((d, v))
                E['known'][d] = v
        return waits

    def _record(self, ident, reads, writes):
        for k in writes:
            self.lastw[k] = ident
            self.readers[k] = []
        for k in reads:
            self.readers.setdefault(k, []).append(ident)

    def _emit(self, en, waits, fn, inc):
        eng = self.eng[en]
        for d, v in waits:
            eng.wait_ge(self.sems[d], v)
        if fn is not None:
            fn(eng).then_inc(self.sems[inc[0]], inc[1])
            self.nins += 1

    def op(self, en, fn, reads=(), writes=()):
        E = self.E[en]
        waits = self._waits(en, self._deps(reads, writes))
        E['cnt'] += 1
        self._emit(en, waits, fn, (en, 1))
        self._record((en, E['cnt']), reads, writes)

    def dma(self, en, out, in_, reads=(), writes=(), **kw):
        dn = self.dq[en][self.dma_rr[en]]
        self.dma_rr[en] = (self.dma_rr[en] + 1) % len(self.dq[en])
        Dq = self.dmas[dn]
        deps = self._deps(reads, writes)
        if Dq['cnt'] > 0:
            deps[dn] = max(deps.get(dn, 0), Dq['cnt'])
        waits = self._waits(en, deps)
        Dq['cnt'] += 16
        self._emit(en, waits, (lambda e: e.dma_start(out=out, in_=in_, **kw)), (dn, 16))
        self._record((dn, Dq['cnt']), reads, writes)

    def barrier(self):
        cur = {n: self.E[n]['cnt'] for n in self.E}
        cur.update({n: self.dmas[n]['cnt'] for n in self.dmas})
        for en in self.E:
            waits = self._waits(en, {d: v for d, v in cur.items() if v > 0})
            self._emit(en, waits, None, None)

    def final_wait(self, en, keys):
        self._emit(en, self._waits(en, self._deps(keys, ())), None, None)


PV = {}


def _pv_layout():
    off = 0
    for name, n in [('bin', 63), ('s5d', 2), ('glub', 2), ('hglb', 8), ('hgnw', 2), ('rdec', 4), ('mu', 14),
                    ('w0', 4), ('a0', 4), ('kk', 2), ('ka', 2), ('rk', 2), ('gnw', 2), ('gnb', 2), ('adab', 16),
                    ('lamre', 16), ('lamim', 16), ('ldt', 16), ('rdech', 8)]:
        PV[name] = (off, n)
        off += n
    return off


NPV = _pv_layout()


def _colmap():
    cm = list(range(0, 3584))
    lora = [-1] * 128
    for r in range(16):
        lora[r] = 3584 + r
        lora[32 + r] = 3600 + r
        lora[64 + r] = 3616 + r
        lora[80 + r] = 3632 + r
    cm += lora
    cm += list(range(3648, 3904))
    cm += list(range(3904, 8000))
    return np.array(cm)


CMAP = _colmap()


def _fm(v):
    return np.ascontiguousarray(v.reshape(-1, 128).T)


def _masks():
    t = np.arange(128)
    s_, t_ = t[:, None], t[None, :]
    m = []
    b32 = (s_ // 32) == (t_ // 32)
    b64 = (s_ // 64) == (t_ // 64)
    m.append(b32 & (t_ >= s_))
    m.append(b32 & (t_ <= s_))
    m.append(b64 & (t_ > s_))
    m.append(b64 & (t_ < s_))
    m.append(b64 & (t_ >= s_))
    m.append(b64 & (t_ <= s_))
    for d in range(2):
        for lv in range(6):
            sz = 1 << lv
            blk = (s_ // (2 * sz)) == (t_ // (2 * sz))
            hs, ht = (s_ // sz) % 2, (t_ // sz) % 2
            if d == 0:
                m.append(blk & (ht == 1) & (hs == 0))
            else:
                m.append(blk & (ht == 0) & (hs == 1))
    return np.stack([x.astype(np.float32) for x in m], 1)


def _rot_tables():
    n = 16
    freqs = 10000.0 ** (-np.arange(n, dtype=np.float32) / n)
    tt = np.arange(2048)
    rows = (tt // 64).astype(np.float32)
    cols = (tt % 64).astype(np.float32)
    cos = np.zeros((128, 2048), np.float32)
    sins = np.zeros((128, 2048), np.float32)
    pm = np.zeros((128, 128), np.float32)
    for p in range(128):
        i = p % 64
        pos = rows if i < 32 else cols
        ii = i % 32
        ang = pos * freqs[ii % 16]
        cos[p] = np.cos(ang)
        if ii < 16:
            sins[p] = -np.sin(ang)
            partner = p + 16
        else:
            sins[p] = np.sin(ang)
            partner = p - 16
        pm[partner, p] = 1.0
    return cos, sins, pm


def prep_shared(inp):
    sh = {}
    L = 2
    w_in = inp['w_in']
    wn = np.zeros((L, 1024, NCOL), np.float32)
    valid = CMAP >= 0
    wn[:, :, valid] = w_in[:, :, CMAP[valid]]
    sh['w_in'] = np.ascontiguousarray(wn.reshape(L, 8, 128, NCOL).transpose(0, 2, 1, 3))
    bn = np.zeros((L, NCOL), np.float32)
    bn[:, valid] = inp['b_in'][:, CMAP[valid]]
    sh['ada_w'] = np.ascontiguousarray(inp['ada_w'].reshape(L, 8, 128, 3072).transpose(0, 2, 1, 3))
    pv = np.zeros((L, 128, NPV), np.float32)

    def put(l, name, arr):
        o, n = PV[name]
        assert arr.shape == (128, n), (name, arr.shape)
        pv[l, :, o:o + n] = arr
    for l in range(L):
        put(l, 'bin', _fm(bn[l]))
        put(l, 's5d', _fm(inp['s5_d'][l]))
        put(l, 'glub', _fm(inp['s5_glu_b'][l]))
        put(l, 'hglb', np.concatenate([_fm(inp['hg_lb'][ll, d]) for ll in range(2) for d in range(2)], 1))
        put(l, 'hgnw', _fm(inp['hg_norm_w'][l]))
        rd = np.zeros((128, 4), np.float32)
        for d in range(2):
            for j in range(2):
                rd[:64, d * 2 + j] = inp['ret_decay'][l, d, 2 * j]
                rd[64:, d * 2 + j] = inp['ret_decay'][l, d, 2 * j + 1]
        put(l, 'rdec', rd)
        put(l, 'rdech', np.ascontiguousarray(np.broadcast_to(inp['ret_decay'][l].reshape(1, 8), (128, 8))))
        mu = np.zeros((2, 7 * 128), np.float32)
        mu[:, :768] = inp['rw_mu'][l][:, :768]
        lv = CMAP[3584:3712]
        ok = lv >= 0
        mu[:, 768:896][:, ok] = inp['rw_mu'][l][:, lv[ok] - 2816]
        put(l, 'mu', np.concatenate([_fm(mu[0]), _fm(mu[1])], 1))
        put(l, 'w0', np.concatenate([_fm(inp['rw_w0'][l, d]) for d in range(2)], 1))
        put(l, 'a0', np.concatenate([_fm(inp['rw_a0'][l, d]) for d in range(2)], 1))
        for nm, key in [('kk', 'rw_kk'), ('ka', 'rw_ka'), ('rk', 'rw_rk'), ('gnw', 'rw_gn_w'), ('gnb', 'rw_gn_b')]:
            put(l, nm, _fm(inp[key][l]))
        put(l, 'adab', _fm(inp['ada_b'][l][:2048]))
        for nm, key in [('lamre', 's5_lam_re'), ('lamim', 's5_lam_im')]:
            a = inp[key][l].reshape(2, 8, 2, 64)
            put(l, nm, np.ascontiguousarray(a.transpose(2, 3, 0, 1).reshape(128, 16)))
        a = np.broadcast_to(inp['s5_log_dt'][l].reshape(2, 8, 2, 1), (2, 8, 2, 64))
        put(l, 'ldt', np.ascontiguousarray(a.transpose(2, 3, 0, 1).reshape(128, 16)))
    sh['pv'] = pv
    bt = np.zeros((L, 128, 2, 4, 2, 128), np.float32)
    ct = np.zeros((L, 128, 8, 2, 128), np.float32)
    for l in range(L):
        for g in range(16):
            i, g2 = g // 2, g % 2
            for q in range(16):
                c = g * 16 + q
                j, p = c // 128, c % 128
                bt[l, p, j, i % 4, 0, g2 * 64:(g2 + 1) * 64] = inp['s5_b_re'][l, g, :, q]
                bt[l, p, j, i % 4, 1, g2 * 64:(g2 + 1) * 64] = inp['s5_b_im'][l, g, :, q]
            m0 = (i % 4) * 32 + g2 * 16
            ct[l, g2 * 64:(g2 + 1) * 64, i, 0, m0:m0 + 16] = inp['s5_c_re'][l, g].T
            ct[l, g2 * 64:(g2 + 1) * 64, i, 1, m0:m0 + 16] = inp['s5_c_im'][l, g].T
    sh['s5bt'] = bt
    sh['s5ct'] = ct
    sh['gluw'] = np.ascontiguousarray(inp['s5_glu_w'].reshape(L, 2, 128, 256).transpose(0, 2, 1, 3))
    lw2 = np.zeros((L, 128, 2, 256), np.float32)
    for l in range(L):
        lw2[l, 0:16, 0] = inp['rw_w2'][l, 0]
        lw2[l, 32:48, 1] = inp['rw_w2'][l, 1]
        lw2[l, 64:80, 0] = inp['rw_a2'][l, 0]
        lw2[l, 80:96, 1] = inp['rw_a2'][l, 1]
    sh['lw2'] = lw2
    sh['wbr'] = np.ascontiguousarray(inp['w_branch'].reshape(L, 4, 2, 128, 1024).transpose(0, 3, 1, 2, 4))
    sh['wout'] = np.ascontiguousarray(inp['w_out'].reshape(L, 8, 128, 1024).transpose(0, 2, 1, 3))
    rows = np.zeros((L, 128, 4096 + 512), np.float32)
    for l in range(L):
        rows[l, :, 0:1024] = inp['b_out'][l][None]
        rows[l, :, 1024:2048] = inp['ln_w'][l][None]
        rows[l, :, 2048:3072] = inp['ln_b'][l][None]
        rows[l, :, 3072:4096] = inp['ada_b'][l][None, 2048:3072]
        rows[l, :, 4096:4352] = inp['b_in'][l][None, 1280:1536]
        rows[l, :, 4352:4608] = inp['b_in'][l][None, 2304:2560]
    sh['rows'] = rows
    sh['masks'] = _masks()
    cos, sins, pm = _rot_tables()
    sh['rcos'] = cos
    sh['rsin'] = sins
    t = np.arange(128)
    cst = np.zeros((128, 9, 128), np.float32)
    cst[:, 0] = pm
    cst[:, 1] = ((t[:, None] // 64) == (t[None, :] // 64))
    cst[:, 2] = np.maximum(t[None, :] - t[:, None], 0)
    cst[:, 3] = np.maximum(t[:, None] - t[None, :], 0)
    cst[:, 4] = (t[None, :] >= t[:, None])
    cst[:, 5] = (t[None, :] <= t[:, None])
    cst[:, 6, :64] = ((t[:, None] % 64) == np.arange(64)[None, :])
    cst[:, 6, 64:68] = ((t[:, None] // 32) == np.arange(4)[None, :])
    cst[:, 6, 68] = 127 - t
    cst[:, 6, 69] = t
    cst[:, 7] = t[None, :] + 1.0
    cst[:, 8] = 128.0 - t[None, :]
    sh['cst'] = cst
    return sh


def prep_core(inp, b):
    pc = {}
    pc['hin'] = np.ascontiguousarray(np.concatenate([inp['ctx'][b], inp['x'][b]], 0))
    cv = np.stack([inp['c'][b], inp['c_ctx']], -1)
    pc['cvec'] = np.ascontiguousarray(cv.reshape(8, 128, 2).transpose(1, 0, 2))
    return pc


SHAPES = dict(hin=[NT, DM], cvec=[128, 8, 2], w_in=[2, 128, 8, NCOL], ada_w=[2, 128, 8, 3072], pv=[2, 128, NPV],
              s5bt=[2, 128, 2, 4, 2, 128], s5ct=[2, 128, 8, 2, 128], gluw=[2, 128, 2, 256], lw2=[2, 128, 2, 256],
              wbr=[2, 128, 4, 2, 1024], wout=[2, 128, 8, 1024], rows=[2, 128, 4608], masks=[128, 18, 128],
              rcos=[128, 2048], rsin=[128, 2048], cst=[128, 9, 128])


def build(debug=(), nlayers=2, phases=('s5', 'hg', 'ret', 'rw', 'merge'), stop=None):
    nc = bass.Bass("TRN2", target_bir_lowering=False)
    S = Sched(nc)
    dr = {k: nc.dram_tensor(k, list(v), F32, kind="ExternalInput").ap() for k, v in SHAPES.items()}
    out_d = nc.dram_tensor("out", [2048, DM], F32, kind="ExternalOutput").ap()
    h1_d = nc.dram_tensor("h1", [NT, DM], F32, kind="Internal").ap()
    sgd = nc.dram_tensor("sgd", [32, 128, NT], BF16, kind="Internal").ap()
    pre_sg = set()
    dbg_d = {}

    def dbg_out(name, shape):
        dbg_d[name] = nc.dram_tensor("dbg_" + name, list(shape), F32, kind="ExternalOutput").ap()
        return dbg_d[name]

    uid = [0]

    def key(p='k'):
        uid[0] += 1
        return '%s%d' % (p, uid[0])

    with contextlib.ExitStack() as top:
        S.sems = {n: top.enter_context(nc.semaphore(n)) for n in S.sem_names()}

        minrem = {}

        def sb(st, name, shape, dt=F32):
            uid[0] += 1
            t_ = st.enter_context(nc.sbuf_tensor("%s_%d" % (name, uid[0]), list(shape), dt))
            if MEMDBG:
                pre = name[:2]
                minrem[pre] = min(minrem.get(pre, 1 << 30), nc.sbuf_bytes_remaining)
            return t_

        def ps(st, name, shape, dt=F32):
            uid[0] += 1
            return st.enter_context(nc.psum_tensor("%s_%d" % (name, uid[0]), list(shape), dt))

        def mm(out, lhsT, rhs, r, w, start=True, stop=True):
            S.op('pe', lambda e: e.matmul(out, lhsT=lhsT, rhs=rhs, start=start, stop=stop), reads=r, writes=w)

        def tr(out, in_, ident, r, w):
            S.op('pe', lambda e: e.transpose(out, in_, ident), reads=r, writes=w)

        def act(out, in_, func, r, w, bias=0.0, scale=1.0):
            S.op('act', lambda e: e.activation(out=out, in_=in_, func=func, bias=bias, scale=scale), reads=r, writes=w)

        def tt(en, out, in0, in1, op, r, w):
            S.op(en, lambda e: e.tensor_tensor(out=out, in0=in0, in1=in1, op=op), reads=r, writes=w)

        def ts(en, out, in0, s1, s2, op0, op1, r, w):
            if s2 is None:
                S.op(en, lambda e: e.tensor_scalar(out=out, in0=in0, scalar1=s1, scalar2=None, op0=op0), reads=r, writes=w)
            else:
                S.op(en, lambda e: e.tensor_scalar(out=out, in0=in0, scalar1=s1, scalar2=s2, op0=op0, op1=op1),
                     reads=r, writes=w)

        def stt(out, in0, sc, in1, op0, op1, r, w):
            S.op('dve', lambda e: e.scalar_tensor_tensor(out=out, in0=in0, scalar=sc, in1=in1, op0=op0, op1=op1),
                 reads=r, writes=w)

        def cp(en, out, in_, r, w):
            if en == 'act':
                S.op('act', lambda e: e.copy(out=out, in_=in_), reads=r, writes=w)
            else:
                S.op(en, lambda e: e.tensor_copy(out=out, in_=in_), reads=r, writes=w)

        def memset(en, ap, val, w):
            S.op(en, lambda e: e.memset(ap, val), writes=w)

        def run_pipelined(gens, stagger):
            it = iter(gens)
            active, pending, rounds = [], True, 0
            while pending or active:
                if pending and rounds % stagger == 0:
                    try:
                        active.append(next(it))
                    except StopIteration:
                        pending = False
                for g in list(active):
                    try:
                        next(g)
                    except StopIteration:
                        active.remove(g)
                rounds += 1

        def mkbanks(st_, n, prefix):
            bl = [ps(st_, "%s%d" % (prefix, i), [128, 512], F32) for i in range(n)]
            cnt = [0]

            def bank():
                i = cnt[0] % n
                cnt[0] += 1
                return bl[i], '%s%d' % (prefix, i)
            return bank

        def gate_jobs(l, last, st_, bankfn, kds):
            wgt = [sb(st_, "gjw%d" % i, [128, 8, 128], BF16) for i in range(2)]
            sgs = [sb(st_, "gjs%d" % i, [128, 512], BF16) for i in range(2)]
            cnt = [0]

            def job(i, kd):
                k, dt_ = kd // 8, kd % 8
                w_, wk_ = wgt[i % 2], 'gjw%d' % (i % 2)
                c0 = 3968 + k * 1024 + dt_ * 128
                S.dma('pool', w_[:], dr['w_in'][l][:, :, c0:c0 + 128], writes=[wk_])
                yield
                for (n0, nn) in BLOCKS:
                    if last and n0 < 256:
                        continue
                    pg_, pgk_ = bankfn()
                    for jj in range(8):
                        mm(pg_[:, 0:nn], w_[:, jj, :], uT[:, jj, n0:n0 + nn], [wk_] + uTk[n0 // 128:(n0 + nn) // 128], [pgk_],
                           start=(jj == 0), stop=(jj == 7))
                    yield
                    c_ = cnt[0] % 2
                    cnt[0] += 1
                    act(sgs[c_][:, 0:nn], pg_[:, 0:nn], AF.Sigmoid, [pgk_, 'pvt'], ['gjs%d' % c_], bias=pv('bin', 31 + k * 8 + dt_))
                    yield
                    S.dma('sp', sgd[kd][:, n0:n0 + nn], sgs[c_][:, 0:nn], reads=['gjs%d' % c_], writes=['sgd'])
                    yield
                pre_sg.add((l, kd))
            return [job(i, kd) for i, kd in enumerate(kds)]

        def interleave(main, extra, every):
            out, ei = [], 0
            extra = list(extra)
            for i, g in enumerate(main):
                out.append(g)
                if (i + 1) % every == 0 and ei < len(extra):
                    out.append(extra[ei])
                    ei += 1
            out.extend(extra[ei:])
            return out

        def dbg_dump(name, ap, shape, r):
            if name in debug:
                d = dbg_out(name, shape)
                S.dma('sp', d, ap, reads=r, writes=['dbgout_' + name])

        cstb = sb(top, "cstb", [128, 3, 128], BF16)
        cstf = sb(top, "cstf", [128, 7, 128], F32)
        maskb = sb(top, "maskb", [128, 18, 128], BF16)
        silc = sb(top, "silc", [128, 8, 2], F32)
        S.dma('pool', cstb[:, 0:2, :], dr['cst'][:, 0:2, :], writes=['cstb'])
        S.dma('sp', cstf[:], dr['cst'][:, 2:9, :], writes=['cstf'])
        S.dma('pool', maskb[:], dr['masks'], writes=['maskb'])
        S.dma('sp', silc[:], dr['cvec'], writes=['silc'])
        memset('pool', cstb[:, 2, :], 0.0, ['cstb'])
        S.op('pool', lambda e: e.affine_select(out=cstb[:, 2, :], in_=cstb[:, 2, :], pattern=[[-1, 128]],
                                               compare_op=ALU.not_equal, fill=1.0, base=0, channel_multiplier=1),
             reads=['cstb'], writes=['cstb'])
        act(silc[:], silc[:], AF.Silu, ['silc'], ['silc'])
        identb = cstb[:, 2, :]
        bonesb = cstb[:, 1, :]

        uT = sb(top, "uT", [128, 8, NT], BF16)
        Y = sb(top, "Y", [128, 4, 2, NT], BF16)
        pvt = sb(top, "pvt", [128, NPV], F32)
        if debug:
            memset('pool', Y[:], 0.0, ['Y0', 'Y1', 'Y2', 'Y3'])
        modfm = sb(top, "modfm", [128, 16, 2], F32)
        gatebc = sb(top, "gatebc", [128, 2, DM], F32)

        def pv(name, j=None, n=1):
            o, cnt = PV[name]
            if j is None:
                return pvt[:, o:o + cnt]
            return pvt[:, o + j:o + j + n]

        PHASES = {}
        def proj_fm(st, wt, wk, mlist, evac, pp, ppk):
            cnt = 0
            for (n0, nn) in BLOCKS:
                for mi, m in enumerate(mlist):
                    p_, pk_ = pp[cnt % len(pp)], ppk[cnt % len(pp)]
                    cnt += 1
                    for j in range(8):
                        mm(p_[:, 0:nn], wt[:, j, m * 128:(m + 1) * 128], uT[:, j, n0:n0 + nn],
                           ['%s%d' % (wk, m // 2)] + uTk[n0 // 128:(n0 + nn) // 128], [pk_], start=(j == 0), stop=(j == 7))
                    evac(mi, m, n0, nn, p_, pk_)

        def phase_s5(l, h_src, last):
            L = 128
            with contextlib.ExitStack() as st:
                btb = sb(st, "btb", [128, 2, 4, 2, 128], BF16)
                ctb = sb(st, "ctb", [128, 8, 2, 128], BF16)
                glub = sb(st, "glub", [128, 2, 256], BF16)
                S.dma('pool', btb[:], dr['s5bt'][l], writes=['btb'])
                S.dma('pool', ctb[:], dr['s5ct'][l], writes=['ctb'])
                S.dma('pool', glub[:], dr['gluw'][l], writes=['glub'])
                ts('pool', ctb[:, :, 1, :], ctb[:, :, 1, :], -1.0, 0.0, ALU.mult, ALU.add, ['ctb'], ['ctb'])
                ub = sb(st, "s5u", [128, 2, NT], BF16)
                zs = sb(st, "s5z", [128, 2, NT], BF16)
                yacc = sb(st, "yacc", [128, 2, NT], F32)
                PT = sb(st, "s5PT", [128, 16, 2, L], F32)
                QT = sb(st, "s5QT", [128, 16, 2, L], F32)
                sst = sb(st, "s5st", [128, 16, 2], F32)
                ones = sb(st, "s5ones", [128, L], F32)
                memset('pool', yacc[:], 0.0, ['yacc'])
                memset('pool', sst[:], 0.0, ['sst'])
                memset('pool', ones[:], 1.0, ['s5ones'])
                with contextlib.ExitStack() as st2:
                    wsu = sb(st2, "wsu", [128, 8, 512], BF16)
                    for pc_ in range(2):
                        S.dma('pool', wsu[:, :, pc_ * 256:(pc_ + 1) * 256], dr['w_in'][l][:, :, pc_ * 256:(pc_ + 1) * 256], writes=['wsu%d' % pc_])
                    pp = [ps(st2, "s5pp%d" % i, [128, 512], F32) for i in range(2)]

                    def evac(mi, m, n0, nn, p_, pk_):
                        if m < 2:
                            act(ub[:, m, n0:n0 + nn], p_[:, 0:nn], AF.Identity, [pk_, 'pvt'], ['s5u'], bias=pv('bin', m))
                        else:
                            act(zs[:, m - 2, n0:n0 + nn], p_[:, 0:nn], AF.Silu, [pk_, 'pvt'], ['s5z'], bias=pv('bin', m))
                    proj_fm(st2, wsu, 'wsu', [0, 1, 2, 3], evac, pp, ['s5pp0', 's5pp1'])
                    sm = sb(st2, "s5sm", [128, 20, 16], F32)
                    K_ = 's5sm'

                    def Sm(i):
                        return sm[:, i, :]

                    def T2(o, a, b, op):
                        tt('dve', Sm(o), a if not isinstance(a, int) else Sm(a), b if not isinstance(b, int) else Sm(b), op,
                           [K_, 'pvt'], [K_])
                    lamre, lamim = pv('lamre'), pv('lamim')
                    act(Sm(0), pv('ldt'), AF.Exp, ['pvt'], [K_])
                    T2(1, lamre, 0, ALU.mult)
                    act(Sm(2), Sm(1), AF.Exp, [K_], [K_])
                    act(Sm(3), Sm(1), AF.Exp, [K_], [K_], scale=-1.0)
                    T2(4, lamim, 0, ALU.mult)
                    ts('dve', Sm(5), Sm(4), PI / 2, None, ALU.add, None, [K_], [K_])
                    for x in (4, 5):
                        for _ in range(4):
                            ts('dve', Sm(16), Sm(x), PI, 2 * PI, ALU.is_gt, ALU.mult, [K_], [K_])
                            T2(x, x, 16, ALU.subtract)
                    act(Sm(6), Sm(4), AF.Sin, [K_], [K_])
                    act(Sm(7), Sm(5), AF.Sin, [K_], [K_])
                    T2(8, 2, 7, ALU.mult)
                    T2(9, 2, 6, ALU.mult)
                    T2(10, 3, 7, ALU.mult)
                    stt(Sm(11), Sm(3), -1.0, Sm(6), ALU.mult, ALU.mult, [K_], [K_])
                    ts('dve', Sm(12), Sm(8), -1.0, None, ALU.add, None, [K_], [K_])
                    T2(16, lamre, lamre, ALU.mult)
                    T2(17, lamim, lamim, ALU.mult)
                    T2(13, 16, 17, ALU.add)
                    S.op('dve', lambda e: e.reciprocal(out=Sm(13), in_=Sm(13)), reads=[K_], writes=[K_])
                    T2(16, 12, lamre, ALU.mult)
                    T2(17, 9, lamim, ALU.mult)
                    T2(16, 16, 17, ALU.add)
                    T2(14, 16, 13, ALU.mult)
                    T2(16, 9, lamre, ALU.mult)
                    T2(17, 12, lamim, ALU.mult)
                    T2(16, 16, 17, ALU.subtract)
                    T2(15, 16, 13, ALU.mult)
                    tmpa = sb(st2, "s5ta", [128, 16, L], F32)
                    tmpb = sb(st2, "s5tb", [128, 16, L], F32)

                    def cmul_bc(dst_re, dst_im, src_re, src_im, s_re, s_im, m):
                        sr = s_re.unsqueeze(2).broadcast_to([128, 16, m])
                        si = s_im.unsqueeze(2).broadcast_to([128, 16, m])
                        ta, tb = tmpa[:, :, 0:m], tmpb[:, :, 0:m]
                        kk_ = ['s5tab', 's5ta', 's5tb', 's5tc', K_]
                        tt('dve', ta, src_re, sr, ALU.mult, kk_, ['s5ta'])
                        tt('dve', tb, src_im, si, ALU.mult, kk_, ['s5tb'])
                        tt('dve', dst_re, ta, tb, ALU.subtract, kk_, ['s5tab'])
                        tt('dve', ta, src_re, si, ALU.mult, kk_, ['s5ta'])
                        tt('dve', tb, src_im, sr, ALU.mult, kk_, ['s5tb'])
                        tt('dve', dst_im, ta, tb, ALU.add, kk_, ['s5tab'])
                    for (TB, a_re, a_im) in ((PT, 8, 9), (QT, 10, 11)):
                        cp('dve', TB[:, :, 0, 0], Sm(a_re), [K_], ['s5tab'])
                        cp('dve', TB[:, :, 1, 0], Sm(a_im), [K_], ['s5tab'])
                        m = 1
                        while m < L:
                            cmul_bc(TB[:, :, 0, m:2 * m], TB[:, :, 1, m:2 * m], TB[:, :, 0, 0:m], TB[:, :, 1, 0:m],
                                    TB[:, :, 0, m - 1], TB[:, :, 1, m - 1], m)
                            m *= 2
                    tmpc = sb(st2, "s5tc", [128, 16, L], F32)
                    cp('dve', tmpc[:], QT[:, :, 0, :], ['s5tab'], ['s5tc'])
                    cmul_bc(QT[:, :, 0, :], QT[:, :, 1, :], tmpc[:], QT[:, :, 1, :], Sm(14), Sm(15), L)
                    S.barrier()
                with contextlib.ExitStack() as st2:
                    NB = 8
                    xa = [sb(st2, "s5xa%d" % i, [128, 2, L], F32) for i in range(NB)]
                    xb_ = [sb(st2, "s5xb%d" % i, [128, 2, L], F32) for i in range(NB)]
                    cw = [sb(st2, "s5cw%d" % i, [128, 2, L], F32) for i in range(NB)]
                    hb = [sb(st2, "s5hb%d" % i, [128, 2, L], BF16) for i in range(NB)]
                    pbu = [ps(st2, "s5pb%d" % i, [128, 2, 2, L], F32) for i in range(4)]
                    py = [ps(st2, "s5py%d" % i, [128, 512], F32) for i in range(2)]
                    orders = [list(range(NTL)), [1, 0] + list(range(NTL - 1, 1, -1))]
                    def s5group(gi, step, d, j):
                        c = orders[d][step]
                        n0 = c * L
                        rev = (d == 1)
                        U = []
                        for ii in range(4):
                            un = gi * 4 + ii
                            bnk = (un // 2) % 4
                            U.append(dict(ii=ii, i=j * 4 + ii, q=d * 8 + j * 4 + ii, pb=pbu[bnk][:, un % 2], pbk='s5pb%d' % bnk,
                                          A=xa[un % NB], Ak='s5xa%d' % (un % NB), B=xb_[un % NB], Bk='s5xb%d' % (un % NB),
                                          C=cw[un % NB], Ck='s5cw%d' % (un % NB), H=hb[un % NB], Hk='s5hb%d' % (un % NB)))
                        for u in U:
                            for ri in range(2):
                                mm(u['pb'][:, ri, :], btb[:, j, u['ii'], ri, :], ub[:, j, n0:n0 + L], ['btb', 's5u'], [u['pbk']])
                        yield
                        for u in U:
                            src = u['pb'][:, :, ::-1] if rev else u['pb'][:, :, :]
                            tt('dve', u['A'][:], src, QT[:, u['q'], 0:1, :].broadcast_to([128, 2, L]), ALU.mult,
                               [u['pbk'], 's5tab'], [u['Ak']])
                        yield
                        for u in U:
                            src = u['pb'][:, ::-1, ::-1] if rev else u['pb'][:, ::-1, :]
                            tt('dve', u['B'][:], src, QT[:, u['q'], 1:2, :].broadcast_to([128, 2, L]), ALU.mult,
                               [u['pbk'], 's5tab'], [u['Bk']])
                        yield
                        for u in U:
                            tt('dve', u['A'][:, 0, :], u['A'][:, 0, :], u['B'][:, 0, :], ALU.subtract, [u['Ak'], u['Bk']], [u['Ak']])
                        yield
                        for u in U:
                            tt('dve', u['A'][:, 1, :], u['A'][:, 1, :], u['B'][:, 1, :], ALU.add, [u['Ak'], u['Bk']], [u['Ak']])
                        yield
                        for ri in range(2):
                            for u in U:
                                q = u['q']
                                S.op('dve', lambda e, u=u, ri=ri, q=q: e.tensor_tensor_scan(
                                    out=u['C'][:, ri, :], data0=ones[:], data1=u['A'][:, ri, :], initial=sst[:, q, ri:ri + 1],
                                    op0=ALU.mult, op1=ALU.add), reads=[u['Ak'], 's5ones', 'sst%d' % q, 'sst'], writes=[u['Ck']])
                            yield
                        for u in U:
                            tt('pool', u['A'][:], u['C'][:], PT[:, u['q'], 0:1, :].broadcast_to([128, 2, L]), ALU.mult,
                               [u['Ck'], 's5tab', u['Ak']], [u['Ak']])
                        yield
                        for u in U:
                            tt('pool', u['B'][:], u['C'][:, ::-1, :], PT[:, u['q'], 1:2, :].broadcast_to([128, 2, L]), ALU.mult,
                               [u['Ck'], 's5tab', u['Bk']], [u['Bk']])
                        yield
                        for u in U:
                            tt('pool', u['A'][:, 0, :], u['A'][:, 0, :], u['B'][:, 0, :], ALU.subtract, [u['Ak'], u['Bk']], [u['Ak']])
                        yield
                        for u in U:
                            tt('pool', u['A'][:, 1, :], u['A'][:, 1, :], u['B'][:, 1, :], ALU.add, [u['Ak'], u['Bk']], [u['Ak']])
                        yield
                        for u in U:
                            cp('pool', sst[:, u['q'], :], u['A'][:, :, L - 1], [u['Ak']], ['sst%d' % u['q']])
                        yield
                        for u in U:
                            hsrc = u['A'][:, :, ::-1] if rev else u['A'][:]
                            cp('act', u['H'][:], hsrc, [u['Ak']], [u['Hk']])
                        yield
                        pyr = py[gi % 2][:, 0:L]
                        pyk = 's5py%d' % (gi % 2)
                        for k_, u in enumerate(U):
                            for ri in range(2):
                                mm(pyr, ctb[:, u['i'], ri, :], u['H'][:, ri, :], ['ctb', u['Hk']], [pyk],
                                   start=(k_ == 0 and ri == 0), stop=(k_ == 3 and ri == 1))
                        yield
                        yield
                        yield
                        tt('dve', yacc[:, j, n0:n0 + L], yacc[:, j, n0:n0 + L], pyr, ALU.add, [pyk, 'yacc'], ['yacc'])

                    glist = [(step, d, j) for step in range(NTL) for d in range(2) for j in range(2)]
                    gbank = mkbanks(st2, 2, "s5gk") if (GATE_PRE and GJ_S5) else None
                    gj = gate_jobs(l, last, st2, gbank, GJ_S5) if (GATE_PRE and GJ_S5) else []
                    run_pipelined(interleave([s5group(gi, *g) for gi, g in enumerate(glist)], gj, 2), S5_STAGGER)
                    S.barrier()
                for j in range(2):
                    stt(yacc[:, j, :], ub[:, j, :], pv('s5d', j), yacc[:, j, :], ALU.mult, ALU.add, ['s5u', 'yacc', 'pvt'],
                        ['yacc'])
                dbg_dump('ya%d' % l, yacc[:], [128, 2, NT], ['yacc'])
                with contextlib.ExitStack() as st2:
                    t1 = [sb(st2, "s5g1_%d" % i, [128, 512], F32) for i in range(2)]
                    t2 = [sb(st2, "s5g2_%d" % i, [128, 512], BF16) for i in range(2)]
                    pg = [ps(st2, "s5pg%d" % i, [128, 512], F32) for i in range(2)]
                    cnt = 0
                    for (n0, nn) in BLOCKS:
                        for j in range(2):
                            a, ak = t1[cnt % 2], 's5g1_%d' % (cnt % 2)
                            cnt += 1
                            ysl = yacc[:, j, n0:n0 + nn]
                            act(a[:, 0:nn], ysl, AF.Square, ['yacc'], [ak])
                            ts('dve', a[:, 0:nn], a[:, 0:nn], 0.044715, 1.0, ALU.mult, ALU.add, [ak], [ak])
                            tt('dve', a[:, 0:nn], a[:, 0:nn], ysl, ALU.mult, [ak, 'yacc'], [ak])
                            act(a[:, 0:nn], a[:, 0:nn], AF.Sigmoid, [ak], [ak], scale=1.5957691216057308)
                            tt('dve', ub[:, j, n0:n0 + nn], a[:, 0:nn], ysl, ALU.mult, [ak, 'yacc'], ['s5u'])
                    cnt = 0
                    for (n0, nn) in BLOCKS:
                        for m in range(2):
                            p_, pk_ = pg[cnt % 2], 's5pg%d' % (cnt % 2)
                            b_, bk_ = t2[cnt % 2], 's5g2_%d' % (cnt % 2)
                            cnt += 1
                            for jc in range(2):
                                mm(p_[:, 0:nn], glub[:, jc, m * 128:(m + 1) * 128], ub[:, jc, n0:n0 + nn], ['glub', 's5u'], [pk_],
                                   start=(jc == 0), stop=(jc == 1))
                            act(b_[:, 0:nn], p_[:, 0:nn], AF.Sigmoid, [pk_, 'pvt'], [bk_], bias=pv('glub', m))
                            tt('dve', b_[:, 0:nn], b_[:, 0:nn], ub[:, m, n0:n0 + nn], ALU.mult, [bk_, 's5u'], [bk_])
                            tt('pool', Y[:, 0, m, n0:n0 + nn], b_[:, 0:nn], zs[:, m, n0:n0 + nn], ALU.mult, [bk_, 's5z'], ['Y0'])
                    S.barrier()
                S.barrier()
        PHASES['s5'] = phase_s5
        def phase_hg(l, h_src, last):
            with contextlib.ExitStack() as st:
                QP = [sb(st, "hgQP%d" % d, [128, 2, NT], BF16) for d in range(2)]
                KP = [sb(st, "hgKP%d" % d, [128, 2, NT], BF16) for d in range(2)]
                G = sb(st, "hgG", [128, 2, 72, 2], F32)
                VT = sb(st, "hgVT", [128, NTL, 256], BF16)
                zs = sb(st, "hgzs", [128, 2, NT], BF16)
                lbt = sb(st, "hglbt", [128, 2, 4], F32)
                if l == 0:
                    memset('pool', lbt[:, 0, :], 0.0, ['hglbt'])
                    memset('pool', lbt[:, 1, :], 1.0, ['hglbt'])
                else:
                    o_, _ = PV['hglb']
                    tt('dve', lbt[:, 0, :], pvt[:, o_ + 4:o_ + 8], pvt[:, o_:o_ + 4], ALU.subtract, ['pvt'], ['hglbt'])
                    act(lbt[:, 0, :], lbt[:, 0, :], AF.Sigmoid, ['hglbt'], ['hglbt'])
                    ts('dve', lbt[:, 1, :], lbt[:, 0, :], -1.0, 1.0, ALU.mult, ALU.add, ['hglbt'], ['hglbt'])
                with contextlib.ExitStack() as st2:
                    wh = sb(st2, "hgw", [128, 8, 1280], BF16)
                    for pc_ in (0, 4, 1, 2, 3):
                        S.dma('pool', wh[:, :, pc_ * 256:(pc_ + 1) * 256], dr['w_in'][l][:, :, 512 + pc_ * 256:512 + (pc_ + 1) * 256], writes=['hgw%d' % pc_])
                    brow = sb(st2, "hgbrow", [128, 256], F32)
                    S.dma('sp', brow[:], dr['rows'][l][:, 4096:4352], writes=['hgbrow'])
                    R32 = sb(st2, "hgR32", [128, 512], F32)
                    memset('pool', R32[:], 1.0, ['hgR32'])
                    memset('pool', R32[:, 0:512:32], 0.0, ['hgR32'])
                    QS = [sb(st2, "hgQS%d" % i, [128, 2, 512], BF16) for i in range(2)]
                    T = [[sb(st2, "hgT%d_%d" % (i, k), [128, 512], F32) for k in range(4)] for i in range(2)]
                    pp = [ps(st2, "hgpp%d" % i, [128, 512], F32) for i in range(3)]
                    pt = [ps(st2, "hgpt%d" % i, [128, 512], F32) for i in range(2)]
                    def hgproj(cnt, ic, m, n0, nn):
                        ukeys = uTk[n0 // 128:(n0 + nn) // 128]
                        p_, pk_ = pp[cnt % 3], 'hgpp%d' % (cnt % 3)
                        bi = (n0 // 512) % 2 if n0 else 0
                        for jj in range(8):
                            mm(p_[:, 0:nn], wh[:, jj, m * 128:(m + 1) * 128], uT[:, jj, n0:n0 + nn], ['hgw%d' % (m // 2)] + ukeys, [pk_],
                               start=(jj == 0), stop=(jj == 7))
                        yield
                        bias = pv('bin', 4 + m)
                        if m < 2:
                            act(QS[bi][:, m, 0:nn], p_[:, 0:nn], AF.Silu, [pk_, 'pvt'], ['hgQS%d' % bi], bias=bias)
                            return
                        if m >= 8:
                            act(zs[:, m - 8, n0:n0 + nn], p_[:, 0:nn], AF.Silu, [pk_, 'pvt'], ['hgzs'], bias=bias)
                            return
                        d, j = (m - 2) // 2, (m - 2) % 2
                        Ts = T[ic % 2]
                        Tk = ['hgT%d_%d' % (ic % 2, k) for k in range(4)]
                        t1, t2, t3, t4 = [x[:, 0:nn] for x in Ts]
                        act(t1, p_[:, 0:nn], AF.Sigmoid, [pk_, 'pvt'], [Tk[0]], bias=bias)
                        yield
                        ts('dve', t1, t1, lbt[:, 1, d * 2 + j:d * 2 + j + 1], lbt[:, 0, d * 2 + j:d * 2 + j + 1], ALU.mult, ALU.add,
                           [Tk[0], 'hglbt'], [Tk[0]])
                        yield
                        act(t2, t1, AF.Ln, [Tk[0]], [Tk[1]])
                        yield
                        if d == 0:
                            S.op('dve', lambda e: e.tensor_tensor_scan(out=t3, data0=R32[:, 0:nn], data1=t2, initial=0.0,
                                                                       op0=ALU.mult, op1=ALU.add),
                                 reads=[Tk[1], 'hgR32'], writes=[Tk[2]])
                        else:
                            S.op('dve', lambda e: e.tensor_tensor_scan(out=t3[:, ::-1],
                                                                       data0=R32[:, 0:nn], data1=t2[:, ::-1], initial=0.0,
                                                                       op0=ALU.mult, op1=ALU.add),
                                 reads=[Tk[1], 'hgR32'], writes=[Tk[2]])
                        yield
                        ts('dve', t3, t3, -80.0, None, ALU.max, None, [Tk[2]], [Tk[2]])
                        ts('dve', t1, t1, -1.0, 1.0, ALU.mult, ALU.add, [Tk[0]], [Tk[0]])
                        yield
                        act(t4, t3, AF.Exp, [Tk[2]], [Tk[3]])
                        act(t2, t3, AF.Exp, [Tk[2]], [Tk[1]], scale=-1.0)
                        yield
                        tt('pool', KP[d][:, j, n0:n0 + nn], t1, t2, ALU.mult, [Tk[0], Tk[1]], ['hgKP%d' % d])
                        tt('pool', QP[d][:, j, n0:n0 + nn], QS[bi][:, j, 0:nn], t4, ALU.mult, ['hgQS%d' % bi, Tk[3]], ['hgQP%d' % d])
                        c0 = n0 // 32
                        gsrc = t4[:, 31::32] if d == 0 else t4[:, 0::32]
                        cp('act', G[:, d, c0:c0 + nn // 32, j], gsrc, [Tk[3]], ['hgG'])

                    plist = []
                    cnt = 0
                    ic = 0
                    for (n0, nn) in BLOCKS:
                        for m in (0, 1, 8, 9, 2, 3, 4, 5):
                            plist.append((cnt, ic, m, n0, nn))
                            cnt += 1
                            if 2 <= m < 8:
                                ic += 1
                    run_pipelined((hgproj(*p) for p in plist), 4)
                    for t in range(NTL):
                        p_, pk_ = pt[t % 2], 'hgpt%d' % (t % 2)
                        for jj in range(8):
                            mm(p_[:, 0:256], uT[:, jj, t * 128:(t + 1) * 128], wh[:, jj, 768:1024], ['hgw3', uTk[t]], [pk_],
                               start=(jj == 0), stop=(jj == 7))
                        tt('dve', VT[:, t, :], p_[:, 0:256], brow[:], ALU.add, [pk_, 'hgbrow'], ['hgVT'])
                    S.barrier()
                Sall = [sb(st, "hgSall%d" % d, [128, 2, 72, 64], BF16) for d in range(2)]
                with contextlib.ExitStack() as st2:
                    Sst = [sb(st2, "hgS%d" % d, [128, 2, 64], F32) for d in range(2)]
                    kTm = [sb(st2, "hgkTm%d" % i, [128, 4, 256], BF16) for i in range(3)]
                    Ug = [sb(st2, "hgUg%d" % i, [128, 4, 2, 64], F32) for i in range(3)]
                    ptr = [ps(st2, "hgptr%d" % i, [128, 8, 128], BF16) for i in range(2)]
                    pU = [ps(st2, "hgpU%d" % i, [128, 4, 2, 64], F32) for i in range(3)]
                    orders = [list(range(NTL)), [1, 0] + list(range(NTL - 1, 1, -1))]
                    for d in range(2):
                        memset('pool', Sst[d][:], 0.0, ['hgS%d' % d])
                    def hgchain(it, step, d):
                        t = orders[d][step]
                        pr, prk = ptr[it % 2], 'hgptr%d' % (it % 2)
                        km, kmk = kTm[it % 3], 'hgkTm%d' % (it % 3)
                        pu, puk = pU[it % 3], 'hgpU%d' % (it % 3)
                        ug, ugk = Ug[it % 3], 'hgUg%d' % (it % 3)
                        for j in range(2):
                            tr(pr[:, j, :], KP[d][:, j, t * 128:(t + 1) * 128], identb, ['hgKP%d' % d, 'cstb'], [prk])
                        yield
                        for cc in range(4):
                            prf = pr[:, 0:2, :].rearrange("p a b -> p (a b)")
                            if cc % 2 == 0:
                                ts('dve', km[:, cc, :], prf, cstf[:, 4, 64 + cc:64 + cc + 1], None, ALU.mult, None, [prk, 'cstf'], [kmk])
                            else:
                                act(km[:, cc, :], prf, AF.Identity, [prk, 'cstf'], [kmk], scale=cstf[:, 4, 64 + cc:64 + cc + 1])
                        yield
                        for cc in range(4):
                            for h in range(4):
                                hp = (h % 2) * 64
                                mm(pu[hp:hp + 64, cc, h // 2, :], km[:, cc, h * 64:(h + 1) * 64], VT[:, t, h * 64:(h + 1) * 64],
                                   [kmk, 'hgVT'], [puk])
                        yield
                        tt('dve', ug[:], pu[:], G[:, d, t * 4:(t + 1) * 4, :].unsqueeze(3).broadcast_to([128, 4, 2, 64]), ALU.mult,
                           [puk, 'hgG'], [ugk])
                        yield
                        ccs = range(4) if d == 0 else range(3, -1, -1)
                        for cc in ccs:
                            c = t * 4 + cc
                            cp('act', Sall[d][:, :, c, :], Sst[d][:], ['hgS%d' % d], ['hgSall%d_%d' % (d, t)])
                            for j in range(2):
                                stt(Sst[d][:, j, :], Sst[d][:, j, :], G[:, d, c, j:j + 1], ug[:, cc, j, :], ALU.mult, ALU.add,
                                    ['hgS%d' % d, 'hgG', ugk], ['hgS%d' % d])
                            yield

                    gbank = mkbanks(st2, 3, "hggk") if GJ_SPLIT[0] else None
                    gj = gate_jobs(l, last, st2, gbank, GJ_SPLIT[0]) if (GATE_PRE and GJ_SPLIT[0]) else []
                    run_pipelined(interleave([hgchain(i_, sd[0], sd[1]) for i_, sd in enumerate([(s_, d_) for s_ in range(NTL) for d_ in range(2)])], gj, 3), 3)
                    S.barrier()
                with contextlib.ExitStack() as st2:
                    if ('yb%d' % l) in debug:
                        dbgbuf = sb(st2, "dbgbuf", [128, 2, NT], F32)
                    AT = [[sb(st2, "hgAT%d_%d" % (i, d), [128, 4, 128], BF16) for d in range(2)] for i in range(2)]
                    sq = [sb(st2, "hgsq%d" % i, [128, 2, 128], BF16) for i in range(2)]
                    rr = [sb(st2, "hgrr%d" % i, [128, 2, 128], F32) for i in range(2)]
                    ob = [sb(st2, "hgob%d" % i, [128, 2, 128], F32) for i in range(2)]
                    bank = mkbanks(st2, 8, "hgbk")

                    def hgout(t):
                        i2 = t % 2
                        tsl = slice(t * 128, (t + 1) * 128)
                        pas = {}
                        for d in range(2):
                            for par in range(2):
                                pas[(d, par)] = bank()
                            for h in range(4):
                                hp = (h % 2) * 64
                                pa, pak = pas[(d, h % 2)]
                                pav = pa[:, 0:256].rearrange("p (a b) -> p a b", a=2)
                                mm(pav[:, h // 2, :], KP[d][hp:hp + 64, h // 2, tsl], QP[d][hp:hp + 64, h // 2, tsl],
                                   ['hgKP%d' % d, 'hgQP%d' % d], [pak])
                        yield
                        for d in range(2):
                            for par in range(2):
                                pa, pak = pas[(d, par)]
                                pav = pa[:, 0:256].rearrange("p (a b) -> p a b", a=2)
                                tt('dve', AT[i2][d][:, par::2, :], pav, maskb[:, d, :].unsqueeze(1).broadcast_to([128, 2, 128]), ALU.mult,
                                   [pak, 'maskb'], ['hgAT%d_%d' % (i2, d)])
                        yield
                        pos = [bank() for _ in range(2)]
                        povs = [pos[par][0][:, 0:256].rearrange("p (a b) -> p a b", a=2) for par in range(2)]
                        for h in range(4):
                            hp = (h % 2) * 64
                            pok = pos[h % 2][1]
                            reg = povs[h % 2][hp:hp + 64, h // 2, :]
                            first = True
                            for d in range(2):
                                mm(reg, VT[:, t, h * 64:(h + 1) * 64], AT[i2][d][:, h, :], ['hgVT', 'hgAT%d_%d' % (i2, d)], [pok],
                                   start=first, stop=False)
                                first = False
                                for cc in range(4):
                                    c = t * 4 + cc
                                    mm(reg[:, cc * 32:(cc + 1) * 32], Sall[d][hp:hp + 64, h // 2, c, :],
                                       QP[d][hp:hp + 64, h // 2, t * 128 + cc * 32:t * 128 + (cc + 1) * 32],
                                       ['hgSall%d_%d' % (d, t), 'hgQP%d' % d], [pok], start=False, stop=(d == 1 and cc == 3))
                        yield
                        obk = 'hgob%d' % i2
                        cp('act', ob[i2][0:64], povs[0][0:64], [pos[0][1]], [obk])
                        cp('dve', ob[i2][64:128], povs[1][64:128], [pos[1][1]], [obk])
                        yield
                        pov = ob[i2][:]
                        pok = obk
                        if ('yb%d' % l) in debug:
                            cp('pool', dbgbuf[:, :, tsl], pov, [pok], ['dbgbuf'])
                        act(sq[i2][:], pov, AF.Square, [pok], ['hgsq%d' % i2])
                        yield
                        pss_, psk = bank()
                        psv = pss_[:, 0:256].rearrange("p (a b) -> p a b", a=2)
                        for j in range(2):
                            mm(psv[:, j, :], bonesb, sq[i2][:, j, :], ['cstb', 'hgsq%d' % i2], [psk])
                        yield
                        act(rr[i2][:], psv, AF.Sqrt, [psk], ['hgrr%d' % i2], bias=RMS_EPS, scale=1.0 / 64)
                        yield
                        S.op('dve', lambda e: e.reciprocal(out=rr[i2][:], in_=rr[i2][:]), reads=['hgrr%d' % i2], writes=['hgrr%d' % i2])
                        yield
                        tt('dve', rr[i2][:], pov, rr[i2][:], ALU.mult, [pok, 'hgrr%d' % i2], ['hgrr%d' % i2])
                        yield
                        for j in range(2):
                            stt(Y[:, 1, j, tsl], rr[i2][:, j, :], pv('hgnw', j), zs[:, j, tsl], ALU.mult, ALU.mult,
                                ['hgrr%d' % i2, 'pvt', 'hgzs'], ['Y1'])

                    gj = gate_jobs(l, last, st2, bank, GJ_SPLIT[1]) if (GATE_PRE and GJ_SPLIT[1]) else []
                    run_pipelined(interleave([hgout(t) for t in range(NTL) if not (last and t < 2 and not debug)], gj, 3), 5)
                    if ('yb%d' % l) in debug:
                        dbg_dump('yb%d' % l, dbgbuf[:], [128, 2, NT], ['dbgbuf'])
                    S.barrier()
                S.barrier()
        PHASES['hg'] = phase_hg
        def phase_ret(l, h_src, last):
            with contextlib.ExitStack() as st:
                QR = sb(st, "rtQR", [128, 2, NT], BF16)
                KR = sb(st, "rtKR", [128, 2, NT], BF16)
                VT = sb(st, "rtVT", [128, NTL, 256], BF16)
                zs = sb(st, "rtzs", [128, 2, NT], BF16)
                Sall = [sb(st, "rtSall%d" % d, [128, 2, NTL, 64], BF16) for d in range(2)]
                LG = sb(st, "rtLG", [128, 4], F32)
                GL = sb(st, "rtGL", [128, 4], F32)
                LGH = sb(st, "rtLGH", [128, 8], F32)
                QDEC = sb(st, "rtQDEC", [128, 2, 2, 128], F32)
                KDEC = sb(st, "rtKDEC", [128, 2, 4], F32)
                DS = sb(st, "rtDS", [128, 4, 128], F32)
                tb8 = sb(st, "rtb8", [128, 2], F32)
                K_ = 'rttab'
                act(LG[:], pv('rdec'), AF.Exp, ['pvt'], [K_])
                ts('dve', LG[:], LG[:], -1.0, None, ALU.mult, None, [K_], [K_])
                act(GL[:], LG[:], AF.Exp, [K_], [K_], scale=128.0)
                act(LGH[:], pv('rdech'), AF.Exp, ['pvt'], [K_])
                ts('dve', LGH[:], LGH[:], -1.0, None, ALU.mult, None, [K_], [K_])
                for d in range(2):
                    for j in range(2):
                        act(QDEC[:, d, j, :], cstf[:, 5 + d, :], AF.Exp, ['cstf', K_], [K_], scale=LG[:, d * 2 + j:d * 2 + j + 1])
                    act(KDEC[:, d, :], LGH[:, d * 4:(d + 1) * 4], AF.Exp, ['cstf', K_], [K_], scale=cstf[:, 4, 68 + d:69 + d])
                with contextlib.ExitStack() as st2:
                    ta = sb(st2, "rtta", [128, 128], F32)
                    tb = sb(st2, "rttb", [128, 128], F32)
                    for h in range(4):
                        act(ta[:], cstf[:, 0, :], AF.Exp, ['cstf', K_], ['rtta'], scale=LGH[:, h:h + 1])
                        tt('dve', ta[:], ta[:], cstf[:, 2, :], ALU.mult, ['rtta', 'cstf'], ['rtta'])
                        act(tb[:], cstf[:, 1, :], AF.Exp, ['cstf', K_], ['rttb'], scale=LGH[:, 4 + h:5 + h])
                        tt('dve', tb[:], tb[:], cstf[:, 3, :], ALU.mult, ['rttb', 'cstf'], ['rttb'])
                        tt('dve', DS[:, h, :], ta[:], tb[:], ALU.add, ['rtta', 'rttb'], [K_])
                    ts('dve', tb8[:], pv('bin', 16, 2), 0.125, None, ALU.mult, None, ['pvt'], [K_])
                    S.barrier()
                if stop == 'ret_tab':
                    return
                with contextlib.ExitStack() as st2:
                    wr = sb(st2, "rtw", [128, 8, 1024], BF16)
                    for pc_ in range(4):
                        S.dma('pool', wr[:, :, pc_ * 256:(pc_ + 1) * 256], dr['w_in'][l][:, :, 1792 + pc_ * 256:1792 + (pc_ + 1) * 256], writes=['rtw%d' % pc_])
                    brow = sb(st2, "rtbrow", [128, 256], F32)
                    S.dma('sp', brow[:], dr['rows'][l][:, 4352:4608], writes=['rtbrow'])
                    COS = sb(st2, "rtcos", [128, 2048], F32)
                    SIN = sb(st2, "rtsin", [128, 2048], F32)
                    permf = sb(st2, "rtperm", [128, 128], F32)
                    S.dma('sp', COS[:], dr['rcos'], writes=['rtcos'])
                    S.dma('act', SIN[:], dr['rsin'], writes=['rtsin'])
                    S.dma('sp', permf[:], dr['cst'][:, 0, :], writes=['rtperm'])
                    qf = [sb(st2, "rtqf%d" % i, [128, 512], F32) for i in range(2)]
                    t1 = [sb(st2, "rtt1_%d" % i, [128, 512], F32) for i in range(2)]
                    pp = [ps(st2, "rtpp%d" % i, [128, 512], F32) for i in range(2)]
                    pq = [ps(st2, "rtpq%d" % i, [128, 512], F32) for i in range(2)]
                    pt = [ps(st2, "rtpt%d" % i, [128, 512], F32) for i in range(2)]
                    def rtproj(cnt, rc, m, n0, nn):
                        ukeys = uTk[n0 // 128:(n0 + nn) // 128]
                        p_, pk_ = pp[cnt % 2], 'rtpp%d' % (cnt % 2)
                        for jj in range(8):
                            mm(p_[:, 0:nn], wr[:, jj, m * 128:(m + 1) * 128], uT[:, jj, n0:n0 + nn], ['rtw%d' % (m // 2)] + ukeys, [pk_],
                               start=(jj == 0), stop=(jj == 7))
                        yield
                        if m >= 6:
                            act(zs[:, m - 6, n0:n0 + nn], p_[:, 0:nn], AF.Silu, [pk_, 'pvt'], ['rtzs'], bias=pv('bin', 14 + m))
                            return
                        isk = m >= 2
                        j = m % 2
                        dst = (KR if isk else QR)[:, j, n0:n0 + nn]
                        dk = 'rtKR' if isk else 'rtQR'
                        if n0 < 256:
                            if isk:
                                act(dst, p_[:, 0:nn], AF.Identity, [pk_, K_], [dk], bias=tb8[:, j:j + 1], scale=0.125)
                            else:
                                act(dst, p_[:, 0:nn], AF.Identity, [pk_, 'pvt'], [dk], bias=pv('bin', 14 + m))
                            return
                        q_, qk_ = qf[rc % 2], 'rtqf%d' % (rc % 2)
                        a_, ak_ = t1[rc % 2], 'rtt1_%d' % (rc % 2)
                        r_, rk_ = pq[rc % 2], 'rtpq%d' % (rc % 2)
                        if isk:
                            act(q_[:, 0:nn], p_[:, 0:nn], AF.Identity, [pk_, K_], [qk_], bias=tb8[:, j:j + 1], scale=0.125)
                        else:
                            act(q_[:, 0:nn], p_[:, 0:nn], AF.Identity, [pk_, 'pvt'], [qk_], bias=pv('bin', 14 + m))
                        yield
                        mm(r_[:, 0:nn], permf[:], q_[:, 0:nn], ['rtperm', qk_], [rk_])
                        yield
                        tsl = slice(n0 - 256, n0 - 256 + nn)
                        tt('dve', a_[:, 0:nn], r_[:, 0:nn], SIN[:, tsl], ALU.mult, [rk_, 'rtsin'], [ak_])
                        tt('pool', q_[:, 0:nn], q_[:, 0:nn], COS[:, tsl], ALU.mult, [qk_, 'rtcos'], [qk_])
                        yield
                        tt('dve', dst, a_[:, 0:nn], q_[:, 0:nn], ALU.add, [ak_, qk_], [dk])

                    plist = []
                    cnt = 0
                    rc = 0
                    for (n0, nn) in BLOCKS:
                        for m in (0, 1, 2, 3, 6, 7):
                            plist.append((cnt, rc, m, n0, nn))
                            cnt += 1
                            if m < 6 and n0 >= 256:
                                rc += 1
                    run_pipelined((rtproj(*p) for p in plist), 2)
                    for t in range(NTL):
                        p_, pk_ = pt[t % 2], 'rtpt%d' % (t % 2)
                        for jj in range(8):
                            mm(p_[:, 0:256], uT[:, jj, t * 128:(t + 1) * 128], wr[:, jj, 512:768], ['rtw2', uTk[t]], [pk_],
                               start=(jj == 0), stop=(jj == 7))
                        tt('dve', VT[:, t, :], p_[:, 0:256], brow[:], ALU.add, [pk_, 'rtbrow'], ['rtVT'])
                    S.barrier()
                if stop == 'ret_proj':
                    return
                with contextlib.ExitStack() as st2:
                    Sst = [sb(st2, "rtS%d" % d, [128, 2, 64], F32) for d in range(2)]
                    kT = [sb(st2, "rtkT%d" % i, [128, 256], BF16) for i in range(3)]
                    ptr = [ps(st2, "rtptr%d" % i, [128, 8, 128], BF16) for i in range(2)]
                    pU = [ps(st2, "rtpU%d" % i, [128, 512], F32) for i in range(3)]
                    orders = [list(range(NTL)), [1, 0] + list(range(NTL - 1, 1, -1))]
                    for d in range(2):
                        memset('pool', Sst[d][:], 0.0, ['rtS%d' % d])
                    def rtchain(it, step, d):
                        t = orders[d][step]
                        pr, prk = ptr[it % 2], 'rtptr%d' % (it % 2)
                        kt, ktk = kT[it % 3], 'rtkT%d' % (it % 3)
                        pu, puk = pU[it % 3], 'rtpU%d' % (it % 3)
                        puv = pu[:, 0:128].rearrange("p (a b) -> p a b", a=2)
                        for j in range(2):
                            tr(pr[:, j, :], KR[:, j, t * 128:(t + 1) * 128], identb, ['rtKR', 'cstb'], [prk])
                        yield
                        tt('dve', kt[:].rearrange("p (h k) -> p h k", h=4), pr[:, 0:2, :].rearrange("p a (b k) -> p (a b) k", b=2),
                           KDEC[:, d, :].unsqueeze(2).broadcast_to([128, 4, 64]), ALU.mult, [prk, K_], [ktk])
                        yield
                        for h in range(4):
                            hp = (h % 2) * 64
                            mm(puv[hp:hp + 64, h // 2, :], kt[:, h * 64:(h + 1) * 64], VT[:, t, h * 64:(h + 1) * 64], [ktk, 'rtVT'], [puk])
                        yield
                        cp('act', Sall[d][:, :, t, :], Sst[d][:], ['rtS%d' % d], ['rtSall%d_%d' % (d, t)])
                        for j in range(2):
                            stt(Sst[d][:, j, :], Sst[d][:, j, :], GL[:, d * 2 + j:d * 2 + j + 1], puv[:, j, :], ALU.mult, ALU.add,
                                ['rtS%d' % d, K_, puk], ['rtS%d' % d])

                    gbank = mkbanks(st2, 3, "rtgk") if GJ_SPLIT[2] else None
                    gj = gate_jobs(l, last, st2, gbank, GJ_SPLIT[2]) if (GATE_PRE and GJ_SPLIT[2]) else []
                    run_pipelined(interleave([rtchain(i_, sd[0], sd[1]) for i_, sd in enumerate([(s_, d_) for s_ in range(NTL) for d_ in range(2)])], gj, 4), 2)
                    S.barrier()
                if stop == 'ret_chain':
                    return
                with contextlib.ExitStack() as st2:
                    if ('yc%d' % l) in debug:
                        dbgbuf = sb(st2, "dbgbuf", [128, 2, NT], F32)
                    AT = [sb(st2, "rtAT%d" % i, [128, 4, 128], BF16) for i in range(2)]
                    qd = [[sb(st2, "rtqd%d_%d" % (i, d), [128, 2, 128], BF16) for d in range(2)] for i in range(2)]
                    sq = [sb(st2, "rtsq%d" % i, [128, 2, 128], BF16) for i in range(2)]
                    rr = [sb(st2, "rtrr%d" % i, [128, 2, 128], F32) for i in range(2)]
                    ob = [sb(st2, "rtob%d" % i, [128, 2, 128], F32) for i in range(2)]
                    bank = mkbanks(st2, 8, "rtbk")

                    def rtout(t):
                        i2 = t % 2
                        tsl = slice(t * 128, (t + 1) * 128)
                        pas = [bank() for _ in range(2)]
                        for h in range(4):
                            hp = (h % 2) * 64
                            pav = pas[h % 2][0][:, 0:256].rearrange("p (a b) -> p a b", a=2)
                            mm(pav[:, h // 2, :], KR[hp:hp + 64, h // 2, tsl], QR[hp:hp + 64, h // 2, tsl], ['rtKR', 'rtQR'], [pas[h % 2][1]])
                        for d in range(2):
                            tt('pool', qd[i2][d][:], QR[:, :, tsl], QDEC[:, d, :, :], ALU.mult, ['rtQR', K_], ['rtqd%d_%d' % (i2, d)])
                        yield
                        for par in range(2):
                            pav = pas[par][0][:, 0:256].rearrange("p (a b) -> p a b", a=2)
                            tt('dve', AT[i2][:, par::2, :], pav, DS[:, par::2, :], ALU.mult, [pas[par][1], K_], ['rtAT%d' % i2])
                        yield
                        pos = [bank() for _ in range(2)]
                        povs = [pos[par][0][:, 0:256].rearrange("p (a b) -> p a b", a=2) for par in range(2)]
                        for h in range(4):
                            hp = (h % 2) * 64
                            pok = pos[h % 2][1]
                            reg = povs[h % 2][hp:hp + 64, h // 2, :]
                            mm(reg, VT[:, t, h * 64:(h + 1) * 64], AT[i2][:, h, :], ['rtVT', 'rtAT%d' % i2], [pok], start=True, stop=False)
                            for d in range(2):
                                mm(reg, Sall[d][hp:hp + 64, h // 2, t, :], qd[i2][d][hp:hp + 64, h // 2, :],
                                   ['rtSall%d_%d' % (d, t), 'rtqd%d_%d' % (i2, d)], [pok], start=False, stop=(d == 1))
                        yield
                        obk = 'rtob%d' % i2
                        cp('act', ob[i2][0:64], povs[0][0:64], [pos[0][1]], [obk])
                        cp('dve', ob[i2][64:128], povs[1][64:128], [pos[1][1]], [obk])
                        yield
                        pov = ob[i2][:]
                        pok = obk
                        if ('yc%d' % l) in debug:
                            cp('pool', dbgbuf[:, :, tsl], pov, [pok], ['dbgbuf'])
                        act(sq[i2][:], pov, AF.Square, [pok], ['rtsq%d' % i2])
                        yield
                        pss_, psk = bank()
                        psv = pss_[:, 0:256].rearrange("p (a b) -> p a b", a=2)
                        for j in range(2):
                            mm(psv[:, j, :], bonesb, sq[i2][:, j, :], ['cstb', 'rtsq%d' % i2], [psk])
                        yield
                        act(rr[i2][:], psv, AF.Sqrt, [psk], ['rtrr%d' % i2], bias=RMS_EPS, scale=1.0 / 64)
                        yield
                        S.op('dve', lambda e: e.reciprocal(out=rr[i2][:], in_=rr[i2][:]), reads=['rtrr%d' % i2], writes=['rtrr%d' % i2])
                        yield
                        tt('dve', rr[i2][:], pov, rr[i2][:], ALU.mult, [pok, 'rtrr%d' % i2], ['rtrr%d' % i2])
                        yield
                        tt('pool', Y[:, 2, :, tsl], rr[i2][:], zs[:, :, tsl], ALU.mult, ['rtrr%d' % i2, 'rtzs'], ['Y2'])

                    gj = gate_jobs(l, last, st2, bank, GJ_SPLIT[3]) if (GATE_PRE and GJ_SPLIT[3]) else []
                    run_pipelined(interleave([rtout(t) for t in range(NTL) if not (last and t < 2 and not debug)], gj, 3), 5)
                    if ('yc%d' % l) in debug:
                        dbg_dump('yc%d' % l, dbgbuf[:], [128, 2, NT], ['dbgbuf'])
                    S.barrier()
                S.barrier()
        PHASES['ret'] = phase_ret
        def phase_rw(l, h_src, last):
            with contextlib.ExitStack() as st:
                RB = sb(st, "rwRB", [128, 2, NT], BF16)
                KB = sb(st, "rwKB", [128, 2, NT], BF16)
                VB = sb(st, "rwVB", [128, 2, NT], BF16)
                LB = sb(st, "rwLB", [128, NT], BF16)
                zs = sb(st, "rwzs", [128, 2, NT], BF16)
                vT = sb(st, "rwvT", [128, NTL, 256], BF16)
                lw2b = sb(st, "rwlw2", [128, 2, 256], BF16)
                S.dma('pool', lw2b[:], dr['lw2'][l], writes=['rwlw2'])
                oka = sb(st, "rwoka", [128, 2], F32)
                ts('dve', oka[:], pv('ka'), -1.0, 1.0, ALU.mult, ALU.add, ['pvt'], ['rwoka'])
                seen_b, seen_o = set(), set()
                with contextlib.ExitStack() as st2:
                    ww = sb(st2, "rww", [128, 8, 1152], BF16)
                    for pc_ in range(9):
                        S.dma('pool', ww[:, :, pc_ * 128:(pc_ + 1) * 128], dr['w_in'][l][:, :, 2816 + pc_ * 128:2816 + (pc_ + 1) * 128], writes=['rww%d' % pc_])
                    XR = sb(st2, "rwXR", [128, NT + 4], F32)
                    XS = sb(st2, "rwXS", [128, NT], F32)
                    c0 = sb(st2, "rwc0", [128, 7], F32)
                    pp = [ps(st2, "rwpp%d" % i, [128, 512], F32) for i in range(3)]
                    ptr = [ps(st2, "rwptr%d" % i, [128, 8, 128], BF16) for i in range(2)]
                    o_mu, _ = PV['mu']
                    mu0, mu1 = pvt[:, o_mu:o_mu + 7], pvt[:, o_mu + 7:o_mu + 14]
                    tt('dve', c0[:], mu0, mu1, ALU.add, ['pvt'], ['rwc0'])
                    ts('dve', c0[:], c0[:], -1.0, 1.0, ALU.mult, ALU.add, ['rwc0'], ['rwc0'])
                    memset('pool', XR[:], 0.0, ['rwXR'])
                    cnt = 0
                    for m in range(9):
                        for (n0, nn) in BLOCKS:
                            p_, pk_ = pp[cnt % 3], 'rwpp%d' % (cnt % 3)
                            cnt += 1
                            for jj in range(8):
                                mm(p_[:, 0:nn], ww[:, jj, m * 128:(m + 1) * 128], uT[:, jj, n0:n0 + nn],
                                   ['rww%d' % m] + uTk[n0 // 128:(n0 + nn) // 128], [pk_], start=(jj == 0), stop=(jj == 7))
                            if m >= 7:
                                act(zs[:, m - 7, n0:n0 + nn], p_[:, 0:nn], AF.Silu, [pk_, 'pvt'], ['rwzs'], bias=pv('bin', 22 + m))
                            else:
                                xo = 1 if n0 < 256 else 3
                                act(XR[:, n0 + xo:n0 + xo + nn], p_[:, 0:nn], AF.Identity, [pk_, 'pvt'], ['rwXR'], bias=pv('bin', 22 + m))
                        if m >= 7:
                            continue
                        for (b0, ln, o0) in ((1, 256, 0), (259, 2048, 256)):
                            ts('dve', XS[:, o0:o0 + ln], XR[:, b0:b0 + ln], c0[:, m:m + 1], None, ALU.mult, None, ['rwXR', 'rwc0'], ['rwXS'])
                            stt(XS[:, o0:o0 + ln], XR[:, b0 - 1:b0 - 1 + ln], mu0[:, m:m + 1], XS[:, o0:o0 + ln], ALU.mult, ALU.add,
                                ['rwXR', 'pvt', 'rwXS'], ['rwXS'])
                            stt(XS[:, o0:o0 + ln], XR[:, b0 + 1:b0 + 1 + ln], mu1[:, m:m + 1], XS[:, o0:o0 + ln], ALU.mult, ALU.add,
                                ['rwXR', 'pvt', 'rwXS'], ['rwXS'])
                        if m < 6:
                            dstT, dk = [(RB, 'rwRB'), (KB, 'rwKB'), (VB, 'rwVB')][m // 2]
                            cp('act', dstT[:, m % 2, :], XS[:], ['rwXS'], [dk])
                        else:
                            act(LB[0:64, :], XS[0:64, :], AF.Tanh, ['rwXS'], ['rwLB'])
                            cp('pool', LB[64:128, :], XS[64:128, :], ['rwXS'], ['rwLB'])
                    for t in range(NTL):
                        pr, prk = ptr[t % 2], 'rwptr%d' % (t % 2)
                        for j in range(2):
                            tr(pr[:, j, :], VB[:, j, t * 128:(t + 1) * 128], identb, ['rwVB', 'cstb'], [prk])
                        cp('dve' if t % 2 == 0 else 'act', vT[:, t, :], pr[:, 0:2, :].rearrange("p a b -> p (a b)"), [prk], ['rwvT'])
                    S.barrier()
                if stop == 'rw_proj':
                    return
                OS = sb(st, "rwOS", [128, 2, NT], F32)
                with contextlib.ExitStack() as st2:
                    def B(name, shape, dt=BF16):
                        return sb(st2, "rw_" + name, shape, dt), "rw_" + name
                    R64, R64k = B("R64", [128, 256], BF16)
                    memset('pool', R64[:], 1.0, [R64k])
                    memset('pool', R64[:, 0:256:64], 0.0, [R64k])
                    LW, LWk = B("LW", [128, 2, 128], F32)
                    SA, SAk = B("SA", [128, 2, 128], F32)
                    LGm, LGk = B("LG", [128, 2, 128], F32)
                    U0, U0k = LGm, LGk
                    EG, EGk = B("EG", [128, 2, 128], F32)
                    ENG, ENGk = B("ENG", [128, 2, 128], F32)
                    EGM, EGMk = B("EGM", [128, 2, 128], F32)
                    TA, TAk = B("TA", [128, 2, 128], F32)
                    TB_, TBk = B("TB", [128, 2, 128], F32)
                    SQ, SQk = B("SQ", [128, 2, 128])
                    RKD, RKDk = SQ, SQk
                    OBt = (None, None)
                    Zst = [B("Z%d" % d, [128, 2, 64], F32) for d in range(2)]
                    BUF = [dict() for _ in range(2)]
                    for d_ in range(2):
                        BUF[d_]['KKN'] = B("KKN_%d" % d_, [128, 2, 128])
                        BUF[d_]['KT'] = [B("KT_%d_%d" % (d_, s_), [128, 3, 2, 128]) for s_ in range(2)]
                        BUF[d_]['RT'] = [B("RT_%d_%d" % (d_, s_), [128, 2, 128]) for s_ in range(2)]
                        for j_ in range(2):
                            sfx = "_%d_%d" % (d_, j_)
                            SB = dict()
                            SB['TM'] = B("TM" + sfx, [128, 3, 128])
                            for nm_ in ('A1T', 'A2T', 'A3T', 'A4T', 'ALT', 'Tm', 'TTm', 'Xb', 'RHS', 'BYb'):
                                SB[nm_] = B(nm_ + sfx, [128, 2, 128])
                            SB['NY'] = B("NY" + sfx, [128, 2, 64])
                            SB['RH'] = B("RH" + sfx, [128, 128])
                            SB['GTb'] = B("GTb" + sfx, [128, 2, 128])
                            SB['ZLG'] = B("ZLG" + sfx, [128, 2, 64], F32)
                            SB['Z0b'] = B("Z0b" + sfx, [128, 2, 64])
                            BUF[d_][j_] = SB
                        BUF[d_]['GLt'] = [B("GLt_%d_%d" % (d_, s_), [128, 2, 2], F32) for s_ in range(2)]
                    banks = [ps(st2, "rwbank%d" % i, [128, 512], F32) for i in range(8)]
                    bcnt = [0]

                    def bank():
                        i = bcnt[0] % 8
                        bcnt[0] += 1
                        return banks[i], 'rwbank%d' % i
                    for d in range(2):
                        memset('pool', Zst[d][0][:], 0.0, [Zst[d][1], 'rw_Zs_%d_0' % d, 'rw_Zs_%d_1' % d])
                    for d_ in range(2):
                        for j_ in range(2):
                            memset('pool', BUF[d_][j_]['GTb'][0][:], 0.0, [BUF[d_][j_]['GTb'][1]])
                    orders = [list(range(NTL)), [1, 0] + list(range(NTL - 1, 1, -1))]
                    bc3 = lambda ap: ap.unsqueeze(2).broadcast_to([128, 2, 128])
                    def prep(d, t, slot):
                        KKN, KKNk = BUF[d]['KKN']
                        KT, KTk = BUF[d]['KT'][slot]
                        RTb, RTk = BUF[d]['RT'][slot]
                        GLt, GLk = BUF[d]['GLt'][slot]
                        tsl = slice(t * 128, (t + 1) * 128)
                        rev = (d == 1)
                        Z, Zk = Zst[d]
                        plw, plwk = bank()
                        pla, plak = bank()
                        plwv = plw[:, 0:256].rearrange("p (j t) -> p j t", j=2)
                        plav = pla[:, 0:256].rearrange("p (j t) -> p j t", j=2)
                        wb_ = 32 * d
                        for j in range(2):
                            mm(plwv[:, j, :], lw2b[wb_:wb_ + 16, d, j * 128:(j + 1) * 128], LB[wb_:wb_ + 16, tsl], ['rwlw2', 'rwLB'], [plwk])
                        for j in range(2):
                            mm(plav[:, j, :], lw2b[64:96, d, j * 128:(j + 1) * 128], LB[64:96, tsl], ['rwlw2', 'rwLB'], [plak])
                        for j in range(2):
                            act(LW[:, j, :], plwv[:, j, :], AF.Sigmoid, [plwk, 'pvt'], [LWk], bias=pv('w0', d * 2 + j))
                            act(SA[:, j, :], plav[:, j, :], AF.Sigmoid, [plak, 'pvt'], [SAk], bias=pv('a0', d * 2 + j))
                        ts('dve', LW[:], LW[:], -0.6065306597126334, None, ALU.mult, None, [LWk], [LWk])
                        yield
                        lwf = LW[:].rearrange("p a b -> p (a b)")
                        lgf = LGm[:].rearrange("p a b -> p (a b)")
                        if not rev:
                            S.op('dve', lambda e: e.tensor_tensor_scan(out=lgf, data0=R64[:], data1=lwf, initial=0.0, op0=ALU.mult, op1=ALU.add),
                                 reads=[LWk, R64k], writes=[LGk])
                        else:
                            S.op('dve', lambda e: e.tensor_tensor_scan(out=lgf[:, ::-1], data0=R64[:], data1=lwf[:, ::-1], initial=0.0,
                                                                       op0=ALU.mult, op1=ALU.add), reads=[LWk, R64k], writes=[LGk])
                        act(EG[:], LGm[:], AF.Exp, [LGk], [EGk])
                        yield
                        act(ENG[:], LGm[:], AF.Exp, [LGk], [ENGk], scale=-1.0)
                        yield
                        tt('pool', TA[:], LGm[:], LW[:], ALU.subtract, [LGk, LWk], [TAk])
                        yield
                        act(EGM[:], TA[:], AF.Exp, [TAk], [EGMk])
                        yield
                        gsrc = EG[:, :, 63::64] if not rev else EG[:, :, 0::64]
                        cp('pool', GLt[:], gsrc, [EGk], [GLk])
                        yield
                        tt('dve', TA[:], KB[:, :, tsl], bc3(pv('kk')), ALU.mult, ['rwKB', 'pvt', TAk], [TAk])
                        yield
                        act(SQ[:], TA[:], AF.Square, [TAk], [SQk])
                        yield
                        pss_, pssk = bank()
                        pssv = pss_[:, 0:256].rearrange("p (a b) -> p a b", a=2)
                        for j in range(2):
                            mm(pssv[:, j, :], bonesb, SQ[:, j, :], ['cstb', SQk], [pssk])
                        act(TB_[:], pssv, AF.Sqrt, [pssk], [TBk])
                        yield
                        ts('dve', TB_[:], TB_[:], 1e-12, None, ALU.max, None, [TBk], [TBk])
                        yield
                        S.op('dve', lambda e: e.reciprocal(out=TB_[:], in_=TB_[:]), reads=[TBk], writes=[TBk])
                        tt('dve', KKN[:], TA[:], TB_[:], ALU.mult, [TAk, TBk], [KKNk])
                        yield
                        tt('pool', KT[:, 0], KKN[:], EGM[:], ALU.mult, [KKNk, EGMk], [KTk])
                        yield
                        tt('dve', TA[:], SA[:], ENG[:], ALU.mult, [SAk, ENGk, TAk], [TAk])
                        yield
                        tt('pool', KT[:, 1], KKN[:], TA[:], ALU.mult, [KKNk, TAk], [KTk])
                        yield
                        tt('dve', U0[:], SA[:], bc3(pv('ka')), ALU.mult, [SAk, 'pvt'], [U0k])
                        yield
                        tt('dve', U0[:], U0[:], bc3(oka[:]), ALU.add, [U0k, 'rwoka'], [U0k])
                        yield
                        tt('pool', TB_[:], U0[:], ENG[:], ALU.mult, [U0k, ENGk, TBk], [TBk])
                        yield
                        tt('pool', KT[:, 2], KB[:, :, tsl], TB_[:], ALU.mult, ['rwKB', TBk], [KTk])
                        yield
                        tt('dve', RTb[:], RB[:, :, tsl], EG[:], ALU.mult, ['rwRB', EGk], [RTk])
                        yield
                        tt('dve', U0[:], U0[:], KB[:, :, tsl], ALU.mult, [U0k, 'rwKB'], [U0k])
                        yield
                        tt('dve', U0[:], U0[:], bc3(pv('rk')), ALU.mult, [U0k, 'pvt'], [U0k])
                        yield
                        tt('pool', RKD[:], U0[:], RB[:, :, tsl], ALU.mult, [U0k, 'rwRB'], [RKDk])
                        yield
                        pbn, pbnk = bank()
                        pbnv = pbn[:, 0:256].rearrange("p (a b) -> p a b", a=2)
                        for j in range(2):
                            mm(pbnv[:, j, :], bonesb, RKD[:, j, :], ['cstb', RKDk], [pbnk])
                        if t not in seen_b:
                            seen_b.add(t)
                            tt('dve', Y[:, 3, :, tsl], pbnv, VB[:, :, tsl], ALU.mult, [pbnk, 'rwVB'], ['Y3'])
                        else:
                            tt('dve', TA[:], pbnv, VB[:, :, tsl], ALU.mult, [pbnk, 'rwVB', TAk], [TAk])
                            tt('pool', Y[:, 3, :, tsl], Y[:, 3, :, tsl], TA[:], ALU.add, ['Y3', TAk], ['Y3'])

                    def prep_pair(step):
                        for d_ in range(2):
                            yield from prep(d_, orders[d_][step], step % 2)

                    def unit(d, t, slot):
                        KT, KTk = BUF[d]['KT'][slot]
                        RTb, RTk = BUF[d]['RT'][slot]
                        GLt, GLk = BUF[d]['GLt'][slot]
                        tsl = slice(t * 128, (t + 1) * 128)
                        rev = (d == 1)
                        subs = [stream(d, j, t, rev, tsl, KT, KTk, RTb, RTk, GLt, GLk) for j in range(2)]
                        while subs:
                            for g in list(subs):
                                try:
                                    next(g)
                                except StopIteration:
                                    subs.remove(g)
                                yield

                    def stream(d, j, t, rev, tsl, KT, KTk, RTb, RTk, GLt, GLk):
                        SB = BUF[d][j]
                        TM, TMk = SB['TM']
                        A1T, A1k = SB['A1T']
                        A2T, A2k = SB['A2T']
                        A3T, A3k = SB['A3T']
                        A4T, A4k = SB['A4T']
                        ALT, ALk = SB['ALT']
                        Tm, Tmk = SB['Tm']
                        TTm, TTk = SB['TTm']
                        Xb, Xbk = SB['Xb']
                        RHS, RHSk = SB['RHS']
                        BYb, BYk = SB['BYb']
                        NY, NYk = SB['NY']
                        RH, RHk = SB['RH']
                        GTb, GTk = SB['GTb']
                        ZLG, ZLGk = SB['ZLG']
                        Z0b, Z0k = SB['Z0b']
                        Z, _zk = Zst[d]
                        Zk = 'rw_Zs_%d_%d' % (d, j)
                        ptb, ptbk = bank()
                        ptv = ptb[:].bitcast(BF16).rearrange("p (a b) -> p a b", a=8)
                        for x in range(3):
                            tr(ptv[:, x, :], KT[:, x, j, :], identb, [KTk, 'cstb'], [ptbk])
                        cp('act', TM[:], ptv[:, 0:3, :], [ptbk], [TMk])
                        yield

                        def amat(dst, dstk, li, ri_src, ri_k, mslot):
                            pas = []
                            for par in range(2):
                                hp = par * 64
                                pa, pak = bank()
                                rhs = (RTb[hp:hp + 64, j, :] if ri_src is None else KT[hp:hp + 64, ri_src, j, :])
                                mm(pa[:, 0:128], KT[hp:hp + 64, li, j, :], rhs, [KTk, ri_k], [pak])
                                pas.append((pa, pak))
                            return pas

                        def aevac(pas, dst, dstk, mslot):
                            for par, (pa, pak) in enumerate(pas):
                                if mslot is None:
                                    cp('act', dst[:, par, :], pa[:, 0:128], [pak], [dstk])
                                else:
                                    tt('dve', dst[:, par, :], pa[:, 0:128], maskb[:, mslot, :], ALU.mult, [pak, 'maskb'], [dstk])
                        for (dst, dstk, li, rs, rk, ms) in ((A1T, A1k, 1, 0, KTk, None), (A2T, A2k, 2, 0, KTk, 2 + d),
                                                            (A3T, A3k, 1, None, RTk, 4 + d), (A4T, A4k, 2, None, RTk, 4 + d)):
                            pas = amat(dst, dstk, li, rs, rk, ms)
                            aevac(pas, dst, dstk, ms)
                            yield
                        idb2 = identb.unsqueeze(1).broadcast_to([128, 2, 128])
                        cp('pool', Tm[:], idb2, ['cstb'], [Tmk])
                        cp('pool', TTm[:], idb2, ['cstb'], [TTk])
                        for lv in range(6):
                            tt('pool', ALT[:], A1T[:], maskb[:, 6 + d * 6 + lv, :].unsqueeze(1).broadcast_to([128, 2, 128]), ALU.mult,
                               [A1k, 'maskb'], [ALk])
                            yield
                            px, pxk = bank()
                            pxv = px[:, 0:256].rearrange("p (h t) -> p h t", h=2)
                            for par in range(2):
                                mm(pxv[:, par, :], ALT[:, par, :], Tm[:, par, :], [ALk, Tmk], [pxk])
                            cp('act', Xb[:], pxv, [pxk], [Xbk])
                            yield
                            py_, pyk = bank()
                            pyv = py_[:].rearrange("p (x h t) -> p x h t", x=2, h=2)
                            for par in range(2):
                                mm(pyv[:, 0, par, :], Xb[:, par, :], TTm[:, par, :], [Xbk, TTk], [pyk])
                            if lv < 5:
                                for par in range(2):
                                    mm(pyv[:, 1, par, :], TTm[:, par, :], Xb[:, par, :], [Xbk, TTk], [pyk])
                            if lv < 5:
                                tt('dve', Tm[:], Tm[:], pyv[:, 1], ALU.subtract, [Tmk, pyk], [Tmk])
                            tt('dve', TTm[:], TTm[:], pyv[:, 0], ALU.subtract, [TTk, pyk], [TTk])
                            yield
                        pw, pwk = bank()
                        pwv = pw[:, 0:128].rearrange("p (h v) -> p h v", h=2)
                        for par in range(2):
                            h = 2 * j + par
                            mm(pwv[:, par, :], A2T[:, par, :], vT[:, t, h * 64:(h + 1) * 64], [A2k, 'rwvT'], [pwk])
                        cp('pool', RHS[:, :, 0:64], TM[:, 0, :].rearrange("p (h k) -> p h k", h=2), [TMk], [RHSk])
                        cp('act', RHS[:, :, 64:128], pwv, [pwk], [RHSk])
                        yield
                        pby, pbyk = bank()
                        pbyv = pby[:, 0:256].rearrange("p (h t) -> p h t", h=2)
                        for par in range(2):
                            mm(pbyv[:, par, :], TTm[:, par, :], RHS[:, par, :], [TTk, RHSk], [pbyk])
                        cp('act', BYb[:], pbyv, [pbyk], [BYk])
                        yield
                        ts('pool', NY[:], BYb[:, :, 64:128], -1.0, 0.0, ALU.mult, ALU.add, [BYk], [NYk])
                        pr_, prk = bank()
                        for par in range(2):
                            hp = par * 64
                            mm(pr_[hp:hp + 64, 0:128], BYb[:, par, 0:64], A3T[:, par, :], [BYk, A3k], [prk])
                        tt('dve', RH[:], RTb[:, j, :], pr_[:, 0:128], ALU.subtract, [RTk, prk], [RHk])
                        yield
                        for c in range(2):
                            cs = slice(c * 64, (c + 1) * 64)
                            pg_, pgk = bank()
                            pgv = pg_[:, 0:128].rearrange("p (x v) -> p x v", x=2)
                            for par in range(2):
                                hp = par * 64
                                h = 2 * j + par
                                hc = slice(h * 64, (h + 1) * 64)
                                pc = slice(par * 64, (par + 1) * 64)
                                mm(pgv[hp:hp + 64, 0, :], BYb[cs, par, 0:64], TM[cs, 1, pc], [BYk, TMk], [pgk])
                                mm(pgv[hp:hp + 64, 1, :], TM[cs, 2, pc], vT[cs, t, hc], [TMk, 'rwvT'], [pgk], start=True, stop=False)
                                mm(pgv[hp:hp + 64, 1, :], TM[cs, 1, pc], NY[cs, par, :], [TMk, NYk], [pgk], start=False, stop=True)
                            for par in range(2):
                                hp = par * 64
                                tt('dve', GTb[hp:hp + 64, c, hp:hp + 64], cstf[hp:hp + 64, 4, 0:64], pgv[hp:hp + 64, 0, :], ALU.subtract,
                                   ['cstf', pgk], [GTk])
                            ts('dve', ZLG[:, c, :], pgv[:, 1, :], GLt[:, j, c:c + 1], None, ALU.mult, None, [pgk, GLk], [ZLGk])
                            yield
                        for c in ((0, 1) if not rev else (1, 0)):
                            cp('act', Z0b[:, c, :], Z[:, j, :], [Zk], [Z0k])
                            yield
                            pn, pnk = bank()
                            mm(pn[:, 0:64], GTb[:, c, :], Z0b[:, c, :], [GTk, Z0k], [pnk])
                            stt(Z[:, j, :], pn[:, 0:64], GLt[:, j, c:c + 1], ZLG[:, c, :], ALU.mult, ALU.add, [pnk, GLk, ZLGk, Zk], [Zk])
                            yield
                        for par in range(2):
                            hp = par * 64
                            h = 2 * j + par
                            hc = slice(h * 64, (h + 1) * 64)
                            po_, pok = bank()
                            reg = po_[hp:hp + 64, 0:128]
                            mm(reg, vT[:, t, hc], A4T[:, par, :], ['rwvT', A4k], [pok], start=True, stop=False)
                            mm(reg, NY[:, par, :], A3T[:, par, :], [NYk, A3k], [pok], start=False, stop=False)
                            for c in range(2):
                                mm(reg[:, c * 64:(c + 1) * 64], Z0b[hp:hp + 64, c, :], RH[hp:hp + 64, c * 64:(c + 1) * 64],
                                   [Z0k, RHk], [pok], start=False, stop=(c == 1))
                            osl = OS[hp:hp + 64, j, tsl]
                            osk = 'rwOS%d_%d' % (t, j)
                            if (t, j, par) not in seen_o:
                                seen_o.add((t, j, par))
                                cp('dve' if par == 0 else 'act', osl, reg, [pok], [osk])
                            else:
                                tt('dve', osl, osl, reg, ALU.add, [pok, osk], [osk])
                            yield

                    for _ in prep_pair(0):
                        pass
                    for step in range(NTL):
                        gens = [unit(d, orders[d][step], step % 2) for d in range(2)]
                        if step + 1 < NTL:
                            gens.append(prep_pair(step + 1))
                        while gens:
                            for g in list(gens):
                                try:
                                    next(g)
                                except StopIteration:
                                    gens.remove(g)
                    S.barrier()
                if stop is not None and stop.startswith('rw_'):
                    return
                with contextlib.ExitStack() as st2:
                    ob = [sb(st2, "rwob%d" % i, [128, 2, 128], BF16) for i in range(2)]
                    cen = [sb(st2, "rwcen%d" % i, [128, 2, 128], F32) for i in range(2)]
                    rs = [sb(st2, "rwrs%d" % i, [128, 2, 128], F32) for i in range(2)]
                    pm_ = [ps(st2, "rwpm%d" % i, [128, 512], F32) for i in range(2)]
                    pv_ = [ps(st2, "rwpv%d" % i, [128, 512], F32) for i in range(2)]
                    for t in range(NTL):
                        i2 = t % 2
                        tsl = slice(t * 128, (t + 1) * 128)
                        osk = 'rwOS%d_0' % t
                        osk1 = 'rwOS%d_1' % t
                        cp('act', ob[i2][:], OS[:, :, tsl], [osk, osk1], ['rwob%d' % i2])
                        pmv = pm_[i2][:, 0:256].rearrange("p (a b) -> p a b", a=2)
                        for j in range(2):
                            mm(pmv[:, j, :], bonesb, ob[i2][:, j, :], ['cstb', 'rwob%d' % i2], ['rwpm%d' % i2])
                        stt(cen[i2][:], pmv, -1.0 / 64, OS[:, :, tsl], ALU.mult, ALU.add, ['rwpm%d' % i2, osk, osk1], ['rwcen%d' % i2])
                        act(ob[i2][:], cen[i2][:], AF.Square, ['rwcen%d' % i2], ['rwob%d' % i2])
                        pvv = pv_[i2][:, 0:256].rearrange("p (a b) -> p a b", a=2)
                        for j in range(2):
                            mm(pvv[:, j, :], bonesb, ob[i2][:, j, :], ['cstb', 'rwob%d' % i2], ['rwpv%d' % i2])
                        act(rs[i2][:], pvv, AF.Sqrt, ['rwpv%d' % i2], ['rwrs%d' % i2], bias=RW_GN_EPS, scale=1.0 / 64)
                        S.op('dve', lambda e: e.reciprocal(out=rs[i2][:], in_=rs[i2][:]), reads=['rwrs%d' % i2], writes=['rwrs%d' % i2])
                        tt('dve', cen[i2][:], cen[i2][:], rs[i2][:], ALU.mult, ['rwcen%d' % i2, 'rwrs%d' % i2], ['rwcen%d' % i2])
                        tt('pool', cen[i2][:], cen[i2][:], bc3(pv('gnw')), ALU.mult, ['rwcen%d' % i2, 'pvt'], ['rwcen%d' % i2])
                        tt('pool', cen[i2][:], cen[i2][:], bc3(pv('gnb')), ALU.add, ['rwcen%d' % i2, 'pvt'], ['rwcen%d' % i2])
                        tt('dve', cen[i2][:], cen[i2][:], Y[:, 3, :, tsl], ALU.add, ['rwcen%d' % i2, 'Y3'], ['rwcen%d' % i2])
                        if ('yd%d' % l) in debug:
                            cp('act', OS[:, :, tsl], cen[i2][:], ['rwcen%d' % i2], [osk, osk1])
                        tt('dve', Y[:, 3, :, tsl], cen[i2][:], zs[:, :, tsl], ALU.mult, ['rwcen%d' % i2, 'rwzs'], ['Y3'])
                    if ('yd%d' % l) in debug:
                        dbg_dump('yd%d' % l, OS[:], [128, 2, NT], ['rwOS%d_%d' % (t, j_) for t in range(NTL) for j_ in range(2)])
                    S.barrier()
                S.barrier()
        PHASES['rw'] = phase_rw
        def phase_merge(l, h_src, last):
            h_dst = out_d if last else h1_d
            with contextlib.ExitStack() as st:
                MG = sb(st, "mgMG", [128, 8, NT], BF16)
                wbr = sb(st, "mgwbr", [128, 4, 2, DM], BF16)
                S.dma('pool', wbr[:], dr['wbr'][l], writes=['mgwbr'])
                with contextlib.ExitStack() as st2:
                    wg = [sb(st2, "mgwg%d" % i, [128, 8, 4, 128], BF16) for i in range(2)]
                    sg = [sb(st2, "mgsg%d" % i, [128, 512], BF16) for i in range(3)]
                    ac = [sb(st2, "mgac%d" % i, [128, 512], F32) for i in range(2)]
                    tm = [sb(st2, "mgtm%d" % i, [128, 512], F32) for i in range(2)]
                    pgl = [ps(st2, "mgpg%d" % i, [128, 512], F32) for i in range(3)]
                    pbr = [ps(st2, "mgpb%d" % i, [128, 512], F32) for i in range(3)]
                    cg = 0
                    ca = 0
                    def load_wg(dt_):
                        for k in range(4):
                            if (l, k * 8 + dt_) in pre_sg:
                                continue
                            c0 = 3968 + k * 1024 + dt_ * 128
                            S.dma('pool', wg[dt_ % 2][:, :, k, :], dr['w_in'][l][:, :, c0:c0 + 128], writes=['mgwg%d' % (dt_ % 2)])
                    load_wg(0)
                    for dt_ in range(8):
                        w_, wk_ = wg[dt_ % 2], 'mgwg%d' % (dt_ % 2)
                        if dt_ + 1 < 8:
                            load_wg(dt_ + 1)
                        for (n0, nn) in BLOCKS:
                            if last and n0 < 256:
                                continue
                            a_, ak_ = ac[ca % 2], 'mgac%d' % (ca % 2)
                            t_, tk_ = tm[ca % 2], 'mgtm%d' % (ca % 2)
                            ca += 1
                            for k in range(4):
                                pg_, pgk_ = pgl[cg % 3], 'mgpg%d' % (cg % 3)
                                pb_, pbk_ = pbr[cg % 3], 'mgpb%d' % (cg % 3)
                                s_, sk_ = sg[cg % 3], 'mgsg%d' % (cg % 3)
                                cg += 1
                                if (l, k * 8 + dt_) in pre_sg:
                                    S.dma('sp' if cg % 2 == 0 else 'act', s_[:, 0:nn], sgd[k * 8 + dt_][:, n0:n0 + nn], reads=['sgd'], writes=[sk_])
                                else:
                                    for jj in range(8):
                                        mm(pg_[:, 0:nn], w_[:, jj, k, :], uT[:, jj, n0:n0 + nn], [wk_] + uTk[n0 // 128:(n0 + nn) // 128], [pgk_],
                                           start=(jj == 0), stop=(jj == 7))
                                    act(s_[:, 0:nn], pg_[:, 0:nn], AF.Sigmoid, [pgk_, 'pvt'], [sk_], bias=pv('bin', 31 + k * 8 + dt_))
                                for jc in range(2):
                                    mm(pb_[:, 0:nn], wbr[:, k, jc, dt_ * 128:(dt_ + 1) * 128], Y[:, k, jc, n0:n0 + nn], ['mgwbr', 'Y%d' % k], [pbk_],
                                       start=(jc == 0), stop=(jc == 1))
                                if k == 0:
                                    tt('dve', a_[:, 0:nn], pb_[:, 0:nn], s_[:, 0:nn], ALU.mult, [pbk_, sk_], [ak_])
                                else:
                                    tt('dve', t_[:, 0:nn], pb_[:, 0:nn], s_[:, 0:nn], ALU.mult, [pbk_, sk_], [tk_])
                                    if k < 3:
                                        tt('pool', a_[:, 0:nn], a_[:, 0:nn], t_[:, 0:nn], ALU.add, [ak_, tk_], [ak_])
                                    else:
                                        tt('pool', MG[:, dt_, n0:n0 + nn], a_[:, 0:nn], t_[:, 0:nn], ALU.add, [ak_, tk_], ['mgMG%d' % (n0 // 512 if n0 else 9)])
                    S.barrier()
                if ('merged%d' % l) in debug:
                    with contextlib.ExitStack() as st2:
                        mf = sb(st2, "mgf", [128, 8, NT], F32)
                        cp('dve', mf[:], MG[:], ['mgMG%d' % i for i in (9, 0, 1, 2, 3)], ['mgf'])
                        dbg_dump('merged%d' % l, mf[:], [128, 8, NT], ['mgf'])
                        S.barrier()
                with contextlib.ExitStack() as st2:
                    wo = sb(st2, "mgwo", [128, 8, DM], BF16)
                    S.dma('pool', wo[:], dr['wout'][l], writes=['mgwo'])
                    rows = sb(st2, "mgrows", [128, 3, DM], F32)
                    S.dma('sp', rows[:], dr['rows'][l][:, 0:3072].rearrange("p (a b) -> p a b", a=3), writes=['mgrows'])
                    hin_ = [sb(st2, "mghin%d" % i, [128, DM], F32) for i in range(2)]
                    ot = [sb(st2, "mgot%d" % i, [128, DM], F32) for i in range(2)]
                    stat = [sb(st2, "mgst%d" % i, [128, 16], F32) for i in range(2)]
                    po = [[ps(st2, "mgpo%d_%d" % (i, hh), [128, 512], F32) for hh in range(2)] for i in range(2)]
                    def mgout(it, t):
                        i2 = it % 2
                        ci = 1 if t < 2 else 0
                        tsl = slice(t * 128, (t + 1) * 128)
                        mgk = 'mgMG%d' % (9 if t < 2 else (t - 2) // 4)
                        hk_, ok_, sk_ = 'mghin%d' % i2, 'mgot%d' % i2, 'mgst%d' % i2
                        hi, o_, sti = hin_[i2], ot[i2], stat[i2]
                        S.dma('sp', hi[:], h_src[t * 128:(t + 1) * 128, :], writes=[hk_])
                        for hh in range(2):
                            pk_ = 'mgpo%d_%d' % (i2, hh)
                            for jj in range(8):
                                mm(po[i2][hh][:], MG[:, jj, tsl], wo[:, jj, hh * 512:(hh + 1) * 512], [mgk, 'mgwo'], [pk_], start=(jj == 0), stop=(jj == 7))
                        yield
                        for hh in range(2):
                            pk_ = 'mgpo%d_%d' % (i2, hh)
                            tt('dve', o_[:, hh * 512:(hh + 1) * 512], po[i2][hh][:], rows[:, 0, hh * 512:(hh + 1) * 512], ALU.add, [pk_, 'mgrows'], [ok_])
                        yield
                        tt('dve', o_[:], o_[:], gatebc[:, ci, :], ALU.mult, [ok_, 'gatebc'], [ok_])
                        yield
                        stt(o_[:], hi[:], ALPHA, o_[:], ALU.mult, ALU.add, [hk_, ok_], [ok_])
                        yield
                        S.op('dve', lambda e: e.bn_stats(out=sti[:, 0:6], in_=o_[:, 0:512]), reads=[ok_], writes=[sk_])
                        S.op('dve', lambda e: e.bn_stats(out=sti[:, 6:12], in_=o_[:, 512:1024]), reads=[ok_], writes=[sk_])
                        yield
                        S.op('dve', lambda e: e.bn_aggr(out=sti[:, 12:14], in_=sti[:, 0:12]), reads=[sk_], writes=[sk_])
                        yield
                        act(sti[:, 14:15], sti[:, 13:14], AF.Sqrt, [sk_], [sk_], bias=LN_EPS)
                        yield
                        S.op('dve', lambda e: e.reciprocal(out=sti[:, 14:15], in_=sti[:, 14:15]), reads=[sk_], writes=[sk_])
                        yield
                        stt(sti[:, 15:16], sti[:, 12:13], -1.0, sti[:, 14:15], ALU.mult, ALU.mult, [sk_], [sk_])
                        yield
                        act(o_[:], o_[:], AF.Identity, [ok_, sk_], [ok_], bias=sti[:, 15:16], scale=sti[:, 14:15])
                        yield
                        tt('pool', o_[:, 0:512], o_[:, 0:512], rows[:, 1, 0:512], ALU.mult, [ok_, 'mgrows'], [ok_])
                        tt('dve', o_[:, 512:1024], o_[:, 512:1024], rows[:, 1, 512:1024], ALU.mult, [ok_, 'mgrows'], [ok_])
                        yield
                        tt('pool', o_[:, 0:512], o_[:, 0:512], rows[:, 2, 0:512], ALU.add, [ok_, 'mgrows'], [ok_])
                        tt('dve', o_[:, 512:1024], o_[:, 512:1024], rows[:, 2, 512:1024], ALU.add, [ok_, 'mgrows'], [ok_])
                        yield
                        if last:
                            S.dma('sp', out_d[(t - 2) * 128:(t - 1) * 128, :], o_[:], reads=[ok_], writes=['outfinal'])
                        else:
                            S.dma('sp', h1_d[t * 128:(t + 1) * 128, :], o_[:], reads=[ok_], writes=['h1'])

                    tl = [t for t in range(NTL) if not (last and t < 2)]
                    run_pipelined((mgout(i_, t) for i_, t in enumerate(tl)), 6)
                    S.barrier()
                S.barrier()
        PHASES['merge'] = phase_merge
        for l in range(nlayers):
            last = (l == nlayers - 1)
            h_src = dr['hin'] if l == 0 else h1_d
            S.dma('sp', pvt[:], dr['pv'][l], writes=['pvt'])
            with contextlib.ExitStack() as st:
                adw = [sb(st, "adw%d" % i, [128, 8, 512], F32) for i in range(2)]
                scb = sb(st, "scb", [128, 2, 8, 128], F32)
                grow = sb(st, "grow", [128, DM], F32)
                pm0 = ps(st, "pm0", [128, 16, 2], F32)
                pg = [ps(st, "pg%d" % i, [128, 512], F32) for i in range(2)]
                for i in range(2):
                    cp('dve', scb[:, i], silc[:, :, i:i + 1].broadcast_to([128, 8, 128]), ['silc'], ['scb'])
                S.dma('sp', grow[:], dr['rows'][l][:, 3072:4096], writes=['grow'])
                for ch in range(6):
                    buf = adw[ch % 2]
                    bk = 'adw%d' % (ch % 2)
                    S.dma('sp' if ch % 2 == 0 else 'act', buf[:], dr['ada_w'][l][:, :, ch * 512:(ch + 1) * 512], writes=[bk])
                    if ch < 4:
                        for mloc in range(4):
                            m = ch * 4 + mloc
                            for j in range(8):
                                mm(pm0[:, m, :], buf[:, j, mloc * 128:(mloc + 1) * 128], silc[:, j, :], [bk, 'silc'],
                                   ['pm0'], start=(j == 0), stop=(j == 7))
                    else:
                        for i in range(2):
                            for j in range(8):
                                mm(pg[i][:], scb[:, i, j, :], buf[:, j, :], [bk, 'scb'], ['pg%d' % i],
                                   start=(j == 0), stop=(j == 7))
                            tt('dve', gatebc[:, i, (ch - 4) * 512:(ch - 3) * 512], pg[i][:],
                               grow[:, (ch - 4) * 512:(ch - 3) * 512], ALU.add, ['pg%d' % i, 'grow'], ['gatebc'])
                tt('dve', modfm[:], pm0[:], pv('adab').unsqueeze(2).broadcast_to([128, 16, 2]), ALU.add,
                   ['pm0', 'pvt'], ['modfm'])
                ts('dve', modfm[:, 8:16, :], modfm[:, 8:16, :], 1.0, None, ALU.add, None, ['modfm'], ['modfm'])
                dbg_dump('modfm%d' % l, modfm[:], [128, 16, 2], ['modfm'])
                dbg_dump('gatebc%d' % l, gatebc[:], [128, 2, DM], ['gatebc'])
                S.barrier()
            with contextlib.ExitStack() as st:
                xin = [sb(st, "xin%d" % i, [128, DM], F32) for i in range(3)]
                xn = [sb(st, "xn%d" % i, [128, DM], BF16) for i in range(2)]
                stat = [sb(st, "stat%d" % i, [128, 16], F32) for i in range(3)]
                ptr = [ps(st, "ptr%d" % i, [128, 8, 128], BF16) for i in range(2)]
                def p1tile(t):
                    xi, xk = xin[t % 3], 'xin%d' % (t % 3)
                    sti, sk = stat[t % 3], 'stat%d' % (t % 3)
                    xo, xok = xn[t % 2], 'xn%d' % (t % 2)
                    pt, ptk = ptr[t % 2], 'ptr%d' % (t % 2)
                    ci = 1 if t < 2 else 0
                    S.dma('sp' if t % 2 == 0 else 'act', xi[:], h_src[t * 128:(t + 1) * 128, :], writes=[xk])
                    yield
                    S.op('dve', lambda e: e.bn_stats(out=sti[:, 0:6], in_=xi[:, 0:512]), reads=[xk], writes=[sk])
                    S.op('dve', lambda e: e.bn_stats(out=sti[:, 6:12], in_=xi[:, 512:1024]), reads=[xk], writes=[sk])
                    yield
                    S.op('dve', lambda e: e.bn_aggr(out=sti[:, 12:14], in_=sti[:, 0:12]), reads=[sk], writes=[sk])
                    yield
                    act(sti[:, 14:15], sti[:, 13:14], AF.Sqrt, [sk], [sk], bias=LN_EPS)
                    yield
                    S.op('dve', lambda e: e.reciprocal(out=sti[:, 14:15], in_=sti[:, 14:15]), reads=[sk], writes=[sk])
                    yield
                    stt(sti[:, 15:16], sti[:, 12:13], -1.0, sti[:, 14:15], ALU.mult, ALU.mult, [sk], [sk])
                    yield
                    act(xo[:], xi[:], AF.Identity, [xk, sk], [xok], bias=sti[:, 15:16], scale=sti[:, 14:15])
                    yield
                    for j in range(8):
                        tr(pt[:, j, :], xo[:, j * 128:(j + 1) * 128], identb, [xok, 'cstb'], [ptk])
                    yield
                    for j in range(8):
                        if j % 2 == 0:
                            act(uT[:, j, t * 128:(t + 1) * 128], pt[:, j, :], AF.Identity, [ptk, 'modfm'], ['uT%d' % t],
                                bias=modfm[:, j, ci:ci + 1], scale=modfm[:, 8 + j, ci:ci + 1])
                        else:
                            ts('dve', uT[:, j, t * 128:(t + 1) * 128], pt[:, j, :], modfm[:, 8 + j, ci:ci + 1],
                               modfm[:, j, ci:ci + 1], ALU.mult, ALU.add, [ptk, 'modfm'], ['uT%d' % t])

                run_pipelined((p1tile(t) for t in range(NTL)), 4)
                if ('uT%d' % l) in debug:
                    utf = sb(st, "utf", [128, 8, NT], F32)
                    cp('dve', utf[:], uT[:], ['uT%d' % t for t in range(NTL)], ['utf'])
                    dbg_dump('uT%d' % l, utf[:], [128, 8, NT], ['utf'])
                S.barrier()
            uTk = ['uT%d' % t for t in range(NTL)]

            for ph in list(PHASES):
                if ph in phases:
                    PHASES[ph](l, h_src, last)
            if ('h%d' % l) in debug and not last:
                d_ = dbg_out('h%d' % l, [NT, DM])
                S.dma('sp', d_, h1_d, writes=['dbgout_h%d' % l])
                S.barrier()
            if ('Y%d' % l) in debug:
                with contextlib.ExitStack() as st:
                    yf = sb(st, "yf", [128, 4, 2, NT], F32)
                    cp('dve', yf[:], Y[:], ['Y0', 'Y1', 'Y2', 'Y3'], ['yf'])
                    dbg_dump('Y%d' % l, yf[:], [128, 4, 2, NT], ['yf'])
                    S.barrier()

        S.final_wait('sp', ['outfinal'] + ['dbgout_' + n for n in dbg_d])
    if MEMDBG:
        print('SBUF min remaining by prefix:', minrem)
    return nc, dbg_d


def kernel(**inputs):
    inp = {k: np.asarray(v) for k, v in inputs.items()}
    sh = prep_shared(inp)
    nc, _ = build()
    in_maps = []
    for b in range(8):
        m = dict(sh)
        m.update(prep_core(inp, b))
        in_maps.append(m)
    res = run_bass_kernel_spmd(nc, in_maps, core_ids=list(range(8)))
    return np.stack([np.asarray(res.results[b]['out'], dtype=np.float32) for b in range(8)], 0)
```

```python
import contextlib
import numpy as np
import concourse.bass as bass
import concourse.mybir as mybir
from concourse.bass_utils import run_bass_kernel_spmd

F32 = mybir.dt.float32
BF16 = mybir.dt.bfloat16
AF = mybir.ActivationFunctionType
ALU = mybir.AluOpType
AX = mybir.AxisListType

NT = 2304
NTL = 18
DM = 1024
NCOL = 8064
BLOCKS = [(0, 256), (256, 512), (768, 512), (1280, 512), (1792, 512)]
LN_EPS = 1e-5
RMS_EPS = 1e-6
RW_GN_EPS = 64e-5
ALPHA = (2 * 2) ** 0.25
PI = float(np.pi)
MEMDBG = False
GATE_PRE = False
S5_STAGGER = 6
GJ_SPLIT = [[], [], [], []]
GJ_S5 = list(range(32))


class Sched:
    NDMA = 16

    def __init__(self, nc, same_engine_waits=True):
        self.nc = nc
        self.same = same_engine_waits
        self.eng = dict(pe=nc.tensor, act=nc.scalar, dve=nc.vector, pool=nc.gpsimd, sp=nc.sync)
        self.E = {n: dict(cnt=0, known={}) for n in self.eng}
        self.dq = {'sp': ['dsp%d' % i for i in range(8)], 'act': ['dac%d' % i for i in range(4)],
                   'pool': ['dpl%d' % i for i in range(8)]}
        self.dmas = {n: dict(cnt=0) for q in self.dq.values() for n in q}
        self.dma_rr = {'sp': 0, 'act': 0, 'pool': 0}
        self.lastw = {}
        self.readers = {}
        self.sems = None
        self.nins = 0

    def sem_names(self):
        return list(self.E.keys()) + list(self.dmas.keys())

    def _deps(self, reads, writes):
        deps = {}

        def add(w):
            if w is not None:
                deps[w[0]] = max(deps.get(w[0], 0), w[1])
        for k in reads:
            add(self.lastw.get(k))
        for k in writes:
            add(self.lastw.get(k))
            for r in self.readers.get(k, ()):
                add(r)
        return deps

    def _waits(self, en, deps):
        E = self.E[en]
        waits = []
        for d, v in deps.items():
            if d == en and (en == 'pe' or not self.same):
                continue
            if E['known'].get(d, 0) < v:
                waits.append((d, v))
                E['known'][d] = v
        return waits

    def _record(self, ident, reads, writes):
        for k in writes:
            self.lastw[k] = ident
            self.readers[k] = []
        for k in reads:
            self.readers.setdefault(k, []).append(ident)

    def _emit(self, en, waits, fn, inc):
        eng = self.eng[en]
        for d, v in waits:
            eng.wait_ge(self.sems[d], v)
        if fn is not None:
            fn(eng).then_inc(self.sems[inc[0]], inc[1])
            self.nins += 1

    def op(self, en, fn, reads=(), writes=()):
        E = self.E[en]
        waits = self._waits(en, self._deps(reads, writes))
        E['cnt'] += 1
        self._emit(en, waits, fn, (en, 1))
        self._record((en, E['cnt']), reads, writes)

    def dma(self, en, out, in_, reads=(), writes=(), **kw):
        dn = self.dq[en][self.dma_rr[en]]
        self.dma_rr[en] = (self.dma_rr[en] + 1) % len(self.dq[en])
        Dq = self.dmas[dn]
        deps = self._deps(reads, writes)
        if Dq['cnt'] > 0:
            deps[dn] = max(deps.get(dn, 0), Dq['cnt'])
        waits = self._waits(en, deps)
        Dq['cnt'] += 16
        self._emit(en, waits, (lambda e: e.dma_start(out=out, in_=in_, **kw)), (dn, 16))
        self._record((dn, Dq['cnt']), reads, writes)

    def barrier(self):
        cur = {n: self.E[n]['cnt'] for n in self.E}
        cur.update({n: self.dmas[n]['cnt'] for n in self.dmas})
        for en in self.E:
            waits = self._waits(en, {d: v for d, v in cur.items() if v > 0})
            self._emit(en, waits, None, None)

    def final_wait(self, en, keys):
        self._emit(en, self._waits(en, self._deps(keys, ())), None, None)


PV = {}


def _pv_layout():
    off = 0
    for name, n in [('bin', 63), ('s5d', 2), ('glub', 2), ('hglb', 8), ('hgnw', 2), ('rdec', 4), ('mu', 14),
                    ('w0', 4), ('a0', 4), ('kk', 2), ('ka', 2), ('rk', 2), ('gnw', 2), ('gnb', 2), ('adab', 16),
                    ('lamre', 16), ('lamim', 16), ('ldt', 16), ('rdech', 8)]:
        PV[name] = (off, n)
        off += n
    return off


NPV = _pv_layout()


def _colmap():
    cm = list(range(0, 3584))
    lora = [-1] * 128
    for r in range(16):
        lora[r] = 3584 + r
        lora[32 + r] = 3600 + r
        lora[64 + r] = 3616 + r
        lora[80 + r] = 3632 + r
    cm += lora
    cm += list(range(3648, 3904))
    cm += list(range(3904, 8000))
    return np.array(cm)


CMAP = _colmap()


def _fm(v):
    return np.ascontiguousarray(v.reshape(-1, 128).T)


def _masks():
    t = np.arange(128)
    s_, t_ = t[:, None], t[None, :]
    m = []
    b32 = (s_ // 32) == (t_ // 32)
    b64 = (s_ // 64) == (t_ // 64)
    m.append(b32 & (t_ >= s_))
    m.append(b32 & (t_ <= s_))
    m.append(b64 & (t_ > s_))
    m.append(b64 & (t_ < s_))
    m.append(b64 & (t_ >= s_))
    m.append(b64 & (t_ <= s_))
    for d in range(2):
        for lv in range(6):
            sz = 1 << lv
            blk = (s_ // (2 * sz)) == (t_ // (2 * sz))
            hs, ht = (s_ // sz) % 2, (t_ // sz) % 2
            if d == 0:
                m.append(blk & (ht == 1) & (hs == 0))
            else:
                m.append(blk & (ht == 0) & (hs == 1))
    return np.stack([x.astype(np.float32) for x in m], 1)


def _rot_tables():
    n = 16
    freqs = 10000.0 ** (-np.arange(n, dtype=np.float32) / n)
    tt = np.arange(2048)
    rows = (tt // 64).astype(np.float32)
    cols = (tt % 64).astype(np.float32)
    cos = np.zeros((128, 2048), np.float32)
    sins = np.zeros((128, 2048), np.float32)
    pm = np.zeros((128, 128), np.float32)
    for p in range(128):
        i = p % 64
        pos = rows if i < 32 else cols
        ii = i % 32
        ang = pos * freqs[ii % 16]
        cos[p] = np.cos(ang)
        if ii < 16:
            sins[p] = -np.sin(ang)
            partner = p + 16
        else:
            sins[p] = np.sin(ang)
            partner = p - 16
        pm[partner, p] = 1.0
    return cos, sins, pm


def prep_shared(inp):
    sh = {}
    L = 2
    w_in = inp['w_in']
    wn = np.zeros((L, 1024, NCOL), np.float32)
    valid = CMAP >= 0
    wn[:, :, valid] = w_in[:, :, CMAP[valid]]
    sh['w_in'] = np.ascontiguousarray(wn.reshape(L, 8, 128, NCOL).transpose(0, 2, 1, 3))
    bn = np.zeros((L, NCOL), np.float32)
    bn[:, valid] = inp['b_in'][:, CMAP[valid]]
    sh['ada_w'] = np.ascontiguousarray(inp['ada_w'].reshape(L, 8, 128, 3072).transpose(0, 2, 1, 3))
    pv = np.zeros((L, 128, NPV), np.float32)

    def put(l, name, arr):
        o, n = PV[name]
        assert arr.shape == (128, n), (name, arr.shape)
        pv[l, :, o:o + n] = arr
    for l in range(L):
        put(l, 'bin', _fm(bn[l]))
        put(l, 's5d', _fm(inp['s5_d'][l]))
        put(l, 'glub', _fm(inp['s5_glu_b'][l]))
        put(l, 'hglb', np.concatenate([_fm(inp['hg_lb'][ll, d]) for ll in range(2) for d in range(2)], 1))
        put(l, 'hgnw', _fm(inp['hg_norm_w'][l]))
        rd = np.zeros((128, 4), np.float32)
        for d in range(2):
            for j in range(2):
                rd[:64, d * 2 + j] = inp['ret_decay'][l, d, 2 * j]
                rd[64:, d * 2 + j] = inp['ret_decay'][l, d, 2 * j + 1]
        put(l, 'rdec', rd)
        put(l, 'rdech', np.ascontiguousarray(np.broadcast_to(inp['ret_decay'][l].reshape(1, 8), (128, 8))))
        mu = np.zeros((2, 7 * 128), np.float32)
        mu[:, :768] = inp['rw_mu'][l][:, :768]
        lv = CMAP[3584:3712]
        ok = lv >= 0
        mu[:, 768:896][:, ok] = inp['rw_mu'][l][:, lv[ok] - 2816]
        put(l, 'mu', np.concatenate([_fm(mu[0]), _fm(mu[1])], 1))
        put(l, 'w0', np.concatenate([_fm(inp['rw_w0'][l, d]) for d in range(2)], 1))
        put(l, 'a0', np.concatenate([_fm(inp['rw_a0'][l, d]) for d in range(2)], 1))
        for nm, key in [('kk', 'rw_kk'), ('ka', 'rw_ka'), ('rk', 'rw_rk'), ('gnw', 'rw_gn_w'), ('gnb', 'rw_gn_b')]:
            put(l, nm, _fm(inp[key][l]))
        put(l, 'adab', _fm(inp['ada_b'][l][:2048]))
        for nm, key in [('lamre', 's5_lam_re'), ('lamim', 's5_lam_im')]:
            a = inp[key][l].reshape(2, 8, 2, 64)
            put(l, nm, np.ascontiguousarray(a.transpose(2, 3, 0, 1).reshape(128, 16)))
        a = np.broadcast_to(inp['s5_log_dt'][l].reshape(2, 8, 2, 1), (2, 8, 2, 64))
        put(l, 'ldt', np.ascontiguousarray(a.transpose(2, 3, 0, 1).reshape(128, 16)))
    sh['pv'] = pv
    bt = np.zeros((L, 128, 2, 4, 2, 128), np.float32)
    ct = np.zeros((L, 128, 8, 2, 128), np.float32)
    for l in range(L):
        for g in range(16):
            i, g2 = g // 2, g % 2
            for q in range(16):
                c = g * 16 + q
                j, p = c // 128, c % 128
                bt[l, p, j, i % 4, 0, g2 * 64:(g2 + 1) * 64] = inp['s5_b_re'][l, g, :, q]
                bt[l, p, j, i % 4, 1, g2 * 64:(g2 + 1) * 64] = inp['s5_b_im'][l, g, :, q]
            m0 = (i % 4) * 32 + g2 * 16
            ct[l, g2 * 64:(g2 + 1) * 64, i, 0, m0:m0 + 16] = inp['s5_c_re'][l, g].T
            ct[l, g2 * 64:(g2 + 1) * 64, i, 1, m0:m0 + 16] = inp['s5_c_im'][l, g].T
    sh['s5bt'] = bt
    sh['s5ct'] = ct
    sh['gluw'] = np.ascontiguousarray(inp['s5_glu_w'].reshape(L, 2, 128, 256).transpose(0, 2, 1, 3))
    lw2 = np.zeros((L, 128, 2, 256), np.float32)
    for l in range(L):
        lw2[l, 0:16, 0] = inp['rw_w2'][l, 0]
        lw2[l, 32:48, 1] = inp['rw_w2'][l, 1]
        lw2[l, 64:80, 0] = inp['rw_a2'][l, 0]
        lw2[l, 80:96, 1] = inp['rw_a2'][l, 1]
    sh['lw2'] = lw2
    sh['wbr'] = np.ascontiguousarray(inp['w_branch'].reshape(L, 4, 2, 128, 1024).transpose(0, 3, 1, 2, 4))
    sh['wout'] = np.ascontiguousarray(inp['w_out'].reshape(L, 8, 128, 1024).transpose(0, 2, 1, 3))
    rows = np.zeros((L, 128, 4096 + 512), np.float32)
    for l in range(L):
        rows[l, :, 0:1024] = inp['b_out'][l][None]
        rows[l, :, 1024:2048] = inp['ln_w'][l][None]
        rows[l, :, 2048:3072] = inp['ln_b'][l][None]
        rows[l, :, 3072:4096] = inp['ada_b'][l][None, 2048:3072]
        rows[l, :, 4096:4352] = inp['b_in'][l][None, 1280:1536]
        rows[l, :, 4352:4608] = inp['b_in'][l][None, 2304:2560]
    sh['rows'] = rows
    sh['masks'] = _masks()
    cos, sins, pm = _rot_tables()
    sh['rcos'] = cos
    sh['rsin'] = sins
    t = np.arange(128)
    cst = np.zeros((128, 9, 128), np.float32)
    cst[:, 0] = pm
    cst[:, 1] = ((t[:, None] // 64) == (t[None, :] // 64))
    cst[:, 2] = np.maximum(t[None, :] - t[:, None], 0)
    cst[:, 3] = np.maximum(t[:, None] - t[None, :], 0)
    cst[:, 4] = (t[None, :] >= t[:, None])
    cst[:, 5] = (t[None, :] <= t[:, None])
    cst[:, 6, :64] = ((t[:, None] % 64) == np.arange(64)[None, :])
    cst[:, 6, 64:68] = ((t[:, None] // 32) == np.arange(4)[None, :])
    cst[:, 6, 68] = 127 - t
    cst[:, 6, 69] = t
    cst[:, 7] = t[None, :] + 1.0
    cst[:, 8] = 128.0 - t[None, :]
    sh['cst'] = cst
    return sh


def prep_core(inp, b):
    pc = {}
    pc['hin'] = np.ascontiguousarray(np.concatenate([inp['ctx'][b], inp['x'][b]], 0))
    cv = np.stack([inp['c'][b], inp['c_ctx']], -1)
    pc['cvec'] = np.ascontiguousarray(cv.reshape(8, 128, 2).transpose(1, 0, 2))
    return pc


SHAPES = dict(hin=[NT, DM], cvec=[128, 8, 2], w_in=[2, 128, 8, NCOL], ada_w=[2, 128, 8, 3072], pv=[2, 128, NPV],
              s5bt=[2, 128, 2, 4, 2, 128], s5ct=[2, 128, 8, 2, 128], gluw=[2, 128, 2, 256], lw2=[2, 128, 2, 256],
              wbr=[2, 128, 4, 2, 1024], wout=[2, 128, 8, 1024], rows=[2, 128, 4608], masks=[128, 18, 128],
              rcos=[128, 2048], rsin=[128, 2048], cst=[128, 9, 128])


def build(debug=(), nlayers=2, phases=('s5', 'hg', 'ret', 'rw', 'merge'), stop=None):
    nc = bass.Bass("TRN2", target_bir_lowering=False)
    S = Sched(nc)
    dr = {k: nc.dram_tensor(k, list(v), F32, kind="ExternalInput").ap() for k, v in SHAPES.items()}
    out_d = nc.dram_tensor("out", [2048, DM], F32, kind="ExternalOutput").ap()
    h1_d = nc.dram_tensor("h1", [NT, DM], F32, kind="Internal").ap()
    sgd = nc.dram_tensor("sgd", [32, 128, NT], BF16, kind="Internal").ap()
    pre_sg = set()
    dbg_d = {}

    def dbg_out(name, shape):
        dbg_d[name] = nc.dram_tensor("dbg_" + name, list(shape), F32, kind="ExternalOutput").ap()
        return dbg_d[name]

    uid = [0]

    def key(p='k'):
        uid[0] += 1
        return '%s%d' % (p, uid[0])

    with contextlib.ExitStack() as top:
        S.sems = {n: top.enter_context(nc.semaphore(n)) for n in S.sem_names()}

        minrem = {}

        def sb(st, name, shape, dt=F32):
            uid[0] += 1
            t_ = st.enter_context(nc.sbuf_tensor("%s_%d" % (name, uid[0]), list(shape), dt))
            if MEMDBG:
                pre = name[:2]
                minrem[pre] = min(minrem.get(pre, 1 << 30), nc.sbuf_bytes_remaining)
            return t_

        def ps(st, name, shape, dt=F32):
            uid[0] += 1
            return st.enter_context(nc.psum_tensor("%s_%d" % (name, uid[0]), list(shape), dt))

        def mm(out, lhsT, rhs, r, w, start=True, stop=True):
            S.op('pe', lambda e: e.matmul(out, lhsT=lhsT, rhs=rhs, start=start, stop=stop), reads=r, writes=w)

        def tr(out, in_, ident, r, w):
            S.op('pe', lambda e: e.transpose(out, in_, ident), reads=r, writes=w)

        def act(out, in_, func, r, w, bias=0.0, scale=1.0):
            S.op('act', lambda e: e.activation(out=out, in_=in_, func=func, bias=bias, scale=scale), reads=r, writes=w)

        def tt(en, out, in0, in1, op, r, w):
            S.op(en, lambda e: e.tensor_tensor(out=out, in0=in0, in1=in1, op=op), reads=r, writes=w)

        def ts(en, out, in0, s1, s2, op0, op1, r, w):
            if s2 is None:
                S.op(en, lambda e: e.tensor_scalar(out=out, in0=in0, scalar1=s1, scalar2=None, op0=op0), reads=r, writes=w)
            else:
                S.op(en, lambda e: e.tensor_scalar(out=out, in0=in0, scalar1=s1, scalar2=s2, op0=op0, op1=op1),
                     reads=r, writes=w)

        def stt(out, in0, sc, in1, op0, op1, r, w):
            S.op('dve', lambda e: e.scalar_tensor_tensor(out=out, in0=in0, scalar=sc, in1=in1, op0=op0, op1=op1),
                 reads=r, writes=w)

        def cp(en, out, in_, r, w):
            if en == 'act':
                S.op('act', lambda e: e.copy(out=out, in_=in_), reads=r, writes=w)
            else:
                S.op(en, lambda e: e.tensor_copy(out=out, in_=in_), reads=r, writes=w)

        def memset(en, ap, val, w):
            S.op(en, lambda e: e.memset(ap, val), writes=w)

        def run_pipelined(gens, stagger):
            it = iter(gens)
            active, pending, rounds = [], True, 0
            while pending or active:
                if pending and rounds % stagger == 0:
                    try:
                        active.append(next(it))
                    except StopIteration:
                        pending = False
                for g in list(active):
                    try:
                        next(g)
                    except StopIteration:
                        active.remove(g)
                rounds += 1

        def mkbanks(st_, n, prefix):
            bl = [ps(st_, "%s%d" % (prefix, i), [128, 512], F32) for i in range(n)]
            cnt = [0]

            def bank():
                i = cnt[0] % n
                cnt[0] += 1
                return bl[i], '%s%d' % (prefix, i)
            return bank

        def gate_jobs(l, last, st_, bankfn, kds):
            wgt = [sb(st_, "gjw%d" % i, [128, 8, 128], BF16) for i in range(2)]
            sgs = [sb(st_, "gjs%d" % i, [128, 512], BF16) for i in range(2)]
            cnt = [0]

            def job(i, kd):
                k, dt_ = kd // 8, kd % 8
                w_, wk_ = wgt[i % 2], 'gjw%d' % (i % 2)
                c0 = 3968 + k * 1024 + dt_ * 128
                S.dma('pool', w_[:], dr['w_in'][l][:, :, c0:c0 + 128], writes=[wk_])
                yield
                for (n0, nn) in BLOCKS:
                    if last and n0 < 256:
                        continue
                    pg_, pgk_ = bankfn()
                    for jj in range(8):
                        mm(pg_[:, 0:nn], w_[:, jj, :], uT[:, jj, n0:n0 + nn], [wk_] + uTk[n0 // 128:(n0 + nn) // 128], [pgk_],
                           start=(jj == 0), stop=(jj == 7))
                    yield
                    c_ = cnt[0] % 2
                    cnt[0] += 1
                    act(sgs[c_][:, 0:nn], pg_[:, 0:nn], AF.Sigmoid, [pgk_, 'pvt'], ['gjs%d' % c_], bias=pv('bin', 31 + k * 8 + dt_))
                    yield
                    S.dma('sp', sgd[kd][:, n0:n0 + nn], sgs[c_][:, 0:nn], reads=['gjs%d' % c_], writes=['sgd'])
                    yield
                pre_sg.add((l, kd))
            return [job(i, kd) for i, kd in enumerate(kds)]

        def interleave(main, extra, every):
            out, ei = [], 0
            extra = list(extra)
            for i, g in enumerate(main):
                out.append(g)
                if (i + 1) % every == 0 and ei < len(extra):
                    out.append(extra[ei])
                    ei += 1
            out.extend(extra[ei:])
            return out

        def dbg_dump(name, ap, shape, r):
            if name in debug:
                d = dbg_out(name, shape)
                S.dma('sp', d, ap, reads=r, writes=['dbgout_' + name])

        cstb = sb(top, "cstb", [128, 3, 128], BF16)
        cstf = sb(top, "cstf", [128, 7, 128], F32)
        maskb = sb(top, "maskb", [128, 18, 128], BF16)
        silc = sb(top, "silc", [128, 8, 2], F32)
        S.dma('pool', cstb[:, 0:2, :], dr['cst'][:, 0:2, :], writes=['cstb'])
        S.dma('sp', cstf[:], dr['cst'][:, 2:9, :], writes=['cstf'])
        S.dma('pool', maskb[:], dr['masks'], writes=['maskb'])
        S.dma('sp', silc[:], dr['cvec'], writes=['silc'])
        memset('pool', cstb[:, 2, :], 0.0, ['cstb'])
        S.op('pool', lambda e: e.affine_select(out=cstb[:, 2, :], in_=cstb[:, 2, :], pattern=[[-1, 128]],
                                               compare_op=ALU.not_equal, fill=1.0, base=0, channel_multiplier=1),
             reads=['cstb'], writes=['cstb'])
        act(silc[:], silc[:], AF.Silu, ['silc'], ['silc'])
        identb = cstb[:, 2, :]
        bonesb = cstb[:, 1, :]

        uT = sb(top, "uT", [128, 8, NT], BF16)
        Y = sb(top, "Y", [128, 4, 2, NT], BF16)
        pvt = sb(top, "pvt", [128, NPV], F32)
        if debug:
            memset('pool', Y[:], 0.0, ['Y0', 'Y1', 'Y2', 'Y3'])
        modfm = sb(top, "modfm", [128, 16, 2], F32)
        gatebc = sb(top, "gatebc", [128, 2, DM], F32)

        def pv(name, j=None, n=1):
            o, cnt = PV[name]
            if j is None:
                return pvt[:, o:o + cnt]
            return pvt[:, o + j:o + j + n]

        PHASES = {}
        def proj_fm(st, wt, wk, mlist, evac, pp, ppk):
            cnt = 0
            for (n0, nn) in BLOCKS:
                for mi, m in enumerate(mlist):
                    p_, pk_ = pp[cnt % len(pp)], ppk[cnt % len(pp)]
                    cnt += 1
                    for j in range(8):
                        mm(p_[:, 0:nn], wt[:, j, m * 128:(m + 1) * 128], uT[:, j, n0:n0 + nn],
                           ['%s%d' % (wk, m // 2)] + uTk[n0 // 128:(n0 + nn) // 128], [pk_], start=(j == 0), stop=(j == 7))
                    evac(mi, m, n0, nn, p_, pk_)

        def phase_s5(l, h_src, last):
            L = 128
            with contextlib.ExitStack() as st:
                btb = sb(st, "btb", [128, 2, 4, 2, 128], BF16)
                ctb = sb(st, "ctb", [128, 8, 2, 128], BF16)
                glub = sb(st, "glub", [128, 2, 256], BF16)
                S.dma('pool', btb[:], dr['s5bt'][l], writes=['btb'])
                S.dma('pool', ctb[:], dr['s5ct'][l], writes=['ctb'])
                S.dma('pool', glub[:], dr['gluw'][l], writes=['glub'])
                ts('pool', ctb[:, :, 1, :], ctb[:, :, 1, :], -1.0, 0.0, ALU.mult, ALU.add, ['ctb'], ['ctb'])
                ub = sb(st, "s5u", [128, 2, NT], BF16)
                zs = sb(st, "s5z", [128, 2, NT], BF16)
                yacc = sb(st, "yacc", [128, 2, NT], F32)
                PT = sb(st, "s5PT", [128, 16, 2, L], F32)
                QT = sb(st, "s5QT", [128, 16, 2, L], F32)
                sst = sb(st, "s5st", [128, 16, 2], F32)
                ones = sb(st, "s5ones", [128, L], F32)
                memset('pool', yacc[:], 0.0, ['yacc'])
                memset('pool', sst[:], 0.0, ['sst'])
                memset('pool', ones[:], 1.0, ['s5ones'])
                with contextlib.ExitStack() as st2:
                    wsu = sb(st2, "wsu", [128, 8, 512], BF16)
                    for pc_ in range(2):
                        S.dma('pool', wsu[:, :, pc_ * 256:(pc_ + 1) * 256], dr['w_in'][l][:, :, pc_ * 256:(pc_ + 1) * 256], writes=['wsu%d' % pc_])
                    pp = [ps(st2, "s5pp%d" % i, [128, 512], F32) for i in range(2)]

                    def evac(mi, m, n0, nn, p_, pk_):
                        if m < 2:
                            act(ub[:, m, n0:n0 + nn], p_[:, 0:nn], AF.Identity, [pk_, 'pvt'], ['s5u'], bias=pv('bin', m))
                        else:
                            act(zs[:, m - 2, n0:n0 + nn], p_[:, 0:nn], AF.Silu, [pk_, 'pvt'], ['s5z'], bias=pv('bin', m))
                    proj_fm(st2, wsu, 'wsu', [0, 1, 2, 3], evac, pp, ['s5pp0', 's5pp1'])
                    sm = sb(st2, "s5sm", [128, 20, 16], F32)
                    K_ = 's5sm'

                    def Sm(i):
                        return sm[:, i, :]

                    def T2(o, a, b, op):
                        tt('dve', Sm(o), a if not isinstance(a, int) else Sm(a), b if not isinstance(b, int) else Sm(b), op,
                           [K_, 'pvt'], [K_])
                    lamre, lamim = pv('lamre'), pv('lamim')
                    act(Sm(0), pv('ldt'), AF.Exp, ['pvt'], [K_])
                    T2(1, lamre, 0, ALU.mult)
                    act(Sm(2), Sm(1), AF.Exp, [K_], [K_])
                    act(Sm(3), Sm(1), AF.Exp, [K_], [K_], scale=-1.0)
                    T2(4, lamim, 0, ALU.mult)
                    ts('dve', Sm(5), Sm(4), PI / 2, None, ALU.add, None, [K_], [K_])
                    for x in (4, 5):
                        for _ in range(4):
                            ts('dve', Sm(16), Sm(x), PI, 2 * PI, ALU.is_gt, ALU.mult, [K_], [K_])
                            T2(x, x, 16, ALU.subtract)
                    act(Sm(6), Sm(4), AF.Sin, [K_], [K_])
                    act(Sm(7), Sm(5), AF.Sin, [K_], [K_])
                    T2(8, 2, 7, ALU.mult)
                    T2(9, 2, 6, ALU.mult)
                    T2(10, 3, 7, ALU.mult)
                    stt(Sm(11), Sm(3), -1.0, Sm(6), ALU.mult, ALU.mult, [K_], [K_])
                    ts('dve', Sm(12), Sm(8), -1.0, None, ALU.add, None, [K_], [K_])
                    T2(16, lamre, lamre, ALU.mult)
                    T2(17, lamim, lamim, ALU.mult)
                    T2(13, 16, 17, ALU.add)
                    S.op('dve', lambda e: e.reciprocal(out=Sm(13), in_=Sm(13)), reads=[K_], writes=[K_])
                    T2(16, 12, lamre, ALU.mult)
                    T2(17, 9, lamim, ALU.mult)
                    T2(16, 16, 17, ALU.add)
                    T2(14, 16, 13, ALU.mult)
                    T2(16, 9, lamre, ALU.mult)
                    T2(17, 12, lamim, ALU.mult)
                    T2(16, 16, 17, ALU.subtract)
                    T2(15, 16, 13, ALU.mult)
                    tmpa = sb(st2, "s5ta", [128, 16, L], F32)
                    tmpb = sb(st2, "s5tb", [128, 16, L], F32)

                    def cmul_bc(dst_re, dst_im, src_re, src_im, s_re, s_im, m):
                        sr = s_re.unsqueeze(2).broadcast_to([128, 16, m])
                        si = s_im.unsqueeze(2).broadcast_to([128, 16, m])
                        ta, tb = tmpa[:, :, 0:m], tmpb[:, :, 0:m]
                        kk_ = ['s5tab', 's5ta', 's5tb', 's5tc', K_]
                        tt('dve', ta, src_re, sr, ALU.mult, kk_, ['s5ta'])
                        tt('dve', tb, src_im, si, ALU.mult, kk_, ['s5tb'])
                        tt('dve', dst_re, ta, tb, ALU.subtract, kk_, ['s5tab'])
                        tt('dve', ta, src_re, si, ALU.mult, kk_, ['s5ta'])
                        tt('dve', tb, src_im, sr, ALU.mult, kk_, ['s5tb'])
                        tt('dve', dst_im, ta, tb, ALU.add, kk_, ['s5tab'])
                    for (TB, a_re, a_im) in ((PT, 8, 9), (QT, 10, 11)):
                        cp('dve', TB[:, :, 0, 0], Sm(a_re), [K_], ['s5tab'])
                        cp('dve', TB[:, :, 1, 0], Sm(a_im), [K_], ['s5tab'])
                        m = 1
                        while m < L:
                            cmul_bc(TB[:, :, 0, m:2 * m], TB[:, :, 1, m:2 * m], TB[:, :, 0, 0:m], TB[:, :, 1, 0:m],
                                    TB[:, :, 0, m - 1], TB[:, :, 1, m - 1], m)
                            m *= 2
                    tmpc = sb(st2, "s5tc", [128, 16, L], F32)
                    cp('dve', tmpc[:], QT[:, :, 0, :], ['s5tab'], ['s5tc'])
                    cmul_bc(QT[:, :, 0, :], QT[:, :, 1, :], tmpc[:], QT[:, :, 1, :], Sm(14), Sm(15), L)
                    S.barrier()
                with contextlib.ExitStack() as st2:
                    NB = 8
                    xa = [sb(st2, "s5xa%d" % i, [128, 2, L], F32) for i in range(NB)]
                    xb_ = [sb(st2, "s5xb%d" % i, [128, 2, L], F32) for i in range(NB)]
                    cw = [sb(st2, "s5cw%d" % i, [128, 2, L], F32) for i in range(NB)]
                    hb = [sb(st2, "s5hb%d" % i, [128, 2, L], BF16) for i in range(NB)]
                    pbu = [ps(st2, "s5pb%d" % i, [128, 2, 2, L], F32) for i in range(4)]
                    py = [ps(st2, "s5py%d" % i, [128, 512], F32) for i in range(2)]
                    orders = [list(range(NTL)), [1, 0] + list(range(NTL - 1, 1, -1))]
                    def s5group(gi, step, d, j):
                        c = orders[d][step]
                        n0 = c * L
                        rev = (d == 1)
                        U = []
                        for ii in range(4):
                            un = gi * 4 + ii
                            bnk = (un // 2) % 4
                            U.append(dict(ii=ii, i=j * 4 + ii, q=d * 8 + j * 4 + ii, pb=pbu[bnk][:, un % 2], pbk='s5pb%d' % bnk,
                                          A=xa[un % NB], Ak='s5xa%d' % (un % NB), B=xb_[un % NB], Bk='s5xb%d' % (un % NB),
                                          C=cw[un % NB], Ck='s5cw%d' % (un % NB), H=hb[un % NB], Hk='s5hb%d' % (un % NB)))
                        for u in U:
                            for ri in range(2):
                                mm(u['pb'][:, ri, :], btb[:, j, u['ii'], ri, :], ub[:, j, n0:n0 + L], ['btb', 's5u'], [u['pbk']])
                        yield
                        for u in U:
                            src = u['pb'][:, :, ::-1] if rev else u['pb'][:, :, :]
                            tt('dve', u['A'][:], src, QT[:, u['q'], 0:1, :].broadcast_to([128, 2, L]), ALU.mult,
                               [u['pbk'], 's5tab'], [u['Ak']])
                        yield
                        for u in U:
                            src = u['pb'][:, ::-1, ::-1] if rev else u['pb'][:, ::-1, :]
                            tt('dve', u['B'][:], src, QT[:, u['q'], 1:2, :].broadcast_to([128, 2, L]), ALU.mult,
                               [u['pbk'], 's5tab'], [u['Bk']])
                        yield
                        for u in U:
                            tt('dve', u['A'][:, 0, :], u['A'][:, 0, :], u['B'][:, 0, :], ALU.subtract, [u['Ak'], u['Bk']], [u['Ak']])
                        yield
                        for u in U:
                            tt('dve', u['A'][:, 1, :], u['A'][:, 1, :], u['B'][:, 1, :], ALU.add, [u['Ak'], u['Bk']], [u['Ak']])
                        yield
                        for ri in range(2):
                            for u in U:
                                q = u['q']
                                S.op('dve', lambda e, u=u, ri=ri, q=q: e.tensor_tensor_scan(
                                    out=u['C'][:, ri, :], data0=ones[:], data1=u['A'][:, ri, :], initial=sst[:, q, ri:ri + 1],
                                    op0=ALU.mult, op1=ALU.add), reads=[u['Ak'], 's5ones', 'sst%d' % q, 'sst'], writes=[u['Ck']])
                            yield
                        for u in U:
                            tt('pool', u['A'][:], u['C'][:], PT[:, u['q'], 0:1, :].broadcast_to([128, 2, L]), ALU.mult,
                               [u['Ck'], 's5tab', u['Ak']], [u['Ak']])
                        yield
                        for u in U:
                            tt('pool', u['B'][:], u['C'][:, ::-1, :], PT[:, u['q'], 1:2, :].broadcast_to([128, 2, L]), ALU.mult,
                               [u['Ck'], 's5tab', u['Bk']], [u['Bk']])
                        yield
                        for u in U:
                            tt('pool', u['A'][:, 0, :], u['A'][:, 0, :], u['B'][:, 0, :], ALU.subtract, [u['Ak'], u['Bk']], [u['Ak']])
                        yield
                        for u in U:
                            tt('pool', u['A'][:, 1, :], u['A'][:, 1, :], u['B'][:, 1, :], ALU.add, [u['Ak'], u['Bk']], [u['Ak']])
                        yield
                        for u in U:
                            cp('pool', sst[:, u['q'], :], u['A'][:, :, L - 1], [u['Ak']], ['sst%d' % u['q']])
                        yield
                        for u in U:
                            hsrc = u['A'][:, :, ::-1] if rev else u['A'][:]
                            cp('act', u['H'][:], hsrc, [u['Ak']], [u['Hk']])
                        yield
                        pyr = py[gi % 2][:, 0:L]
                        pyk = 's5py%d' % (gi % 2)
                        for k_, u in enumerate(U):
                            for ri in range(2):
                                mm(pyr, ctb[:, u['i'], ri, :], u['H'][:, ri, :], ['ctb', u['Hk']], [pyk],
                                   start=(k_ == 0 and ri == 0), stop=(k_ == 3 and ri == 1))
                        yield
                        yield
                        yield
                        tt('dve', yacc[:, j, n0:n0 + L], yacc[:, j, n0:n0 + L], pyr, ALU.add, [pyk, 'yacc'], ['yacc'])

                    glist = [(step, d, j) for step in range(NTL) for d in range(2) for j in range(2)]
                    gbank = mkbanks(st2, 2, "s5gk") if (GATE_PRE and GJ_S5) else None
                    gj = gate_jobs(l, last, st2, gbank, GJ_S5) if (GATE_PRE and GJ_S5) else []
                    run_pipelined(interleave([s5group(gi, *g) for gi, g in enumerate(glist)], gj, 2), S5_STAGGER)
                    S.barrier()
                for j in range(2):
                    stt(yacc[:, j, :], ub[:, j, :], pv('s5d', j), yacc[:, j, :], ALU.mult, ALU.add, ['s5u', 'yacc', 'pvt'],
                        ['yacc'])
                dbg_dump('ya%d' % l, yacc[:], [128, 2, NT], ['yacc'])
                with contextlib.ExitStack() as st2:
                    t1 = [sb(st2, "s5g1_%d" % i, [128, 512], F32) for i in range(2)]
                    t2 = [sb(st2, "s5g2_%d" % i, [128, 512], BF16) for i in range(2)]
                    pg = [ps(st2, "s5pg%d" % i, [128, 512], F32) for i in range(2)]
                    cnt = 0
                    for (n0, nn) in BLOCKS:
                        for j in range(2):
                            a, ak = t1[cnt % 2], 's5g1_%d' % (cnt % 2)
                            cnt += 1
                            ysl = yacc[:, j, n0:n0 + nn]
                            act(a[:, 0:nn], ysl, AF.Square, ['yacc'], [ak])
                            ts('dve', a[:, 0:nn], a[:, 0:nn], 0.044715, 1.0, ALU.mult, ALU.add, [ak], [ak])
                            tt('dve', a[:, 0:nn], a[:, 0:nn], ysl, ALU.mult, [ak, 'yacc'], [ak])
                            act(a[:, 0:nn], a[:, 0:nn], AF.Sigmoid, [ak], [ak], scale=1.5957691216057308)
                            tt('dve', ub[:, j, n0:n0 + nn], a[:, 0:nn], ysl, ALU.mult, [ak, 'yacc'], ['s5u'])
                    cnt = 0
                    for (n0, nn) in BLOCKS:
                        for m in range(2):
                            p_, pk_ = pg[cnt % 2], 's5pg%d' % (cnt % 2)
                            b_, bk_ = t2[cnt % 2], 's5g2_%d' % (cnt % 2)
                            cnt += 1
                            for jc in range(2):
                                mm(p_[:, 0:nn], glub[:, jc, m * 128:(m + 1) * 128], ub[:, jc, n0:n0 + nn], ['glub', 's5u'], [pk_],
                                   start=(jc == 0), stop=(jc == 1))
                            act(b_[:, 0:nn], p_[:, 0:nn], AF.Sigmoid, [pk_, 'pvt'], [bk_], bias=pv('glub', m))
                            tt('dve', b_[:, 0:nn], b_[:, 0:nn], ub[:, m, n0:n0 + nn], ALU.mult, [bk_, 's5u'], [bk_])
                            tt('pool', Y[:, 0, m, n0:n0 + nn], b_[:, 0:nn], zs[:, m, n0:n0 + nn], ALU.mult, [bk_, 's5z'], ['Y0'])
                    S.barrier()
                S.barrier()
        PHASES['s5'] = phase_s5
        def phase_hg(l, h_src, last):
            with contextlib.ExitStack() as st:
                QP = [sb(st, "hgQP%d" % d, [128, 2, NT], BF16) for d in range(2)]
                KP = [sb(st, "hgKP%d" % d, [128, 2, NT], BF16) for d in range(2)]
                G = sb(st, "hgG", [128, 2, 72, 2], F32)
                VT = sb(st, "hgVT", [128, NTL, 256], BF16)
                zs = sb(st, "hgzs", [128, 2, NT], BF16)
                lbt = sb(st, "hglbt", [128, 2, 4], F32)
                if l == 0:
                    memset('pool', lbt[:, 0, :], 0.0, ['hglbt'])
                    memset('pool', lbt[:, 1, :], 1.0, ['hglbt'])
                else:
                    o_, _ = PV['hglb']
                    tt('dve', lbt[:, 0, :], pvt[:, o_ + 4:o_ + 8], pvt[:, o_:o_ + 4], ALU.subtract, ['pvt'], ['hglbt'])
                    act(lbt[:, 0, :], lbt[:, 0, :], AF.Sigmoid, ['hglbt'], ['hglbt'])
                    ts('dve', lbt[:, 1, :], lbt[:, 0, :], -1.0, 1.0, ALU.mult, ALU.add, ['hglbt'], ['hglbt'])
                with contextlib.ExitStack() as st2:
                    wh = sb(st2, "hgw", [128, 8, 1280], BF16)
                    for pc_ in (0, 4, 1, 2, 3):
                        S.dma('pool', wh[:, :, pc_ * 256:(pc_ + 1) * 256], dr['w_in'][l][:, :, 512 + pc_ * 256:512 + (pc_ + 1) * 256], writes=['hgw%d' % pc_])
                    brow = sb(st2, "hgbrow", [128, 256], F32)
                    S.dma('sp', brow[:], dr['rows'][l][:, 4096:4352], writes=['hgbrow'])
                    R32 = sb(st2, "hgR32", [128, 512], F32)
                    memset('pool', R32[:], 1.0, ['hgR32'])
                    memset('pool', R32[:, 0:512:32], 0.0, ['hgR32'])
                    QS = [sb(st2, "hgQS%d" % i, [128, 2, 512], BF16) for i in range(2)]
                    T = [[sb(st2, "hgT%d_%d" % (i, k), [128, 512], F32) for k in range(4)] for i in range(2)]
                    pp = [ps(st2, "hgpp%d" % i, [128, 512], F32) for i in range(3)]
                    pt = [ps(st2, "hgpt%d" % i, [128, 512], F32) for i in range(2)]
                    def hgproj(cnt, ic, m, n0, nn):
                        ukeys = uTk[n0 // 128:(n0 + nn) // 128]
                        p_, pk_ = pp[cnt % 3], 'hgpp%d' % (cnt % 3)
                        bi = (n0 // 512) % 2 if n0 else 0
                        for jj in range(8):
                            mm(p_[:, 0:nn], wh[:, jj, m * 128:(m + 1) * 128], uT[:, jj, n0:n0 + nn], ['hgw%d' % (m // 2)] + ukeys, [pk_],
                               start=(jj == 0), stop=(jj == 7))
                        yield
                        bias = pv('bin', 4 + m)
                        if m < 2:
                            act(QS[bi][:, m, 0:nn], p_[:, 0:nn], AF.Silu, [pk_, 'pvt'], ['hgQS%d' % bi], bias=bias)
                            return
                        if m >= 8:
                            act(zs[:, m - 8, n0:n0 + nn], p_[:, 0:nn], AF.Silu, [pk_, 'pvt'], ['hgzs'], bias=bias)
                            return
                        d, j = (m - 2) // 2, (m - 2) % 2
                        Ts = T[ic % 2]
                        Tk = ['hgT%d_%d' % (ic % 2, k) for k in range(4)]
                        t1, t2, t3, t4 = [x[:, 0:nn] for x in Ts]
                        act(t1, p_[:, 0:nn], AF.Sigmoid, [pk_, 'pvt'], [Tk[0]], bias=bias)
                        yield
                        ts('dve', t1, t1, lbt[:, 1, d * 2 + j:d * 2 + j + 1], lbt[:, 0, d * 2 + j:d * 2 + j + 1], ALU.mult, ALU.add,
                           [Tk[0], 'hglbt'], [Tk[0]])
                        yield
                        act(t2, t1, AF.Ln, [Tk[0]], [Tk[1]])
                        yield
                        if d == 0:
                            S.op('dve', lambda e: e.tensor_tensor_scan(out=t3, data0=R32[:, 0:nn], data1=t2, initial=0.0,
                                                                       op0=ALU.mult, op1=ALU.add),
                                 reads=[Tk[1], 'hgR32'], writes=[Tk[2]])
                        else:
                            S.op('dve', lambda e: e.tensor_tensor_scan(out=t3[:, ::-1],
                                                                       data0=R32[:, 0:nn], data1=t2[:, ::-1], initial=0.0,
                                                                       op0=ALU.mult, op1=ALU.add),
                                 reads=[Tk[1], 'hgR32'], writes=[Tk[2]])
                        yield
                        ts('dve', t3, t3, -80.0, None, ALU.max, None, [Tk[2]], [Tk[2]])
                        ts('dve', t1, t1, -1.0, 1.0, ALU.mult, ALU.add, [Tk[0]], [Tk[0]])
                        yield
                        act(t4, t3, AF.Exp, [Tk[2]], [Tk[3]])
                        act(t2, t3, AF.Exp, [Tk[2]], [Tk[1]], scale=-1.0)
                        yield
                        tt('pool', KP[d][:, j, n0:n0 + nn], t1, t2, ALU.mult, [Tk[0], Tk[1]], ['hgKP%d' % d])
                        tt('pool', QP[d][:, j, n0:n0 + nn], QS[bi][:, j, 0:nn], t4, ALU.mult, ['hgQS%d' % bi, Tk[3]], ['hgQP%d' % d])
                        c0 = n0 // 32
                        gsrc = t4[:, 31::32] if d == 0 else t4[:, 0::32]
                        cp('act', G[:, d, c0:c0 + nn // 32, j], gsrc, [Tk[3]], ['hgG'])

                    plist = []
                    cnt = 0
                    ic = 0
                    for (n0, nn) in BLOCKS:
                        for m in (0, 1, 8, 9, 2, 3, 4, 5):
                            plist.append((cnt, ic, m, n0, nn))
                            cnt += 1
                            if 2 <= m < 8:
                                ic += 1
                    run_pipelined((hgproj(*p) for p in plist), 4)
                    for t in range(NTL):
                        p_, pk_ = pt[t % 2], 'hgpt%d' % (t % 2)
                        for jj in range(8):
                            mm(p_[:, 0:256], uT[:, jj, t * 128:(t + 1) * 128], wh[:, jj, 768:1024], ['hgw3', uTk[t]], [pk_],
                               start=(jj == 0), stop=(jj == 7))
                        tt('dve', VT[:, t, :], p_[:, 0:256], brow[:], ALU.add, [pk_, 'hgbrow'], ['hgVT'])
                    S.barrier()
                Sall = [sb(st, "hgSall%d" % d, [128, 2, 72, 64], BF16) for d in range(2)]
                with contextlib.ExitStack() as st2:
                    Sst = [sb(st2, "hgS%d" % d, [128, 2, 64], F32) for d in range(2)]
                    kTm = [sb(st2, "hgkTm%d" % i, [128, 4, 256], BF16) for i in range(3)]
                    Ug = [sb(st2, "hgUg%d" % i, [128, 4, 2, 64], F32) for i in range(3)]
                    ptr = [ps(st2, "hgptr%d" % i, [128, 8, 128], BF16) for i in range(2)]
                    pU = [ps(st2, "hgpU%d" % i, [128, 4, 2, 64], F32) for i in range(3)]
                    orders = [list(range(NTL)), [1, 0] + list(range(NTL - 1, 1, -1))]
                    for d in range(2):
                        memset('pool', Sst[d][:], 0.0, ['hgS%d' % d])
                    def hgchain(it, step, d):
                        t = orders[d][step]
                        pr, prk = ptr[it % 2], 'hgptr%d' % (it % 2)
                        km, kmk = kTm[it % 3], 'hgkTm%d' % (it % 3)
                        pu, puk = pU[it % 3], 'hgpU%d' % (it % 3)
                        ug, ugk = Ug[it % 3], 'hgUg%d' % (it % 3)
                        for j in range(2):
                            tr(pr[:, j, :], KP[d][:, j, t * 128:(t + 1) * 128], identb, ['hgKP%d' % d, 'cstb'], [prk])
                        yield
                        for cc in range(4):
                            prf = pr[:, 0:2, :].rearrange("p a b -> p (a b)")
                            if cc % 2 == 0:
                                ts('dve', km[:, cc, :], prf, cstf[:, 4, 64 + cc:64 + cc + 1], None, ALU.mult, None, [prk, 'cstf'], [kmk])
                            else:
                                act(km[:, cc, :], prf, AF.Identity, [prk, 'cstf'], [kmk], scale=cstf[:, 4, 64 + cc:64 + cc + 1])
                        yield
                        for cc in range(4):
                            for h in range(4):
                                hp = (h % 2) * 64
                                mm(pu[hp:hp + 64, cc, h // 2, :], km[:, cc, h * 64:(h + 1) * 64], VT[:, t, h * 64:(h + 1) * 64],
                                   [kmk, 'hgVT'], [puk])
                        yield
                        tt('dve', ug[:], pu[:], G[:, d, t * 4:(t + 1) * 4, :].unsqueeze(3).broadcast_to([128, 4, 2, 64]), ALU.mult,
                           [puk, 'hgG'], [ugk])
                        yield
                        ccs = range(4) if d == 0 else range(3, -1, -1)
                        for cc in ccs:
                            c = t * 4 + cc
                            cp('act', Sall[d][:, :, c, :], Sst[d][:], ['hgS%d' % d], ['hgSall%d_%d' % (d, t)])
                            for j in range(2):
                                stt(Sst[d][:, j, :], Sst[d][:, j, :], G[:, d, c, j:j + 1], ug[:, cc, j, :], ALU.mult, ALU.add,
                                    ['hgS%d' % d, 'hgG', ugk], ['hgS%d' % d])
                            yield

                    gbank = mkbanks(st2, 3, "hggk") if GJ_SPLIT[0] else None
                    gj = gate_jobs(l, last, st2, gbank, GJ_SPLIT[0]) if (GATE_PRE and GJ_SPLIT[0]) else []
                    run_pipelined(interleave([hgchain(i_, sd[0], sd[1]) for i_, sd in enumerate([(s_, d_) for s_ in range(NTL) for d_ in range(2)])], gj, 3), 3)
                    S.barrier()
                with contextlib.ExitStack() as st2:
                    if ('yb%d' % l) in debug:
                        dbgbuf = sb(st2, "dbgbuf", [128, 2, NT], F32)
                    AT = [[sb(st2, "hgAT%d_%d" % (i, d), [128, 4, 128], BF16) for d in range(2)] for i in range(2)]
                    sq = [sb(st2, "hgsq%d" % i, [128, 2, 128], BF16) for i in range(2)]
                    rr = [sb(st2, "hgrr%d" % i, [128, 2, 128], F32) for i in range(2)]
                    ob = [sb(st2, "hgob%d" % i, [128, 2, 128], F32) for i in range(2)]
                    bank = mkbanks(st2, 8, "hgbk")

                    def hgout(t):
                        i2 = t % 2
                        tsl = slice(t * 128, (t + 1) * 128)
                        pas = {}
                        for d in range(2):
                            for par in range(2):
                                pas[(d, par)] = bank()
                            for h in range(4):
                                hp = (h % 2) * 64
                                pa, pak = pas[(d, h % 2)]
                                pav = pa[:, 0:256].rearrange("p (a b) -> p a b", a=2)
                                mm(pav[:, h // 2, :], KP[d][hp:hp + 64, h // 2, tsl], QP[d][hp:hp + 64, h // 2, tsl],
                                   ['hgKP%d' % d, 'hgQP%d' % d], [pak])
                        yield
                        for d in range(2):
                            for par in range(2):
                                pa, pak = pas[(d, par)]
                                pav = pa[:, 0:256].rearrange("p (a b) -> p a b", a=2)
                                tt('dve', AT[i2][d][:, par::2, :], pav, maskb[:, d, :].unsqueeze(1).broadcast_to([128, 2, 128]), ALU.mult,
                                   [pak, 'maskb'], ['hgAT%d_%d' % (i2, d)])
                        yield
                        pos = [bank() for _ in range(2)]
                        povs = [pos[par][0][:, 0:256].rearrange("p (a b) -> p a b", a=2) for par in range(2)]
                        for h in range(4):
                            hp = (h % 2) * 64
                            pok = pos[h % 2][1]
                            reg = povs[h % 2][hp:hp + 64, h // 2, :]
                            first = True
                            for d in range(2):
                                mm(reg, VT[:, t, h * 64:(h + 1) * 64], AT[i2][d][:, h, :], ['hgVT', 'hgAT%d_%d' % (i2, d)], [pok],
                                   start=first, stop=False)
                                first = False
                                for cc in range(4):
                                    c = t * 4 + cc
                                    mm(reg[:, cc * 32:(cc + 1) * 32], Sall[d][hp:hp + 64, h // 2, c, :],
                                       QP[d][hp:hp + 64, h // 2, t * 128 + cc * 32:t * 128 + (cc + 1) * 32],
                                       ['hgSall%d_%d' % (d, t), 'hgQP%d' % d], [pok], start=False, stop=(d == 1 and cc == 3))
                        yield
                        obk = 'hgob%d' % i2
                        cp('act', ob[i2][0:64], povs[0][0:64], [pos[0][1]], [obk])
                        cp('dve', ob[i2][64:128], povs[1][64:128], [pos[1][1]], [obk])
                        yield
                        pov = ob[i2][:]
                        pok = obk
                        if ('yb%d' % l) in debug:
                            cp('pool', dbgbuf[:, :, tsl], pov, [pok], ['dbgbuf'])
                        act(sq[i2][:], pov, AF.Square, [pok], ['hgsq%d' % i2])
                        yield
                        pss_, psk = bank()
                        psv = pss_[:, 0:256].rearrange("p (a b) -> p a b", a=2)
                        for j in range(2):
                            mm(psv[:, j, :], bonesb, sq[i2][:, j, :], ['cstb', 'hgsq%d' % i2], [psk])
                        yield
                        act(rr[i2][:], psv, AF.Sqrt, [psk], ['hgrr%d' % i2], bias=RMS_EPS, scale=1.0 / 64)
                        yield
                        S.op('dve', lambda e: e.reciprocal(out=rr[i2][:], in_=rr[i2][:]), reads=['hgrr%d' % i2], writes=['hgrr%d' % i2])
                        yield
                        tt('dve', rr[i2][:], pov, rr[i2][:], ALU.mult, [pok, 'hgrr%d' % i2], ['hgrr%d' % i2])
                        yield
                        for j in range(2):
                            stt(Y[:, 1, j, tsl], rr[i2][:, j, :], pv('hgnw', j), zs[:, j, tsl], ALU.mult, ALU.mult,
                                ['hgrr%d' % i2, 'pvt', 'hgzs'], ['Y1'])

                    gj = gate_jobs(l, last, st2, bank, GJ_SPLIT[1]) if (GATE_PRE and GJ_SPLIT[1]) else []
                    run_pipelined(interleave([hgout(t) for t in range(NTL) if not (last and t < 2 and not debug)], gj, 3), 5)
                    if ('yb%d' % l) in debug:
                        dbg_dump('yb%d' % l, dbgbuf[:], [128, 2, NT], ['dbgbuf'])
                    S.barrier()
                S.barrier()
        PHASES['hg'] = phase_hg
        def phase_ret(l, h_src, last):
            with contextlib.ExitStack() as st:
                QR = sb(st, "rtQR", [128, 2, NT], BF16)
                KR = sb(st, "rtKR", [128, 2, NT], BF16)
                VT = sb(st, "rtVT", [128, NTL, 256], BF16)
                zs = sb(st, "rtzs", [128, 2, NT], BF16)
                Sall = [sb(st, "rtSall%d" % d, [128, 2, NTL, 64], BF16) for d in range(2)]
                LG = sb(st, "rtLG", [128, 4], F32)
                GL = sb(st, "rtGL", [128, 4], F32)
                LGH = sb(st, "rtLGH", [128, 8], F32)
                QDEC = sb(st, "rtQDEC", [128, 2, 2, 128], F32)
                KDEC = sb(st, "rtKDEC", [128, 2, 4], F32)
                DS = sb(st, "rtDS", [128, 4, 128], F32)
                tb8 = sb(st, "rtb8", [128, 2], F32)
                K_ = 'rttab'
                act(LG[:], pv('rdec'), AF.Exp, ['pvt'], [K_])
                ts('dve', LG[:], LG[:], -1.0, None, ALU.mult, None, [K_], [K_])
                act(GL[:], LG[:], AF.Exp, [K_], [K_], scale=128.0)
                act(LGH[:], pv('rdech'), AF.Exp, ['pvt'], [K_])
                ts('dve', LGH[:], LGH[:], -1.0, None, ALU.mult, None, [K_], [K_])
                for d in range(2):
                    for j in range(2):
                        act(QDEC[:, d, j, :], cstf[:, 5 + d, :], AF.Exp, ['cstf', K_], [K_], scale=LG[:, d * 2 + j:d * 2 + j + 1])
                    act(KDEC[:, d, :], LGH[:, d * 4:(d + 1) * 4], AF.Exp, ['cstf', K_], [K_], scale=cstf[:, 4, 68 + d:69 + d])
                with contextlib.ExitStack() as st2:
                    ta = sb(st2, "rtta", [128, 128], F32)
                    tb = sb(st2, "rttb", [128, 128], F32)
                    for h in range(4):
                        act(ta[:], cstf[:, 0, :], AF.Exp, ['cstf', K_], ['rtta'], scale=LGH[:, h:h + 1])
                        tt('dve', ta[:], ta[:], cstf[:, 2, :], ALU.mult, ['rtta', 'cstf'], ['rtta'])
                        act(tb[:], cstf[:, 1, :], AF.Exp, ['cstf', K_], ['rttb'], scale=LGH[:, 4 + h:5 + h])
                        tt('dve', tb[:], tb[:], cstf[:, 3, :], ALU.mult, ['rttb', 'cstf'], ['rttb'])
                        tt('dve', DS[:, h, :], ta[:], tb[:], ALU.add, ['rtta', 'rttb'], [K_])
                    ts('dve', tb8[:], pv('bin', 16, 2), 0.125, None, ALU.mult, None, ['pvt'], [K_])
                    S.barrier()
                if stop == 'ret_tab':
                    return
                with contextlib.ExitStack() as st2:
                    wr = sb(st2, "rtw", [128, 8, 1024], BF16)
                    for pc_ in range(4):
                        S.dma('pool', wr[:, :, pc_ * 256:(pc_ + 1) * 256], dr['w_in'][l][:, :, 1792 + pc_ * 256:1792 + (pc_ + 1) * 256], writes=['rtw%d' % pc_])
                    brow = sb(st2, "rtbrow", [128, 256], F32)
                    S.dma('sp', brow[:], dr['rows'][l][:, 4352:4608], writes=['rtbrow'])
                    COS = sb(st2, "rtcos", [128, 2048], F32)
                    SIN = sb(st2, "rtsin", [128, 2048], F32)
                    permf = sb(st2, "rtperm", [128, 128], F32)
                    S.dma('sp', COS[:], dr['rcos'], writes=['rtcos'])
                    S.dma('act', SIN[:], dr['rsin'], writes=['rtsin'])
                    S.dma('sp', permf[:], dr['cst'][:, 0, :], writes=['rtperm'])
                    qf = [sb(st2, "rtqf%d" % i, [128, 512], F32) for i in range(2)]
                    t1 = [sb(st2, "rtt1_%d" % i, [128, 512], F32) for i in range(2)]
                    pp = [ps(st2, "rtpp%d" % i, [128, 512], F32) for i in range(2)]
                    pq = [ps(st2, "rtpq%d" % i, [128, 512], F32) for i in range(2)]
                    pt = [ps(st2, "rtpt%d" % i, [128, 512], F32) for i in range(2)]
                    def rtproj(cnt, rc, m, n0, nn):
                        ukeys = uTk[n0 // 128:(n0 + nn) // 128]
                        p_, pk_ = pp[cnt % 2], 'rtpp%d' % (cnt % 2)
                        for jj in range(8):
                            mm(p_[:, 0:nn], wr[:, jj, m * 128:(m + 1) * 128], uT[:, jj, n0:n0 + nn], ['rtw%d' % (m // 2)] + ukeys, [pk_],
                               start=(jj == 0), stop=(jj == 7))
                        yield
                        if m >= 6:
                            act(zs[:, m - 6, n0:n0 + nn], p_[:, 0:nn], AF.Silu, [pk_, 'pvt'], ['rtzs'], bias=pv('bin', 14 + m))
                            return
                        isk = m >= 2
                        j = m % 2
                        dst = (KR if isk else QR)[:, j, n0:n0 + nn]
                        dk = 'rtKR' if isk else 'rtQR'
                        if n0 < 256:
                            if isk:
                                act(dst, p_[:, 0:nn], AF.Identity, [pk_, K_], [dk], bias=tb8[:, j:j + 1], scale=0.125)
                            else:
                                act(dst, p_[:, 0:nn], AF.Identity, [pk_, 'pvt'], [dk], bias=pv('bin', 14 + m))
                            return
                        q_, qk_ = qf[rc % 2], 'rtqf%d' % (rc % 2)
                        a_, ak_ = t1[rc % 2], 'rtt1_%d' % (rc % 2)
                        r_, rk_ = pq[rc % 2], 'rtpq%d' % (rc % 2)
                        if isk:
                            act(q_[:, 0:nn], p_[:, 0:nn], AF.Identity, [pk_, K_], [qk_], bias=tb8[:, j:j + 1], scale=0.125)
                        else:
                            act(q_[:, 0:nn], p_[:, 0:nn], AF.Identity, [pk_, 'pvt'], [qk_], bias=pv('bin', 14 + m))
                        yield
                        mm(r_[:, 0:nn], permf[:], q_[:, 0:nn], ['rtperm', qk_], [rk_])
                        yield
                        tsl = slice(n0 - 256, n0 - 256 + nn)
                        tt('dve', a_[:, 0:nn], r_[:, 0:nn], SIN[:, tsl], ALU.mult, [rk_, 'rtsin'], [ak_])
                        tt('pool', q_[:, 0:nn], q_[:, 0:nn], COS[:, tsl], ALU.mult, [qk_, 'rtcos'], [qk_])
                        yield
                        tt('dve', dst, a_[:, 0:nn], q_[:, 0:nn], ALU.add, [ak_, qk_], [dk])

                    plist = []
                    cnt = 0
                    rc = 0
                    for (n0, nn) in BLOCKS:
                        for m in (0, 1, 2, 3, 6, 7):
                            plist.append((cnt, rc, m, n0, nn))
                            cnt += 1
                            if m < 6 and n0 >= 256:
                                rc += 1
                    run_pipelined((rtproj(*p) for p in plist), 2)
                    for t in range(NTL):
                        p_, pk_ = pt[t % 2], 'rtpt%d' % (t % 2)
                        for jj in range(8):
                            mm(p_[:, 0:256], uT[:, jj, t * 128:(t + 1) * 128], wr[:, jj, 512:768], ['rtw2', uTk[t]], [pk_],
                               start=(jj == 0), stop=(jj == 7))
                        tt('dve', VT[:, t, :], p_[:, 0:256], brow[:], ALU.add, [pk_, 'rtbrow'], ['rtVT'])
                    S.barrier()
                if stop == 'ret_proj':
                    return
                with contextlib.ExitStack() as st2:
                    Sst = [sb(st2, "rtS%d" % d, [128, 2, 64], F32) for d in range(2)]
                    kT = [sb(st2, "rtkT%d" % i, [128, 256], BF16) for i in range(3)]
                    ptr = [ps(st2, "rtptr%d" % i, [128, 8, 128], BF16) for i in range(2)]
                    pU = [ps(st2, "rtpU%d" % i, [128, 512], F32) for i in range(3)]
                    orders = [list(range(NTL)), [1, 0] + list(range(NTL - 1, 1, -1))]
                    for d in range(2):
                        memset('pool', Sst[d][:], 0.0, ['rtS%d' % d])
                    def rtchain(it, step, d):
                        t = orders[d][step]
                        pr, prk = ptr[it % 2], 'rtptr%d' % (it % 2)
                        kt, ktk = kT[it % 3], 'rtkT%d' % (it % 3)
                        pu, puk = pU[it % 3], 'rtpU%d' % (it % 3)
                        puv = pu[:, 0:128].rearrange("p (a b) -> p a b", a=2)
                        for j in range(2):
                            tr(pr[:, j, :], KR[:, j, t * 128:(t + 1) * 128], identb, ['rtKR', 'cstb'], [prk])
                        yield
                        tt('dve', kt[:].rearrange("p (h k) -> p h k", h=4), pr[:, 0:2, :].rearrange("p a (b k) -> p (a b) k", b=2),
                           KDEC[:, d, :].unsqueeze(2).broadcast_to([128, 4, 64]), ALU.mult, [prk, K_], [ktk])
                        yield
                        for h in range(4):
                            hp = (h % 2) * 64
                            mm(puv[hp:hp + 64, h // 2, :], kt[:, h * 64:(h + 1) * 64], VT[:, t, h * 64:(h + 1) * 64], [ktk, 'rtVT'], [puk])
                        yield
                        cp('act', Sall[d][:, :, t, :], Sst[d][:], ['rtS%d' % d], ['rtSall%d_%d' % (d, t)])
                        for j in range(2):
                            stt(Sst[d][:, j, :], Sst[d][:, j, :], GL[:, d * 2 + j:d * 2 + j + 1], puv[:, j, :], ALU.mult, ALU.add,
                                ['rtS%d' % d, K_, puk], ['rtS%d' % d])

                    gbank = mkbanks(st2, 3, "rtgk") if GJ_SPLIT[2] else None
                    gj = gate_jobs(l, last, st2, gbank, GJ_SPLIT[2]) if (GATE_PRE and GJ_SPLIT[2]) else []
                    run_pipelined(interleave([rtchain(i_, sd[0], sd[1]) for i_, sd in enumerate([(s_, d_) for s_ in range(NTL) for d_ in range(2)])], gj, 4), 2)
                    S.barrier()
                if stop == 'ret_chain':
                    return
                with contextlib.ExitStack() as st2:
                    if ('yc%d' % l) in debug:
                        dbgbuf = sb(st2, "dbgbuf", [128, 2, NT], F32)
                    AT = [sb(st2, "rtAT%d" % i, [128, 4, 128], BF16) for i in range(2)]
                    qd = [[sb(st2, "rtqd%d_%d" % (i, d), [128, 2, 128], BF16) for d in range(2)] for i in range(2)]
                    sq = [sb(st2, "rtsq%d" % i, [128, 2, 128], BF16) for i in range(2)]
                    rr = [sb(st2, "rtrr%d" % i, [128, 2, 128], F32) for i in range(2)]
                    ob = [sb(st2, "rtob%d" % i, [128, 2, 128], F32) for i in range(2)]
                    bank = mkbanks(st2, 8, "rtbk")

                    def rtout(t):
                        i2 = t % 2
                        tsl = slice(t * 128, (t + 1) * 128)
                        pas = [bank() for _ in range(2)]
                        for h in range(4):
                            hp = (h % 2) * 64
                            pav = pas[h % 2][0][:, 0:256].rearrange("p (a b) -> p a b", a=2)
                            mm(pav[:, h // 2, :], KR[hp:hp + 64, h // 2, tsl], QR[hp:hp + 64, h // 2, tsl], ['rtKR', 'rtQR'], [pas[h % 2][1]])
                        for d in range(2):
                            tt('pool', qd[i2][d][:], QR[:, :, tsl], QDEC[:, d, :, :], ALU.mult, ['rtQR', K_], ['rtqd%d_%d' % (i2, d)])
                        yield
                        for par in range(2):
                            pav = pas[par][0][:, 0:256].rearrange("p (a b) -> p a b", a=2)
                            tt('dve', AT[i2][:, par::2, :], pav, DS[:, par::2, :], ALU.mult, [pas[par][1], K_], ['rtAT%d' % i2])
                        yield
                        pos = [bank() for _ in range(2)]
                        povs = [pos[par][0][:, 0:256].rearrange("p (a b) -> p a b", a=2) for par in range(2)]
                        for h in range(4):
                            hp = (h % 2) * 64
                            pok = pos[h % 2][1]
                            reg = povs[h % 2][hp:hp + 64, h // 2, :]
                            mm(reg, VT[:, t, h * 64:(h + 1) * 64], AT[i2][:, h, :], ['rtVT', 'rtAT%d' % i2], [pok], start=True, stop=False)
                            for d in range(2):
                                mm(reg, Sall[d][hp:hp + 64, h // 2, t, :], qd[i2][d][hp:hp + 64, h // 2, :],
                                   ['rtSall%d_%d' % (d, t), 'rtqd%d_%d' % (i2, d)], [pok], start=False, stop=(d == 1))
                        yield
                        obk = 'rtob%d' % i2
                        cp('act', ob[i2][0:64], povs[0][0:64], [pos[0][1]], [obk])
                        cp('dve', ob[i2][64:128], povs[1][64:128], [pos[1][1]], [obk])
                        yield
                        pov = ob[i2][:]
                        pok = obk
                        if ('yc%d' % l) in debug:
                            cp('pool', dbgbuf[:, :, tsl], pov, [pok], ['dbgbuf'])
                        act(sq[i2][:], pov, AF.Square, [pok], ['rtsq%d' % i2])
                        yield
                        pss_, psk = bank()
                        psv = pss_[:, 0:256].rearrange("p (a b) -> p a b", a=2)
                        for j in range(2):
                            mm(psv[:, j, :], bonesb, sq[i2][:, j, :], ['cstb', 'rtsq%d' % i2], [psk])
                        yield
                        act(rr[i2][:], psv, AF.Sqrt, [psk], ['rtrr%d' % i2], bias=RMS_EPS, scale=1.0 / 64)
                        yield
                        S.op('dve', lambda e: e.reciprocal(out=rr[i2][:], in_=rr[i2][:]), reads=['rtrr%d' % i2], writes=['rtrr%d' % i2])
                        yield
                        tt('dve', rr[i2][:], pov, rr[i2][:], ALU.mult, [pok, 'rtrr%d' % i2], ['rtrr%d' % i2])
                        yield
                        tt('pool', Y[:, 2, :, tsl], rr[i2][:], zs[:, :, tsl], ALU.mult, ['rtrr%d' % i2, 'rtzs'], ['Y2'])

                    gj = gate_jobs(l, last, st2, bank, GJ_SPLIT[3]) if (GATE_PRE and GJ_SPLIT[3]) else []
                    run_pipelined(interleave([rtout(t) for t in range(NTL) if not (last and t < 2 and not debug)], gj, 3), 5)
                    if ('yc%d' % l) in debug:
                        dbg_dump('yc%d' % l, dbgbuf[:], [128, 2, NT], ['dbgbuf'])
                    S.barrier()
                S.barrier()
        PHASES['ret'] = phase_ret
        def phase_rw(l, h_src, last):
            with contextlib.ExitStack() as st:
                RB = sb(st, "rwRB", [128, 2, NT], BF16)
                KB = sb(st, "rwKB", [128, 2, NT], BF16)
                VB = sb(st, "rwVB", [128, 2, NT], BF16)
                LB = sb(st, "rwLB", [128, NT], BF16)
                zs = sb(st, "rwzs", [128, 2, NT], BF16)
                vT = sb(st, "rwvT", [128, NTL, 256], BF16)
                lw2b = sb(st, "rwlw2", [128, 2, 256], BF16)
                S.dma('pool', lw2b[:], dr['lw2'][l], writes=['rwlw2'])
                oka = sb(st, "rwoka", [128, 2], F32)
                ts('dve', oka[:], pv('ka'), -1.0, 1.0, ALU.mult, ALU.add, ['pvt'], ['rwoka'])
                seen_b, seen_o = set(), set()
                with contextlib.ExitStack() as st2:
                    ww = sb(st2, "rww", [128, 8, 1152], BF16)
                    for pc_ in range(9):
                        S.dma('pool', ww[:, :, pc_ * 128:(pc_ + 1) * 128], dr['w_in'][l][:, :, 2816 + pc_ * 128:2816 + (pc_ + 1) * 128], writes=['rww%d' % pc_])
                    XRs = [sb(st2, "rwXR%d" % i, [128, NT + 4], F32) for i in range(2)]
                    XSs = [sb(st2, "rwXS%d" % i, [128, NT], F32) for i in range(2)]
                    c0 = sb(st2, "rwc0", [128, 7], F32)
                    pp = [ps(st2, "rwpp%d" % i, [128, 512], F32) for i in range(3)]
                    ptr = [ps(st2, "rwptr%d" % i, [128, 8, 128], BF16) for i in range(2)]
                    o_mu, _ = PV['mu']
                    mu0, mu1 = pvt[:, o_mu:o_mu + 7], pvt[:, o_mu + 7:o_mu + 14]
                    tt('dve', c0[:], mu0, mu1, ALU.add, ['pvt'], ['rwc0'])
                    ts('dve', c0[:], c0[:], -1.0, 1.0, ALU.mult, ALU.add, ['rwc0'], ['rwc0'])
                    for i in range(2):
                        memset('pool', XRs[i][:], 0.0, ['rwXR%d' % i])
                    cnt = 0
                    for m in (0, 1, 2, 3, 4, 5, 7, 6, 8):
                        XR, XRk = XRs[m % 2], 'rwXR%d' % (m % 2)
                        XS, XSk = XSs[m % 2], 'rwXS%d' % (m % 2)
                        for (n0, nn) in BLOCKS:
                            p_, pk_ = pp[cnt % 3], 'rwpp%d' % (cnt % 3)
                            cnt += 1
                            for jj in range(8):
                                mm(p_[:, 0:nn], ww[:, jj, m * 128:(m + 1) * 128], uT[:, jj, n0:n0 + nn],
                                   ['rww%d' % m] + uTk[n0 // 128:(n0 + nn) // 128], [pk_], start=(jj == 0), stop=(jj == 7))
                            if m >= 7:
                                act(zs[:, m - 7, n0:n0 + nn], p_[:, 0:nn], AF.Silu, [pk_, 'pvt'], ['rwzs'], bias=pv('bin', 22 + m))
                            else:
                                xo = 1 if n0 < 256 else 3
                                act(XR[:, n0 + xo:n0 + xo + nn], p_[:, 0:nn], AF.Identity, [pk_, 'pvt'], [XRk], bias=pv('bin', 22 + m))
                        if m >= 7:
                            continue
                        for (b0, ln, o0) in ((1, 256, 0), (259, 2048, 256)):
                            ts('dve', XS[:, o0:o0 + ln], XR[:, b0:b0 + ln], c0[:, m:m + 1], None, ALU.mult, None, [XRk, 'rwc0'], [XSk])
                            stt(XS[:, o0:o0 + ln], XR[:, b0 - 1:b0 - 1 + ln], mu0[:, m:m + 1], XS[:, o0:o0 + ln], ALU.mult, ALU.add,
                                [XRk, 'pvt', XSk], [XSk])
                            if m < 6:
                                dstT, dk = [(RB, 'rwRB'), (KB, 'rwKB'), (VB, 'rwVB')][m // 2]
                                stt(dstT[:, m % 2, o0:o0 + ln], XR[:, b0 + 1:b0 + 1 + ln], mu1[:, m:m + 1], XS[:, o0:o0 + ln], ALU.mult, ALU.add,
                                    [XRk, 'pvt', XSk], [dk])
                            else:
                                stt(XS[:, o0:o0 + ln], XR[:, b0 + 1:b0 + 1 + ln], mu1[:, m:m + 1], XS[:, o0:o0 + ln], ALU.mult, ALU.add,
                                    [XRk, 'pvt', XSk], [XSk])
                        if m == 6:
                            act(LB[0:64, :], XS[0:64, :], AF.Tanh, [XSk], ['rwLB'])
                            cp('pool', LB[64:128, :], XS[64:128, :], [XSk], ['rwLB'])
                    for t in range(NTL):
                        pr, prk = ptr[t % 2], 'rwptr%d' % (t % 2)
                        for j in range(2):
                            tr(pr[:, j, :], VB[:, j, t * 128:(t + 1) * 128], identb, ['rwVB', 'cstb'], [prk])
                        cp('dve' if t % 2 == 0 else 'act', vT[:, t, :], pr[:, 0:2, :].rearrange("p a b -> p (a b)"), [prk], ['rwvT'])
                    S.barrier()
                if stop == 'rw_proj':
                    return
                OS = sb(st, "rwOS", [128, 2, NT], F32)
                with contextlib.ExitStack() as st2:
                    def B(name, shape, dt=BF16):
                        return sb(st2, "rw_" + name, shape, dt), "rw_" + name
                    R64, R64k = B("R64", [128, 256], BF16)
                    memset('pool', R64[:], 1.0, [R64k])
                    memset('pool', R64[:, 0:256:64], 0.0, [R64k])
                    LW, LWk = B("LW", [128, 2, 128], F32)
                    SA, SAk = B("SA", [128, 2, 128], F32)
                    LGm, LGk = B("LG", [128, 2, 128], F32)
                    U0, U0k = LGm, LGk
                    EG, EGk = B("EG", [128, 2, 128], F32)
                    ENG, ENGk = B("ENG", [128, 2, 128], F32)
                    EGM, EGMk = B("EGM", [128, 2, 128], F32)
                    TA, TAk = B("TA", [128, 2, 128], F32)
                    TB_, TBk = B("TB", [128, 2, 128], F32)
                    SQ, SQk = B("SQ", [128, 2, 128])
                    RKD, RKDk = SQ, SQk
                    OBt = (None, None)
                    Zst = [B("Z%d" % d, [128, 2, 64], F32) for d in range(2)]
                    BUF = [dict() for _ in range(2)]
                    for d_ in range(2):
                        BUF[d_]['KKN'] = B("KKN_%d" % d_, [128, 2, 128])
                        BUF[d_]['KT'] = [B("KT_%d_%d" % (d_, s_), [128, 3, 2, 128]) for s_ in range(2)]
                        BUF[d_]['RT'] = [B("RT_%d_%d" % (d_, s_), [128, 2, 128]) for s_ in range(2)]
                        for j_ in range(2):
                            sfx = "_%d_%d" % (d_, j_)
                            SB = dict()
                            SB['TM'] = B("TM" + sfx, [128, 3, 128])
                            for nm_ in ('A1T', 'A2T', 'A3T', 'A4T', 'ALT', 'Tm', 'TTm', 'Xb', 'RHS', 'BYb'):
                                SB[nm_] = B(nm_ + sfx, [128, 2, 128])
                            SB['NY'] = B("NY" + sfx, [128, 2, 64])
                            SB['RH'] = B("RH" + sfx, [128, 128])
                            SB['GTb'] = B("GTb" + sfx, [128, 2, 128])
                            SB['ZLG'] = B("ZLG" + sfx, [128, 2, 64], F32)
                            SB['Z0b'] = B("Z0b" + sfx, [128, 2, 64])
                            BUF[d_][j_] = SB
                        BUF[d_]['GLt'] = [B("GLt_%d_%d" % (d_, s_), [128, 2, 2], F32) for s_ in range(2)]
                    banks = [ps(st2, "rwbank%d" % i, [128, 512], F32) for i in range(8)]
                    bcnt = [0]

                    def bank():
                        i = bcnt[0] % 8
                        bcnt[0] += 1
                        return banks[i], 'rwbank%d' % i
                    for d in range(2):
                        memset('pool', Zst[d][0][:], 0.0, [Zst[d][1], 'rw_Zs_%d_0' % d, 'rw_Zs_%d_1' % d])
                    for d_ in range(2):
                        for j_ in range(2):
                            memset('pool', BUF[d_][j_]['GTb'][0][:], 0.0, [BUF[d_][j_]['GTb'][1]])
                    orders = [list(range(NTL)), [1, 0] + list(range(NTL - 1, 1, -1))]
                    bc3 = lambda ap: ap.unsqueeze(2).broadcast_to([128, 2, 128])
                    def prep(d, t, slot):
                        KKN, KKNk = BUF[d]['KKN']
                        KT, KTk = BUF[d]['KT'][slot]
                        RTb, RTk = BUF[d]['RT'][slot]
                        GLt, GLk = BUF[d]['GLt'][slot]
                        tsl = slice(t * 128, (t + 1) * 128)
                        rev = (d == 1)
                        Z, Zk = Zst[d]
                        plw, plwk = bank()
                        pla, plak = bank()
                        plwv = plw[:, 0:256].rearrange("p (j t) -> p j t", j=2)
                        plav = pla[:, 0:256].rearrange("p (j t) -> p j t", j=2)
                        wb_ = 32 * d
                        for j in range(2):
                            mm(plwv[:, j, :], lw2b[wb_:wb_ + 16, d, j * 128:(j + 1) * 128], LB[wb_:wb_ + 16, tsl], ['rwlw2', 'rwLB'], [plwk])
                        for j in range(2):
                            mm(plav[:, j, :], lw2b[64:96, d, j * 128:(j + 1) * 128], LB[64:96, tsl], ['rwlw2', 'rwLB'], [plak])
                        for j in range(2):
                            act(LW[:, j, :], plwv[:, j, :], AF.Sigmoid, [plwk, 'pvt'], [LWk], bias=pv('w0', d * 2 + j))
                            act(SA[:, j, :], plav[:, j, :], AF.Sigmoid, [plak, 'pvt'], [SAk], bias=pv('a0', d * 2 + j))
                        ts('dve', LW[:], LW[:], -0.6065306597126334, None, ALU.mult, None, [LWk], [LWk])
                        yield
                        lwf = LW[:].rearrange("p a b -> p (a b)")
                        lgf = LGm[:].rearrange("p a b -> p (a b)")
                        if not rev:
                            S.op('dve', lambda e: e.tensor_tensor_scan(out=lgf, data0=R64[:], data1=lwf, initial=0.0, op0=ALU.mult, op1=ALU.add),
                                 reads=[LWk, R64k], writes=[LGk])
                        else:
                            S.op('dve', lambda e: e.tensor_tensor_scan(out=lgf[:, ::-1], data0=R64[:], data1=lwf[:, ::-1], initial=0.0,
                                                                       op0=ALU.mult, op1=ALU.add), reads=[LWk, R64k], writes=[LGk])
                        act(EG[:], LGm[:], AF.Exp, [LGk], [EGk])
                        yield
                        act(ENG[:], LGm[:], AF.Exp, [LGk], [ENGk], scale=-1.0)
                        yield
                        tt('pool', TA[:], LGm[:], LW[:], ALU.subtract, [LGk, LWk], [TAk])
                        yield
                        act(EGM[:], TA[:], AF.Exp, [TAk], [EGMk])
                        yield
                        gsrc = EG[:, :, 63::64] if not rev else EG[:, :, 0::64]
                        cp('pool', GLt[:], gsrc, [EGk], [GLk])
                        yield
                        tt('dve', TA[:], KB[:, :, tsl], bc3(pv('kk')), ALU.mult, ['rwKB', 'pvt', TAk], [TAk])
                        yield
                        act(SQ[:], TA[:], AF.Square, [TAk], [SQk])
                        yield
                        pss_, pssk = bank()
                        pssv = pss_[:, 0:256].rearrange("p (a b) -> p a b", a=2)
                        for j in range(2):
                            mm(pssv[:, j, :], bonesb, SQ[:, j, :], ['cstb', SQk], [pssk])
                        act(TB_[:], pssv, AF.Sqrt, [pssk], [TBk])
                        yield
                        ts('dve', TB_[:], TB_[:], 1e-12, None, ALU.max, None, [TBk], [TBk])
                        yield
                        S.op('dve', lambda e: e.reciprocal(out=TB_[:], in_=TB_[:]), reads=[TBk], writes=[TBk])
                        tt('dve', KKN[:], TA[:], TB_[:], ALU.mult, [TAk, TBk], [KKNk])
                        yield
                        tt('pool', KT[:, 0], KKN[:], EGM[:], ALU.mult, [KKNk, EGMk], [KTk])
                        yield
                        tt('dve', TA[:], SA[:], ENG[:], ALU.mult, [SAk, ENGk, TAk], [TAk])
                        yield
                        tt('pool', KT[:, 1], KKN[:], TA[:], ALU.mult, [KKNk, TAk], [KTk])
                        yield
                        tt('dve', U0[:], SA[:], bc3(pv('ka')), ALU.mult, [SAk, 'pvt'], [U0k])
                        yield
                        tt('dve', U0[:], U0[:], bc3(oka[:]), ALU.add, [U0k, 'rwoka'], [U0k])
                        yield
                        tt('pool', TB_[:], U0[:], ENG[:], ALU.mult, [U0k, ENGk, TBk], [TBk])
                        yield
                        tt('pool', KT[:, 2], KB[:, :, tsl], TB_[:], ALU.mult, ['rwKB', TBk], [KTk])
                        yield
                        tt('dve', RTb[:], RB[:, :, tsl], EG[:], ALU.mult, ['rwRB', EGk], [RTk])
                        yield
                        tt('dve', U0[:], U0[:], KB[:, :, tsl], ALU.mult, [U0k, 'rwKB'], [U0k])
                        yield
                        tt('dve', U0[:], U0[:], bc3(pv('rk')), ALU.mult, [U0k, 'pvt'], [U0k])
                        yield
                        tt('pool', RKD[:], U0[:], RB[:, :, tsl], ALU.mult, [U0k, 'rwRB'], [RKDk])
                        yield
                        pbn, pbnk = bank()
                        pbnv = pbn[:, 0:256].rearrange("p (a b) -> p a b", a=2)
                        for j in range(2):
                            mm(pbnv[:, j, :], bonesb, RKD[:, j, :], ['cstb', RKDk], [pbnk])
                        if t not in seen_b:
                            seen_b.add(t)
                            tt('dve', Y[:, 3, :, tsl], pbnv, VB[:, :, tsl], ALU.mult, [pbnk, 'rwVB'], ['Y3'])
                        else:
                            tt('dve', TA[:], pbnv, VB[:, :, tsl], ALU.mult, [pbnk, 'rwVB', TAk], [TAk])
                            tt('pool', Y[:, 3, :, tsl], Y[:, 3, :, tsl], TA[:], ALU.add, ['Y3', TAk], ['Y3'])

                    def prep_pair(step):
                        for d_ in range(2):
                            yield from prep(d_, orders[d_][step], step % 2)

                    def unit(d, t, slot):
                        KT, KTk = BUF[d]['KT'][slot]
                        RTb, RTk = BUF[d]['RT'][slot]
                        GLt, GLk = BUF[d]['GLt'][slot]
                        tsl = slice(t * 128, (t + 1) * 128)
                        rev = (d == 1)
                        subs = [stream(d, j, t, rev, tsl, KT, KTk, RTb, RTk, GLt, GLk) for j in range(2)]
                        while subs:
                            for g in list(subs):
                                try:
                                    next(g)
                                except StopIteration:
                                    subs.remove(g)
                                yield

                    def stream(d, j, t, rev, tsl, KT, KTk, RTb, RTk, GLt, GLk):
                        SB = BUF[d][j]
                        TM, TMk = SB['TM']
                        A1T, A1k = SB['A1T']
                        A2T, A2k = SB['A2T']
                        A3T, A3k = SB['A3T']
                        A4T, A4k = SB['A4T']
                        ALT, ALk = SB['ALT']
                        Tm, Tmk = SB['Tm']
                        TTm, TTk = SB['TTm']
                        Xb, Xbk = SB['Xb']
                        RHS, RHSk = SB['RHS']
                        BYb, BYk = SB['BYb']
                        NY, NYk = SB['NY']
                        RH, RHk = SB['RH']
                        GTb, GTk = SB['GTb']
                        ZLG, ZLGk = SB['ZLG']
                        Z0b, Z0k = SB['Z0b']
                        Z, _zk = Zst[d]
                        Zk = 'rw_Zs_%d_%d' % (d, j)
                        ptb, ptbk = bank()
                        ptv = ptb[:].bitcast(BF16).rearrange("p (a b) -> p a b", a=8)
                        for x in range(3):
                            tr(ptv[:, x, :], KT[:, x, j, :], identb, [KTk, 'cstb'], [ptbk])
                        cp('act', TM[:], ptv[:, 0:3, :], [ptbk], [TMk])
                        yield

                        def amat(dst, dstk, li, ri_src, ri_k, mslot):
                            pas = []
                            for par in range(2):
                                hp = par * 64
                                pa, pak = bank()
                                rhs = (RTb[hp:hp + 64, j, :] if ri_src is None else KT[hp:hp + 64, ri_src, j, :])
                                mm(pa[:, 0:128], KT[hp:hp + 64, li, j, :], rhs, [KTk, ri_k], [pak])
                                pas.append((pa, pak))
                            return pas

                        def aevac(pas, dst, dstk, mslot):
                            for par, (pa, pak) in enumerate(pas):
                                if mslot is None:
                                    cp('act', dst[:, par, :], pa[:, 0:128], [pak], [dstk])
                                else:
                                    tt('dve', dst[:, par, :], pa[:, 0:128], maskb[:, mslot, :], ALU.mult, [pak, 'maskb'], [dstk])
                        for (dst, dstk, li, rs, rk, ms) in ((A1T, A1k, 1, 0, KTk, None), (A2T, A2k, 2, 0, KTk, 2 + d),
                                                            (A3T, A3k, 1, None, RTk, 4 + d), (A4T, A4k, 2, None, RTk, 4 + d)):
                            pas = amat(dst, dstk, li, rs, rk, ms)
                            aevac(pas, dst, dstk, ms)
                            yield
                        idb2 = identb.unsqueeze(1).broadcast_to([128, 2, 128])
                        cp('pool', Tm[:], idb2, ['cstb'], [Tmk])
                        cp('pool', TTm[:], idb2, ['cstb'], [TTk])
                        for lv in range(6):
                            tt('pool', ALT[:], A1T[:], maskb[:, 6 + d * 6 + lv, :].unsqueeze(1).broadcast_to([128, 2, 128]), ALU.mult,
                               [A1k, 'maskb'], [ALk])
                            yield
                            px, pxk = bank()
                            pxv = px[:, 0:256].rearrange("p (h t) -> p h t", h=2)
                            for par in range(2):
                                mm(pxv[:, par, :], ALT[:, par, :], Tm[:, par, :], [ALk, Tmk], [pxk])
                            cp('act', Xb[:], pxv, [pxk], [Xbk])
                            yield
                            py_, pyk = bank()
                            pyv = py_[:].rearrange("p (x h t) -> p x h t", x=2, h=2)
                            for par in range(2):
                                mm(pyv[:, 0, par, :], Xb[:, par, :], TTm[:, par, :], [Xbk, TTk], [pyk])
                            if lv < 5:
                                for par in range(2):
                                    mm(pyv[:, 1, par, :], TTm[:, par, :], Xb[:, par, :], [Xbk, TTk], [pyk])
                            if lv < 5:
                                tt('dve', Tm[:], Tm[:], pyv[:, 1], ALU.subtract, [Tmk, pyk], [Tmk])
                            tt('dve', TTm[:], TTm[:], pyv[:, 0], ALU.subtract, [TTk, pyk], [TTk])
                            yield
                        pw, pwk = bank()
                        pwv = pw[:, 0:128].rearrange("p (h v) -> p h v", h=2)
                        for par in range(2):
                            h = 2 * j + par
                            mm(pwv[:, par, :], A2T[:, par, :], vT[:, t, h * 64:(h + 1) * 64], [A2k, 'rwvT'], [pwk])
                        cp('pool', RHS[:, :, 0:64], TM[:, 0, :].rearrange("p (h k) -> p h k", h=2), [TMk], [RHSk])
                        cp('act', RHS[:, :, 64:128], pwv, [pwk], [RHSk])
                        yield
                        pby, pbyk = bank()
                        pbyv = pby[:, 0:256].rearrange("p (h t) -> p h t", h=2)
                        for par in range(2):
                            mm(pbyv[:, par, :], TTm[:, par, :], RHS[:, par, :], [TTk, RHSk], [pbyk])
                        cp('act', BYb[:], pbyv, [pbyk], [BYk])
                        yield
                        ts('pool', NY[:], BYb[:, :, 64:128], -1.0, 0.0, ALU.mult, ALU.add, [BYk], [NYk])
                        pr_, prk = bank()
                        for par in range(2):
                            hp = par * 64
                            mm(pr_[hp:hp + 64, 0:128], BYb[:, par, 0:64], A3T[:, par, :], [BYk, A3k], [prk])
                        tt('dve', RH[:], RTb[:, j, :], pr_[:, 0:128], ALU.subtract, [RTk, prk], [RHk])
                        yield
                        for c in range(2):
                            cs = slice(c * 64, (c + 1) * 64)
                            pg_, pgk = bank()
                            pgv = pg_[:, 0:128].rearrange("p (x v) -> p x v", x=2)
                            for par in range(2):
                                hp = par * 64
                                h = 2 * j + par
                                hc = slice(h * 64, (h + 1) * 64)
                                pc = slice(par * 64, (par + 1) * 64)
                                mm(pgv[hp:hp + 64, 0, :], BYb[cs, par, 0:64], TM[cs, 1, pc], [BYk, TMk], [pgk])
                                mm(pgv[hp:hp + 64, 1, :], TM[cs, 2, pc], vT[cs, t, hc], [TMk, 'rwvT'], [pgk], start=True, stop=False)
                                mm(pgv[hp:hp + 64, 1, :], TM[cs, 1, pc], NY[cs, par, :], [TMk, NYk], [pgk], start=False, stop=True)
                            for par in range(2):
                                hp = par * 64
                                tt('dve', GTb[hp:hp + 64, c, hp:hp + 64], cstf[hp:hp + 64, 4, 0:64], pgv[hp:hp + 64, 0, :], ALU.subtract,
                                   ['cstf', pgk], [GTk])
                            ts('dve', ZLG[:, c, :], pgv[:, 1, :], GLt[:, j, c:c + 1], None, ALU.mult, None, [pgk, GLk], [ZLGk])
                            yield
                        for c in ((0, 1) if not rev else (1, 0)):
                            cp('act', Z0b[:, c, :], Z[:, j, :], [Zk], [Z0k])
                            yield
                            pn, pnk = bank()
                            mm(pn[:, 0:64], GTb[:, c, :], Z0b[:, c, :], [GTk, Z0k], [pnk])
                            stt(Z[:, j, :], pn[:, 0:64], GLt[:, j, c:c + 1], ZLG[:, c, :], ALU.mult, ALU.add, [pnk, GLk, ZLGk, Zk], [Zk])
                            yield
                        for par in range(2):
                            hp = par * 64
                            h = 2 * j + par
                            hc = slice(h * 64, (h + 1) * 64)
                            po_, pok = bank()
                            reg = po_[hp:hp + 64, 0:128]
                            mm(reg, vT[:, t, hc], A4T[:, par, :], ['rwvT', A4k], [pok], start=True, stop=False)
                            mm(reg, NY[:, par, :], A3T[:, par, :], [NYk, A3k], [pok], start=False, stop=False)
                            for c in range(2):
                                mm(reg[:, c * 64:(c + 1) * 64], Z0b[hp:hp + 64, c, :], RH[hp:hp + 64, c * 64:(c + 1) * 64],
                                   [Z0k, RHk], [pok], start=False, stop=(c == 1))
                            osl = OS[hp:hp + 64, j, tsl]
                            osk = 'rwOS%d_%d' % (t, j)
                            if (t, j, par) not in seen_o:
                                seen_o.add((t, j, par))
                                cp('dve' if par == 0 else 'act', osl, reg, [pok], [osk])
                            else:
                                tt('dve', osl, osl, reg, ALU.add, [pok, osk], [osk])
                            yield

                    for _ in prep_pair(0):
                        pass
                    for step in range(NTL):
                        gens = [unit(d, orders[d][step], step % 2) for d in range(2)]
                        if step + 1 < NTL:
                            gens.append(prep_pair(step + 1))
                        while gens:
                            for g in list(gens):
                                try:
                                    next(g)
                                except StopIteration:
                                    gens.remove(g)
                    S.barrier()
                if stop is not None and stop.startswith('rw_'):
                    return
                with contextlib.ExitStack() as st2:
                    ob = [sb(st2, "rwob%d" % i, [128, 2, 128], BF16) for i in range(2)]
                    cen = [sb(st2, "rwcen%d" % i, [128, 2, 128], F32) for i in range(2)]
                    rs = [sb(st2, "rwrs%d" % i, [128, 2, 128], F32) for i in range(2)]
                    pm_ = [ps(st2, "rwpm%d" % i, [128, 512], F32) for i in range(2)]
                    pv_ = [ps(st2, "rwpv%d" % i, [128, 512], F32) for i in range(2)]
                    for t in range(NTL):
                        i2 = t % 2
                        tsl = slice(t * 128, (t + 1) * 128)
                        osk = 'rwOS%d_0' % t
                        osk1 = 'rwOS%d_1' % t
                        cp('act', ob[i2][:], OS[:, :, tsl], [osk, osk1], ['rwob%d' % i2])
                        pmv = pm_[i2][:, 0:256].rearrange("p (a b) -> p a b", a=2)
                        for j in range(2):
                            mm(pmv[:, j, :], bonesb, ob[i2][:, j, :], ['cstb', 'rwob%d' % i2], ['rwpm%d' % i2])
                        stt(cen[i2][:], pmv, -1.0 / 64, OS[:, :, tsl], ALU.mult, ALU.add, ['rwpm%d' % i2, osk, osk1], ['rwcen%d' % i2])
                        act(ob[i2][:], cen[i2][:], AF.Square, ['rwcen%d' % i2], ['rwob%d' % i2])
                        pvv = pv_[i2][:, 0:256].rearrange("p (a b) -> p a b", a=2)
                        for j in range(2):
                            mm(pvv[:, j, :], bonesb, ob[i2][:, j, :], ['cstb', 'rwob%d' % i2], ['rwpv%d' % i2])
                        act(rs[i2][:], pvv, AF.Sqrt, ['rwpv%d' % i2], ['rwrs%d' % i2], bias=RW_GN_EPS, scale=1.0 / 64)
                        S.op('dve', lambda e: e.reciprocal(out=rs[i2][:], in_=rs[i2][:]), reads=['rwrs%d' % i2], writes=['rwrs%d' % i2])
                        tt('dve', cen[i2][:], cen[i2][:], rs[i2][:], ALU.mult, ['rwcen%d' % i2, 'rwrs%d' % i2], ['rwcen%d' % i2])
                        tt('pool', cen[i2][:], cen[i2][:], bc3(pv('gnw')), ALU.mult, ['rwcen%d' % i2, 'pvt'], ['rwcen%d' % i2])
                        tt('pool', cen[i2][:], cen[i2][:], bc3(pv('gnb')), ALU.add, ['rwcen%d' % i2, 'pvt'], ['rwcen%d' % i2])
                        tt('dve', cen[i2][:], cen[i2][:], Y[:, 3, :, tsl], ALU.add, ['rwcen%d' % i2, 'Y3'], ['rwcen%d' % i2])
                        if ('yd%d' % l) in debug:
                            cp('act', OS[:, :, tsl], cen[i2][:], ['rwcen%d' % i2], [osk, osk1])
                        tt('dve', Y[:, 3, :, tsl], cen[i2][:], zs[:, :, tsl], ALU.mult, ['rwcen%d' % i2, 'rwzs'], ['Y3'])
                    if ('yd%d' % l) in debug:
                        dbg_dump('yd%d' % l, OS[:], [128, 2, NT], ['rwOS%d_%d' % (t, j_) for t in range(NTL) for j_ in range(2)])
                    S.barrier()
                S.barrier()
        PHASES['rw'] = phase_rw
        def phase_merge(l, h_src, last):
            h_dst = out_d if last else h1_d
            with contextlib.ExitStack() as st:
                MG = sb(st, "mgMG", [128, 8, NT], BF16)
                wbr = sb(st, "mgwbr", [128, 4, 2, DM], BF16)
                S.dma('pool', wbr[:], dr['wbr'][l], writes=['mgwbr'])
                with contextlib.ExitStack() as st2:
                    wg = [sb(st2, "mgwg%d" % i, [128, 8, 4, 128], BF16) for i in range(2)]
                    sg = [sb(st2, "mgsg%d" % i, [128, 512], BF16) for i in range(3)]
                    ac = [sb(st2, "mgac%d" % i, [128, 512], F32) for i in range(2)]
                    tm = [sb(st2, "mgtm%d" % i, [128, 512], F32) for i in range(2)]
                    pgl = [ps(st2, "mgpg%d" % i, [128, 512], F32) for i in range(3)]
                    pbr = [ps(st2, "mgpb%d" % i, [128, 512], F32) for i in range(3)]
                    cg = 0
                    ca = 0
                    def load_wg(dt_):
                        for k in range(4):
                            if (l, k * 8 + dt_) in pre_sg:
                                continue
                            c0 = 3968 + k * 1024 + dt_ * 128
                            S.dma('pool', wg[dt_ % 2][:, :, k, :], dr['w_in'][l][:, :, c0:c0 + 128], writes=['mgwg%d' % (dt_ % 2)])
                    load_wg(0)
                    for dt_ in range(8):
                        w_, wk_ = wg[dt_ % 2], 'mgwg%d' % (dt_ % 2)
                        if dt_ + 1 < 8:
                            load_wg(dt_ + 1)
                        for (n0, nn) in BLOCKS:
                            if last and n0 < 256:
                                continue
                            a_, ak_ = ac[ca % 2], 'mgac%d' % (ca % 2)
                            t_, tk_ = tm[ca % 2], 'mgtm%d' % (ca % 2)
                            ca += 1
                            for k in range(4):
                                pg_, pgk_ = pgl[cg % 3], 'mgpg%d' % (cg % 3)
                                pb_, pbk_ = pbr[cg % 3], 'mgpb%d' % (cg % 3)
                                s_, sk_ = sg[cg % 3], 'mgsg%d' % (cg % 3)
                                cg += 1
                                if (l, k * 8 + dt_) in pre_sg:
                                    S.dma('sp' if cg % 2 == 0 else 'act', s_[:, 0:nn], sgd[k * 8 + dt_][:, n0:n0 + nn], reads=['sgd'], writes=[sk_])
                                else:
                                    for jj in range(8):
                                        mm(pg_[:, 0:nn], w_[:, jj, k, :], uT[:, jj, n0:n0 + nn], [wk_] + uTk[n0 // 128:(n0 + nn) // 128], [pgk_],
                                           start=(jj == 0), stop=(jj == 7))
                                    act(s_[:, 0:nn], pg_[:, 0:nn], AF.Sigmoid, [pgk_, 'pvt'], [sk_], bias=pv('bin', 31 + k * 8 + dt_))
                                for jc in range(2):
                                    mm(pb_[:, 0:nn], wbr[:, k, jc, dt_ * 128:(dt_ + 1) * 128], Y[:, k, jc, n0:n0 + nn], ['mgwbr', 'Y%d' % k], [pbk_],
                                       start=(jc == 0), stop=(jc == 1))
                                if k == 0:
                                    tt('dve', a_[:, 0:nn], pb_[:, 0:nn], s_[:, 0:nn], ALU.mult, [pbk_, sk_], [ak_])
                                else:
                                    tt('dve', t_[:, 0:nn], pb_[:, 0:nn], s_[:, 0:nn], ALU.mult, [pbk_, sk_], [tk_])
                                    if k < 3:
                                        tt('pool', a_[:, 0:nn], a_[:, 0:nn], t_[:, 0:nn], ALU.add, [ak_, tk_], [ak_])
                                    else:
                                        tt('pool', MG[:, dt_, n0:n0 + nn], a_[:, 0:nn], t_[:, 0:nn], ALU.add, [ak_, tk_], ['mgMG%d' % (n0 // 512 if n0 else 9)])
                    S.barrier()
                if ('merged%d' % l) in debug:
                    with contextlib.ExitStack() as st2:
                        mf = sb(st2, "mgf", [128, 8, NT], F32)
                        cp('dve', mf[:], MG[:], ['mgMG%d' % i for i in (9, 0, 1, 2, 3)], ['mgf'])
                        dbg_dump('merged%d' % l, mf[:], [128, 8, NT], ['mgf'])
                        S.barrier()
                with contextlib.ExitStack() as st2:
                    wo = sb(st2, "mgwo", [128, 8, DM], BF16)
                    S.dma('pool', wo[:], dr['wout'][l], writes=['mgwo'])
                    rows = sb(st2, "mgrows", [128, 3, DM], F32)
                    S.dma('sp', rows[:], dr['rows'][l][:, 0:3072].rearrange("p (a b) -> p a b", a=3), writes=['mgrows'])
                    hin_ = [sb(st2, "mghin%d" % i, [128, DM], F32) for i in range(2)]
                    ot = [sb(st2, "mgot%d" % i, [128, DM], F32) for i in range(2)]
                    stat = [sb(st2, "mgst%d" % i, [128, 16], F32) for i in range(2)]
                    po = [[ps(st2, "mgpo%d_%d" % (i, hh), [128, 512], F32) for hh in range(2)] for i in range(2)]
                    def mgout(it, t):
                        i2 = it % 2
                        ci = 1 if t < 2 else 0
                        tsl = slice(t * 128, (t + 1) * 128)
                        mgk = 'mgMG%d' % (9 if t < 2 else (t - 2) // 4)
                        hk_, ok_, sk_ = 'mghin%d' % i2, 'mgot%d' % i2, 'mgst%d' % i2
                        hi, o_, sti = hin_[i2], ot[i2], stat[i2]
                        S.dma('sp', hi[:], h_src[t * 128:(t + 1) * 128, :], writes=[hk_])
                        for hh in range(2):
                            pk_ = 'mgpo%d_%d' % (i2, hh)
                            for jj in range(8):
                                mm(po[i2][hh][:], MG[:, jj, tsl], wo[:, jj, hh * 512:(hh + 1) * 512], [mgk, 'mgwo'], [pk_], start=(jj == 0), stop=(jj == 7))
                        yield
                        for hh in range(2):
                            pk_ = 'mgpo%d_%d' % (i2, hh)
                            tt('dve', o_[:, hh * 512:(hh + 1) * 512], po[i2][hh][:], rows[:, 0, hh * 512:(hh + 1) * 512], ALU.add, [pk_, 'mgrows'], [ok_])
                        yield
                        tt('dve', o_[:], o_[:], gatebc[:, ci, :], ALU.mult, [ok_, 'gatebc'], [ok_])
                        yield
                        stt(o_[:], hi[:], ALPHA, o_[:], ALU.mult, ALU.add, [hk_, ok_], [ok_])
                        yield
                        S.op('dve', lambda e: e.bn_stats(out=sti[:, 0:6], in_=o_[:, 0:512]), reads=[ok_], writes=[sk_])
                        S.op('dve', lambda e: e.bn_stats(out=sti[:, 6:12], in_=o_[:, 512:1024]), reads=[ok_], writes=[sk_])
                        yield
                        S.op('dve', lambda e: e.bn_aggr(out=sti[:, 12:14], in_=sti[:, 0:12]), reads=[sk_], writes=[sk_])
                        yield
                        act(sti[:, 14:15], sti[:, 13:14], AF.Sqrt, [sk_], [sk_], bias=LN_EPS)
                        yield
                        S.op('dve', lambda e: e.reciprocal(out=sti[:, 14:15], in_=sti[:, 14:15]), reads=[sk_], writes=[sk_])
                        yield
                        stt(sti[:, 15:16], sti[:, 12:13], -1.0, sti[:, 14:15], ALU.mult, ALU.mult, [sk_], [sk_])
                        yield
                        act(o_[:], o_[:], AF.Identity, [ok_, sk_], [ok_], bias=sti[:, 15:16], scale=sti[:, 14:15])
                        yield
                        tt('pool', o_[:, 0:512], o_[:, 0:512], rows[:, 1, 0:512], ALU.mult, [ok_, 'mgrows'], [ok_])
                        tt('dve', o_[:, 512:1024], o_[:, 512:1024], rows[:, 1, 512:1024], ALU.mult, [ok_, 'mgrows'], [ok_])
                        yield
                        tt('pool', o_[:, 0:512], o_[:, 0:512], rows[:, 2, 0:512], ALU.add, [ok_, 'mgrows'], [ok_])
                        tt('dve', o_[:, 512:1024], o_[:, 512:1024], rows[:, 2, 512:1024], ALU.add, [ok_, 'mgrows'], [ok_])
                        yield
                        if last:
                            S.dma('sp', out_d[(t - 2) * 128:(t - 1) * 128, :], o_[:], reads=[ok_], writes=['outfinal'])
                        else:
                            S.dma('sp', h1_d[t * 128:(t + 1) * 128, :], o_[:], reads=[ok_], writes=['h1'])

                    tl = [t for t in range(NTL) if not (last and t < 2)]
                    run_pipelined((mgout(i_, t) for i_, t in enumerate(tl)), 6)
                    S.barrier()
                S.barrier()
        PHASES['merge'] = phase_merge
        for l in range(nlayers):
            last = (l == nlayers - 1)
            h_src = dr['hin'] if l == 0 else h1_d
            S.dma('sp', pvt[:], dr['pv'][l], writes=['pvt'])
            with contextlib.ExitStack() as st:
                adw = [sb(st, "adw%d" % i, [128, 8, 512], F32) for i in range(2)]
                scb = sb(st, "scb", [128, 2, 8, 128], F32)
                grow = sb(st, "grow", [128, DM], F32)
                pm0 = ps(st, "pm0", [128, 16, 2], F32)
                pg = [ps(st, "pg%d" % i, [128, 512], F32) for i in range(2)]
                for i in range(2):
                    cp('dve', scb[:, i], silc[:, :, i:i + 1].broadcast_to([128, 8, 128]), ['silc'], ['scb'])
                S.dma('sp', grow[:], dr['rows'][l][:, 3072:4096], writes=['grow'])
                for ch in range(6):
                    buf = adw[ch % 2]
                    bk = 'adw%d' % (ch % 2)
                    S.dma('sp' if ch % 2 == 0 else 'act', buf[:], dr['ada_w'][l][:, :, ch * 512:(ch + 1) * 512], writes=[bk])
                    if ch < 4:
                        for mloc in range(4):
                            m = ch * 4 + mloc
                            for j in range(8):
                                mm(pm0[:, m, :], buf[:, j, mloc * 128:(mloc + 1) * 128], silc[:, j, :], [bk, 'silc'],
                                   ['pm0'], start=(j == 0), stop=(j == 7))
                    else:
                        for i in range(2):
                            for j in range(8):
                                mm(pg[i][:], scb[:, i, j, :], buf[:, j, :], [bk, 'scb'], ['pg%d' % i],
                                   start=(j == 0), stop=(j == 7))
                            tt('dve', gatebc[:, i, (ch - 4) * 512:(ch - 3) * 512], pg[i][:],
                               grow[:, (ch - 4) * 512:(ch - 3) * 512], ALU.add, ['pg%d' % i, 'grow'], ['gatebc'])
                tt('dve', modfm[:], pm0[:], pv('adab').unsqueeze(2).broadcast_to([128, 16, 2]), ALU.add,
                   ['pm0', 'pvt'], ['modfm'])
                ts('dve', modfm[:, 8:16, :], modfm[:, 8:16, :], 1.0, None, ALU.add, None, ['modfm'], ['modfm'])
                dbg_dump('modfm%d' % l, modfm[:], [128, 16, 2], ['modfm'])
                dbg_dump('gatebc%d' % l, gatebc[:], [128, 2, DM], ['gatebc'])
                S.barrier()
            with contextlib.ExitStack() as st:
                xin = [sb(st, "xin%d" % i, [128, DM], F32) for i in range(3)]
                xn = [sb(st, "xn%d" % i, [128, DM], BF16) for i in range(2)]
                stat = [sb(st, "stat%d" % i, [128, 16], F32) for i in range(3)]
                ptr = [ps(st, "ptr%d" % i, [128, 8, 128], BF16) for i in range(2)]
                def p1tile(t):
                    xi, xk = xin[t % 3], 'xin%d' % (t % 3)
                    sti, sk = stat[t % 3], 'stat%d' % (t % 3)
                    xo, xok = xn[t % 2], 'xn%d' % (t % 2)
                    pt, ptk = ptr[t % 2], 'ptr%d' % (t % 2)
                    ci = 1 if t < 2 else 0
                    S.dma('sp' if t % 2 == 0 else 'act', xi[:], h_src[t * 128:(t + 1) * 128, :], writes=[xk])
                    yield
                    S.op('dve', lambda e: e.bn_stats(out=sti[:, 0:6], in_=xi[:, 0:512]), reads=[xk], writes=[sk])
                    S.op('dve', lambda e: e.bn_stats(out=sti[:, 6:12], in_=xi[:, 512:1024]), reads=[xk], writes=[sk])
                    yield
                    S.op('dve', lambda e: e.bn_aggr(out=sti[:, 12:14], in_=sti[:, 0:12]), reads=[sk], writes=[sk])
                    yield
                    act(sti[:, 14:15], sti[:, 13:14], AF.Sqrt, [sk], [sk], bias=LN_EPS)
                    yield
                    S.op('dve', lambda e: e.reciprocal(out=sti[:, 14:15], in_=sti[:, 14:15]), reads=[sk], writes=[sk])
                    yield
                    stt(sti[:, 15:16], sti[:, 12:13], -1.0, sti[:, 14:15], ALU.mult, ALU.mult, [sk], [sk])
                    yield
                    act(xo[:], xi[:], AF.Identity, [xk, sk], [xok], bias=sti[:, 15:16], scale=sti[:, 14:15])
                    yield
                    for j in range(8):
                        tr(pt[:, j, :], xo[:, j * 128:(j + 1) * 128], identb, [xok, 'cstb'], [ptk])
                    yield
                    for j in range(8):
                        if j % 2 == 0:
                            act(uT[:, j, t * 128:(t + 1) * 128], pt[:, j, :], AF.Identity, [ptk, 'modfm'], ['uT%d' % t],
                                bias=modfm[:, j, ci:ci + 1], scale=modfm[:, 8 + j, ci:ci + 1])
                        else:
                            ts('dve', uT[:, j, t * 128:(t + 1) * 128], pt[:, j, :], modfm[:, 8 + j, ci:ci + 1],
                               modfm[:, j, ci:ci + 1], ALU.mult, ALU.add, [ptk, 'modfm'], ['uT%d' % t])

                run_pipelined((p1tile(t) for t in range(NTL)), 4)
                if ('uT%d' % l) in debug:
                    utf = sb(st, "utf", [128, 8, NT], F32)
                    cp('dve', utf[:], uT[:], ['uT%d' % t for t in range(NTL)], ['utf'])
                    dbg_dump('uT%d' % l, utf[:], [128, 8, NT], ['utf'])
                S.barrier()
            uTk = ['uT%d' % t for t in range(NTL)]

            for ph in list(PHASES):
                if ph in phases:
                    PHASES[ph](l, h_src, last)
            if ('h%d' % l) in debug and not last:
                d_ = dbg_out('h%d' % l, [NT, DM])
                S.dma('sp', d_, h1_d, writes=['dbgout_h%d' % l])
                S.barrier()
            if ('Y%d' % l) in debug:
                with contextlib.ExitStack() as st:
                    yf = sb(st, "yf", [128, 4, 2, NT], F32)
                    cp('dve', yf[:], Y[:], ['Y0', 'Y1', 'Y2', 'Y3'], ['yf'])
                    dbg_dump('Y%d' % l, yf[:], [128, 4, 2, NT], ['yf'])
                    S.barrier()

        S.final_wait('sp', ['outfinal'] + ['dbgout_' + n for n in dbg_d])
    if MEMDBG:
        print('SBUF min remaining by prefix:', minrem)
    return nc, dbg_d


def kernel(**inputs):
    inp = {k: np.asarray(v) for k, v in inputs.items()}
    sh = prep_shared(inp)
    nc, _ = build()
    in_maps = []
    for b in range(8):
        m = dict(sh)
        m.update(prep_core(inp, b))
        in_maps.append(m)
    res = run_bass_kernel_spmd(nc, in_maps, core_ids=list(range(8)))
    return np.stack([np.asarray(res.results[b]['out'], dtype=np.float32) for b in range(8)], 0)
```

```python
import contextlib
import numpy as np
import concourse.bass as bass
import concourse.mybir as mybir
from concourse.bass_utils import run_bass_kernel_spmd

F32 = mybir.dt.float32
BF16 = mybir.dt.bfloat16
AF = mybir.ActivationFunctionType
ALU = mybir.AluOpType
AX = mybir.AxisListType

NT = 2304
NTL = 18
DM = 1024
NCOL = 8064
BLOCKS = [(0, 256), (256, 512), (768, 512), (1280, 512), (1792, 512)]
LN_EPS = 1e-5
RMS_EPS = 1e-6
RW_GN_EPS = 64e-5
ALPHA = (2 * 2) ** 0.25
PI = float(np.pi)
MEMDBG = False
GATE_PRE = False
S5_STAGGER = 6
STG = dict(hgchain=2, out=5, rtchain=1, hgproj=4, mgout=6, p1=4, s5=6)
GJ_SPLIT = [[], [], [], []]
GJ_S5 = list(range(32))


class Sched:
    NDMA = 16

    def __init__(self, nc, same_engine_waits=True):
        self.nc = nc
        self.same = same_engine_waits
        self.eng = dict(pe=nc.tensor, act=nc.scalar, dve=nc.vector, pool=nc.gpsimd, sp=nc.sync)
        self.E = {n: dict(cnt=0, known={}) for n in self.eng}
        self.dq = {'sp': ['dsp%d' % i for i in range(8)], 'act': ['dac%d' % i for i in range(4)],
                   'pool': ['dpl%d' % i for i in range(8)]}
        self.dmas = {n: dict(cnt=0) for q in self.dq.values() for n in q}
        self.dma_rr = {'sp': 0, 'act': 0, 'pool': 0}
        self.lastw = {}
        self.readers = {}
        self.sems = None
        self.nins = 0

    def sem_names(self):
        return list(self.E.keys()) + list(self.dmas.keys())

    def _deps(self, reads, writes):
        deps = {}

        def add(w):
            if w is not None:
                deps[w[0]] = max(deps.get(w[0], 0), w[1])
        for k in reads:
            add(self.lastw.get(k))
        for k in writes:
            add(self.lastw.get(k))
            for r in self.readers.get(k, ()):
                add(r)
        return deps

    def _waits(self, en, deps):
        E = self.E[en]
        waits = []
        for d, v in deps.items():
            if d == en and (en == 'pe' or not self.same):
                continue
            if E['known'].get(d, 0) < v:
                waits.append((d, v))
                E['known'][d] = v
        return waits

    def _record(self, ident, reads, writes):
        for k in writes:
            self.lastw[k] = ident
            self.readers[k] = []
        for k in reads:
            self.readers.setdefault(k, []).append(ident)

    def _emit(self, en, waits, fn, inc):
        eng = self.eng[en]
        for d, v in waits:
            eng.wait_ge(self.sems[d], v)
        if fn is not None:
            fn(eng).then_inc(self.sems[inc[0]], inc[1])
            self.nins += 1

    def op(self, en, fn, reads=(), writes=()):
        E = self.E[en]
        waits = self._waits(en, self._deps(reads, writes))
        E['cnt'] += 1
        self._emit(en, waits, fn, (en, 1))
        self._record((en, E['cnt']), reads, writes)

    def dma(self, en, out, in_, reads=(), writes=(), **kw):
        dn = self.dq[en][self.dma_rr[en]]
        self.dma_rr[en] = (self.dma_rr[en] + 1) % len(self.dq[en])
        Dq = self.dmas[dn]
        deps = self._deps(reads, writes)
        if Dq['cnt'] > 0:
            deps[dn] = max(deps.get(dn, 0), Dq['cnt'])
        waits = self._waits(en, deps)
        Dq['cnt'] += 16
        self._emit(en, waits, (lambda e: e.dma_start(out=out, in_=in_, **kw)), (dn, 16))
        self._record((dn, Dq['cnt']), reads, writes)

    def barrier(self):
        cur = {n: self.E[n]['cnt'] for n in self.E}
        cur.update({n: self.dmas[n]['cnt'] for n in self.dmas})
        for en in self.E:
            waits = self._waits(en, {d: v for d, v in cur.items() if v > 0})
            self._emit(en, waits, None, None)

    def final_wait(self, en, keys):
        self._emit(en, self._waits(en, self._deps(keys, ())), None, None)


PV = {}


def _pv_layout():
    off = 0
    for name, n in [('bin', 63), ('s5d', 2), ('glub', 2), ('hglb', 8), ('hgnw', 2), ('rdec', 4), ('mu', 14),
                    ('w0', 4), ('a0', 4), ('kk', 2), ('ka', 2), ('rk', 2), ('gnw', 2), ('gnb', 2), ('adab', 16),
                    ('lamre', 16), ('lamim', 16), ('ldt', 16), ('rdech', 8)]:
        PV[name] = (off, n)
        off += n
    return off


NPV = _pv_layout()


def _colmap():
    cm = list(range(0, 3584))
    lora = [-1] * 128
    for r in range(16):
        lora[r] = 3584 + r
        lora[32 + r] = 3600 + r
        lora[64 + r] = 3616 + r
        lora[80 + r] = 3632 + r
    cm += lora
    cm += list(range(3648, 3904))
    cm += list(range(3904, 8000))
    return np.array(cm)


CMAP = _colmap()


def _fm(v):
    return np.ascontiguousarray(v.reshape(-1, 128).T)


def _masks():
    t = np.arange(128)
    s_, t_ = t[:, None], t[None, :]
    m = []
    b32 = (s_ // 32) == (t_ // 32)
    b64 = (s_ // 64) == (t_ // 64)
    m.append(b32 & (t_ >= s_))
    m.append(b32 & (t_ <= s_))
    m.append(b64 & (t_ > s_))
    m.append(b64 & (t_ < s_))
    m.append(b64 & (t_ >= s_))
    m.append(b64 & (t_ <= s_))
    for d in range(2):
        for lv in range(6):
            sz = 1 << lv
            blk = (s_ // (2 * sz)) == (t_ // (2 * sz))
            hs, ht = (s_ // sz) % 2, (t_ // sz) % 2
            if d == 0:
                m.append(blk & (ht == 1) & (hs == 0))
            else:
                m.append(blk & (ht == 0) & (hs == 1))
    return np.stack([x.astype(np.float32) for x in m], 1)


def _rot_tables():
    n = 16
    freqs = 10000.0 ** (-np.arange(n, dtype=np.float32) / n)
    tt = np.arange(2048)
    rows = (tt // 64).astype(np.float32)
    cols = (tt % 64).astype(np.float32)
    cos = np.zeros((128, 2048), np.float32)
    sins = np.zeros((128, 2048), np.float32)
    pm = np.zeros((128, 128), np.float32)
    for p in range(128):
        i = p % 64
        pos = rows if i < 32 else cols
        ii = i % 32
        ang = pos * freqs[ii % 16]
        cos[p] = np.cos(ang)
        if ii < 16:
            sins[p] = -np.sin(ang)
            partner = p + 16
        else:
            sins[p] = np.sin(ang)
            partner = p - 16
        pm[partner, p] = 1.0
    return cos, sins, pm


def prep_shared(inp):
    sh = {}
    L = 2
    w_in = inp['w_in']
    wn = np.zeros((L, 1024, NCOL), np.float32)
    valid = CMAP >= 0
    wn[:, :, valid] = w_in[:, :, CMAP[valid]]
    sh['w_in'] = np.ascontiguousarray(wn.reshape(L, 8, 128, NCOL).transpose(0, 2, 1, 3))
    bn = np.zeros((L, NCOL), np.float32)
    bn[:, valid] = inp['b_in'][:, CMAP[valid]]
    sh['ada_w'] = np.ascontiguousarray(inp['ada_w'].reshape(L, 8, 128, 3072).transpose(0, 2, 1, 3))
    pv = np.zeros((L, 128, NPV), np.float32)

    def put(l, name, arr):
        o, n = PV[name]
        assert arr.shape == (128, n), (name, arr.shape)
        pv[l, :, o:o + n] = arr
    for l in range(L):
        put(l, 'bin', _fm(bn[l]))
        put(l, 's5d', _fm(inp['s5_d'][l]))
        put(l, 'glub', _fm(inp['s5_glu_b'][l]))
        put(l, 'hglb', np.concatenate([_fm(inp['hg_lb'][ll, d]) for ll in range(2) for d in range(2)], 1))
        put(l, 'hgnw', _fm(inp['hg_norm_w'][l]))
        rd = np.zeros((128, 4), np.float32)
        for d in range(2):
            for j in range(2):
                rd[:64, d * 2 + j] = inp['ret_decay'][l, d, 2 * j]
                rd[64:, d * 2 + j] = inp['ret_decay'][l, d, 2 * j + 1]
        put(l, 'rdec', rd)
        put(l, 'rdech', np.ascontiguousarray(np.broadcast_to(inp['ret_decay'][l].reshape(1, 8), (128, 8))))
        mu = np.zeros((2, 7 * 128), np.float32)
        mu[:, :768] = inp['rw_mu'][l][:, :768]
        lv = CMAP[3584:3712]
        ok = lv >= 0
        mu[:, 768:896][:, ok] = inp['rw_mu'][l][:, lv[ok] - 2816]
        put(l, 'mu', np.concatenate([_fm(mu[0]), _fm(mu[1])], 1))
        put(l, 'w0', np.concatenate([_fm(inp['rw_w0'][l, d]) for d in range(2)], 1))
        put(l, 'a0', np.concatenate([_fm(inp['rw_a0'][l, d]) for d in range(2)], 1))
        for nm, key in [('kk', 'rw_kk'), ('ka', 'rw_ka'), ('rk', 'rw_rk'), ('gnw', 'rw_gn_w'), ('gnb', 'rw_gn_b')]:
            put(l, nm, _fm(inp[key][l]))
        put(l, 'adab', _fm(inp['ada_b'][l][:2048]))
        for nm, key in [('lamre', 's5_lam_re'), ('lamim', 's5_lam_im')]:
            a = inp[key][l].reshape(2, 8, 2, 64)
            put(l, nm, np.ascontiguousarray(a.transpose(2, 3, 0, 1).reshape(128, 16)))
        a = np.broadcast_to(inp['s5_log_dt'][l].reshape(2, 8, 2, 1), (2, 8, 2, 64))
        put(l, 'ldt', np.ascontiguousarray(a.transpose(2, 3, 0, 1).reshape(128, 16)))
    sh['pv'] = pv
    bt = np.zeros((L, 128, 2, 4, 2, 128), np.float32)
    ct = np.zeros((L, 128, 8, 2, 128), np.float32)
    for l in range(L):
        for g in range(16):
            i, g2 = g // 2, g % 2
            for q in range(16):
                c = g * 16 + q
                j, p = c // 128, c % 128
                bt[l, p, j, i % 4, 0, g2 * 64:(g2 + 1) * 64] = inp['s5_b_re'][l, g, :, q]
                bt[l, p, j, i % 4, 1, g2 * 64:(g2 + 1) * 64] = inp['s5_b_im'][l, g, :, q]
            m0 = (i % 4) * 32 + g2 * 16
            ct[l, g2 * 64:(g2 + 1) * 64, i, 0, m0:m0 + 16] = inp['s5_c_re'][l, g].T
            ct[l, g2 * 64:(g2 + 1) * 64, i, 1, m0:m0 + 16] = inp['s5_c_im'][l, g].T
    sh['s5bt'] = bt
    sh['s5ct'] = ct
    sh['gluw'] = np.ascontiguousarray(inp['s5_glu_w'].reshape(L, 2, 128, 256).transpose(0, 2, 1, 3))
    lw2 = np.zeros((L, 128, 2, 256), np.float32)
    for l in range(L):
        lw2[l, 0:16, 0] = inp['rw_w2'][l, 0]
        lw2[l, 32:48, 1] = inp['rw_w2'][l, 1]
        lw2[l, 64:80, 0] = inp['rw_a2'][l, 0]
        lw2[l, 80:96, 1] = inp['rw_a2'][l, 1]
    sh['lw2'] = lw2
    sh['wbr'] = np.ascontiguousarray(inp['w_branch'].reshape(L, 4, 2, 128, 1024).transpose(0, 3, 1, 2, 4))
    sh['wout'] = np.ascontiguousarray(inp['w_out'].reshape(L, 8, 128, 1024).transpose(0, 2, 1, 3))
    rows = np.zeros((L, 128, 4096 + 512), np.float32)
    for l in range(L):
        rows[l, :, 0:1024] = inp['b_out'][l][None]
        rows[l, :, 1024:2048] = inp['ln_w'][l][None]
        rows[l, :, 2048:3072] = inp['ln_b'][l][None]
        rows[l, :, 3072:4096] = inp['ada_b'][l][None, 2048:3072]
        rows[l, :, 4096:4352] = inp['b_in'][l][None, 1280:1536]
        rows[l, :, 4352:4608] = inp['b_in'][l][None, 2304:2560]
    sh['rows'] = rows
    sh['masks'] = _masks()
    cos, sins, pm = _rot_tables()
    sh['rcos'] = cos
    sh['rsin'] = sins
    t = np.arange(128)
    cst = np.zeros((128, 9, 128), np.float32)
    cst[:, 0] = pm
    cst[:, 1] = ((t[:, None] // 64) == (t[None, :] // 64))
    cst[:, 2] = np.maximum(t[None, :] - t[:, None], 0)
    cst[:, 3] = np.maximum(t[:, None] - t[None, :], 0)
    cst[:, 4] = (t[None, :] >= t[:, None])
    cst[:, 5] = (t[None, :] <= t[:, None])
    cst[:, 6, :64] = ((t[:, None] % 64) == np.arange(64)[None, :])
    cst[:, 6, 64:68] = ((t[:, None] // 32) == np.arange(4)[None, :])
    cst[:, 6, 68] = 127 - t
    cst[:, 6, 69] = t
    cst[:, 7] = t[None, :] + 1.0
    cst[:, 8] = 128.0 - t[None, :]
    sh['cst'] = cst
    return sh


def prep_core(inp, b):
    pc = {}
    pc['hin'] = np.ascontiguousarray(np.concatenate([inp['ctx'][b], inp['x'][b]], 0))
    cv = np.stack([inp['c'][b], inp['c_ctx']], -1)
    pc['cvec'] = np.ascontiguousarray(cv.reshape(8, 128, 2).transpose(1, 0, 2))
    return pc


SHAPES = dict(hin=[NT, DM], cvec=[128, 8, 2], w_in=[2, 128, 8, NCOL], ada_w=[2, 128, 8, 3072], pv=[2, 128, NPV],
              s5bt=[2, 128, 2, 4, 2, 128], s5ct=[2, 128, 8, 2, 128], gluw=[2, 128, 2, 256], lw2=[2, 128, 2, 256],
              wbr=[2, 128, 4, 2, 1024], wout=[2, 128, 8, 1024], rows=[2, 128, 4608], masks=[128, 18, 128],
              rcos=[128, 2048], rsin=[128, 2048], cst=[128, 9, 128])


def build(debug=(), nlayers=2, phases=('s5', 'hg', 'ret', 'rw', 'merge'), stop=None):
    nc = bass.Bass("TRN2", target_bir_lowering=False)
    S = Sched(nc)
    dr = {k: nc.dram_tensor(k, list(v), F32, kind="ExternalInput").ap() for k, v in SHAPES.items()}
    out_d = nc.dram_tensor("out", [2048, DM], F32, kind="ExternalOutput").ap()
    h1_d = nc.dram_tensor("h1", [NT, DM], F32, kind="Internal").ap()
    sgd = nc.dram_tensor("sgd", [32, 128, NT], BF16, kind="Internal").ap()
    pre_sg = set()
    dbg_d = {}

    def dbg_out(name, shape):
        dbg_d[name] = nc.dram_tensor("dbg_" + name, list(shape), F32, kind="ExternalOutput").ap()
        return dbg_d[name]

    uid = [0]

    def key(p='k'):
        uid[0] += 1
        return '%s%d' % (p, uid[0])

    with contextlib.ExitStack() as top:
        S.sems = {n: top.enter_context(nc.semaphore(n)) for n in S.sem_names()}

        minrem = {}

        def sb(st, name, shape, dt=F32):
            uid[0] += 1
            t_ = st.enter_context(nc.sbuf_tensor("%s_%d" % (name, uid[0]), list(shape), dt))
            if MEMDBG:
                pre = name[:2]
                minrem[pre] = min(minrem.get(pre, 1 << 30), nc.sbuf_bytes_remaining)
            return t_

        def ps(st, name, shape, dt=F32):
            uid[0] += 1
            return st.enter_context(nc.psum_tensor("%s_%d" % (name, uid[0]), list(shape), dt))

        def mm(out, lhsT, rhs, r, w, start=True, stop=True):
            S.op('pe', lambda e: e.matmul(out, lhsT=lhsT, rhs=rhs, start=start, stop=stop), reads=r, writes=w)

        def tr(out, in_, ident, r, w):
            S.op('pe', lambda e: e.transpose(out, in_, ident), reads=r, writes=w)

        def act(out, in_, func, r, w, bias=0.0, scale=1.0):
            S.op('act', lambda e: e.activation(out=out, in_=in_, func=func, bias=bias, scale=scale), reads=r, writes=w)

        def tt(en, out, in0, in1, op, r, w):
            S.op(en, lambda e: e.tensor_tensor(out=out, in0=in0, in1=in1, op=op), reads=r, writes=w)

        def ts(en, out, in0, s1, s2, op0, op1, r, w):
            if s2 is None:
                S.op(en, lambda e: e.tensor_scalar(out=out, in0=in0, scalar1=s1, scalar2=None, op0=op0), reads=r, writes=w)
            else:
                S.op(en, lambda e: e.tensor_scalar(out=out, in0=in0, scalar1=s1, scalar2=s2, op0=op0, op1=op1),
                     reads=r, writes=w)

        def stt(out, in0, sc, in1, op0, op1, r, w):
            S.op('dve', lambda e: e.scalar_tensor_tensor(out=out, in0=in0, scalar=sc, in1=in1, op0=op0, op1=op1),
                 reads=r, writes=w)

        def cp(en, out, in_, r, w):
            if en == 'act':
                S.op('act', lambda e: e.copy(out=out, in_=in_), reads=r, writes=w)
            else:
                S.op(en, lambda e: e.tensor_copy(out=out, in_=in_), reads=r, writes=w)

        def memset(en, ap, val, w):
            S.op(en, lambda e: e.memset(ap, val), writes=w)

        def run_pipelined(gens, stagger):
            it = iter(gens)
            active, pending, rounds = [], True, 0
            while pending or active:
                if pending and rounds % stagger == 0:
                    try:
                        active.append(next(it))
                    except StopIteration:
                        pending = False
                for g in list(active):
                    try:
                        next(g)
                    except StopIteration:
                        active.remove(g)
                rounds += 1

        def mkbanks(st_, n, prefix):
            bl = [ps(st_, "%s%d" % (prefix, i), [128, 512], F32) for i in range(n)]
            cnt = [0]

            def bank():
                i = cnt[0] % n
                cnt[0] += 1
                return bl[i], '%s%d' % (prefix, i)
            return bank

        def gate_jobs(l, last, st_, bankfn, kds):
            wgt = [sb(st_, "gjw%d" % i, [128, 8, 128], BF16) for i in range(2)]
            sgs = [sb(st_, "gjs%d" % i, [128, 512], BF16) for i in range(2)]
            cnt = [0]

            def job(i, kd):
                k, dt_ = kd // 8, kd % 8
                w_, wk_ = wgt[i % 2], 'gjw%d' % (i % 2)
                c0 = 3968 + k * 1024 + dt_ * 128
                S.dma('pool', w_[:], dr['w_in'][l][:, :, c0:c0 + 128], writes=[wk_])
                yield
                for (n0, nn) in BLOCKS:
                    if last and n0 < 256:
                        continue
                    pg_, pgk_ = bankfn()
                    for jj in range(8):
                        mm(pg_[:, 0:nn], w_[:, jj, :], uT[:, jj, n0:n0 + nn], [wk_] + uTk[n0 // 128:(n0 + nn) // 128], [pgk_],
                           start=(jj == 0), stop=(jj == 7))
                    yield
                    c_ = cnt[0] % 2
                    cnt[0] += 1
                    act(sgs[c_][:, 0:nn], pg_[:, 0:nn], AF.Sigmoid, [pgk_, 'pvt'], ['gjs%d' % c_], bias=pv('bin', 31 + k * 8 + dt_))
                    yield
                    S.dma('sp', sgd[kd][:, n0:n0 + nn], sgs[c_][:, 0:nn], reads=['gjs%d' % c_], writes=['sgd'])
                    yield
                pre_sg.add((l, kd))
            return [job(i, kd) for i, kd in enumerate(kds)]

        def interleave(main, extra, every):
            out, ei = [], 0
            extra = list(extra)
            for i, g in enumerate(main):
                out.append(g)
                if (i + 1) % every == 0 and ei < len(extra):
                    out.append(extra[ei])
                    ei += 1
            out.extend(extra[ei:])
            return out

        def dbg_dump(name, ap, shape, r):
            if name in debug:
                d = dbg_out(name, shape)
                S.dma('sp', d, ap, reads=r, writes=['dbgout_' + name])

        cstb = sb(top, "cstb", [128, 3, 128], BF16)
        cstf = sb(top, "cstf", [128, 7, 128], F32)
        maskb = sb(top, "maskb", [128, 18, 128], BF16)
        silc = sb(top, "silc", [128, 8, 2], F32)
        S.dma('pool', cstb[:, 0:2, :], dr['cst'][:, 0:2, :], writes=['cstb'])
        S.dma('sp', cstf[:], dr['cst'][:, 2:9, :], writes=['cstf'])
        S.dma('pool', maskb[:], dr['masks'], writes=['maskb'])
        S.dma('sp', silc[:], dr['cvec'], writes=['silc'])
        memset('pool', cstb[:, 2, :], 0.0, ['cstb'])
        S.op('pool', lambda e: e.affine_select(out=cstb[:, 2, :], in_=cstb[:, 2, :], pattern=[[-1, 128]],
                                               compare_op=ALU.not_equal, fill=1.0, base=0, channel_multiplier=1),
             reads=['cstb'], writes=['cstb'])
        act(silc[:], silc[:], AF.Silu, ['silc'], ['silc'])
        identb = cstb[:, 2, :]
        bonesb = cstb[:, 1, :]

        uT = sb(top, "uT", [128, 8, NT], BF16)
        Y = sb(top, "Y", [128, 4, 2, NT], BF16)
        pvt = sb(top, "pvt", [128, NPV], F32)
        if debug:
            memset('pool', Y[:], 0.0, ['Y0', 'Y1', 'Y2', 'Y3'])
        modfm = sb(top, "modfm", [128, 16, 2], F32)
        gatebc = sb(top, "gatebc", [128, 2, DM], F32)

        def pv(name, j=None, n=1):
            o, cnt = PV[name]
            if j is None:
                return pvt[:, o:o + cnt]
            return pvt[:, o + j:o + j + n]

        PHASES = {}
        def proj_fm(st, wt, wk, mlist, evac, pp, ppk):
            cnt = 0
            for (n0, nn) in BLOCKS:
                for mi, m in enumerate(mlist):
                    p_, pk_ = pp[cnt % len(pp)], ppk[cnt % len(pp)]
                    cnt += 1
                    for j in range(8):
                        mm(p_[:, 0:nn], wt[:, j, m * 128:(m + 1) * 128], uT[:, j, n0:n0 + nn],
                           ['%s%d' % (wk, m // 2)] + uTk[n0 // 128:(n0 + nn) // 128], [pk_], start=(j == 0), stop=(j == 7))
                    evac(mi, m, n0, nn, p_, pk_)

        def phase_s5(l, h_src, last):
            L = 128
            with contextlib.ExitStack() as st:
                btb = sb(st, "btb", [128, 2, 4, 2, 128], BF16)
                ctb = sb(st, "ctb", [128, 8, 2, 128], BF16)
                glub = sb(st, "glub", [128, 2, 256], BF16)
                S.dma('pool', btb[:], dr['s5bt'][l], writes=['btb'])
                S.dma('pool', ctb[:], dr['s5ct'][l], writes=['ctb'])
                S.dma('pool', glub[:], dr['gluw'][l], writes=['glub'])
                ts('pool', ctb[:, :, 1, :], ctb[:, :, 1, :], -1.0, 0.0, ALU.mult, ALU.add, ['ctb'], ['ctb'])
                ub = sb(st, "s5u", [128, 2, NT], BF16)
                zs = sb(st, "s5z", [128, 2, NT], BF16)
                yacc = sb(st, "yacc", [128, 2, NT], F32)
                PT = sb(st, "s5PT", [128, 16, 2, L], F32)
                QT = sb(st, "s5QT", [128, 16, 2, L], F32)
                sst = sb(st, "s5st", [128, 16, 2], F32)
                ones = sb(st, "s5ones", [128, L], F32)
                memset('pool', yacc[:], 0.0, ['yacc'])
                memset('pool', sst[:], 0.0, ['sst'])
                memset('pool', ones[:], 1.0, ['s5ones'])
                with contextlib.ExitStack() as st2:
                    wsu = sb(st2, "wsu", [128, 8, 512], BF16)
                    for pc_ in range(2):
                        S.dma('pool', wsu[:, :, pc_ * 256:(pc_ + 1) * 256], dr['w_in'][l][:, :, pc_ * 256:(pc_ + 1) * 256], writes=['wsu%d' % pc_])
                    pp = [ps(st2, "s5pp%d" % i, [128, 512], F32) for i in range(2)]

                    def evac(mi, m, n0, nn, p_, pk_):
                        if m < 2:
                            act(ub[:, m, n0:n0 + nn], p_[:, 0:nn], AF.Identity, [pk_, 'pvt'], ['s5u'], bias=pv('bin', m))
                        else:
                            act(zs[:, m - 2, n0:n0 + nn], p_[:, 0:nn], AF.Silu, [pk_, 'pvt'], ['s5z'], bias=pv('bin', m))
                    proj_fm(st2, wsu, 'wsu', [0, 1, 2, 3], evac, pp, ['s5pp0', 's5pp1'])
                    sm = sb(st2, "s5sm", [128, 20, 16], F32)
                    K_ = 's5sm'

                    def Sm(i):
                        return sm[:, i, :]

                    def T2(o, a, b, op):
                        tt('dve', Sm(o), a if not isinstance(a, int) else Sm(a), b if not isinstance(b, int) else Sm(b), op,
                           [K_, 'pvt'], [K_])
                    lamre, lamim = pv('lamre'), pv('lamim')
                    act(Sm(0), pv('ldt'), AF.Exp, ['pvt'], [K_])
                    T2(1, lamre, 0, ALU.mult)
                    act(Sm(2), Sm(1), AF.Exp, [K_], [K_])
                    act(Sm(3), Sm(1), AF.Exp, [K_], [K_], scale=-1.0)
                    T2(4, lamim, 0, ALU.mult)
                    ts('dve', Sm(5), Sm(4), PI / 2, None, ALU.add, None, [K_], [K_])
                    for x in (4, 5):
                        for _ in range(4):
                            ts('dve', Sm(16), Sm(x), PI, 2 * PI, ALU.is_gt, ALU.mult, [K_], [K_])
                            T2(x, x, 16, ALU.subtract)
                    act(Sm(6), Sm(4), AF.Sin, [K_], [K_])
                    act(Sm(7), Sm(5), AF.Sin, [K_], [K_])
                    T2(8, 2, 7, ALU.mult)
                    T2(9, 2, 6, ALU.mult)
                    T2(10, 3, 7, ALU.mult)
                    stt(Sm(11), Sm(3), -1.0, Sm(6), ALU.mult, ALU.mult, [K_], [K_])
                    ts('dve', Sm(12), Sm(8), -1.0, None, ALU.add, None, [K_], [K_])
                    T2(16, lamre, lamre, ALU.mult)
                    T2(17, lamim, lamim, ALU.mult)
                    T2(13, 16, 17, ALU.add)
                    S.op('dve', lambda e: e.reciprocal(out=Sm(13), in_=Sm(13)), reads=[K_], writes=[K_])
                    T2(16, 12, lamre, ALU.mult)
                    T2(17, 9, lamim, ALU.mult)
                    T2(16, 16, 17, ALU.add)
                    T2(14, 16, 13, ALU.mult)
                    T2(16, 9, lamre, ALU.mult)
                    T2(17, 12, lamim, ALU.mult)
                    T2(16, 16, 17, ALU.subtract)
                    T2(15, 16, 13, ALU.mult)
                    tmpa = sb(st2, "s5ta", [128, 16, L], F32)
                    tmpb = sb(st2, "s5tb", [128, 16, L], F32)

                    def cmul_bc(dst_re, dst_im, src_re, src_im, s_re, s_im, m):
                        sr = s_re.unsqueeze(2).broadcast_to([128, 16, m])
                        si = s_im.unsqueeze(2).broadcast_to([128, 16, m])
                        ta, tb = tmpa[:, :, 0:m], tmpb[:, :, 0:m]
                        kk_ = ['s5tab', 's5ta', 's5tb', 's5tc', K_]
                        tt('dve', ta, src_re, sr, ALU.mult, kk_, ['s5ta'])
                        tt('dve', tb, src_im, si, ALU.mult, kk_, ['s5tb'])
                        tt('dve', dst_re, ta, tb, ALU.subtract, kk_, ['s5tab'])
                        tt('dve', ta, src_re, si, ALU.mult, kk_, ['s5ta'])
                        tt('dve', tb, src_im, sr, ALU.mult, kk_, ['s5tb'])
                        tt('dve', dst_im, ta, tb, ALU.add, kk_, ['s5tab'])
                    for (TB, a_re, a_im) in ((PT, 8, 9), (QT, 10, 11)):
                        cp('dve', TB[:, :, 0, 0], Sm(a_re), [K_], ['s5tab'])
                        cp('dve', TB[:, :, 1, 0], Sm(a_im), [K_], ['s5tab'])
                        m = 1
                        while m < L:
                            cmul_bc(TB[:, :, 0, m:2 * m], TB[:, :, 1, m:2 * m], TB[:, :, 0, 0:m], TB[:, :, 1, 0:m],
                                    TB[:, :, 0, m - 1], TB[:, :, 1, m - 1], m)
                            m *= 2
                    tmpc = sb(st2, "s5tc", [128, 16, L], F32)
                    cp('dve', tmpc[:], QT[:, :, 0, :], ['s5tab'], ['s5tc'])
                    cmul_bc(QT[:, :, 0, :], QT[:, :, 1, :], tmpc[:], QT[:, :, 1, :], Sm(14), Sm(15), L)
                    S.barrier()
                with contextlib.ExitStack() as st2:
                    NB = 8
                    xa = [sb(st2, "s5xa%d" % i, [128, 2, L], F32) for i in range(NB)]
                    xb_ = [sb(st2, "s5xb%d" % i, [128, 2, L], F32) for i in range(NB)]
                    cw = [sb(st2, "s5cw%d" % i, [128, 2, L], F32) for i in range(NB)]
                    hb = [sb(st2, "s5hb%d" % i, [128, 2, L], BF16) for i in range(NB)]
                    pbu = [ps(st2, "s5pb%d" % i, [128, 2, 2, L], F32) for i in range(4)]
                    py = [ps(st2, "s5py%d" % i, [128, 512], F32) for i in range(2)]
                    orders = [list(range(NTL)), [1, 0] + list(range(NTL - 1, 1, -1))]
                    def s5group(gi, step, d, j):
                        c = orders[d][step]
                        n0 = c * L
                        rev = (d == 1)
                        U = []
                        for ii in range(4):
                            un = gi * 4 + ii
                            bnk = (un // 2) % 4
                            U.append(dict(ii=ii, i=j * 4 + ii, q=d * 8 + j * 4 + ii, pb=pbu[bnk][:, un % 2], pbk='s5pb%d' % bnk,
                                          A=xa[un % NB], Ak='s5xa%d' % (un % NB), B=xb_[un % NB], Bk='s5xb%d' % (un % NB),
                                          C=cw[un % NB], Ck='s5cw%d' % (un % NB), H=hb[un % NB], Hk='s5hb%d' % (un % NB)))
                        for u in U:
                            for ri in range(2):
                                mm(u['pb'][:, ri, :], btb[:, j, u['ii'], ri, :], ub[:, j, n0:n0 + L], ['btb', 's5u'], [u['pbk']])
                        yield
                        for u in U:
                            src = u['pb'][:, :, ::-1] if rev else u['pb'][:, :, :]
                            tt('dve', u['A'][:], src, QT[:, u['q'], 0:1, :].broadcast_to([128, 2, L]), ALU.mult,
                               [u['pbk'], 's5tab'], [u['Ak']])
                        yield
                        for u in U:
                            src = u['pb'][:, ::-1, ::-1] if rev else u['pb'][:, ::-1, :]
                            tt('dve', u['B'][:], src, QT[:, u['q'], 1:2, :].broadcast_to([128, 2, L]), ALU.mult,
                               [u['pbk'], 's5tab'], [u['Bk']])
                        yield
                        for u in U:
                            tt('dve', u['A'][:, 0, :], u['A'][:, 0, :], u['B'][:, 0, :], ALU.subtract, [u['Ak'], u['Bk']], [u['Ak']])
                        yield
                        for u in U:
                            tt('dve', u['A'][:, 1, :], u['A'][:, 1, :], u['B'][:, 1, :], ALU.add, [u['Ak'], u['Bk']], [u['Ak']])
                        yield
                        for ri in range(2):
                            for u in U:
                                q = u['q']
                                S.op('dve', lambda e, u=u, ri=ri, q=q: e.tensor_tensor_scan(
                                    out=u['C'][:, ri, :], data0=ones[:], data1=u['A'][:, ri, :], initial=sst[:, q, ri:ri + 1],
                                    op0=ALU.mult, op1=ALU.add), reads=[u['Ak'], 's5ones', 'sst%d' % q, 'sst'], writes=[u['Ck']])
                            yield
                        for u in U:
                            tt('pool', u['A'][:], u['C'][:], PT[:, u['q'], 0:1, :].broadcast_to([128, 2, L]), ALU.mult,
                               [u['Ck'], 's5tab', u['Ak']], [u['Ak']])
                        yield
                        for u in U:
                            tt('pool', u['B'][:], u['C'][:, ::-1, :], PT[:, u['q'], 1:2, :].broadcast_to([128, 2, L]), ALU.mult,
                               [u['Ck'], 's5tab', u['Bk']], [u['Bk']])
                        yield
                        for u in U:
                            tt('pool', u['A'][:, 0, :], u['A'][:, 0, :], u['B'][:, 0, :], ALU.subtract, [u['Ak'], u['Bk']], [u['Ak']])
                        yield
                        for u in U:
                            tt('pool', u['A'][:, 1, :], u['A'][:, 1, :], u['B'][:, 1, :], ALU.add, [u['Ak'], u['Bk']], [u['Ak']])
                        yield
                        for u in U:
                            cp('pool', sst[:, u['q'], :], u['A'][:, :, L - 1], [u['Ak']], ['sst%d' % u['q']])
                        yield
                        for u in U:
                            hsrc = u['A'][:, :, ::-1] if rev else u['A'][:]
                            cp('act', u['H'][:], hsrc, [u['Ak']], [u['Hk']])
                        yield
                        pyr = py[gi % 2][:, 0:L]
                        pyk = 's5py%d' % (gi % 2)
                        for k_, u in enumerate(U):
                            for ri in range(2):
                                mm(pyr, ctb[:, u['i'], ri, :], u['H'][:, ri, :], ['ctb', u['Hk']], [pyk],
                                   start=(k_ == 0 and ri == 0), stop=(k_ == 3 and ri == 1))
                        yield
                        yield
                        yield
                        tt('dve', yacc[:, j, n0:n0 + L], yacc[:, j, n0:n0 + L], pyr, ALU.add, [pyk, 'yacc'], ['yacc'])

                    glist = [(step, d, j) for step in range(NTL) for d in range(2) for j in range(2)]
                    gbank = mkbanks(st2, 2, "s5gk") if (GATE_PRE and GJ_S5) else None
                    gj = gate_jobs(l, last, st2, gbank, GJ_S5) if (GATE_PRE and GJ_S5) else []
                    run_pipelined(interleave([s5group(gi, *g) for gi, g in enumerate(glist)], gj, 2), STG['s5'])
                    S.barrier()
                for j in range(2):
                    stt(yacc[:, j, :], ub[:, j, :], pv('s5d', j), yacc[:, j, :], ALU.mult, ALU.add, ['s5u', 'yacc', 'pvt'],
                        ['yacc'])
                dbg_dump('ya%d' % l, yacc[:], [128, 2, NT], ['yacc'])
                with contextlib.ExitStack() as st2:
                    t1 = [sb(st2, "s5g1_%d" % i, [128, 512], F32) for i in range(2)]
                    t2 = [sb(st2, "s5g2_%d" % i, [128, 512], BF16) for i in range(2)]
                    pg = [ps(st2, "s5pg%d" % i, [128, 512], F32) for i in range(2)]
                    cnt = 0
                    for (n0, nn) in BLOCKS:
                        for j in range(2):
                            a, ak = t1[cnt % 2], 's5g1_%d' % (cnt % 2)
                            cnt += 1
                            ysl = yacc[:, j, n0:n0 + nn]
                            act(a[:, 0:nn], ysl, AF.Square, ['yacc'], [ak])
                            ts('dve', a[:, 0:nn], a[:, 0:nn], 0.044715, 1.0, ALU.mult, ALU.add, [ak], [ak])
                            tt('dve', a[:, 0:nn], a[:, 0:nn], ysl, ALU.mult, [ak, 'yacc'], [ak])
                            act(a[:, 0:nn], a[:, 0:nn], AF.Sigmoid, [ak], [ak], scale=1.5957691216057308)
                            tt('dve', ub[:, j, n0:n0 + nn], a[:, 0:nn], ysl, ALU.mult, [ak, 'yacc'], ['s5u'])
                    cnt = 0
                    for (n0, nn) in BLOCKS:
                        for m in range(2):
                            p_, pk_ = pg[cnt % 2], 's5pg%d' % (cnt % 2)
                            b_, bk_ = t2[cnt % 2], 's5g2_%d' % (cnt % 2)
                            cnt += 1
                            for jc in range(2):
                                mm(p_[:, 0:nn], glub[:, jc, m * 128:(m + 1) * 128], ub[:, jc, n0:n0 + nn], ['glub', 's5u'], [pk_],
                                   start=(jc == 0), stop=(jc == 1))
                            act(b_[:, 0:nn], p_[:, 0:nn], AF.Sigmoid, [pk_, 'pvt'], [bk_], bias=pv('glub', m))
                            tt('dve', b_[:, 0:nn], b_[:, 0:nn], ub[:, m, n0:n0 + nn], ALU.mult, [bk_, 's5u'], [bk_])
                            tt('pool', Y[:, 0, m, n0:n0 + nn], b_[:, 0:nn], zs[:, m, n0:n0 + nn], ALU.mult, [bk_, 's5z'], ['Y0'])
                    S.barrier()
                S.barrier()
        PHASES['s5'] = phase_s5
        def phase_hg(l, h_src, last):
            with contextlib.ExitStack() as st:
                QP = [sb(st, "hgQP%d" % d, [128, 2, NT], BF16) for d in range(2)]
                KP = [sb(st, "hgKP%d" % d, [128, 2, NT], BF16) for d in range(2)]
                G = sb(st, "hgG", [128, 2, 72, 2], F32)
                VT = sb(st, "hgVT", [128, NTL, 256], BF16)
                zs = sb(st, "hgzs", [128, 2, NT], BF16)
                lbt = sb(st, "hglbt", [128, 2, 4], F32)
                if l == 0:
                    memset('pool', lbt[:, 0, :], 0.0, ['hglbt'])
                    memset('pool', lbt[:, 1, :], 1.0, ['hglbt'])
                else:
                    o_, _ = PV['hglb']
                    tt('dve', lbt[:, 0, :], pvt[:, o_ + 4:o_ + 8], pvt[:, o_:o_ + 4], ALU.subtract, ['pvt'], ['hglbt'])
                    act(lbt[:, 0, :], lbt[:, 0, :], AF.Sigmoid, ['hglbt'], ['hglbt'])
                    ts('dve', lbt[:, 1, :], lbt[:, 0, :], -1.0, 1.0, ALU.mult, ALU.add, ['hglbt'], ['hglbt'])
                with contextlib.ExitStack() as st2:
                    wh = sb(st2, "hgw", [128, 8, 1280], BF16)
                    for pc_ in (0, 4, 1, 2, 3):
                        S.dma('pool', wh[:, :, pc_ * 256:(pc_ + 1) * 256], dr['w_in'][l][:, :, 512 + pc_ * 256:512 + (pc_ + 1) * 256], writes=['hgw%d' % pc_])
                    brow = sb(st2, "hgbrow", [128, 256], F32)
                    S.dma('sp', brow[:], dr['rows'][l][:, 4096:4352], writes=['hgbrow'])
                    R32 = sb(st2, "hgR32", [128, 512], F32)
                    memset('pool', R32[:], 1.0, ['hgR32'])
                    memset('pool', R32[:, 0:512:32], 0.0, ['hgR32'])
                    QS = [sb(st2, "hgQS%d" % i, [128, 2, 512], BF16) for i in range(2)]
                    T = [[sb(st2, "hgT%d_%d" % (i, k), [128, 512], F32) for k in range(4)] for i in range(2)]
                    pp = [ps(st2, "hgpp%d" % i, [128, 512], F32) for i in range(3)]
                    pt = [ps(st2, "hgpt%d" % i, [128, 512], F32) for i in range(2)]
                    def hgproj(cnt, ic, m, n0, nn):
                        ukeys = uTk[n0 // 128:(n0 + nn) // 128]
                        p_, pk_ = pp[cnt % 3], 'hgpp%d' % (cnt % 3)
                        bi = (n0 // 512) % 2 if n0 else 0
                        for jj in range(8):
                            mm(p_[:, 0:nn], wh[:, jj, m * 128:(m + 1) * 128], uT[:, jj, n0:n0 + nn], ['hgw%d' % (m // 2)] + ukeys, [pk_],
                               start=(jj == 0), stop=(jj == 7))
                        yield
                        bias = pv('bin', 4 + m)
                        if m < 2:
                            act(QS[bi][:, m, 0:nn], p_[:, 0:nn], AF.Silu, [pk_, 'pvt'], ['hgQS%d' % bi], bias=bias)
                            return
                        if m >= 8:
                            act(zs[:, m - 8, n0:n0 + nn], p_[:, 0:nn], AF.Silu, [pk_, 'pvt'], ['hgzs'], bias=bias)
                            return
                        d, j = (m - 2) // 2, (m - 2) % 2
                        Ts = T[ic % 2]
                        Tk = ['hgT%d_%d' % (ic % 2, k) for k in range(4)]
                        t1, t2, t3, t4 = [x[:, 0:nn] for x in Ts]
                        act(t1, p_[:, 0:nn], AF.Sigmoid, [pk_, 'pvt'], [Tk[0]], bias=bias)
                        yield
                        ts('dve', t1, t1, lbt[:, 1, d * 2 + j:d * 2 + j + 1], lbt[:, 0, d * 2 + j:d * 2 + j + 1], ALU.mult, ALU.add,
                           [Tk[0], 'hglbt'], [Tk[0]])
                        yield
                        act(t2, t1, AF.Ln, [Tk[0]], [Tk[1]])
                        yield
                        if d == 0:
                            S.op('dve', lambda e: e.tensor_tensor_scan(out=t3, data0=R32[:, 0:nn], data1=t2, initial=0.0,
                                                                       op0=ALU.mult, op1=ALU.add),
                                 reads=[Tk[1], 'hgR32'], writes=[Tk[2]])
                        else:
                            S.op('dve', lambda e: e.tensor_tensor_scan(out=t3[:, ::-1],
                                                                       data0=R32[:, 0:nn], data1=t2[:, ::-1], initial=0.0,
                                                                       op0=ALU.mult, op1=ALU.add),
                                 reads=[Tk[1], 'hgR32'], writes=[Tk[2]])
                        yield
                        ts('dve', t3, t3, -80.0, None, ALU.max, None, [Tk[2]], [Tk[2]])
                        ts('dve', t1, t1, -1.0, 1.0, ALU.mult, ALU.add, [Tk[0]], [Tk[0]])
                        yield
                        act(t4, t3, AF.Exp, [Tk[2]], [Tk[3]])
                        act(t2, t3, AF.Exp, [Tk[2]], [Tk[1]], scale=-1.0)
                        yield
                        tt('pool', KP[d][:, j, n0:n0 + nn], t1, t2, ALU.mult, [Tk[0], Tk[1]], ['hgKP%d' % d])
                        tt('pool', QP[d][:, j, n0:n0 + nn], QS[bi][:, j, 0:nn], t4, ALU.mult, ['hgQS%d' % bi, Tk[3]], ['hgQP%d' % d])
                        c0 = n0 // 32
                        gsrc = t4[:, 31::32] if d == 0 else t4[:, 0::32]
                        cp('act', G[:, d, c0:c0 + nn // 32, j], gsrc, [Tk[3]], ['hgG'])

                    plist = []
                    cnt = 0
                    ic = 0
                    for (n0, nn) in BLOCKS:
                        for m in (0, 1, 8, 9, 2, 3, 4, 5):
                            plist.append((cnt, ic, m, n0, nn))
                            cnt += 1
                            if 2 <= m < 8:
                                ic += 1
                    run_pipelined((hgproj(*p) for p in plist), STG['hgproj'])
                    for t in range(NTL):
                        p_, pk_ = pt[t % 2], 'hgpt%d' % (t % 2)
                        for jj in range(8):
                            mm(p_[:, 0:256], uT[:, jj, t * 128:(t + 1) * 128], wh[:, jj, 768:1024], ['hgw3', uTk[t]], [pk_],
                               start=(jj == 0), stop=(jj == 7))
                        tt('dve', VT[:, t, :], p_[:, 0:256], brow[:], ALU.add, [pk_, 'hgbrow'], ['hgVT'])
                    S.barrier()
                Sall = [sb(st, "hgSall%d" % d, [128, 2, 72, 64], BF16) for d in range(2)]
                with contextlib.ExitStack() as st2:
                    Sst = [sb(st2, "hgS%d" % d, [128, 2, 64], F32) for d in range(2)]
                    kTm = [sb(st2, "hgkTm%d" % i, [128, 4, 256], BF16) for i in range(3)]
                    Ug = [sb(st2, "hgUg%d" % i, [128, 4, 2, 64], F32) for i in range(3)]
                    ptr = [ps(st2, "hgptr%d" % i, [128, 8, 128], BF16) for i in range(2)]
                    pU = [ps(st2, "hgpU%d" % i, [128, 4, 2, 64], F32) for i in range(3)]
                    orders = [list(range(NTL)), [1, 0] + list(range(NTL - 1, 1, -1))]
                    for d in range(2):
                        memset('pool', Sst[d][:], 0.0, ['hgS%d' % d])
                    def hgchain(it, step, d):
                        t = orders[d][step]
                        pr, prk = ptr[it % 2], 'hgptr%d' % (it % 2)
                        km, kmk = kTm[it % 3], 'hgkTm%d' % (it % 3)
                        pu, puk = pU[it % 3], 'hgpU%d' % (it % 3)
                        ug, ugk = Ug[it % 3], 'hgUg%d' % (it % 3)
                        for j in range(2):
                            tr(pr[:, j, :], KP[d][:, j, t * 128:(t + 1) * 128], identb, ['hgKP%d' % d, 'cstb'], [prk])
                        yield
                        for cc in range(4):
                            prf = pr[:, 0:2, :].rearrange("p a b -> p (a b)")
                            if cc % 2 == 0:
                                ts('dve', km[:, cc, :], prf, cstf[:, 4, 64 + cc:64 + cc + 1], None, ALU.mult, None, [prk, 'cstf'], [kmk])
                            else:
                                act(km[:, cc, :], prf, AF.Identity, [prk, 'cstf'], [kmk], scale=cstf[:, 4, 64 + cc:64 + cc + 1])
                        yield
                        for cc in range(4):
                            for h in range(4):
                                hp = (h % 2) * 64
                                mm(pu[hp:hp + 64, cc, h // 2, :], km[:, cc, h * 64:(h + 1) * 64], VT[:, t, h * 64:(h + 1) * 64],
                                   [kmk, 'hgVT'], [puk])
                        yield
                        tt('dve', ug[:], pu[:], G[:, d, t * 4:(t + 1) * 4, :].unsqueeze(3).broadcast_to([128, 4, 2, 64]), ALU.mult,
                           [puk, 'hgG'], [ugk])
                        yield
                        ccs = range(4) if d == 0 else range(3, -1, -1)
                        for cc in ccs:
                            c = t * 4 + cc
                            cp('act', Sall[d][:, :, c, :], Sst[d][:], ['hgS%d' % d], ['hgSall%d_%d' % (d, t)])
                            for j in range(2):
                                stt(Sst[d][:, j, :], Sst[d][:, j, :], G[:, d, c, j:j + 1], ug[:, cc, j, :], ALU.mult, ALU.add,
                                    ['hgS%d' % d, 'hgG', ugk], ['hgS%d' % d])
                            yield

                    gbank = mkbanks(st2, 3, "hggk") if GJ_SPLIT[0] else None
                    gj = gate_jobs(l, last, st2, gbank, GJ_SPLIT[0]) if (GATE_PRE and GJ_SPLIT[0]) else []
                    run_pipelined(interleave([hgchain(i_, sd[0], sd[1]) for i_, sd in enumerate([(s_, d_) for s_ in range(NTL) for d_ in range(2)])], gj, 3), STG['hgchain'])
                    S.barrier()
                with contextlib.ExitStack() as st2:
                    if ('yb%d' % l) in debug:
                        dbgbuf = sb(st2, "dbgbuf", [128, 2, NT], F32)
                    AT = [[sb(st2, "hgAT%d_%d" % (i, d), [128, 4, 128], BF16) for d in range(2)] for i in range(2)]
                    sq = [sb(st2, "hgsq%d" % i, [128, 2, 128], BF16) for i in range(2)]
                    rr = [sb(st2, "hgrr%d" % i, [128, 2, 128], F32) for i in range(2)]
                    ob = [sb(st2, "hgob%d" % i, [128, 2, 128], F32) for i in range(2)]
                    bank = mkbanks(st2, 8, "hgbk")

                    def hgout(t):
                        i2 = t % 2
                        tsl = slice(t * 128, (t + 1) * 128)
                        pas = {}
                        for d in range(2):
                            for par in range(2):
                                pas[(d, par)] = bank()
                            for h in range(4):
                                hp = (h % 2) * 64
                                pa, pak = pas[(d, h % 2)]
                                pav = pa[:, 0:256].rearrange("p (a b) -> p a b", a=2)
                                mm(pav[:, h // 2, :], KP[d][hp:hp + 64, h // 2, tsl], QP[d][hp:hp + 64, h // 2, tsl],
                                   ['hgKP%d' % d, 'hgQP%d' % d], [pak])
                        yield
                        for d in range(2):
                            for par in range(2):
                                pa, pak = pas[(d, par)]
                                pav = pa[:, 0:256].rearrange("p (a b) -> p a b", a=2)
                                tt('dve', AT[i2][d][:, par::2, :], pav, maskb[:, d, :].unsqueeze(1).broadcast_to([128, 2, 128]), ALU.mult,
                                   [pak, 'maskb'], ['hgAT%d_%d' % (i2, d)])
                        yield
                        pos = [bank() for _ in range(2)]
                        povs = [pos[par][0][:, 0:256].rearrange("p (a b) -> p a b", a=2) for par in range(2)]
                        for h in range(4):
                            hp = (h % 2) * 64
                            pok = pos[h % 2][1]
                            reg = povs[h % 2][hp:hp + 64, h // 2, :]
                            first = True
                            for d in range(2):
                                mm(reg, VT[:, t, h * 64:(h + 1) * 64], AT[i2][d][:, h, :], ['hgVT', 'hgAT%d_%d' % (i2, d)], [pok],
                                   start=first, stop=False)
                                first = False
                                for cc in range(4):
                                    c = t * 4 + cc
                                    mm(reg[:, cc * 32:(cc + 1) * 32], Sall[d][hp:hp + 64, h // 2, c, :],
                                       QP[d][hp:hp + 64, h // 2, t * 128 + cc * 32:t * 128 + (cc + 1) * 32],
                                       ['hgSall%d_%d' % (d, t), 'hgQP%d' % d], [pok], start=False, stop=(d == 1 and cc == 3))
                        yield
                        obk = 'hgob%d' % i2
                        cp('act', ob[i2][0:64], povs[0][0:64], [pos[0][1]], [obk])
                        cp('dve', ob[i2][64:128], povs[1][64:128], [pos[1][1]], [obk])
                        yield
                        pov = ob[i2][:]
                        pok = obk
                        if ('yb%d' % l) in debug:
                            cp('pool', dbgbuf[:, :, tsl], pov, [pok], ['dbgbuf'])
                        act(sq[i2][:], pov, AF.Square, [pok], ['hgsq%d' % i2])
                        yield
                        pss_, psk = bank()
                        psv = pss_[:, 0:256].rearrange("p (a b) -> p a b", a=2)
                        for j in range(2):
                            mm(psv[:, j, :], bonesb, sq[i2][:, j, :], ['cstb', 'hgsq%d' % i2], [psk])
                        yield
                        act(rr[i2][:], psv, AF.Sqrt, [psk], ['hgrr%d' % i2], bias=RMS_EPS, scale=1.0 / 64)
                        yield
                        S.op('dve', lambda e: e.reciprocal(out=rr[i2][:], in_=rr[i2][:]), reads=['hgrr%d' % i2], writes=['hgrr%d' % i2])
                        yield
                        tt('dve', rr[i2][:], pov, rr[i2][:], ALU.mult, [pok, 'hgrr%d' % i2], ['hgrr%d' % i2])
                        yield
                        for j in range(2):
                            stt(Y[:, 1, j, tsl], rr[i2][:, j, :], pv('hgnw', j), zs[:, j, tsl], ALU.mult, ALU.mult,
                                ['hgrr%d' % i2, 'pvt', 'hgzs'], ['Y1'])

                    gj = gate_jobs(l, last, st2, bank, GJ_SPLIT[1]) if (GATE_PRE and GJ_SPLIT[1]) else []
                    run_pipelined(interleave([hgout(t) for t in range(NTL) if not (last and t < 2 and not debug)], gj, 3), STG['out'])
                    if ('yb%d' % l) in debug:
                        dbg_dump('yb%d' % l, dbgbuf[:], [128, 2, NT], ['dbgbuf'])
                    S.barrier()
                S.barrier()
        PHASES['hg'] = phase_hg
        def phase_ret(l, h_src, last):
            with contextlib.ExitStack() as st:
                QR = sb(st, "rtQR", [128, 2, NT], BF16)
                KR = sb(st, "rtKR", [128, 2, NT], BF16)
                VT = sb(st, "rtVT", [128, NTL, 256], BF16)
                zs = sb(st, "rtzs", [128, 2, NT], BF16)
                Sall = [sb(st, "rtSall%d" % d, [128, 2, NTL, 64], BF16) for d in range(2)]
                LG = sb(st, "rtLG", [128, 4], F32)
                GL = sb(st, "rtGL", [128, 4], F32)
                LGH = sb(st, "rtLGH", [128, 8], F32)
                QDEC = sb(st, "rtQDEC", [128, 2, 2, 128], F32)
                KDEC = sb(st, "rtKDEC", [128, 2, 4], F32)
                DS = sb(st, "rtDS", [128, 4, 128], F32)
                tb8 = sb(st, "rtb8", [128, 2], F32)
                K_ = 'rttab'
                act(LG[:], pv('rdec'), AF.Exp, ['pvt'], [K_])
                ts('dve', LG[:], LG[:], -1.0, None, ALU.mult, None, [K_], [K_])
                act(GL[:], LG[:], AF.Exp, [K_], [K_], scale=128.0)
                act(LGH[:], pv('rdech'), AF.Exp, ['pvt'], [K_])
                ts('dve', LGH[:], LGH[:], -1.0, None, ALU.mult, None, [K_], [K_])
                for d in range(2):
                    for j in range(2):
                        act(QDEC[:, d, j, :], cstf[:, 5 + d, :], AF.Exp, ['cstf', K_], [K_], scale=LG[:, d * 2 + j:d * 2 + j + 1])
                    act(KDEC[:, d, :], LGH[:, d * 4:(d + 1) * 4], AF.Exp, ['cstf', K_], [K_], scale=cstf[:, 4, 68 + d:69 + d])
                with contextlib.ExitStack() as st2:
                    ta = sb(st2, "rtta", [128, 128], F32)
                    tb = sb(st2, "rttb", [128, 128], F32)
                    for h in range(4):
                        act(ta[:], cstf[:, 0, :], AF.Exp, ['cstf', K_], ['rtta'], scale=LGH[:, h:h + 1])
                        tt('dve', ta[:], ta[:], cstf[:, 2, :], ALU.mult, ['rtta', 'cstf'], ['rtta'])
                        act(tb[:], cstf[:, 1, :], AF.Exp, ['cstf', K_], ['rttb'], scale=LGH[:, 4 + h:5 + h])
                        tt('dve', tb[:], tb[:], cstf[:, 3, :], ALU.mult, ['rttb', 'cstf'], ['rttb'])
                        tt('dve', DS[:, h, :], ta[:], tb[:], ALU.add, ['rtta', 'rttb'], [K_])
                    ts('dve', tb8[:], pv('bin', 16, 2), 0.125, None, ALU.mult, None, ['pvt'], [K_])
                    S.barrier()
                if stop == 'ret_tab':
                    return
                with contextlib.ExitStack() as st2:
                    wr = sb(st2, "rtw", [128, 8, 1024], BF16)
                    for pc_ in range(4):
                        S.dma('pool', wr[:, :, pc_ * 256:(pc_ + 1) * 256], dr['w_in'][l][:, :, 1792 + pc_ * 256:1792 + (pc_ + 1) * 256], writes=['rtw%d' % pc_])
                    brow = sb(st2, "rtbrow", [128, 256], F32)
                    S.dma('sp', brow[:], dr['rows'][l][:, 4352:4608], writes=['rtbrow'])
                    COS = sb(st2, "rtcos", [128, 2048], F32)
                    SIN = sb(st2, "rtsin", [128, 2048], F32)
                    permf = sb(st2, "rtperm", [128, 128], F32)
                    S.dma('sp', COS[:], dr['rcos'], writes=['rtcos'])
                    S.dma('act', SIN[:], dr['rsin'], writes=['rtsin'])
                    S.dma('sp', permf[:], dr['cst'][:, 0, :], writes=['rtperm'])
                    qf = [sb(st2, "rtqf%d" % i, [128, 512], F32) for i in range(2)]
                    t1 = [sb(st2, "rtt1_%d" % i, [128, 512], F32) for i in range(2)]
                    pp = [ps(st2, "rtpp%d" % i, [128, 512], F32) for i in range(2)]
                    pq = [ps(st2, "rtpq%d" % i, [128, 512], F32) for i in range(2)]
                    pt = [ps(st2, "rtpt%d" % i, [128, 512], F32) for i in range(2)]
                    def rtproj(cnt, rc, m, n0, nn):
                        ukeys = uTk[n0 // 128:(n0 + nn) // 128]
                        p_, pk_ = pp[cnt % 2], 'rtpp%d' % (cnt % 2)
                        for jj in range(8):
                            mm(p_[:, 0:nn], wr[:, jj, m * 128:(m + 1) * 128], uT[:, jj, n0:n0 + nn], ['rtw%d' % (m // 2)] + ukeys, [pk_],
                               start=(jj == 0), stop=(jj == 7))
                        yield
                        if m >= 6:
                            act(zs[:, m - 6, n0:n0 + nn], p_[:, 0:nn], AF.Silu, [pk_, 'pvt'], ['rtzs'], bias=pv('bin', 14 + m))
                            return
                        isk = m >= 2
                        j = m % 2
                        dst = (KR if isk else QR)[:, j, n0:n0 + nn]
                        dk = 'rtKR' if isk else 'rtQR'
                        if n0 < 256:
                            if isk:
                                act(dst, p_[:, 0:nn], AF.Identity, [pk_, K_], [dk], bias=tb8[:, j:j + 1], scale=0.125)
                            else:
                                act(dst, p_[:, 0:nn], AF.Identity, [pk_, 'pvt'], [dk], bias=pv('bin', 14 + m))
                            return
                        q_, qk_ = qf[rc % 2], 'rtqf%d' % (rc % 2)
                        a_, ak_ = t1[rc % 2], 'rtt1_%d' % (rc % 2)
                        r_, rk_ = pq[rc % 2], 'rtpq%d' % (rc % 2)
                        if isk:
                            act(q_[:, 0:nn], p_[:, 0:nn], AF.Identity, [pk_, K_], [qk_], bias=tb8[:, j:j + 1], scale=0.125)
                        else:
                            act(q_[:, 0:nn], p_[:, 0:nn], AF.Identity, [pk_, 'pvt'], [qk_], bias=pv('bin', 14 + m))
                        yield
                        mm(r_[:, 0:nn], permf[:], q_[:, 0:nn], ['rtperm', qk_], [rk_])
                        yield
                        tsl = slice(n0 - 256, n0 - 256 + nn)
                        tt('dve', a_[:, 0:nn], r_[:, 0:nn], SIN[:, tsl], ALU.mult, [rk_, 'rtsin'], [ak_])
                        tt('pool', q_[:, 0:nn], q_[:, 0:nn], COS[:, tsl], ALU.mult, [qk_, 'rtcos'], [qk_])
                        yield
                        tt('dve', dst, a_[:, 0:nn], q_[:, 0:nn], ALU.add, [ak_, qk_], [dk])

                    plist = []
                    cnt = 0
                    rc = 0
                    for (n0, nn) in BLOCKS:
                        for m in (0, 1, 2, 3, 6, 7):
                            plist.append((cnt, rc, m, n0, nn))
                            cnt += 1
                            if m < 6 and n0 >= 256:
                                rc += 1
                    run_pipelined((rtproj(*p) for p in plist), 2)
                    for t in range(NTL):
                        p_, pk_ = pt[t % 2], 'rtpt%d' % (t % 2)
                        for jj in range(8):
                            mm(p_[:, 0:256], uT[:, jj, t * 128:(t + 1) * 128], wr[:, jj, 512:768], ['rtw2', uTk[t]], [pk_],
                               start=(jj == 0), stop=(jj == 7))
                        tt('dve', VT[:, t, :], p_[:, 0:256], brow[:], ALU.add, [pk_, 'rtbrow'], ['rtVT'])
                    S.barrier()
                if stop == 'ret_proj':
                    return
                with contextlib.ExitStack() as st2:
                    Sst = [sb(st2, "rtS%d" % d, [128, 2, 64], F32) for d in range(2)]
                    kT = [sb(st2, "rtkT%d" % i, [128, 256], BF16) for i in range(3)]
                    ptr = [ps(st2, "rtptr%d" % i, [128, 8, 128], BF16) for i in range(2)]
                    pU = [ps(st2, "rtpU%d" % i, [128, 512], F32) for i in range(3)]
                    orders = [list(range(NTL)), [1, 0] + list(range(NTL - 1, 1, -1))]
                    for d in range(2):
                        memset('pool', Sst[d][:], 0.0, ['rtS%d' % d])
                    def rtchain(it, step, d):
                        t = orders[d][step]
                        pr, prk = ptr[it % 2], 'rtptr%d' % (it % 2)
                        kt, ktk = kT[it % 3], 'rtkT%d' % (it % 3)
                        pu, puk = pU[it % 3], 'rtpU%d' % (it % 3)
                        puv = pu[:, 0:128].rearrange("p (a b) -> p a b", a=2)
                        for j in range(2):
                            tr(pr[:, j, :], KR[:, j, t * 128:(t + 1) * 128], identb, ['rtKR', 'cstb'], [prk])
                        yield
                        tt('dve', kt[:].rearrange("p (h k) -> p h k", h=4), pr[:, 0:2, :].rearrange("p a (b k) -> p (a b) k", b=2),
                           KDEC[:, d, :].unsqueeze(2).broadcast_to([128, 4, 64]), ALU.mult, [prk, K_], [ktk])
                        yield
                        for h in range(4):
                            hp = (h % 2) * 64
                            mm(puv[hp:hp + 64, h // 2, :], kt[:, h * 64:(h + 1) * 64], VT[:, t, h * 64:(h + 1) * 64], [ktk, 'rtVT'], [puk])
                        yield
                        cp('act', Sall[d][:, :, t, :], Sst[d][:], ['rtS%d' % d], ['rtSall%d_%d' % (d, t)])
                        for j in range(2):
                            stt(Sst[d][:, j, :], Sst[d][:, j, :], GL[:, d * 2 + j:d * 2 + j + 1], puv[:, j, :], ALU.mult, ALU.add,
                                ['rtS%d' % d, K_, puk], ['rtS%d' % d])

                    gbank = mkbanks(st2, 3, "rtgk") if GJ_SPLIT[2] else None
                    gj = gate_jobs(l, last, st2, gbank, GJ_SPLIT[2]) if (GATE_PRE and GJ_SPLIT[2]) else []
                    run_pipelined(interleave([rtchain(i_, sd[0], sd[1]) for i_, sd in enumerate([(s_, d_) for s_ in range(NTL) for d_ in range(2)])], gj, 4), STG['rtchain'])
                    S.barrier()
                if stop == 'ret_chain':
                    return
                with contextlib.ExitStack() as st2:
                    if ('yc%d' % l) in debug:
                        dbgbuf = sb(st2, "dbgbuf", [128, 2, NT], F32)
                    AT = [sb(st2, "rtAT%d" % i, [128, 4, 128], BF16) for i in range(2)]
                    qd = [[sb(st2, "rtqd%d_%d" % (i, d), [128, 2, 128], BF16) for d in range(2)] for i in range(2)]
                    sq = [sb(st2, "rtsq%d" % i, [128, 2, 128], BF16) for i in range(2)]
                    rr = [sb(st2, "rtrr%d" % i, [128, 2, 128], F32) for i in range(2)]
                    ob = [sb(st2, "rtob%d" % i, [128, 2, 128], F32) for i in range(2)]
                    bank = mkbanks(st2, 8, "rtbk")

                    def rtout(t):
                        i2 = t % 2
                        tsl = slice(t * 128, (t + 1) * 128)
                        pas = [bank() for _ in range(2)]
                        for h in range(4):
                            hp = (h % 2) * 64
                            pav = pas[h % 2][0][:, 0:256].rearrange("p (a b) -> p a b", a=2)
                            mm(pav[:, h // 2, :], KR[hp:hp + 64, h // 2, tsl], QR[hp:hp + 64, h // 2, tsl], ['rtKR', 'rtQR'], [pas[h % 2][1]])
                        for d in range(2):
                            tt('pool', qd[i2][d][:], QR[:, :, tsl], QDEC[:, d, :, :], ALU.mult, ['rtQR', K_], ['rtqd%d_%d' % (i2, d)])
                        yield
                        for par in range(2):
                            pav = pas[par][0][:, 0:256].rearrange("p (a b) -> p a b", a=2)
                            tt('dve', AT[i2][:, par::2, :], pav, DS[:, par::2, :], ALU.mult, [pas[par][1], K_], ['rtAT%d' % i2])
                        yield
                        pos = [bank() for _ in range(2)]
                        povs = [pos[par][0][:, 0:256].rearrange("p (a b) -> p a b", a=2) for par in range(2)]
                        for h in range(4):
                            hp = (h % 2) * 64
                            pok = pos[h % 2][1]
                            reg = povs[h % 2][hp:hp + 64, h // 2, :]
                            mm(reg, VT[:, t, h * 64:(h + 1) * 64], AT[i2][:, h, :], ['rtVT', 'rtAT%d' % i2], [pok], start=True, stop=False)
                            for d in range(2):
                                mm(reg, Sall[d][hp:hp + 64, h // 2, t, :], qd[i2][d][hp:hp + 64, h // 2, :],
                                   ['rtSall%d_%d' % (d, t), 'rtqd%d_%d' % (i2, d)], [pok], start=False, stop=(d == 1))
                        yield
                        obk = 'rtob%d' % i2
                        cp('act', ob[i2][0:64], povs[0][0:64], [pos[0][1]], [obk])
                        cp('dve', ob[i2][64:128], povs[1][64:128], [pos[1][1]], [obk])
                        yield
                        pov = ob[i2][:]
                        pok = obk
                        if ('yc%d' % l) in debug:
                            cp('pool', dbgbuf[:, :, tsl], pov, [pok], ['dbgbuf'])
                        act(sq[i2][:], pov, AF.Square, [pok], ['rtsq%d' % i2])
                        yield
                        pss_, psk = bank()
                        psv = pss_[:, 0:256].rearrange("p (a b) -> p a b", a=2)
                        for j in range(2):
                            mm(psv[:, j, :], bonesb, sq[i2][:, j, :], ['cstb', 'rtsq%d' % i2], [psk])
                        yield
                        act(rr[i2][:], psv, AF.Sqrt, [psk], ['rtrr%d' % i2], bias=RMS_EPS, scale=1.0 / 64)
                        yield
                        S.op('dve', lambda e: e.reciprocal(out=rr[i2][:], in_=rr[i2][:]), reads=['rtrr%d' % i2], writes=['rtrr%d' % i2])
                        yield
                        tt('dve', rr[i2][:], pov, rr[i2][:], ALU.mult, [pok, 'rtrr%d' % i2], ['rtrr%d' % i2])
                        yield
                        tt('pool', Y[:, 2, :, tsl], rr[i2][:], zs[:, :, tsl], ALU.mult, ['rtrr%d' % i2, 'rtzs'], ['Y2'])

                    gj = gate_jobs(l, last, st2, bank, GJ_SPLIT[3]) if (GATE_PRE and GJ_SPLIT[3]) else []
                    run_pipelined(interleave([rtout(t) for t in range(NTL) if not (last and t < 2 and not debug)], gj, 3), STG['out'])
                    if ('yc%d' % l) in debug:
                        dbg_dump('yc%d' % l, dbgbuf[:], [128, 2, NT], ['dbgbuf'])
                    S.barrier()
                S.barrier()
        PHASES['ret'] = phase_ret
        def phase_rw(l, h_src, last):
            with contextlib.ExitStack() as st:
                RB = sb(st, "rwRB", [128, 2, NT], BF16)
                KB = sb(st, "rwKB", [128, 2, NT], BF16)
                VB = sb(st, "rwVB", [128, 2, NT], BF16)
                LB = sb(st, "rwLB", [128, NT], BF16)
                zs = sb(st, "rwzs", [128, 2, NT], BF16)
                vT = sb(st, "rwvT", [128, NTL, 256], BF16)
                lw2b = sb(st, "rwlw2", [128, 2, 256], BF16)
                S.dma('pool', lw2b[:], dr['lw2'][l], writes=['rwlw2'])
                oka = sb(st, "rwoka", [128, 2], F32)
                ts('dve', oka[:], pv('ka'), -1.0, 1.0, ALU.mult, ALU.add, ['pvt'], ['rwoka'])
                seen_b, seen_o = set(), set()
                with contextlib.ExitStack() as st2:
                    ww = sb(st2, "rww", [128, 8, 1152], BF16)
                    for pc_ in range(9):
                        S.dma('pool', ww[:, :, pc_ * 128:(pc_ + 1) * 128], dr['w_in'][l][:, :, 2816 + pc_ * 128:2816 + (pc_ + 1) * 128], writes=['rww%d' % pc_])
                    XRs = [sb(st2, "rwXR%d" % i, [128, NT + 4], F32) for i in range(2)]
                    XSs = [sb(st2, "rwXS%d" % i, [128, NT], F32) for i in range(2)]
                    c0 = sb(st2, "rwc0", [128, 7], F32)
                    pp = [ps(st2, "rwpp%d" % i, [128, 512], F32) for i in range(3)]
                    ptr = [ps(st2, "rwptr%d" % i, [128, 8, 128], BF16) for i in range(2)]
                    o_mu, _ = PV['mu']
                    mu0, mu1 = pvt[:, o_mu:o_mu + 7], pvt[:, o_mu + 7:o_mu + 14]
                    tt('dve', c0[:], mu0, mu1, ALU.add, ['pvt'], ['rwc0'])
                    ts('dve', c0[:], c0[:], -1.0, 1.0, ALU.mult, ALU.add, ['rwc0'], ['rwc0'])
                    for i in range(2):
                        memset('pool', XRs[i][:], 0.0, ['rwXR%d' % i])
                    cnt = 0
                    for m in (0, 1, 2, 3, 4, 5, 7, 6, 8):
                        XR, XRk = XRs[m % 2], 'rwXR%d' % (m % 2)
                        XS, XSk = XSs[m % 2], 'rwXS%d' % (m % 2)
                        for (n0, nn) in BLOCKS:
                            p_, pk_ = pp[cnt % 3], 'rwpp%d' % (cnt % 3)
                            cnt += 1
                            for jj in range(8):
                                mm(p_[:, 0:nn], ww[:, jj, m * 128:(m + 1) * 128], uT[:, jj, n0:n0 + nn],
                                   ['rww%d' % m] + uTk[n0 // 128:(n0 + nn) // 128], [pk_], start=(jj == 0), stop=(jj == 7))
                            if m >= 7:
                                act(zs[:, m - 7, n0:n0 + nn], p_[:, 0:nn], AF.Silu, [pk_, 'pvt'], ['rwzs'], bias=pv('bin', 22 + m))
                            else:
                                xo = 1 if n0 < 256 else 3
                                act(XR[:, n0 + xo:n0 + xo + nn], p_[:, 0:nn], AF.Identity, [pk_, 'pvt'], [XRk], bias=pv('bin', 22 + m))
                        if m >= 7:
                            continue
                        for (b0, ln, o0) in ((1, 256, 0), (259, 2048, 256)):
                            ts('dve', XS[:, o0:o0 + ln], XR[:, b0:b0 + ln], c0[:, m:m + 1], None, ALU.mult, None, [XRk, 'rwc0'], [XSk])
                            stt(XS[:, o0:o0 + ln], XR[:, b0 - 1:b0 - 1 + ln], mu0[:, m:m + 1], XS[:, o0:o0 + ln], ALU.mult, ALU.add,
                                [XRk, 'pvt', XSk], [XSk])
                            if m < 6:
                                dstT, dk = [(RB, 'rwRB'), (KB, 'rwKB'), (VB, 'rwVB')][m // 2]
                                stt(dstT[:, m % 2, o0:o0 + ln], XR[:, b0 + 1:b0 + 1 + ln], mu1[:, m:m + 1], XS[:, o0:o0 + ln], ALU.mult, ALU.add,
                                    [XRk, 'pvt', XSk], [dk])
                            else:
                                stt(XS[:, o0:o0 + ln], XR[:, b0 + 1:b0 + 1 + ln], mu1[:, m:m + 1], XS[:, o0:o0 + ln], ALU.mult, ALU.add,
                                    [XRk, 'pvt', XSk], [XSk])
                        if m == 6:
                            act(LB[0:64, :], XS[0:64, :], AF.Tanh, [XSk], ['rwLB'])
                            cp('pool', LB[64:128, :], XS[64:128, :], [XSk], ['rwLB'])
                    for t in range(NTL):
                        pr, prk = ptr[t % 2], 'rwptr%d' % (t % 2)
                        for j in range(2):
                            tr(pr[:, j, :], VB[:, j, t * 128:(t + 1) * 128], identb, ['rwVB', 'cstb'], [prk])
                        cp('dve' if t % 2 == 0 else 'act', vT[:, t, :], pr[:, 0:2, :].rearrange("p a b -> p (a b)"), [prk], ['rwvT'])
                    S.barrier()
                if stop == 'rw_proj':
                    return
                OS = sb(st, "rwOS", [128, 2, NT], F32)
                with contextlib.ExitStack() as st2:
                    def B(name, shape, dt=BF16):
                        return sb(st2, "rw_" + name, shape, dt), "rw_" + name
                    R64, R64k = B("R64", [128, 256], BF16)
                    memset('pool', R64[:], 1.0, [R64k])
                    memset('pool', R64[:, 0:256:64], 0.0, [R64k])
                    LW, LWk = B("LW", [128, 2, 128], F32)
                    SA, SAk = B("SA", [128, 2, 128], F32)
                    LGm, LGk = B("LG", [128, 2, 128], F32)
                    U0, U0k = LGm, LGk
                    EG, EGk = B("EG", [128, 2, 128], F32)
                    ENG, ENGk = B("ENG", [128, 2, 128], F32)
                    EGM, EGMk = B("EGM", [128, 2, 128], F32)
                    TA, TAk = B("TA", [128, 2, 128], F32)
                    TB_, TBk = B("TB", [128, 2, 128], F32)
                    SQ, SQk = B("SQ", [128, 2, 128])
                    RKD, RKDk = SQ, SQk
                    OBt = (None, None)
                    Zst = [B("Z%d" % d, [128, 2, 64], F32) for d in range(2)]
                    BUF = [dict() for _ in range(2)]
                    for d_ in range(2):
                        BUF[d_]['KKN'] = B("KKN_%d" % d_, [128, 2, 128])
                        BUF[d_]['KT'] = [B("KT_%d_%d" % (d_, s_), [128, 3, 2, 128]) for s_ in range(2)]
                        BUF[d_]['RT'] = [B("RT_%d_%d" % (d_, s_), [128, 2, 128]) for s_ in range(2)]
                        for j_ in range(2):
                            sfx = "_%d_%d" % (d_, j_)
                            SB = dict()
                            SB['TM'] = B("TM" + sfx, [128, 3, 128])
                            for nm_ in ('A1T', 'A2T', 'A3T', 'A4T', 'ALT', 'Tm', 'TTm', 'Xb', 'RHS', 'BYb'):
                                SB[nm_] = B(nm_ + sfx, [128, 2, 128])
                            SB['NY'] = B("NY" + sfx, [128, 2, 64])
                            SB['RH'] = B("RH" + sfx, [128, 128])
                            SB['GTb'] = B("GTb" + sfx, [128, 2, 128])
                            SB['ZLG'] = B("ZLG" + sfx, [128, 2, 64], F32)
                            SB['Z0b'] = B("Z0b" + sfx, [128, 2, 64])
                            BUF[d_][j_] = SB
                        BUF[d_]['GLt'] = [B("GLt_%d_%d" % (d_, s_), [128, 2, 2], F32) for s_ in range(2)]
                    banks = [ps(st2, "rwbank%d" % i, [128, 512], F32) for i in range(8)]
                    bcnt = [0]

                    def bank():
                        i = bcnt[0] % 8
                        bcnt[0] += 1
                        return banks[i], 'rwbank%d' % i
                    for d in range(2):
                        memset('pool', Zst[d][0][:], 0.0, [Zst[d][1], 'rw_Zs_%d_0' % d, 'rw_Zs_%d_1' % d])
                    for d_ in range(2):
                        for j_ in range(2):
                            memset('pool', BUF[d_][j_]['GTb'][0][:], 0.0, [BUF[d_][j_]['GTb'][1]])
                    orders = [list(range(NTL)), [1, 0] + list(range(NTL - 1, 1, -1))]
                    bc3 = lambda ap: ap.unsqueeze(2).broadcast_to([128, 2, 128])
                    def prep(d, t, slot):
                        KKN, KKNk = BUF[d]['KKN']
                        KT, KTk = BUF[d]['KT'][slot]
                        RTb, RTk = BUF[d]['RT'][slot]
                        GLt, GLk = BUF[d]['GLt'][slot]
                        tsl = slice(t * 128, (t + 1) * 128)
                        rev = (d == 1)
                        Z, Zk = Zst[d]
                        plw, plwk = bank()
                        pla, plak = bank()
                        plwv = plw[:, 0:256].rearrange("p (j t) -> p j t", j=2)
                        plav = pla[:, 0:256].rearrange("p (j t) -> p j t", j=2)
                        wb_ = 32 * d
                        for j in range(2):
                            mm(plwv[:, j, :], lw2b[wb_:wb_ + 16, d, j * 128:(j + 1) * 128], LB[wb_:wb_ + 16, tsl], ['rwlw2', 'rwLB'], [plwk])
                        for j in range(2):
                            mm(plav[:, j, :], lw2b[64:96, d, j * 128:(j + 1) * 128], LB[64:96, tsl], ['rwlw2', 'rwLB'], [plak])
                        for j in range(2):
                            act(LW[:, j, :], plwv[:, j, :], AF.Sigmoid, [plwk, 'pvt'], [LWk], bias=pv('w0', d * 2 + j))
                            act(SA[:, j, :], plav[:, j, :], AF.Sigmoid, [plak, 'pvt'], [SAk], bias=pv('a0', d * 2 + j))
                        ts('dve', LW[:], LW[:], -0.6065306597126334, None, ALU.mult, None, [LWk], [LWk])
                        yield
                        lwf = LW[:].rearrange("p a b -> p (a b)")
                        lgf = LGm[:].rearrange("p a b -> p (a b)")
                        if not rev:
                            S.op('dve', lambda e: e.tensor_tensor_scan(out=lgf, data0=R64[:], data1=lwf, initial=0.0, op0=ALU.mult, op1=ALU.add),
                                 reads=[LWk, R64k], writes=[LGk])
                        else:
                            S.op('dve', lambda e: e.tensor_tensor_scan(out=lgf[:, ::-1], data0=R64[:], data1=lwf[:, ::-1], initial=0.0,
                                                                       op0=ALU.mult, op1=ALU.add), reads=[LWk, R64k], writes=[LGk])
                        act(EG[:], LGm[:], AF.Exp, [LGk], [EGk])
                        yield
                        act(ENG[:], LGm[:], AF.Exp, [LGk], [ENGk], scale=-1.0)
                        yield
                        tt('pool', TA[:], LGm[:], LW[:], ALU.subtract, [LGk, LWk], [TAk])
                        yield
                        act(EGM[:], TA[:], AF.Exp, [TAk], [EGMk])
                        yield
                        gsrc = EG[:, :, 63::64] if not rev else EG[:, :, 0::64]
                        cp('pool', GLt[:], gsrc, [EGk], [GLk])
                        yield
                        tt('dve', TA[:], KB[:, :, tsl], bc3(pv('kk')), ALU.mult, ['rwKB', 'pvt', TAk], [TAk])
                        yield
                        act(SQ[:], TA[:], AF.Square, [TAk], [SQk])
                        yield
                        pss_, pssk = bank()
                        pssv = pss_[:, 0:256].rearrange("p (a b) -> p a b", a=2)
                        for j in range(2):
                            mm(pssv[:, j, :], bonesb, SQ[:, j, :], ['cstb', SQk], [pssk])
                        act(TB_[:], pssv, AF.Sqrt, [pssk], [TBk])
                        yield
                        ts('dve', TB_[:], TB_[:], 1e-12, None, ALU.max, None, [TBk], [TBk])
                        yield
                        S.op('dve', lambda e: e.reciprocal(out=TB_[:], in_=TB_[:]), reads=[TBk], writes=[TBk])
                        tt('dve', KKN[:], TA[:], TB_[:], ALU.mult, [TAk, TBk], [KKNk])
                        yield
                        tt('pool', KT[:, 0], KKN[:], EGM[:], ALU.mult, [KKNk, EGMk], [KTk])
                        yield
                        tt('dve', TA[:], SA[:], ENG[:], ALU.mult, [SAk, ENGk, TAk], [TAk])
                        yield
                        tt('pool', KT[:, 1], KKN[:], TA[:], ALU.mult, [KKNk, TAk], [KTk])
                        yield
                        tt('dve', U0[:], SA[:], bc3(pv('ka')), ALU.mult, [SAk, 'pvt'], [U0k])
                        yield
                        tt('dve', U0[:], U0[:], bc3(oka[:]), ALU.add, [U0k, 'rwoka'], [U0k])
                        yield
                        tt('pool', TB_[:], U0[:], ENG[:], ALU.mult, [U0k, ENGk, TBk], [TBk])
                        yield
                        tt('pool', KT[:, 2], KB[:, :, tsl], TB_[:], ALU.mult, ['rwKB', TBk], [KTk])
                        yield
                        tt('dve', RTb[:], RB[:, :, tsl], EG[:], ALU.mult, ['rwRB', EGk], [RTk])
                        yield
                        tt('dve', U0[:], U0[:], KB[:, :, tsl], ALU.mult, [U0k, 'rwKB'], [U0k])
                        yield
                        tt('dve', U0[:], U0[:], bc3(pv('rk')), ALU.mult, [U0k, 'pvt'], [U0k])
                        yield
                        tt('pool', RKD[:], U0[:], RB[:, :, tsl], ALU.mult, [U0k, 'rwRB'], [RKDk])
                        yield
                        pbn, pbnk = bank()
                        pbnv = pbn[:, 0:256].rearrange("p (a b) -> p a b", a=2)
                        for j in range(2):
                            mm(pbnv[:, j, :], bonesb, RKD[:, j, :], ['cstb', RKDk], [pbnk])
                        if t not in seen_b:
                            seen_b.add(t)
                            tt('dve', Y[:, 3, :, tsl], pbnv, VB[:, :, tsl], ALU.mult, [pbnk, 'rwVB'], ['Y3'])
                        else:
                            tt('dve', TA[:], pbnv, VB[:, :, tsl], ALU.mult, [pbnk, 'rwVB', TAk], [TAk])
                            tt('pool', Y[:, 3, :, tsl], Y[:, 3, :, tsl], TA[:], ALU.add, ['Y3', TAk], ['Y3'])

                    def prep_pair(step):
                        for d_ in range(2):
                            yield from prep(d_, orders[d_][step], step % 2)

                    def unit(d, t, slot):
                        KT, KTk = BUF[d]['KT'][slot]
                        RTb, RTk = BUF[d]['RT'][slot]
                        GLt, GLk = BUF[d]['GLt'][slot]
                        tsl = slice(t * 128, (t + 1) * 128)
                        rev = (d == 1)
                        subs = [stream(d, j, t, rev, tsl, KT, KTk, RTb, RTk, GLt, GLk) for j in range(2)]
                        while subs:
                            for g in list(subs):
                                try:
                                    next(g)
                                except StopIteration:
                                    subs.remove(g)
                                yield

                    def stream(d, j, t, rev, tsl, KT, KTk, RTb, RTk, GLt, GLk):
                        SB = BUF[d][j]
                        TM, TMk = SB['TM']
                        A1T, A1k = SB['A1T']
                        A2T, A2k = SB['A2T']
                        A3T, A3k = SB['A3T']
                        A4T, A4k = SB['A4T']
                        ALT, ALk = SB['ALT']
                        Tm, Tmk = SB['Tm']
                        TTm, TTk = SB['TTm']
                        Xb, Xbk = SB['Xb']
                        RHS, RHSk = SB['RHS']
                        BYb, BYk = SB['BYb']
                        NY, NYk = SB['NY']
                        RH, RHk = SB['RH']
                        GTb, GTk = SB['GTb']
                        ZLG, ZLGk = SB['ZLG']
                        Z0b, Z0k = SB['Z0b']
                        Z, _zk = Zst[d]
                        Zk = 'rw_Zs_%d_%d' % (d, j)
                        ptb, ptbk = bank()
                        ptv = ptb[:].bitcast(BF16).rearrange("p (a b) -> p a b", a=8)
                        for x in range(3):
                            tr(ptv[:, x, :], KT[:, x, j, :], identb, [KTk, 'cstb'], [ptbk])
                        cp('act', TM[:], ptv[:, 0:3, :], [ptbk], [TMk])
                        yield

                        def amat(dst, dstk, li, ri_src, ri_k, mslot):
                            pas = []
                            for par in range(2):
                                hp = par * 64
                                pa, pak = bank()
                                rhs = (RTb[hp:hp + 64, j, :] if ri_src is None else KT[hp:hp + 64, ri_src, j, :])
                                mm(pa[:, 0:128], KT[hp:hp + 64, li, j, :], rhs, [KTk, ri_k], [pak])
                                pas.append((pa, pak))
                            return pas

                        def aevac(pas, dst, dstk, mslot):
                            for par, (pa, pak) in enumerate(pas):
                                if mslot is None:
                                    cp('act', dst[:, par, :], pa[:, 0:128], [pak], [dstk])
                                else:
                                    tt('dve', dst[:, par, :], pa[:, 0:128], maskb[:, mslot, :], ALU.mult, [pak, 'maskb'], [dstk])
                        for (dst, dstk, li, rs, rk, ms) in ((A1T, A1k, 1, 0, KTk, None), (A2T, A2k, 2, 0, KTk, 2 + d),
                                                            (A3T, A3k, 1, None, RTk, 4 + d), (A4T, A4k, 2, None, RTk, 4 + d)):
                            pas = amat(dst, dstk, li, rs, rk, ms)
                            aevac(pas, dst, dstk, ms)
                            yield
                        idb2 = identb.unsqueeze(1).broadcast_to([128, 2, 128])
                        cp('pool', Tm[:], idb2, ['cstb'], [Tmk])
                        cp('pool', TTm[:], idb2, ['cstb'], [TTk])
                        for lv in range(6):
                            tt('pool', ALT[:], A1T[:], maskb[:, 6 + d * 6 + lv, :].unsqueeze(1).broadcast_to([128, 2, 128]), ALU.mult,
                               [A1k, 'maskb'], [ALk])
                            yield
                            px, pxk = bank()
                            pxv = px[:, 0:256].rearrange("p (h t) -> p h t", h=2)
                            for par in range(2):
                                mm(pxv[:, par, :], ALT[:, par, :], Tm[:, par, :], [ALk, Tmk], [pxk])
                            cp('act', Xb[:], pxv, [pxk], [Xbk])
                            yield
                            py_, pyk = bank()
                            pyv = py_[:].rearrange("p (x h t) -> p x h t", x=2, h=2)
                            for par in range(2):
                                mm(pyv[:, 0, par, :], Xb[:, par, :], TTm[:, par, :], [Xbk, TTk], [pyk])
                            if lv < 5:
                                for par in range(2):
                                    mm(pyv[:, 1, par, :], TTm[:, par, :], Xb[:, par, :], [Xbk, TTk], [pyk])
                            if lv < 5:
                                tt('dve', Tm[:], Tm[:], pyv[:, 1], ALU.subtract, [Tmk, pyk], [Tmk])
                            tt('dve', TTm[:], TTm[:], pyv[:, 0], ALU.subtract, [TTk, pyk], [TTk])
                            yield
                        pw, pwk = bank()
                        pwv = pw[:, 0:128].rearrange("p (h v) -> p h v", h=2)
                        for par in range(2):
                            h = 2 * j + par
                            mm(pwv[:, par, :], A2T[:, par, :], vT[:, t, h * 64:(h + 1) * 64], [A2k, 'rwvT'], [pwk])
                        cp('pool', RHS[:, :, 0:64], TM[:, 0, :].rearrange("p (h k) -> p h k", h=2), [TMk], [RHSk])
                        cp('act', RHS[:, :, 64:128], pwv, [pwk], [RHSk])
                        yield
                        pby, pbyk = bank()
                        pbyv = pby[:, 0:256].rearrange("p (h t) -> p h t", h=2)
                        for par in range(2):
                            mm(pbyv[:, par, :], TTm[:, par, :], RHS[:, par, :], [TTk, RHSk], [pbyk])
                        cp('act', BYb[:], pbyv, [pbyk], [BYk])
                        yield
                        ts('pool', NY[:], BYb[:, :, 64:128], -1.0, 0.0, ALU.mult, ALU.add, [BYk], [NYk])
                        pr_, prk = bank()
                        for par in range(2):
                            hp = par * 64
                            mm(pr_[hp:hp + 64, 0:128], BYb[:, par, 0:64], A3T[:, par, :], [BYk, A3k], [prk])
                        tt('dve', RH[:], RTb[:, j, :], pr_[:, 0:128], ALU.subtract, [RTk, prk], [RHk])
                        yield
                        for c in range(2):
                            cs = slice(c * 64, (c + 1) * 64)
                            pg_, pgk = bank()
                            pgv = pg_[:, 0:128].rearrange("p (x v) -> p x v", x=2)
                            for par in range(2):
                                hp = par * 64
                                h = 2 * j + par
                                hc = slice(h * 64, (h + 1) * 64)
                                pc = slice(par * 64, (par + 1) * 64)
                                mm(pgv[hp:hp + 64, 0, :], BYb[cs, par, 0:64], TM[cs, 1, pc], [BYk, TMk], [pgk])
                                mm(pgv[hp:hp + 64, 1, :], TM[cs, 2, pc], vT[cs, t, hc], [TMk, 'rwvT'], [pgk], start=True, stop=False)
                                mm(pgv[hp:hp + 64, 1, :], TM[cs, 1, pc], NY[cs, par, :], [TMk, NYk], [pgk], start=False, stop=True)
                            for par in range(2):
                                hp = par * 64
                                tt('dve', GTb[hp:hp + 64, c, hp:hp + 64], cstf[hp:hp + 64, 4, 0:64], pgv[hp:hp + 64, 0, :], ALU.subtract,
                                   ['cstf', pgk], [GTk])
                            ts('dve', ZLG[:, c, :], pgv[:, 1, :], GLt[:, j, c:c + 1], None, ALU.mult, None, [pgk, GLk], [ZLGk])
                            yield
                        for c in ((0, 1) if not rev else (1, 0)):
                            cp('act', Z0b[:, c, :], Z[:, j, :], [Zk], [Z0k])
                            yield
                            pn, pnk = bank()
                            mm(pn[:, 0:64], GTb[:, c, :], Z0b[:, c, :], [GTk, Z0k], [pnk])
                            stt(Z[:, j, :], pn[:, 0:64], GLt[:, j, c:c + 1], ZLG[:, c, :], ALU.mult, ALU.add, [pnk, GLk, ZLGk, Zk], [Zk])
                            yield
                        for par in range(2):
                            hp = par * 64
                            h = 2 * j + par
                            hc = slice(h * 64, (h + 1) * 64)
                            po_, pok = bank()
                            reg = po_[hp:hp + 64, 0:128]
                            mm(reg, vT[:, t, hc], A4T[:, par, :], ['rwvT', A4k], [pok], start=True, stop=False)
                            mm(reg, NY[:, par, :], A3T[:, par, :], [NYk, A3k], [pok], start=False, stop=False)
                            for c in range(2):
                                mm(reg[:, c * 64:(c + 1) * 64], Z0b[hp:hp + 64, c, :], RH[hp:hp + 64, c * 64:(c + 1) * 64],
                                   [Z0k, RHk], [pok], start=False, stop=(c == 1))
                            osl = OS[hp:hp + 64, j, tsl]
                            osk = 'rwOS%d_%d' % (t, j)
                            if (t, j, par) not in seen_o:
                                seen_o.add((t, j, par))
                                cp('dve' if par == 0 else 'act', osl, reg, [pok], [osk])
                            else:
                                tt('dve', osl, osl, reg, ALU.add, [pok, osk], [osk])
                            yield

                    for _ in prep_pair(0):
                        pass
                    for step in range(NTL):
                        gens = [unit(d, orders[d][step], step % 2) for d in range(2)]
                        if step + 1 < NTL:
                            gens.append(prep_pair(step + 1))
                        while gens:
                            for g in list(gens):
                                try:
                                    next(g)
                                except StopIteration:
                                    gens.remove(g)
                    S.barrier()
                if stop is not None and stop.startswith('rw_'):
                    return
                with contextlib.ExitStack() as st2:
                    ob = [sb(st2, "rwob%d" % i, [128, 2, 128], BF16) for i in range(2)]
                    cen = [sb(st2, "rwcen%d" % i, [128, 2, 128], F32) for i in range(2)]
                    rs = [sb(st2, "rwrs%d" % i, [128, 2, 128], F32) for i in range(2)]
                    pm_ = [ps(st2, "rwpm%d" % i, [128, 512], F32) for i in range(2)]
                    pv_ = [ps(st2, "rwpv%d" % i, [128, 512], F32) for i in range(2)]
                    for t in range(NTL):
                        i2 = t % 2
                        tsl = slice(t * 128, (t + 1) * 128)
                        osk = 'rwOS%d_0' % t
                        osk1 = 'rwOS%d_1' % t
                        cp('act', ob[i2][:], OS[:, :, tsl], [osk, osk1], ['rwob%d' % i2])
                        pmv = pm_[i2][:, 0:256].rearrange("p (a b) -> p a b", a=2)
                        for j in range(2):
                            mm(pmv[:, j, :], bonesb, ob[i2][:, j, :], ['cstb', 'rwob%d' % i2], ['rwpm%d' % i2])
                        stt(cen[i2][:], pmv, -1.0 / 64, OS[:, :, tsl], ALU.mult, ALU.add, ['rwpm%d' % i2, osk, osk1], ['rwcen%d' % i2])
                        act(ob[i2][:], cen[i2][:], AF.Square, ['rwcen%d' % i2], ['rwob%d' % i2])
                        pvv = pv_[i2][:, 0:256].rearrange("p (a b) -> p a b", a=2)
                        for j in range(2):
                            mm(pvv[:, j, :], bonesb, ob[i2][:, j, :], ['cstb', 'rwob%d' % i2], ['rwpv%d' % i2])
                        act(rs[i2][:], pvv, AF.Sqrt, ['rwpv%d' % i2], ['rwrs%d' % i2], bias=RW_GN_EPS, scale=1.0 / 64)
                        S.op('dve', lambda e: e.reciprocal(out=rs[i2][:], in_=rs[i2][:]), reads=['rwrs%d' % i2], writes=['rwrs%d' % i2])
                        tt('dve', cen[i2][:], cen[i2][:], rs[i2][:], ALU.mult, ['rwcen%d' % i2, 'rwrs%d' % i2], ['rwcen%d' % i2])
                        tt('pool', cen[i2][:], cen[i2][:], bc3(pv('gnw')), ALU.mult, ['rwcen%d' % i2, 'pvt'], ['rwcen%d' % i2])
                        tt('pool', cen[i2][:], cen[i2][:], bc3(pv('gnb')), ALU.add, ['rwcen%d' % i2, 'pvt'], ['rwcen%d' % i2])
                        tt('dve', cen[i2][:], cen[i2][:], Y[:, 3, :, tsl], ALU.add, ['rwcen%d' % i2, 'Y3'], ['rwcen%d' % i2])
                        if ('yd%d' % l) in debug:
                            cp('act', OS[:, :, tsl], cen[i2][:], ['rwcen%d' % i2], [osk, osk1])
                        tt('dve', Y[:, 3, :, tsl], cen[i2][:], zs[:, :, tsl], ALU.mult, ['rwcen%d' % i2, 'rwzs'], ['Y3'])
                    if ('yd%d' % l) in debug:
                        dbg_dump('yd%d' % l, OS[:], [128, 2, NT], ['rwOS%d_%d' % (t, j_) for t in range(NTL) for j_ in range(2)])
                    S.barrier()
                S.barrier()
        PHASES['rw'] = phase_rw
        def phase_merge(l, h_src, last):
            h_dst = out_d if last else h1_d
            with contextlib.ExitStack() as st:
                MG = sb(st, "mgMG", [128, 8, NT], BF16)
                wbr = sb(st, "mgwbr", [128, 4, 2, DM], BF16)
                S.dma('pool', wbr[:], dr['wbr'][l], writes=['mgwbr'])
                with contextlib.ExitStack() as st2:
                    wg = [sb(st2, "mgwg%d" % i, [128, 8, 4, 128], BF16) for i in range(2)]
                    sg = [sb(st2, "mgsg%d" % i, [128, 512], BF16) for i in range(3)]
                    ac = [sb(st2, "mgac%d" % i, [128, 512], F32) for i in range(2)]
                    tm = [sb(st2, "mgtm%d" % i, [128, 512], F32) for i in range(2)]
                    pgl = [ps(st2, "mgpg%d" % i, [128, 512], F32) for i in range(3)]
                    pbr = [ps(st2, "mgpb%d" % i, [128, 512], F32) for i in range(3)]
                    cg = 0
                    ca = 0
                    def load_wg(dt_):
                        for k in range(4):
                            if (l, k * 8 + dt_) in pre_sg:
                                continue
                            c0 = 3968 + k * 1024 + dt_ * 128
                            S.dma('pool', wg[dt_ % 2][:, :, k, :], dr['w_in'][l][:, :, c0:c0 + 128], writes=['mgwg%d' % (dt_ % 2)])
                    load_wg(0)
                    for dt_ in range(8):
                        w_, wk_ = wg[dt_ % 2], 'mgwg%d' % (dt_ % 2)
                        if dt_ + 1 < 8:
                            load_wg(dt_ + 1)
                        for (n0, nn) in BLOCKS:
                            if last and n0 < 256:
                                continue
                            a_, ak_ = ac[ca % 2], 'mgac%d' % (ca % 2)
                            t_, tk_ = tm[ca % 2], 'mgtm%d' % (ca % 2)
                            ca += 1
                            for k in range(4):
                                pg_, pgk_ = pgl[cg % 3], 'mgpg%d' % (cg % 3)
                                pb_, pbk_ = pbr[cg % 3], 'mgpb%d' % (cg % 3)
                                s_, sk_ = sg[cg % 3], 'mgsg%d' % (cg % 3)
                                cg += 1
                                if (l, k * 8 + dt_) in pre_sg:
                                    S.dma('sp' if cg % 2 == 0 else 'act', s_[:, 0:nn], sgd[k * 8 + dt_][:, n0:n0 + nn], reads=['sgd'], writes=[sk_])
                                else:
                                    for jj in range(8):
                                        mm(pg_[:, 0:nn], w_[:, jj, k, :], uT[:, jj, n0:n0 + nn], [wk_] + uTk[n0 // 128:(n0 + nn) // 128], [pgk_],
                                           start=(jj == 0), stop=(jj == 7))
                                    act(s_[:, 0:nn], pg_[:, 0:nn], AF.Sigmoid, [pgk_, 'pvt'], [sk_], bias=pv('bin', 31 + k * 8 + dt_))
                                for jc in range(2):
                                    mm(pb_[:, 0:nn], wbr[:, k, jc, dt_ * 128:(dt_ + 1) * 128], Y[:, k, jc, n0:n0 + nn], ['mgwbr', 'Y%d' % k], [pbk_],
                                       start=(jc == 0), stop=(jc == 1))
                                if k == 0:
                                    tt('dve', a_[:, 0:nn], pb_[:, 0:nn], s_[:, 0:nn], ALU.mult, [pbk_, sk_], [ak_])
                                else:
                                    tt('dve', t_[:, 0:nn], pb_[:, 0:nn], s_[:, 0:nn], ALU.mult, [pbk_, sk_], [tk_])
                                    if k < 3:
                                        tt('pool', a_[:, 0:nn], a_[:, 0:nn], t_[:, 0:nn], ALU.add, [ak_, tk_], [ak_])
                                    else:
                                        tt('pool', MG[:, dt_, n0:n0 + nn], a_[:, 0:nn], t_[:, 0:nn], ALU.add, [ak_, tk_], ['mgMG%d' % (n0 // 512 if n0 else 9)])
                    S.barrier()
                if ('merged%d' % l) in debug:
                    with contextlib.ExitStack() as st2:
                        mf = sb(st2, "mgf", [128, 8, NT], F32)
                        cp('dve', mf[:], MG[:], ['mgMG%d' % i for i in (9, 0, 1, 2, 3)], ['mgf'])
                        dbg_dump('merged%d' % l, mf[:], [128, 8, NT], ['mgf'])
                        S.barrier()
                with contextlib.ExitStack() as st2:
                    wo = sb(st2, "mgwo", [128, 8, DM], BF16)
                    S.dma('pool', wo[:], dr['wout'][l], writes=['mgwo'])
                    rows = sb(st2, "mgrows", [128, 3, DM], F32)
                    S.dma('sp', rows[:], dr['rows'][l][:, 0:3072].rearrange("p (a b) -> p a b", a=3), writes=['mgrows'])
                    hin_ = [sb(st2, "mghin%d" % i, [128, DM], F32) for i in range(2)]
                    ot = [sb(st2, "mgot%d" % i, [128, DM], F32) for i in range(2)]
                    stat = [sb(st2, "mgst%d" % i, [128, 16], F32) for i in range(2)]
                    po = [[ps(st2, "mgpo%d_%d" % (i, hh), [128, 512], F32) for hh in range(2)] for i in range(2)]
                    def mgout(it, t):
                        i2 = it % 2
                        ci = 1 if t < 2 else 0
                        tsl = slice(t * 128, (t + 1) * 128)
                        mgk = 'mgMG%d' % (9 if t < 2 else (t - 2) // 4)
                        hk_, ok_, sk_ = 'mghin%d' % i2, 'mgot%d' % i2, 'mgst%d' % i2
                        hi, o_, sti = hin_[i2], ot[i2], stat[i2]
                        S.dma('sp', hi[:], h_src[t * 128:(t + 1) * 128, :], writes=[hk_])
                        for hh in range(2):
                            pk_ = 'mgpo%d_%d' % (i2, hh)
                            for jj in range(8):
                                mm(po[i2][hh][:], MG[:, jj, tsl], wo[:, jj, hh * 512:(hh + 1) * 512], [mgk, 'mgwo'], [pk_], start=(jj == 0), stop=(jj == 7))
                        yield
                        for hh in range(2):
                            pk_ = 'mgpo%d_%d' % (i2, hh)
                            tt('dve', o_[:, hh * 512:(hh + 1) * 512], po[i2][hh][:], rows[:, 0, hh * 512:(hh + 1) * 512], ALU.add, [pk_, 'mgrows'], [ok_])
                        yield
                        tt('dve', o_[:], o_[:], gatebc[:, ci, :], ALU.mult, [ok_, 'gatebc'], [ok_])
                        yield
                        stt(o_[:], hi[:], ALPHA, o_[:], ALU.mult, ALU.add, [hk_, ok_], [ok_])
                        yield
                        S.op('dve', lambda e: e.bn_stats(out=sti[:, 0:6], in_=o_[:, 0:512]), reads=[ok_], writes=[sk_])
                        S.op('dve', lambda e: e.bn_stats(out=sti[:, 6:12], in_=o_[:, 512:1024]), reads=[ok_], writes=[sk_])
                        yield
                        S.op('dve', lambda e: e.bn_aggr(out=sti[:, 12:14], in_=sti[:, 0:12]), reads=[sk_], writes=[sk_])
                        yield
                        act(sti[:, 14:15], sti[:, 13:14], AF.Sqrt, [sk_], [sk_], bias=LN_EPS)
                        yield
                        S.op('dve', lambda e: e.reciprocal(out=sti[:, 14:15], in_=sti[:, 14:15]), reads=[sk_], writes=[sk_])
                        yield
                        stt(sti[:, 15:16], sti[:, 12:13], -1.0, sti[:, 14:15], ALU.mult, ALU.mult, [sk_], [sk_])
                        yield
                        act(o_[:], o_[:], AF.Identity, [ok_, sk_], [ok_], bias=sti[:, 15:16], scale=sti[:, 14:15])
                        yield
                        tt('pool', o_[:, 0:512], o_[:, 0:512], rows[:, 1, 0:512], ALU.mult, [ok_, 'mgrows'], [ok_])
                        tt('dve', o_[:, 512:1024], o_[:, 512:1024], rows[:, 1, 512:1024], ALU.mult, [ok_, 'mgrows'], [ok_])
                        yield
                        tt('pool', o_[:, 0:512], o_[:, 0:512], rows[:, 2, 0:512], ALU.add, [ok_, 'mgrows'], [ok_])
                        tt('dve', o_[:, 512:1024], o_[:, 512:1024], rows[:, 2, 512:1024], ALU.add, [ok_, 'mgrows'], [ok_])
                        yield
                        if last:
                            S.dma('sp', out_d[(t - 2) * 128:(t - 1) * 128, :], o_[:], reads=[ok_], writes=['outfinal'])
                        else:
                            S.dma('sp', h1_d[t * 128:(t + 1) * 128, :], o_[:], reads=[ok_], writes=['h1'])

                    tl = [t for t in range(NTL) if not (last and t < 2)]
                    run_pipelined((mgout(i_, t) for i_, t in enumerate(tl)), STG['mgout'])
                    S.barrier()
                S.barrier()
        PHASES['merge'] = phase_merge
        for l in range(nlayers):
            last = (l == nlayers - 1)
            h_src = dr['hin'] if l == 0 else h1_d
            S.dma('sp', pvt[:], dr['pv'][l], writes=['pvt'])
            with contextlib.ExitStack() as st:
                adw = [sb(st, "adw%d" % i, [128, 8, 512], F32) for i in range(2)]
                scb = sb(st, "scb", [128, 2, 8, 128], F32)
                grow = sb(st, "grow", [128, DM], F32)
                pm0 = ps(st, "pm0", [128, 16, 2], F32)
                pg = [ps(st, "pg%d" % i, [128, 512], F32) for i in range(2)]
                for i in range(2):
                    cp('dve', scb[:, i], silc[:, :, i:i + 1].broadcast_to([128, 8, 128]), ['silc'], ['scb'])
                S.dma('sp', grow[:], dr['rows'][l][:, 3072:4096], writes=['grow'])
                for ch in range(6):
                    buf = adw[ch % 2]
                    bk = 'adw%d' % (ch % 2)
                    S.dma('sp' if ch % 2 == 0 else 'act', buf[:], dr['ada_w'][l][:, :, ch * 512:(ch + 1) * 512], writes=[bk])
                    if ch < 4:
                        for mloc in range(4):
                            m = ch * 4 + mloc
                            for j in range(8):
                                mm(pm0[:, m, :], buf[:, j, mloc * 128:(mloc + 1) * 128], silc[:, j, :], [bk, 'silc'],
                                   ['pm0'], start=(j == 0), stop=(j == 7))
                    else:
                        for i in range(2):
                            for j in range(8):
                                mm(pg[i][:], scb[:, i, j, :], buf[:, j, :], [bk, 'scb'], ['pg%d' % i],
                                   start=(j == 0), stop=(j == 7))
                            tt('dve', gatebc[:, i, (ch - 4) * 512:(ch - 3) * 512], pg[i][:],
                               grow[:, (ch - 4) * 512:(ch - 3) * 512], ALU.add, ['pg%d' % i, 'grow'], ['gatebc'])
                tt('dve', modfm[:], pm0[:], pv('adab').unsqueeze(2).broadcast_to([128, 16, 2]), ALU.add,
                   ['pm0', 'pvt'], ['modfm'])
                ts('dve', modfm[:, 8:16, :], modfm[:, 8:16, :], 1.0, None, ALU.add, None, ['modfm'], ['modfm'])
                dbg_dump('modfm%d' % l, modfm[:], [128, 16, 2], ['modfm'])
                dbg_dump('gatebc%d' % l, gatebc[:], [128, 2, DM], ['gatebc'])
                S.barrier()
            with contextlib.ExitStack() as st:
                xin = [sb(st, "xin%d" % i, [128, DM], F32) for i in range(3)]
                xn = [sb(st, "xn%d" % i, [128, DM], BF16) for i in range(2)]
                stat = [sb(st, "stat%d" % i, [128, 16], F32) for i in range(3)]
                ptr = [ps(st, "ptr%d" % i, [128, 8, 128], BF16) for i in range(2)]
                def p1tile(t):
                    xi, xk = xin[t % 3], 'xin%d' % (t % 3)
                    sti, sk = stat[t % 3], 'stat%d' % (t % 3)
                    xo, xok = xn[t % 2], 'xn%d' % (t % 2)
                    pt, ptk = ptr[t % 2], 'ptr%d' % (t % 2)
                    ci = 1 if t < 2 else 0
                    S.dma('sp' if t % 2 == 0 else 'act', xi[:], h_src[t * 128:(t + 1) * 128, :], writes=[xk])
                    yield
                    S.op('dve', lambda e: e.bn_stats(out=sti[:, 0:6], in_=xi[:, 0:512]), reads=[xk], writes=[sk])
                    S.op('dve', lambda e: e.bn_stats(out=sti[:, 6:12], in_=xi[:, 512:1024]), reads=[xk], writes=[sk])
                    yield
                    S.op('dve', lambda e: e.bn_aggr(out=sti[:, 12:14], in_=sti[:, 0:12]), reads=[sk], writes=[sk])
                    yield
                    act(sti[:, 14:15], sti[:, 13:14], AF.Sqrt, [sk], [sk], bias=LN_EPS)
                    yield
                    S.op('dve', lambda e: e.reciprocal(out=sti[:, 14:15], in_=sti[:, 14:15]), reads=[sk], writes=[sk])
                    yield
                    stt(sti[:, 15:16], sti[:, 12:13], -1.0, sti[:, 14:15], ALU.mult, ALU.mult, [sk], [sk])
                    yield
                    act(xo[:], xi[:], AF.Identity, [xk, sk], [xok], bias=sti[:, 15:16], scale=sti[:, 14:15])
                    yield
                    for j in range(8):
                        tr(pt[:, j, :], xo[:, j * 128:(j + 1) * 128], identb, [xok, 'cstb'], [ptk])
                    yield
                    for j in range(8):
                        if j % 2 == 0:
                            act(uT[:, j, t * 128:(t + 1) * 128], pt[:, j, :], AF.Identity, [ptk, 'modfm'], ['uT%d' % t],
                                bias=modfm[:, j, ci:ci + 1], scale=modfm[:, 8 + j, ci:ci + 1])
                        else:
                            ts('dve', uT[:, j, t * 128:(t + 1) * 128], pt[:, j, :], modfm[:, 8 + j, ci:ci + 1],
                               modfm[:, j, ci:ci + 1], ALU.mult, ALU.add, [ptk, 'modfm'], ['uT%d' % t])

                run_pipelined((p1tile(t) for t in range(NTL)), STG['p1'])
                if ('uT%d' % l) in debug:
                    utf = sb(st, "utf", [128, 8, NT], F32)
                    cp('dve', utf[:], uT[:], ['uT%d' % t for t in range(NTL)], ['utf'])
                    dbg_dump('uT%d' % l, utf[:], [128, 8, NT], ['utf'])
                S.barrier()
            uTk = ['uT%d' % t for t in range(NTL)]

            for ph in list(PHASES):
                if ph in phases:
                    PHASES[ph](l, h_src, last)
            if ('h%d' % l) in debug and not last:
                d_ = dbg_out('h%d' % l, [NT, DM])
                S.dma('sp', d_, h1_d, writes=['dbgout_h%d' % l])
                S.barrier()
            if ('Y%d' % l) in debug:
                with contextlib.ExitStack() as st:
                    yf = sb(st, "yf", [128, 4, 2, NT], F32)
                    cp('dve', yf[:], Y[:], ['Y0', 'Y1', 'Y2', 'Y3'], ['yf'])
                    dbg_dump('Y%d' % l, yf[:], [128, 4, 2, NT], ['yf'])
                    S.barrier()

        S.final_wait('sp', ['outfinal'] + ['dbgout_' + n for n in dbg_d])
    if MEMDBG:
        print('SBUF min remaining by prefix:', minrem)
    return nc, dbg_d


def kernel(**inputs):
    inp = {k: np.asarray(v) for k, v in inputs.items()}
    sh = prep_shared(inp)
    nc, _ = build()
    in_maps = []
    for b in range(8):
        m = dict(sh)
        m.update(prep_core(inp, b))
        in_maps.append(m)
    res = run_bass_kernel_spmd(nc, in_maps, core_ids=list(range(8)))
    return np.stack([np.asarray(res.results[b]['out'], dtype=np.float32) for b in range(8)], 0)
```

```python
import contextlib
import numpy as np
import concourse.bass as bass
import concourse.mybir as mybir
from concourse.bass_utils import run_bass_kernel_spmd

F32 = mybir.dt.float32
BF16 = mybir.dt.bfloat16
AF = mybir.ActivationFunctionType
ALU = mybir.AluOpType
AX = mybir.AxisListType

NT = 2304
NTL = 18
DM = 1024
NCOL = 8064
BLOCKS = [(0, 256), (256, 512), (768, 512), (1280, 512), (1792, 512)]
LN_EPS = 1e-5
RMS_EPS = 1e-6
RW_GN_EPS = 64e-5
ALPHA = (2 * 2) ** 0.25
PI = float(np.pi)
MEMDBG = False
GATE_PRE = False
S5_STAGGER = 6
STG = dict(hgchain=2, out=4, rtchain=1, hgproj=4, mgout=6, p1=4, s5=6)
GJ_SPLIT = [[], [], [], []]
GJ_S5 = list(range(32))


class Sched:
    NDMA = 16

    def __init__(self, nc, same_engine_waits=True):
        self.nc = nc
        self.same = same_engine_waits
        self.eng = dict(pe=nc.tensor, act=nc.scalar, dve=nc.vector, pool=nc.gpsimd, sp=nc.sync)
        self.E = {n: dict(cnt=0, known={}) for n in self.eng}
        self.dq = {'sp': ['dsp%d' % i for i in range(8)], 'act': ['dac%d' % i for i in range(4)],
                   'pool': ['dpl%d' % i for i in range(8)]}
        self.dmas = {n: dict(cnt=0) for q in self.dq.values() for n in q}
        self.dma_rr = {'sp': 0, 'act': 0, 'pool': 0}
        self.lastw = {}
        self.readers = {}
        self.sems = None
        self.nins = 0

    def sem_names(self):
        return list(self.E.keys()) + list(self.dmas.keys())

    def _deps(self, reads, writes):
        deps = {}

        def add(w):
            if w is not None:
                deps[w[0]] = max(deps.get(w[0], 0), w[1])
        for k in reads:
            add(self.lastw.get(k))
        for k in writes:
            add(self.lastw.get(k))
            for r in self.readers.get(k, ()):
                add(r)
        return deps

    def _waits(self, en, deps):
        E = self.E[en]
        waits = []
        for d, v in deps.items():
            if d == en and (en == 'pe' or not self.same):
                continue
            if E['known'].get(d, 0) < v:
                waits.append((d, v))
                E['known'][d] = v
        return waits

    def _record(self, ident, reads, writes):
        for k in writes:
            self.lastw[k] = ident
            self.readers[k] = []
        for k in reads:
            self.readers.setdefault(k, []).append(ident)

    def _emit(self, en, waits, fn, inc):
        eng = self.eng[en]
        for d, v in waits:
            eng.wait_ge(self.sems[d], v)
        if fn is not None:
            fn(eng).then_inc(self.sems[inc[0]], inc[1])
            self.nins += 1

    def op(self, en, fn, reads=(), writes=()):
        E = self.E[en]
        waits = self._waits(en, self._deps(reads, writes))
        E['cnt'] += 1
        self._emit(en, waits, fn, (en, 1))
        self._record((en, E['cnt']), reads, writes)

    def dma(self, en, out, in_, reads=(), writes=(), **kw):
        dn = self.dq[en][self.dma_rr[en]]
        self.dma_rr[en] = (self.dma_rr[en] + 1) % len(self.dq[en])
        Dq = self.dmas[dn]
        deps = self._deps(reads, writes)
        if Dq['cnt'] > 0:
            deps[dn] = max(deps.get(dn, 0), Dq['cnt'])
        waits = self._waits(en, deps)
        Dq['cnt'] += 16
        self._emit(en, waits, (lambda e: e.dma_start(out=out, in_=in_, **kw)), (dn, 16))
        self._record((dn, Dq['cnt']), reads, writes)

    def barrier(self):
        cur = {n: self.E[n]['cnt'] for n in self.E}
        cur.update({n: self.dmas[n]['cnt'] for n in self.dmas})
        for en in self.E:
            waits = self._waits(en, {d: v for d, v in cur.items() if v > 0})
            self._emit(en, waits, None, None)

    def final_wait(self, en, keys):
        self._emit(en, self._waits(en, self._deps(keys, ())), None, None)


PV = {}


def _pv_layout():
    off = 0
    for name, n in [('bin', 63), ('s5d', 2), ('glub', 2), ('hglb', 8), ('hgnw', 2), ('rdec', 4), ('mu', 14),
                    ('w0', 4), ('a0', 4), ('kk', 2), ('ka', 2), ('rk', 2), ('gnw', 2), ('gnb', 2), ('adab', 16),
                    ('lamre', 16), ('lamim', 16), ('ldt', 16), ('rdech', 8)]:
        PV[name] = (off, n)
        off += n
    return off


NPV = _pv_layout()


def _colmap():
    cm = list(range(0, 3584))
    lora = [-1] * 128
    for r in range(16):
        lora[r] = 3584 + r
        lora[32 + r] = 3600 + r
        lora[64 + r] = 3616 + r
        lora[80 + r] = 3632 + r
    cm += lora
    cm += list(range(3648, 3904))
    cm += list(range(3904, 8000))
    return np.array(cm)


CMAP = _colmap()


def _fm(v):
    return np.ascontiguousarray(v.reshape(-1, 128).T)


def _masks():
    t = np.arange(128)
    s_, t_ = t[:, None], t[None, :]
    m = []
    b32 = (s_ // 32) == (t_ // 32)
    b64 = (s_ // 64) == (t_ // 64)
    m.append(b32 & (t_ >= s_))
    m.append(b32 & (t_ <= s_))
    m.append(b64 & (t_ > s_))
    m.append(b64 & (t_ < s_))
    m.append(b64 & (t_ >= s_))
    m.append(b64 & (t_ <= s_))
    for d in range(2):
        for lv in range(6):
            sz = 1 << lv
            blk = (s_ // (2 * sz)) == (t_ // (2 * sz))
            hs, ht = (s_ // sz) % 2, (t_ // sz) % 2
            if d == 0:
                m.append(blk & (ht == 1) & (hs == 0))
            else:
                m.append(blk & (ht == 0) & (hs == 1))
    return np.stack([x.astype(np.float32) for x in m], 1)


def _rot_tables():
    n = 16
    freqs = 10000.0 ** (-np.arange(n, dtype=np.float32) / n)
    tt = np.arange(2048)
    rows = (tt // 64).astype(np.float32)
    cols = (tt % 64).astype(np.float32)
    cos = np.zeros((128, 2048), np.float32)
    sins = np.zeros((128, 2048), np.float32)
    pm = np.zeros((128, 128), np.float32)
    for p in range(128):
        i = p % 64
        pos = rows if i < 32 else cols
        ii = i % 32
        ang = pos * freqs[ii % 16]
        cos[p] = np.cos(ang)
        if ii < 16:
            sins[p] = -np.sin(ang)
            partner = p + 16
        else:
            sins[p] = np.sin(ang)
            partner = p - 16
        pm[partner, p] = 1.0
    return cos, sins, pm


def prep_shared(inp):
    sh = {}
    L = 2
    w_in = inp['w_in']
    wn = np.zeros((L, 1024, NCOL), np.float32)
    valid = CMAP >= 0
    wn[:, :, valid] = w_in[:, :, CMAP[valid]]
    sh['w_in'] = np.ascontiguousarray(wn.reshape(L, 8, 128, NCOL).transpose(0, 2, 1, 3))
    bn = np.zeros((L, NCOL), np.float32)
    bn[:, valid] = inp['b_in'][:, CMAP[valid]]
    sh['ada_w'] = np.ascontiguousarray(inp['ada_w'].reshape(L, 8, 128, 3072).transpose(0, 2, 1, 3))
    pv = np.zeros((L, 128, NPV), np.float32)

    def put(l, name, arr):
        o, n = PV[name]
        assert arr.shape == (128, n), (name, arr.shape)
        pv[l, :, o:o + n] = arr
    for l in range(L):
        put(l, 'bin', _fm(bn[l]))
        put(l, 's5d', _fm(inp['s5_d'][l]))
        put(l, 'glub', _fm(inp['s5_glu_b'][l]))
        put(l, 'hglb', np.concatenate([_fm(inp['hg_lb'][ll, d]) for ll in range(2) for d in range(2)], 1))
        put(l, 'hgnw', _fm(inp['hg_norm_w'][l]))
        rd = np.zeros((128, 4), np.float32)
        for d in range(2):
            for j in range(2):
                rd[:64, d * 2 + j] = inp['ret_decay'][l, d, 2 * j]
                rd[64:, d * 2 + j] = inp['ret_decay'][l, d, 2 * j + 1]
        put(l, 'rdec', rd)
        put(l, 'rdech', np.ascontiguousarray(np.broadcast_to(inp['ret_decay'][l].reshape(1, 8), (128, 8))))
        mu = np.zeros((2, 7 * 128), np.float32)
        mu[:, :768] = inp['rw_mu'][l][:, :768]
        lv = CMAP[3584:3712]
        ok = lv >= 0
        mu[:, 768:896][:, ok] = inp['rw_mu'][l][:, lv[ok] - 2816]
        put(l, 'mu', np.concatenate([_fm(mu[0]), _fm(mu[1])], 1))
        put(l, 'w0', np.concatenate([_fm(inp['rw_w0'][l, d]) for d in range(2)], 1))
        put(l, 'a0', np.concatenate([_fm(inp['rw_a0'][l, d]) for d in range(2)], 1))
        for nm, key in [('kk', 'rw_kk'), ('ka', 'rw_ka'), ('rk', 'rw_rk'), ('gnw', 'rw_gn_w'), ('gnb', 'rw_gn_b')]:
            put(l, nm, _fm(inp[key][l]))
        put(l, 'adab', _fm(inp['ada_b'][l][:2048]))
        for nm, key in [('lamre', 's5_lam_re'), ('lamim', 's5_lam_im')]:
            a = inp[key][l].reshape(2, 8, 2, 64)
            put(l, nm, np.ascontiguousarray(a.transpose(2, 3, 0, 1).reshape(128, 16)))
        a = np.broadcast_to(inp['s5_log_dt'][l].reshape(2, 8, 2, 1), (2, 8, 2, 64))
        put(l, 'ldt', np.ascontiguousarray(a.transpose(2, 3, 0, 1).reshape(128, 16)))
    sh['pv'] = pv
    bt = np.zeros((L, 128, 2, 4, 2, 128), np.float32)
    ct = np.zeros((L, 128, 8, 2, 128), np.float32)
    for l in range(L):
        for g in range(16):
            i, g2 = g // 2, g % 2
            for q in range(16):
                c = g * 16 + q
                j, p = c // 128, c % 128
                bt[l, p, j, i % 4, 0, g2 * 64:(g2 + 1) * 64] = inp['s5_b_re'][l, g, :, q]
                bt[l, p, j, i % 4, 1, g2 * 64:(g2 + 1) * 64] = inp['s5_b_im'][l, g, :, q]
            m0 = (i % 4) * 32 + g2 * 16
            ct[l, g2 * 64:(g2 + 1) * 64, i, 0, m0:m0 + 16] = inp['s5_c_re'][l, g].T
            ct[l, g2 * 64:(g2 + 1) * 64, i, 1, m0:m0 + 16] = inp['s5_c_im'][l, g].T
    sh['s5bt'] = bt
    sh['s5ct'] = ct
    sh['gluw'] = np.ascontiguousarray(inp['s5_glu_w'].reshape(L, 2, 128, 256).transpose(0, 2, 1, 3))
    lw2 = np.zeros((L, 128, 2, 256), np.float32)
    for l in range(L):
        lw2[l, 0:16, 0] = inp['rw_w2'][l, 0]
        lw2[l, 32:48, 1] = inp['rw_w2'][l, 1]
        lw2[l, 64:80, 0] = inp['rw_a2'][l, 0]
        lw2[l, 80:96, 1] = inp['rw_a2'][l, 1]
    sh['lw2'] = lw2
    sh['wbr'] = np.ascontiguousarray(inp['w_branch'].reshape(L, 4, 2, 128, 1024).transpose(0, 3, 1, 2, 4))
    sh['wout'] = np.ascontiguousarray(inp['w_out'].reshape(L, 8, 128, 1024).transpose(0, 2, 1, 3))
    rows = np.zeros((L, 128, 4096 + 512), np.float32)
    for l in range(L):
        rows[l, :, 0:1024] = inp['b_out'][l][None]
        rows[l, :, 1024:2048] = inp['ln_w'][l][None]
        rows[l, :, 2048:3072] = inp['ln_b'][l][None]
        rows[l, :, 3072:4096] = inp['ada_b'][l][None, 2048:3072]
        rows[l, :, 4096:4352] = inp['b_in'][l][None, 1280:1536]
        rows[l, :, 4352:4608] = inp['b_in'][l][None, 2304:2560]
    sh['rows'] = rows
    sh['masks'] = _masks()
    cos, sins, pm = _rot_tables()
    sh['rcos'] = cos
    sh['rsin'] = sins
    t = np.arange(128)
    cst = np.zeros((128, 9, 128), np.float32)
    cst[:, 0] = pm
    cst[:, 1] = ((t[:, None] // 64) == (t[None, :] // 64))
    cst[:, 2] = np.maximum(t[None, :] - t[:, None], 0)
    cst[:, 3] = np.maximum(t[:, None] - t[None, :], 0)
    cst[:, 4] = (t[None, :] >= t[:, None])
    cst[:, 5] = (t[None, :] <= t[:, None])
    cst[:, 6, :64] = ((t[:, None] % 64) == np.arange(64)[None, :])
    cst[:, 6, 64:68] = ((t[:, None] // 32) == np.arange(4)[None, :])
    cst[:, 6, 68] = 127 - t
    cst[:, 6, 69] = t
    cst[:, 7] = t[None, :] + 1.0
    cst[:, 8] = 128.0 - t[None, :]
    sh['cst'] = cst
    return sh


def prep_core(inp, b):
    pc = {}
    pc['hin'] = np.ascontiguousarray(np.concatenate([inp['ctx'][b], inp['x'][b]], 0))
    cv = np.stack([inp['c'][b], inp['c_ctx']], -1)
    pc['cvec'] = np.ascontiguousarray(cv.reshape(8, 128, 2).transpose(1, 0, 2))
    return pc


SHAPES = dict(hin=[NT, DM], cvec=[128, 8, 2], w_in=[2, 128, 8, NCOL], ada_w=[2, 128, 8, 3072], pv=[2, 128, NPV],
              s5bt=[2, 128, 2, 4, 2, 128], s5ct=[2, 128, 8, 2, 128], gluw=[2, 128, 2, 256], lw2=[2, 128, 2, 256],
              wbr=[2, 128, 4, 2, 1024], wout=[2, 128, 8, 1024], rows=[2, 128, 4608], masks=[128, 18, 128],
              rcos=[128, 2048], rsin=[128, 2048], cst=[128, 9, 128])


def build(debug=(), nlayers=2, phases=('s5', 'hg', 'ret', 'rw', 'merge'), stop=None):
    nc = bass.Bass("TRN2", target_bir_lowering=False)
    S = Sched(nc)
    dr = {k: nc.dram_tensor(k, list(v), F32, kind="ExternalInput").ap() for k, v in SHAPES.items()}
    out_d = nc.dram_tensor("out", [2048, DM], F32, kind="ExternalOutput").ap()
    h1_d = nc.dram_tensor("h1", [NT, DM], F32, kind="Internal").ap()
    sgd = nc.dram_tensor("sgd", [32, 128, NT], BF16, kind="Internal").ap()
    pre_sg = set()
    dbg_d = {}

    def dbg_out(name, shape):
        dbg_d[name] = nc.dram_tensor("dbg_" + name, list(shape), F32, kind="ExternalOutput").ap()
        return dbg_d[name]

    uid = [0]

    def key(p='k'):
        uid[0] += 1
        return '%s%d' % (p, uid[0])

    with contextlib.ExitStack() as top:
        S.sems = {n: top.enter_context(nc.semaphore(n)) for n in S.sem_names()}

        minrem = {}

        def sb(st, name, shape, dt=F32):
            uid[0] += 1
            t_ = st.enter_context(nc.sbuf_tensor("%s_%d" % (name, uid[0]), list(shape), dt))
            if MEMDBG:
                pre = name[:2]
                minrem[pre] = min(minrem.get(pre, 1 << 30), nc.sbuf_bytes_remaining)
            return t_

        def ps(st, name, shape, dt=F32):
            uid[0] += 1
            return st.enter_context(nc.psum_tensor("%s_%d" % (name, uid[0]), list(shape), dt))

        def mm(out, lhsT, rhs, r, w, start=True, stop=True):
            S.op('pe', lambda e: e.matmul(out, lhsT=lhsT, rhs=rhs, start=start, stop=stop), reads=r, writes=w)

        def tr(out, in_, ident, r, w):
            S.op('pe', lambda e: e.transpose(out, in_, ident), reads=r, writes=w)

        def act(out, in_, func, r, w, bias=0.0, scale=1.0):
            S.op('act', lambda e: e.activation(out=out, in_=in_, func=func, bias=bias, scale=scale), reads=r, writes=w)

        def tt(en, out, in0, in1, op, r, w):
            S.op(en, lambda e: e.tensor_tensor(out=out, in0=in0, in1=in1, op=op), reads=r, writes=w)

        def ts(en, out, in0, s1, s2, op0, op1, r, w):
            if s2 is None:
                S.op(en, lambda e: e.tensor_scalar(out=out, in0=in0, scalar1=s1, scalar2=None, op0=op0), reads=r, writes=w)
            else:
                S.op(en, lambda e: e.tensor_scalar(out=out, in0=in0, scalar1=s1, scalar2=s2, op0=op0, op1=op1),
                     reads=r, writes=w)

        def stt(out, in0, sc, in1, op0, op1, r, w):
            S.op('dve', lambda e: e.scalar_tensor_tensor(out=out, in0=in0, scalar=sc, in1=in1, op0=op0, op1=op1),
                 reads=r, writes=w)

        def cp(en, out, in_, r, w):
            if en == 'act':
                S.op('act', lambda e: e.copy(out=out, in_=in_), reads=r, writes=w)
            else:
                S.op(en, lambda e: e.tensor_copy(out=out, in_=in_), reads=r, writes=w)

        def memset(en, ap, val, w):
            S.op(en, lambda e: e.memset(ap, val), writes=w)

        def run_pipelined(gens, stagger):
            it = iter(gens)
            active, pending, rounds = [], True, 0
            while pending or active:
                if pending and rounds % stagger == 0:
                    try:
                        active.append(next(it))
                    except StopIteration:
                        pending = False
                for g in list(active):
                    try:
                        next(g)
                    except StopIteration:
                        active.remove(g)
                rounds += 1

        def mkbanks(st_, n, prefix):
            bl = [ps(st_, "%s%d" % (prefix, i), [128, 512], F32) for i in range(n)]
            cnt = [0]

            def bank():
                i = cnt[0] % n
                cnt[0] += 1
                return bl[i], '%s%d' % (prefix, i)
            return bank

        def gate_jobs(l, last, st_, bankfn, kds):
            wgt = [sb(st_, "gjw%d" % i, [128, 8, 128], BF16) for i in range(2)]
            sgs = [sb(st_, "gjs%d" % i, [128, 512], BF16) for i in range(2)]
            cnt = [0]

            def job(i, kd):
                k, dt_ = kd // 8, kd % 8
                w_, wk_ = wgt[i % 2], 'gjw%d' % (i % 2)
                c0 = 3968 + k * 1024 + dt_ * 128
                S.dma('pool', w_[:], dr['w_in'][l][:, :, c0:c0 + 128], writes=[wk_])
                yield
                for (n0, nn) in BLOCKS:
                    if last and n0 < 256:
                        continue
                    pg_, pgk_ = bankfn()
                    for jj in range(8):
                        mm(pg_[:, 0:nn], w_[:, jj, :], uT[:, jj, n0:n0 + nn], [wk_] + uTk[n0 // 128:(n0 + nn) // 128], [pgk_],
                           start=(jj == 0), stop=(jj == 7))
                    yield
                    c_ = cnt[0] % 2
                    cnt[0] += 1
                    act(sgs[c_][:, 0:nn], pg_[:, 0:nn], AF.Sigmoid, [pgk_, 'pvt'], ['gjs%d' % c_], bias=pv('bin', 31 + k * 8 + dt_))
                    yield
                    S.dma('sp', sgd[kd][:, n0:n0 + nn], sgs[c_][:, 0:nn], reads=['gjs%d' % c_], writes=['sgd'])
                    yield
                pre_sg.add((l, kd))
            return [job(i, kd) for i, kd in enumerate(kds)]

        def interleave(main, extra, every):
            out, ei = [], 0
            extra = list(extra)
            for i, g in enumerate(main):
                out.append(g)
                if (i + 1) % every == 0 and ei < len(extra):
                    out.append(extra[ei])
                    ei += 1
            out.extend(extra[ei:])
            return out

        def dbg_dump(name, ap, shape, r):
            if name in debug:
                d = dbg_out(name, shape)
                S.dma('sp', d, ap, reads=r, writes=['dbgout_' + name])

        cstb = sb(top, "cstb", [128, 3, 128], BF16)
        cstf = sb(top, "cstf", [128, 7, 128], F32)
        maskb = sb(top, "maskb", [128, 18, 128], BF16)
        silc = sb(top, "silc", [128, 8, 2], F32)
        S.dma('pool', cstb[:, 0:2, :], dr['cst'][:, 0:2, :], writes=['cstb'])
        S.dma('sp', cstf[:], dr['cst'][:, 2:9, :], writes=['cstf'])
        S.dma('pool', maskb[:], dr['masks'], writes=['maskb'])
        S.dma('sp', silc[:], dr['cvec'], writes=['silc'])
        memset('pool', cstb[:, 2, :], 0.0, ['cstb'])
        S.op('pool', lambda e: e.affine_select(out=cstb[:, 2, :], in_=cstb[:, 2, :], pattern=[[-1, 128]],
                                               compare_op=ALU.not_equal, fill=1.0, base=0, channel_multiplier=1),
             reads=['cstb'], writes=['cstb'])
        act(silc[:], silc[:], AF.Silu, ['silc'], ['silc'])
        identb = cstb[:, 2, :]
        bonesb = cstb[:, 1, :]

        uT = sb(top, "uT", [128, 8, NT], BF16)
        Y = sb(top, "Y", [128, 4, 2, NT], BF16)
        pvt = sb(top, "pvt", [128, NPV], F32)
        if debug:
            memset('pool', Y[:], 0.0, ['Y0', 'Y1', 'Y2', 'Y3'])
        modfm = sb(top, "modfm", [128, 16, 2], F32)
        gatebc = sb(top, "gatebc", [128, 2, DM], F32)

        def pv(name, j=None, n=1):
            o, cnt = PV[name]
            if j is None:
                return pvt[:, o:o + cnt]
            return pvt[:, o + j:o + j + n]

        PHASES = {}
        def proj_fm(st, wt, wk, mlist, evac, pp, ppk):
            cnt = 0
            for (n0, nn) in BLOCKS:
                for mi, m in enumerate(mlist):
                    p_, pk_ = pp[cnt % len(pp)], ppk[cnt % len(pp)]
                    cnt += 1
                    for j in range(8):
                        mm(p_[:, 0:nn], wt[:, j, m * 128:(m + 1) * 128], uT[:, j, n0:n0 + nn],
                           ['%s%d' % (wk, m // 2)] + uTk[n0 // 128:(n0 + nn) // 128], [pk_], start=(j == 0), stop=(j == 7))
                    evac(mi, m, n0, nn, p_, pk_)

        def phase_s5(l, h_src, last):
            L = 128
            with contextlib.ExitStack() as st:
                btb = sb(st, "btb", [128, 2, 4, 2, 128], BF16)
                ctb = sb(st, "ctb", [128, 8, 2, 128], BF16)
                glub = sb(st, "glub", [128, 2, 256], BF16)
                S.dma('pool', btb[:], dr['s5bt'][l], writes=['btb'])
                S.dma('pool', ctb[:], dr['s5ct'][l], writes=['ctb'])
                S.dma('pool', glub[:], dr['gluw'][l], writes=['glub'])
                ts('pool', ctb[:, :, 1, :], ctb[:, :, 1, :], -1.0, 0.0, ALU.mult, ALU.add, ['ctb'], ['ctb'])
                ub = sb(st, "s5u", [128, 2, NT], BF16)
                zs = sb(st, "s5z", [128, 2, NT], BF16)
                yacc = sb(st, "yacc", [128, 2, NT], F32)
                PT = sb(st, "s5PT", [128, 16, 2, L], F32)
                QT = sb(st, "s5QT", [128, 16, 2, L], F32)
                sst = sb(st, "s5st", [128, 16, 2], F32)
                ones = sb(st, "s5ones", [128, L], F32)
                memset('pool', yacc[:], 0.0, ['yacc'])
                memset('pool', sst[:], 0.0, ['sst'])
                memset('pool', ones[:], 1.0, ['s5ones'])
                with contextlib.ExitStack() as st2:
                    wsu = sb(st2, "wsu", [128, 8, 512], BF16)
                    for pc_ in range(2):
                        S.dma('pool', wsu[:, :, pc_ * 256:(pc_ + 1) * 256], dr['w_in'][l][:, :, pc_ * 256:(pc_ + 1) * 256], writes=['wsu%d' % pc_])
                    pp = [ps(st2, "s5pp%d" % i, [128, 512], F32) for i in range(2)]

                    def evac(mi, m, n0, nn, p_, pk_):
                        if m < 2:
                            act(ub[:, m, n0:n0 + nn], p_[:, 0:nn], AF.Identity, [pk_, 'pvt'], ['s5u'], bias=pv('bin', m))
                        else:
                            act(zs[:, m - 2, n0:n0 + nn], p_[:, 0:nn], AF.Silu, [pk_, 'pvt'], ['s5z'], bias=pv('bin', m))
                    proj_fm(st2, wsu, 'wsu', [0, 1, 2, 3], evac, pp, ['s5pp0', 's5pp1'])
                    sm = sb(st2, "s5sm", [128, 20, 16], F32)
                    K_ = 's5sm'

                    def Sm(i):
                        return sm[:, i, :]

                    def T2(o, a, b, op):
                        tt('dve', Sm(o), a if not isinstance(a, int) else Sm(a), b if not isinstance(b, int) else Sm(b), op,
                           [K_, 'pvt'], [K_])
                    lamre, lamim = pv('lamre'), pv('lamim')
                    act(Sm(0), pv('ldt'), AF.Exp, ['pvt'], [K_])
                    T2(1, lamre, 0, ALU.mult)
                    act(Sm(2), Sm(1), AF.Exp, [K_], [K_])
                    act(Sm(3), Sm(1), AF.Exp, [K_], [K_], scale=-1.0)
                    T2(4, lamim, 0, ALU.mult)
                    ts('dve', Sm(5), Sm(4), PI / 2, None, ALU.add, None, [K_], [K_])
                    for x in (4, 5):
                        for _ in range(4):
                            ts('dve', Sm(16), Sm(x), PI, 2 * PI, ALU.is_gt, ALU.mult, [K_], [K_])
                            T2(x, x, 16, ALU.subtract)
                    act(Sm(6), Sm(4), AF.Sin, [K_], [K_])
                    act(Sm(7), Sm(5), AF.Sin, [K_], [K_])
                    T2(8, 2, 7, ALU.mult)
                    T2(9, 2, 6, ALU.mult)
                    T2(10, 3, 7, ALU.mult)
                    stt(Sm(11), Sm(3), -1.0, Sm(6), ALU.mult, ALU.mult, [K_], [K_])
                    ts('dve', Sm(12), Sm(8), -1.0, None, ALU.add, None, [K_], [K_])
                    T2(16, lamre, lamre, ALU.mult)
                    T2(17, lamim, lamim, ALU.mult)
                    T2(13, 16, 17, ALU.add)
                    S.op('dve', lambda e: e.reciprocal(out=Sm(13), in_=Sm(13)), reads=[K_], writes=[K_])
                    T2(16, 12, lamre, ALU.mult)
                    T2(17, 9, lamim, ALU.mult)
                    T2(16, 16, 17, ALU.add)
                    T2(14, 16, 13, ALU.mult)
                    T2(16, 9, lamre, ALU.mult)
                    T2(17, 12, lamim, ALU.mult)
                    T2(16, 16, 17, ALU.subtract)
                    T2(15, 16, 13, ALU.mult)
                    tmpa = sb(st2, "s5ta", [128, 16, L], F32)
                    tmpb = sb(st2, "s5tb", [128, 16, L], F32)

                    def cmul_bc(dst_re, dst_im, src_re, src_im, s_re, s_im, m):
                        sr = s_re.unsqueeze(2).broadcast_to([128, 16, m])
                        si = s_im.unsqueeze(2).broadcast_to([128, 16, m])
                        ta, tb = tmpa[:, :, 0:m], tmpb[:, :, 0:m]
                        kk_ = ['s5tab', 's5ta', 's5tb', 's5tc', K_]
                        tt('dve', ta, src_re, sr, ALU.mult, kk_, ['s5ta'])
                        tt('dve', tb, src_im, si, ALU.mult, kk_, ['s5tb'])
                        tt('dve', dst_re, ta, tb, ALU.subtract, kk_, ['s5tab'])
                        tt('dve', ta, src_re, si, ALU.mult, kk_, ['s5ta'])
                        tt('dve', tb, src_im, sr, ALU.mult, kk_, ['s5tb'])
                        tt('dve', dst_im, ta, tb, ALU.add, kk_, ['s5tab'])
                    for (TB, a_re, a_im) in ((PT, 8, 9), (QT, 10, 11)):
                        cp('dve', TB[:, :, 0, 0], Sm(a_re), [K_], ['s5tab'])
                        cp('dve', TB[:, :, 1, 0], Sm(a_im), [K_], ['s5tab'])
                        m = 1
                        while m < L:
                            cmul_bc(TB[:, :, 0, m:2 * m], TB[:, :, 1, m:2 * m], TB[:, :, 0, 0:m], TB[:, :, 1, 0:m],
                                    TB[:, :, 0, m - 1], TB[:, :, 1, m - 1], m)
                            m *= 2
                    tmpc = sb(st2, "s5tc", [128, 16, L], F32)
                    cp('dve', tmpc[:], QT[:, :, 0, :], ['s5tab'], ['s5tc'])
                    cmul_bc(QT[:, :, 0, :], QT[:, :, 1, :], tmpc[:], QT[:, :, 1, :], Sm(14), Sm(15), L)
                    S.barrier()
                with contextlib.ExitStack() as st2:
                    NB = 8
                    xa = [sb(st2, "s5xa%d" % i, [128, 2, L], F32) for i in range(NB)]
                    xb_ = [sb(st2, "s5xb%d" % i, [128, 2, L], F32) for i in range(NB)]
                    cw = [sb(st2, "s5cw%d" % i, [128, 2, L], F32) for i in range(NB)]
                    hb = [sb(st2, "s5hb%d" % i, [128, 2, L], BF16) for i in range(NB)]
                    pbu = [ps(st2, "s5pb%d" % i, [128, 2, 2, L], F32) for i in range(4)]
                    py = [ps(st2, "s5py%d" % i, [128, 512], F32) for i in range(2)]
                    orders = [list(range(NTL)), [1, 0] + list(range(NTL - 1, 1, -1))]
                    def s5group(gi, step, d, j):
                        c = orders[d][step]
                        n0 = c * L
                        rev = (d == 1)
                        U = []
                        for ii in range(4):
                            un = gi * 4 + ii
                            bnk = (un // 2) % 4
                            U.append(dict(ii=ii, i=j * 4 + ii, q=d * 8 + j * 4 + ii, pb=pbu[bnk][:, un % 2], pbk='s5pb%d' % bnk,
                                          A=xa[un % NB], Ak='s5xa%d' % (un % NB), B=xb_[un % NB], Bk='s5xb%d' % (un % NB),
                                          C=cw[un % NB], Ck='s5cw%d' % (un % NB), H=hb[un % NB], Hk='s5hb%d' % (un % NB)))
                        for u in U:
                            for ri in range(2):
                                mm(u['pb'][:, ri, :], btb[:, j, u['ii'], ri, :], ub[:, j, n0:n0 + L], ['btb', 's5u'], [u['pbk']])
                        yield
                        for u in U:
                            src = u['pb'][:, :, ::-1] if rev else u['pb'][:, :, :]
                            tt('dve', u['A'][:], src, QT[:, u['q'], 0:1, :].broadcast_to([128, 2, L]), ALU.mult,
                               [u['pbk'], 's5tab'], [u['Ak']])
                        yield
                        for u in U:
                            src = u['pb'][:, ::-1, ::-1] if rev else u['pb'][:, ::-1, :]
                            tt('dve', u['B'][:], src, QT[:, u['q'], 1:2, :].broadcast_to([128, 2, L]), ALU.mult,
                               [u['pbk'], 's5tab'], [u['Bk']])
                        yield
                        for u in U:
                            tt('dve', u['A'][:, 0, :], u['A'][:, 0, :], u['B'][:, 0, :], ALU.subtract, [u['Ak'], u['Bk']], [u['Ak']])
                        yield
                        for u in U:
                            tt('dve', u['A'][:, 1, :], u['A'][:, 1, :], u['B'][:, 1, :], ALU.add, [u['Ak'], u['Bk']], [u['Ak']])
                        yield
                        for ri in range(2):
                            for u in U:
                                q = u['q']
                                S.op('dve', lambda e, u=u, ri=ri, q=q: e.tensor_tensor_scan(
                                    out=u['C'][:, ri, :], data0=ones[:], data1=u['A'][:, ri, :], initial=sst[:, q, ri:ri + 1],
                                    op0=ALU.mult, op1=ALU.add), reads=[u['Ak'], 's5ones', 'sst%d' % q, 'sst'], writes=[u['Ck']])
                            yield
                        for u in U:
                            tt('pool', u['A'][:], u['C'][:], PT[:, u['q'], 0:1, :].broadcast_to([128, 2, L]), ALU.mult,
                               [u['Ck'], 's5tab', u['Ak']], [u['Ak']])
                        yield
                        for u in U:
                            tt('pool', u['B'][:], u['C'][:, ::-1, :], PT[:, u['q'], 1:2, :].broadcast_to([128, 2, L]), ALU.mult,
                               [u['Ck'], 's5tab', u['Bk']], [u['Bk']])
                        yield
                        for u in U:
                            tt('pool', u['A'][:, 0, :], u['A'][:, 0, :], u['B'][:, 0, :], ALU.subtract, [u['Ak'], u['Bk']], [u['Ak']])
                        yield
                        for u in U:
                            tt('pool', u['A'][:, 1, :], u['A'][:, 1, :], u['B'][:, 1, :], ALU.add, [u['Ak'], u['Bk']], [u['Ak']])
                        yield
                        for u in U:
                            cp('pool', sst[:, u['q'], :], u['A'][:, :, L - 1], [u['Ak']], ['sst%d' % u['q']])
                        yield
                        for u in U:
                            hsrc = u['A'][:, :, ::-1] if rev else u['A'][:]
                            cp('act', u['H'][:], hsrc, [u['Ak']], [u['Hk']])
                        yield
                        pyr = py[gi % 2][:, 0:L]
                        pyk = 's5py%d' % (gi % 2)
                        for k_, u in enumerate(U):
                            for ri in range(2):
                                mm(pyr, ctb[:, u['i'], ri, :], u['H'][:, ri, :], ['ctb', u['Hk']], [pyk],
                                   start=(k_ == 0 and ri == 0), stop=(k_ == 3 and ri == 1))
                        yield
                        yield
                        yield
                        tt('dve', yacc[:, j, n0:n0 + L], yacc[:, j, n0:n0 + L], pyr, ALU.add, [pyk, 'yacc'], ['yacc'])

                    glist = [(step, d, j) for step in range(NTL) for d in range(2) for j in range(2)]
                    gbank = mkbanks(st2, 2, "s5gk") if (GATE_PRE and GJ_S5) else None
                    gj = gate_jobs(l, last, st2, gbank, GJ_S5) if (GATE_PRE and GJ_S5) else []
                    run_pipelined(interleave([s5group(gi, *g) for gi, g in enumerate(glist)], gj, 2), STG['s5'])
                    S.barrier()
                for j in range(2):
                    stt(yacc[:, j, :], ub[:, j, :], pv('s5d', j), yacc[:, j, :], ALU.mult, ALU.add, ['s5u', 'yacc', 'pvt'],
                        ['yacc'])
                dbg_dump('ya%d' % l, yacc[:], [128, 2, NT], ['yacc'])
                with contextlib.ExitStack() as st2:
                    t1 = [sb(st2, "s5g1_%d" % i, [128, 512], F32) for i in range(2)]
                    t2 = [sb(st2, "s5g2_%d" % i, [128, 512], BF16) for i in range(2)]
                    pg = [ps(st2, "s5pg%d" % i, [128, 512], F32) for i in range(2)]
                    cnt = 0
                    for (n0, nn) in BLOCKS:
                        for j in range(2):
                            a, ak = t1[cnt % 2], 's5g1_%d' % (cnt % 2)
                            cnt += 1
                            ysl = yacc[:, j, n0:n0 + nn]
                            act(a[:, 0:nn], ysl, AF.Square, ['yacc'], [ak])
                            ts('dve', a[:, 0:nn], a[:, 0:nn], 0.044715, 1.0, ALU.mult, ALU.add, [ak], [ak])
                            tt('dve', a[:, 0:nn], a[:, 0:nn], ysl, ALU.mult, [ak, 'yacc'], [ak])
                            act(a[:, 0:nn], a[:, 0:nn], AF.Sigmoid, [ak], [ak], scale=1.5957691216057308)
                            tt('dve', ub[:, j, n0:n0 + nn], a[:, 0:nn], ysl, ALU.mult, [ak, 'yacc'], ['s5u'])
                    cnt = 0
                    for (n0, nn) in BLOCKS:
                        for m in range(2):
                            p_, pk_ = pg[cnt % 2], 's5pg%d' % (cnt % 2)
                            b_, bk_ = t2[cnt % 2], 's5g2_%d' % (cnt % 2)
                            cnt += 1
                            for jc in range(2):
                                mm(p_[:, 0:nn], glub[:, jc, m * 128:(m + 1) * 128], ub[:, jc, n0:n0 + nn], ['glub', 's5u'], [pk_],
                                   start=(jc == 0), stop=(jc == 1))
                            act(b_[:, 0:nn], p_[:, 0:nn], AF.Sigmoid, [pk_, 'pvt'], [bk_], bias=pv('glub', m))
                            tt('dve', b_[:, 0:nn], b_[:, 0:nn], ub[:, m, n0:n0 + nn], ALU.mult, [bk_, 's5u'], [bk_])
                            tt('pool', Y[:, 0, m, n0:n0 + nn], b_[:, 0:nn], zs[:, m, n0:n0 + nn], ALU.mult, [bk_, 's5z'], ['Y0'])
                    S.barrier()
                S.barrier()
        PHASES['s5'] = phase_s5
        def phase_hg(l, h_src, last):
            with contextlib.ExitStack() as st:
                QP = [sb(st, "hgQP%d" % d, [128, 2, NT], BF16) for d in range(2)]
                KP = [sb(st, "hgKP%d" % d, [128, 2, NT], BF16) for d in range(2)]
                G = sb(st, "hgG", [128, 2, 72, 2], F32)
                VT = sb(st, "hgVT", [128, NTL, 256], BF16)
                zs = sb(st, "hgzs", [128, 2, NT], BF16)
                lbt = sb(st, "hglbt", [128, 2, 4], F32)
                if l == 0:
                    memset('pool', lbt[:, 0, :], 0.0, ['hglbt'])
                    memset('pool', lbt[:, 1, :], 1.0, ['hglbt'])
                else:
                    o_, _ = PV['hglb']
                    tt('dve', lbt[:, 0, :], pvt[:, o_ + 4:o_ + 8], pvt[:, o_:o_ + 4], ALU.subtract, ['pvt'], ['hglbt'])
                    act(lbt[:, 0, :], lbt[:, 0, :], AF.Sigmoid, ['hglbt'], ['hglbt'])
                    ts('dve', lbt[:, 1, :], lbt[:, 0, :], -1.0, 1.0, ALU.mult, ALU.add, ['hglbt'], ['hglbt'])
                with contextlib.ExitStack() as st2:
                    wh = sb(st2, "hgw", [128, 8, 1280], BF16)
                    for pc_ in (0, 4, 1, 2, 3):
                        S.dma('pool', wh[:, :, pc_ * 256:(pc_ + 1) * 256], dr['w_in'][l][:, :, 512 + pc_ * 256:512 + (pc_ + 1) * 256], writes=['hgw%d' % pc_])
                    brow = sb(st2, "hgbrow", [128, 256], F32)
                    S.dma('sp', brow[:], dr['rows'][l][:, 4096:4352], writes=['hgbrow'])
                    R32 = sb(st2, "hgR32", [128, 512], F32)
                    memset('pool', R32[:], 1.0, ['hgR32'])
                    memset('pool', R32[:, 0:512:32], 0.0, ['hgR32'])
                    QS = [sb(st2, "hgQS%d" % i, [128, 2, 512], BF16) for i in range(2)]
                    T = [[sb(st2, "hgT%d_%d" % (i, k), [128, 512], F32) for k in range(4)] for i in range(2)]
                    pp = [ps(st2, "hgpp%d" % i, [128, 512], F32) for i in range(3)]
                    pt = [ps(st2, "hgpt%d" % i, [128, 512], F32) for i in range(2)]
                    def hgproj(cnt, ic, m, n0, nn):
                        ukeys = uTk[n0 // 128:(n0 + nn) // 128]
                        p_, pk_ = pp[cnt % 3], 'hgpp%d' % (cnt % 3)
                        bi = (n0 // 512) % 2 if n0 else 0
                        for jj in range(8):
                            mm(p_[:, 0:nn], wh[:, jj, m * 128:(m + 1) * 128], uT[:, jj, n0:n0 + nn], ['hgw%d' % (m // 2)] + ukeys, [pk_],
                               start=(jj == 0), stop=(jj == 7))
                        yield
                        bias = pv('bin', 4 + m)
                        if m < 2:
                            act(QS[bi][:, m, 0:nn], p_[:, 0:nn], AF.Silu, [pk_, 'pvt'], ['hgQS%d' % bi], bias=bias)
                            return
                        if m >= 8:
                            act(zs[:, m - 8, n0:n0 + nn], p_[:, 0:nn], AF.Silu, [pk_, 'pvt'], ['hgzs'], bias=bias)
                            return
                        d, j = (m - 2) // 2, (m - 2) % 2
                        Ts = T[ic % 2]
                        Tk = ['hgT%d_%d' % (ic % 2, k) for k in range(4)]
                        t1, t2, t3, t4 = [x[:, 0:nn] for x in Ts]
                        act(t1, p_[:, 0:nn], AF.Sigmoid, [pk_, 'pvt'], [Tk[0]], bias=bias)
                        yield
                        ts('dve', t1, t1, lbt[:, 1, d * 2 + j:d * 2 + j + 1], lbt[:, 0, d * 2 + j:d * 2 + j + 1], ALU.mult, ALU.add,
                           [Tk[0], 'hglbt'], [Tk[0]])
                        yield
                        act(t2, t1, AF.Ln, [Tk[0]], [Tk[1]])
                        yield
                        if d == 0:
                            S.op('dve', lambda e: e.tensor_tensor_scan(out=t3, data0=R32[:, 0:nn], data1=t2, initial=0.0,
                                                                       op0=ALU.mult, op1=ALU.add),
                                 reads=[Tk[1], 'hgR32'], writes=[Tk[2]])
                        else:
                            S.op('dve', lambda e: e.tensor_tensor_scan(out=t3[:, ::-1],
                                                                       data0=R32[:, 0:nn], data1=t2[:, ::-1], initial=0.0,
                                                                       op0=ALU.mult, op1=ALU.add),
                                 reads=[Tk[1], 'hgR32'], writes=[Tk[2]])
                        yield
                        ts('dve', t3, t3, -80.0, None, ALU.max, None, [Tk[2]], [Tk[2]])
                        ts('dve', t1, t1, -1.0, 1.0, ALU.mult, ALU.add, [Tk[0]], [Tk[0]])
                        yield
                        act(t4, t3, AF.Exp, [Tk[2]], [Tk[3]])
                        act(t2, t3, AF.Exp, [Tk[2]], [Tk[1]], scale=-1.0)
                        yield
                        tt('pool', KP[d][:, j, n0:n0 + nn], t1, t2, ALU.mult, [Tk[0], Tk[1]], ['hgKP%d' % d])
                        tt('pool', QP[d][:, j, n0:n0 + nn], QS[bi][:, j, 0:nn], t4, ALU.mult, ['hgQS%d' % bi, Tk[3]], ['hgQP%d' % d])
                        c0 = n0 // 32
                        gsrc = t4[:, 31::32] if d == 0 else t4[:, 0::32]
                        cp('act', G[:, d, c0:c0 + nn // 32, j], gsrc, [Tk[3]], ['hgG'])

                    plist = []
                    cnt = 0
                    ic = 0
                    for (n0, nn) in BLOCKS:
                        for m in (0, 1, 8, 9, 2, 3, 4, 5):
                            plist.append((cnt, ic, m, n0, nn))
                            cnt += 1
                            if 2 <= m < 8:
                                ic += 1
                    run_pipelined((hgproj(*p) for p in plist), STG['hgproj'])
                    for t in range(NTL):
                        p_, pk_ = pt[t % 2], 'hgpt%d' % (t % 2)
                        for jj in range(8):
                            mm(p_[:, 0:256], uT[:, jj, t * 128:(t + 1) * 128], wh[:, jj, 768:1024], ['hgw3', uTk[t]], [pk_],
                               start=(jj == 0), stop=(jj == 7))
                        tt('dve', VT[:, t, :], p_[:, 0:256], brow[:], ALU.add, [pk_, 'hgbrow'], ['hgVT'])
                    S.barrier()
                Sall = [sb(st, "hgSall%d" % d, [128, 2, 72, 64], BF16) for d in range(2)]
                with contextlib.ExitStack() as st2:
                    Sst = [sb(st2, "hgS%d" % d, [128, 2, 64], F32) for d in range(2)]
                    kTm = [sb(st2, "hgkTm%d" % i, [128, 4, 256], BF16) for i in range(3)]
                    Ug = [sb(st2, "hgUg%d" % i, [128, 4, 2, 64], F32) for i in range(3)]
                    ptr = [ps(st2, "hgptr%d" % i, [128, 8, 128], BF16) for i in range(2)]
                    pU = [ps(st2, "hgpU%d" % i, [128, 4, 2, 64], F32) for i in range(3)]
                    orders = [list(range(NTL)), [1, 0] + list(range(NTL - 1, 1, -1))]
                    for d in range(2):
                        memset('pool', Sst[d][:], 0.0, ['hgS%d' % d])
                    def hgchain(it, step, d):
                        t = orders[d][step]
                        pr, prk = ptr[it % 2], 'hgptr%d' % (it % 2)
                        km, kmk = kTm[it % 3], 'hgkTm%d' % (it % 3)
                        pu, puk = pU[it % 3], 'hgpU%d' % (it % 3)
                        ug, ugk = Ug[it % 3], 'hgUg%d' % (it % 3)
                        for j in range(2):
                            tr(pr[:, j, :], KP[d][:, j, t * 128:(t + 1) * 128], identb, ['hgKP%d' % d, 'cstb'], [prk])
                        yield
                        for cc in range(4):
                            prf = pr[:, 0:2, :].rearrange("p a b -> p (a b)")
                            if cc % 2 == 0:
                                ts('dve', km[:, cc, :], prf, cstf[:, 4, 64 + cc:64 + cc + 1], None, ALU.mult, None, [prk, 'cstf'], [kmk])
                            else:
                                act(km[:, cc, :], prf, AF.Identity, [prk, 'cstf'], [kmk], scale=cstf[:, 4, 64 + cc:64 + cc + 1])
                        yield
                        for cc in range(4):
                            for h in range(4):
                                hp = (h % 2) * 64
                                mm(pu[hp:hp + 64, cc, h // 2, :], km[:, cc, h * 64:(h + 1) * 64], VT[:, t, h * 64:(h + 1) * 64],
                                   [kmk, 'hgVT'], [puk])
                        yield
                        tt('dve', ug[:], pu[:], G[:, d, t * 4:(t + 1) * 4, :].unsqueeze(3).broadcast_to([128, 4, 2, 64]), ALU.mult,
                           [puk, 'hgG'], [ugk])
                        yield
                        ccs = range(4) if d == 0 else range(3, -1, -1)
                        for cc in ccs:
                            c = t * 4 + cc
                            cp('act', Sall[d][:, :, c, :], Sst[d][:], ['hgS%d' % d], ['hgSall%d_%d' % (d, t)])
                            for j in range(2):
                                stt(Sst[d][:, j, :], Sst[d][:, j, :], G[:, d, c, j:j + 1], ug[:, cc, j, :], ALU.mult, ALU.add,
                                    ['hgS%d' % d, 'hgG', ugk], ['hgS%d' % d])
                            yield

                    gbank = mkbanks(st2, 3, "hggk") if GJ_SPLIT[0] else None
                    gj = gate_jobs(l, last, st2, gbank, GJ_SPLIT[0]) if (GATE_PRE and GJ_SPLIT[0]) else []
                    run_pipelined(interleave([hgchain(i_, sd[0], sd[1]) for i_, sd in enumerate([(s_, d_) for s_ in range(NTL) for d_ in range(2)])], gj, 3), STG['hgchain'])
                    S.barrier()
                with contextlib.ExitStack() as st2:
                    if ('yb%d' % l) in debug:
                        dbgbuf = sb(st2, "dbgbuf", [128, 2, NT], F32)
                    AT = [[sb(st2, "hgAT%d_%d" % (i, d), [128, 4, 128], BF16) for d in range(2)] for i in range(3)]
                    sq = [sb(st2, "hgsq%d" % i, [128, 2, 128], BF16) for i in range(3)]
                    rr = [sb(st2, "hgrr%d" % i, [128, 2, 128], F32) for i in range(3)]
                    ob = [sb(st2, "hgob%d" % i, [128, 2, 128], F32) for i in range(3)]
                    bank = mkbanks(st2, 8, "hgbk")

                    def hgout(t):
                        i2 = t % 3
                        tsl = slice(t * 128, (t + 1) * 128)
                        pas = {}
                        for d in range(2):
                            for par in range(2):
                                pas[(d, par)] = bank()
                            for h in range(4):
                                hp = (h % 2) * 64
                                pa, pak = pas[(d, h % 2)]
                                pav = pa[:, 0:256].rearrange("p (a b) -> p a b", a=2)
                                mm(pav[:, h // 2, :], KP[d][hp:hp + 64, h // 2, tsl], QP[d][hp:hp + 64, h // 2, tsl],
                                   ['hgKP%d' % d, 'hgQP%d' % d], [pak])
                        yield
                        for d in range(2):
                            for par in range(2):
                                pa, pak = pas[(d, par)]
                                pav = pa[:, 0:256].rearrange("p (a b) -> p a b", a=2)
                                tt('dve', AT[i2][d][:, par::2, :], pav, maskb[:, d, :].unsqueeze(1).broadcast_to([128, 2, 128]), ALU.mult,
                                   [pak, 'maskb'], ['hgAT%d_%d' % (i2, d)])
                        yield
                        pos = [bank() for _ in range(2)]
                        povs = [pos[par][0][:, 0:256].rearrange("p (a b) -> p a b", a=2) for par in range(2)]
                        for h in range(4):
                            hp = (h % 2) * 64
                            pok = pos[h % 2][1]
                            reg = povs[h % 2][hp:hp + 64, h // 2, :]
                            first = True
                            for d in range(2):
                                mm(reg, VT[:, t, h * 64:(h + 1) * 64], AT[i2][d][:, h, :], ['hgVT', 'hgAT%d_%d' % (i2, d)], [pok],
                                   start=first, stop=False)
                                first = False
                                for cc in range(4):
                                    c = t * 4 + cc
                                    mm(reg[:, cc * 32:(cc + 1) * 32], Sall[d][hp:hp + 64, h // 2, c, :],
                                       QP[d][hp:hp + 64, h // 2, t * 128 + cc * 32:t * 128 + (cc + 1) * 32],
                                       ['hgSall%d_%d' % (d, t), 'hgQP%d' % d], [pok], start=False, stop=(d == 1 and cc == 3))
                        yield
                        obk = 'hgob%d' % i2
                        cp('act', ob[i2][0:64], povs[0][0:64], [pos[0][1]], [obk])
                        cp('dve', ob[i2][64:128], povs[1][64:128], [pos[1][1]], [obk])
                        yield
                        pov = ob[i2][:]
                        pok = obk
                        if ('yb%d' % l) in debug:
                            cp('pool', dbgbuf[:, :, tsl], pov, [pok], ['dbgbuf'])
                        act(sq[i2][:], pov, AF.Square, [pok], ['hgsq%d' % i2])
                        yield
                        pss_, psk = bank()
                        psv = pss_[:, 0:256].rearrange("p (a b) -> p a b", a=2)
                        for j in range(2):
                            mm(psv[:, j, :], bonesb, sq[i2][:, j, :], ['cstb', 'hgsq%d' % i2], [psk])
                        yield
                        act(rr[i2][:], psv, AF.Sqrt, [psk], ['hgrr%d' % i2], bias=RMS_EPS, scale=1.0 / 64)
                        yield
                        S.op('dve', lambda e: e.reciprocal(out=rr[i2][:], in_=rr[i2][:]), reads=['hgrr%d' % i2], writes=['hgrr%d' % i2])
                        yield
                        tt('dve', rr[i2][:], pov, rr[i2][:], ALU.mult, [pok, 'hgrr%d' % i2], ['hgrr%d' % i2])
                        yield
                        for j in range(2):
                            stt(Y[:, 1, j, tsl], rr[i2][:, j, :], pv('hgnw', j), zs[:, j, tsl], ALU.mult, ALU.mult,
                                ['hgrr%d' % i2, 'pvt', 'hgzs'], ['Y1'])

                    gj = gate_jobs(l, last, st2, bank, GJ_SPLIT[1]) if (GATE_PRE and GJ_SPLIT[1]) else []
                    run_pipelined(interleave([hgout(t) for t in range(NTL) if not (last and t < 2 and not debug)], gj, 3), STG['out'])
                    if ('yb%d' % l) in debug:
                        dbg_dump('yb%d' % l, dbgbuf[:], [128, 2, NT], ['dbgbuf'])
                    S.barrier()
                S.barrier()
        PHASES['hg'] = phase_hg
        def phase_ret(l, h_src, last):
            with contextlib.ExitStack() as st:
                QR = sb(st, "rtQR", [128, 2, NT], BF16)
                KR = sb(st, "rtKR", [128, 2, NT], BF16)
                VT = sb(st, "rtVT", [128, NTL, 256], BF16)
                zs = sb(st, "rtzs", [128, 2, NT], BF16)
                Sall = [sb(st, "rtSall%d" % d, [128, 2, NTL, 64], BF16) for d in range(2)]
                LG = sb(st, "rtLG", [128, 4], F32)
                GL = sb(st, "rtGL", [128, 4], F32)
                LGH = sb(st, "rtLGH", [128, 8], F32)
                QDEC = sb(st, "rtQDEC", [128, 2, 2, 128], F32)
                KDEC = sb(st, "rtKDEC", [128, 2, 4], F32)
                DS = sb(st, "rtDS", [128, 4, 128], F32)
                tb8 = sb(st, "rtb8", [128, 2], F32)
                K_ = 'rttab'
                act(LG[:], pv('rdec'), AF.Exp, ['pvt'], [K_])
                ts('dve', LG[:], LG[:], -1.0, None, ALU.mult, None, [K_], [K_])
                act(GL[:], LG[:], AF.Exp, [K_], [K_], scale=128.0)
                act(LGH[:], pv('rdech'), AF.Exp, ['pvt'], [K_])
                ts('dve', LGH[:], LGH[:], -1.0, None, ALU.mult, None, [K_], [K_])
                for d in range(2):
                    for j in range(2):
                        act(QDEC[:, d, j, :], cstf[:, 5 + d, :], AF.Exp, ['cstf', K_], [K_], scale=LG[:, d * 2 + j:d * 2 + j + 1])
                    act(KDEC[:, d, :], LGH[:, d * 4:(d + 1) * 4], AF.Exp, ['cstf', K_], [K_], scale=cstf[:, 4, 68 + d:69 + d])
                with contextlib.ExitStack() as st2:
                    ta = sb(st2, "rtta", [128, 128], F32)
                    tb = sb(st2, "rttb", [128, 128], F32)
                    for h in range(4):
                        act(ta[:], cstf[:, 0, :], AF.Exp, ['cstf', K_], ['rtta'], scale=LGH[:, h:h + 1])
                        tt('dve', ta[:], ta[:], cstf[:, 2, :], ALU.mult, ['rtta', 'cstf'], ['rtta'])
                        act(tb[:], cstf[:, 1, :], AF.Exp, ['cstf', K_], ['rttb'], scale=LGH[:, 4 + h:5 + h])
                        tt('dve', tb[:], tb[:], cstf[:, 3, :], ALU.mult, ['rttb', 'cstf'], ['rttb'])
                        tt('dve', DS[:, h, :], ta[:], tb[:], ALU.add, ['rtta', 'rttb'], [K_])
                    ts('dve', tb8[:], pv('bin', 16, 2), 0.125, None, ALU.mult, None, ['pvt'], [K_])
                    S.barrier()
                if stop == 'ret_tab':
                    return
                with contextlib.ExitStack() as st2:
                    wr = sb(st2, "rtw", [128, 8, 1024], BF16)
                    for pc_ in range(4):
                        S.dma('pool', wr[:, :, pc_ * 256:(pc_ + 1) * 256], dr['w_in'][l][:, :, 1792 + pc_ * 256:1792 + (pc_ + 1) * 256], writes=['rtw%d' % pc_])
                    brow = sb(st2, "rtbrow", [128, 256], F32)
                    S.dma('sp', brow[:], dr['rows'][l][:, 4352:4608], writes=['rtbrow'])
                    COS = sb(st2, "rtcos", [128, 2048], F32)
                    SIN = sb(st2, "rtsin", [128, 2048], F32)
                    permf = sb(st2, "rtperm", [128, 128], F32)
                    S.dma('sp', COS[:], dr['rcos'], writes=['rtcos'])
                    S.dma('act', SIN[:], dr['rsin'], writes=['rtsin'])
                    S.dma('sp', permf[:], dr['cst'][:, 0, :], writes=['rtperm'])
                    qf = [sb(st2, "rtqf%d" % i, [128, 512], F32) for i in range(2)]
                    t1 = [sb(st2, "rtt1_%d" % i, [128, 512], F32) for i in range(2)]
                    pp = [ps(st2, "rtpp%d" % i, [128, 512], F32) for i in range(2)]
                    pq = [ps(st2, "rtpq%d" % i, [128, 512], F32) for i in range(2)]
                    pt = [ps(st2, "rtpt%d" % i, [128, 512], F32) for i in range(2)]
                    def rtproj(cnt, rc, m, n0, nn):
                        ukeys = uTk[n0 // 128:(n0 + nn) // 128]
                        p_, pk_ = pp[cnt % 2], 'rtpp%d' % (cnt % 2)
                        for jj in range(8):
                            mm(p_[:, 0:nn], wr[:, jj, m * 128:(m + 1) * 128], uT[:, jj, n0:n0 + nn], ['rtw%d' % (m // 2)] + ukeys, [pk_],
                               start=(jj == 0), stop=(jj == 7))
                        yield
                        if m >= 6:
                            act(zs[:, m - 6, n0:n0 + nn], p_[:, 0:nn], AF.Silu, [pk_, 'pvt'], ['rtzs'], bias=pv('bin', 14 + m))
                            return
                        isk = m >= 2
                        j = m % 2
                        dst = (KR if isk else QR)[:, j, n0:n0 + nn]
                        dk = 'rtKR' if isk else 'rtQR'
                        if n0 < 256:
                            if isk:
                                act(dst, p_[:, 0:nn], AF.Identity, [pk_, K_], [dk], bias=tb8[:, j:j + 1], scale=0.125)
                            else:
                                act(dst, p_[:, 0:nn], AF.Identity, [pk_, 'pvt'], [dk], bias=pv('bin', 14 + m))
                            return
                        q_, qk_ = qf[rc % 2], 'rtqf%d' % (rc % 2)
                        a_, ak_ = t1[rc % 2], 'rtt1_%d' % (rc % 2)
                        r_, rk_ = pq[rc % 2], 'rtpq%d' % (rc % 2)
                        if isk:
                            act(q_[:, 0:nn], p_[:, 0:nn], AF.Identity, [pk_, K_], [qk_], bias=tb8[:, j:j + 1], scale=0.125)
                        else:
                            act(q_[:, 0:nn], p_[:, 0:nn], AF.Identity, [pk_, 'pvt'], [qk_], bias=pv('bin', 14 + m))
                        yield
                        mm(r_[:, 0:nn], permf[:], q_[:, 0:nn], ['rtperm', qk_], [rk_])
                        yield
                        tsl = slice(n0 - 256, n0 - 256 + nn)
                        tt('dve', a_[:, 0:nn], r_[:, 0:nn], SIN[:, tsl], ALU.mult, [rk_, 'rtsin'], [ak_])
                        tt('pool', q_[:, 0:nn], q_[:, 0:nn], COS[:, tsl], ALU.mult, [qk_, 'rtcos'], [qk_])
                        yield
                        tt('dve', dst, a_[:, 0:nn], q_[:, 0:nn], ALU.add, [ak_, qk_], [dk])

                    plist = []
                    cnt = 0
                    rc = 0
                    for (n0, nn) in BLOCKS:
                        for m in (0, 1, 2, 3, 6, 7):
                            plist.append((cnt, rc, m, n0, nn))
                            cnt += 1
                            if m < 6 and n0 >= 256:
                                rc += 1
                    run_pipelined((rtproj(*p) for p in plist), 2)
                    for t in range(NTL):
                        p_, pk_ = pt[t % 2], 'rtpt%d' % (t % 2)
                        for jj in range(8):
                            mm(p_[:, 0:256], uT[:, jj, t * 128:(t + 1) * 128], wr[:, jj, 512:768], ['rtw2', uTk[t]], [pk_],
                               start=(jj == 0), stop=(jj == 7))
                        tt('dve', VT[:, t, :], p_[:, 0:256], brow[:], ALU.add, [pk_, 'rtbrow'], ['rtVT'])
                    S.barrier()
                if stop == 'ret_proj':
                    return
                with contextlib.ExitStack() as st2:
                    Sst = [sb(st2, "rtS%d" % d, [128, 2, 64], F32) for d in range(2)]
                    kT = [sb(st2, "rtkT%d" % i, [128, 256], BF16) for i in range(3)]
                    ptr = [ps(st2, "rtptr%d" % i, [128, 8, 128], BF16) for i in range(2)]
                    pU = [ps(st2, "rtpU%d" % i, [128, 512], F32) for i in range(3)]
                    orders = [list(range(NTL)), [1, 0] + list(range(NTL - 1, 1, -1))]
                    for d in range(2):
                        memset('pool', Sst[d][:], 0.0, ['rtS%d' % d])
                    def rtchain(it, step, d):
                        t = orders[d][step]
                        pr, prk = ptr[it % 2], 'rtptr%d' % (it % 2)
                        kt, ktk = kT[it % 3], 'rtkT%d' % (it % 3)
                        pu, puk = pU[it % 3], 'rtpU%d' % (it % 3)
                        puv = pu[:, 0:128].rearrange("p (a b) -> p a b", a=2)
                        for j in range(2):
                            tr(pr[:, j, :], KR[:, j, t * 128:(t + 1) * 128], identb, ['rtKR', 'cstb'], [prk])
                        yield
                        tt('dve', kt[:].rearrange("p (h k) -> p h k", h=4), pr[:, 0:2, :].rearrange("p a (b k) -> p (a b) k", b=2),
                           KDEC[:, d, :].unsqueeze(2).broadcast_to([128, 4, 64]), ALU.mult, [prk, K_], [ktk])
                        yield
                        for h in range(4):
                            hp = (h % 2) * 64
                            mm(puv[hp:hp + 64, h // 2, :], kt[:, h * 64:(h + 1) * 64], VT[:, t, h * 64:(h + 1) * 64], [ktk, 'rtVT'], [puk])
                        yield
                        cp('act', Sall[d][:, :, t, :], Sst[d][:], ['rtS%d' % d], ['rtSall%d_%d' % (d, t)])
                        for j in range(2):
                            stt(Sst[d][:, j, :], Sst[d][:, j, :], GL[:, d * 2 + j:d * 2 + j + 1], puv[:, j, :], ALU.mult, ALU.add,
                                ['rtS%d' % d, K_, puk], ['rtS%d' % d])

                    gbank = mkbanks(st2, 3, "rtgk") if GJ_SPLIT[2] else None
                    gj = gate_jobs(l, last, st2, gbank, GJ_SPLIT[2]) if (GATE_PRE and GJ_SPLIT[2]) else []
                    run_pipelined(interleave([rtchain(i_, sd[0], sd[1]) for i_, sd in enumerate([(s_, d_) for s_ in range(NTL) for d_ in range(2)])], gj, 4), STG['rtchain'])
                    S.barrier()
                if stop == 'ret_chain':
                    return
                with contextlib.ExitStack() as st2:
                    if ('yc%d' % l) in debug:
                        dbgbuf = sb(st2, "dbgbuf", [128, 2, NT], F32)
                    AT = [sb(st2, "rtAT%d" % i, [128, 4, 128], BF16) for i in range(3)]
                    qd = [[sb(st2, "rtqd%d_%d" % (i, d), [128, 2, 128], BF16) for d in range(2)] for i in range(3)]
                    sq = [sb(st2, "rtsq%d" % i, [128, 2, 128], BF16) for i in range(3)]
                    rr = [sb(st2, "rtrr%d" % i, [128, 2, 128], F32) for i in range(3)]
                    ob = [sb(st2, "rtob%d" % i, [128, 2, 128], F32) for i in range(3)]
                    bank = mkbanks(st2, 8, "rtbk")

                    def rtout(t):
                        i2 = t % 3
                        tsl = slice(t * 128, (t + 1) * 128)
                        pas = [bank() for _ in range(2)]
                        for h in range(4):
                            hp = (h % 2) * 64
                            pav = pas[h % 2][0][:, 0:256].rearrange("p (a b) -> p a b", a=2)
                            mm(pav[:, h // 2, :], KR[hp:hp + 64, h // 2, tsl], QR[hp:hp + 64, h // 2, tsl], ['rtKR', 'rtQR'], [pas[h % 2][1]])
                        for d in range(2):
                            tt('pool', qd[i2][d][:], QR[:, :, tsl], QDEC[:, d, :, :], ALU.mult, ['rtQR', K_], ['rtqd%d_%d' % (i2, d)])
                        yield
                        for par in range(2):
                            pav = pas[par][0][:, 0:256].rearrange("p (a b) -> p a b", a=2)
                            tt('dve', AT[i2][:, par::2, :], pav, DS[:, par::2, :], ALU.mult, [pas[par][1], K_], ['rtAT%d' % i2])
                        yield
                        pos = [bank() for _ in range(2)]
                        povs = [pos[par][0][:, 0:256].rearrange("p (a b) -> p a b", a=2) for par in range(2)]
                        for h in range(4):
                            hp = (h % 2) * 64
                            pok = pos[h % 2][1]
                            reg = povs[h % 2][hp:hp + 64, h // 2, :]
                            mm(reg, VT[:, t, h * 64:(h + 1) * 64], AT[i2][:, h, :], ['rtVT', 'rtAT%d' % i2], [pok], start=True, stop=False)
                            for d in range(2):
                                mm(reg, Sall[d][hp:hp + 64, h // 2, t, :], qd[i2][d][hp:hp + 64, h // 2, :],
                                   ['rtSall%d_%d' % (d, t), 'rtqd%d_%d' % (i2, d)], [pok], start=False, stop=(d == 1))
                        yield
                        obk = 'rtob%d' % i2
                        cp('act', ob[i2][0:64], povs[0][0:64], [pos[0][1]], [obk])
                        cp('dve', ob[i2][64:128], povs[1][64:128], [pos[1][1]], [obk])
                        yield
                        pov = ob[i2][:]
                        pok = obk
                        if ('yc%d' % l) in debug:
                            cp('pool', dbgbuf[:, :, tsl], pov, [pok], ['dbgbuf'])
                        act(sq[i2][:], pov, AF.Square, [pok], ['rtsq%d' % i2])
                        yield
                        pss_, psk = bank()
                        psv = pss_[:, 0:256].rearrange("p (a b) -> p a b", a=2)
                        for j in range(2):
                            mm(psv[:, j, :], bonesb, sq[i2][:, j, :], ['cstb', 'rtsq%d' % i2], [psk])
                        yield
                        act(rr[i2][:], psv, AF.Sqrt, [psk], ['rtrr%d' % i2], bias=RMS_EPS, scale=1.0 / 64)
                        yield
                        S.op('dve', lambda e: e.reciprocal(out=rr[i2][:], in_=rr[i2][:]), reads=['rtrr%d' % i2], writes=['rtrr%d' % i2])
                        yield
                        tt('dve', rr[i2][:], pov, rr[i2][:], ALU.mult, [pok, 'rtrr%d' % i2], ['rtrr%d' % i2])
                        yield
                        tt('pool', Y[:, 2, :, tsl], rr[i2][:], zs[:, :, tsl], ALU.mult, ['rtrr%d' % i2, 'rtzs'], ['Y2'])

                    gj = gate_jobs(l, last, st2, bank, GJ_SPLIT[3]) if (GATE_PRE and GJ_SPLIT[3]) else []
                    run_pipelined(interleave([rtout(t) for t in range(NTL) if not (last and t < 2 and not debug)], gj, 3), STG['out'])
                    if ('yc%d' % l) in debug:
                        dbg_dump('yc%d' % l, dbgbuf[:], [128, 2, NT], ['dbgbuf'])
                    S.barrier()
                S.barrier()
        PHASES['ret'] = phase_ret
        def phase_rw(l, h_src, last):
            with contextlib.ExitStack() as st:
                RB = sb(st, "rwRB", [128, 2, NT], BF16)
                KB = sb(st, "rwKB", [128, 2, NT], BF16)
                VB = sb(st, "rwVB", [128, 2, NT], BF16)
                LB = sb(st, "rwLB", [128, NT], BF16)
                zs = sb(st, "rwzs", [128, 2, NT], BF16)
                vT = sb(st, "rwvT", [128, NTL, 256], BF16)
                lw2b = sb(st, "rwlw2", [128, 2, 256], BF16)
                S.dma('pool', lw2b[:], dr['lw2'][l], writes=['rwlw2'])
                oka = sb(st, "rwoka", [128, 2], F32)
                ts('dve', oka[:], pv('ka'), -1.0, 1.0, ALU.mult, ALU.add, ['pvt'], ['rwoka'])
                seen_b, seen_o = set(), set()
                with contextlib.ExitStack() as st2:
                    ww = sb(st2, "rww", [128, 8, 1152], BF16)
                    for pc_ in range(9):
                        S.dma('pool', ww[:, :, pc_ * 128:(pc_ + 1) * 128], dr['w_in'][l][:, :, 2816 + pc_ * 128:2816 + (pc_ + 1) * 128], writes=['rww%d' % pc_])
                    XRs = [sb(st2, "rwXR%d" % i, [128, NT + 4], F32) for i in range(2)]
                    XSs = [sb(st2, "rwXS%d" % i, [128, NT], F32) for i in range(2)]
                    c0 = sb(st2, "rwc0", [128, 7], F32)
                    pp = [ps(st2, "rwpp%d" % i, [128, 512], F32) for i in range(3)]
                    ptr = [ps(st2, "rwptr%d" % i, [128, 8, 128], BF16) for i in range(2)]
                    o_mu, _ = PV['mu']
                    mu0, mu1 = pvt[:, o_mu:o_mu + 7], pvt[:, o_mu + 7:o_mu + 14]
                    tt('dve', c0[:], mu0, mu1, ALU.add, ['pvt'], ['rwc0'])
                    ts('dve', c0[:], c0[:], -1.0, 1.0, ALU.mult, ALU.add, ['rwc0'], ['rwc0'])
                    for i in range(2):
                        memset('pool', XRs[i][:], 0.0, ['rwXR%d' % i])
                    cnt = 0
                    for m in (0, 1, 2, 3, 4, 5, 7, 6, 8):
                        XR, XRk = XRs[m % 2], 'rwXR%d' % (m % 2)
                        XS, XSk = XSs[m % 2], 'rwXS%d' % (m % 2)
                        for (n0, nn) in BLOCKS:
                            p_, pk_ = pp[cnt % 3], 'rwpp%d' % (cnt % 3)
                            cnt += 1
                            for jj in range(8):
                                mm(p_[:, 0:nn], ww[:, jj, m * 128:(m + 1) * 128], uT[:, jj, n0:n0 + nn],
                                   ['rww%d' % m] + uTk[n0 // 128:(n0 + nn) // 128], [pk_], start=(jj == 0), stop=(jj == 7))
                            if m >= 7:
                                act(zs[:, m - 7, n0:n0 + nn], p_[:, 0:nn], AF.Silu, [pk_, 'pvt'], ['rwzs'], bias=pv('bin', 22 + m))
                            else:
                                xo = 1 if n0 < 256 else 3
                                act(XR[:, n0 + xo:n0 + xo + nn], p_[:, 0:nn], AF.Identity, [pk_, 'pvt'], [XRk], bias=pv('bin', 22 + m))
                        if m >= 7:
                            continue
                        for (b0, ln, o0) in ((1, 256, 0), (259, 2048, 256)):
                            ts('dve', XS[:, o0:o0 + ln], XR[:, b0:b0 + ln], c0[:, m:m + 1], None, ALU.mult, None, [XRk, 'rwc0'], [XSk])
                            stt(XS[:, o0:o0 + ln], XR[:, b0 - 1:b0 - 1 + ln], mu0[:, m:m + 1], XS[:, o0:o0 + ln], ALU.mult, ALU.add,
                                [XRk, 'pvt', XSk], [XSk])
                            if m < 6:
                                dstT, dk = [(RB, 'rwRB'), (KB, 'rwKB'), (VB, 'rwVB')][m // 2]
                                stt(dstT[:, m % 2, o0:o0 + ln], XR[:, b0 + 1:b0 + 1 + ln], mu1[:, m:m + 1], XS[:, o0:o0 + ln], ALU.mult, ALU.add,
                                    [XRk, 'pvt', XSk], [dk])
                            else:
                                stt(XS[:, o0:o0 + ln], XR[:, b0 + 1:b0 + 1 + ln], mu1[:, m:m + 1], XS[:, o0:o0 + ln], ALU.mult, ALU.add,
                                    [XRk, 'pvt', XSk], [XSk])
                        if m == 6:
                            act(LB[0:64, :], XS[0:64, :], AF.Tanh, [XSk], ['rwLB'])
                            cp('pool', LB[64:128, :], XS[64:128, :], [XSk], ['rwLB'])
                    for t in range(NTL):
                        pr, prk = ptr[t % 2], 'rwptr%d' % (t % 2)
                        for j in range(2):
                            tr(pr[:, j, :], VB[:, j, t * 128:(t + 1) * 128], identb, ['rwVB', 'cstb'], [prk])
                        cp('dve' if t % 2 == 0 else 'act', vT[:, t, :], pr[:, 0:2, :].rearrange("p a b -> p (a b)"), [prk], ['rwvT'])
                    S.barrier()
                if stop == 'rw_proj':
                    return
                OS = sb(st, "rwOS", [128, 2, NT], F32)
                with contextlib.ExitStack() as st2:
                    def B(name, shape, dt=BF16):
                        return sb(st2, "rw_" + name, shape, dt), "rw_" + name
                    R64, R64k = B("R64", [128, 256], BF16)
                    memset('pool', R64[:], 1.0, [R64k])
                    memset('pool', R64[:, 0:256:64], 0.0, [R64k])
                    LW, LWk = B("LW", [128, 2, 128], F32)
                    SA, SAk = B("SA", [128, 2, 128], F32)
                    LGm, LGk = B("LG", [128, 2, 128], F32)
                    U0, U0k = LGm, LGk
                    EG, EGk = B("EG", [128, 2, 128], F32)
                    ENG, ENGk = B("ENG", [128, 2, 128], F32)
                    EGM, EGMk = B("EGM", [128, 2, 128], F32)
                    TA, TAk = B("TA", [128, 2, 128], F32)
                    TB_, TBk = B("TB", [128, 2, 128], F32)
                    SQ, SQk = B("SQ", [128, 2, 128])
                    RKD, RKDk = SQ, SQk
                    OBt = (None, None)
                    Zst = [B("Z%d" % d, [128, 2, 64], F32) for d in range(2)]
                    BUF = [dict() for _ in range(2)]
                    for d_ in range(2):
                        BUF[d_]['KKN'] = B("KKN_%d" % d_, [128, 2, 128])
                        BUF[d_]['KT'] = [B("KT_%d_%d" % (d_, s_), [128, 3, 2, 128]) for s_ in range(2)]
                        BUF[d_]['RT'] = [B("RT_%d_%d" % (d_, s_), [128, 2, 128]) for s_ in range(2)]
                        for j_ in range(2):
                            sfx = "_%d_%d" % (d_, j_)
                            SB = dict()
                            SB['TM'] = B("TM" + sfx, [128, 3, 128])
                            for nm_ in ('A1T', 'A2T', 'A3T', 'A4T', 'ALT', 'Tm', 'TTm', 'Xb', 'RHS', 'BYb'):
                                SB[nm_] = B(nm_ + sfx, [128, 2, 128])
                            SB['NY'] = B("NY" + sfx, [128, 2, 64])
                            SB['RH'] = B("RH" + sfx, [128, 128])
                            SB['GTb'] = B("GTb" + sfx, [128, 2, 128])
                            SB['ZLG'] = B("ZLG" + sfx, [128, 2, 64], F32)
                            SB['Z0b'] = B("Z0b" + sfx, [128, 2, 64])
                            BUF[d_][j_] = SB
                        BUF[d_]['GLt'] = [B("GLt_%d_%d" % (d_, s_), [128, 2, 2], F32) for s_ in range(2)]
                    banks = [ps(st2, "rwbank%d" % i, [128, 512], F32) for i in range(8)]
                    bcnt = [0]

                    def bank():
                        i = bcnt[0] % 8
                        bcnt[0] += 1
                        return banks[i], 'rwbank%d' % i
                    for d in range(2):
                        memset('pool', Zst[d][0][:], 0.0, [Zst[d][1], 'rw_Zs_%d_0' % d, 'rw_Zs_%d_1' % d])
                    for d_ in range(2):
                        for j_ in range(2):
                            memset('pool', BUF[d_][j_]['GTb'][0][:], 0.0, [BUF[d_][j_]['GTb'][1]])
                    orders = [list(range(NTL)), [1, 0] + list(range(NTL - 1, 1, -1))]
                    bc3 = lambda ap: ap.unsqueeze(2).broadcast_to([128, 2, 128])
                    def prep(d, t, slot):
                        KKN, KKNk = BUF[d]['KKN']
                        KT, KTk = BUF[d]['KT'][slot]
                        RTb, RTk = BUF[d]['RT'][slot]
                        GLt, GLk = BUF[d]['GLt'][slot]
                        tsl = slice(t * 128, (t + 1) * 128)
                        rev = (d == 1)
                        Z, Zk = Zst[d]
                        plw, plwk = bank()
                        pla, plak = bank()
                        plwv = plw[:, 0:256].rearrange("p (j t) -> p j t", j=2)
                        plav = pla[:, 0:256].rearrange("p (j t) -> p j t", j=2)
                        wb_ = 32 * d
                        for j in range(2):
                            mm(plwv[:, j, :], lw2b[wb_:wb_ + 16, d, j * 128:(j + 1) * 128], LB[wb_:wb_ + 16, tsl], ['rwlw2', 'rwLB'], [plwk])
                        for j in range(2):
                            mm(plav[:, j, :], lw2b[64:96, d, j * 128:(j + 1) * 128], LB[64:96, tsl], ['rwlw2', 'rwLB'], [plak])
                        for j in range(2):
                            act(LW[:, j, :], plwv[:, j, :], AF.Sigmoid, [plwk, 'pvt'], [LWk], bias=pv('w0', d * 2 + j))
                            act(SA[:, j, :], plav[:, j, :], AF.Sigmoid, [plak, 'pvt'], [SAk], bias=pv('a0', d * 2 + j))
                        ts('dve', LW[:], LW[:], -0.6065306597126334, None, ALU.mult, None, [LWk], [LWk])
                        yield
                        lwf = LW[:].rearrange("p a b -> p (a b)")
                        lgf = LGm[:].rearrange("p a b -> p (a b)")
                        if not rev:
                            S.op('dve', lambda e: e.tensor_tensor_scan(out=lgf, data0=R64[:], data1=lwf, initial=0.0, op0=ALU.mult, op1=ALU.add),
                                 reads=[LWk, R64k], writes=[LGk])
                        else:
                            S.op('dve', lambda e: e.tensor_tensor_scan(out=lgf[:, ::-1], data0=R64[:], data1=lwf[:, ::-1], initial=0.0,
                                                                       op0=ALU.mult, op1=ALU.add), reads=[LWk, R64k], writes=[LGk])
                        act(EG[:], LGm[:], AF.Exp, [LGk], [EGk])
                        yield
                        act(ENG[:], LGm[:], AF.Exp, [LGk], [ENGk], scale=-1.0)
                        yield
                        tt('pool', TA[:], LGm[:], LW[:], ALU.subtract, [LGk, LWk], [TAk])
                        yield
                        act(EGM[:], TA[:], AF.Exp, [TAk], [EGMk])
                        yield
                        gsrc = EG[:, :, 63::64] if not rev else EG[:, :, 0::64]
                        cp('pool', GLt[:], gsrc, [EGk], [GLk])
                        yield
                        tt('dve', TA[:], KB[:, :, tsl], bc3(pv('kk')), ALU.mult, ['rwKB', 'pvt', TAk], [TAk])
                        yield
                        act(SQ[:], TA[:], AF.Square, [TAk], [SQk])
                        yield
                        pss_, pssk = bank()
                        pssv = pss_[:, 0:256].rearrange("p (a b) -> p a b", a=2)
                        for j in range(2):
                            mm(pssv[:, j, :], bonesb, SQ[:, j, :], ['cstb', SQk], [pssk])
                        act(TB_[:], pssv, AF.Sqrt, [pssk], [TBk])
                        yield
                        ts('dve', TB_[:], TB_[:], 1e-12, None, ALU.max, None, [TBk], [TBk])
                        yield
                        S.op('dve', lambda e: e.reciprocal(out=TB_[:], in_=TB_[:]), reads=[TBk], writes=[TBk])
                        tt('dve', KKN[:], TA[:], TB_[:], ALU.mult, [TAk, TBk], [KKNk])
                        yield
                        tt('pool', KT[:, 0], KKN[:], EGM[:], ALU.mult, [KKNk, EGMk], [KTk])
                        yield
                        tt('dve', TA[:], SA[:], ENG[:], ALU.mult, [SAk, ENGk, TAk], [TAk])
                        yield
                        tt('pool', KT[:, 1], KKN[:], TA[:], ALU.mult, [KKNk, TAk], [KTk])
                        yield
                        tt('dve', U0[:], SA[:], bc3(pv('ka')), ALU.mult, [SAk, 'pvt'], [U0k])
                        yield
                        tt('dve', U0[:], U0[:], bc3(oka[:]), ALU.add, [U0k, 'rwoka'], [U0k])
                        yield
                        tt('pool', TB_[:], U0[:], ENG[:], ALU.mult, [U0k, ENGk, TBk], [TBk])
                        yield
                        tt('pool', KT[:, 2], KB[:, :, tsl], TB_[:], ALU.mult, ['rwKB', TBk], [KTk])
                        yield
                        tt('dve', RTb[:], RB[:, :, tsl], EG[:], ALU.mult, ['rwRB', EGk], [RTk])
                        yield
                        tt('dve', U0[:], U0[:], KB[:, :, tsl], ALU.mult, [U0k, 'rwKB'], [U0k])
                        yield
                        tt('dve', U0[:], U0[:], bc3(pv('rk')), ALU.mult, [U0k, 'pvt'], [U0k])
                        yield
                        tt('pool', RKD[:], U0[:], RB[:, :, tsl], ALU.mult, [U0k, 'rwRB'], [RKDk])
                        yield
                        pbn, pbnk = bank()
                        pbnv = pbn[:, 0:256].rearrange("p (a b) -> p a b", a=2)
                        for j in range(2):
                            mm(pbnv[:, j, :], bonesb, RKD[:, j, :], ['cstb', RKDk], [pbnk])
                        if t not in seen_b:
                            seen_b.add(t)
                            tt('dve', Y[:, 3, :, tsl], pbnv, VB[:, :, tsl], ALU.mult, [pbnk, 'rwVB'], ['Y3'])
                        else:
                            tt('dve', TA[:], pbnv, VB[:, :, tsl], ALU.mult, [pbnk, 'rwVB', TAk], [TAk])
                            tt('pool', Y[:, 3, :, tsl], Y[:, 3, :, tsl], TA[:], ALU.add, ['Y3', TAk], ['Y3'])

                    def prep_pair(step):
                        for d_ in range(2):
                            yield from prep(d_, orders[d_][step], step % 2)

                    def unit(d, t, slot):
                        KT, KTk = BUF[d]['KT'][slot]
                        RTb, RTk = BUF[d]['RT'][slot]
                        GLt, GLk = BUF[d]['GLt'][slot]
                        tsl = slice(t * 128, (t + 1) * 128)
                        rev = (d == 1)
                        subs = [stream(d, j, t, rev, tsl, KT, KTk, RTb, RTk, GLt, GLk) for j in range(2)]
                        while subs:
                            for g in list(subs):
                                try:
                                    next(g)
                                except StopIteration:
                                    subs.remove(g)
                                yield

                    def stream(d, j, t, rev, tsl, KT, KTk, RTb, RTk, GLt, GLk):
                        SB = BUF[d][j]
                        TM, TMk = SB['TM']
                        A1T, A1k = SB['A1T']
                        A2T, A2k = SB['A2T']
                        A3T, A3k = SB['A3T']
                        A4T, A4k = SB['A4T']
                        ALT, ALk = SB['ALT']
                        Tm, Tmk = SB['Tm']
                        TTm, TTk = SB['TTm']
                        Xb, Xbk = SB['Xb']
                        RHS, RHSk = SB['RHS']
                        BYb, BYk = SB['BYb']
                        NY, NYk = SB['NY']
                        RH, RHk = SB['RH']
                        GTb, GTk = SB['GTb']
                        ZLG, ZLGk = SB['ZLG']
                        Z0b, Z0k = SB['Z0b']
                        Z, _zk = Zst[d]
                        Zk = 'rw_Zs_%d_%d' % (d, j)
                        ptb, ptbk = bank()
                        ptv = ptb[:].bitcast(BF16).rearrange("p (a b) -> p a b", a=8)
                        for x in range(3):
                            tr(ptv[:, x, :], KT[:, x, j, :], identb, [KTk, 'cstb'], [ptbk])
                        cp('act', TM[:], ptv[:, 0:3, :], [ptbk], [TMk])
                        yield

                        def amat(dst, dstk, li, ri_src, ri_k, mslot):
                            pas = []
                            for par in range(2):
                                hp = par * 64
                                pa, pak = bank()
                                rhs = (RTb[hp:hp + 64, j, :] if ri_src is None else KT[hp:hp + 64, ri_src, j, :])
                                mm(pa[:, 0:128], KT[hp:hp + 64, li, j, :], rhs, [KTk, ri_k], [pak])
                                pas.append((pa, pak))
                            return pas

                        def aevac(pas, dst, dstk, mslot):
                            for par, (pa, pak) in enumerate(pas):
                                if mslot is None:
                                    cp('act', dst[:, par, :], pa[:, 0:128], [pak], [dstk])
                                else:
                                    tt('dve', dst[:, par, :], pa[:, 0:128], maskb[:, mslot, :], ALU.mult, [pak, 'maskb'], [dstk])
                        for (dst, dstk, li, rs, rk, ms) in ((A1T, A1k, 1, 0, KTk, None), (A2T, A2k, 2, 0, KTk, 2 + d),
                                                            (A3T, A3k, 1, None, RTk, 4 + d), (A4T, A4k, 2, None, RTk, 4 + d)):
                            pas = amat(dst, dstk, li, rs, rk, ms)
                            aevac(pas, dst, dstk, ms)
                            yield
                        idb2 = identb.unsqueeze(1).broadcast_to([128, 2, 128])
                        cp('pool', Tm[:], idb2, ['cstb'], [Tmk])
                        cp('pool', TTm[:], idb2, ['cstb'], [TTk])
                        for lv in range(6):
                            tt('pool', ALT[:], A1T[:], maskb[:, 6 + d * 6 + lv, :].unsqueeze(1).broadcast_to([128, 2, 128]), ALU.mult,
                               [A1k, 'maskb'], [ALk])
                            yield
                            px, pxk = bank()
                            pxv = px[:, 0:256].rearrange("p (h t) -> p h t", h=2)
                            for par in range(2):
                                mm(pxv[:, par, :], ALT[:, par, :], Tm[:, par, :], [ALk, Tmk], [pxk])
                            cp('act', Xb[:], pxv, [pxk], [Xbk])
                            yield
                            py_, pyk = bank()
                            pyv = py_[:].rearrange("p (x h t) -> p x h t", x=2, h=2)
                            for par in range(2):
                                mm(pyv[:, 0, par, :], Xb[:, par, :], TTm[:, par, :], [Xbk, TTk], [pyk])
                            if lv < 5:
                                for par in range(2):
                                    mm(pyv[:, 1, par, :], TTm[:, par, :], Xb[:, par, :], [Xbk, TTk], [pyk])
                            if lv < 5:
                                tt('dve', Tm[:], Tm[:], pyv[:, 1], ALU.subtract, [Tmk, pyk], [Tmk])
                            tt('dve', TTm[:], TTm[:], pyv[:, 0], ALU.subtract, [TTk, pyk], [TTk])
                            yield
                        pw, pwk = bank()
                        pwv = pw[:, 0:128].rearrange("p (h v) -> p h v", h=2)
                        for par in range(2):
                            h = 2 * j + par
                            mm(pwv[:, par, :], A2T[:, par, :], vT[:, t, h * 64:(h + 1) * 64], [A2k, 'rwvT'], [pwk])
                        cp('pool', RHS[:, :, 0:64], TM[:, 0, :].rearrange("p (h k) -> p h k", h=2), [TMk], [RHSk])
                        cp('act', RHS[:, :, 64:128], pwv, [pwk], [RHSk])
                        yield
                        pby, pbyk = bank()
                        pbyv = pby[:, 0:256].rearrange("p (h t) -> p h t", h=2)
                        for par in range(2):
                            mm(pbyv[:, par, :], TTm[:, par, :], RHS[:, par, :], [TTk, RHSk], [pbyk])
                        cp('act', BYb[:], pbyv, [pbyk], [BYk])
                        yield
                        ts('pool', NY[:], BYb[:, :, 64:128], -1.0, 0.0, ALU.mult, ALU.add, [BYk], [NYk])
                        pr_, prk = bank()
                        for par in range(2):
                            hp = par * 64
                            mm(pr_[hp:hp + 64, 0:128], BYb[:, par, 0:64], A3T[:, par, :], [BYk, A3k], [prk])
                        tt('dve', RH[:], RTb[:, j, :], pr_[:, 0:128], ALU.subtract, [RTk, prk], [RHk])
                        yield
                        for c in range(2):
                            cs = slice(c * 64, (c + 1) * 64)
                            pg_, pgk = bank()
                            pgv = pg_[:, 0:128].rearrange("p (x v) -> p x v", x=2)
                            for par in range(2):
                                hp = par * 64
                                h = 2 * j + par
                                hc = slice(h * 64, (h + 1) * 64)
                                pc = slice(par * 64, (par + 1) * 64)
                                mm(pgv[hp:hp + 64, 0, :], BYb[cs, par, 0:64], TM[cs, 1, pc], [BYk, TMk], [pgk])
                                mm(pgv[hp:hp + 64, 1, :], TM[cs, 2, pc], vT[cs, t, hc], [TMk, 'rwvT'], [pgk], start=True, stop=False)
                                mm(pgv[hp:hp + 64, 1, :], TM[cs, 1, pc], NY[cs, par, :], [TMk, NYk], [pgk], start=False, stop=True)
                            for par in range(2):
                                hp = par * 64
                                tt('dve', GTb[hp:hp + 64, c, hp:hp + 64], cstf[hp:hp + 64, 4, 0:64], pgv[hp:hp + 64, 0, :], ALU.subtract,
                                   ['cstf', pgk], [GTk])
                            ts('dve', ZLG[:, c, :], pgv[:, 1, :], GLt[:, j, c:c + 1], None, ALU.mult, None, [pgk, GLk], [ZLGk])
                            yield
                        for c in ((0, 1) if not rev else (1, 0)):
                            cp('act', Z0b[:, c, :], Z[:, j, :], [Zk], [Z0k])
                            yield
                            pn, pnk = bank()
                            mm(pn[:, 0:64], GTb[:, c, :], Z0b[:, c, :], [GTk, Z0k], [pnk])
                            stt(Z[:, j, :], pn[:, 0:64], GLt[:, j, c:c + 1], ZLG[:, c, :], ALU.mult, ALU.add, [pnk, GLk, ZLGk, Zk], [Zk])
                            yield
                        for par in range(2):
                            hp = par * 64
                            h = 2 * j + par
                            hc = slice(h * 64, (h + 1) * 64)
                            po_, pok = bank()
                            reg = po_[hp:hp + 64, 0:128]
                            mm(reg, vT[:, t, hc], A4T[:, par, :], ['rwvT', A4k], [pok], start=True, stop=False)
                            mm(reg, NY[:, par, :], A3T[:, par, :], [NYk, A3k], [pok], start=False, stop=False)
                            for c in range(2):
                                mm(reg[:, c * 64:(c + 1) * 64], Z0b[hp:hp + 64, c, :], RH[hp:hp + 64, c * 64:(c + 1) * 64],
                                   [Z0k, RHk], [pok], start=False, stop=(c == 1))
                            osl = OS[hp:hp + 64, j, tsl]
                            osk = 'rwOS%d_%d' % (t, j)
                            if (t, j, par) not in seen_o:
                                seen_o.add((t, j, par))
                                cp('dve' if par == 0 else 'act', osl, reg, [pok], [osk])
                            else:
                                tt('dve', osl, osl, reg, ALU.add, [pok, osk], [osk])
                            yield

                    for _ in prep_pair(0):
                        pass
                    for step in range(NTL):
                        gens = [unit(d, orders[d][step], step % 2) for d in range(2)]
                        if step + 1 < NTL:
                            gens.append(prep_pair(step + 1))
                        while gens:
                            for g in list(gens):
                                try:
                                    next(g)
                                except StopIteration:
                                    gens.remove(g)
                    S.barrier()
                if stop is not None and stop.startswith('rw_'):
                    return
                with contextlib.ExitStack() as st2:
                    ob = [sb(st2, "rwob%d" % i, [128, 2, 128], BF16) for i in range(2)]
                    cen = [sb(st2, "rwcen%d" % i, [128, 2, 128], F32) for i in range(2)]
                    rs = [sb(st2, "rwrs%d" % i, [128, 2, 128], F32) for i in range(2)]
                    pm_ = [ps(st2, "rwpm%d" % i, [128, 512], F32) for i in range(2)]
                    pv_ = [ps(st2, "rwpv%d" % i, [128, 512], F32) for i in range(2)]
                    for t in range(NTL):
                        i2 = t % 2
                        tsl = slice(t * 128, (t + 1) * 128)
                        osk = 'rwOS%d_0' % t
                        osk1 = 'rwOS%d_1' % t
                        cp('act', ob[i2][:], OS[:, :, tsl], [osk, osk1], ['rwob%d' % i2])
                        pmv = pm_[i2][:, 0:256].rearrange("p (a b) -> p a b", a=2)
                        for j in range(2):
                            mm(pmv[:, j, :], bonesb, ob[i2][:, j, :], ['cstb', 'rwob%d' % i2], ['rwpm%d' % i2])
                        stt(cen[i2][:], pmv, -1.0 / 64, OS[:, :, tsl], ALU.mult, ALU.add, ['rwpm%d' % i2, osk, osk1], ['rwcen%d' % i2])
                        act(ob[i2][:], cen[i2][:], AF.Square, ['rwcen%d' % i2], ['rwob%d' % i2])
                        pvv = pv_[i2][:, 0:256].rearrange("p (a b) -> p a b", a=2)
                        for j in range(2):
                            mm(pvv[:, j, :], bonesb, ob[i2][:, j, :], ['cstb', 'rwob%d' % i2], ['rwpv%d' % i2])
                        act(rs[i2][:], pvv, AF.Sqrt, ['rwpv%d' % i2], ['rwrs%d' % i2], bias=RW_GN_EPS, scale=1.0 / 64)
                        S.op('dve', lambda e: e.reciprocal(out=rs[i2][:], in_=rs[i2][:]), reads=['rwrs%d' % i2], writes=['rwrs%d' % i2])
                        tt('dve', cen[i2][:], cen[i2][:], rs[i2][:], ALU.mult, ['rwcen%d' % i2, 'rwrs%d' % i2], ['rwcen%d' % i2])
                        tt('pool', cen[i2][:], cen[i2][:], bc3(pv('gnw')), ALU.mult, ['rwcen%d' % i2, 'pvt'], ['rwcen%d' % i2])
                        tt('pool', cen[i2][:], cen[i2][:], bc3(pv('gnb')), ALU.add, ['rwcen%d' % i2, 'pvt'], ['rwcen%d' % i2])
                        tt('dve', cen[i2][:], cen[i2][:], Y[:, 3, :, tsl], ALU.add, ['rwcen%d' % i2, 'Y3'], ['rwcen%d' % i2])
                        if ('yd%d' % l) in debug:
                            cp('act', OS[:, :, tsl], cen[i2][:], ['rwcen%d' % i2], [osk, osk1])
                        tt('dve', Y[:, 3, :, tsl], cen[i2][:], zs[:, :, tsl], ALU.mult, ['rwcen%d' % i2, 'rwzs'], ['Y3'])
                    if ('yd%d' % l) in debug:
                        dbg_dump('yd%d' % l, OS[:], [128, 2, NT], ['rwOS%d_%d' % (t, j_) for t in range(NTL) for j_ in range(2)])
                    S.barrier()
                S.barrier()
        PHASES['rw'] = phase_rw
        def phase_merge(l, h_src, last):
            h_dst = out_d if last else h1_d
            with contextlib.ExitStack() as st:
                MG = sb(st, "mgMG", [128, 8, NT], BF16)
                wbr = sb(st, "mgwbr", [128, 4, 2, DM], BF16)
                S.dma('pool', wbr[:], dr['wbr'][l], writes=['mgwbr'])
                with contextlib.ExitStack() as st2:
                    wg = [sb(st2, "mgwg%d" % i, [128, 8, 4, 128], BF16) for i in range(2)]
                    sg = [sb(st2, "mgsg%d" % i, [128, 512], BF16) for i in range(3)]
                    ac = [sb(st2, "mgac%d" % i, [128, 512], F32) for i in range(2)]
                    tm = [sb(st2, "mgtm%d" % i, [128, 512], F32) for i in range(2)]
                    pgl = [ps(st2, "mgpg%d" % i, [128, 512], F32) for i in range(3)]
                    pbr = [ps(st2, "mgpb%d" % i, [128, 512], F32) for i in range(3)]
                    cg = 0
                    ca = 0
                    def load_wg(dt_):
                        for k in range(4):
                            if (l, k * 8 + dt_) in pre_sg:
                                continue
                            c0 = 3968 + k * 1024 + dt_ * 128
                            S.dma('pool', wg[dt_ % 2][:, :, k, :], dr['w_in'][l][:, :, c0:c0 + 128], writes=['mgwg%d' % (dt_ % 2)])
                    load_wg(0)
                    for dt_ in range(8):
                        w_, wk_ = wg[dt_ % 2], 'mgwg%d' % (dt_ % 2)
                        if dt_ + 1 < 8:
                            load_wg(dt_ + 1)
                        for (n0, nn) in BLOCKS:
                            if last and n0 < 256:
                                continue
                            a_, ak_ = ac[ca % 2], 'mgac%d' % (ca % 2)
                            t_, tk_ = tm[ca % 2], 'mgtm%d' % (ca % 2)
                            ca += 1
                            for k in range(4):
                                pg_, pgk_ = pgl[cg % 3], 'mgpg%d' % (cg % 3)
                                pb_, pbk_ = pbr[cg % 3], 'mgpb%d' % (cg % 3)
                                s_, sk_ = sg[cg % 3], 'mgsg%d' % (cg % 3)
                                cg += 1
                                if (l, k * 8 + dt_) in pre_sg:
                                    S.dma('sp' if cg % 2 == 0 else 'act', s_[:, 0:nn], sgd[k * 8 + dt_][:, n0:n0 + nn], reads=['sgd'], writes=[sk_])
                                else:
                                    for jj in range(8):
                                        mm(pg_[:, 0:nn], w_[:, jj, k, :], uT[:, jj, n0:n0 + nn], [wk_] + uTk[n0 // 128:(n0 + nn) // 128], [pgk_],
                                           start=(jj == 0), stop=(jj == 7))
                                    act(s_[:, 0:nn], pg_[:, 0:nn], AF.Sigmoid, [pgk_, 'pvt'], [sk_], bias=pv('bin', 31 + k * 8 + dt_))
                                for jc in range(2):
                                    mm(pb_[:, 0:nn], wbr[:, k, jc, dt_ * 128:(dt_ + 1) * 128], Y[:, k, jc, n0:n0 + nn], ['mgwbr', 'Y%d' % k], [pbk_],
                                       start=(jc == 0), stop=(jc == 1))
                                if k == 0:
                                    tt('dve', a_[:, 0:nn], pb_[:, 0:nn], s_[:, 0:nn], ALU.mult, [pbk_, sk_], [ak_])
                                else:
                                    tt('dve', t_[:, 0:nn], pb_[:, 0:nn], s_[:, 0:nn], ALU.mult, [pbk_, sk_], [tk_])
                                    if k < 3:
                                        tt('pool', a_[:, 0:nn], a_[:, 0:nn], t_[:, 0:nn], ALU.add, [ak_, tk_], [ak_])
                                    else:
                                        tt('pool', MG[:, dt_, n0:n0 + nn], a_[:, 0:nn], t_[:, 0:nn], ALU.add, [ak_, tk_], ['mgMG%d' % (n0 // 512 if n0 else 9)])
                    S.barrier()
                if ('merged%d' % l) in debug:
                    with contextlib.ExitStack() as st2:
                        mf = sb(st2, "mgf", [128, 8, NT], F32)
                        cp('dve', mf[:], MG[:], ['mgMG%d' % i for i in (9, 0, 1, 2, 3)], ['mgf'])
                        dbg_dump('merged%d' % l, mf[:], [128, 8, NT], ['mgf'])
                        S.barrier()
                with contextlib.ExitStack() as st2:
                    wo = sb(st2, "mgwo", [128, 8, DM], BF16)
                    S.dma('pool', wo[:], dr['wout'][l], writes=['mgwo'])
                    rows = sb(st2, "mgrows", [128, 3, DM], F32)
                    S.dma('sp', rows[:], dr['rows'][l][:, 0:3072].rearrange("p (a b) -> p a b", a=3), writes=['mgrows'])
                    hin_ = [sb(st2, "mghin%d" % i, [128, DM], F32) for i in range(2)]
                    ot = [sb(st2, "mgot%d" % i, [128, DM], F32) for i in range(2)]
                    stat = [sb(st2, "mgst%d" % i, [128, 16], F32) for i in range(2)]
                    po = [[ps(st2, "mgpo%d_%d" % (i, hh), [128, 512], F32) for hh in range(2)] for i in range(2)]
                    def mgout(it, t):
                        i2 = it % 2
                        ci = 1 if t < 2 else 0
                        tsl = slice(t * 128, (t + 1) * 128)
                        mgk = 'mgMG%d' % (9 if t < 2 else (t - 2) // 4)
                        hk_, ok_, sk_ = 'mghin%d' % i2, 'mgot%d' % i2, 'mgst%d' % i2
                        hi, o_, sti = hin_[i2], ot[i2], stat[i2]
                        S.dma('sp', hi[:], h_src[t * 128:(t + 1) * 128, :], writes=[hk_])
                        for hh in range(2):
                            pk_ = 'mgpo%d_%d' % (i2, hh)
                            for jj in range(8):
                                mm(po[i2][hh][:], MG[:, jj, tsl], wo[:, jj, hh * 512:(hh + 1) * 512], [mgk, 'mgwo'], [pk_], start=(jj == 0), stop=(jj == 7))
                        yield
                        for hh in range(2):
                            pk_ = 'mgpo%d_%d' % (i2, hh)
                            tt('dve', o_[:, hh * 512:(hh + 1) * 512], po[i2][hh][:], rows[:, 0, hh * 512:(hh + 1) * 512], ALU.add, [pk_, 'mgrows'], [ok_])
                        yield
                        tt('dve', o_[:], o_[:], gatebc[:, ci, :], ALU.mult, [ok_, 'gatebc'], [ok_])
                        yield
                        stt(o_[:], hi[:], ALPHA, o_[:], ALU.mult, ALU.add, [hk_, ok_], [ok_])
                        yield
                        S.op('dve', lambda e: e.bn_stats(out=sti[:, 0:6], in_=o_[:, 0:512]), reads=[ok_], writes=[sk_])
                        S.op('dve', lambda e: e.bn_stats(out=sti[:, 6:12], in_=o_[:, 512:1024]), reads=[ok_], writes=[sk_])
                        yield
                        S.op('dve', lambda e: e.bn_aggr(out=sti[:, 12:14], in_=sti[:, 0:12]), reads=[sk_], writes=[sk_])
                        yield
                        act(sti[:, 14:15], sti[:, 13:14], AF.Sqrt, [sk_], [sk_], bias=LN_EPS)
                        yield
                        S.op('dve', lambda e: e.reciprocal(out=sti[:, 14:15], in_=sti[:, 14:15]), reads=[sk_], writes=[sk_])
                        yield
                        stt(sti[:, 15:16], sti[:, 12:13], -1.0, sti[:, 14:15], ALU.mult, ALU.mult, [sk_], [sk_])
                        yield
                        act(o_[:], o_[:], AF.Identity, [ok_, sk_], [ok_], bias=sti[:, 15:16], scale=sti[:, 14:15])
                        yield
                        tt('pool', o_[:, 0:512], o_[:, 0:512], rows[:, 1, 0:512], ALU.mult, [ok_, 'mgrows'], [ok_])
                        tt('dve', o_[:, 512:1024], o_[:, 512:1024], rows[:, 1, 512:1024], ALU.mult, [ok_, 'mgrows'], [ok_])
                        yield
                        tt('pool', o_[:, 0:512], o_[:, 0:512], rows[:, 2, 0:512], ALU.add, [ok_, 'mgrows'], [ok_])
                        tt('dve', o_[:, 512:1024], o_[:, 512:1024], rows[:, 2, 512:1024], ALU.add, [ok_, 'mgrows'], [ok_])
                        yield
                        if last:
                            S.dma('sp', out_d[(t - 2) * 128:(t - 1) * 128, :], o_[:], reads=[ok_], writes=['outfinal'])
                        else:
                            S.dma('sp', h1_d[t * 128:(t + 1) * 128, :], o_[:], reads=[ok_], writes=['h1'])

                    tl = [t for t in range(NTL) if not (last and t < 2)]
                    run_pipelined((mgout(i_, t) for i_, t in enumerate(tl)), STG['mgout'])
                    S.barrier()
                S.barrier()
        PHASES['merge'] = phase_merge
        for l in range(nlayers):
            last = (l == nlayers - 1)
            h_src = dr['hin'] if l == 0 else h1_d
            S.dma('sp', pvt[:], dr['pv'][l], writes=['pvt'])
            with contextlib.ExitStack() as st:
                adw = [sb(st, "adw%d" % i, [128, 8, 512], F32) for i in range(2)]
                scb = sb(st, "scb", [128, 2, 8, 128], F32)
                grow = sb(st, "grow", [128, DM], F32)
                pm0 = ps(st, "pm0", [128, 16, 2], F32)
                pg = [ps(st, "pg%d" % i, [128, 512], F32) for i in range(2)]
                for i in range(2):
                    cp('dve', scb[:, i], silc[:, :, i:i + 1].broadcast_to([128, 8, 128]), ['silc'], ['scb'])
                S.dma('sp', grow[:], dr['rows'][l][:, 3072:4096], writes=['grow'])
                for ch in range(6):
                    buf = adw[ch % 2]
                    bk = 'adw%d' % (ch % 2)
                    S.dma('sp' if ch % 2 == 0 else 'act', buf[:], dr['ada_w'][l][:, :, ch * 512:(ch + 1) * 512], writes=[bk])
                    if ch < 4:
                        for mloc in range(4):
                            m = ch * 4 + mloc
                            for j in range(8):
                                mm(pm0[:, m, :], buf[:, j, mloc * 128:(mloc + 1) * 128], silc[:, j, :], [bk, 'silc'],
                                   ['pm0'], start=(j == 0), stop=(j == 7))
                    else:
                        for i in range(2):
                            for j in range(8):
                                mm(pg[i][:], scb[:, i, j, :], buf[:, j, :], [bk, 'scb'], ['pg%d' % i],
                                   start=(j == 0), stop=(j == 7))
                            tt('dve', gatebc[:, i, (ch - 4) * 512:(ch - 3) * 512], pg[i][:],
                               grow[:, (ch - 4) * 512:(ch - 3) * 512], ALU.add, ['pg%d' % i, 'grow'], ['gatebc'])
                tt('dve', modfm[:], pm0[:], pv('adab').unsqueeze(2).broadcast_to([128, 16, 2]), ALU.add,
                   ['pm0', 'pvt'], ['modfm'])
                ts('dve', modfm[:, 8:16, :], modfm[:, 8:16, :], 1.0, None, ALU.add, None, ['modfm'], ['modfm'])
                dbg_dump('modfm%d' % l, modfm[:], [128, 16, 2], ['modfm'])
                dbg_dump('gatebc%d' % l, gatebc[:], [128, 2, DM], ['gatebc'])
                S.barrier()
            with contextlib.ExitStack() as st:
                xin = [sb(st, "xin%d" % i, [128, DM], F32) for i in range(3)]
                xn = [sb(st, "xn%d" % i, [128, DM], BF16) for i in range(2)]
                stat = [sb(st, "stat%d" % i, [128, 16], F32) for i in range(3)]
                ptr = [ps(st, "ptr%d" % i, [128, 8, 128], BF16) for i in range(2)]
                def p1tile(t):
                    xi, xk = xin[t % 3], 'xin%d' % (t % 3)
                    sti, sk = stat[t % 3], 'stat%d' % (t % 3)
                    xo, xok = xn[t % 2], 'xn%d' % (t % 2)
                    pt, ptk = ptr[t % 2], 'ptr%d' % (t % 2)
                    ci = 1 if t < 2 else 0
                    S.dma('sp' if t % 2 == 0 else 'act', xi[:], h_src[t * 128:(t + 1) * 128, :], writes=[xk])
                    yield
                    S.op('dve', lambda e: e.bn_stats(out=sti[:, 0:6], in_=xi[:, 0:512]), reads=[xk], writes=[sk])
                    S.op('dve', lambda e: e.bn_stats(out=sti[:, 6:12], in_=xi[:, 512:1024]), reads=[xk], writes=[sk])
                    yield
                    S.op('dve', lambda e: e.bn_aggr(out=sti[:, 12:14], in_=sti[:, 0:12]), reads=[sk], writes=[sk])
                    yield
                    act(sti[:, 14:15], sti[:, 13:14], AF.Sqrt, [sk], [sk], bias=LN_EPS)
                    yield
                    S.op('dve', lambda e: e.reciprocal(out=sti[:, 14:15], in_=sti[:, 14:15]), reads=[sk], writes=[sk])
                    yield
                    stt(sti[:, 15:16], sti[:, 12:13], -1.0, sti[:, 14:15], ALU.mult, ALU.mult, [sk], [sk])
                    yield
                    act(xo[:], xi[:], AF.Identity, [xk, sk], [xok], bias=sti[:, 15:16], scale=sti[:, 14:15])
                    yield
                    for j in range(8):
                        tr(pt[:, j, :], xo[:, j * 128:(j + 1) * 128], identb, [xok, 'cstb'], [ptk])
                    yield
                    for j in range(8):
                        if j % 2 == 0:
                            act(uT[:, j, t * 128:(t + 1) * 128], pt[:, j, :], AF.Identity, [ptk, 'modfm'], ['uT%d' % t],
                                bias=modfm[:, j, ci:ci + 1], scale=modfm[:, 8 + j, ci:ci + 1])
                        else:
                            ts('dve', uT[:, j, t * 128:(t + 1) * 128], pt[:, j, :], modfm[:, 8 + j, ci:ci + 1],
                               modfm[:, j, ci:ci + 1], ALU.mult, ALU.add, [ptk, 'modfm'], ['uT%d' % t])

                run_pipelined((p1tile(t) for t in range(NTL)), STG['p1'])
                if ('uT%d' % l) in debug:
                    utf = sb(st, "utf", [128, 8, NT], F32)
                    cp('dve', utf[:], uT[:], ['uT%d' % t for t in range(NTL)], ['utf'])
                    dbg_dump('uT%d' % l, utf[:], [128, 8, NT], ['utf'])
                S.barrier()
            uTk = ['uT%d' % t for t in range(NTL)]

            for ph in list(PHASES):
                if ph in phases:
                    PHASES[ph](l, h_src, last)
            if ('h%d' % l) in debug and not last:
                d_ = dbg_out('h%d' % l, [NT, DM])
                S.dma('sp', d_, h1_d, writes=['dbgout_h%d' % l])
                S.barrier()
            if ('Y%d' % l) in debug:
                with contextlib.ExitStack() as st:
                    yf = sb(st, "yf", [128, 4, 2, NT], F32)
                    cp('dve', yf[:], Y[:], ['Y0', 'Y1', 'Y2', 'Y3'], ['yf'])
                    dbg_dump('Y%d' % l, yf[:], [128, 4, 2, NT], ['yf'])
                    S.barrier()

        S.final_wait('sp', ['outfinal'] + ['dbgout_' + n for n in dbg_d])
    if MEMDBG:
        print('SBUF min remaining by prefix:', minrem)
    return nc, dbg_d


def kernel(**inputs):
    inp = {k: np.asarray(v) for k, v in inputs.items()}
    sh = prep_shared(inp)
    nc, _ = build()
    in_maps = []
    for b in range(8):
        m = dict(sh)
        m.update(prep_core(inp, b))
        in_maps.append(m)
    res = run_bass_kernel_spmd(nc, in_maps, core_ids=list(range(8)))
    return np.stack([np.asarray(res.results[b]['out'], dtype=np.float32) for b in range(8)], 0)
```

```python
import contextlib
import numpy as np
import concourse.bass as bass
import concourse.mybir as mybir
from concourse.bass_utils import run_bass_kernel_spmd

F32 = mybir.dt.float32
BF16 = mybir.dt.bfloat16
AF = mybir.ActivationFunctionType
ALU = mybir.AluOpType
AX = mybir.AxisListType

NT = 2304
NTL = 18
DM = 1024
NCOL = 8064
BLOCKS = [(0, 256), (256, 512), (768, 512), (1280, 512), (1792, 512)]
LN_EPS = 1e-5
RMS_EPS = 1e-6
RW_GN_EPS = 64e-5
ALPHA = (2 * 2) ** 0.25
PI = float(np.pi)
MEMDBG = False
GATE_PRE = False
S5_STAGGER = 6
STG = dict(hgchain=2, out=4, rtchain=1, hgproj=4, mgout=6, p1=4, s5=6)
GJ_SPLIT = [[], [], [], []]
GJ_S5 = list(range(32))


class Sched:
    NDMA = 16

    def __init__(self, nc, same_engine_waits=True):
        self.nc = nc
        self.same = same_engine_waits
        self.eng = dict(pe=nc.tensor, act=nc.scalar, dve=nc.vector, pool=nc.gpsimd, sp=nc.sync)
        self.E = {n: dict(cnt=0, known={}) for n in self.eng}
        self.dq = {'sp': ['dsp%d' % i for i in range(8)], 'act': ['dac%d' % i for i in range(4)],
                   'pool': ['dpl%d' % i for i in range(8)]}
        self.dmas = {n: dict(cnt=0) for q in self.dq.values() for n in q}
        self.dma_rr = {'sp': 0, 'act': 0, 'pool': 0}
        self.lastw = {}
        self.readers = {}
        self.sems = None
        self.nins = 0

    def sem_names(self):
        return list(self.E.keys()) + list(self.dmas.keys())

    def _deps(self, reads, writes):
        deps = {}

        def add(w):
            if w is not None:
                deps[w[0]] = max(deps.get(w[0], 0), w[1])
        for k in reads:
            add(self.lastw.get(k))
        for k in writes:
            add(self.lastw.get(k))
            for r in self.readers.get(k, ()):
                add(r)
        return deps

    def _waits(self, en, deps):
        E = self.E[en]
        waits = []
        for d, v in deps.items():
            if d == en and (en == 'pe' or not self.same):
                continue
            if E['known'].get(d, 0) < v:
                waits.append((d, v))
                E['known'][d] = v
        return waits

    def _record(self, ident, reads, writes):
        for k in writes:
            self.lastw[k] = ident
            self.readers[k] = []
        for k in reads:
            self.readers.setdefault(k, []).append(ident)

    def _emit(self, en, waits, fn, inc):
        eng = self.eng[en]
        for d, v in waits:
            eng.wait_ge(self.sems[d], v)
        if fn is not None:
            fn(eng).then_inc(self.sems[inc[0]], inc[1])
            self.nins += 1

    def op(self, en, fn, reads=(), writes=()):
        E = self.E[en]
        waits = self._waits(en, self._deps(reads, writes))
        E['cnt'] += 1
        self._emit(en, waits, fn, (en, 1))
        self._record((en, E['cnt']), reads, writes)

    def dma(self, en, out, in_, reads=(), writes=(), **kw):
        dn = self.dq[en][self.dma_rr[en]]
        self.dma_rr[en] = (self.dma_rr[en] + 1) % len(self.dq[en])
        Dq = self.dmas[dn]
        deps = self._deps(reads, writes)
        if Dq['cnt'] > 0:
            deps[dn] = max(deps.get(dn, 0), Dq['cnt'])
        waits = self._waits(en, deps)
        Dq['cnt'] += 16
        self._emit(en, waits, (lambda e: e.dma_start(out=out, in_=in_, **kw)), (dn, 16))
        self._record((dn, Dq['cnt']), reads, writes)

    def barrier(self):
        cur = {n: self.E[n]['cnt'] for n in self.E}
        cur.update({n: self.dmas[n]['cnt'] for n in self.dmas})
        for en in self.E:
            waits = self._waits(en, {d: v for d, v in cur.items() if v > 0})
            self._emit(en, waits, None, None)

    def final_wait(self, en, keys):
        self._emit(en, self._waits(en, self._deps(keys, ())), None, None)


PV = {}


def _pv_layout():
    off = 0
    for name, n in [('bin', 63), ('s5d', 2), ('glub', 2), ('hglb', 8), ('hgnw', 2), ('rdec', 4), ('mu', 14),
                    ('w0', 4), ('a0', 4), ('kk', 2), ('ka', 2), ('rk', 2), ('gnw', 2), ('gnb', 2), ('adab', 16),
                    ('lamre', 16), ('lamim', 16), ('ldt', 16), ('rdech', 8)]:
        PV[name] = (off, n)
        off += n
    return off


NPV = _pv_layout()


def _colmap():
    cm = list(range(0, 3584))
    lora = [-1] * 128
    for r in range(16):
        lora[r] = 3584 + r
        lora[32 + r] = 3600 + r
        lora[64 + r] = 3616 + r
        lora[80 + r] = 3632 + r
    cm += lora
    cm += list(range(3648, 3904))
    cm += list(range(3904, 8000))
    return np.array(cm)


CMAP = _colmap()


def _fm(v):
    return np.ascontiguousarray(v.reshape(-1, 128).T)


def _masks():
    t = np.arange(128)
    s_, t_ = t[:, None], t[None, :]
    m = []
    b32 = (s_ // 32) == (t_ // 32)
    b64 = (s_ // 64) == (t_ // 64)
    m.append(b32 & (t_ >= s_))
    m.append(b32 & (t_ <= s_))
    m.append(b64 & (t_ > s_))
    m.append(b64 & (t_ < s_))
    m.append(b64 & (t_ >= s_))
    m.append(b64 & (t_ <= s_))
    for d in range(2):
        for lv in range(6):
            sz = 1 << lv
            blk = (s_ // (2 * sz)) == (t_ // (2 * sz))
            hs, ht = (s_ // sz) % 2, (t_ // sz) % 2
            if d == 0:
                m.append(blk & (ht == 1) & (hs == 0))
            else:
                m.append(blk & (ht == 0) & (hs == 1))
    return np.stack([x.astype(np.float32) for x in m], 1)


def _rot_tables():
    n = 16
    freqs = 10000.0 ** (-np.arange(n, dtype=np.float32) / n)
    tt = np.arange(2048)
    rows = (tt // 64).astype(np.float32)
    cols = (tt % 64).astype(np.float32)
    cos = np.zeros((128, 2048), np.float32)
    sins = np.zeros((128, 2048), np.float32)
    pm = np.zeros((128, 128), np.float32)
    for p in range(128):
        i = p % 64
        pos = rows if i < 32 else cols
        ii = i % 32
        ang = pos * freqs[ii % 16]
        cos[p] = np.cos(ang)
        if ii < 16:
            sins[p] = -np.sin(ang)
            partner = p + 16
        else:
            sins[p] = np.sin(ang)
            partner = p - 16
        pm[partner, p] = 1.0
    return cos, sins, pm


def prep_shared(inp):
    sh = {}
    L = 2
    w_in = inp['w_in']
    wn = np.zeros((L, 1024, NCOL), np.float32)
    valid = CMAP >= 0
    wn[:, :, valid] = w_in[:, :, CMAP[valid]]
    sh['w_in'] = np.ascontiguousarray(wn.reshape(L, 8, 128, NCOL).transpose(0, 2, 1, 3))
    bn = np.zeros((L, NCOL), np.float32)
    bn[:, valid] = inp['b_in'][:, CMAP[valid]]
    sh['ada_w'] = np.ascontiguousarray(inp['ada_w'].reshape(L, 8, 128, 3072).transpose(0, 2, 1, 3))
    pv = np.zeros((L, 128, NPV), np.float32)

    def put(l, name, arr):
        o, n = PV[name]
        assert arr.shape == (128, n), (name, arr.shape)
        pv[l, :, o:o + n] = arr
    for l in range(L):
        put(l, 'bin', _fm(bn[l]))
        put(l, 's5d', _fm(inp['s5_d'][l]))
        put(l, 'glub', _fm(inp['s5_glu_b'][l]))
        put(l, 'hglb', np.concatenate([_fm(inp['hg_lb'][ll, d]) for ll in range(2) for d in range(2)], 1))
        put(l, 'hgnw', _fm(inp['hg_norm_w'][l]))
        rd = np.zeros((128, 4), np.float32)
        for d in range(2):
            for j in range(2):
                rd[:64, d * 2 + j] = inp['ret_decay'][l, d, 2 * j]
                rd[64:, d * 2 + j] = inp['ret_decay'][l, d, 2 * j + 1]
        put(l, 'rdec', rd)
        put(l, 'rdech', np.ascontiguousarray(np.broadcast_to(inp['ret_decay'][l].reshape(1, 8), (128, 8))))
        mu = np.zeros((2, 7 * 128), np.float32)
        mu[:, :768] = inp['rw_mu'][l][:, :768]
        lv = CMAP[3584:3712]
        ok = lv >= 0
        mu[:, 768:896][:, ok] = inp['rw_mu'][l][:, lv[ok] - 2816]
        put(l, 'mu', np.concatenate([_fm(mu[0]), _fm(mu[1])], 1))
        put(l, 'w0', np.concatenate([_fm(inp['rw_w0'][l, d]) for d in range(2)], 1))
        put(l, 'a0', np.concatenate([_fm(inp['rw_a0'][l, d]) for d in range(2)], 1))
        for nm, key in [('kk', 'rw_kk'), ('ka', 'rw_ka'), ('rk', 'rw_rk'), ('gnw', 'rw_gn_w'), ('gnb', 'rw_gn_b')]:
            put(l, nm, _fm(inp[key][l]))
        put(l, 'adab', _fm(inp['ada_b'][l][:2048]))
        for nm, key in [('lamre', 's5_lam_re'), ('lamim', 's5_lam_im')]:
            a = inp[key][l].reshape(2, 8, 2, 64)
            put(l, nm, np.ascontiguousarray(a.transpose(2, 3, 0, 1).reshape(128, 16)))
        a = np.broadcast_to(inp['s5_log_dt'][l].reshape(2, 8, 2, 1), (2, 8, 2, 64))
        put(l, 'ldt', np.ascontiguousarray(a.transpose(2, 3, 0, 1).reshape(128, 16)))
    sh['pv'] = pv
    bt = np.zeros((L, 128, 2, 4, 2, 128), np.float32)
    ct = np.zeros((L, 128, 8, 2, 128), np.float32)
    for l in range(L):
        for g in range(16):
            i, g2 = g // 2, g % 2
            for q in range(16):
                c = g * 16 + q
                j, p = c // 128, c % 128
                bt[l, p, j, i % 4, 0, g2 * 64:(g2 + 1) * 64] = inp['s5_b_re'][l, g, :, q]
                bt[l, p, j, i % 4, 1, g2 * 64:(g2 + 1) * 64] = inp['s5_b_im'][l, g, :, q]
            m0 = (i % 4) * 32 + g2 * 16
            ct[l, g2 * 64:(g2 + 1) * 64, i, 0, m0:m0 + 16] = inp['s5_c_re'][l, g].T
            ct[l, g2 * 64:(g2 + 1) * 64, i, 1, m0:m0 + 16] = inp['s5_c_im'][l, g].T
    sh['s5bt'] = bt
    sh['s5ct'] = ct
    sh['gluw'] = np.ascontiguousarray(inp['s5_glu_w'].reshape(L, 2, 128, 256).transpose(0, 2, 1, 3))
    lw2 = np.zeros((L, 128, 2, 256), np.float32)
    for l in range(L):
        lw2[l, 0:16, 0] = inp['rw_w2'][l, 0]
        lw2[l, 32:48, 1] = inp['rw_w2'][l, 1]
        lw2[l, 64:80, 0] = inp['rw_a2'][l, 0]
        lw2[l, 80:96, 1] = inp['rw_a2'][l, 1]
    sh['lw2'] = lw2
    sh['wbr'] = np.ascontiguousarray(inp['w_branch'].reshape(L, 4, 2, 128, 1024).transpose(0, 3, 1, 2, 4))
    sh['wout'] = np.ascontiguousarray(inp['w_out'].reshape(L, 8, 128, 1024).transpose(0, 2, 1, 3))
    rows = np.zeros((L, 128, 4096 + 512), np.float32)
    for l in range(L):
        rows[l, :, 0:1024] = inp['b_out'][l][None]
        rows[l, :, 1024:2048] = inp['ln_w'][l][None]
        rows[l, :, 2048:3072] = inp['ln_b'][l][None]
        rows[l, :, 3072:4096] = inp['ada_b'][l][None, 2048:3072]
        rows[l, :, 4096:4352] = inp['b_in'][l][None, 1280:1536]
        rows[l, :, 4352:4608] = inp['b_in'][l][None, 2304:2560]
    sh['rows'] = rows
    sh['masks'] = _masks()
    cos, sins, pm = _rot_tables()
    sh['rcos'] = cos
    sh['rsin'] = sins
    t = np.arange(128)
    cst = np.zeros((128, 9, 128), np.float32)
    cst[:, 0] = pm
    cst[:, 1] = ((t[:, None] // 64) == (t[None, :] // 64))
    cst[:, 2] = np.maximum(t[None, :] - t[:, None], 0)
    cst[:, 3] = np.maximum(t[:, None] - t[None, :], 0)
    cst[:, 4] = (t[None, :] >= t[:, None])
    cst[:, 5] = (t[None, :] <= t[:, None])
    cst[:, 6, :64] = ((t[:, None] % 64) == np.arange(64)[None, :])
    cst[:, 6, 64:68] = ((t[:, None] // 32) == np.arange(4)[None, :])
    cst[:, 6, 68] = 127 - t
    cst[:, 6, 69] = t
    cst[:, 7] = t[None, :] + 1.0
    cst[:, 8] = 128.0 - t[None, :]
    sh['cst'] = cst
    return sh


def prep_core(inp, b):
    pc = {}
    pc['hin'] = np.ascontiguousarray(np.concatenate([inp['ctx'][b], inp['x'][b]], 0))
    cv = np.stack([inp['c'][b], inp['c_ctx']], -1)
    pc['cvec'] = np.ascontiguousarray(cv.reshape(8, 128, 2).transpose(1, 0, 2))
    return pc


SHAPES = dict(hin=[NT, DM], cvec=[128, 8, 2], w_in=[2, 128, 8, NCOL], ada_w=[2, 128, 8, 3072], pv=[2, 128, NPV],
              s5bt=[2, 128, 2, 4, 2, 128], s5ct=[2, 128, 8, 2, 128], gluw=[2, 128, 2, 256], lw2=[2, 128, 2, 256],
              wbr=[2, 128, 4, 2, 1024], wout=[2, 128, 8, 1024], rows=[2, 128, 4608], masks=[128, 18, 128],
              rcos=[128, 2048], rsin=[128, 2048], cst=[128, 9, 128])


def build(debug=(), nlayers=2, phases=('s5', 'hg', 'ret', 'rw', 'merge'), stop=None):
    nc = bass.Bass("TRN2", target_bir_lowering=False)
    S = Sched(nc)
    dr = {k: nc.dram_tensor(k, list(v), F32, kind="ExternalInput").ap() for k, v in SHAPES.items()}
    out_d = nc.dram_tensor("out", [2048, DM], F32, kind="ExternalOutput").ap()
    h1_d = nc.dram_tensor("h1", [NT, DM], F32, kind="Internal").ap()
    sgd = nc.dram_tensor("sgd", [32, 128, NT], BF16, kind="Internal").ap()
    pre_sg = set()
    dbg_d = {}

    def dbg_out(name, shape):
        dbg_d[name] = nc.dram_tensor("dbg_" + name, list(shape), F32, kind="ExternalOutput").ap()
        return dbg_d[name]

    uid = [0]

    def key(p='k'):
        uid[0] += 1
        return '%s%d' % (p, uid[0])

    with contextlib.ExitStack() as top:
        S.sems = {n: top.enter_context(nc.semaphore(n)) for n in S.sem_names()}

        minrem = {}

        def sb(st, name, shape, dt=F32):
            uid[0] += 1
            t_ = st.enter_context(nc.sbuf_tensor("%s_%d" % (name, uid[0]), list(shape), dt))
            if MEMDBG:
                pre = name[:2]
                minrem[pre] = min(minrem.get(pre, 1 << 30), nc.sbuf_bytes_remaining)
            return t_

        def ps(st, name, shape, dt=F32):
            uid[0] += 1
            return st.enter_context(nc.psum_tensor("%s_%d" % (name, uid[0]), list(shape), dt))

        def mm(out, lhsT, rhs, r, w, start=True, stop=True):
            S.op('pe', lambda e: e.matmul(out, lhsT=lhsT, rhs=rhs, start=start, stop=stop), reads=r, writes=w)

        def tr(out, in_, ident, r, w):
            S.op('pe', lambda e: e.transpose(out, in_, ident), reads=r, writes=w)

        def act(out, in_, func, r, w, bias=0.0, scale=1.0):
            S.op('act', lambda e: e.activation(out=out, in_=in_, func=func, bias=bias, scale=scale), reads=r, writes=w)

        def tt(en, out, in0, in1, op, r, w):
            S.op(en, lambda e: e.tensor_tensor(out=out, in0=in0, in1=in1, op=op), reads=r, writes=w)

        def ts(en, out, in0, s1, s2, op0, op1, r, w):
            if s2 is None:
                S.op(en, lambda e: e.tensor_scalar(out=out, in0=in0, scalar1=s1, scalar2=None, op0=op0), reads=r, writes=w)
            else:
                S.op(en, lambda e: e.tensor_scalar(out=out, in0=in0, scalar1=s1, scalar2=s2, op0=op0, op1=op1),
                     reads=r, writes=w)

        def stt(out, in0, sc, in1, op0, op1, r, w):
            S.op('dve', lambda e: e.scalar_tensor_tensor(out=out, in0=in0, scalar=sc, in1=in1, op0=op0, op1=op1),
                 reads=r, writes=w)

        def cp(en, out, in_, r, w):
            if en == 'act':
                S.op('act', lambda e: e.copy(out=out, in_=in_), reads=r, writes=w)
            else:
                S.op(en, lambda e: e.tensor_copy(out=out, in_=in_), reads=r, writes=w)

        def memset(en, ap, val, w):
            S.op(en, lambda e: e.memset(ap, val), writes=w)

        def run_pipelined(gens, stagger):
            it = iter(gens)
            active, pending, rounds = [], True, 0
            while pending or active:
                if pending and rounds % stagger == 0:
                    try:
                        active.append(next(it))
                    except StopIteration:
                        pending = False
                for g in list(active):
                    try:
                        next(g)
                    except StopIteration:
                        active.remove(g)
                rounds += 1

        def mkbanks(st_, n, prefix):
            bl = [ps(st_, "%s%d" % (prefix, i), [128, 512], F32) for i in range(n)]
            cnt = [0]

            def bank():
                i = cnt[0] % n
                cnt[0] += 1
                return bl[i], '%s%d' % (prefix, i)
            return bank

        def gate_jobs(l, last, st_, bankfn, kds):
            wgt = [sb(st_, "gjw%d" % i, [128, 8, 128], BF16) for i in range(2)]
            sgs = [sb(st_, "gjs%d" % i, [128, 512], BF16) for i in range(2)]
            cnt = [0]

            def job(i, kd):
                k, dt_ = kd // 8, kd % 8
                w_, wk_ = wgt[i % 2], 'gjw%d' % (i % 2)
                c0 = 3968 + k * 1024 + dt_ * 128
                S.dma('pool', w_[:], dr['w_in'][l][:, :, c0:c0 + 128], writes=[wk_])
                yield
                for (n0, nn) in BLOCKS:
                    if last and n0 < 256:
                        continue
                    pg_, pgk_ = bankfn()
                    for jj in range(8):
                        mm(pg_[:, 0:nn], w_[:, jj, :], uT[:, jj, n0:n0 + nn], [wk_] + uTk[n0 // 128:(n0 + nn) // 128], [pgk_],
                           start=(jj == 0), stop=(jj == 7))
                    yield
                    c_ = cnt[0] % 2
                    cnt[0] += 1
                    act(sgs[c_][:, 0:nn], pg_[:, 0:nn], AF.Sigmoid, [pgk_, 'pvt'], ['gjs%d' % c_], bias=pv('bin', 31 + k * 8 + dt_))
                    yield
                    S.dma('sp', sgd[kd][:, n0:n0 + nn], sgs[c_][:, 0:nn], reads=['gjs%d' % c_], writes=['sgd'])
                    yield
                pre_sg.add((l, kd))
            return [job(i, kd) for i, kd in enumerate(kds)]

        def interleave(main, extra, every):
            out, ei = [], 0
            extra = list(extra)
            for i, g in enumerate(main):
                out.append(g)
                if (i + 1) % every == 0 and ei < len(extra):
                    out.append(extra[ei])
                    ei += 1
            out.extend(extra[ei:])
            return out

        def dbg_dump(name, ap, shape, r):
            if name in debug:
                d = dbg_out(name, shape)
                S.dma('sp', d, ap, reads=r, writes=['dbgout_' + name])

        cstb = sb(top, "cstb", [128, 3, 128], BF16)
        cstf = sb(top, "cstf", [128, 7, 128], F32)
        maskb = sb(top, "maskb", [128, 18, 128], BF16)
        silc = sb(top, "silc", [128, 8, 2], F32)
        S.dma('pool', cstb[:, 0:2, :], dr['cst'][:, 0:2, :], writes=['cstb'])
        S.dma('sp', cstf[:], dr['cst'][:, 2:9, :], writes=['cstf'])
        S.dma('pool', maskb[:], dr['masks'], writes=['maskb'])
        S.dma('sp', silc[:], dr['cvec'], writes=['silc'])
        memset('pool', cstb[:, 2, :], 0.0, ['cstb'])
        S.op('pool', lambda e: e.affine_select(out=cstb[:, 2, :], in_=cstb[:, 2, :], pattern=[[-1, 128]],
                                               compare_op=ALU.not_equal, fill=1.0, base=0, channel_multiplier=1),
             reads=['cstb'], writes=['cstb'])
        act(silc[:], silc[:], AF.Silu, ['silc'], ['silc'])
        identb = cstb[:, 2, :]
        bonesb = cstb[:, 1, :]

        uT = sb(top, "uT", [128, 8, NT], BF16)
        Y = sb(top, "Y", [128, 4, 2, NT], BF16)
        pvt = sb(top, "pvt", [128, NPV], F32)
        if debug:
            memset('pool', Y[:], 0.0, ['Y0', 'Y1', 'Y2', 'Y3'])
        modfm = sb(top, "modfm", [128, 16, 2], F32)
        gatebc = sb(top, "gatebc", [128, 2, DM], F32)

        def pv(name, j=None, n=1):
            o, cnt = PV[name]
            if j is None:
                return pvt[:, o:o + cnt]
            return pvt[:, o + j:o + j + n]

        PHASES = {}
        def proj_fm(st, wt, wk, mlist, evac, pp, ppk):
            cnt = 0
            for (n0, nn) in BLOCKS:
                for mi, m in enumerate(mlist):
                    p_, pk_ = pp[cnt % len(pp)], ppk[cnt % len(pp)]
                    cnt += 1
                    for j in range(8):
                        mm(p_[:, 0:nn], wt[:, j, m * 128:(m + 1) * 128], uT[:, j, n0:n0 + nn],
                           ['%s%d' % (wk, m // 2)] + uTk[n0 // 128:(n0 + nn) // 128], [pk_], start=(j == 0), stop=(j == 7))
                    evac(mi, m, n0, nn, p_, pk_)

        def phase_s5(l, h_src, last):
            L = 128
            with contextlib.ExitStack() as st:
                btb = sb(st, "btb", [128, 2, 4, 2, 128], BF16)
                ctb = sb(st, "ctb", [128, 8, 2, 128], BF16)
                glub = sb(st, "glub", [128, 2, 256], BF16)
                S.dma('pool', btb[:], dr['s5bt'][l], writes=['btb'])
                S.dma('pool', ctb[:], dr['s5ct'][l], writes=['ctb'])
                S.dma('pool', glub[:], dr['gluw'][l], writes=['glub'])
                ts('pool', ctb[:, :, 1, :], ctb[:, :, 1, :], -1.0, 0.0, ALU.mult, ALU.add, ['ctb'], ['ctb'])
                ub = sb(st, "s5u", [128, 2, NT], BF16)
                zs = sb(st, "s5z", [128, 2, NT], BF16)
                yacc = sb(st, "yacc", [128, 2, NT], F32)
                PT = sb(st, "s5PT", [128, 16, 2, L], F32)
                QT = sb(st, "s5QT", [128, 16, 2, L], F32)
                sst = sb(st, "s5st", [128, 16, 2], F32)
                ones = sb(st, "s5ones", [128, L], F32)
                memset('pool', yacc[:], 0.0, ['yacc'])
                memset('pool', sst[:], 0.0, ['sst'])
                memset('pool', ones[:], 1.0, ['s5ones'])
                with contextlib.ExitStack() as st2:
                    wsu = sb(st2, "wsu", [128, 8, 512], BF16)
                    for pc_ in range(2):
                        S.dma('pool', wsu[:, :, pc_ * 256:(pc_ + 1) * 256], dr['w_in'][l][:, :, pc_ * 256:(pc_ + 1) * 256], writes=['wsu%d' % pc_])
                    pp = [ps(st2, "s5pp%d" % i, [128, 512], F32) for i in range(2)]

                    def evac(mi, m, n0, nn, p_, pk_):
                        if m < 2:
                            act(ub[:, m, n0:n0 + nn], p_[:, 0:nn], AF.Identity, [pk_, 'pvt'], ['s5u'], bias=pv('bin', m))
                        else:
                            act(zs[:, m - 2, n0:n0 + nn], p_[:, 0:nn], AF.Silu, [pk_, 'pvt'], ['s5z'], bias=pv('bin', m))
                    proj_fm(st2, wsu, 'wsu', [0, 1, 2, 3], evac, pp, ['s5pp0', 's5pp1'])
                    sm = sb(st2, "s5sm", [128, 20, 16], F32)
                    K_ = 's5sm'

                    def Sm(i):
                        return sm[:, i, :]

                    def T2(o, a, b, op):
                        tt('dve', Sm(o), a if not isinstance(a, int) else Sm(a), b if not isinstance(b, int) else Sm(b), op,
                           [K_, 'pvt'], [K_])
                    lamre, lamim = pv('lamre'), pv('lamim')
                    act(Sm(0), pv('ldt'), AF.Exp, ['pvt'], [K_])
                    T2(1, lamre, 0, ALU.mult)
                    act(Sm(2), Sm(1), AF.Exp, [K_], [K_])
                    act(Sm(3), Sm(1), AF.Exp, [K_], [K_], scale=-1.0)
                    T2(4, lamim, 0, ALU.mult)
                    ts('dve', Sm(5), Sm(4), PI / 2, None, ALU.add, None, [K_], [K_])
                    for x in (4, 5):
                        for _ in range(4):
                            ts('dve', Sm(16), Sm(x), PI, 2 * PI, ALU.is_gt, ALU.mult, [K_], [K_])
                            T2(x, x, 16, ALU.subtract)
                        for _ in range(2):
                            ts('dve', Sm(16), Sm(x), -PI, 2 * PI, ALU.is_lt, ALU.mult, [K_], [K_])
                            T2(x, x, 16, ALU.add)
                    act(Sm(6), Sm(4), AF.Sin, [K_], [K_])
                    act(Sm(7), Sm(5), AF.Sin, [K_], [K_])
                    T2(8, 2, 7, ALU.mult)
                    T2(9, 2, 6, ALU.mult)
                    T2(10, 3, 7, ALU.mult)
                    stt(Sm(11), Sm(3), -1.0, Sm(6), ALU.mult, ALU.mult, [K_], [K_])
                    ts('dve', Sm(12), Sm(8), -1.0, None, ALU.add, None, [K_], [K_])
                    T2(16, lamre, lamre, ALU.mult)
                    T2(17, lamim, lamim, ALU.mult)
                    T2(13, 16, 17, ALU.add)
                    S.op('dve', lambda e: e.reciprocal(out=Sm(13), in_=Sm(13)), reads=[K_], writes=[K_])
                    T2(16, 12, lamre, ALU.mult)
                    T2(17, 9, lamim, ALU.mult)
                    T2(16, 16, 17, ALU.add)
                    T2(14, 16, 13, ALU.mult)
                    T2(16, 9, lamre, ALU.mult)
                    T2(17, 12, lamim, ALU.mult)
                    T2(16, 16, 17, ALU.subtract)
                    T2(15, 16, 13, ALU.mult)
                    tmpa = sb(st2, "s5ta", [128, 16, L], F32)
                    tmpb = sb(st2, "s5tb", [128, 16, L], F32)

                    def cmul_bc(dst_re, dst_im, src_re, src_im, s_re, s_im, m):
                        sr = s_re.unsqueeze(2).broadcast_to([128, 16, m])
                        si = s_im.unsqueeze(2).broadcast_to([128, 16, m])
                        ta, tb = tmpa[:, :, 0:m], tmpb[:, :, 0:m]
                        kk_ = ['s5tab', 's5ta', 's5tb', 's5tc', K_]
                        tt('dve', ta, src_re, sr, ALU.mult, kk_, ['s5ta'])
                        tt('dve', tb, src_im, si, ALU.mult, kk_, ['s5tb'])
                        tt('dve', dst_re, ta, tb, ALU.subtract, kk_, ['s5tab'])
                        tt('dve', ta, src_re, si, ALU.mult, kk_, ['s5ta'])
                        tt('dve', tb, src_im, sr, ALU.mult, kk_, ['s5tb'])
                        tt('dve', dst_im, ta, tb, ALU.add, kk_, ['s5tab'])
                    for (TB, a_re, a_im) in ((PT, 8, 9), (QT, 10, 11)):
                        cp('dve', TB[:, :, 0, 0], Sm(a_re), [K_], ['s5tab'])
                        cp('dve', TB[:, :, 1, 0], Sm(a_im), [K_], ['s5tab'])
                        m = 1
                        while m < L:
                            cmul_bc(TB[:, :, 0, m:2 * m], TB[:, :, 1, m:2 * m], TB[:, :, 0, 0:m], TB[:, :, 1, 0:m],
                                    TB[:, :, 0, m - 1], TB[:, :, 1, m - 1], m)
                            m *= 2
                    tmpc = sb(st2, "s5tc", [128, 16, L], F32)
                    cp('dve', tmpc[:], QT[:, :, 0, :], ['s5tab'], ['s5tc'])
                    cmul_bc(QT[:, :, 0, :], QT[:, :, 1, :], tmpc[:], QT[:, :, 1, :], Sm(14), Sm(15), L)
                    S.barrier()
                with contextlib.ExitStack() as st2:
                    NB = 8
                    xa = [sb(st2, "s5xa%d" % i, [128, 2, L], F32) for i in range(NB)]
                    xb_ = [sb(st2, "s5xb%d" % i, [128, 2, L], F32) for i in range(NB)]
                    cw = [sb(st2, "s5cw%d" % i, [128, 2, L], F32) for i in range(NB)]
                    hb = [sb(st2, "s5hb%d" % i, [128, 2, L], BF16) for i in range(NB)]
                    pbu = [ps(st2, "s5pb%d" % i, [128, 2, 2, L], F32) for i in range(4)]
                    py = [ps(st2, "s5py%d" % i, [128, 512], F32) for i in range(2)]
                    orders = [list(range(NTL)), [1, 0] + list(range(NTL - 1, 1, -1))]
                    def s5group(gi, step, d, j):
                        c = orders[d][step]
                        n0 = c * L
                        rev = (d == 1)
                        U = []
                        for ii in range(4):
                            un = gi * 4 + ii
                            bnk = (un // 2) % 4
                            U.append(dict(ii=ii, i=j * 4 + ii, q=d * 8 + j * 4 + ii, pb=pbu[bnk][:, un % 2], pbk='s5pb%d' % bnk,
                                          A=xa[un % NB], Ak='s5xa%d' % (un % NB), B=xb_[un % NB], Bk='s5xb%d' % (un % NB),
                                          C=cw[un % NB], Ck='s5cw%d' % (un % NB), H=hb[un % NB], Hk='s5hb%d' % (un % NB)))
                        for u in U:
                            for ri in range(2):
                                mm(u['pb'][:, ri, :], btb[:, j, u['ii'], ri, :], ub[:, j, n0:n0 + L], ['btb', 's5u'], [u['pbk']])
                        yield
                        for u in U:
                            src = u['pb'][:, :, ::-1] if rev else u['pb'][:, :, :]
                            tt('dve', u['A'][:], src, QT[:, u['q'], 0:1, :].broadcast_to([128, 2, L]), ALU.mult,
                               [u['pbk'], 's5tab'], [u['Ak']])
                        yield
                        for u in U:
                            src = u['pb'][:, ::-1, ::-1] if rev else u['pb'][:, ::-1, :]
                            tt('dve', u['B'][:], src, QT[:, u['q'], 1:2, :].broadcast_to([128, 2, L]), ALU.mult,
                               [u['pbk'], 's5tab'], [u['Bk']])
                        yield
                        for u in U:
                            tt('dve', u['A'][:, 0, :], u['A'][:, 0, :], u['B'][:, 0, :], ALU.subtract, [u['Ak'], u['Bk']], [u['Ak']])
                        yield
                        for u in U:
                            tt('dve', u['A'][:, 1, :], u['A'][:, 1, :], u['B'][:, 1, :], ALU.add, [u['Ak'], u['Bk']], [u['Ak']])
                        yield
                        for ri in range(2):
                            for u in U:
                                q = u['q']
                                S.op('dve', lambda e, u=u, ri=ri, q=q: e.tensor_tensor_scan(
                                    out=u['C'][:, ri, :], data0=ones[:], data1=u['A'][:, ri, :], initial=sst[:, q, ri:ri + 1],
                                    op0=ALU.mult, op1=ALU.add), reads=[u['Ak'], 's5ones', 'sst%d' % q, 'sst'], writes=[u['Ck']])
                            yield
                        for u in U:
                            tt('pool', u['A'][:], u['C'][:], PT[:, u['q'], 0:1, :].broadcast_to([128, 2, L]), ALU.mult,
                               [u['Ck'], 's5tab', u['Ak']], [u['Ak']])
                        yield
                        for u in U:
                            tt('pool', u['B'][:], u['C'][:, ::-1, :], PT[:, u['q'], 1:2, :].broadcast_to([128, 2, L]), ALU.mult,
                               [u['Ck'], 's5tab', u['Bk']], [u['Bk']])
                        yield
                        for u in U:
                            tt('pool', u['A'][:, 0, :], u['A'][:, 0, :], u['B'][:, 0, :], ALU.subtract, [u['Ak'], u['Bk']], [u['Ak']])
                        yield
                        for u in U:
                            tt('pool', u['A'][:, 1, :], u['A'][:, 1, :], u['B'][:, 1, :], ALU.add, [u['Ak'], u['Bk']], [u['Ak']])
                        yield
                        for u in U:
                            cp('pool', sst[:, u['q'], :], u['A'][:, :, L - 1], [u['Ak']], ['sst%d' % u['q']])
                        yield
                        for u in U:
                            hsrc = u['A'][:, :, ::-1] if rev else u['A'][:]
                            cp('act', u['H'][:], hsrc, [u['Ak']], [u['Hk']])
                        yield
                        pyr = py[gi % 2][:, 0:L]
                        pyk = 's5py%d' % (gi % 2)
                        for k_, u in enumerate(U):
                            for ri in range(2):
                                mm(pyr, ctb[:, u['i'], ri, :], u['H'][:, ri, :], ['ctb', u['Hk']], [pyk],
                                   start=(k_ == 0 and ri == 0), stop=(k_ == 3 and ri == 1))
                        yield
                        yield
                        yield
                        tt('dve', yacc[:, j, n0:n0 + L], yacc[:, j, n0:n0 + L], pyr, ALU.add, [pyk, 'yacc'], ['yacc'])

                    glist = [(step, d, j) for step in range(NTL) for d in range(2) for j in range(2)]
                    gbank = mkbanks(st2, 2, "s5gk") if (GATE_PRE and GJ_S5) else None
                    gj = gate_jobs(l, last, st2, gbank, GJ_S5) if (GATE_PRE and GJ_S5) else []
                    run_pipelined(interleave([s5group(gi, *g) for gi, g in enumerate(glist)], gj, 2), STG['s5'])
                    S.barrier()
                for j in range(2):
                    stt(yacc[:, j, :], ub[:, j, :], pv('s5d', j), yacc[:, j, :], ALU.mult, ALU.add, ['s5u', 'yacc', 'pvt'],
                        ['yacc'])
                dbg_dump('ya%d' % l, yacc[:], [128, 2, NT], ['yacc'])
                with contextlib.ExitStack() as st2:
                    t1 = [sb(st2, "s5g1_%d" % i, [128, 512], F32) for i in range(2)]
                    t2 = [sb(st2, "s5g2_%d" % i, [128, 512], BF16) for i in range(2)]
                    pg = [ps(st2, "s5pg%d" % i, [128, 512], F32) for i in range(2)]
                    cnt = 0
                    for (n0, nn) in BLOCKS:
                        for j in range(2):
                            a, ak = t1[cnt % 2], 's5g1_%d' % (cnt % 2)
                            cnt += 1
                            ysl = yacc[:, j, n0:n0 + nn]
                            act(a[:, 0:nn], ysl, AF.Square, ['yacc'], [ak])
                            ts('dve', a[:, 0:nn], a[:, 0:nn], 0.044715, 1.0, ALU.mult, ALU.add, [ak], [ak])
                            tt('dve', a[:, 0:nn], a[:, 0:nn], ysl, ALU.mult, [ak, 'yacc'], [ak])
                            act(a[:, 0:nn], a[:, 0:nn], AF.Sigmoid, [ak], [ak], scale=1.5957691216057308)
                            tt('dve', ub[:, j, n0:n0 + nn], a[:, 0:nn], ysl, ALU.mult, [ak, 'yacc'], ['s5u'])
                    cnt = 0
                    for (n0, nn) in BLOCKS:
                        for m in range(2):
                            p_, pk_ = pg[cnt % 2], 's5pg%d' % (cnt % 2)
                            b_, bk_ = t2[cnt % 2], 's5g2_%d' % (cnt % 2)
                            cnt += 1
                            for jc in range(2):
                                mm(p_[:, 0:nn], glub[:, jc, m * 128:(m + 1) * 128], ub[:, jc, n0:n0 + nn], ['glub', 's5u'], [pk_],
                                   start=(jc == 0), stop=(jc == 1))
                            act(b_[:, 0:nn], p_[:, 0:nn], AF.Sigmoid, [pk_, 'pvt'], [bk_], bias=pv('glub', m))
                            tt('dve', b_[:, 0:nn], b_[:, 0:nn], ub[:, m, n0:n0 + nn], ALU.mult, [bk_, 's5u'], [bk_])
                            tt('pool', Y[:, 0, m, n0:n0 + nn], b_[:, 0:nn], zs[:, m, n0:n0 + nn], ALU.mult, [bk_, 's5z'], ['Y0'])
                    S.barrier()
                S.barrier()
        PHASES['s5'] = phase_s5
        def phase_hg(l, h_src, last):
            with contextlib.ExitStack() as st:
                QP = [sb(st, "hgQP%d" % d, [128, 2, NT], BF16) for d in range(2)]
                KP = [sb(st, "hgKP%d" % d, [128, 2, NT], BF16) for d in range(2)]
                G = sb(st, "hgG", [128, 2, 72, 2], F32)
                VT = sb(st, "hgVT", [128, NTL, 256], BF16)
                zs = sb(st, "hgzs", [128, 2, NT], BF16)
                lbt = sb(st, "hglbt", [128, 2, 4], F32)
                if l == 0:
                    memset('pool', lbt[:, 0, :], 0.0, ['hglbt'])
                    memset('pool', lbt[:, 1, :], 1.0, ['hglbt'])
                else:
                    o_, _ = PV['hglb']
                    tt('dve', lbt[:, 0, :], pvt[:, o_ + 4:o_ + 8], pvt[:, o_:o_ + 4], ALU.subtract, ['pvt'], ['hglbt'])
                    act(lbt[:, 0, :], lbt[:, 0, :], AF.Sigmoid, ['hglbt'], ['hglbt'])
                    ts('dve', lbt[:, 1, :], lbt[:, 0, :], -1.0, 1.0, ALU.mult, ALU.add, ['hglbt'], ['hglbt'])
                with contextlib.ExitStack() as st2:
                    wh = sb(st2, "hgw", [128, 8, 1280], BF16)
                    for pc_ in (0, 4, 1, 2, 3):
                        S.dma('pool', wh[:, :, pc_ * 256:(pc_ + 1) * 256], dr['w_in'][l][:, :, 512 + pc_ * 256:512 + (pc_ + 1) * 256], writes=['hgw%d' % pc_])
                    brow = sb(st2, "hgbrow", [128, 256], F32)
                    S.dma('sp', brow[:], dr['rows'][l][:, 4096:4352], writes=['hgbrow'])
                    R32 = sb(st2, "hgR32", [128, 512], F32)
                    memset('pool', R32[:], 1.0, ['hgR32'])
                    memset('pool', R32[:, 0:512:32], 0.0, ['hgR32'])
                    QS = [sb(st2, "hgQS%d" % i, [128, 2, 512], BF16) for i in range(2)]
                    T = [[sb(st2, "hgT%d_%d" % (i, k), [128, 512], F32) for k in range(4)] for i in range(2)]
                    pp = [ps(st2, "hgpp%d" % i, [128, 512], F32) for i in range(3)]
                    pt = [ps(st2, "hgpt%d" % i, [128, 512], F32) for i in range(2)]
                    def hgproj(cnt, ic, m, n0, nn):
                        ukeys = uTk[n0 // 128:(n0 + nn) // 128]
                        p_, pk_ = pp[cnt % 3], 'hgpp%d' % (cnt % 3)
                        bi = (n0 // 512) % 2 if n0 else 0
                        for jj in range(8):
                            mm(p_[:, 0:nn], wh[:, jj, m * 128:(m + 1) * 128], uT[:, jj, n0:n0 + nn], ['hgw%d' % (m // 2)] + ukeys, [pk_],
                               start=(jj == 0), stop=(jj == 7))
                        yield
                        bias = pv('bin', 4 + m)
                        if m < 2:
                            act(QS[bi][:, m, 0:nn], p_[:, 0:nn], AF.Silu, [pk_, 'pvt'], ['hgQS%d' % bi], bias=bias)
                            return
                        if m >= 8:
                            act(zs[:, m - 8, n0:n0 + nn], p_[:, 0:nn], AF.Silu, [pk_, 'pvt'], ['hgzs'], bias=bias)
                            return
                        d, j = (m - 2) // 2, (m - 2) % 2
                        Ts = T[ic % 2]
                        Tk = ['hgT%d_%d' % (ic % 2, k) for k in range(4)]
                        t1, t2, t3, t4 = [x[:, 0:nn] for x in Ts]
                        act(t1, p_[:, 0:nn], AF.Sigmoid, [pk_, 'pvt'], [Tk[0]], bias=bias)
                        yield
                        ts('dve', t1, t1, lbt[:, 1, d * 2 + j:d * 2 + j + 1], lbt[:, 0, d * 2 + j:d * 2 + j + 1], ALU.mult, ALU.add,
                           [Tk[0], 'hglbt'], [Tk[0]])
                        yield
                        act(t2, t1, AF.Ln, [Tk[0]], [Tk[1]])
                        yield
                        if d == 0:
                            S.op('dve', lambda e: e.tensor_tensor_scan(out=t3, data0=R32[:, 0:nn], data1=t2, initial=0.0,
                                                                       op0=ALU.mult, op1=ALU.add),
                                 reads=[Tk[1], 'hgR32'], writes=[Tk[2]])
                        else:
                            S.op('dve', lambda e: e.tensor_tensor_scan(out=t3[:, ::-1],
                                                                       data0=R32[:, 0:nn], data1=t2[:, ::-1], initial=0.0,
                                                                       op0=ALU.mult, op1=ALU.add),
                                 reads=[Tk[1], 'hgR32'], writes=[Tk[2]])
                        yield
                        ts('dve', t3, t3, -80.0, None, ALU.max, None, [Tk[2]], [Tk[2]])
                        ts('dve', t1, t1, -1.0, 1.0, ALU.mult, ALU.add, [Tk[0]], [Tk[0]])
                        yield
                        act(t4, t3, AF.Exp, [Tk[2]], [Tk[3]])
                        act(t2, t3, AF.Exp, [Tk[2]], [Tk[1]], scale=-1.0)
                        yield
                        tt('pool', KP[d][:, j, n0:n0 + nn], t1, t2, ALU.mult, [Tk[0], Tk[1]], ['hgKP%d' % d])
                        tt('pool', QP[d][:, j, n0:n0 + nn], QS[bi][:, j, 0:nn], t4, ALU.mult, ['hgQS%d' % bi, Tk[3]], ['hgQP%d' % d])
                        c0 = n0 // 32
                        gsrc = t4[:, 31::32] if d == 0 else t4[:, 0::32]
                        cp('act', G[:, d, c0:c0 + nn // 32, j], gsrc, [Tk[3]], ['hgG'])

                    plist = []
                    cnt = 0
                    ic = 0
                    for (n0, nn) in BLOCKS:
                        for m in (0, 1, 8, 9, 2, 3, 4, 5):
                            plist.append((cnt, ic, m, n0, nn))
                            cnt += 1
                            if 2 <= m < 8:
                                ic += 1
                    run_pipelined((hgproj(*p) for p in plist), STG['hgproj'])
                    for t in range(NTL):
                        p_, pk_ = pt[t % 2], 'hgpt%d' % (t % 2)
                        for jj in range(8):
                            mm(p_[:, 0:256], uT[:, jj, t * 128:(t + 1) * 128], wh[:, jj, 768:1024], ['hgw3', uTk[t]], [pk_],
                               start=(jj == 0), stop=(jj == 7))
                        tt('dve', VT[:, t, :], p_[:, 0:256], brow[:], ALU.add, [pk_, 'hgbrow'], ['hgVT'])
                    S.barrier()
                Sall = [sb(st, "hgSall%d" % d, [128, 2, 72, 64], BF16) for d in range(2)]
                with contextlib.ExitStack() as st2:
                    Sst = [sb(st2, "hgS%d" % d, [128, 2, 64], F32) for d in range(2)]
                    kTm = [sb(st2, "hgkTm%d" % i, [128, 4, 256], BF16) for i in range(3)]
                    Ug = [sb(st2, "hgUg%d" % i, [128, 4, 2, 64], F32) for i in range(3)]
                    ptr = [ps(st2, "hgptr%d" % i, [128, 8, 128], BF16) for i in range(2)]
                    pU = [ps(st2, "hgpU%d" % i, [128, 4, 2, 64], F32) for i in range(3)]
                    orders = [list(range(NTL)), [1, 0] + list(range(NTL - 1, 1, -1))]
                    for d in range(2):
                        memset('pool', Sst[d][:], 0.0, ['hgS%d' % d])
                    def hgchain(it, step, d):
                        t = orders[d][step]
                        pr, prk = ptr[it % 2], 'hgptr%d' % (it % 2)
                        km, kmk = kTm[it % 3], 'hgkTm%d' % (it % 3)
                        pu, puk = pU[it % 3], 'hgpU%d' % (it % 3)
                        ug, ugk = Ug[it % 3], 'hgUg%d' % (it % 3)
                        for j in range(2):
                            tr(pr[:, j, :], KP[d][:, j, t * 128:(t + 1) * 128], identb, ['hgKP%d' % d, 'cstb'], [prk])
                        yield
                        for cc in range(4):
                            prf = pr[:, 0:2, :].rearrange("p a b -> p (a b)")
                            if cc % 2 == 0:
                                ts('dve', km[:, cc, :], prf, cstf[:, 4, 64 + cc:64 + cc + 1], None, ALU.mult, None, [prk, 'cstf'], [kmk])
                            else:
                                act(km[:, cc, :], prf, AF.Identity, [prk, 'cstf'], [kmk], scale=cstf[:, 4, 64 + cc:64 + cc + 1])
                        yield
                        for cc in range(4):
                            for h in range(4):
                                hp = (h % 2) * 64
                                mm(pu[hp:hp + 64, cc, h // 2, :], km[:, cc, h * 64:(h + 1) * 64], VT[:, t, h * 64:(h + 1) * 64],
                                   [kmk, 'hgVT'], [puk])
                        yield
                        tt('dve', ug[:], pu[:], G[:, d, t * 4:(t + 1) * 4, :].unsqueeze(3).broadcast_to([128, 4, 2, 64]), ALU.mult,
                           [puk, 'hgG'], [ugk])
                        yield
                        ccs = range(4) if d == 0 else range(3, -1, -1)
                        for cc in ccs:
                            c = t * 4 + cc
                            cp('act', Sall[d][:, :, c, :], Sst[d][:], ['hgS%d' % d], ['hgSall%d_%d' % (d, t)])
                            for j in range(2):
                                stt(Sst[d][:, j, :], Sst[d][:, j, :], G[:, d, c, j:j + 1], ug[:, cc, j, :], ALU.mult, ALU.add,
                                    ['hgS%d' % d, 'hgG', ugk], ['hgS%d' % d])
                            yield

                    gbank = mkbanks(st2, 3, "hggk") if GJ_SPLIT[0] else None
                    gj = gate_jobs(l, last, st2, gbank, GJ_SPLIT[0]) if (GATE_PRE and GJ_SPLIT[0]) else []
                    run_pipelined(interleave([hgchain(i_, sd[0], sd[1]) for i_, sd in enumerate([(s_, d_) for s_ in range(NTL) for d_ in range(2)])], gj, 3), STG['hgchain'])
                    S.barrier()
                with contextlib.ExitStack() as st2:
                    if ('yb%d' % l) in debug:
                        dbgbuf = sb(st2, "dbgbuf", [128, 2, NT], F32)
                    AT = [[sb(st2, "hgAT%d_%d" % (i, d), [128, 4, 128], BF16) for d in range(2)] for i in range(3)]
                    sq = [sb(st2, "hgsq%d" % i, [128, 2, 128], BF16) for i in range(3)]
                    rr = [sb(st2, "hgrr%d" % i, [128, 2, 128], F32) for i in range(3)]
                    ob = [sb(st2, "hgob%d" % i, [128, 2, 128], F32) for i in range(3)]
                    bank = mkbanks(st2, 8, "hgbk")

                    def hgout(t):
                        i2 = t % 3
                        tsl = slice(t * 128, (t + 1) * 128)
                        pas = {}
                        for d in range(2):
                            for par in range(2):
                                pas[(d, par)] = bank()
                            for h in range(4):
                                hp = (h % 2) * 64
                                pa, pak = pas[(d, h % 2)]
                                pav = pa[:, 0:256].rearrange("p (a b) -> p a b", a=2)
                                mm(pav[:, h // 2, :], KP[d][hp:hp + 64, h // 2, tsl], QP[d][hp:hp + 64, h // 2, tsl],
                                   ['hgKP%d' % d, 'hgQP%d' % d], [pak])
                        yield
                        for d in range(2):
                            for par in range(2):
                                pa, pak = pas[(d, par)]
                                pav = pa[:, 0:256].rearrange("p (a b) -> p a b", a=2)
                                tt('dve', AT[i2][d][:, par::2, :], pav, maskb[:, d, :].unsqueeze(1).broadcast_to([128, 2, 128]), ALU.mult,
                                   [pak, 'maskb'], ['hgAT%d_%d' % (i2, d)])
                        yield
                        pos = [bank() for _ in range(2)]
                        povs = [pos[par][0][:, 0:256].rearrange("p (a b) -> p a b", a=2) for par in range(2)]
                        for h in range(4):
                            hp = (h % 2) * 64
                            pok = pos[h % 2][1]
                            reg = povs[h % 2][hp:hp + 64, h // 2, :]
                            first = True
                            for d in range(2):
                                mm(reg, VT[:, t, h * 64:(h + 1) * 64], AT[i2][d][:, h, :], ['hgVT', 'hgAT%d_%d' % (i2, d)], [pok],
                                   start=first, stop=False)
                                first = False
                                for cc in range(4):
                                    c = t * 4 + cc
                                    mm(reg[:, cc * 32:(cc + 1) * 32], Sall[d][hp:hp + 64, h // 2, c, :],
                                       QP[d][hp:hp + 64, h // 2, t * 128 + cc * 32:t * 128 + (cc + 1) * 32],
                                       ['hgSall%d_%d' % (d, t), 'hgQP%d' % d], [pok], start=False, stop=(d == 1 and cc == 3))
                        yield
                        obk = 'hgob%d' % i2
                        cp('act', ob[i2][0:64], povs[0][0:64], [pos[0][1]], [obk])
                        cp('dve', ob[i2][64:128], povs[1][64:128], [pos[1][1]], [obk])
                        yield
                        pov = ob[i2][:]
                        pok = obk
                        if ('yb%d' % l) in debug:
                            cp('pool', dbgbuf[:, :, tsl], pov, [pok], ['dbgbuf'])
                        act(sq[i2][:], pov, AF.Square, [pok], ['hgsq%d' % i2])
                        yield
                        pss_, psk = bank()
                        psv = pss_[:, 0:256].rearrange("p (a b) -> p a b", a=2)
                        for j in range(2):
                            mm(psv[:, j, :], bonesb, sq[i2][:, j, :], ['cstb', 'hgsq%d' % i2], [psk])
                        yield
                        act(rr[i2][:], psv, AF.Sqrt, [psk], ['hgrr%d' % i2], bias=RMS_EPS, scale=1.0 / 64)
                        yield
                        S.op('dve', lambda e: e.reciprocal(out=rr[i2][:], in_=rr[i2][:]), reads=['hgrr%d' % i2], writes=['hgrr%d' % i2])
                        yield
                        tt('dve', rr[i2][:], pov, rr[i2][:], ALU.mult, [pok, 'hgrr%d' % i2], ['hgrr%d' % i2])
                        yield
                        for j in range(2):
                            stt(Y[:, 1, j, tsl], rr[i2][:, j, :], pv('hgnw', j), zs[:, j, tsl], ALU.mult, ALU.mult,
                                ['hgrr%d' % i2, 'pvt', 'hgzs'], ['Y1'])

                    gj = gate_jobs(l, last, st2, bank, GJ_SPLIT[1]) if (GATE_PRE and GJ_SPLIT[1]) else []
                    run_pipelined(interleave([hgout(t) for t in range(NTL) if not (last and t < 2 and not debug)], gj, 3), STG['out'])
                    if ('yb%d' % l) in debug:
                        dbg_dump('yb%d' % l, dbgbuf[:], [128, 2, NT], ['dbgbuf'])
                    S.barrier()
                S.barrier()
        PHASES['hg'] = phase_hg
        def phase_ret(l, h_src, last):
            with contextlib.ExitStack() as st:
                QR = sb(st, "rtQR", [128, 2, NT], BF16)
                KR = sb(st, "rtKR", [128, 2, NT], BF16)
                VT = sb(st, "rtVT", [128, NTL, 256], BF16)
                zs = sb(st, "rtzs", [128, 2, NT], BF16)
                Sall = [sb(st, "rtSall%d" % d, [128, 2, NTL, 64], BF16) for d in range(2)]
                LG = sb(st, "rtLG", [128, 4], F32)
                GL = sb(st, "rtGL", [128, 4], F32)
                LGH = sb(st, "rtLGH", [128, 8], F32)
                QDEC = sb(st, "rtQDEC", [128, 2, 2, 128], F32)
                KDEC = sb(st, "rtKDEC", [128, 2, 4], F32)
                DS = sb(st, "rtDS", [128, 4, 128], F32)
                tb8 = sb(st, "rtb8", [128, 2], F32)
                K_ = 'rttab'
                act(LG[:], pv('rdec'), AF.Exp, ['pvt'], [K_])
                ts('dve', LG[:], LG[:], -1.0, None, ALU.mult, None, [K_], [K_])
                act(GL[:], LG[:], AF.Exp, [K_], [K_], scale=128.0)
                act(LGH[:], pv('rdech'), AF.Exp, ['pvt'], [K_])
                ts('dve', LGH[:], LGH[:], -1.0, None, ALU.mult, None, [K_], [K_])
                for d in range(2):
                    for j in range(2):
                        act(QDEC[:, d, j, :], cstf[:, 5 + d, :], AF.Exp, ['cstf', K_], [K_], scale=LG[:, d * 2 + j:d * 2 + j + 1])
                    act(KDEC[:, d, :], LGH[:, d * 4:(d + 1) * 4], AF.Exp, ['cstf', K_], [K_], scale=cstf[:, 4, 68 + d:69 + d])
                with contextlib.ExitStack() as st2:
                    ta = sb(st2, "rtta", [128, 128], F32)
                    tb = sb(st2, "rttb", [128, 128], F32)
                    for h in range(4):
                        act(ta[:], cstf[:, 0, :], AF.Exp, ['cstf', K_], ['rtta'], scale=LGH[:, h:h + 1])
                        tt('dve', ta[:], ta[:], cstf[:, 2, :], ALU.mult, ['rtta', 'cstf'], ['rtta'])
                        act(tb[:], cstf[:, 1, :], AF.Exp, ['cstf', K_], ['rttb'], scale=LGH[:, 4 + h:5 + h])
                        tt('dve', tb[:], tb[:], cstf[:, 3, :], ALU.mult, ['rttb', 'cstf'], ['rttb'])
                        tt('dve', DS[:, h, :], ta[:], tb[:], ALU.add, ['rtta', 'rttb'], [K_])
                    ts('dve', tb8[:], pv('bin', 16, 2), 0.125, None, ALU.mult, None, ['pvt'], [K_])
                    S.barrier()
                if stop == 'ret_tab':
                    return
                with contextlib.ExitStack() as st2:
                    wr = sb(st2, "rtw", [128, 8, 1024], BF16)
                    for pc_ in range(4):
                        S.dma('pool', wr[:, :, pc_ * 256:(pc_ + 1) * 256], dr['w_in'][l][:, :, 1792 + pc_ * 256:1792 + (pc_ + 1) * 256], writes=['rtw%d' % pc_])
                    brow = sb(st2, "rtbrow", [128, 256], F32)
                    S.dma('sp', brow[:], dr['rows'][l][:, 4352:4608], writes=['rtbrow'])
                    COS = sb(st2, "rtcos", [128, 2048], F32)
                    SIN = sb(st2, "rtsin", [128, 2048], F32)
                    permf = sb(st2, "rtperm", [128, 128], F32)
                    S.dma('sp', COS[:], dr['rcos'], writes=['rtcos'])
                    S.dma('act', SIN[:], dr['rsin'], writes=['rtsin'])
                    S.dma('sp', permf[:], dr['cst'][:, 0, :], writes=['rtperm'])
                    qf = [sb(st2, "rtqf%d" % i, [128, 512], F32) for i in range(2)]
                    t1 = [sb(st2, "rtt1_%d" % i, [128, 512], F32) for i in range(2)]
                    pp = [ps(st2, "rtpp%d" % i, [128, 512], F32) for i in range(2)]
                    pq = [ps(st2, "rtpq%d" % i, [128, 512], F32) for i in range(2)]
                    pt = [ps(st2, "rtpt%d" % i, [128, 512], F32) for i in range(2)]
                    def rtproj(cnt, rc, m, n0, nn):
                        ukeys = uTk[n0 // 128:(n0 + nn) // 128]
                        p_, pk_ = pp[cnt % 2], 'rtpp%d' % (cnt % 2)
                        for jj in range(8):
                            mm(p_[:, 0:nn], wr[:, jj, m * 128:(m + 1) * 128], uT[:, jj, n0:n0 + nn], ['rtw%d' % (m // 2)] + ukeys, [pk_],
                               start=(jj == 0), stop=(jj == 7))
                        yield
                        if m >= 6:
                            act(zs[:, m - 6, n0:n0 + nn], p_[:, 0:nn], AF.Silu, [pk_, 'pvt'], ['rtzs'], bias=pv('bin', 14 + m))
                            return
                        isk = m >= 2
                        j = m % 2
                        dst = (KR if isk else QR)[:, j, n0:n0 + nn]
                        dk = 'rtKR' if isk else 'rtQR'
                        if n0 < 256:
                            if isk:
                                act(dst, p_[:, 0:nn], AF.Identity, [pk_, K_], [dk], bias=tb8[:, j:j + 1], scale=0.125)
                            else:
                                act(dst, p_[:, 0:nn], AF.Identity, [pk_, 'pvt'], [dk], bias=pv('bin', 14 + m))
                            return
                        q_, qk_ = qf[rc % 2], 'rtqf%d' % (rc % 2)
                        a_, ak_ = t1[rc % 2], 'rtt1_%d' % (rc % 2)
                        r_, rk_ = pq[rc % 2], 'rtpq%d' % (rc % 2)
                        if isk:
                            act(q_[:, 0:nn], p_[:, 0:nn], AF.Identity, [pk_, K_], [qk_], bias=tb8[:, j:j + 1], scale=0.125)
                        else:
                            act(q_[:, 0:nn], p_[:, 0:nn], AF.Identity, [pk_, 'pvt'], [qk_], bias=pv('bin', 14 + m))
                        yield
                        mm(r_[:, 0:nn], permf[:], q_[:, 0:nn], ['rtperm', qk_], [rk_])
                        yield
                        tsl = slice(n0 - 256, n0 - 256 + nn)
                        tt('dve', a_[:, 0:nn], r_[:, 0:nn], SIN[:, tsl], ALU.mult, [rk_, 'rtsin'], [ak_])
                        tt('pool', q_[:, 0:nn], q_[:, 0:nn], COS[:, tsl], ALU.mult, [qk_, 'rtcos'], [qk_])
                        yield
                        tt('dve', dst, a_[:, 0:nn], q_[:, 0:nn], ALU.add, [ak_, qk_], [dk])

                    plist = []
                    cnt = 0
                    rc = 0
                    for (n0, nn) in BLOCKS:
                        for m in (0, 1, 2, 3, 6, 7):
                            plist.append((cnt, rc, m, n0, nn))
                            cnt += 1
                            if m < 6 and n0 >= 256:
                                rc += 1
                    run_pipelined((rtproj(*p) for p in plist), 2)
                    for t in range(NTL):
                        p_, pk_ = pt[t % 2], 'rtpt%d' % (t % 2)
                        for jj in range(8):
                            mm(p_[:, 0:256], uT[:, jj, t * 128:(t + 1) * 128], wr[:, jj, 512:768], ['rtw2', uTk[t]], [pk_],
                               start=(jj == 0), stop=(jj == 7))
                        tt('dve', VT[:, t, :], p_[:, 0:256], brow[:], ALU.add, [pk_, 'rtbrow'], ['rtVT'])
                    S.barrier()
                if stop == 'ret_proj':
                    return
                with contextlib.ExitStack() as st2:
                    Sst = [sb(st2, "rtS%d" % d, [128, 2, 64], F32) for d in range(2)]
                    kT = [sb(st2, "rtkT%d" % i, [128, 256], BF16) for i in range(3)]
                    ptr = [ps(st2, "rtptr%d" % i, [128, 8, 128], BF16) for i in range(2)]
                    pU = [ps(st2, "rtpU%d" % i, [128, 512], F32) for i in range(3)]
                    orders = [list(range(NTL)), [1, 0] + list(range(NTL - 1, 1, -1))]
                    for d in range(2):
                        memset('pool', Sst[d][:], 0.0, ['rtS%d' % d])
                    def rtchain(it, step, d):
                        t = orders[d][step]
                        pr, prk = ptr[it % 2], 'rtptr%d' % (it % 2)
                        kt, ktk = kT[it % 3], 'rtkT%d' % (it % 3)
                        pu, puk = pU[it % 3], 'rtpU%d' % (it % 3)
                        puv = pu[:, 0:128].rearrange("p (a b) -> p a b", a=2)
                        for j in range(2):
                            tr(pr[:, j, :], KR[:, j, t * 128:(t + 1) * 128], identb, ['rtKR', 'cstb'], [prk])
                        yield
                        tt('dve', kt[:].rearrange("p (h k) -> p h k", h=4), pr[:, 0:2, :].rearrange("p a (b k) -> p (a b) k", b=2),
                           KDEC[:, d, :].unsqueeze(2).broadcast_to([128, 4, 64]), ALU.mult, [prk, K_], [ktk])
                        yield
                        for h in range(4):
                            hp = (h % 2) * 64
                            mm(puv[hp:hp + 64, h // 2, :], kt[:, h * 64:(h + 1) * 64], VT[:, t, h * 64:(h + 1) * 64], [ktk, 'rtVT'], [puk])
                        yield
                        cp('act', Sall[d][:, :, t, :], Sst[d][:], ['rtS%d' % d], ['rtSall%d_%d' % (d, t)])
                        for j in range(2):
                            stt(Sst[d][:, j, :], Sst[d][:, j, :], GL[:, d * 2 + j:d * 2 + j + 1], puv[:, j, :], ALU.mult, ALU.add,
                                ['rtS%d' % d, K_, puk], ['rtS%d' % d])

                    gbank = mkbanks(st2, 3, "rtgk") if GJ_SPLIT[2] else None
                    gj = gate_jobs(l, last, st2, gbank, GJ_SPLIT[2]) if (GATE_PRE and GJ_SPLIT[2]) else []
                    run_pipelined(interleave([rtchain(i_, sd[0], sd[1]) for i_, sd in enumerate([(s_, d_) for s_ in range(NTL) for d_ in range(2)])], gj, 4), STG['rtchain'])
                    S.barrier()
                if stop == 'ret_chain':
                    return
                with contextlib.ExitStack() as st2:
                    if ('yc%d' % l) in debug:
                        dbgbuf = sb(st2, "dbgbuf", [128, 2, NT], F32)
                    AT = [sb(st2, "rtAT%d" % i, [128, 4, 128], BF16) for i in range(3)]
                    qd = [[sb(st2, "rtqd%d_%d" % (i, d), [128, 2, 128], BF16) for d in range(2)] for i in range(3)]
                    sq = [sb(st2, "rtsq%d" % i, [128, 2, 128], BF16) for i in range(3)]
                    rr = [sb(st2, "rtrr%d" % i, [128, 2, 128], F32) for i in range(3)]
                    ob = [sb(st2, "rtob%d" % i, [128, 2, 128], F32) for i in range(3)]
                    bank = mkbanks(st2, 8, "rtbk")

                    def rtout(t):
                        i2 = t % 3
                        tsl = slice(t * 128, (t + 1) * 128)
                        pas = [bank() for _ in range(2)]
                        for h in range(4):
                            hp = (h % 2) * 64
                            pav = pas[h % 2][0][:, 0:256].rearrange("p (a b) -> p a b", a=2)
                            mm(pav[:, h // 2, :], KR[hp:hp + 64, h // 2, tsl], QR[hp:hp + 64, h // 2, tsl], ['rtKR', 'rtQR'], [pas[h % 2][1]])
                        for d in range(2):
                            tt('pool', qd[i2][d][:], QR[:, :, tsl], QDEC[:, d, :, :], ALU.mult, ['rtQR', K_], ['rtqd%d_%d' % (i2, d)])
                        yield
                        for par in range(2):
                            pav = pas[par][0][:, 0:256].rearrange("p (a b) -> p a b", a=2)
                            tt('dve', AT[i2][:, par::2, :], pav, DS[:, par::2, :], ALU.mult, [pas[par][1], K_], ['rtAT%d' % i2])
                        yield
                        pos = [bank() for _ in range(2)]
                        povs = [pos[par][0][:, 0:256].rearrange("p (a b) -> p a b", a=2) for par in range(2)]
                        for h in range(4):
                            hp = (h % 2) * 64
                            pok = pos[h % 2][1]
                            reg = povs[h % 2][hp:hp + 64, h // 2, :]
                            mm(reg, VT[:, t, h * 64:(h + 1) * 64], AT[i2][:, h, :], ['rtVT', 'rtAT%d' % i2], [pok], start=True, stop=False)
                            for d in range(2):
                                mm(reg, Sall[d][hp:hp + 64, h // 2, t, :], qd[i2][d][hp:hp + 64, h // 2, :],
                                   ['rtSall%d_%d' % (d, t), 'rtqd%d_%d' % (i2, d)], [pok], start=False, stop=(d == 1))
                        yield
                        obk = 'rtob%d' % i2
                        cp('act', ob[i2][0:64], povs[0][0:64], [pos[0][1]], [obk])
                        cp('dve', ob[i2][64:128], povs[1][64:128], [pos[1][1]], [obk])
                        yield
                        pov = ob[i2][:]
                        pok = obk
                        if ('yc%d' % l) in debug:
                            cp('pool', dbgbuf[:, :, tsl], pov, [pok], ['dbgbuf'])
                        act(sq[i2][:], pov, AF.Square, [pok], ['rtsq%d' % i2])
                        yield
                        pss_, psk = bank()
                        psv = pss_[:, 0:256].rearrange("p (a b) -> p a b", a=2)
                        for j in range(2):
                            mm(psv[:, j, :], bonesb, sq[i2][:, j, :], ['cstb', 'rtsq%d' % i2], [psk])
                        yield
                        act(rr[i2][:], psv, AF.Sqrt, [psk], ['rtrr%d' % i2], bias=RMS_EPS, scale=1.0 / 64)
                        yield
                        S.op('dve', lambda e: e.reciprocal(out=rr[i2][:], in_=rr[i2][:]), reads=['rtrr%d' % i2], writes=['rtrr%d' % i2])
                        yield
                        tt('dve', rr[i2][:], pov, rr[i2][:], ALU.mult, [pok, 'rtrr%d' % i2], ['rtrr%d' % i2])
                        yield
                        tt('pool', Y[:, 2, :, tsl], rr[i2][:], zs[:, :, tsl], ALU.mult, ['rtrr%d' % i2, 'rtzs'], ['Y2'])

                    gj = gate_jobs(l, last, st2, bank, GJ_SPLIT[3]) if (GATE_PRE and GJ_SPLIT[3]) else []
                    run_pipelined(interleave([rtout(t) for t in range(NTL) if not (last and t < 2 and not debug)], gj, 3), STG['out'])
                    if ('yc%d' % l) in debug:
                        dbg_dump('yc%d' % l, dbgbuf[:], [128, 2, NT], ['dbgbuf'])
                    S.barrier()
                S.barrier()
        PHASES['ret'] = phase_ret
        def phase_rw(l, h_src, last):
            with contextlib.ExitStack() as st:
                RB = sb(st, "rwRB", [128, 2, NT], BF16)
                KB = sb(st, "rwKB", [128, 2, NT], BF16)
                VB = sb(st, "rwVB", [128, 2, NT], BF16)
                LB = sb(st, "rwLB", [128, NT], BF16)
                zs = sb(st, "rwzs", [128, 2, NT], BF16)
                vT = sb(st, "rwvT", [128, NTL, 256], BF16)
                lw2b = sb(st, "rwlw2", [128, 2, 256], BF16)
                S.dma('pool', lw2b[:], dr['lw2'][l], writes=['rwlw2'])
                oka = sb(st, "rwoka", [128, 2], F32)
                ts('dve', oka[:], pv('ka'), -1.0, 1.0, ALU.mult, ALU.add, ['pvt'], ['rwoka'])
                seen_b, seen_o = set(), set()
                with contextlib.ExitStack() as st2:
                    ww = sb(st2, "rww", [128, 8, 1152], BF16)
                    for pc_ in range(9):
                        S.dma('pool', ww[:, :, pc_ * 128:(pc_ + 1) * 128], dr['w_in'][l][:, :, 2816 + pc_ * 128:2816 + (pc_ + 1) * 128], writes=['rww%d' % pc_])
                    XRs = [sb(st2, "rwXR%d" % i, [128, NT + 4], F32) for i in range(2)]
                    XSs = [sb(st2, "rwXS%d" % i, [128, NT], F32) for i in range(2)]
                    c0 = sb(st2, "rwc0", [128, 7], F32)
                    pp = [ps(st2, "rwpp%d" % i, [128, 512], F32) for i in range(3)]
                    ptr = [ps(st2, "rwptr%d" % i, [128, 8, 128], BF16) for i in range(2)]
                    o_mu, _ = PV['mu']
                    mu0, mu1 = pvt[:, o_mu:o_mu + 7], pvt[:, o_mu + 7:o_mu + 14]
                    tt('dve', c0[:], mu0, mu1, ALU.add, ['pvt'], ['rwc0'])
                    ts('dve', c0[:], c0[:], -1.0, 1.0, ALU.mult, ALU.add, ['rwc0'], ['rwc0'])
                    for i in range(2):
                        memset('pool', XRs[i][:], 0.0, ['rwXR%d' % i])
                    cnt = 0
                    for m in (0, 1, 2, 3, 4, 5, 7, 6, 8):
                        XR, XRk = XRs[m % 2], 'rwXR%d' % (m % 2)
                        XS, XSk = XSs[m % 2], 'rwXS%d' % (m % 2)
                        for (n0, nn) in BLOCKS:
                            p_, pk_ = pp[cnt % 3], 'rwpp%d' % (cnt % 3)
                            cnt += 1
                            for jj in range(8):
                                mm(p_[:, 0:nn], ww[:, jj, m * 128:(m + 1) * 128], uT[:, jj, n0:n0 + nn],
                                   ['rww%d' % m] + uTk[n0 // 128:(n0 + nn) // 128], [pk_], start=(jj == 0), stop=(jj == 7))
                            if m >= 7:
                                act(zs[:, m - 7, n0:n0 + nn], p_[:, 0:nn], AF.Silu, [pk_, 'pvt'], ['rwzs'], bias=pv('bin', 22 + m))
                            else:
                                xo = 1 if n0 < 256 else 3
                                act(XR[:, n0 + xo:n0 + xo + nn], p_[:, 0:nn], AF.Identity, [pk_, 'pvt'], [XRk], bias=pv('bin', 22 + m))
                        if m >= 7:
                            continue
                        for (b0, ln, o0) in ((1, 256, 0), (259, 2048, 256)):
                            ts('dve', XS[:, o0:o0 + ln], XR[:, b0:b0 + ln], c0[:, m:m + 1], None, ALU.mult, None, [XRk, 'rwc0'], [XSk])
                            stt(XS[:, o0:o0 + ln], XR[:, b0 - 1:b0 - 1 + ln], mu0[:, m:m + 1], XS[:, o0:o0 + ln], ALU.mult, ALU.add,
                                [XRk, 'pvt', XSk], [XSk])
                            if m < 6:
                                dstT, dk = [(RB, 'rwRB'), (KB, 'rwKB'), (VB, 'rwVB')][m // 2]
                                stt(dstT[:, m % 2, o0:o0 + ln], XR[:, b0 + 1:b0 + 1 + ln], mu1[:, m:m + 1], XS[:, o0:o0 + ln], ALU.mult, ALU.add,
                                    [XRk, 'pvt', XSk], [dk])
                            else:
                                stt(XS[:, o0:o0 + ln], XR[:, b0 + 1:b0 + 1 + ln], mu1[:, m:m + 1], XS[:, o0:o0 + ln], ALU.mult, ALU.add,
                                    [XRk, 'pvt', XSk], [XSk])
                        if m == 6:
                            act(LB[0:64, :], XS[0:64, :], AF.Tanh, [XSk], ['rwLB'])
                            cp('pool', LB[64:128, :], XS[64:128, :], [XSk], ['rwLB'])
                    for t in range(NTL):
                        pr, prk = ptr[t % 2], 'rwptr%d' % (t % 2)
                        for j in range(2):
                            tr(pr[:, j, :], VB[:, j, t * 128:(t + 1) * 128], identb, ['rwVB', 'cstb'], [prk])
                        cp('dve' if t % 2 == 0 else 'act', vT[:, t, :], pr[:, 0:2, :].rearrange("p a b -> p (a b)"), [prk], ['rwvT'])
                    S.barrier()
                if stop == 'rw_proj':
                    return
                OS = sb(st, "rwOS", [128, 2, NT], F32)
                with contextlib.ExitStack() as st2:
                    def B(name, shape, dt=BF16):
                        return sb(st2, "rw_" + name, shape, dt), "rw_" + name
                    R64, R64k = B("R64", [128, 256], BF16)
                    memset('pool', R64[:], 1.0, [R64k])
                    memset('pool', R64[:, 0:256:64], 0.0, [R64k])
                    LW, LWk = B("LW", [128, 2, 128], F32)
                    SA, SAk = B("SA", [128, 2, 128], F32)
                    LGm, LGk = B("LG", [128, 2, 128], F32)
                    U0, U0k = LGm, LGk
                    EG, EGk = B("EG", [128, 2, 128], F32)
                    ENG, ENGk = B("ENG", [128, 2, 128], F32)
                    EGM, EGMk = B("EGM", [128, 2, 128], F32)
                    TA, TAk = B("TA", [128, 2, 128], F32)
                    TB_, TBk = B("TB", [128, 2, 128], F32)
                    SQ, SQk = B("SQ", [128, 2, 128])
                    RKD, RKDk = SQ, SQk
                    OBt = (None, None)
                    Zst = [B("Z%d" % d, [128, 2, 64], F32) for d in range(2)]
                    BUF = [dict() for _ in range(2)]
                    for d_ in range(2):
                        BUF[d_]['KKN'] = B("KKN_%d" % d_, [128, 2, 128])
                        BUF[d_]['KT'] = [B("KT_%d_%d" % (d_, s_), [128, 3, 2, 128]) for s_ in range(2)]
                        BUF[d_]['RT'] = [B("RT_%d_%d" % (d_, s_), [128, 2, 128]) for s_ in range(2)]
                        for j_ in range(2):
                            sfx = "_%d_%d" % (d_, j_)
                            SB = dict()
                            SB['TM'] = B("TM" + sfx, [128, 3, 128])
                            for nm_ in ('A1T', 'A2T', 'A3T', 'A4T', 'ALT', 'Tm', 'TTm', 'Xb', 'RHS', 'BYb'):
                                SB[nm_] = B(nm_ + sfx, [128, 2, 128])
                            SB['NY'] = B("NY" + sfx, [128, 2, 64])
                            SB['RH'] = B("RH" + sfx, [128, 128])
                            SB['GTb'] = B("GTb" + sfx, [128, 2, 128])
                            SB['ZLG'] = B("ZLG" + sfx, [128, 2, 64], F32)
                            SB['Z0b'] = B("Z0b" + sfx, [128, 2, 64])
                            BUF[d_][j_] = SB
                        BUF[d_]['GLt'] = [B("GLt_%d_%d" % (d_, s_), [128, 2, 2], F32) for s_ in range(2)]
                    banks = [ps(st2, "rwbank%d" % i, [128, 512], F32) for i in range(8)]
                    bcnt = [0]

                    def bank():
                        i = bcnt[0] % 8
                        bcnt[0] += 1
                        return banks[i], 'rwbank%d' % i
                    for d in range(2):
                        memset('pool', Zst[d][0][:], 0.0, [Zst[d][1], 'rw_Zs_%d_0' % d, 'rw_Zs_%d_1' % d])
                    for d_ in range(2):
                        for j_ in range(2):
                            memset('pool', BUF[d_][j_]['GTb'][0][:], 0.0, [BUF[d_][j_]['GTb'][1]])
                    orders = [list(range(NTL)), [1, 0] + list(range(NTL - 1, 1, -1))]
                    bc3 = lambda ap: ap.unsqueeze(2).broadcast_to([128, 2, 128])
                    def prep(d, t, slot):
                        KKN, KKNk = BUF[d]['KKN']
                        KT, KTk = BUF[d]['KT'][slot]
                        RTb, RTk = BUF[d]['RT'][slot]
                        GLt, GLk = BUF[d]['GLt'][slot]
                        tsl = slice(t * 128, (t + 1) * 128)
                        rev = (d == 1)
                        Z, Zk = Zst[d]
                        plw, plwk = bank()
                        pla, plak = bank()
                        plwv = plw[:, 0:256].rearrange("p (j t) -> p j t", j=2)
                        plav = pla[:, 0:256].rearrange("p (j t) -> p j t", j=2)
                        wb_ = 32 * d
                        for j in range(2):
                            mm(plwv[:, j, :], lw2b[wb_:wb_ + 16, d, j * 128:(j + 1) * 128], LB[wb_:wb_ + 16, tsl], ['rwlw2', 'rwLB'], [plwk])
                        for j in range(2):
                            mm(plav[:, j, :], lw2b[64:96, d, j * 128:(j + 1) * 128], LB[64:96, tsl], ['rwlw2', 'rwLB'], [plak])
                        for j in range(2):
                            act(LW[:, j, :], plwv[:, j, :], AF.Sigmoid, [plwk, 'pvt'], [LWk], bias=pv('w0', d * 2 + j))
                            act(SA[:, j, :], plav[:, j, :], AF.Sigmoid, [plak, 'pvt'], [SAk], bias=pv('a0', d * 2 + j))
                        ts('dve', LW[:], LW[:], -0.6065306597126334, None, ALU.mult, None, [LWk], [LWk])
                        yield
                        lwf = LW[:].rearrange("p a b -> p (a b)")
                        lgf = LGm[:].rearrange("p a b -> p (a b)")
                        if not rev:
                            S.op('dve', lambda e: e.tensor_tensor_scan(out=lgf, data0=R64[:], data1=lwf, initial=0.0, op0=ALU.mult, op1=ALU.add),
                                 reads=[LWk, R64k], writes=[LGk])
                        else:
                            S.op('dve', lambda e: e.tensor_tensor_scan(out=lgf[:, ::-1], data0=R64[:], data1=lwf[:, ::-1], initial=0.0,
                                                                       op0=ALU.mult, op1=ALU.add), reads=[LWk, R64k], writes=[LGk])
                        act(EG[:], LGm[:], AF.Exp, [LGk], [EGk])
                        yield
                        act(ENG[:], LGm[:], AF.Exp, [LGk], [ENGk], scale=-1.0)
                        yield
                        tt('pool', TA[:], LGm[:], LW[:], ALU.subtract, [LGk, LWk], [TAk])
                        yield
                        act(EGM[:], TA[:], AF.Exp, [TAk], [EGMk])
                        yield
                        gsrc = EG[:, :, 63::64] if not rev else EG[:, :, 0::64]
                        cp('pool', GLt[:], gsrc, [EGk], [GLk])
                        yield
                        tt('dve', TA[:], KB[:, :, tsl], bc3(pv('kk')), ALU.mult, ['rwKB', 'pvt', TAk], [TAk])
                        yield
                        act(SQ[:], TA[:], AF.Square, [TAk], [SQk])
                        yield
                        pss_, pssk = bank()
                        pssv = pss_[:, 0:256].rearrange("p (a b) -> p a b", a=2)
                        for j in range(2):
                            mm(pssv[:, j, :], bonesb, SQ[:, j, :], ['cstb', SQk], [pssk])
                        act(TB_[:], pssv, AF.Sqrt, [pssk], [TBk])
                        yield
                        ts('dve', TB_[:], TB_[:], 1e-12, None, ALU.max, None, [TBk], [TBk])
                        yield
                        S.op('dve', lambda e: e.reciprocal(out=TB_[:], in_=TB_[:]), reads=[TBk], writes=[TBk])
                        tt('dve', KKN[:], TA[:], TB_[:], ALU.mult, [TAk, TBk], [KKNk])
                        yield
                        tt('pool', KT[:, 0], KKN[:], EGM[:], ALU.mult, [KKNk, EGMk], [KTk])
                        yield
                        tt('dve', TA[:], SA[:], ENG[:], ALU.mult, [SAk, ENGk, TAk], [TAk])
                        yield
                        tt('pool', KT[:, 1], KKN[:], TA[:], ALU.mult, [KKNk, TAk], [KTk])
                        yield
                        tt('dve', U0[:], SA[:], bc3(pv('ka')), ALU.mult, [SAk, 'pvt'], [U0k])
                        yield
                        tt('dve', U0[:], U0[:], bc3(oka[:]), ALU.add, [U0k, 'rwoka'], [U0k])
                        yield
                        tt('pool', TB_[:], U0[:], ENG[:], ALU.mult, [U0k, ENGk, TBk], [TBk])
                        yield
                        tt('pool', KT[:, 2], KB[:, :, tsl], TB_[:], ALU.mult, ['rwKB', TBk], [KTk])
                        yield
                        tt('dve', RTb[:], RB[:, :, tsl], EG[:], ALU.mult, ['rwRB', EGk], [RTk])
                        yield
                        tt('dve', U0[:], U0[:], KB[:, :, tsl], ALU.mult, [U0k, 'rwKB'], [U0k])
                        yield
                        tt('dve', U0[:], U0[:], bc3(pv('rk')), ALU.mult, [U0k, 'pvt'], [U0k])
                        yield
                        tt('pool', RKD[:], U0[:], RB[:, :, tsl], ALU.mult, [U0k, 'rwRB'], [RKDk])
                        yield
                        pbn, pbnk = bank()
                        pbnv = pbn[:, 0:256].rearrange("p (a b) -> p a b", a=2)
                        for j in range(2):
                            mm(pbnv[:, j, :], bonesb, RKD[:, j, :], ['cstb', RKDk], [pbnk])
                        if t not in seen_b:
                            seen_b.add(t)
                            tt('dve', Y[:, 3, :, tsl], pbnv, VB[:, :, tsl], ALU.mult, [pbnk, 'rwVB'], ['Y3'])
                        else:
                            tt('dve', TA[:], pbnv, VB[:, :, tsl], ALU.mult, [pbnk, 'rwVB', TAk], [TAk])
                            tt('pool', Y[:, 3, :, tsl], Y[:, 3, :, tsl], TA[:], ALU.add, ['Y3', TAk], ['Y3'])

                    def prep_pair(step):
                        for d_ in range(2):
                            yield from prep(d_, orders[d_][step], step % 2)

                    def unit(d, t, slot):
                        KT, KTk = BUF[d]['KT'][slot]
                        RTb, RTk = BUF[d]['RT'][slot]
                        GLt, GLk = BUF[d]['GLt'][slot]
                        tsl = slice(t * 128, (t + 1) * 128)
                        rev = (d == 1)
                        subs = [stream(d, j, t, rev, tsl, KT, KTk, RTb, RTk, GLt, GLk) for j in range(2)]
                        while subs:
                            for g in list(subs):
                                try:
                                    next(g)
                                except StopIteration:
                                    subs.remove(g)
                                yield

                    def stream(d, j, t, rev, tsl, KT, KTk, RTb, RTk, GLt, GLk):
                        SB = BUF[d][j]
                        TM, TMk = SB['TM']
                        A1T, A1k = SB['A1T']
                        A2T, A2k = SB['A2T']
                        A3T, A3k = SB['A3T']
                        A4T, A4k = SB['A4T']
                        ALT, ALk = SB['ALT']
                        Tm, Tmk = SB['Tm']
                        TTm, TTk = SB['TTm']
                        Xb, Xbk = SB['Xb']
                        RHS, RHSk = SB['RHS']
                        BYb, BYk = SB['BYb']
                        NY, NYk = SB['NY']
                        RH, RHk = SB['RH']
                        GTb, GTk = SB['GTb']
                        ZLG, ZLGk = SB['ZLG']
                        Z0b, Z0k = SB['Z0b']
                        Z, _zk = Zst[d]
                        Zk = 'rw_Zs_%d_%d' % (d, j)
                        ptb, ptbk = bank()
                        ptv = ptb[:].bitcast(BF16).rearrange("p (a b) -> p a b", a=8)
                        for x in range(3):
                            tr(ptv[:, x, :], KT[:, x, j, :], identb, [KTk, 'cstb'], [ptbk])
                        cp('act', TM[:], ptv[:, 0:3, :], [ptbk], [TMk])
                        yield

                        def amat(dst, dstk, li, ri_src, ri_k, mslot):
                            pas = []
                            for par in range(2):
                                hp = par * 64
                                pa, pak = bank()
                                rhs = (RTb[hp:hp + 64, j, :] if ri_src is None else KT[hp:hp + 64, ri_src, j, :])
                                mm(pa[:, 0:128], KT[hp:hp + 64, li, j, :], rhs, [KTk, ri_k], [pak])
                                pas.append((pa, pak))
                            return pas

                        def aevac(pas, dst, dstk, mslot):
                            for par, (pa, pak) in enumerate(pas):
                                if mslot is None:
                                    cp('act', dst[:, par, :], pa[:, 0:128], [pak], [dstk])
                                else:
                                    tt('dve', dst[:, par, :], pa[:, 0:128], maskb[:, mslot, :], ALU.mult, [pak, 'maskb'], [dstk])
                        for (dst, dstk, li, rs, rk, ms) in ((A1T, A1k, 1, 0, KTk, None), (A2T, A2k, 2, 0, KTk, 2 + d),
                                                            (A3T, A3k, 1, None, RTk, 4 + d), (A4T, A4k, 2, None, RTk, 4 + d)):
                            pas = amat(dst, dstk, li, rs, rk, ms)
                            aevac(pas, dst, dstk, ms)
                            yield
                        idb2 = identb.unsqueeze(1).broadcast_to([128, 2, 128])
                        cp('pool', Tm[:], idb2, ['cstb'], [Tmk])
                        cp('pool', TTm[:], idb2, ['cstb'], [TTk])
                        for lv in range(6):
                            tt('pool', ALT[:], A1T[:], maskb[:, 6 + d * 6 + lv, :].unsqueeze(1).broadcast_to([128, 2, 128]), ALU.mult,
                               [A1k, 'maskb'], [ALk])
                            yield
                            px, pxk = bank()
                            pxv = px[:, 0:256].rearrange("p (h t) -> p h t", h=2)
                            for par in range(2):
                                mm(pxv[:, par, :], ALT[:, par, :], Tm[:, par, :], [ALk, Tmk], [pxk])
                            cp('act', Xb[:], pxv, [pxk], [Xbk])
                            yield
                            py_, pyk = bank()
                            pyv = py_[:].rearrange("p (x h t) -> p x h t", x=2, h=2)
                            for par in range(2):
                                mm(pyv[:, 0, par, :], Xb[:, par, :], TTm[:, par, :], [Xbk, TTk], [pyk])
                            if lv < 5:
                                for par in range(2):
                                    mm(pyv[:, 1, par, :], TTm[:, par, :], Xb[:, par, :], [Xbk, TTk], [pyk])
                            if lv < 5:
                                tt('dve', Tm[:], Tm[:], pyv[:, 1], ALU.subtract, [Tmk, pyk], [Tmk])
                            tt('dve', TTm[:], TTm[:], pyv[:, 0], ALU.subtract, [TTk, pyk], [TTk])
                            yield
                        pw, pwk = bank()
                        pwv = pw[:, 0:128].rearrange("p (h v) -> p h v", h=2)
                        for par in range(2):
                            h = 2 * j + par
                            mm(pwv[:, par, :], A2T[:, par, :], vT[:, t, h * 64:(h + 1) * 64], [A2k, 'rwvT'], [pwk])
                        cp('pool', RHS[:, :, 0:64], TM[:, 0, :].rearrange("p (h k) -> p h k", h=2), [TMk], [RHSk])
                        cp('act', RHS[:, :, 64:128], pwv, [pwk], [RHSk])
                        yield
                        pby, pbyk = bank()
                        pbyv = pby[:, 0:256].rearrange("p (h t) -> p h t", h=2)
                        for par in range(2):
                            mm(pbyv[:, par, :], TTm[:, par, :], RHS[:, par, :], [TTk, RHSk], [pbyk])
                        cp('act', BYb[:], pbyv, [pbyk], [BYk])
                        yield
                        ts('pool', NY[:], BYb[:, :, 64:128], -1.0, 0.0, ALU.mult, ALU.add, [BYk], [NYk])
                        pr_, prk = bank()
                        for par in range(2):
                            hp = par * 64
                            mm(pr_[hp:hp + 64, 0:128], BYb[:, par, 0:64], A3T[:, par, :], [BYk, A3k], [prk])
                        tt('dve', RH[:], RTb[:, j, :], pr_[:, 0:128], ALU.subtract, [RTk, prk], [RHk])
                        yield
                        for c in range(2):
                            cs = slice(c * 64, (c + 1) * 64)
                            pg_, pgk = bank()
                            pgv = pg_[:, 0:128].rearrange("p (x v) -> p x v", x=2)
                            for par in range(2):
                                hp = par * 64
                                h = 2 * j + par
                                hc = slice(h * 64, (h + 1) * 64)
                                pc = slice(par * 64, (par + 1) * 64)
                                mm(pgv[hp:hp + 64, 0, :], BYb[cs, par, 0:64], TM[cs, 1, pc], [BYk, TMk], [pgk])
                                mm(pgv[hp:hp + 64, 1, :], TM[cs, 2, pc], vT[cs, t, hc], [TMk, 'rwvT'], [pgk], start=True, stop=False)
                                mm(pgv[hp:hp + 64, 1, :], TM[cs, 1, pc], NY[cs, par, :], [TMk, NYk], [pgk], start=False, stop=True)
                            for par in range(2):
                                hp = par * 64
                                tt('dve', GTb[hp:hp + 64, c, hp:hp + 64], cstf[hp:hp + 64, 4, 0:64], pgv[hp:hp + 64, 0, :], ALU.subtract,
                                   ['cstf', pgk], [GTk])
                            ts('dve', ZLG[:, c, :], pgv[:, 1, :], GLt[:, j, c:c + 1], None, ALU.mult, None, [pgk, GLk], [ZLGk])
                            yield
                        for c in ((0, 1) if not rev else (1, 0)):
                            cp('act', Z0b[:, c, :], Z[:, j, :], [Zk], [Z0k])
                            yield
                            pn, pnk = bank()
                            mm(pn[:, 0:64], GTb[:, c, :], Z0b[:, c, :], [GTk, Z0k], [pnk])
                            stt(Z[:, j, :], pn[:, 0:64], GLt[:, j, c:c + 1], ZLG[:, c, :], ALU.mult, ALU.add, [pnk, GLk, ZLGk, Zk], [Zk])
                            yield
                        for par in range(2):
                            hp = par * 64
                            h = 2 * j + par
                            hc = slice(h * 64, (h + 1) * 64)
                            po_, pok = bank()
                            reg = po_[hp:hp + 64, 0:128]
                            mm(reg, vT[:, t, hc], A4T[:, par, :], ['rwvT', A4k], [pok], start=True, stop=False)
                            mm(reg, NY[:, par, :], A3T[:, par, :], [NYk, A3k], [pok], start=False, stop=False)
                            for c in range(2):
                                mm(reg[:, c * 64:(c + 1) * 64], Z0b[hp:hp + 64, c, :], RH[hp:hp + 64, c * 64:(c + 1) * 64],
                                   [Z0k, RHk], [pok], start=False, stop=(c == 1))
                            osl = OS[hp:hp + 64, j, tsl]
                            osk = 'rwOS%d_%d' % (t, j)
                            if (t, j, par) not in seen_o:
                                seen_o.add((t, j, par))
                                cp('dve' if par == 0 else 'act', osl, reg, [pok], [osk])
                            else:
                                tt('dve', osl, osl, reg, ALU.add, [pok, osk], [osk])
                            yield

                    for _ in prep_pair(0):
                        pass
                    for step in range(NTL):
                        gens = [unit(d, orders[d][step], step % 2) for d in range(2)]
                        if step + 1 < NTL:
                            gens.append(prep_pair(step + 1))
                        while gens:
                            for g in list(gens):
                                try:
                                    next(g)
                                except StopIteration:
                                    gens.remove(g)
                    S.barrier()
                if stop is not None and stop.startswith('rw_'):
                    return
                with contextlib.ExitStack() as st2:
                    ob = [sb(st2, "rwob%d" % i, [128, 2, 128], BF16) for i in range(2)]
                    cen = [sb(st2, "rwcen%d" % i, [128, 2, 128], F32) for i in range(2)]
                    rs = [sb(st2, "rwrs%d" % i, [128, 2, 128], F32) for i in range(2)]
                    pm_ = [ps(st2, "rwpm%d" % i, [128, 512], F32) for i in range(2)]
                    pv_ = [ps(st2, "rwpv%d" % i, [128, 512], F32) for i in range(2)]
                    for t in range(NTL):
                        i2 = t % 2
                        tsl = slice(t * 128, (t + 1) * 128)
                        osk = 'rwOS%d_0' % t
                        osk1 = 'rwOS%d_1' % t
                        cp('act', ob[i2][:], OS[:, :, tsl], [osk, osk1], ['rwob%d' % i2])
                        pmv = pm_[i2][:, 0:256].rearrange("p (a b) -> p a b", a=2)
                        for j in range(2):
                            mm(pmv[:, j, :], bonesb, ob[i2][:, j, :], ['cstb', 'rwob%d' % i2], ['rwpm%d' % i2])
                        stt(cen[i2][:], pmv, -1.0 / 64, OS[:, :, tsl], ALU.mult, ALU.add, ['rwpm%d' % i2, osk, osk1], ['rwcen%d' % i2])
                        act(ob[i2][:], cen[i2][:], AF.Square, ['rwcen%d' % i2], ['rwob%d' % i2])
                        pvv = pv_[i2][:, 0:256].rearrange("p (a b) -> p a b", a=2)
                        for j in range(2):
                            mm(pvv[:, j, :], bonesb, ob[i2][:, j, :], ['cstb', 'rwob%d' % i2], ['rwpv%d' % i2])
                        act(rs[i2][:], pvv, AF.Sqrt, ['rwpv%d' % i2], ['rwrs%d' % i2], bias=RW_GN_EPS, scale=1.0 / 64)
                        S.op('dve', lambda e: e.reciprocal(out=rs[i2][:], in_=rs[i2][:]), reads=['rwrs%d' % i2], writes=['rwrs%d' % i2])
                        tt('dve', cen[i2][:], cen[i2][:], rs[i2][:], ALU.mult, ['rwcen%d' % i2, 'rwrs%d' % i2], ['rwcen%d' % i2])
                        tt('pool', cen[i2][:], cen[i2][:], bc3(pv('gnw')), ALU.mult, ['rwcen%d' % i2, 'pvt'], ['rwcen%d' % i2])
                        tt('pool', cen[i2][:], cen[i2][:], bc3(pv('gnb')), ALU.add, ['rwcen%d' % i2, 'pvt'], ['rwcen%d' % i2])
                        tt('dve', cen[i2][:], cen[i2][:], Y[:, 3, :, tsl], ALU.add, ['rwcen%d' % i2, 'Y3'], ['rwcen%d' % i2])
                        if ('yd%d' % l) in debug:
                            cp('act', OS[:, :, tsl], cen[i2][:], ['rwcen%d' % i2], [osk, osk1])
                        tt('dve', Y[:, 3, :, tsl], cen[i2][:], zs[:, :, tsl], ALU.mult, ['rwcen%d' % i2, 'rwzs'], ['Y3'])
                    if ('yd%d' % l) in debug:
                        dbg_dump('yd%d' % l, OS[:], [128, 2, NT], ['rwOS%d_%d' % (t, j_) for t in range(NTL) for j_ in range(2)])
                    S.barrier()
                S.barrier()
        PHASES['rw'] = phase_rw
        def phase_merge(l, h_src, last):
            h_dst = out_d if last else h1_d
            with contextlib.ExitStack() as st:
                MG = sb(st, "mgMG", [128, 8, NT], BF16)
                wbr = sb(st, "mgwbr", [128, 4, 2, DM], BF16)
                S.dma('pool', wbr[:], dr['wbr'][l], writes=['mgwbr'])
                with contextlib.ExitStack() as st2:
                    wg = [sb(st2, "mgwg%d" % i, [128, 8, 4, 128], BF16) for i in range(2)]
                    sg = [sb(st2, "mgsg%d" % i, [128, 512], BF16) for i in range(3)]
                    ac = [sb(st2, "mgac%d" % i, [128, 512], F32) for i in range(2)]
                    tm = [sb(st2, "mgtm%d" % i, [128, 512], F32) for i in range(2)]
                    pgl = [ps(st2, "mgpg%d" % i, [128, 512], F32) for i in range(3)]
                    pbr = [ps(st2, "mgpb%d" % i, [128, 512], F32) for i in range(3)]
                    cg = 0
                    ca = 0
                    def load_wg(dt_):
                        for k in range(4):
                            if (l, k * 8 + dt_) in pre_sg:
                                continue
                            c0 = 3968 + k * 1024 + dt_ * 128
                            S.dma('pool', wg[dt_ % 2][:, :, k, :], dr['w_in'][l][:, :, c0:c0 + 128], writes=['mgwg%d' % (dt_ % 2)])
                    load_wg(0)
                    for dt_ in range(8):
                        w_, wk_ = wg[dt_ % 2], 'mgwg%d' % (dt_ % 2)
                        if dt_ + 1 < 8:
                            load_wg(dt_ + 1)
                        for (n0, nn) in BLOCKS:
                            if last and n0 < 256:
                                continue
                            a_, ak_ = ac[ca % 2], 'mgac%d' % (ca % 2)
                            t_, tk_ = tm[ca % 2], 'mgtm%d' % (ca % 2)
                            ca += 1
                            for k in range(4):
                                pg_, pgk_ = pgl[cg % 3], 'mgpg%d' % (cg % 3)
                                pb_, pbk_ = pbr[cg % 3], 'mgpb%d' % (cg % 3)
                                s_, sk_ = sg[cg % 3], 'mgsg%d' % (cg % 3)
                                cg += 1
                                if (l, k * 8 + dt_) in pre_sg:
                                    S.dma('sp' if cg % 2 == 0 else 'act', s_[:, 0:nn], sgd[k * 8 + dt_][:, n0:n0 + nn], reads=['sgd'], writes=[sk_])
                                else:
                                    for jj in range(8):
                                        mm(pg_[:, 0:nn], w_[:, jj, k, :], uT[:, jj, n0:n0 + nn], [wk_] + uTk[n0 // 128:(n0 + nn) // 128], [pgk_],
                                           start=(jj == 0), stop=(jj == 7))
                                    act(s_[:, 0:nn], pg_[:, 0:nn], AF.Sigmoid, [pgk_, 'pvt'], [sk_], bias=pv('bin', 31 + k * 8 + dt_))
                                for jc in range(2):
                                    mm(pb_[:, 0:nn], wbr[:, k, jc, dt_ * 128:(dt_ + 1) * 128], Y[:, k, jc, n0:n0 + nn], ['mgwbr', 'Y%d' % k], [pbk_],
                                       start=(jc == 0), stop=(jc == 1))
                                if k == 0:
                                    tt('dve', a_[:, 0:nn], pb_[:, 0:nn], s_[:, 0:nn], ALU.mult, [pbk_, sk_], [ak_])
                                else:
                                    tt('dve', t_[:, 0:nn], pb_[:, 0:nn], s_[:, 0:nn], ALU.mult, [pbk_, sk_], [tk_])
                                    if k < 3:
                                        tt('pool', a_[:, 0:nn], a_[:, 0:nn], t_[:, 0:nn], ALU.add, [ak_, tk_], [ak_])
                                    else:
                                        tt('pool', MG[:, dt_, n0:n0 + nn], a_[:, 0:nn], t_[:, 0:nn], ALU.add, [ak_, tk_], ['mgMG%d' % (n0 // 512 if n0 else 9)])
                    S.barrier()
                if ('merged%d' % l) in debug:
                    with contextlib.ExitStack() as st2:
                        mf = sb(st2, "mgf", [128, 8, NT], F32)
                        cp('dve', mf[:], MG[:], ['mgMG%d' % i for i in (9, 0, 1, 2, 3)], ['mgf'])
                        dbg_dump('merged%d' % l, mf[:], [128, 8, NT], ['mgf'])
                        S.barrier()
                with contextlib.ExitStack() as st2:
                    wo = sb(st2, "mgwo", [128, 8, DM], BF16)
                    S.dma('pool', wo[:], dr['wout'][l], writes=['mgwo'])
                    rows = sb(st2, "mgrows", [128, 3, DM], F32)
                    S.dma('sp', rows[:], dr['rows'][l][:, 0:3072].rearrange("p (a b) -> p a b", a=3), writes=['mgrows'])
                    hin_ = [sb(st2, "mghin%d" % i, [128, DM], F32) for i in range(2)]
                    ot = [sb(st2, "mgot%d" % i, [128, DM], F32) for i in range(2)]
                    stat = [sb(st2, "mgst%d" % i, [128, 16], F32) for i in range(2)]
                    po = [[ps(st2, "mgpo%d_%d" % (i, hh), [128, 512], F32) for hh in range(2)] for i in range(2)]
                    def mgout(it, t):
                        i2 = it % 2
                        ci = 1 if t < 2 else 0
                        tsl = slice(t * 128, (t + 1) * 128)
                        mgk = 'mgMG%d' % (9 if t < 2 else (t - 2) // 4)
                        hk_, ok_, sk_ = 'mghin%d' % i2, 'mgot%d' % i2, 'mgst%d' % i2
                        hi, o_, sti = hin_[i2], ot[i2], stat[i2]
                        S.dma('sp', hi[:], h_src[t * 128:(t + 1) * 128, :], writes=[hk_])
                        for hh in range(2):
                            pk_ = 'mgpo%d_%d' % (i2, hh)
                            for jj in range(8):
                                mm(po[i2][hh][:], MG[:, jj, tsl], wo[:, jj, hh * 512:(hh + 1) * 512], [mgk, 'mgwo'], [pk_], start=(jj == 0), stop=(jj == 7))
                        yield
                        for hh in range(2):
                            pk_ = 'mgpo%d_%d' % (i2, hh)
                            tt('dve', o_[:, hh * 512:(hh + 1) * 512], po[i2][hh][:], rows[:, 0, hh * 512:(hh + 1) * 512], ALU.add, [pk_, 'mgrows'], [ok_])
                        yield
                        tt('dve', o_[:], o_[:], gatebc[:, ci, :], ALU.mult, [ok_, 'gatebc'], [ok_])
                        yield
                        stt(o_[:], hi[:], ALPHA, o_[:], ALU.mult, ALU.add, [hk_, ok_], [ok_])
                        yield
                        S.op('dve', lambda e: e.bn_stats(out=sti[:, 0:6], in_=o_[:, 0:512]), reads=[ok_], writes=[sk_])
                        S.op('dve', lambda e: e.bn_stats(out=sti[:, 6:12], in_=o_[:, 512:1024]), reads=[ok_], writes=[sk_])
                        yield
                        S.op('dve', lambda e: e.bn_aggr(out=sti[:, 12:14], in_=sti[:, 0:12]), reads=[sk_], writes=[sk_])
                        yield
                        act(sti[:, 14:15], sti[:, 13:14], AF.Sqrt, [sk_], [sk_], bias=LN_EPS)
                        yield
                        S.op('dve', lambda e: e.reciprocal(out=sti[:, 14:15], in_=sti[:, 14:15]), reads=[sk_], writes=[sk_])
                        yield
                        stt(sti[:, 15:16], sti[:, 12:13], -1.0, sti[:, 14:15], ALU.mult, ALU.mult, [sk_], [sk_])
                        yield
                        act(o_[:], o_[:], AF.Identity, [ok_, sk_], [ok_], bias=sti[:, 15:16], scale=sti[:, 14:15])
                        yield
                        tt('pool', o_[:, 0:512], o_[:, 0:512], rows[:, 1, 0:512], ALU.mult, [ok_, 'mgrows'], [ok_])
                        tt('dve', o_[:, 512:1024], o_[:, 512:1024], rows[:, 1, 512:1024], ALU.mult, [ok_, 'mgrows'], [ok_])
                        yield
                        tt('pool', o_[:, 0:512], o_[:, 0:512], rows[:, 2, 0:512], ALU.add, [ok_, 'mgrows'], [ok_])
                        tt('dve', o_[:, 512:1024], o_[:, 512:1024], rows[:, 2, 512:1024], ALU.add, [ok_, 'mgrows'], [ok_])
                        yield
                        if last:
                            S.dma('sp', out_d[(t - 2) * 128:(t - 1) * 128, :], o_[:], reads=[ok_], writes=['outfinal'])
                        else:
                            S.dma('sp', h1_d[t * 128:(t + 1) * 128, :], o_[:], reads=[ok_], writes=['h1'])

                    tl = [t for t in range(NTL) if not (last and t < 2)]
                    run_pipelined((mgout(i_, t) for i_, t in enumerate(tl)), STG['mgout'])
                    S.barrier()
                S.barrier()
        PHASES['merge'] = phase_merge
        for l in range(nlayers):
            last = (l == nlayers - 1)
            h_src = dr['hin'] if l == 0 else h1_d
            S.dma('sp', pvt[:], dr['pv'][l], writes=['pvt'])
            with contextlib.ExitStack() as st:
                adw = [sb(st, "adw%d" % i, [128, 8, 512], F32) for i in range(2)]
                scb = sb(st, "scb", [128, 2, 8, 128], F32)
                grow = sb(st, "grow", [128, DM], F32)
                pm0 = ps(st, "pm0", [128, 16, 2], F32)
                pg = [ps(st, "pg%d" % i, [128, 512], F32) for i in range(2)]
                for i in range(2):
                    cp('dve', scb[:, i], silc[:, :, i:i + 1].broadcast_to([128, 8, 128]), ['silc'], ['scb'])
                S.dma('sp', grow[:], dr['rows'][l][:, 3072:4096], writes=['grow'])
                for ch in range(6):
                    buf = adw[ch % 2]
                    bk = 'adw%d' % (ch % 2)
                    S.dma('sp' if ch % 2 == 0 else 'act', buf[:], dr['ada_w'][l][:, :, ch * 512:(ch + 1) * 512], writes=[bk])
                    if ch < 4:
                        for mloc in range(4):
                            m = ch * 4 + mloc
                            for j in range(8):
                                mm(pm0[:, m, :], buf[:, j, mloc * 128:(mloc + 1) * 128], silc[:, j, :], [bk, 'silc'],
                                   ['pm0'], start=(j == 0), stop=(j == 7))
                    else:
                        for i in range(2):
                            for j in range(8):
                                mm(pg[i][:], scb[:, i, j, :], buf[:, j, :], [bk, 'scb'], ['pg%d' % i],
                                   start=(j == 0), stop=(j == 7))
                            tt('dve', gatebc[:, i, (ch - 4) * 512:(ch - 3) * 512], pg[i][:],
                               grow[:, (ch - 4) * 512:(ch - 3) * 512], ALU.add, ['pg%d' % i, 'grow'], ['gatebc'])
                tt('dve', modfm[:], pm0[:], pv('adab').unsqueeze(2).broadcast_to([128, 16, 2]), ALU.add,
                   ['pm0', 'pvt'], ['modfm'])
                ts('dve', modfm[:, 8:16, :], modfm[:, 8:16, :], 1.0, None, ALU.add, None, ['modfm'], ['modfm'])
                dbg_dump('modfm%d' % l, modfm[:], [128, 16, 2], ['modfm'])
                dbg_dump('gatebc%d' % l, gatebc[:], [128, 2, DM], ['gatebc'])
                S.barrier()
            with contextlib.ExitStack() as st:
                xin = [sb(st, "xin%d" % i, [128, DM], F32) for i in range(3)]
                xn = [sb(st, "xn%d" % i, [128, DM], BF16) for i in range(2)]
                stat = [sb(st, "stat%d" % i, [128, 16], F32) for i in range(3)]
                ptr = [ps(st, "ptr%d" % i, [128, 8, 128], BF16) for i in range(2)]
                def p1tile(t):
                    xi, xk = xin[t % 3], 'xin%d' % (t % 3)
                    sti, sk = stat[t % 3], 'stat%d' % (t % 3)
                    xo, xok = xn[t % 2], 'xn%d' % (t % 2)
                    pt, ptk = ptr[t % 2], 'ptr%d' % (t % 2)
                    ci = 1 if t < 2 else 0
                    S.dma('sp' if t % 2 == 0 else 'act', xi[:], h_src[t * 128:(t + 1) * 128, :], writes=[xk])
                    yield
                    S.op('dve', lambda e: e.bn_stats(out=sti[:, 0:6], in_=xi[:, 0:512]), reads=[xk], writes=[sk])
                    S.op('dve', lambda e: e.bn_stats(out=sti[:, 6:12], in_=xi[:, 512:1024]), reads=[xk], writes=[sk])
                    yield
                    S.op('dve', lambda e: e.bn_aggr(out=sti[:, 12:14], in_=sti[:, 0:12]), reads=[sk], writes=[sk])
                    yield
                    act(sti[:, 14:15], sti[:, 13:14], AF.Sqrt, [sk], [sk], bias=LN_EPS)
                    yield
                    S.op('dve', lambda e: e.reciprocal(out=sti[:, 14:15], in_=sti[:, 14:15]), reads=[sk], writes=[sk])
                    yield
                    stt(sti[:, 15:16], sti[:, 12:13], -1.0, sti[:, 14:15], ALU.mult, ALU.mult, [sk], [sk])
                    yield
                    act(xo[:], xi[:], AF.Identity, [xk, sk], [xok], bias=sti[:, 15:16], scale=sti[:, 14:15])
                    yield
                    for j in range(8):
                        tr(pt[:, j, :], xo[:, j * 128:(j + 1) * 128], identb, [xok, 'cstb'], [ptk])
                    yield
                    for j in range(8):
                        if j % 2 == 0:
                            act(uT[:, j, t * 128:(t + 1) * 128], pt[:, j, :], AF.Identity, [ptk, 'modfm'], ['uT%d' % t],
                                bias=modfm[:, j, ci:ci + 1], scale=modfm[:, 8 + j, ci:ci + 1])
                        else:
                            ts('dve', uT[:, j, t * 128:(t + 1) * 128], pt[:, j, :], modfm[:, 8 + j, ci:ci + 1],
                               modfm[:, j, ci:ci + 1], ALU.mult, ALU.add, [ptk, 'modfm'], ['uT%d' % t])

                run_pipelined((p1tile(t) for t in range(NTL)), STG['p1'])
                if ('uT%d' % l) in debug:
                    utf = sb(st, "utf", [128, 8, NT], F32)
                    cp('dve', utf[:], uT[:], ['uT%d' % t for t in range(NTL)], ['utf'])
                    dbg_dump('uT%d' % l, utf[:], [128, 8, NT], ['utf'])
                S.barrier()
            uTk = ['uT%d' % t for t in range(NTL)]

            for ph in list(PHASES):
                if ph in phases:
                    PHASES[ph](l, h_src, last)
            if ('h%d' % l) in debug and not last:
                d_ = dbg_out('h%d' % l, [NT, DM])
                S.dma('sp', d_, h1_d, writes=['dbgout_h%d' % l])
                S.barrier()
            if ('Y%d' % l) in debug:
                with contextlib.ExitStack() as st:
                    yf = sb(st, "yf", [128, 4, 2, NT], F32)
                    cp('dve', yf[:], Y[:], ['Y0', 'Y1', 'Y2', 'Y3'], ['yf'])
                    dbg_dump('Y%d' % l, yf[:], [128, 4, 2, NT], ['yf'])
                    S.barrier()

        S.final_wait('sp', ['outfinal'] + ['dbgout_' + n for n in dbg_d])
    if MEMDBG:
        print('SBUF min remaining by prefix:', minrem)
    return nc, dbg_d


def kernel(**inputs):
    inp = {k: np.asarray(v) for k, v in inputs.items()}
    sh = prep_shared(inp)
    nc, _ = build()
    in_maps = []
    for b in range(8):
        m = dict(sh)
        m.update(prep_core(inp, b))
        in_maps.append(m)
    res = run_bass_kernel_spmd(nc, in_maps, core_ids=list(range(8)))
    return np.stack([np.asarray(res.results[b]['out'], dtype=np.float32) for b in range(8)], 0)
```

```python
import contextlib
import numpy as np
import concourse.bass as bass
import concourse.mybir as mybir
from concourse.bass_utils import run_bass_kernel_spmd

F32 = mybir.dt.float32
BF16 = mybir.dt.bfloat16
AF = mybir.ActivationFunctionType
ALU = mybir.AluOpType
AX = mybir.AxisListType

NT = 2304
NTL = 18
DM = 1024
NCOL = 8064
BLOCKS = [(0, 256), (256, 512), (768, 512), (1280, 512), (1792, 512)]
LN_EPS = 1e-5
RMS_EPS = 1e-6
RW_GN_EPS = 64e-5
ALPHA = (2 * 2) ** 0.25
PI = float(np.pi)
MEMDBG = False
GATE_PRE = False
S5_STAGGER = 6
STG = dict(hgchain=2, out=4, rtchain=1, hgproj=4, mgout=6, p1=4, s5=6)
GJ_SPLIT = [[], [], [], []]
GJ_S5 = list(range(32))


class Sched:
    NDMA = 16

    def __init__(self, nc, same_engine_waits=True):
        self.nc = nc
        self.same = same_engine_waits
        self.eng = dict(pe=nc.tensor, act=nc.scalar, dve=nc.vector, pool=nc.gpsimd, sp=nc.sync)
        self.E = {n: dict(cnt=0, known={}) for n in self.eng}
        self.dq = {'sp': ['dsp%d' % i for i in range(8)], 'act': ['dac%d' % i for i in range(4)],
                   'pool': ['dpl%d' % i for i in range(8)]}
        self.dmas = {n: dict(cnt=0) for q in self.dq.values() for n in q}
        self.dma_rr = {'sp': 0, 'act': 0, 'pool': 0}
        self.lastw = {}
        self.readers = {}
        self.sems = None
        self.nins = 0

    def sem_names(self):
        return list(self.E.keys()) + list(self.dmas.keys())

    def _deps(self, reads, writes):
        deps = {}

        def add(w):
            if w is not None:
                deps[w[0]] = max(deps.get(w[0], 0), w[1])
        for k in reads:
            add(self.lastw.get(k))
        for k in writes:
            add(self.lastw.get(k))
            for r in self.readers.get(k, ()):
                add(r)
        return deps

    def _waits(self, en, deps):
        E = self.E[en]
        waits = []
        for d, v in deps.items():
            if d == en and (en == 'pe' or not self.same):
                continue
            if E['known'].get(d, 0) < v:
                waits.append((d, v))
                E['known'][d] = v
        return waits

    def _record(self, ident, reads, writes):
        for k in writes:
            self.lastw[k] = ident
            self.readers[k] = []
        for k in reads:
            self.readers.setdefault(k, []).append(ident)

    def _emit(self, en, waits, fn, inc):
        eng = self.eng[en]
        for d, v in waits:
            eng.wait_ge(self.sems[d], v)
        if fn is not None:
            fn(eng).then_inc(self.sems[inc[0]], inc[1])
            self.nins += 1

    def op(self, en, fn, reads=(), writes=()):
        E = self.E[en]
        waits = self._waits(en, self._deps(reads, writes))
        E['cnt'] += 1
        self._emit(en, waits, fn, (en, 1))
        self._record((en, E['cnt']), reads, writes)

    def dma(self, en, out, in_, reads=(), writes=(), **kw):
        dn = self.dq[en][self.dma_rr[en]]
        self.dma_rr[en] = (self.dma_rr[en] + 1) % len(self.dq[en])
        Dq = self.dmas[dn]
        deps = self._deps(reads, writes)
        if Dq['cnt'] > 0:
            deps[dn] = max(deps.get(dn, 0), Dq['cnt'])
        waits = self._waits(en, deps)
        Dq['cnt'] += 16
        self._emit(en, waits, (lambda e: e.dma_start(out=out, in_=in_, **kw)), (dn, 16))
        self._record((dn, Dq['cnt']), reads, writes)

    def barrier(self):
        cur = {n: self.E[n]['cnt'] for n in self.E}
        cur.update({n: self.dmas[n]['cnt'] for n in self.dmas})
        for en in self.E:
            waits = self._waits(en, {d: v for d, v in cur.items() if v > 0})
            self._emit(en, waits, None, None)

    def final_wait(self, en, keys):
        self._emit(en, self._waits(en, self._deps(keys, ())), None, None)


PV = {}


def _pv_layout():
    off = 0
    for name, n in [('bin', 63), ('s5d', 2), ('glub', 2), ('hglb', 8), ('hgnw', 2), ('rdec', 4), ('mu', 14),
                    ('w0', 4), ('a0', 4), ('kk', 2), ('ka', 2), ('rk', 2), ('gnw', 2), ('gnb', 2), ('adab', 16),
                    ('lamre', 16), ('lamim', 16), ('ldt', 16), ('rdech', 8)]:
        PV[name] = (off, n)
        off += n
    return off


NPV = _pv_layout()


def _colmap():
    cm = list(range(0, 3584))
    lora = [-1] * 128
    for r in range(16):
        lora[r] = 3584 + r
        lora[32 + r] = 3600 + r
        lora[64 + r] = 3616 + r
        lora[80 + r] = 3632 + r
    cm += lora
    cm += list(range(3648, 3904))
    cm += list(range(3904, 8000))
    return np.array(cm)


CMAP = _colmap()


def _fm(v):
    return np.ascontiguousarray(v.reshape(-1, 128).T)


def _masks():
    t = np.arange(128)
    s_, t_ = t[:, None], t[None, :]
    m = []
    b32 = (s_ // 32) == (t_ // 32)
    b64 = (s_ // 64) == (t_ // 64)
    m.append(b32 & (t_ >= s_))
    m.append(b32 & (t_ <= s_))
    m.append(b64 & (t_ > s_))
    m.append(b64 & (t_ < s_))
    m.append(b64 & (t_ >= s_))
    m.append(b64 & (t_ <= s_))
    for d in range(2):
        for lv in range(6):
            sz = 1 << lv
            blk = (s_ // (2 * sz)) == (t_ // (2 * sz))
            hs, ht = (s_ // sz) % 2, (t_ // sz) % 2
            if d == 0:
                m.append(blk & (ht == 1) & (hs == 0))
            else:
                m.append(blk & (ht == 0) & (hs == 1))
    return np.stack([x.astype(np.float32) for x in m], 1)


def _rot_tables():
    n = 16
    freqs = 10000.0 ** (-np.arange(n, dtype=np.float32) / n)
    tt = np.arange(2048)
    rows = (tt // 64).astype(np.float32)
    cols = (tt % 64).astype(np.float32)
    cos = np.zeros((128, 2048), np.float32)
    sins = np.zeros((128, 2048), np.float32)
    pm = np.zeros((128, 128), np.float32)
    for p in range(128):
        i = p % 64
        pos = rows if i < 32 else cols
        ii = i % 32
        ang = pos * freqs[ii % 16]
        cos[p] = np.cos(ang)
        if ii < 16:
            sins[p] = -np.sin(ang)
            partner = p + 16
        else:
            sins[p] = np.sin(ang)
            partner = p - 16
        pm[partner, p] = 1.0
    return cos, sins, pm


def prep_shared(inp):
    sh = {}
    L = 2
    w_in = inp['w_in']
    wn = np.zeros((L, 1024, NCOL), np.float32)
    valid = CMAP >= 0
    wn[:, :, valid] = w_in[:, :, CMAP[valid]]
    sh['w_in'] = np.ascontiguousarray(wn.reshape(L, 8, 128, NCOL).transpose(0, 2, 1, 3))
    bn = np.zeros((L, NCOL), np.float32)
    bn[:, valid] = inp['b_in'][:, CMAP[valid]]
    sh['ada_w'] = np.ascontiguousarray(inp['ada_w'].reshape(L, 8, 128, 3072).transpose(0, 2, 1, 3))
    pv = np.zeros((L, 128, NPV), np.float32)

    def put(l, name, arr):
        o, n = PV[name]
        assert arr.shape == (128, n), (name, arr.shape)
        pv[l, :, o:o + n] = arr
    for l in range(L):
        put(l, 'bin', _fm(bn[l]))
        put(l, 's5d', _fm(inp['s5_d'][l]))
        put(l, 'glub', _fm(inp['s5_glu_b'][l]))
        put(l, 'hglb', np.concatenate([_fm(inp['hg_lb'][ll, d]) for ll in range(2) for d in range(2)], 1))
        put(l, 'hgnw', _fm(inp['hg_norm_w'][l]))
        rd = np.zeros((128, 4), np.float32)
        for d in range(2):
            for j in range(2):
                rd[:64, d * 2 + j] = inp['ret_decay'][l, d, 2 * j]
                rd[64:, d * 2 + j] = inp['ret_decay'][l, d, 2 * j + 1]
        put(l, 'rdec', rd)
        put(l, 'rdech', np.ascontiguousarray(np.broadcast_to(inp['ret_decay'][l].reshape(1, 8), (128, 8))))
        mu = np.zeros((2, 7 * 128), np.float32)
        mu[:, :768] = inp['rw_mu'][l][:, :768]
        lv = CMAP[3584:3712]
        ok = lv >= 0
        mu[:, 768:896][:, ok] = inp['rw_mu'][l][:, lv[ok] - 2816]
        put(l, 'mu', np.concatenate([_fm(mu[0]), _fm(mu[1])], 1))
        put(l, 'w0', np.concatenate([_fm(inp['rw_w0'][l, d]) for d in range(2)], 1))
        put(l, 'a0', np.concatenate([_fm(inp['rw_a0'][l, d]) for d in range(2)], 1))
        for nm, key in [('kk', 'rw_kk'), ('ka', 'rw_ka'), ('rk', 'rw_rk'), ('gnw', 'rw_gn_w'), ('gnb', 'rw_gn_b')]:
            put(l, nm, _fm(inp[key][l]))
        put(l, 'adab', _fm(inp['ada_b'][l][:2048]))
        for nm, key in [('lamre', 's5_lam_re'), ('lamim', 's5_lam_im')]:
            a = inp[key][l].reshape(2, 8, 2, 64)
            put(l, nm, np.ascontiguousarray(a.transpose(2, 3, 0, 1).reshape(128, 16)))
        a = np.broadcast_to(inp['s5_log_dt'][l].reshape(2, 8, 2, 1), (2, 8, 2, 64))
        put(l, 'ldt', np.ascontiguousarray(a.transpose(2, 3, 0, 1).reshape(128, 16)))
    sh['pv'] = pv
    bt = np.zeros((L, 128, 2, 4, 2, 128), np.float32)
    ct = np.zeros((L, 128, 8, 2, 128), np.float32)
    for l in range(L):
        for g in range(16):
            i, g2 = g // 2, g % 2
            for q in range(16):
                c = g * 16 + q
                j, p = c // 128, c % 128
                bt[l, p, j, i % 4, 0, g2 * 64:(g2 + 1) * 64] = inp['s5_b_re'][l, g, :, q]
                bt[l, p, j, i % 4, 1, g2 * 64:(g2 + 1) * 64] = inp['s5_b_im'][l, g, :, q]
            m0 = (i % 4) * 32 + g2 * 16
            ct[l, g2 * 64:(g2 + 1) * 64, i, 0, m0:m0 + 16] = inp['s5_c_re'][l, g].T
            ct[l, g2 * 64:(g2 + 1) * 64, i, 1, m0:m0 + 16] = inp['s5_c_im'][l, g].T
    sh['s5bt'] = bt
    sh['s5ct'] = ct
    sh['gluw'] = np.ascontiguousarray(inp['s5_glu_w'].reshape(L, 2, 128, 256).transpose(0, 2, 1, 3))
    lw2 = np.zeros((L, 128, 2, 256), np.float32)
    for l in range(L):
        lw2[l, 0:16, 0] = inp['rw_w2'][l, 0]
        lw2[l, 32:48, 1] = inp['rw_w2'][l, 1]
        lw2[l, 64:80, 0] = inp['rw_a2'][l, 0]
        lw2[l, 80:96, 1] = inp['rw_a2'][l, 1]
    sh['lw2'] = lw2
    sh['wbr'] = np.ascontiguousarray(inp['w_branch'].reshape(L, 4, 2, 128, 1024).transpose(0, 3, 1, 2, 4))
    sh['wout'] = np.ascontiguousarray(inp['w_out'].reshape(L, 8, 128, 1024).transpose(0, 2, 1, 3))
    rows = np.zeros((L, 128, 4096 + 512), np.float32)
    for l in range(L):
        rows[l, :, 0:1024] = inp['b_out'][l][None]
        rows[l, :, 1024:2048] = inp['ln_w'][l][None]
        rows[l, :, 2048:3072] = inp['ln_b'][l][None]
        rows[l, :, 3072:4096] = inp['ada_b'][l][None, 2048:3072]
        rows[l, :, 4096:4352] = inp['b_in'][l][None, 1280:1536]
        rows[l, :, 4352:4608] = inp['b_in'][l][None, 2304:2560]
    sh['rows'] = rows
    sh['masks'] = _masks()
    cos, sins, pm = _rot_tables()
    sh['rcos'] = cos
    sh['rsin'] = sins
    t = np.arange(128)
    cst = np.zeros((128, 9, 128), np.float32)
    cst[:, 0] = pm
    cst[:, 1] = ((t[:, None] // 64) == (t[None, :] // 64))
    cst[:, 2] = np.maximum(t[None, :] - t[:, None], 0)
    cst[:, 3] = np.maximum(t[:, None] - t[None, :], 0)
    cst[:, 4] = (t[None, :] >= t[:, None])
    cst[:, 5] = (t[None, :] <= t[:, None])
    cst[:, 6, :64] = ((t[:, None] % 64) == np.arange(64)[None, :])
    cst[:, 6, 64:68] = ((t[:, None] // 32) == np.arange(4)[None, :])
    cst[:, 6, 68] = 127 - t
    cst[:, 6, 69] = t
    cst[:, 7] = t[None, :] + 1.0
    cst[:, 8] = 128.0 - t[None, :]
    sh['cst'] = cst
    return sh


def prep_core(inp, b):
    pc = {}
    pc['hin'] = np.ascontiguousarray(np.concatenate([inp['ctx'][b], inp['x'][b]], 0))
    cv = np.stack([inp['c'][b], inp['c_ctx']], -1)
    pc['cvec'] = np.ascontiguousarray(cv.reshape(8, 128, 2).transpose(1, 0, 2))
    return pc


SHAPES = dict(hin=[NT, DM], cvec=[128, 8, 2], w_in=[2, 128, 8, NCOL], ada_w=[2, 128, 8, 3072], pv=[2, 128, NPV],
              s5bt=[2, 128, 2, 4, 2, 128], s5ct=[2, 128, 8, 2, 128], gluw=[2, 128, 2, 256], lw2=[2, 128, 2, 256],
              wbr=[2, 128, 4, 2, 1024], wout=[2, 128, 8, 1024], rows=[2, 128, 4608], masks=[128, 18, 128],
              rcos=[128, 2048], rsin=[128, 2048], cst=[128, 9, 128])


def build(debug=(), nlayers=2, phases=('s5', 'hg', 'ret', 'rw', 'merge'), stop=None):
    nc = bass.Bass("TRN2", target_bir_lowering=False)
    S = Sched(nc)
    dr = {k: nc.dram_tensor(k, list(v), F32, kind="ExternalInput").ap() for k, v in SHAPES.items()}
    out_d = nc.dram_tensor("out", [2048, DM], F32, kind="ExternalOutput").ap()
    h1_d = nc.dram_tensor("h1", [NT, DM], F32, kind="Internal").ap()
    sgd = nc.dram_tensor("sgd", [32, 128, NT], BF16, kind="Internal").ap()
    pre_sg = set()
    dbg_d = {}

    def dbg_out(name, shape):
        dbg_d[name] = nc.dram_tensor("dbg_" + name, list(shape), F32, kind="ExternalOutput").ap()
        return dbg_d[name]

    uid = [0]

    def key(p='k'):
        uid[0] += 1
        return '%s%d' % (p, uid[0])

    with contextlib.ExitStack() as top:
        S.sems = {n: top.enter_context(nc.semaphore(n)) for n in S.sem_names()}

        minrem = {}

        def sb(st, name, shape, dt=F32):
            uid[0] += 1
            t_ = st.enter_context(nc.sbuf_tensor("%s_%d" % (name, uid[0]), list(shape), dt))
            if MEMDBG:
                pre = name[:2]
                minrem[pre] = min(minrem.get(pre, 1 << 30), nc.sbuf_bytes_remaining)
            return t_

        def ps(st, name, shape, dt=F32):
            uid[0] += 1
            return st.enter_context(nc.psum_tensor("%s_%d" % (name, uid[0]), list(shape), dt))

        def mm(out, lhsT, rhs, r, w, start=True, stop=True):
            S.op('pe', lambda e: e.matmul(out, lhsT=lhsT, rhs=rhs, start=start, stop=stop), reads=r, writes=w)

        def tr(out, in_, ident, r, w):
            S.op('pe', lambda e: e.transpose(out, in_, ident), reads=r, writes=w)

        def act(out, in_, func, r, w, bias=0.0, scale=1.0):
            S.op('act', lambda e: e.activation(out=out, in_=in_, func=func, bias=bias, scale=scale), reads=r, writes=w)

        def tt(en, out, in0, in1, op, r, w):
            S.op(en, lambda e: e.tensor_tensor(out=out, in0=in0, in1=in1, op=op), reads=r, writes=w)

        def ts(en, out, in0, s1, s2, op0, op1, r, w):
            if s2 is None:
                S.op(en, lambda e: e.tensor_scalar(out=out, in0=in0, scalar1=s1, scalar2=None, op0=op0), reads=r, writes=w)
            else:
                S.op(en, lambda e: e.tensor_scalar(out=out, in0=in0, scalar1=s1, scalar2=s2, op0=op0, op1=op1),
                     reads=r, writes=w)

        def stt(out, in0, sc, in1, op0, op1, r, w):
            S.op('dve', lambda e: e.scalar_tensor_tensor(out=out, in0=in0, scalar=sc, in1=in1, op0=op0, op1=op1),
                 reads=r, writes=w)

        def cp(en, out, in_, r, w):
            if en == 'act':
                S.op('act', lambda e: e.copy(out=out, in_=in_), reads=r, writes=w)
            else:
                S.op(en, lambda e: e.tensor_copy(out=out, in_=in_), reads=r, writes=w)

        def memset(en, ap, val, w):
            S.op(en, lambda e: e.memset(ap, val), writes=w)

        def run_pipelined(gens, stagger):
            it = iter(gens)
            active, pending, rounds = [], True, 0
            while pending or active:
                if pending and rounds % stagger == 0:
                    try:
                        active.append(next(it))
                    except StopIteration:
                        pending = False
                for g in list(active):
                    try:
                        next(g)
                    except StopIteration:
                        active.remove(g)
                rounds += 1

        def mkbanks(st_, n, prefix):
            bl = [ps(st_, "%s%d" % (prefix, i), [128, 512], F32) for i in range(n)]
            cnt = [0]

            def bank():
                i = cnt[0] % n
                cnt[0] += 1
                return bl[i], '%s%d' % (prefix, i)
            return bank

        def gate_jobs(l, last, st_, bankfn, kds):
            wgt = [sb(st_, "gjw%d" % i, [128, 8, 128], BF16) for i in range(2)]
            sgs = [sb(st_, "gjs%d" % i, [128, 512], BF16) for i in range(2)]
            cnt = [0]

            def job(i, kd):
                k, dt_ = kd // 8, kd % 8
                w_, wk_ = wgt[i % 2], 'gjw%d' % (i % 2)
                c0 = 3968 + k * 1024 + dt_ * 128
                S.dma('pool', w_[:], dr['w_in'][l][:, :, c0:c0 + 128], writes=[wk_])
                yield
                for (n0, nn) in BLOCKS:
                    if last and n0 < 256:
                        continue
                    pg_, pgk_ = bankfn()
                    for jj in range(8):
                        mm(pg_[:, 0:nn], w_[:, jj, :], uT[:, jj, n0:n0 + nn], [wk_] + uTk[n0 // 128:(n0 + nn) // 128], [pgk_],
                           start=(jj == 0), stop=(jj == 7))
                    yield
                    c_ = cnt[0] % 2
                    cnt[0] += 1
                    act(sgs[c_][:, 0:nn], pg_[:, 0:nn], AF.Sigmoid, [pgk_, 'pvt'], ['gjs%d' % c_], bias=pv('bin', 31 + k * 8 + dt_))
                    yield
                    S.dma('sp', sgd[kd][:, n0:n0 + nn], sgs[c_][:, 0:nn], reads=['gjs%d' % c_], writes=['sgd'])
                    yield
                pre_sg.add((l, kd))
            return [job(i, kd) for i, kd in enumerate(kds)]

        def interleave(main, extra, every):
            out, ei = [], 0
            extra = list(extra)
            for i, g in enumerate(main):
                out.append(g)
                if (i + 1) % every == 0 and ei < len(extra):
                    out.append(extra[ei])
                    ei += 1
            out.extend(extra[ei:])
            return out

        def dbg_dump(name, ap, shape, r):
            if name in debug:
                d = dbg_out(name, shape)
                S.dma('sp', d, ap, reads=r, writes=['dbgout_' + name])

        cstb = sb(top, "cstb", [128, 3, 128], BF16)
        cstf = sb(top, "cstf", [128, 7, 128], F32)
        maskb = sb(top, "maskb", [128, 18, 128], BF16)
        silc = sb(top, "silc", [128, 8, 2], F32)
        S.dma('pool', cstb[:, 0:2, :], dr['cst'][:, 0:2, :], writes=['cstb'])
        S.dma('sp', cstf[:], dr['cst'][:, 2:9, :], writes=['cstf'])
        S.dma('pool', maskb[:], dr['masks'], writes=['maskb'])
        S.dma('sp', silc[:], dr['cvec'], writes=['silc'])
        memset('pool', cstb[:, 2, :], 0.0, ['cstb'])
        S.op('pool', lambda e: e.affine_select(out=cstb[:, 2, :], in_=cstb[:, 2, :], pattern=[[-1, 128]],
                                               compare_op=ALU.not_equal, fill=1.0, base=0, channel_multiplier=1),
             reads=['cstb'], writes=['cstb'])
        act(silc[:], silc[:], AF.Silu, ['silc'], ['silc'])
        identb = cstb[:, 2, :]
        bonesb = cstb[:, 1, :]

        uT = sb(top, "uT", [128, 8, NT], BF16)
        Y = sb(top, "Y", [128, 4, 2, NT], BF16)
        pvt = sb(top, "pvt", [128, NPV], F32)
        if debug:
            memset('pool', Y[:], 0.0, ['Y0', 'Y1', 'Y2', 'Y3'])
        modfm = sb(top, "modfm", [128, 16, 2], F32)
        gatebc = sb(top, "gatebc", [128, 2, DM], F32)

        def pv(name, j=None, n=1):
            o, cnt = PV[name]
            if j is None:
                return pvt[:, o:o + cnt]
            return pvt[:, o + j:o + j + n]

        PHASES = {}
        def proj_fm(st, wt, wk, mlist, evac, pp, ppk):
            cnt = 0
            for (n0, nn) in BLOCKS:
                for mi, m in enumerate(mlist):
                    p_, pk_ = pp[cnt % len(pp)], ppk[cnt % len(pp)]
                    cnt += 1
                    for j in range(8):
                        mm(p_[:, 0:nn], wt[:, j, m * 128:(m + 1) * 128], uT[:, j, n0:n0 + nn],
                           ['%s%d' % (wk, m // 2)] + uTk[n0 // 128:(n0 + nn) // 128], [pk_], start=(j == 0), stop=(j == 7))
                    evac(mi, m, n0, nn, p_, pk_)

        def phase_s5(l, h_src, last):
            L = 128
            with contextlib.ExitStack() as st:
                btb = sb(st, "btb", [128, 2, 4, 2, 128], BF16)
                ctb = sb(st, "ctb", [128, 8, 2, 128], BF16)
                glub = sb(st, "glub", [128, 2, 256], BF16)
                S.dma('pool', btb[:], dr['s5bt'][l], writes=['btb'])
                S.dma('pool', ctb[:], dr['s5ct'][l], writes=['ctb'])
                S.dma('pool', glub[:], dr['gluw'][l], writes=['glub'])
                ts('pool', ctb[:, :, 1, :], ctb[:, :, 1, :], -1.0, 0.0, ALU.mult, ALU.add, ['ctb'], ['ctb'])
                ub = sb(st, "s5u", [128, 2, NT], BF16)
                zs = sb(st, "s5z", [128, 2, NT], BF16)
                yacc = sb(st, "yacc", [128, 2, NT], F32)
                PT = sb(st, "s5PT", [128, 16, 2, L], F32)
                QT = sb(st, "s5QT", [128, 16, 2, L], F32)
                sst = sb(st, "s5st", [128, 16, 2], F32)
                ones = sb(st, "s5ones", [128, L], F32)
                memset('pool', yacc[:], 0.0, ['yacc'])
                memset('pool', sst[:], 0.0, ['sst'])
                memset('pool', ones[:], 1.0, ['s5ones'])
                with contextlib.ExitStack() as st2:
                    wsu = sb(st2, "wsu", [128, 8, 512], BF16)
                    for pc_ in range(2):
                        S.dma('pool', wsu[:, :, pc_ * 256:(pc_ + 1) * 256], dr['w_in'][l][:, :, pc_ * 256:(pc_ + 1) * 256], writes=['wsu%d' % pc_])
                    pp = [ps(st2, "s5pp%d" % i, [128, 512], F32) for i in range(2)]

                    def evac(mi, m, n0, nn, p_, pk_):
                        if m < 2:
                            act(ub[:, m, n0:n0 + nn], p_[:, 0:nn], AF.Identity, [pk_, 'pvt'], ['s5u'], bias=pv('bin', m))
                        else:
                            act(zs[:, m - 2, n0:n0 + nn], p_[:, 0:nn], AF.Silu, [pk_, 'pvt'], ['s5z'], bias=pv('bin', m))
                    proj_fm(st2, wsu, 'wsu', [0, 1, 2, 3], evac, pp, ['s5pp0', 's5pp1'])
                    sm = sb(st2, "s5sm", [128, 20, 16], F32)
                    K_ = 's5sm'

                    def Sm(i):
                        return sm[:, i, :]

                    def T2(o, a, b, op):
                        tt('dve', Sm(o), a if not isinstance(a, int) else Sm(a), b if not isinstance(b, int) else Sm(b), op,
                           [K_, 'pvt'], [K_])
                    lamre, lamim = pv('lamre'), pv('lamim')
                    act(Sm(0), pv('ldt'), AF.Exp, ['pvt'], [K_])
                    T2(1, lamre, 0, ALU.mult)
                    act(Sm(2), Sm(1), AF.Exp, [K_], [K_])
                    act(Sm(3), Sm(1), AF.Exp, [K_], [K_], scale=-1.0)
                    T2(4, lamim, 0, ALU.mult)
                    ts('dve', Sm(5), Sm(4), PI / 2, None, ALU.add, None, [K_], [K_])
                    for x in (4, 5):
                        for _ in range(4):
                            ts('dve', Sm(16), Sm(x), PI, 2 * PI, ALU.is_gt, ALU.mult, [K_], [K_])
                            T2(x, x, 16, ALU.subtract)
                        for _ in range(2):
                            ts('dve', Sm(16), Sm(x), -PI, 2 * PI, ALU.is_lt, ALU.mult, [K_], [K_])
                            T2(x, x, 16, ALU.add)
                    act(Sm(6), Sm(4), AF.Sin, [K_], [K_])
                    act(Sm(7), Sm(5), AF.Sin, [K_], [K_])
                    T2(8, 2, 7, ALU.mult)
                    T2(9, 2, 6, ALU.mult)
                    T2(10, 3, 7, ALU.mult)
                    stt(Sm(11), Sm(3), -1.0, Sm(6), ALU.mult, ALU.mult, [K_], [K_])
                    ts('dve', Sm(12), Sm(8), -1.0, None, ALU.add, None, [K_], [K_])
                    T2(16, lamre, lamre, ALU.mult)
                    T2(17, lamim, lamim, ALU.mult)
                    T2(13, 16, 17, ALU.add)
                    S.op('dve', lambda e: e.reciprocal(out=Sm(13), in_=Sm(13)), reads=[K_], writes=[K_])
                    T2(16, 12, lamre, ALU.mult)
                    T2(17, 9, lamim, ALU.mult)
                    T2(16, 16, 17, ALU.add)
                    T2(14, 16, 13, ALU.mult)
                    T2(16, 9, lamre, ALU.mult)
                    T2(17, 12, lamim, ALU.mult)
                    T2(16, 16, 17, ALU.subtract)
                    T2(15, 16, 13, ALU.mult)
                    tmpa = sb(st2, "s5ta", [128, 16, L], F32)
                    tmpb = sb(st2, "s5tb", [128, 16, L], F32)

                    def cmul_bc(dst_re, dst_im, src_re, src_im, s_re, s_im, m):
                        sr = s_re.unsqueeze(2).broadcast_to([128, 16, m])
                        si = s_im.unsqueeze(2).broadcast_to([128, 16, m])
                        ta, tb = tmpa[:, :, 0:m], tmpb[:, :, 0:m]
                        kk_ = ['s5tab', 's5ta', 's5tb', 's5tc', K_]
                        tt('dve', ta, src_re, sr, ALU.mult, kk_, ['s5ta'])
                        tt('dve', tb, src_im, si, ALU.mult, kk_, ['s5tb'])
                        tt('dve', dst_re, ta, tb, ALU.subtract, kk_, ['s5tab'])
                        tt('dve', ta, src_re, si, ALU.mult, kk_, ['s5ta'])
                        tt('dve', tb, src_im, sr, ALU.mult, kk_, ['s5tb'])
                        tt('dve', dst_im, ta, tb, ALU.add, kk_, ['s5tab'])
                    for (TB, a_re, a_im) in ((PT, 8, 9), (QT, 10, 11)):
                        cp('dve', TB[:, :, 0, 0], Sm(a_re), [K_], ['s5tab'])
                        cp('dve', TB[:, :, 1, 0], Sm(a_im), [K_], ['s5tab'])
                        m = 1
                        while m < L:
                            cmul_bc(TB[:, :, 0, m:2 * m], TB[:, :, 1, m:2 * m], TB[:, :, 0, 0:m], TB[:, :, 1, 0:m],
                                    TB[:, :, 0, m - 1], TB[:, :, 1, m - 1], m)
                            m *= 2
                    tmpc = sb(st2, "s5tc", [128, 16, L], F32)
                    cp('dve', tmpc[:], QT[:, :, 0, :], ['s5tab'], ['s5tc'])
                    cmul_bc(QT[:, :, 0, :], QT[:, :, 1, :], tmpc[:], QT[:, :, 1, :], Sm(14), Sm(15), L)
                    S.barrier()
                with contextlib.ExitStack() as st2:
                    NB = 8
                    xa = [sb(st2, "s5xa%d" % i, [128, 2, L], F32) for i in range(NB)]
                    xb_ = [sb(st2, "s5xb%d" % i, [128, 2, L], F32) for i in range(NB)]
                    cw = [sb(st2, "s5cw%d" % i, [128, 2, L], F32) for i in range(NB)]
                    hb = [sb(st2, "s5hb%d" % i, [128, 2, L], BF16) for i in range(NB)]
                    pbu = [ps(st2, "s5pb%d" % i, [128, 2, 2, L], F32) for i in range(4)]
                    py = [ps(st2, "s5py%d" % i, [128, 512], F32) for i in range(2)]
                    orders = [list(range(NTL)), [1, 0] + list(range(NTL - 1, 1, -1))]
                    def s5group(gi, step, d, j):
                        c = orders[d][step]
                        n0 = c * L
                        rev = (d == 1)
                        U = []
                        for ii in range(4):
                            un = gi * 4 + ii
                            bnk = (un // 2) % 4
                            U.append(dict(ii=ii, i=j * 4 + ii, q=d * 8 + j * 4 + ii, pb=pbu[bnk][:, un % 2], pbk='s5pb%d' % bnk,
                                          A=xa[un % NB], Ak='s5xa%d' % (un % NB), B=xb_[un % NB], Bk='s5xb%d' % (un % NB),
                                          C=cw[un % NB], Ck='s5cw%d' % (un % NB), H=hb[un % NB], Hk='s5hb%d' % (un % NB)))
                        for u in U:
                            for ri in range(2):
                                mm(u['pb'][:, ri, :], btb[:, j, u['ii'], ri, :], ub[:, j, n0:n0 + L], ['btb', 's5u'], [u['pbk']])
                        yield
                        for u in U:
                            src = u['pb'][:, :, ::-1] if rev else u['pb'][:, :, :]
                            tt('dve', u['A'][:], src, QT[:, u['q'], 0:1, :].broadcast_to([128, 2, L]), ALU.mult,
                               [u['pbk'], 's5tab'], [u['Ak']])
                        yield
                        for u in U:
                            src = u['pb'][:, ::-1, ::-1] if rev else u['pb'][:, ::-1, :]
                            tt('dve', u['B'][:], src, QT[:, u['q'], 1:2, :].broadcast_to([128, 2, L]), ALU.mult,
                               [u['pbk'], 's5tab'], [u['Bk']])
                        yield
                        for u in U:
                            tt('dve', u['A'][:, 0, :], u['A'][:, 0, :], u['B'][:, 0, :], ALU.subtract, [u['Ak'], u['Bk']], [u['Ak']])
                        yield
                        for u in U:
                            tt('dve', u['A'][:, 1, :], u['A'][:, 1, :], u['B'][:, 1, :], ALU.add, [u['Ak'], u['Bk']], [u['Ak']])
                        yield
                        for ri in range(2):
                            for u in U:
                                q = u['q']
                                S.op('dve', lambda e, u=u, ri=ri, q=q: e.tensor_tensor_scan(
                                    out=u['C'][:, ri, :], data0=ones[:], data1=u['A'][:, ri, :], initial=sst[:, q, ri:ri + 1],
                                    op0=ALU.mult, op1=ALU.add), reads=[u['Ak'], 's5ones', 'sst%d' % q, 'sst'], writes=[u['Ck']])
                            yield
                        for u in U:
                            tt('pool', u['A'][:], u['C'][:], PT[:, u['q'], 0:1, :].broadcast_to([128, 2, L]), ALU.mult,
                               [u['Ck'], 's5tab', u['Ak']], [u['Ak']])
                        yield
                        for u in U:
                            tt('pool', u['B'][:], u['C'][:, ::-1, :], PT[:, u['q'], 1:2, :].broadcast_to([128, 2, L]), ALU.mult,
                               [u['Ck'], 's5tab', u['Bk']], [u['Bk']])
                        yield
                        for u in U:
                            tt('pool', u['A'][:, 0, :], u['A'][:, 0, :], u['B'][:, 0, :], ALU.subtract, [u['Ak'], u['Bk']], [u['Ak']])
                        yield
                        for u in U:
                            tt('pool', u['A'][:, 1, :], u['A'][:, 1, :], u['B'][:, 1, :], ALU.add, [u['Ak'], u['Bk']], [u['Ak']])
                        yield
                        for u in U:
                            cp('pool', sst[:, u['q'], :], u['A'][:, :, L - 1], [u['Ak']], ['sst%d' % u['q']])
                        yield
                        for u in U:
                            hsrc = u['A'][:, :, ::-1] if rev else u['A'][:]
                            cp('act', u['H'][:], hsrc, [u['Ak']], [u['Hk']])
                        yield
                        pyr = py[gi % 2][:, 0:L]
                        pyk = 's5py%d' % (gi % 2)
                        for k_, u in enumerate(U):
                            for ri in range(2):
                                mm(pyr, ctb[:, u['i'], ri, :], u['H'][:, ri, :], ['ctb', u['Hk']], [pyk],
                                   start=(k_ == 0 and ri == 0), stop=(k_ == 3 and ri == 1))
                        yield
                        yield
                        yield
                        tt('dve', yacc[:, j, n0:n0 + L], yacc[:, j, n0:n0 + L], pyr, ALU.add, [pyk, 'yacc'], ['yacc'])

                    glist = [(step, d, j) for step in range(NTL) for d in range(2) for j in range(2)]
                    gbank = mkbanks(st2, 2, "s5gk") if (GATE_PRE and GJ_S5) else None
                    gj = gate_jobs(l, last, st2, gbank, GJ_S5) if (GATE_PRE and GJ_S5) else []
                    run_pipelined(interleave([s5group(gi, *g) for gi, g in enumerate(glist)], gj, 2), STG['s5'])
                    S.barrier()
                for j in range(2):
                    stt(yacc[:, j, :], ub[:, j, :], pv('s5d', j), yacc[:, j, :], ALU.mult, ALU.add, ['s5u', 'yacc', 'pvt'],
                        ['yacc'])
                dbg_dump('ya%d' % l, yacc[:], [128, 2, NT], ['yacc'])
                with contextlib.ExitStack() as st2:
                    t1 = [sb(st2, "s5g1_%d" % i, [128, 512], F32) for i in range(2)]
                    t2 = [sb(st2, "s5g2_%d" % i, [128, 512], BF16) for i in range(2)]
                    pg = [ps(st2, "s5pg%d" % i, [128, 512], F32) for i in range(2)]
                    cnt = 0
                    for (n0, nn) in BLOCKS:
                        for j in range(2):
                            a, ak = t1[cnt % 2], 's5g1_%d' % (cnt % 2)
                            cnt += 1
                            ysl = yacc[:, j, n0:n0 + nn]
                            act(a[:, 0:nn], ysl, AF.Square, ['yacc'], [ak])
                            ts('dve', a[:, 0:nn], a[:, 0:nn], 0.044715, 1.0, ALU.mult, ALU.add, [ak], [ak])
                            tt('dve', a[:, 0:nn], a[:, 0:nn], ysl, ALU.mult, [ak, 'yacc'], [ak])
                            act(a[:, 0:nn], a[:, 0:nn], AF.Sigmoid, [ak], [ak], scale=1.5957691216057308)
                            tt('dve', ub[:, j, n0:n0 + nn], a[:, 0:nn], ysl, ALU.mult, [ak, 'yacc'], ['s5u'])
                    cnt = 0
                    for (n0, nn) in BLOCKS:
                        for m in range(2):
                            p_, pk_ = pg[cnt % 2], 's5pg%d' % (cnt % 2)
                            b_, bk_ = t2[cnt % 2], 's5g2_%d' % (cnt % 2)
                            cnt += 1
                            for jc in range(2):
                                mm(p_[:, 0:nn], glub[:, jc, m * 128:(m + 1) * 128], ub[:, jc, n0:n0 + nn], ['glub', 's5u'], [pk_],
                                   start=(jc == 0), stop=(jc == 1))
                            act(b_[:, 0:nn], p_[:, 0:nn], AF.Sigmoid, [pk_, 'pvt'], [bk_], bias=pv('glub', m))
                            tt('dve', b_[:, 0:nn], b_[:, 0:nn], ub[:, m, n0:n0 + nn], ALU.mult, [bk_, 's5u'], [bk_])
                            tt('pool', Y[:, 0, m, n0:n0 + nn], b_[:, 0:nn], zs[:, m, n0:n0 + nn], ALU.mult, [bk_, 's5z'], ['Y0'])
                    S.barrier()
                S.barrier()
        PHASES['s5'] = phase_s5
        def phase_hg(l, h_src, last):
            with contextlib.ExitStack() as st:
                QP = [sb(st, "hgQP%d" % d, [128, 2, NT], BF16) for d in range(2)]
                KP = [sb(st, "hgKP%d" % d, [128, 2, NT], BF16) for d in range(2)]
                G = sb(st, "hgG", [128, 2, 72, 2], F32)
                VT = sb(st, "hgVT", [128, NTL, 256], BF16)
                zs = sb(st, "hgzs", [128, 2, NT], BF16)
                lbt = sb(st, "hglbt", [128, 2, 4], F32)
                if l == 0:
                    memset('pool', lbt[:, 0, :], 0.0, ['hglbt'])
                    memset('pool', lbt[:, 1, :], 1.0, ['hglbt'])
                else:
                    o_, _ = PV['hglb']
                    tt('dve', lbt[:, 0, :], pvt[:, o_ + 4:o_ + 8], pvt[:, o_:o_ + 4], ALU.subtract, ['pvt'], ['hglbt'])
                    act(lbt[:, 0, :], lbt[:, 0, :], AF.Sigmoid, ['hglbt'], ['hglbt'])
                    ts('dve', lbt[:, 1, :], lbt[:, 0, :], -1.0, 1.0, ALU.mult, ALU.add, ['hglbt'], ['hglbt'])
                with contextlib.ExitStack() as st2:
                    wh = sb(st2, "hgw", [128, 8, 1280], BF16)
                    for pc_ in (0, 4, 1, 2, 3):
                        S.dma('pool', wh[:, :, pc_ * 256:(pc_ + 1) * 256], dr['w_in'][l][:, :, 512 + pc_ * 256:512 + (pc_ + 1) * 256], writes=['hgw%d' % pc_])
                    brow = sb(st2, "hgbrow", [128, 256], F32)
                    S.dma('sp', brow[:], dr['rows'][l][:, 4096:4352], writes=['hgbrow'])
                    R32 = sb(st2, "hgR32", [128, 512], F32)
                    memset('pool', R32[:], 1.0, ['hgR32'])
                    memset('pool', R32[:, 0:512:32], 0.0, ['hgR32'])
                    QS = [sb(st2, "hgQS%d" % i, [128, 2, 512], BF16) for i in range(2)]
                    T = [[sb(st2, "hgT%d_%d" % (i, k), [128, 512], F32) for k in range(4)] for i in range(2)]
                    pp = [ps(st2, "hgpp%d" % i, [128, 512], F32) for i in range(3)]
                    pt = [ps(st2, "hgpt%d" % i, [128, 512], F32) for i in range(2)]
                    def hgproj(cnt, ic, m, n0, nn):
                        ukeys = uTk[n0 // 128:(n0 + nn) // 128]
                        p_, pk_ = pp[cnt % 3], 'hgpp%d' % (cnt % 3)
                        bi = (n0 // 512) % 2 if n0 else 0
                        for jj in range(8):
                            mm(p_[:, 0:nn], wh[:, jj, m * 128:(m + 1) * 128], uT[:, jj, n0:n0 + nn], ['hgw%d' % (m // 2)] + ukeys, [pk_],
                               start=(jj == 0), stop=(jj == 7))
                        yield
                        bias = pv('bin', 4 + m)
                        if m < 2:
                            act(QS[bi][:, m, 0:nn], p_[:, 0:nn], AF.Silu, [pk_, 'pvt'], ['hgQS%d' % bi], bias=bias)
                            return
                        if m >= 8:
                            act(zs[:, m - 8, n0:n0 + nn], p_[:, 0:nn], AF.Silu, [pk_, 'pvt'], ['hgzs'], bias=bias)
                            return
                        d, j = (m - 2) // 2, (m - 2) % 2
                        Ts = T[ic % 2]
                        Tk = ['hgT%d_%d' % (ic % 2, k) for k in range(4)]
                        t1, t2, t3, t4 = [x[:, 0:nn] for x in Ts]
                        act(t1, p_[:, 0:nn], AF.Sigmoid, [pk_, 'pvt'], [Tk[0]], bias=bias)
                        yield
                        ts('dve', t1, t1, lbt[:, 1, d * 2 + j:d * 2 + j + 1], lbt[:, 0, d * 2 + j:d * 2 + j + 1], ALU.mult, ALU.add,
                           [Tk[0], 'hglbt'], [Tk[0]])
                        yield
                        act(t2, t1, AF.Ln, [Tk[0]], [Tk[1]])
                        yield
                        if d == 0:
                            S.op('dve', lambda e: e.tensor_tensor_scan(out=t3, data0=R32[:, 0:nn], data1=t2, initial=0.0,
                                                                       op0=ALU.mult, op1=ALU.add),
                                 reads=[Tk[1], 'hgR32'], writes=[Tk[2]])
                        else:
                            S.op('dve', lambda e: e.tensor_tensor_scan(out=t3[:, ::-1],
                                                                       data0=R32[:, 0:nn], data1=t2[:, ::-1], initial=0.0,
                                                                       op0=ALU.mult, op1=ALU.add),
                                 reads=[Tk[1], 'hgR32'], writes=[Tk[2]])
                        yield
                        ts('dve', t3, t3, -80.0, None, ALU.max, None, [Tk[2]], [Tk[2]])
                        ts('dve', t1, t1, -1.0, 1.0, ALU.mult, ALU.add, [Tk[0]], [Tk[0]])
                        yield
                        act(t4, t3, AF.Exp, [Tk[2]], [Tk[3]])
                        act(t2, t3, AF.Exp, [Tk[2]], [Tk[1]], scale=-1.0)
                        yield
                        tt('pool', KP[d][:, j, n0:n0 + nn], t1, t2, ALU.mult, [Tk[0], Tk[1]], ['hgKP%d' % d])
                        tt('pool', QP[d][:, j, n0:n0 + nn], QS[bi][:, j, 0:nn], t4, ALU.mult, ['hgQS%d' % bi, Tk[3]], ['hgQP%d' % d])
                        c0 = n0 // 32
                        gsrc = t4[:, 31::32] if d == 0 else t4[:, 0::32]
                        cp('act', G[:, d, c0:c0 + nn // 32, j], gsrc, [Tk[3]], ['hgG'])

                    plist = []
                    cnt = 0
                    ic = 0
                    for (n0, nn) in BLOCKS:
                        for m in (0, 1, 8, 9, 2, 3, 4, 5):
                            plist.append((cnt, ic, m, n0, nn))
                            cnt += 1
                            if 2 <= m < 8:
                                ic += 1
                    run_pipelined((hgproj(*p) for p in plist), STG['hgproj'])
                    for t in range(NTL):
                        p_, pk_ = pt[t % 2], 'hgpt%d' % (t % 2)
                        for jj in range(8):
                            mm(p_[:, 0:256], uT[:, jj, t * 128:(t + 1) * 128], wh[:, jj, 768:1024], ['hgw3', uTk[t]], [pk_],
                               start=(jj == 0), stop=(jj == 7))
                        tt('dve', VT[:, t, :], p_[:, 0:256], brow[:], ALU.add, [pk_, 'hgbrow'], ['hgVT'])
                    S.barrier()
                Sall = [sb(st, "hgSall%d" % d, [128, 2, 72, 64], BF16) for d in range(2)]
                with contextlib.ExitStack() as st2:
                    Sst = [sb(st2, "hgS%d" % d, [128, 2, 64], F32) for d in range(2)]
                    kTm = [sb(st2, "hgkTm%d" % i, [128, 4, 256], BF16) for i in range(3)]
                    Ug = [sb(st2, "hgUg%d" % i, [128, 4, 2, 64], F32) for i in range(3)]
                    ptr = [ps(st2, "hgptr%d" % i, [128, 8, 128], BF16) for i in range(2)]
                    pU = [ps(st2, "hgpU%d" % i, [128, 4, 2, 64], F32) for i in range(3)]
                    orders = [list(range(NTL)), [1, 0] + list(range(NTL - 1, 1, -1))]
                    for d in range(2):
                        memset('pool', Sst[d][:], 0.0, ['hgS%d' % d])
                    def hgchain(it, step, d):
                        t = orders[d][step]
                        pr, prk = ptr[it % 2], 'hgptr%d' % (it % 2)
                        km, kmk = kTm[it % 3], 'hgkTm%d' % (it % 3)
                        pu, puk = pU[it % 3], 'hgpU%d' % (it % 3)
                        ug, ugk = Ug[it % 3], 'hgUg%d' % (it % 3)
                        for j in range(2):
                            tr(pr[:, j, :], KP[d][:, j, t * 128:(t + 1) * 128], identb, ['hgKP%d' % d, 'cstb'], [prk])
                        yield
                        for cc in range(4):
                            prf = pr[:, 0:2, :].rearrange("p a b -> p (a b)")
                            if cc % 2 == 0:
                                ts('dve', km[:, cc, :], prf, cstf[:, 4, 64 + cc:64 + cc + 1], None, ALU.mult, None, [prk, 'cstf'], [kmk])
                            else:
                                act(km[:, cc, :], prf, AF.Identity, [prk, 'cstf'], [kmk], scale=cstf[:, 4, 64 + cc:64 + cc + 1])
                        yield
                        for cc in range(4):
                            for h in range(4):
                                hp = (h % 2) * 64
                                mm(pu[hp:hp + 64, cc, h // 2, :], km[:, cc, h * 64:(h + 1) * 64], VT[:, t, h * 64:(h + 1) * 64],
                                   [kmk, 'hgVT'], [puk])
                        yield
                        tt('dve', ug[:], pu[:], G[:, d, t * 4:(t + 1) * 4, :].unsqueeze(3).broadcast_to([128, 4, 2, 64]), ALU.mult,
                           [puk, 'hgG'], [ugk])
                        yield
                        ccs = range(4) if d == 0 else range(3, -1, -1)
                        for cc in ccs:
                            c = t * 4 + cc
                            cp('act', Sall[d][:, :, c, :], Sst[d][:], ['hgS%d' % d], ['hgSall%d_%d' % (d, t)])
                            for j in range(2):
                                stt(Sst[d][:, j, :], Sst[d][:, j, :], G[:, d, c, j:j + 1], ug[:, cc, j, :], ALU.mult, ALU.add,
                                    ['hgS%d' % d, 'hgG', ugk], ['hgS%d' % d])
                            yield

                    gbank = mkbanks(st2, 3, "hggk") if GJ_SPLIT[0] else None
                    gj = gate_jobs(l, last, st2, gbank, GJ_SPLIT[0]) if (GATE_PRE and GJ_SPLIT[0]) else []
                    run_pipelined(interleave([hgchain(i_, sd[0], sd[1]) for i_, sd in enumerate([(s_, d_) for s_ in range(NTL) for d_ in range(2)])], gj, 3), STG['hgchain'])
                    S.barrier()
                with contextlib.ExitStack() as st2:
                    if ('yb%d' % l) in debug:
                        dbgbuf = sb(st2, "dbgbuf", [128, 2, NT], F32)
                    AT = [[sb(st2, "hgAT%d_%d" % (i, d), [128, 4, 128], BF16) for d in range(2)] for i in range(3)]
                    sq = [sb(st2, "hgsq%d" % i, [128, 2, 128], BF16) for i in range(3)]
                    rr = [sb(st2, "hgrr%d" % i, [128, 2, 128], F32) for i in range(3)]
                    ob = [sb(st2, "hgob%d" % i, [128, 2, 128], F32) for i in range(3)]
                    bank = mkbanks(st2, 8, "hgbk")

                    def hgout(t):
                        i2 = t % 3
                        tsl = slice(t * 128, (t + 1) * 128)
                        pas = {}
                        for d in range(2):
                            for par in range(2):
                                pas[(d, par)] = bank()
                            for h in range(4):
                                hp = (h % 2) * 64
                                pa, pak = pas[(d, h % 2)]
                                pav = pa[:, 0:256].rearrange("p (a b) -> p a b", a=2)
                                mm(pav[:, h // 2, :], KP[d][hp:hp + 64, h // 2, tsl], QP[d][hp:hp + 64, h // 2, tsl],
                                   ['hgKP%d' % d, 'hgQP%d' % d], [pak])
                        yield
                        for d in range(2):
                            for par in range(2):
                                pa, pak = pas[(d, par)]
                                pav = pa[:, 0:256].rearrange("p (a b) -> p a b", a=2)
                                tt('dve', AT[i2][d][:, par::2, :], pav, maskb[:, d, :].unsqueeze(1).broadcast_to([128, 2, 128]), ALU.mult,
                                   [pak, 'maskb'], ['hgAT%d_%d' % (i2, d)])
                        yield
                        pos = [bank() for _ in range(2)]
                        povs = [pos[par][0][:, 0:256].rearrange("p (a b) -> p a b", a=2) for par in range(2)]
                        for h in range(4):
                            hp = (h % 2) * 64
                            pok = pos[h % 2][1]
                            reg = povs[h % 2][hp:hp + 64, h // 2, :]
                            first = True
                            for d in range(2):
                                mm(reg, VT[:, t, h * 64:(h + 1) * 64], AT[i2][d][:, h, :], ['hgVT', 'hgAT%d_%d' % (i2, d)], [pok],
                                   start=first, stop=False)
                                first = False
                                for cc in range(4):
                                    c = t * 4 + cc
                                    mm(reg[:, cc * 32:(cc + 1) * 32], Sall[d][hp:hp + 64, h // 2, c, :],
                                       QP[d][hp:hp + 64, h // 2, t * 128 + cc * 32:t * 128 + (cc + 1) * 32],
                                       ['hgSall%d_%d' % (d, t), 'hgQP%d' % d], [pok], start=False, stop=(d == 1 and cc == 3))
                        yield
                        obk = 'hgob%d' % i2
                        cp('act', ob[i2][0:64], povs[0][0:64], [pos[0][1]], [obk])
                        cp('dve', ob[i2][64:128], povs[1][64:128], [pos[1][1]], [obk])
                        yield
                        pov = ob[i2][:]
                        pok = obk
                        if ('yb%d' % l) in debug:
                            cp('pool', dbgbuf[:, :, tsl], pov, [pok], ['dbgbuf'])
                        act(sq[i2][:], pov, AF.Square, [pok], ['hgsq%d' % i2])
                        yield
                        pss_, psk = bank()
                        psv = pss_[:, 0:256].rearrange("p (a b) -> p a b", a=2)
                        for j in range(2):
                            mm(psv[:, j, :], bonesb, sq[i2][:, j, :], ['cstb', 'hgsq%d' % i2], [psk])
                        yield
                        act(rr[i2][:], psv, AF.Ln, [psk], ['hgrr%d' % i2], bias=RMS_EPS, scale=1.0 / 64)
                        yield
                        act(rr[i2][:], rr[i2][:], AF.Exp, ['hgrr%d' % i2], ['hgrr%d' % i2], scale=-0.5)
                        yield
                        tt('dve', rr[i2][:], pov, rr[i2][:], ALU.mult, [pok, 'hgrr%d' % i2], ['hgrr%d' % i2])
                        yield
                        for j in range(2):
                            stt(Y[:, 1, j, tsl], rr[i2][:, j, :], pv('hgnw', j), zs[:, j, tsl], ALU.mult, ALU.mult,
                                ['hgrr%d' % i2, 'pvt', 'hgzs'], ['Y1'])

                    gj = gate_jobs(l, last, st2, bank, GJ_SPLIT[1]) if (GATE_PRE and GJ_SPLIT[1]) else []
                    run_pipelined(interleave([hgout(t) for t in range(NTL) if not (last and t < 2 and not debug)], gj, 3), STG['out'])
                    if ('yb%d' % l) in debug:
                        dbg_dump('yb%d' % l, dbgbuf[:], [128, 2, NT], ['dbgbuf'])
                    S.barrier()
                S.barrier()
        PHASES['hg'] = phase_hg
        def phase_ret(l, h_src, last):
            with contextlib.ExitStack() as st:
                QR = sb(st, "rtQR", [128, 2, NT], BF16)
                KR = sb(st, "rtKR", [128, 2, NT], BF16)
                VT = sb(st, "rtVT", [128, NTL, 256], BF16)
                zs = sb(st, "rtzs", [128, 2, NT], BF16)
                Sall = [sb(st, "rtSall%d" % d, [128, 2, NTL, 64], BF16) for d in range(2)]
                LG = sb(st, "rtLG", [128, 4], F32)
                GL = sb(st, "rtGL", [128, 4], F32)
                LGH = sb(st, "rtLGH", [128, 8], F32)
                QDEC = sb(st, "rtQDEC", [128, 2, 2, 128], F32)
                KDEC = sb(st, "rtKDEC", [128, 2, 4], F32)
                DS = sb(st, "rtDS", [128, 4, 128], F32)
                tb8 = sb(st, "rtb8", [128, 2], F32)
                K_ = 'rttab'
                act(LG[:], pv('rdec'), AF.Exp, ['pvt'], [K_])
                ts('dve', LG[:], LG[:], -1.0, None, ALU.mult, None, [K_], [K_])
                act(GL[:], LG[:], AF.Exp, [K_], [K_], scale=128.0)
                act(LGH[:], pv('rdech'), AF.Exp, ['pvt'], [K_])
                ts('dve', LGH[:], LGH[:], -1.0, None, ALU.mult, None, [K_], [K_])
                for d in range(2):
                    for j in range(2):
                        act(QDEC[:, d, j, :], cstf[:, 5 + d, :], AF.Exp, ['cstf', K_], [K_], scale=LG[:, d * 2 + j:d * 2 + j + 1])
                    act(KDEC[:, d, :], LGH[:, d * 4:(d + 1) * 4], AF.Exp, ['cstf', K_], [K_], scale=cstf[:, 4, 68 + d:69 + d])
                with contextlib.ExitStack() as st2:
                    ta = sb(st2, "rtta", [128, 128], F32)
                    tb = sb(st2, "rttb", [128, 128], F32)
                    for h in range(4):
                        act(ta[:], cstf[:, 0, :], AF.Exp, ['cstf', K_], ['rtta'], scale=LGH[:, h:h + 1])
                        tt('dve', ta[:], ta[:], cstf[:, 2, :], ALU.mult, ['rtta', 'cstf'], ['rtta'])
                        act(tb[:], cstf[:, 1, :], AF.Exp, ['cstf', K_], ['rttb'], scale=LGH[:, 4 + h:5 + h])
                        tt('dve', tb[:], tb[:], cstf[:, 3, :], ALU.mult, ['rttb', 'cstf'], ['rttb'])
                        tt('dve', DS[:, h, :], ta[:], tb[:], ALU.add, ['rtta', 'rttb'], [K_])
                    ts('dve', tb8[:], pv('bin', 16, 2), 0.125, None, ALU.mult, None, ['pvt'], [K_])
                    S.barrier()
                if stop == 'ret_tab':
                    return
                with contextlib.ExitStack() as st2:
                    wr = sb(st2, "rtw", [128, 8, 1024], BF16)
                    for pc_ in range(4):
                        S.dma('pool', wr[:, :, pc_ * 256:(pc_ + 1) * 256], dr['w_in'][l][:, :, 1792 + pc_ * 256:1792 + (pc_ + 1) * 256], writes=['rtw%d' % pc_])
                    brow = sb(st2, "rtbrow", [128, 256], F32)
                    S.dma('sp', brow[:], dr['rows'][l][:, 4352:4608], writes=['rtbrow'])
                    COS = sb(st2, "rtcos", [128, 2048], F32)
                    SIN = sb(st2, "rtsin", [128, 2048], F32)
                    permf = sb(st2, "rtperm", [128, 128], F32)
                    S.dma('sp', COS[:], dr['rcos'], writes=['rtcos'])
                    S.dma('act', SIN[:], dr['rsin'], writes=['rtsin'])
                    S.dma('sp', permf[:], dr['cst'][:, 0, :], writes=['rtperm'])
                    qf = [sb(st2, "rtqf%d" % i, [128, 512], F32) for i in range(2)]
                    t1 = [sb(st2, "rtt1_%d" % i, [128, 512], F32) for i in range(2)]
                    pp = [ps(st2, "rtpp%d" % i, [128, 512], F32) for i in range(2)]
                    pq = [ps(st2, "rtpq%d" % i, [128, 512], F32) for i in range(2)]
                    pt = [ps(st2, "rtpt%d" % i, [128, 512], F32) for i in range(2)]
                    def rtproj(cnt, rc, m, n0, nn):
                        ukeys = uTk[n0 // 128:(n0 + nn) // 128]
                        p_, pk_ = pp[cnt % 2], 'rtpp%d' % (cnt % 2)
                        for jj in range(8):
                            mm(p_[:, 0:nn], wr[:, jj, m * 128:(m + 1) * 128], uT[:, jj, n0:n0 + nn], ['rtw%d' % (m // 2)] + ukeys, [pk_],
                               start=(jj == 0), stop=(jj == 7))
                        yield
                        if m >= 6:
                            act(zs[:, m - 6, n0:n0 + nn], p_[:, 0:nn], AF.Silu, [pk_, 'pvt'], ['rtzs'], bias=pv('bin', 14 + m))
                            return
                        isk = m >= 2
                        j = m % 2
                        dst = (KR if isk else QR)[:, j, n0:n0 + nn]
                        dk = 'rtKR' if isk else 'rtQR'
                        if n0 < 256:
                            if isk:
                                act(dst, p_[:, 0:nn], AF.Identity, [pk_, K_], [dk], bias=tb8[:, j:j + 1], scale=0.125)
                            else:
                                act(dst, p_[:, 0:nn], AF.Identity, [pk_, 'pvt'], [dk], bias=pv('bin', 14 + m))
                            return
                        q_, qk_ = qf[rc % 2], 'rtqf%d' % (rc % 2)
                        a_, ak_ = t1[rc % 2], 'rtt1_%d' % (rc % 2)
                        r_, rk_ = pq[rc % 2], 'rtpq%d' % (rc % 2)
                        if isk:
                            act(q_[:, 0:nn], p_[:, 0:nn], AF.Identity, [pk_, K_], [qk_], bias=tb8[:, j:j + 1], scale=0.125)
                        else:
                            act(q_[:, 0:nn], p_[:, 0:nn], AF.Identity, [pk_, 'pvt'], [qk_], bias=pv('bin', 14 + m))
                        yield
                        mm(r_[:, 0:nn], permf[:], q_[:, 0:nn], ['rtperm', qk_], [rk_])
                        yield
                        tsl = slice(n0 - 256, n0 - 256 + nn)
                        tt('dve', a_[:, 0:nn], r_[:, 0:nn], SIN[:, tsl], ALU.mult, [rk_, 'rtsin'], [ak_])
                        tt('pool', q_[:, 0:nn], q_[:, 0:nn], COS[:, tsl], ALU.mult, [qk_, 'rtcos'], [qk_])
                        yield
                        tt('dve', dst, a_[:, 0:nn], q_[:, 0:nn], ALU.add, [ak_, qk_], [dk])

                    plist = []
                    cnt = 0
                    rc = 0
                    for (n0, nn) in BLOCKS:
                        for m in (0, 1, 2, 3, 6, 7):
                            plist.append((cnt, rc, m, n0, nn))
                            cnt += 1
                            if m < 6 and n0 >= 256:
                                rc += 1
                    run_pipelined((rtproj(*p) for p in plist), 2)
                    for t in range(NTL):
                        p_, pk_ = pt[t % 2], 'rtpt%d' % (t % 2)
                        for jj in range(8):
                            mm(p_[:, 0:256], uT[:, jj, t * 128:(t + 1) * 128], wr[:, jj, 512:768], ['rtw2', uTk[t]], [pk_],
                               start=(jj == 0), stop=(jj == 7))
                        tt('dve', VT[:, t, :], p_[:, 0:256], brow[:], ALU.add, [pk_, 'rtbrow'], ['rtVT'])
                    S.barrier()
                if stop == 'ret_proj':
                    return
                with contextlib.ExitStack() as st2:
                    Sst = [sb(st2, "rtS%d" % d, [128, 2, 64], F32) for d in range(2)]
                    kT = [sb(st2, "rtkT%d" % i, [128, 256], BF16) for i in range(3)]
                    ptr = [ps(st2, "rtptr%d" % i, [128, 8, 128], BF16) for i in range(2)]
                    pU = [ps(st2, "rtpU%d" % i, [128, 512], F32) for i in range(3)]
                    orders = [list(range(NTL)), [1, 0] + list(range(NTL - 1, 1, -1))]
                    for d in range(2):
                        memset('pool', Sst[d][:], 0.0, ['rtS%d' % d])
                    def rtchain(it, step, d):
                        t = orders[d][step]
                        pr, prk = ptr[it % 2], 'rtptr%d' % (it % 2)
                        kt, ktk = kT[it % 3], 'rtkT%d' % (it % 3)
                        pu, puk = pU[it % 3], 'rtpU%d' % (it % 3)
                        puv = pu[:, 0:128].rearrange("p (a b) -> p a b", a=2)
                        for j in range(2):
                            tr(pr[:, j, :], KR[:, j, t * 128:(t + 1) * 128], identb, ['rtKR', 'cstb'], [prk])
                        yield
                        tt('dve', kt[:].rearrange("p (h k) -> p h k", h=4), pr[:, 0:2, :].rearrange("p a (b k) -> p (a b) k", b=2),
                           KDEC[:, d, :].unsqueeze(2).broadcast_to([128, 4, 64]), ALU.mult, [prk, K_], [ktk])
                        yield
                        for h in range(4):
                            hp = (h % 2) * 64
                            mm(puv[hp:hp + 64, h // 2, :], kt[:, h * 64:(h + 1) * 64], VT[:, t, h * 64:(h + 1) * 64], [ktk, 'rtVT'], [puk])
                        yield
                        cp('act', Sall[d][:, :, t, :], Sst[d][:], ['rtS%d' % d], ['rtSall%d_%d' % (d, t)])
                        for j in range(2):
                            stt(Sst[d][:, j, :], Sst[d][:, j, :], GL[:, d * 2 + j:d * 2 + j + 1], puv[:, j, :], ALU.mult, ALU.add,
                                ['rtS%d' % d, K_, puk], ['rtS%d' % d])

                    gbank = mkbanks(st2, 3, "rtgk") if GJ_SPLIT[2] else None
                    gj = gate_jobs(l, last, st2, gbank, GJ_SPLIT[2]) if (GATE_PRE and GJ_SPLIT[2]) else []
                    run_pipelined(interleave([rtchain(i_, sd[0], sd[1]) for i_, sd in enumerate([(s_, d_) for s_ in range(NTL) for d_ in range(2)])], gj, 4), STG['rtchain'])
                    S.barrier()
                if stop == 'ret_chain':
                    return
                with contextlib.ExitStack() as st2:
                    if ('yc%d' % l) in debug:
                        dbgbuf = sb(st2, "dbgbuf", [128, 2, NT], F32)
                    AT = [sb(st2, "rtAT%d" % i, [128, 4, 128], BF16) for i in range(3)]
                    qd = [[sb(st2, "rtqd%d_%d" % (i, d), [128, 2, 128], BF16) for d in range(2)] for i in range(3)]
                    sq = [sb(st2, "rtsq%d" % i, [128, 2, 128], BF16) for i in range(3)]
                    rr = [sb(st2, "rtrr%d" % i, [128, 2, 128], F32) for i in range(3)]
                    ob = [sb(st2, "rtob%d" % i, [128, 2, 128], F32) for i in range(3)]
                    bank = mkbanks(st2, 8, "rtbk")

                    def rtout(t):
                        i2 = t % 3
                        tsl = slice(t * 128, (t + 1) * 128)
                        pas = [bank() for _ in range(2)]
                        for h in range(4):
                            hp = (h % 2) * 64
                            pav = pas[h % 2][0][:, 0:256].rearrange("p (a b) -> p a b", a=2)
                            mm(pav[:, h // 2, :], KR[hp:hp + 64, h // 2, tsl], QR[hp:hp + 64, h // 2, tsl], ['rtKR', 'rtQR'], [pas[h % 2][1]])
                        for d in range(2):
                            tt('pool', qd[i2][d][:], QR[:, :, tsl], QDEC[:, d, :, :], ALU.mult, ['rtQR', K_], ['rtqd%d_%d' % (i2, d)])
                        yield
                        for par in range(2):
                            pav = pas[par][0][:, 0:256].rearrange("p (a b) -> p a b", a=2)
                            tt('dve', AT[i2][:, par::2, :], pav, DS[:, par::2, :], ALU.mult, [pas[par][1], K_], ['rtAT%d' % i2])
                        yield
                        pos = [bank() for _ in range(2)]
                        povs = [pos[par][0][:, 0:256].rearrange("p (a b) -> p a b", a=2) for par in range(2)]
                        for h in range(4):
                            hp = (h % 2) * 64
                            pok = pos[h % 2][1]
                            reg = povs[h % 2][hp:hp + 64, h // 2, :]
                            mm(reg, VT[:, t, h * 64:(h + 1) * 64], AT[i2][:, h, :], ['rtVT', 'rtAT%d' % i2], [pok], start=True, stop=False)
                            for d in range(2):
                                mm(reg, Sall[d][hp:hp + 64, h // 2, t, :], qd[i2][d][hp:hp + 64, h // 2, :],
                                   ['rtSall%d_%d' % (d, t), 'rtqd%d_%d' % (i2, d)], [pok], start=False, stop=(d == 1))
                        yield
                        obk = 'rtob%d' % i2
                        cp('act', ob[i2][0:64], povs[0][0:64], [pos[0][1]], [obk])
                        cp('dve', ob[i2][64:128], povs[1][64:128], [pos[1][1]], [obk])
                        yield
                        pov = ob[i2][:]
                        pok = obk
                        if ('yc%d' % l) in debug:
                            cp('pool', dbgbuf[:, :, tsl], pov, [pok], ['dbgbuf'])
                        act(sq[i2][:], pov, AF.Square, [pok], ['rtsq%d' % i2])
                        yield
                        pss_, psk = bank()
                        psv = pss_[:, 0:256].rearrange("p (a b) -> p a b", a=2)
                        for j in range(2):
                            mm(psv[:, j, :], bonesb, sq[i2][:, j, :], ['cstb', 'rtsq%d' % i2], [psk])
                        yield
                        act(rr[i2][:], psv, AF.Ln, [psk], ['rtrr%d' % i2], bias=RMS_EPS, scale=1.0 / 64)
                        yield
                        act(rr[i2][:], rr[i2][:], AF.Exp, ['rtrr%d' % i2], ['rtrr%d' % i2], scale=-0.5)
                        yield
                        tt('dve', rr[i2][:], pov, rr[i2][:], ALU.mult, [pok, 'rtrr%d' % i2], ['rtrr%d' % i2])
                        yield
                        tt('pool', Y[:, 2, :, tsl], rr[i2][:], zs[:, :, tsl], ALU.mult, ['rtrr%d' % i2, 'rtzs'], ['Y2'])

                    gj = gate_jobs(l, last, st2, bank, GJ_SPLIT[3]) if (GATE_PRE and GJ_SPLIT[3]) else []
                    run_pipelined(interleave([rtout(t) for t in range(NTL) if not (last and t < 2 and not debug)], gj, 3), STG['out'])
                    if ('yc%d' % l) in debug:
                        dbg_dump('yc%d' % l, dbgbuf[:], [128, 2, NT], ['dbgbuf'])
                    S.barrier()
                S.barrier()
        PHASES['ret'] = phase_ret
        def phase_rw(l, h_src, last):
            with contextlib.ExitStack() as st:
                RB = sb(st, "rwRB", [128, 2, NT], BF16)
                KB = sb(st, "rwKB", [128, 2, NT], BF16)
                VB = sb(st, "rwVB", [128, 2, NT], BF16)
                LB = sb(st, "rwLB", [128, NT], BF16)
                zs = sb(st, "rwzs", [128, 2, NT], BF16)
                vT = sb(st, "rwvT", [128, NTL, 256], BF16)
                lw2b = sb(st, "rwlw2", [128, 2, 256], BF16)
                S.dma('pool', lw2b[:], dr['lw2'][l], writes=['rwlw2'])
                oka = sb(st, "rwoka", [128, 2], F32)
                ts('dve', oka[:], pv('ka'), -1.0, 1.0, ALU.mult, ALU.add, ['pvt'], ['rwoka'])
                seen_b, seen_o = set(), set()
                with contextlib.ExitStack() as st2:
                    ww = sb(st2, "rww", [128, 8, 1152], BF16)
                    for pc_ in range(9):
                        S.dma('pool', ww[:, :, pc_ * 128:(pc_ + 1) * 128], dr['w_in'][l][:, :, 2816 + pc_ * 128:2816 + (pc_ + 1) * 128], writes=['rww%d' % pc_])
                    XRs = [sb(st2, "rwXR%d" % i, [128, NT + 4], F32) for i in range(2)]
                    XSs = [sb(st2, "rwXS%d" % i, [128, NT], F32) for i in range(2)]
                    c0 = sb(st2, "rwc0", [128, 7], F32)
                    pp = [ps(st2, "rwpp%d" % i, [128, 512], F32) for i in range(3)]
                    ptr = [ps(st2, "rwptr%d" % i, [128, 8, 128], BF16) for i in range(2)]
                    o_mu, _ = PV['mu']
                    mu0, mu1 = pvt[:, o_mu:o_mu + 7], pvt[:, o_mu + 7:o_mu + 14]
                    tt('dve', c0[:], mu0, mu1, ALU.add, ['pvt'], ['rwc0'])
                    ts('dve', c0[:], c0[:], -1.0, 1.0, ALU.mult, ALU.add, ['rwc0'], ['rwc0'])
                    for i in range(2):
                        memset('pool', XRs[i][:], 0.0, ['rwXR%d' % i])
                    cnt = 0
                    for m in (0, 1, 2, 3, 4, 5, 7, 6, 8):
                        XR, XRk = XRs[m % 2], 'rwXR%d' % (m % 2)
                        XS, XSk = XSs[m % 2], 'rwXS%d' % (m % 2)
                        for (n0, nn) in BLOCKS:
                            p_, pk_ = pp[cnt % 3], 'rwpp%d' % (cnt % 3)
                            cnt += 1
                            for jj in range(8):
                                mm(p_[:, 0:nn], ww[:, jj, m * 128:(m + 1) * 128], uT[:, jj, n0:n0 + nn],
                                   ['rww%d' % m] + uTk[n0 // 128:(n0 + nn) // 128], [pk_], start=(jj == 0), stop=(jj == 7))
                            if m >= 7:
                                act(zs[:, m - 7, n0:n0 + nn], p_[:, 0:nn], AF.Silu, [pk_, 'pvt'], ['rwzs'], bias=pv('bin', 22 + m))
                            else:
                                xo = 1 if n0 < 256 else 3
                                act(XR[:, n0 + xo:n0 + xo + nn], p_[:, 0:nn], AF.Identity, [pk_, 'pvt'], [XRk], bias=pv('bin', 22 + m))
                        if m >= 7:
                            continue
                        for (b0, ln, o0) in ((1, 256, 0), (259, 2048, 256)):
                            ts('dve', XS[:, o0:o0 + ln], XR[:, b0:b0 + ln], c0[:, m:m + 1], None, ALU.mult, None, [XRk, 'rwc0'], [XSk])
                            stt(XS[:, o0:o0 + ln], XR[:, b0 - 1:b0 - 1 + ln], mu0[:, m:m + 1], XS[:, o0:o0 + ln], ALU.mult, ALU.add,
                                [XRk, 'pvt', XSk], [XSk])
                            if m < 6:
                                dstT, dk = [(RB, 'rwRB'), (KB, 'rwKB'), (VB, 'rwVB')][m // 2]
                                stt(dstT[:, m % 2, o0:o0 + ln], XR[:, b0 + 1:b0 + 1 + ln], mu1[:, m:m + 1], XS[:, o0:o0 + ln], ALU.mult, ALU.add,
                                    [XRk, 'pvt', XSk], [dk])
                            else:
                                stt(XS[:, o0:o0 + ln], XR[:, b0 + 1:b0 + 1 + ln], mu1[:, m:m + 1], XS[:, o0:o0 + ln], ALU.mult, ALU.add,
                                    [XRk, 'pvt', XSk], [XSk])
                        if m == 6:
                            act(LB[0:64, :], XS[0:64, :], AF.Tanh, [XSk], ['rwLB'])
                            cp('pool', LB[64:128, :], XS[64:128, :], [XSk], ['rwLB'])
                    for t in range(NTL):
                        pr, prk = ptr[t % 2], 'rwptr%d' % (t % 2)
                        for j in range(2):
                            tr(pr[:, j, :], VB[:, j, t * 128:(t + 1) * 128], identb, ['rwVB', 'cstb'], [prk])
                        cp('dve' if t % 2 == 0 else 'act', vT[:, t, :], pr[:, 0:2, :].rearrange("p a b -> p (a b)"), [prk], ['rwvT'])
                    S.barrier()
                if stop == 'rw_proj':
                    return
                OS = sb(st, "rwOS", [128, 2, NT], F32)
                with contextlib.ExitStack() as st2:
                    def B(name, shape, dt=BF16):
                        return sb(st2, "rw_" + name, shape, dt), "rw_" + name
                    R64, R64k = B("R64", [128, 256], BF16)
                    memset('pool', R64[:], 1.0, [R64k])
                    memset('pool', R64[:, 0:256:64], 0.0, [R64k])
                    LW, LWk = B("LW", [128, 2, 128], F32)
                    SA, SAk = B("SA", [128, 2, 128], F32)
                    LGm, LGk = B("LG", [128, 2, 128], F32)
                    U0, U0k = LGm, LGk
                    EG, EGk = B("EG", [128, 2, 128], F32)
                    ENG, ENGk = B("ENG", [128, 2, 128], F32)
                    EGM, EGMk = B("EGM", [128, 2, 128], F32)
                    TA, TAk = B("TA", [128, 2, 128], F32)
                    TB_, TBk = B("TB", [128, 2, 128], F32)
                    SQ, SQk = B("SQ", [128, 2, 128])
                    RKD, RKDk = SQ, SQk
                    OBt = (None, None)
                    Zst = [B("Z%d" % d, [128, 2, 64], F32) for d in range(2)]
                    BUF = [dict() for _ in range(2)]
                    for d_ in range(2):
                        BUF[d_]['KKN'] = B("KKN_%d" % d_, [128, 2, 128])
                        BUF[d_]['KT'] = [B("KT_%d_%d" % (d_, s_), [128, 3, 2, 128]) for s_ in range(2)]
                        BUF[d_]['RT'] = [B("RT_%d_%d" % (d_, s_), [128, 2, 128]) for s_ in range(2)]
                        for j_ in range(2):
                            sfx = "_%d_%d" % (d_, j_)
                            SB = dict()
                            SB['TM'] = B("TM" + sfx, [128, 3, 128])
                            for nm_ in ('A1T', 'A2T', 'A3T', 'A4T', 'ALT', 'Tm', 'TTm', 'Xb', 'RHS', 'BYb'):
                                SB[nm_] = B(nm_ + sfx, [128, 2, 128])
                            SB['NY'] = B("NY" + sfx, [128, 2, 64])
                            SB['RH'] = B("RH" + sfx, [128, 128])
                            SB['GTb'] = B("GTb" + sfx, [128, 2, 128])
                            SB['ZLG'] = B("ZLG" + sfx, [128, 2, 64], F32)
                            SB['Z0b'] = B("Z0b" + sfx, [128, 2, 64])
                            BUF[d_][j_] = SB
                        BUF[d_]['GLt'] = [B("GLt_%d_%d" % (d_, s_), [128, 2, 2], F32) for s_ in range(2)]
                    banks = [ps(st2, "rwbank%d" % i, [128, 512], F32) for i in range(8)]
                    bcnt = [0]

                    def bank():
                        i = bcnt[0] % 8
                        bcnt[0] += 1
                        return banks[i], 'rwbank%d' % i
                    for d in range(2):
                        memset('pool', Zst[d][0][:], 0.0, [Zst[d][1], 'rw_Zs_%d_0' % d, 'rw_Zs_%d_1' % d])
                    for d_ in range(2):
                        for j_ in range(2):
                            memset('pool', BUF[d_][j_]['GTb'][0][:], 0.0, [BUF[d_][j_]['GTb'][1]])
                    orders = [list(range(NTL)), [1, 0] + list(range(NTL - 1, 1, -1))]
                    bc3 = lambda ap: ap.unsqueeze(2).broadcast_to([128, 2, 128])
                    def prep(d, t, slot):
                        KKN, KKNk = BUF[d]['KKN']
                        KT, KTk = BUF[d]['KT'][slot]
                        RTb, RTk = BUF[d]['RT'][slot]
                        GLt, GLk = BUF[d]['GLt'][slot]
                        tsl = slice(t * 128, (t + 1) * 128)
                        rev = (d == 1)
                        Z, Zk = Zst[d]
                        plw, plwk = bank()
                        pla, plak = bank()
                        plwv = plw[:, 0:256].rearrange("p (j t) -> p j t", j=2)
                        plav = pla[:, 0:256].rearrange("p (j t) -> p j t", j=2)
                        wb_ = 32 * d
                        for j in range(2):
                            mm(plwv[:, j, :], lw2b[wb_:wb_ + 16, d, j * 128:(j + 1) * 128], LB[wb_:wb_ + 16, tsl], ['rwlw2', 'rwLB'], [plwk])
                        for j in range(2):
                            mm(plav[:, j, :], lw2b[64:96, d, j * 128:(j + 1) * 128], LB[64:96, tsl], ['rwlw2', 'rwLB'], [plak])
                        for j in range(2):
                            act(LW[:, j, :], plwv[:, j, :], AF.Sigmoid, [plwk, 'pvt'], [LWk], bias=pv('w0', d * 2 + j))
                            act(SA[:, j, :], plav[:, j, :], AF.Sigmoid, [plak, 'pvt'], [SAk], bias=pv('a0', d * 2 + j))
                        ts('dve', LW[:], LW[:], -0.6065306597126334, None, ALU.mult, None, [LWk], [LWk])
                        yield
                        lwf = LW[:].rearrange("p a b -> p (a b)")
                        lgf = LGm[:].rearrange("p a b -> p (a b)")
                        if not rev:
                            S.op('dve', lambda e: e.tensor_tensor_scan(out=lgf, data0=R64[:], data1=lwf, initial=0.0, op0=ALU.mult, op1=ALU.add),
                                 reads=[LWk, R64k], writes=[LGk])
                        else:
                            S.op('dve', lambda e: e.tensor_tensor_scan(out=lgf[:, ::-1], data0=R64[:], data1=lwf[:, ::-1], initial=0.0,
                                                                       op0=ALU.mult, op1=ALU.add), reads=[LWk, R64k], writes=[LGk])
                        act(EG[:], LGm[:], AF.Exp, [LGk], [EGk])
                        yield
                        act(ENG[:], LGm[:], AF.Exp, [LGk], [ENGk], scale=-1.0)
                        yield
                        tt('pool', TA[:], LGm[:], LW[:], ALU.subtract, [LGk, LWk], [TAk])
                        yield
                        act(EGM[:], TA[:], AF.Exp, [TAk], [EGMk])
                        yield
                        gsrc = EG[:, :, 63::64] if not rev else EG[:, :, 0::64]
                        cp('pool', GLt[:], gsrc, [EGk], [GLk])
                        yield
                        tt('dve', TA[:], KB[:, :, tsl], bc3(pv('kk')), ALU.mult, ['rwKB', 'pvt', TAk], [TAk])
                        yield
                        act(SQ[:], TA[:], AF.Square, [TAk], [SQk])
                        yield
                        pss_, pssk = bank()
                        pssv = pss_[:, 0:256].rearrange("p (a b) -> p a b", a=2)
                        for j in range(2):
                            mm(pssv[:, j, :], bonesb, SQ[:, j, :], ['cstb', SQk], [pssk])
                        act(TB_[:], pssv, AF.Ln, [pssk], [TBk], bias=1e-24)
                        yield
                        act(TB_[:], TB_[:], AF.Exp, [TBk], [TBk], scale=-0.5)
                        yield
                        tt('dve', KKN[:], TA[:], TB_[:], ALU.mult, [TAk, TBk], [KKNk])
                        yield
                        tt('pool', KT[:, 0], KKN[:], EGM[:], ALU.mult, [KKNk, EGMk], [KTk])
                        yield
                        tt('dve', TA[:], SA[:], ENG[:], ALU.mult, [SAk, ENGk, TAk], [TAk])
                        yield
                        tt('pool', KT[:, 1], KKN[:], TA[:], ALU.mult, [KKNk, TAk], [KTk])
                        yield
                        tt('dve', U0[:], SA[:], bc3(pv('ka')), ALU.mult, [SAk, 'pvt'], [U0k])
                        yield
                        tt('dve', U0[:], U0[:], bc3(oka[:]), ALU.add, [U0k, 'rwoka'], [U0k])
                        yield
                        tt('pool', TB_[:], U0[:], ENG[:], ALU.mult, [U0k, ENGk, TBk], [TBk])
                        yield
                        tt('pool', KT[:, 2], KB[:, :, tsl], TB_[:], ALU.mult, ['rwKB', TBk], [KTk])
                        yield
                        tt('dve', RTb[:], RB[:, :, tsl], EG[:], ALU.mult, ['rwRB', EGk], [RTk])
                        yield
                        tt('dve', U0[:], U0[:], KB[:, :, tsl], ALU.mult, [U0k, 'rwKB'], [U0k])
                        yield
                        tt('dve', U0[:], U0[:], bc3(pv('rk')), ALU.mult, [U0k, 'pvt'], [U0k])
                        yield
                        tt('pool', RKD[:], U0[:], RB[:, :, tsl], ALU.mult, [U0k, 'rwRB'], [RKDk])
                        yield
                        pbn, pbnk = bank()
                        pbnv = pbn[:, 0:256].rearrange("p (a b) -> p a b", a=2)
                        for j in range(2):
                            mm(pbnv[:, j, :], bonesb, RKD[:, j, :], ['cstb', RKDk], [pbnk])
                        if t not in seen_b:
                            seen_b.add(t)
                            tt('dve', Y[:, 3, :, tsl], pbnv, VB[:, :, tsl], ALU.mult, [pbnk, 'rwVB'], ['Y3'])
                        else:
                            tt('dve', TA[:], pbnv, VB[:, :, tsl], ALU.mult, [pbnk, 'rwVB', TAk], [TAk])
                            tt('pool', Y[:, 3, :, tsl], Y[:, 3, :, tsl], TA[:], ALU.add, ['Y3', TAk], ['Y3'])

                    def prep_pair(step):
                        for d_ in range(2):
                            yield from prep(d_, orders[d_][step], step % 2)

                    def unit(d, t, slot):
                        KT, KTk = BUF[d]['KT'][slot]
                        RTb, RTk = BUF[d]['RT'][slot]
                        GLt, GLk = BUF[d]['GLt'][slot]
                        tsl = slice(t * 128, (t + 1) * 128)
                        rev = (d == 1)
                        subs = [stream(d, j, t, rev, tsl, KT, KTk, RTb, RTk, GLt, GLk) for j in range(2)]
                        while subs:
                            for g in list(subs):
                                try:
                                    next(g)
                                except StopIteration:
                                    subs.remove(g)
                                yield

                    def stream(d, j, t, rev, tsl, KT, KTk, RTb, RTk, GLt, GLk):
                        SB = BUF[d][j]
                        TM, TMk = SB['TM']
                        A1T, A1k = SB['A1T']
                        A2T, A2k = SB['A2T']
                        A3T, A3k = SB['A3T']
                        A4T, A4k = SB['A4T']
                        ALT, ALk = SB['ALT']
                        Tm, Tmk = SB['Tm']
                        TTm, TTk = SB['TTm']
                        Xb, Xbk = SB['Xb']
                        RHS, RHSk = SB['RHS']
                        BYb, BYk = SB['BYb']
                        NY, NYk = SB['NY']
                        RH, RHk = SB['RH']
                        GTb, GTk = SB['GTb']
                        ZLG, ZLGk = SB['ZLG']
                        Z0b, Z0k = SB['Z0b']
                        Z, _zk = Zst[d]
                        Zk = 'rw_Zs_%d_%d' % (d, j)
                        ptb, ptbk = bank()
                        ptv = ptb[:].bitcast(BF16).rearrange("p (a b) -> p a b", a=8)
                        for x in range(3):
                            tr(ptv[:, x, :], KT[:, x, j, :], identb, [KTk, 'cstb'], [ptbk])
                        cp('act', TM[:], ptv[:, 0:3, :], [ptbk], [TMk])
                        yield

                        def amat(dst, dstk, li, ri_src, ri_k, mslot):
                            pas = []
                            for par in range(2):
                                hp = par * 64
                                pa, pak = bank()
                                rhs = (RTb[hp:hp + 64, j, :] if ri_src is None else KT[hp:hp + 64, ri_src, j, :])
                                mm(pa[:, 0:128], KT[hp:hp + 64, li, j, :], rhs, [KTk, ri_k], [pak])
                                pas.append((pa, pak))
                            return pas

                        def aevac(pas, dst, dstk, mslot):
                            for par, (pa, pak) in enumerate(pas):
                                if mslot is None:
                                    cp('act', dst[:, par, :], pa[:, 0:128], [pak], [dstk])
                                else:
                                    tt('dve', dst[:, par, :], pa[:, 0:128], maskb[:, mslot, :], ALU.mult, [pak, 'maskb'], [dstk])
                        for (dst, dstk, li, rs, rk, ms) in ((A1T, A1k, 1, 0, KTk, None), (A2T, A2k, 2, 0, KTk, 2 + d),
                                                            (A3T, A3k, 1, None, RTk, 4 + d), (A4T, A4k, 2, None, RTk, 4 + d)):
                            pas = amat(dst, dstk, li, rs, rk, ms)
                            aevac(pas, dst, dstk, ms)
                            yield
                        idb2 = identb.unsqueeze(1).broadcast_to([128, 2, 128])
                        cp('pool', Tm[:], idb2, ['cstb'], [Tmk])
                        cp('pool', TTm[:], idb2, ['cstb'], [TTk])
                        for lv in range(6):
                            tt('pool', ALT[:], A1T[:], maskb[:, 6 + d * 6 + lv, :].unsqueeze(1).broadcast_to([128, 2, 128]), ALU.mult,
                               [A1k, 'maskb'], [ALk])
                            yield
                            px, pxk = bank()
                            pxv = px[:, 0:256].rearrange("p (h t) -> p h t", h=2)
                            for par in range(2):
                                mm(pxv[:, par, :], ALT[:, par, :], Tm[:, par, :], [ALk, Tmk], [pxk])
                            cp('act', Xb[:], pxv, [pxk], [Xbk])
                            yield
                            py_, pyk = bank()
                            pyv = py_[:].rearrange("p (x h t) -> p x h t", x=2, h=2)
                            for par in range(2):
                                mm(pyv[:, 0, par, :], Xb[:, par, :], TTm[:, par, :], [Xbk, TTk], [pyk])
                            if lv < 5:
                                for par in range(2):
                                    mm(pyv[:, 1, par, :], TTm[:, par, :], Xb[:, par, :], [Xbk, TTk], [pyk])
                            if lv < 5:
                                tt('dve', Tm[:], Tm[:], pyv[:, 1], ALU.subtract, [Tmk, pyk], [Tmk])
                            tt('dve', TTm[:], TTm[:], pyv[:, 0], ALU.subtract, [TTk, pyk], [TTk])
                            yield
                        pw, pwk = bank()
                        pwv = pw[:, 0:128].rearrange("p (h v) -> p h v", h=2)
                        for par in range(2):
                            h = 2 * j + par
                            mm(pwv[:, par, :], A2T[:, par, :], vT[:, t, h * 64:(h + 1) * 64], [A2k, 'rwvT'], [pwk])
                        cp('pool', RHS[:, :, 0:64], TM[:, 0, :].rearrange("p (h k) -> p h k", h=2), [TMk], [RHSk])
                        cp('act', RHS[:, :, 64:128], pwv, [pwk], [RHSk])
                        yield
                        pby, pbyk = bank()
                        pbyv = pby[:, 0:256].rearrange("p (h t) -> p h t", h=2)
                        for par in range(2):
                            mm(pbyv[:, par, :], TTm[:, par, :], RHS[:, par, :], [TTk, RHSk], [pbyk])
                        cp('act', BYb[:], pbyv, [pbyk], [BYk])
                        yield
                        ts('pool', NY[:], BYb[:, :, 64:128], -1.0, 0.0, ALU.mult, ALU.add, [BYk], [NYk])
                        pr_, prk = bank()
                        for par in range(2):
                            hp = par * 64
                            mm(pr_[hp:hp + 64, 0:128], BYb[:, par, 0:64], A3T[:, par, :], [BYk, A3k], [prk])
                        tt('dve', RH[:], RTb[:, j, :], pr_[:, 0:128], ALU.subtract, [RTk, prk], [RHk])
                        yield
                        for c in range(2):
                            cs = slice(c * 64, (c + 1) * 64)
                            pg_, pgk = bank()
                            pgv = pg_[:, 0:128].rearrange("p (x v) -> p x v", x=2)
                            for par in range(2):
                                hp = par * 64
                                h = 2 * j + par
                                hc = slice(h * 64, (h + 1) * 64)
                                pc = slice(par * 64, (par + 1) * 64)
                                mm(pgv[hp:hp + 64, 0, :], BYb[cs, par, 0:64], TM[cs, 1, pc], [BYk, TMk], [pgk])
                                mm(pgv[hp:hp + 64, 1, :], TM[cs, 2, pc], vT[cs, t, hc], [TMk, 'rwvT'], [pgk], start=True, stop=False)
                                mm(pgv[hp:hp + 64, 1, :], TM[cs, 1, pc], NY[cs, par, :], [TMk, NYk], [pgk], start=False, stop=True)
                            for par in range(2):
                                hp = par * 64
                                tt('dve', GTb[hp:hp + 64, c, hp:hp + 64], cstf[hp:hp + 64, 4, 0:64], pgv[hp:hp + 64, 0, :], ALU.subtract,
                                   ['cstf', pgk], [GTk])
                            ts('dve', ZLG[:, c, :], pgv[:, 1, :], GLt[:, j, c:c + 1], None, ALU.mult, None, [pgk, GLk], [ZLGk])
                            yield
                        for c in ((0, 1) if not rev else (1, 0)):
                            cp('act', Z0b[:, c, :], Z[:, j, :], [Zk], [Z0k])
                            yield
                            pn, pnk = bank()
                            mm(pn[:, 0:64], GTb[:, c, :], Z0b[:, c, :], [GTk, Z0k], [pnk])
                            stt(Z[:, j, :], pn[:, 0:64], GLt[:, j, c:c + 1], ZLG[:, c, :], ALU.mult, ALU.add, [pnk, GLk, ZLGk, Zk], [Zk])
                            yield
                        for par in range(2):
                            hp = par * 64
                            h = 2 * j + par
                            hc = slice(h * 64, (h + 1) * 64)
                            po_, pok = bank()
                            reg = po_[hp:hp + 64, 0:128]
                            mm(reg, vT[:, t, hc], A4T[:, par, :], ['rwvT', A4k], [pok], start=True, stop=False)
                            mm(reg, NY[:, par, :], A3T[:, par, :], [NYk, A3k], [pok], start=False, stop=False)
                            for c in range(2):
                                mm(reg[:, c * 64:(c + 1) * 64], Z0b[hp:hp + 64, c, :], RH[hp:hp + 64, c * 64:(c + 1) * 64],
                                   [Z0k, RHk], [pok], start=False, stop=(c == 1))
                            osl = OS[hp:hp + 64, j, tsl]
                            osk = 'rwOS%d_%d' % (t, j)
                            if (t, j, par) not in seen_o:
                                seen_o.add((t, j, par))
                                cp('dve' if par == 0 else 'act', osl, reg, [pok], [osk])
                            else:
                                tt('dve', osl, osl, reg, ALU.add, [pok, osk], [osk])
                            yield

                    for _ in prep_pair(0):
                        pass
                    for step in range(NTL):
                        gens = [unit(d, orders[d][step], step % 2) for d in range(2)]
                        if step + 1 < NTL:
                            gens.append(prep_pair(step + 1))
                        while gens:
                            for g in list(gens):
                                try:
                                    next(g)
                                except StopIteration:
                                    gens.remove(g)
                    S.barrier()
                if stop is not None and stop.startswith('rw_'):
                    return
                with contextlib.ExitStack() as st2:
                    ob = [sb(st2, "rwob%d" % i, [128, 2, 128], BF16) for i in range(2)]
                    cen = [sb(st2, "rwcen%d" % i, [128, 2, 128], F32) for i in range(2)]
                    rs = [sb(st2, "rwrs%d" % i, [128, 2, 128], F32) for i in range(2)]
                    pm_ = [ps(st2, "rwpm%d" % i, [128, 512], F32) for i in range(2)]
                    pv_ = [ps(st2, "rwpv%d" % i, [128, 512], F32) for i in range(2)]
                    for t in range(NTL):
                        i2 = t % 2
                        tsl = slice(t * 128, (t + 1) * 128)
                        osk = 'rwOS%d_0' % t
                        osk1 = 'rwOS%d_1' % t
                        cp('act', ob[i2][:], OS[:, :, tsl], [osk, osk1], ['rwob%d' % i2])
                        pmv = pm_[i2][:, 0:256].rearrange("p (a b) -> p a b", a=2)
                        for j in range(2):
                            mm(pmv[:, j, :], bonesb, ob[i2][:, j, :], ['cstb', 'rwob%d' % i2], ['rwpm%d' % i2])
                        stt(cen[i2][:], pmv, -1.0 / 64, OS[:, :, tsl], ALU.mult, ALU.add, ['rwpm%d' % i2, osk, osk1], ['rwcen%d' % i2])
                        act(ob[i2][:], cen[i2][:], AF.Square, ['rwcen%d' % i2], ['rwob%d' % i2])
                        pvv = pv_[i2][:, 0:256].rearrange("p (a b) -> p a b", a=2)
                        for j in range(2):
                            mm(pvv[:, j, :], bonesb, ob[i2][:, j, :], ['cstb', 'rwob%d' % i2], ['rwpv%d' % i2])
                        act(rs[i2][:], pvv, AF.Ln, ['rwpv%d' % i2], ['rwrs%d' % i2], bias=RW_GN_EPS, scale=1.0 / 64)
                        act(rs[i2][:], rs[i2][:], AF.Exp, ['rwrs%d' % i2], ['rwrs%d' % i2], scale=-0.5)
                        tt('dve', cen[i2][:], cen[i2][:], rs[i2][:], ALU.mult, ['rwcen%d' % i2, 'rwrs%d' % i2], ['rwcen%d' % i2])
                        tt('pool', cen[i2][:], cen[i2][:], bc3(pv('gnw')), ALU.mult, ['rwcen%d' % i2, 'pvt'], ['rwcen%d' % i2])
                        tt('pool', cen[i2][:], cen[i2][:], bc3(pv('gnb')), ALU.add, ['rwcen%d' % i2, 'pvt'], ['rwcen%d' % i2])
                        tt('dve', cen[i2][:], cen[i2][:], Y[:, 3, :, tsl], ALU.add, ['rwcen%d' % i2, 'Y3'], ['rwcen%d' % i2])
                        if ('yd%d' % l) in debug:
                            cp('act', OS[:, :, tsl], cen[i2][:], ['rwcen%d' % i2], [osk, osk1])
                        tt('dve', Y[:, 3, :, tsl], cen[i2][:], zs[:, :, tsl], ALU.mult, ['rwcen%d' % i2, 'rwzs'], ['Y3'])
                    if ('yd%d' % l) in debug:
                        dbg_dump('yd%d' % l, OS[:], [128, 2, NT], ['rwOS%d_%d' % (t, j_) for t in range(NTL) for j_ in range(2)])
                    S.barrier()
                S.barrier()
        PHASES['rw'] = phase_rw
        def phase_merge(l, h_src, last):
            h_dst = out_d if last else h1_d
            with contextlib.ExitStack() as st:
                MG = sb(st, "mgMG", [128, 8, NT], BF16)
                wbr = sb(st, "mgwbr", [128, 4, 2, DM], BF16)
                S.dma('pool', wbr[:], dr['wbr'][l], writes=['mgwbr'])
                with contextlib.ExitStack() as st2:
                    wg = [sb(st2, "mgwg%d" % i, [128, 8, 4, 128], BF16) for i in range(2)]
                    sg = [sb(st2, "mgsg%d" % i, [128, 512], BF16) for i in range(3)]
                    ac = [sb(st2, "mgac%d" % i, [128, 512], F32) for i in range(2)]
                    tm = [sb(st2, "mgtm%d" % i, [128, 512], F32) for i in range(2)]
                    pgl = [ps(st2, "mgpg%d" % i, [128, 512], F32) for i in range(3)]
                    pbr = [ps(st2, "mgpb%d" % i, [128, 512], F32) for i in range(3)]
                    cg = 0
                    ca = 0
                    def load_wg(dt_):
                        for k in range(4):
                            if (l, k * 8 + dt_) in pre_sg:
                                continue
                            c0 = 3968 + k * 1024 + dt_ * 128
                            S.dma('pool', wg[dt_ % 2][:, :, k, :], dr['w_in'][l][:, :, c0:c0 + 128], writes=['mgwg%d' % (dt_ % 2)])
                    load_wg(0)
                    for dt_ in range(8):
                        w_, wk_ = wg[dt_ % 2], 'mgwg%d' % (dt_ % 2)
                        if dt_ + 1 < 8:
                            load_wg(dt_ + 1)
                        for (n0, nn) in BLOCKS:
                            if last and n0 < 256:
                                continue
                            a_, ak_ = ac[ca % 2], 'mgac%d' % (ca % 2)
                            t_, tk_ = tm[ca % 2], 'mgtm%d' % (ca % 2)
                            ca += 1
                            for k in range(4):
                                pg_, pgk_ = pgl[cg % 3], 'mgpg%d' % (cg % 3)
                                pb_, pbk_ = pbr[cg % 3], 'mgpb%d' % (cg % 3)
                                s_, sk_ = sg[cg % 3], 'mgsg%d' % (cg % 3)
                                cg += 1
                                if (l, k * 8 + dt_) in pre_sg:
                                    S.dma('sp' if cg % 2 == 0 else 'act', s_[:, 0:nn], sgd[k * 8 + dt_][:, n0:n0 + nn], reads=['sgd'], writes=[sk_])
                                else:
                                    for jj in range(8):
                                        mm(pg_[:, 0:nn], w_[:, jj, k, :], uT[:, jj, n0:n0 + nn], [wk_] + uTk[n0 // 128:(n0 + nn) // 128], [pgk_],
                                           start=(jj == 0), stop=(jj == 7))
                                    act(s_[:, 0:nn], pg_[:, 0:nn], AF.Sigmoid, [pgk_, 'pvt'], [sk_], bias=pv('bin', 31 + k * 8 + dt_))
                                for jc in range(2):
                                    mm(pb_[:, 0:nn], wbr[:, k, jc, dt_ * 128:(dt_ + 1) * 128], Y[:, k, jc, n0:n0 + nn], ['mgwbr', 'Y%d' % k], [pbk_],
                                       start=(jc == 0), stop=(jc == 1))
                                if k == 0:
                                    tt('dve', a_[:, 0:nn], pb_[:, 0:nn], s_[:, 0:nn], ALU.mult, [pbk_, sk_], [ak_])
                                else:
                                    tt('dve', t_[:, 0:nn], pb_[:, 0:nn], s_[:, 0:nn], ALU.mult, [pbk_, sk_], [tk_])
                                    if k < 3:
                                        tt('pool', a_[:, 0:nn], a_[:, 0:nn], t_[:, 0:nn], ALU.add, [ak_, tk_], [ak_])
                                    else:
                                        tt('pool', MG[:, dt_, n0:n0 + nn], a_[:, 0:nn], t_[:, 0:nn], ALU.add, [ak_, tk_], ['mgMG%d' % (n0 // 512 if n0 else 9)])
                    S.barrier()
                if ('merged%d' % l) in debug:
                    with contextlib.ExitStack() as st2:
                        mf = sb(st2, "mgf", [128, 8, NT], F32)
                        cp('dve', mf[:], MG[:], ['mgMG%d' % i for i in (9, 0, 1, 2, 3)], ['mgf'])
                        dbg_dump('merged%d' % l, mf[:], [128, 8, NT], ['mgf'])
                        S.barrier()
                with contextlib.ExitStack() as st2:
                    wo = sb(st2, "mgwo", [128, 8, DM], BF16)
                    S.dma('pool', wo[:], dr['wout'][l], writes=['mgwo'])
                    rows = sb(st2, "mgrows", [128, 3, DM], F32)
                    S.dma('sp', rows[:], dr['rows'][l][:, 0:3072].rearrange("p (a b) -> p a b", a=3), writes=['mgrows'])
                    hin_ = [sb(st2, "mghin%d" % i, [128, DM], F32) for i in range(2)]
                    ot = [sb(st2, "mgot%d" % i, [128, DM], F32) for i in range(2)]
                    stat = [sb(st2, "mgst%d" % i, [128, 16], F32) for i in range(2)]
                    po = [[ps(st2, "mgpo%d_%d" % (i, hh), [128, 512], F32) for hh in range(2)] for i in range(2)]
                    def mgout(it, t):
                        i2 = it % 2
                        ci = 1 if t < 2 else 0
                        tsl = slice(t * 128, (t + 1) * 128)
                        mgk = 'mgMG%d' % (9 if t < 2 else (t - 2) // 4)
                        hk_, ok_, sk_ = 'mghin%d' % i2, 'mgot%d' % i2, 'mgst%d' % i2
                        hi, o_, sti = hin_[i2], ot[i2], stat[i2]
                        S.dma('sp', hi[:], h_src[t * 128:(t + 1) * 128, :], writes=[hk_])
                        for hh in range(2):
                            pk_ = 'mgpo%d_%d' % (i2, hh)
                            for jj in range(8):
                                mm(po[i2][hh][:], MG[:, jj, tsl], wo[:, jj, hh * 512:(hh + 1) * 512], [mgk, 'mgwo'], [pk_], start=(jj == 0), stop=(jj == 7))
                        yield
                        for hh in range(2):
                            pk_ = 'mgpo%d_%d' % (i2, hh)
                            tt('dve', o_[:, hh * 512:(hh + 1) * 512], po[i2][hh][:], rows[:, 0, hh * 512:(hh + 1) * 512], ALU.add, [pk_, 'mgrows'], [ok_])
                        yield
                        tt('dve', o_[:], o_[:], gatebc[:, ci, :], ALU.mult, [ok_, 'gatebc'], [ok_])
                        yield
                        stt(o_[:], hi[:], ALPHA, o_[:], ALU.mult, ALU.add, [hk_, ok_], [ok_])
                        yield
                        S.op('dve', lambda e: e.bn_stats(out=sti[:, 0:6], in_=o_[:, 0:512]), reads=[ok_], writes=[sk_])
                        S.op('dve', lambda e: e.bn_stats(out=sti[:, 6:12], in_=o_[:, 512:1024]), reads=[ok_], writes=[sk_])
                        yield
                        S.op('dve', lambda e: e.bn_aggr(out=sti[:, 12:14], in_=sti[:, 0:12]), reads=[sk_], writes=[sk_])
                        yield
                        act(sti[:, 14:15], sti[:, 13:14], AF.Sqrt, [sk_], [sk_], bias=LN_EPS)
                        yield
                        S.op('dve', lambda e: e.reciprocal(out=sti[:, 14:15], in_=sti[:, 14:15]), reads=[sk_], writes=[sk_])
                        yield
                        stt(sti[:, 15:16], sti[:, 12:13], -1.0, sti[:, 14:15], ALU.mult, ALU.mult, [sk_], [sk_])
                        yield
                        act(o_[:], o_[:], AF.Identity, [ok_, sk_], [ok_], bias=sti[:, 15:16], scale=sti[:, 14:15])
                        yield
                        tt('pool', o_[:, 0:512], o_[:, 0:512], rows[:, 1, 0:512], ALU.mult, [ok_, 'mgrows'], [ok_])
                        tt('dve', o_[:, 512:1024], o_[:, 512:1024], rows[:, 1, 512:1024], ALU.mult, [ok_, 'mgrows'], [ok_])
                        yield
                        tt('pool', o_[:, 0:512], o_[:, 0:512], rows[:, 2, 0:512], ALU.add, [ok_, 'mgrows'], [ok_])
                        tt('dve', o_[:, 512:1024], o_[:, 512:1024], rows[:, 2, 512:1024], ALU.add, [ok_, 'mgrows'], [ok_])
                        yield
                        if last:
                            S.dma('sp', out_d[(t - 2) * 128:(t - 1) * 128, :], o_[:], reads=[ok_], writes=['outfinal'])
                        else:
                            S.dma('sp', h1_d[t * 128:(t + 1) * 128, :], o_[:], reads=[ok_], writes=['h1'])

                    tl = [t for t in range(NTL) if not (last and t < 2)]
                    run_pipelined((mgout(i_, t) for i_, t in enumerate(tl)), STG['mgout'])
                    S.barrier()
                S.barrier()
        PHASES['merge'] = phase_merge
        for l in range(nlayers):
            last = (l == nlayers - 1)
            h_src = dr['hin'] if l == 0 else h1_d
            S.dma('sp', pvt[:], dr['pv'][l], writes=['pvt'])
            with contextlib.ExitStack() as st:
                adw = [sb(st, "adw%d" % i, [128, 8, 512], F32) for i in range(2)]
                scb = sb(st, "scb", [128, 2, 8, 128], F32)
                grow = sb(st, "grow", [128, DM], F32)
                pm0 = ps(st, "pm0", [128, 16, 2], F32)
                pg = [ps(st, "pg%d" % i, [128, 512], F32) for i in range(2)]
                for i in range(2):
                    cp('dve', scb[:, i], silc[:, :, i:i + 1].broadcast_to([128, 8, 128]), ['silc'], ['scb'])
                S.dma('sp', grow[:], dr['rows'][l][:, 3072:4096], writes=['grow'])
                for ch in range(6):
                    buf = adw[ch % 2]
                    bk = 'adw%d' % (ch % 2)
                    S.dma('sp' if ch % 2 == 0 else 'act', buf[:], dr['ada_w'][l][:, :, ch * 512:(ch + 1) * 512], writes=[bk])
                    if ch < 4:
                        for mloc in range(4):
                            m = ch * 4 + mloc
                            for j in range(8):
                                mm(pm0[:, m, :], buf[:, j, mloc * 128:(mloc + 1) * 128], silc[:, j, :], [bk, 'silc'],
                                   ['pm0'], start=(j == 0), stop=(j == 7))
                    else:
                        for i in range(2):
                            for j in range(8):
                                mm(pg[i][:], scb[:, i, j, :], buf[:, j, :], [bk, 'scb'], ['pg%d' % i],
                                   start=(j == 0), stop=(j == 7))
                            tt('dve', gatebc[:, i, (ch - 4) * 512:(ch - 3) * 512], pg[i][:],
                               grow[:, (ch - 4) * 512:(ch - 3) * 512], ALU.add, ['pg%d' % i, 'grow'], ['gatebc'])
                tt('dve', modfm[:], pm0[:], pv('adab').unsqueeze(2).broadcast_to([128, 16, 2]), ALU.add,
                   ['pm0', 'pvt'], ['modfm'])
                ts('dve', modfm[:, 8:16, :], modfm[:, 8:16, :], 1.0, None, ALU.add, None, ['modfm'], ['modfm'])
                dbg_dump('modfm%d' % l, modfm[:], [128, 16, 2], ['modfm'])
                dbg_dump('gatebc%d' % l, gatebc[:], [128, 2, DM], ['gatebc'])
                S.barrier()
            with contextlib.ExitStack() as st:
                xin = [sb(st, "xin%d" % i, [128, DM], F32) for i in range(3)]
                xn = [sb(st, "xn%d" % i, [128, DM], BF16) for i in range(2)]
                stat = [sb(st, "stat%d" % i, [128, 16], F32) for i in range(3)]
                ptr = [ps(st, "ptr%d" % i, [128, 8, 128], BF16) for i in range(2)]
                def p1tile(t):
                    xi, xk = xin[t % 3], 'xin%d' % (t % 3)
                    sti, sk = stat[t % 3], 'stat%d' % (t % 3)
                    xo, xok = xn[t % 2], 'xn%d' % (t % 2)
                    pt, ptk = ptr[t % 2], 'ptr%d' % (t % 2)
                    ci = 1 if t < 2 else 0
                    S.dma('sp' if t % 2 == 0 else 'act', xi[:], h_src[t * 128:(t + 1) * 128, :], writes=[xk])
                    yield
                    S.op('dve', lambda e: e.bn_stats(out=sti[:, 0:6], in_=xi[:, 0:512]), reads=[xk], writes=[sk])
                    S.op('dve', lambda e: e.bn_stats(out=sti[:, 6:12], in_=xi[:, 512:1024]), reads=[xk], writes=[sk])
                    yield
                    S.op('dve', lambda e: e.bn_aggr(out=sti[:, 12:14], in_=sti[:, 0:12]), reads=[sk], writes=[sk])
                    yield
                    act(sti[:, 14:15], sti[:, 13:14], AF.Sqrt, [sk], [sk], bias=LN_EPS)
                    yield
                    S.op('dve', lambda e: e.reciprocal(out=sti[:, 14:15], in_=sti[:, 14:15]), reads=[sk], writes=[sk])
                    yield
                    stt(sti[:, 15:16], sti[:, 12:13], -1.0, sti[:, 14:15], ALU.mult, ALU.mult, [sk], [sk])
                    yield
                    act(xo[:], xi[:], AF.Identity, [xk, sk], [xok], bias=sti[:, 15:16], scale=sti[:, 14:15])
                    yield
                    for j in range(8):
                        tr(pt[:, j, :], xo[:, j * 128:(j + 1) * 128], identb, [xok, 'cstb'], [ptk])
                    yield
                    for j in range(8):
                        if j % 2 == 0:
                            act(uT[:, j, t * 128:(t + 1) * 128], pt[:, j, :], AF.Identity, [ptk, 'modfm'], ['uT%d' % t],
                                bias=modfm[:, j, ci:ci + 1], scale=modfm[:, 8 + j, ci:ci + 1])
                        else:
                            ts('dve', uT[:, j, t * 128:(t + 1) * 128], pt[:, j, :], modfm[:, 8 + j, ci:ci + 1],
                               modfm[:, j, ci:ci + 1], ALU.mult, ALU.add, [ptk, 'modfm'], ['uT%d' % t])

                run_pipelined((p1tile(t) for t in range(NTL)), STG['p1'])
                if ('uT%d' % l) in debug:
                    utf = sb(st, "utf", [128, 8, NT], F32)
                    cp('dve', utf[:], uT[:], ['uT%d' % t for t in range(NTL)], ['utf'])
                    dbg_dump('uT%d' % l, utf[:], [128, 8, NT], ['utf'])
                S.barrier()
            uTk = ['uT%d' % t for t in range(NTL)]

            for ph in list(PHASES):
                if ph in phases:
                    PHASES[ph](l, h_src, last)
            if ('h%d' % l) in debug and not last:
                d_ = dbg_out('h%d' % l, [NT, DM])
                S.dma('sp', d_, h1_d, writes=['dbgout_h%d' % l])
                S.barrier()
            if ('Y%d' % l) in debug:
                with contextlib.ExitStack() as st:
                    yf = sb(st, "yf", [128, 4, 2, NT], F32)
                    cp('dve', yf[:], Y[:], ['Y0', 'Y1', 'Y2', 'Y3'], ['yf'])
                    dbg_dump('Y%d' % l, yf[:], [128, 4, 2, NT], ['yf'])
                    S.barrier()

        S.final_wait('sp', ['outfinal'] + ['dbgout_' + n for n in dbg_d])
    if MEMDBG:
        print('SBUF min remaining by prefix:', minrem)
    return nc, dbg_d


def kernel(**inputs):
    inp = {k: np.asarray(v) for k, v in inputs.items()}
    sh = prep_shared(inp)
    nc, _ = build()
    in_maps = []
    for b in range(8):
        m = dict(sh)
        m.update(prep_core(inp, b))
        in_maps.append(m)
    res = run_bass_kernel_spmd(nc, in_maps, core_ids=list(range(8)))
    return np.stack([np.asarray(res.results[b]['out'], dtype=np.float32) for b in range(8)], 0)
```

```python
import contextlib
import numpy as np
import concourse.bass as bass
import concourse.mybir as mybir
from concourse.bass_utils import run_bass_kernel_spmd

F32 = mybir.dt.float32
BF16 = mybir.dt.bfloat16
AF = mybir.ActivationFunctionType
ALU = mybir.AluOpType
AX = mybir.AxisListType

NT = 2304
NTL = 18
DM = 1024
NCOL = 8064
BLOCKS = [(0, 256), (256, 512), (768, 512), (1280, 512), (1792, 512)]
LN_EPS = 1e-5
RMS_EPS = 1e-6
RW_GN_EPS = 64e-5
ALPHA = (2 * 2) ** 0.25
PI = float(np.pi)
MEMDBG = False
GATE_PRE = False
S5_STAGGER = 6
STG = dict(hgchain=2, out=4, rtchain=1, hgproj=4, mgout=6, p1=4, s5=6)
GJ_SPLIT = [[], [], [], []]
GJ_S5 = list(range(32))


class Sched:
    NDMA = 16

    def __init__(self, nc, same_engine_waits=True):
        self.nc = nc
        self.same = same_engine_waits
        self.eng = dict(pe=nc.tensor, act=nc.scalar, dve=nc.vector, pool=nc.gpsimd, sp=nc.sync)
        self.E = {n: dict(cnt=0, known={}) for n in self.eng}
        self.dq = {'sp': ['dsp%d' % i for i in range(8)], 'act': ['dac%d' % i for i in range(4)],
                   'pool': ['dpl%d' % i for i in range(8)]}
        self.dmas = {n: dict(cnt=0) for q in self.dq.values() for n in q}
        self.dma_rr = {'sp': 0, 'act': 0, 'pool': 0}
        self.lastw = {}
        self.readers = {}
        self.sems = None
        self.nins = 0

    def sem_names(self):
        return list(self.E.keys()) + list(self.dmas.keys())

    def _deps(self, reads, writes):
        deps = {}

        def add(w):
            if w is not None:
                deps[w[0]] = max(deps.get(w[0], 0), w[1])
        for k in reads:
            add(self.lastw.get(k))
        for k in writes:
            add(self.lastw.get(k))
            for r in self.readers.get(k, ()):
                add(r)
        return deps

    def _waits(self, en, deps):
        E = self.E[en]
        waits = []
        for d, v in deps.items():
            if d == en and (en == 'pe' or not self.same):
                continue
            if E['known'].get(d, 0) < v:
                waits.append((d, v))
                E['known'][d] = v
        return waits

    def _record(self, ident, reads, writes):
        for k in writes:
            self.lastw[k] = ident
            self.readers[k] = []
        for k in reads:
            self.readers.setdefault(k, []).append(ident)

    def _emit(self, en, waits, fn, inc):
        eng = self.eng[en]
        for d, v in waits:
            eng.wait_ge(self.sems[d], v)
        if fn is not None:
            fn(eng).then_inc(self.sems[inc[0]], inc[1])
            self.nins += 1

    def op(self, en, fn, reads=(), writes=()):
        E = self.E[en]
        waits = self._waits(en, self._deps(reads, writes))
        E['cnt'] += 1
        self._emit(en, waits, fn, (en, 1))
        self._record((en, E['cnt']), reads, writes)

    def dma(self, en, out, in_, reads=(), writes=(), **kw):
        dn = self.dq[en][self.dma_rr[en]]
        self.dma_rr[en] = (self.dma_rr[en] + 1) % len(self.dq[en])
        Dq = self.dmas[dn]
        deps = self._deps(reads, writes)
        if Dq['cnt'] > 0:
            deps[dn] = max(deps.get(dn, 0), Dq['cnt'])
        waits = self._waits(en, deps)
        Dq['cnt'] += 16
        self._emit(en, waits, (lambda e: e.dma_start(out=out, in_=in_, **kw)), (dn, 16))
        self._record((dn, Dq['cnt']), reads, writes)

    def barrier(self):
        cur = {n: self.E[n]['cnt'] for n in self.E}
        cur.update({n: self.dmas[n]['cnt'] for n in self.dmas})
        for en in self.E:
            waits = self._waits(en, {d: v for d, v in cur.items() if v > 0})
            self._emit(en, waits, None, None)

    def final_wait(self, en, keys):
        self._emit(en, self._waits(en, self._deps(keys, ())), None, None)


PV = {}


def _pv_layout():
    off = 0
    for name, n in [('bin', 63), ('s5d', 2), ('glub', 2), ('hglb', 8), ('hgnw', 2), ('rdec', 4), ('mu', 14),
                    ('w0', 4), ('a0', 4), ('kk', 2), ('ka', 2), ('rk', 2), ('gnw', 2), ('gnb', 2), ('adab', 16),
                    ('lamre', 16), ('lamim', 16), ('ldt', 16), ('rdech', 8)]:
        PV[name] = (off, n)
        off += n
    return off


NPV = _pv_layout()


def _colmap():
    cm = list(range(0, 3584))
    lora = [-1] * 128
    for r in range(16):
        lora[r] = 3584 + r
        lora[32 + r] = 3600 + r
        lora[64 + r] = 3616 + r
        lora[80 + r] = 3632 + r
    cm += lora
    cm += list(range(3648, 3904))
    cm += list(range(3904, 8000))
    return np.array(cm)


CMAP = _colmap()


def _fm(v):
    return np.ascontiguousarray(v.reshape(-1, 128).T)


def _masks():
    t = np.arange(128)
    s_, t_ = t[:, None], t[None, :]
    m = []
    b32 = (s_ // 32) == (t_ // 32)
    b64 = (s_ // 64) == (t_ // 64)
    m.append(b32 & (t_ >= s_))
    m.append(b32 & (t_ <= s_))
    m.append(b64 & (t_ > s_))
    m.append(b64 & (t_ < s_))
    m.append(b64 & (t_ >= s_))
    m.append(b64 & (t_ <= s_))
    for d in range(2):
        for lv in range(6):
            sz = 1 << lv
            blk = (s_ // (2 * sz)) == (t_ // (2 * sz))
            hs, ht = (s_ // sz) % 2, (t_ // sz) % 2
            if d == 0:
                m.append(blk & (ht == 1) & (hs == 0))
            else:
                m.append(blk & (ht == 0) & (hs == 1))
    return np.stack([x.astype(np.float32) for x in m], 1)


def _rot_tables():
    n = 16
    freqs = 10000.0 ** (-np.arange(n, dtype=np.float32) / n)
    tt = np.arange(2048)
    rows = (tt // 64).astype(np.float32)
    cols = (tt % 64).astype(np.float32)
    cos = np.zeros((128, 2048), np.float32)
    sins = np.zeros((128, 2048), np.float32)
    pm = np.zeros((128, 128), np.float32)
    for p in range(128):
        i = p % 64
        pos = rows if i < 32 else cols
        ii = i % 32
        ang = pos * freqs[ii % 16]
        cos[p] = np.cos(ang)
        if ii < 16:
            sins[p] = -np.sin(ang)
            partner = p + 16
        else:
            sins[p] = np.sin(ang)
            partner = p - 16
        pm[partner, p] = 1.0
    return cos, sins, pm


def prep_shared(inp):
    sh = {}
    L = 2
    w_in = inp['w_in']
    wn = np.zeros((L, 1024, NCOL), np.float32)
    valid = CMAP >= 0
    wn[:, :, valid] = w_in[:, :, CMAP[valid]]
    sh['w_in'] = np.ascontiguousarray(wn.reshape(L, 8, 128, NCOL).transpose(0, 2, 1, 3))
    bn = np.zeros((L, NCOL), np.float32)
    bn[:, valid] = inp['b_in'][:, CMAP[valid]]
    sh['ada_w'] = np.ascontiguousarray(inp['ada_w'].reshape(L, 8, 128, 3072).transpose(0, 2, 1, 3))
    pv = np.zeros((L, 128, NPV), np.float32)

    def put(l, name, arr):
        o, n = PV[name]
        assert arr.shape == (128, n), (name, arr.shape)
        pv[l, :, o:o + n] = arr
    for l in range(L):
        put(l, 'bin', _fm(bn[l]))
        put(l, 's5d', _fm(inp['s5_d'][l]))
        put(l, 'glub', _fm(inp['s5_glu_b'][l]))
        put(l, 'hglb', np.concatenate([_fm(inp['hg_lb'][ll, d]) for ll in range(2) for d in range(2)], 1))
        put(l, 'hgnw', _fm(inp['hg_norm_w'][l]))
        rd = np.zeros((128, 4), np.float32)
        for d in range(2):
            for j in range(2):
                rd[:64, d * 2 + j] = inp['ret_decay'][l, d, 2 * j]
                rd[64:, d * 2 + j] = inp['ret_decay'][l, d, 2 * j + 1]
        put(l, 'rdec', rd)
        put(l, 'rdech', np.ascontiguousarray(np.broadcast_to(inp['ret_decay'][l].reshape(1, 8), (128, 8))))
        mu = np.zeros((2, 7 * 128), np.float32)
        mu[:, :768] = inp['rw_mu'][l][:, :768]
        lv = CMAP[3584:3712]
        ok = lv >= 0
        mu[:, 768:896][:, ok] = inp['rw_mu'][l][:, lv[ok] - 2816]
        put(l, 'mu', np.concatenate([_fm(mu[0]), _fm(mu[1])], 1))
        put(l, 'w0', np.concatenate([_fm(inp['rw_w0'][l, d]) for d in range(2)], 1))
        put(l, 'a0', np.concatenate([_fm(inp['rw_a0'][l, d]) for d in range(2)], 1))
        for nm, key in [('kk', 'rw_kk'), ('ka', 'rw_ka'), ('rk', 'rw_rk'), ('gnw', 'rw_gn_w'), ('gnb', 'rw_gn_b')]:
            put(l, nm, _fm(inp[key][l]))
        put(l, 'adab', _fm(inp['ada_b'][l][:2048]))
        for nm, key in [('lamre', 's5_lam_re'), ('lamim', 's5_lam_im')]:
            a = inp[key][l].reshape(2, 8, 2, 64)
            put(l, nm, np.ascontiguousarray(a.transpose(2, 3, 0, 1).reshape(128, 16)))
        a = np.broadcast_to(inp['s5_log_dt'][l].reshape(2, 8, 2, 1), (2, 8, 2, 64))
        put(l, 'ldt', np.ascontiguousarray(a.transpose(2, 3, 0, 1).reshape(128, 16)))
    sh['pv'] = pv
    bt = np.zeros((L, 128, 2, 4, 2, 128), np.float32)
    ct = np.zeros((L, 128, 8, 2, 128), np.float32)
    for l in range(L):
        for g in range(16):
            i, g2 = g // 2, g % 2
            for q in range(16):
                c = g * 16 + q
                j, p = c // 128, c % 128
                bt[l, p, j, i % 4, 0, g2 * 64:(g2 + 1) * 64] = inp['s5_b_re'][l, g, :, q]
                bt[l, p, j, i % 4, 1, g2 * 64:(g2 + 1) * 64] = inp['s5_b_im'][l, g, :, q]
            m0 = (i % 4) * 32 + g2 * 16
            ct[l, g2 * 64:(g2 + 1) * 64, i, 0, m0:m0 + 16] = inp['s5_c_re'][l, g].T
            ct[l, g2 * 64:(g2 + 1) * 64, i, 1, m0:m0 + 16] = inp['s5_c_im'][l, g].T
    sh['s5bt'] = bt
    sh['s5ct'] = ct
    sh['gluw'] = np.ascontiguousarray(inp['s5_glu_w'].reshape(L, 2, 128, 256).transpose(0, 2, 1, 3))
    lw2 = np.zeros((L, 128, 2, 256), np.float32)
    for l in range(L):
        lw2[l, 0:16, 0] = inp['rw_w2'][l, 0]
        lw2[l, 32:48, 1] = inp['rw_w2'][l, 1]
        lw2[l, 64:80, 0] = inp['rw_a2'][l, 0]
        lw2[l, 80:96, 1] = inp['rw_a2'][l, 1]
    sh['lw2'] = lw2
    sh['wbr'] = np.ascontiguousarray(inp['w_branch'].reshape(L, 4, 2, 128, 1024).transpose(0, 3, 1, 2, 4))
    sh['wout'] = np.ascontiguousarray(inp['w_out'].reshape(L, 8, 128, 1024).transpose(0, 2, 1, 3))
    rows = np.zeros((L, 128, 4096 + 512), np.float32)
    for l in range(L):
        rows[l, :, 0:1024] = inp['b_out'][l][None]
        rows[l, :, 1024:2048] = inp['ln_w'][l][None]
        rows[l, :, 2048:3072] = inp['ln_b'][l][None]
        rows[l, :, 3072:4096] = inp['ada_b'][l][None, 2048:3072]
        rows[l, :, 4096:4352] = inp['b_in'][l][None, 1280:1536]
        rows[l, :, 4352:4608] = inp['b_in'][l][None, 2304:2560]
    sh['rows'] = rows
    sh['masks'] = _masks()
    cos, sins, pm = _rot_tables()
    sh['rcos'] = cos
    sh['rsin'] = sins
    t = np.arange(128)
    cst = np.zeros((128, 9, 128), np.float32)
    cst[:, 0] = pm
    cst[:, 1] = ((t[:, None] // 64) == (t[None, :] // 64))
    cst[:, 2] = np.maximum(t[None, :] - t[:, None], 0)
    cst[:, 3] = np.maximum(t[:, None] - t[None, :], 0)
    cst[:, 4] = (t[None, :] >= t[:, None])
    cst[:, 5] = (t[None, :] <= t[:, None])
    cst[:, 6, :64] = ((t[:, None] % 64) == np.arange(64)[None, :])
    cst[:, 6, 64:68] = ((t[:, None] // 32) == np.arange(4)[None, :])
    cst[:, 6, 68] = 127 - t
    cst[:, 6, 69] = t
    cst[:, 7] = t[None, :] + 1.0
    cst[:, 8] = 128.0 - t[None, :]
    sh['cst'] = cst
    return sh


def prep_core(inp, b):
    pc = {}
    pc['hin'] = np.ascontiguousarray(np.concatenate([inp['ctx'][b], inp['x'][b]], 0))
    cv = np.stack([inp['c'][b], inp['c_ctx']], -1)
    pc['cvec'] = np.ascontiguousarray(cv.reshape(8, 128, 2).transpose(1, 0, 2))
    return pc


SHAPES = dict(hin=[NT, DM], cvec=[128, 8, 2], w_in=[2, 128, 8, NCOL], ada_w=[2, 128, 8, 3072], pv=[2, 128, NPV],
              s5bt=[2, 128, 2, 4, 2, 128], s5ct=[2, 128, 8, 2, 128], gluw=[2, 128, 2, 256], lw2=[2, 128, 2, 256],
              wbr=[2, 128, 4, 2, 1024], wout=[2, 128, 8, 1024], rows=[2, 128, 4608], masks=[128, 18, 128],
              rcos=[128, 2048], rsin=[128, 2048], cst=[128, 9, 128])


def build(debug=(), nlayers=2, phases=('s5', 'hg', 'ret', 'rw', 'merge'), stop=None):
    nc = bass.Bass("TRN2", target_bir_lowering=False)
    S = Sched(nc)
    dr = {k: nc.dram_tensor(k, list(v), F32, kind="ExternalInput").ap() for k, v in SHAPES.items()}
    out_d = nc.dram_tensor("out", [2048, DM], F32, kind="ExternalOutput").ap()
    h1_d = nc.dram_tensor("h1", [NT, DM], F32, kind="Internal").ap()
    sgd = nc.dram_tensor("sgd", [32, 128, NT], BF16, kind="Internal").ap()
    pre_sg = set()
    dbg_d = {}

    def dbg_out(name, shape):
        dbg_d[name] = nc.dram_tensor("dbg_" + name, list(shape), F32, kind="ExternalOutput").ap()
        return dbg_d[name]

    uid = [0]

    def key(p='k'):
        uid[0] += 1
        return '%s%d' % (p, uid[0])

    with contextlib.ExitStack() as top:
        S.sems = {n: top.enter_context(nc.semaphore(n)) for n in S.sem_names()}

        minrem = {}

        def sb(st, name, shape, dt=F32):
            uid[0] += 1
            t_ = st.enter_context(nc.sbuf_tensor("%s_%d" % (name, uid[0]), list(shape), dt))
            if MEMDBG:
                pre = name[:2]
                minrem[pre] = min(minrem.get(pre, 1 << 30), nc.sbuf_bytes_remaining)
            return t_

        def ps(st, name, shape, dt=F32):
            uid[0] += 1
            return st.enter_context(nc.psum_tensor("%s_%d" % (name, uid[0]), list(shape), dt))

        def mm(out, lhsT, rhs, r, w, start=True, stop=True):
            S.op('pe', lambda e: e.matmul(out, lhsT=lhsT, rhs=rhs, start=start, stop=stop), reads=r, writes=w)

        def tr(out, in_, ident, r, w):
            S.op('pe', lambda e: e.transpose(out, in_, ident), reads=r, writes=w)

        def act(out, in_, func, r, w, bias=0.0, scale=1.0):
            S.op('act', lambda e: e.activation(out=out, in_=in_, func=func, bias=bias, scale=scale), reads=r, writes=w)

        def tt(en, out, in0, in1, op, r, w):
            S.op(en, lambda e: e.tensor_tensor(out=out, in0=in0, in1=in1, op=op), reads=r, writes=w)

        def ts(en, out, in0, s1, s2, op0, op1, r, w):
            if s2 is None:
                S.op(en, lambda e: e.tensor_scalar(out=out, in0=in0, scalar1=s1, scalar2=None, op0=op0), reads=r, writes=w)
            else:
                S.op(en, lambda e: e.tensor_scalar(out=out, in0=in0, scalar1=s1, scalar2=s2, op0=op0, op1=op1),
                     reads=r, writes=w)

        def stt(out, in0, sc, in1, op0, op1, r, w):
            S.op('dve', lambda e: e.scalar_tensor_tensor(out=out, in0=in0, scalar=sc, in1=in1, op0=op0, op1=op1),
                 reads=r, writes=w)

        def cp(en, out, in_, r, w):
            if en == 'act':
                S.op('act', lambda e: e.copy(out=out, in_=in_), reads=r, writes=w)
            else:
                S.op(en, lambda e: e.tensor_copy(out=out, in_=in_), reads=r, writes=w)

        def memset(en, ap, val, w):
            S.op(en, lambda e: e.memset(ap, val), writes=w)

        def run_pipelined(gens, stagger):
            it = iter(gens)
            active, pending, rounds = [], True, 0
            while pending or active:
                if pending and rounds % stagger == 0:
                    try:
                        active.append(next(it))
                    except StopIteration:
                        pending = False
                for g in list(active):
                    try:
                        next(g)
                    except StopIteration:
                        active.remove(g)
                rounds += 1

        def mkbanks(st_, n, prefix):
            bl = [ps(st_, "%s%d" % (prefix, i), [128, 512], F32) for i in range(n)]
            cnt = [0]

            def bank():
                i = cnt[0] % n
                cnt[0] += 1
                return bl[i], '%s%d' % (prefix, i)
            return bank

        def gate_jobs(l, last, st_, bankfn, kds):
            wgt = [sb(st_, "gjw%d" % i, [128, 8, 128], BF16) for i in range(2)]
            sgs = [sb(st_, "gjs%d" % i, [128, 512], BF16) for i in range(2)]
            cnt = [0]

            def job(i, kd):
                k, dt_ = kd // 8, kd % 8
                w_, wk_ = wgt[i % 2], 'gjw%d' % (i % 2)
                c0 = 3968 + k * 1024 + dt_ * 128
                S.dma('pool', w_[:], dr['w_in'][l][:, :, c0:c0 + 128], writes=[wk_])
                yield
                for (n0, nn) in BLOCKS:
                    if last and n0 < 256:
                        continue
                    pg_, pgk_ = bankfn()
                    for jj in range(8):
                        mm(pg_[:, 0:nn], w_[:, jj, :], uT[:, jj, n0:n0 + nn], [wk_] + uTk[n0 // 128:(n0 + nn) // 128], [pgk_],
                           start=(jj == 0), stop=(jj == 7))
                    yield
                    c_ = cnt[0] % 2
                    cnt[0] += 1
                    act(sgs[c_][:, 0:nn], pg_[:, 0:nn], AF.Sigmoid, [pgk_, 'pvt'], ['gjs%d' % c_], bias=pv('bin', 31 + k * 8 + dt_))
                    yield
                    S.dma('sp', sgd[kd][:, n0:n0 + nn], sgs[c_][:, 0:nn], reads=['gjs%d' % c_], writes=['sgd'])
                    yield
                pre_sg.add((l, kd))
            return [job(i, kd) for i, kd in enumerate(kds)]

        def interleave(main, extra, every):
            out, ei = [], 0
            extra = list(extra)
            for i, g in enumerate(main):
                out.append(g)
                if (i + 1) % every == 0 and ei < len(extra):
                    out.append(extra[ei])
                    ei += 1
            out.extend(extra[ei:])
            return out

        def dbg_dump(name, ap, shape, r):
            if name in debug:
                d = dbg_out(name, shape)
                S.dma('sp', d, ap, reads=r, writes=['dbgout_' + name])

        cstb = sb(top, "cstb", [128, 3, 128], BF16)
        cstf = sb(top, "cstf", [128, 7, 128], F32)
        maskb = sb(top, "maskb", [128, 18, 128], BF16)
        silc = sb(top, "silc", [128, 8, 2], F32)
        S.dma('pool', cstb[:, 0:2, :], dr['cst'][:, 0:2, :], writes=['cstb'])
        S.dma('sp', cstf[:], dr['cst'][:, 2:9, :], writes=['cstf'])
        S.dma('pool', maskb[:], dr['masks'], writes=['maskb'])
        S.dma('sp', silc[:], dr['cvec'], writes=['silc'])
        memset('pool', cstb[:, 2, :], 0.0, ['cstb'])
        S.op('pool', lambda e: e.affine_select(out=cstb[:, 2, :], in_=cstb[:, 2, :], pattern=[[-1, 128]],
                                               compare_op=ALU.not_equal, fill=1.0, base=0, channel_multiplier=1),
             reads=['cstb'], writes=['cstb'])
        act(silc[:], silc[:], AF.Silu, ['silc'], ['silc'])
        identb = cstb[:, 2, :]
        bonesb = cstb[:, 1, :]

        uT = sb(top, "uT", [128, 8, NT], BF16)
        Y = sb(top, "Y", [128, 4, 2, NT], BF16)
        pvt = sb(top, "pvt", [128, NPV], F32)
        if debug:
            memset('pool', Y[:], 0.0, ['Y0', 'Y1', 'Y2', 'Y3'])
        modfm = sb(top, "modfm", [128, 16, 2], F32)
        gatebc = sb(top, "gatebc", [128, 2, DM], F32)

        def pv(name, j=None, n=1):
            o, cnt = PV[name]
            if j is None:
                return pvt[:, o:o + cnt]
            return pvt[:, o + j:o + j + n]

        PHASES = {}
        def proj_fm(st, wt, wk, mlist, evac, pp, ppk):
            cnt = 0
            for (n0, nn) in BLOCKS:
                for mi, m in enumerate(mlist):
                    p_, pk_ = pp[cnt % len(pp)], ppk[cnt % len(pp)]
                    cnt += 1
                    for j in range(8):
                        mm(p_[:, 0:nn], wt[:, j, m * 128:(m + 1) * 128], uT[:, j, n0:n0 + nn],
                           ['%s%d' % (wk, m // 2)] + uTk[n0 // 128:(n0 + nn) // 128], [pk_], start=(j == 0), stop=(j == 7))
                    evac(mi, m, n0, nn, p_, pk_)

        def phase_s5(l, h_src, last):
            L = 128
            with contextlib.ExitStack() as st:
                btb = sb(st, "btb", [128, 2, 4, 2, 128], BF16)
                ctb = sb(st, "ctb", [128, 8, 2, 128], BF16)
                glub = sb(st, "glub", [128, 2, 256], BF16)
                S.dma('pool', btb[:], dr['s5bt'][l], writes=['btb'])
                S.dma('pool', ctb[:], dr['s5ct'][l], writes=['ctb'])
                S.dma('pool', glub[:], dr['gluw'][l], writes=['glub'])
                ts('pool', ctb[:, :, 1, :], ctb[:, :, 1, :], -1.0, 0.0, ALU.mult, ALU.add, ['ctb'], ['ctb'])
                ub = sb(st, "s5u", [128, 2, NT], BF16)
                zs = sb(st, "s5z", [128, 2, NT], BF16)
                yacc = sb(st, "yacc", [128, 2, NT], F32)
                PT = sb(st, "s5PT", [128, 16, 2, L], F32)
                QT = sb(st, "s5QT", [128, 16, 2, L], F32)
                sst = sb(st, "s5st", [128, 16, 2], F32)
                ones = sb(st, "s5ones", [128, L], F32)
                memset('pool', yacc[:], 0.0, ['yacc'])
                memset('pool', sst[:], 0.0, ['sst'])
                memset('pool', ones[:], 1.0, ['s5ones'])
                with contextlib.ExitStack() as st2:
                    wsu = sb(st2, "wsu", [128, 8, 512], BF16)
                    for pc_ in range(2):
                        S.dma('pool', wsu[:, :, pc_ * 256:(pc_ + 1) * 256], dr['w_in'][l][:, :, pc_ * 256:(pc_ + 1) * 256], writes=['wsu%d' % pc_])
                    pp = [ps(st2, "s5pp%d" % i, [128, 512], F32) for i in range(2)]

                    def evac(mi, m, n0, nn, p_, pk_):
                        if m < 2:
                            act(ub[:, m, n0:n0 + nn], p_[:, 0:nn], AF.Identity, [pk_, 'pvt'], ['s5u'], bias=pv('bin', m))
                        else:
                            act(zs[:, m - 2, n0:n0 + nn], p_[:, 0:nn], AF.Silu, [pk_, 'pvt'], ['s5z'], bias=pv('bin', m))
                    proj_fm(st2, wsu, 'wsu', [0, 1, 2, 3], evac, pp, ['s5pp0', 's5pp1'])
                    sm = sb(st2, "s5sm", [128, 20, 16], F32)
                    K_ = 's5sm'

                    def Sm(i):
                        return sm[:, i, :]

                    def T2(o, a, b, op):
                        tt('dve', Sm(o), a if not isinstance(a, int) else Sm(a), b if not isinstance(b, int) else Sm(b), op,
                           [K_, 'pvt'], [K_])
                    lamre, lamim = pv('lamre'), pv('lamim')
                    act(Sm(0), pv('ldt'), AF.Exp, ['pvt'], [K_])
                    T2(1, lamre, 0, ALU.mult)
                    act(Sm(2), Sm(1), AF.Exp, [K_], [K_])
                    act(Sm(3), Sm(1), AF.Exp, [K_], [K_], scale=-1.0)
                    T2(4, lamim, 0, ALU.mult)
                    ts('dve', Sm(5), Sm(4), PI / 2, None, ALU.add, None, [K_], [K_])
                    for x in (4, 5):
                        for _ in range(4):
                            ts('dve', Sm(16), Sm(x), PI, 2 * PI, ALU.is_gt, ALU.mult, [K_], [K_])
                            T2(x, x, 16, ALU.subtract)
                        for _ in range(2):
                            ts('dve', Sm(16), Sm(x), -PI, 2 * PI, ALU.is_lt, ALU.mult, [K_], [K_])
                            T2(x, x, 16, ALU.add)
                    act(Sm(6), Sm(4), AF.Sin, [K_], [K_])
                    act(Sm(7), Sm(5), AF.Sin, [K_], [K_])
                    T2(8, 2, 7, ALU.mult)
                    T2(9, 2, 6, ALU.mult)
                    T2(10, 3, 7, ALU.mult)
                    stt(Sm(11), Sm(3), -1.0, Sm(6), ALU.mult, ALU.mult, [K_], [K_])
                    ts('dve', Sm(12), Sm(8), -1.0, None, ALU.add, None, [K_], [K_])
                    T2(16, lamre, lamre, ALU.mult)
                    T2(17, lamim, lamim, ALU.mult)
                    T2(13, 16, 17, ALU.add)
                    S.op('dve', lambda e: e.reciprocal(out=Sm(13), in_=Sm(13)), reads=[K_], writes=[K_])
                    T2(16, 12, lamre, ALU.mult)
                    T2(17, 9, lamim, ALU.mult)
                    T2(16, 16, 17, ALU.add)
                    T2(14, 16, 13, ALU.mult)
                    T2(16, 9, lamre, ALU.mult)
                    T2(17, 12, lamim, ALU.mult)
                    T2(16, 16, 17, ALU.subtract)
                    T2(15, 16, 13, ALU.mult)
                    tmpa = sb(st2, "s5ta", [128, 16, L], F32)
                    tmpb = sb(st2, "s5tb", [128, 16, L], F32)

                    def cmul_bc(dst_re, dst_im, src_re, src_im, s_re, s_im, m):
                        sr = s_re.unsqueeze(2).broadcast_to([128, 16, m])
                        si = s_im.unsqueeze(2).broadcast_to([128, 16, m])
                        ta, tb = tmpa[:, :, 0:m], tmpb[:, :, 0:m]
                        kk_ = ['s5tab', 's5ta', 's5tb', 's5tc', K_]
                        tt('dve', ta, src_re, sr, ALU.mult, kk_, ['s5ta'])
                        tt('dve', tb, src_im, si, ALU.mult, kk_, ['s5tb'])
                        tt('dve', dst_re, ta, tb, ALU.subtract, kk_, ['s5tab'])
                        tt('dve', ta, src_re, si, ALU.mult, kk_, ['s5ta'])
                        tt('dve', tb, src_im, sr, ALU.mult, kk_, ['s5tb'])
                        tt('dve', dst_im, ta, tb, ALU.add, kk_, ['s5tab'])
                    for (TB, a_re, a_im) in ((PT, 8, 9), (QT, 10, 11)):
                        cp('dve', TB[:, :, 0, 0], Sm(a_re), [K_], ['s5tab'])
                        cp('dve', TB[:, :, 1, 0], Sm(a_im), [K_], ['s5tab'])
                        m = 1
                        while m < L:
                            cmul_bc(TB[:, :, 0, m:2 * m], TB[:, :, 1, m:2 * m], TB[:, :, 0, 0:m], TB[:, :, 1, 0:m],
                                    TB[:, :, 0, m - 1], TB[:, :, 1, m - 1], m)
                            m *= 2
                    tmpc = sb(st2, "s5tc", [128, 16, L], F32)
                    cp('dve', tmpc[:], QT[:, :, 0, :], ['s5tab'], ['s5tc'])
                    cmul_bc(QT[:, :, 0, :], QT[:, :, 1, :], tmpc[:], QT[:, :, 1, :], Sm(14), Sm(15), L)
                    S.barrier()
                with contextlib.ExitStack() as st2:
                    NB = 8
                    xa = [sb(st2, "s5xa%d" % i, [128, 2, L], F32) for i in range(NB)]
                    xb_ = [sb(st2, "s5xb%d" % i, [128, 2, L], F32) for i in range(NB)]
                    cw = [sb(st2, "s5cw%d" % i, [128, 2, L], F32) for i in range(NB)]
                    hb = [sb(st2, "s5hb%d" % i, [128, 2, L], BF16) for i in range(NB)]
                    pbu = [ps(st2, "s5pb%d" % i, [128, 2, 2, L], F32) for i in range(4)]
                    py = [ps(st2, "s5py%d" % i, [128, 512], F32) for i in range(2)]
                    orders = [list(range(NTL)), [1, 0] + list(range(NTL - 1, 1, -1))]
                    def s5group(gi, step, d, j):
                        c = orders[d][step]
                        n0 = c * L
                        rev = (d == 1)
                        U = []
                        for ii in range(4):
                            un = gi * 4 + ii
                            bnk = (un // 2) % 4
                            U.append(dict(ii=ii, i=j * 4 + ii, q=d * 8 + j * 4 + ii, pb=pbu[bnk][:, un % 2], pbk='s5pb%d' % bnk,
                                          A=xa[un % NB], Ak='s5xa%d' % (un % NB), B=xb_[un % NB], Bk='s5xb%d' % (un % NB),
                                          C=cw[un % NB], Ck='s5cw%d' % (un % NB), H=hb[un % NB], Hk='s5hb%d' % (un % NB)))
                        for u in U:
                            for ri in range(2):
                                mm(u['pb'][:, ri, :], btb[:, j, u['ii'], ri, :], ub[:, j, n0:n0 + L], ['btb', 's5u'], [u['pbk']])
                        yield
                        for u in U:
                            src = u['pb'][:, :, ::-1] if rev else u['pb'][:, :, :]
                            tt('dve', u['A'][:], src, QT[:, u['q'], 0:1, :].broadcast_to([128, 2, L]), ALU.mult,
                               [u['pbk'], 's5tab'], [u['Ak']])
                        yield
                        for u in U:
                            src = u['pb'][:, ::-1, ::-1] if rev else u['pb'][:, ::-1, :]
                            tt('dve', u['B'][:], src, QT[:, u['q'], 1:2, :].broadcast_to([128, 2, L]), ALU.mult,
                               [u['pbk'], 's5tab'], [u['Bk']])
                        yield
                        for u in U:
                            tt('dve', u['A'][:, 0, :], u['A'][:, 0, :], u['B'][:, 0, :], ALU.subtract, [u['Ak'], u['Bk']], [u['Ak']])
                        yield
                        for u in U:
                            tt('dve', u['A'][:, 1, :], u['A'][:, 1, :], u['B'][:, 1, :], ALU.add, [u['Ak'], u['Bk']], [u['Ak']])
                        yield
                        for ri in range(2):
                            for u in U:
                                q = u['q']
                                S.op('dve', lambda e, u=u, ri=ri, q=q: e.tensor_tensor_scan(
                                    out=u['C'][:, ri, :], data0=ones[:], data1=u['A'][:, ri, :], initial=sst[:, q, ri:ri + 1],
                                    op0=ALU.mult, op1=ALU.add), reads=[u['Ak'], 's5ones', 'sst%d' % q, 'sst'], writes=[u['Ck']])
                            yield
                        for u in U:
                            tt('pool', u['A'][:], u['C'][:], PT[:, u['q'], 0:1, :].broadcast_to([128, 2, L]), ALU.mult,
                               [u['Ck'], 's5tab', u['Ak']], [u['Ak']])
                        yield
                        for u in U:
                            tt('pool', u['B'][:], u['C'][:, ::-1, :], PT[:, u['q'], 1:2, :].broadcast_to([128, 2, L]), ALU.mult,
                               [u['Ck'], 's5tab', u['Bk']], [u['Bk']])
                        yield
                        for u in U:
                            tt('pool', u['A'][:, 0, :], u['A'][:, 0, :], u['B'][:, 0, :], ALU.subtract, [u['Ak'], u['Bk']], [u['Ak']])
                        yield
                        for u in U:
                            tt('pool', u['A'][:, 1, :], u['A'][:, 1, :], u['B'][:, 1, :], ALU.add, [u['Ak'], u['Bk']], [u['Ak']])
                        yield
                        for u in U:
                            cp('pool', sst[:, u['q'], :], u['A'][:, :, L - 1], [u['Ak']], ['sst%d' % u['q']])
                        yield
                        for u in U:
                            hsrc = u['A'][:, :, ::-1] if rev else u['A'][:]
                            cp('act', u['H'][:], hsrc, [u['Ak']], [u['Hk']])
                        yield
                        pyr = py[gi % 2][:, 0:L]
                        pyk = 's5py%d' % (gi % 2)
                        for k_, u in enumerate(U):
                            for ri in range(2):
                                mm(pyr, ctb[:, u['i'], ri, :], u['H'][:, ri, :], ['ctb', u['Hk']], [pyk],
                                   start=(k_ == 0 and ri == 0), stop=(k_ == 3 and ri == 1))
                        yield
                        yield
                        yield
                        tt('dve', yacc[:, j, n0:n0 + L], yacc[:, j, n0:n0 + L], pyr, ALU.add, [pyk, 'yacc'], ['yacc'])

                    glist = [(step, d, j) for step in range(NTL) for d in range(2) for j in range(2)]
                    gbank = mkbanks(st2, 2, "s5gk") if (GATE_PRE and GJ_S5) else None
                    gj = gate_jobs(l, last, st2, gbank, GJ_S5) if (GATE_PRE and GJ_S5) else []
                    run_pipelined(interleave([s5group(gi, *g) for gi, g in enumerate(glist)], gj, 2), STG['s5'])
                    S.barrier()
                for j in range(2):
                    stt(yacc[:, j, :], ub[:, j, :], pv('s5d', j), yacc[:, j, :], ALU.mult, ALU.add, ['s5u', 'yacc', 'pvt'],
                        ['yacc'])
                dbg_dump('ya%d' % l, yacc[:], [128, 2, NT], ['yacc'])
                with contextlib.ExitStack() as st2:
                    t1 = [sb(st2, "s5g1_%d" % i, [128, 512], F32) for i in range(2)]
                    t2 = [sb(st2, "s5g2_%d" % i, [128, 512], BF16) for i in range(2)]
                    pg = [ps(st2, "s5pg%d" % i, [128, 512], F32) for i in range(2)]
                    cnt = 0
                    for (n0, nn) in BLOCKS:
                        for j in range(2):
                            a, ak = t1[cnt % 2], 's5g1_%d' % (cnt % 2)
                            cnt += 1
                            ysl = yacc[:, j, n0:n0 + nn]
                            act(a[:, 0:nn], ysl, AF.Square, ['yacc'], [ak])
                            ts('dve', a[:, 0:nn], a[:, 0:nn], 0.044715, 1.0, ALU.mult, ALU.add, [ak], [ak])
                            tt('dve', a[:, 0:nn], a[:, 0:nn], ysl, ALU.mult, [ak, 'yacc'], [ak])
                            act(a[:, 0:nn], a[:, 0:nn], AF.Sigmoid, [ak], [ak], scale=1.5957691216057308)
                            tt('dve', ub[:, j, n0:n0 + nn], a[:, 0:nn], ysl, ALU.mult, [ak, 'yacc'], ['s5u'])
                    cnt = 0
                    for (n0, nn) in BLOCKS:
                        for m in range(2):
                            p_, pk_ = pg[cnt % 2], 's5pg%d' % (cnt % 2)
                            b_, bk_ = t2[cnt % 2], 's5g2_%d' % (cnt % 2)
                            cnt += 1
                            for jc in range(2):
                                mm(p_[:, 0:nn], glub[:, jc, m * 128:(m + 1) * 128], ub[:, jc, n0:n0 + nn], ['glub', 's5u'], [pk_],
                                   start=(jc == 0), stop=(jc == 1))
                            act(b_[:, 0:nn], p_[:, 0:nn], AF.Sigmoid, [pk_, 'pvt'], [bk_], bias=pv('glub', m))
                            tt('dve', b_[:, 0:nn], b_[:, 0:nn], ub[:, m, n0:n0 + nn], ALU.mult, [bk_, 's5u'], [bk_])
                            tt('pool', Y[:, 0, m, n0:n0 + nn], b_[:, 0:nn], zs[:, m, n0:n0 + nn], ALU.mult, [bk_, 's5z'], ['Y0'])
                    S.barrier()
                S.barrier()
        PHASES['s5'] = phase_s5
        def phase_hg(l, h_src, last):
            with contextlib.ExitStack() as st:
                QP = [sb(st, "hgQP%d" % d, [128, 2, NT], BF16) for d in range(2)]
                KP = [sb(st, "hgKP%d" % d, [128, 2, NT], BF16) for d in range(2)]
                G = sb(st, "hgG", [128, 2, 72, 2], F32)
                VT = sb(st, "hgVT", [128, NTL, 256], BF16)
                zs = sb(st, "hgzs", [128, 2, NT], BF16)
                lbt = sb(st, "hglbt", [128, 2, 4], F32)
                if l == 0:
                    memset('pool', lbt[:, 0, :], 0.0, ['hglbt'])
                    memset('pool', lbt[:, 1, :], 1.0, ['hglbt'])
                else:
                    o_, _ = PV['hglb']
                    tt('dve', lbt[:, 0, :], pvt[:, o_ + 4:o_ + 8], pvt[:, o_:o_ + 4], ALU.subtract, ['pvt'], ['hglbt'])
                    act(lbt[:, 0, :], lbt[:, 0, :], AF.Sigmoid, ['hglbt'], ['hglbt'])
                    ts('dve', lbt[:, 1, :], lbt[:, 0, :], -1.0, 1.0, ALU.mult, ALU.add, ['hglbt'], ['hglbt'])
                with contextlib.ExitStack() as st2:
                    wh = sb(st2, "hgw", [128, 8, 1280], BF16)
                    for pc_ in (0, 4, 1, 2, 3):
                        S.dma('pool', wh[:, :, pc_ * 256:(pc_ + 1) * 256], dr['w_in'][l][:, :, 512 + pc_ * 256:512 + (pc_ + 1) * 256], writes=['hgw%d' % pc_])
                    brow = sb(st2, "hgbrow", [128, 256], F32)
                    S.dma('sp', brow[:], dr['rows'][l][:, 4096:4352], writes=['hgbrow'])
                    R32 = sb(st2, "hgR32", [128, 512], F32)
                    memset('pool', R32[:], 1.0, ['hgR32'])
                    memset('pool', R32[:, 0:512:32], 0.0, ['hgR32'])
                    QS = [sb(st2, "hgQS%d" % i, [128, 2, 512], BF16) for i in range(2)]
                    T = [[sb(st2, "hgT%d_%d" % (i, k), [128, 512], F32) for k in range(4)] for i in range(2)]
                    pp = [ps(st2, "hgpp%d" % i, [128, 512], F32) for i in range(3)]
                    pt = [ps(st2, "hgpt%d" % i, [128, 512], F32) for i in range(2)]
                    def hgproj(cnt, ic, m, n0, nn):
                        ukeys = uTk[n0 // 128:(n0 + nn) // 128]
                        p_, pk_ = pp[cnt % 3], 'hgpp%d' % (cnt % 3)
                        bi = (n0 // 512) % 2 if n0 else 0
                        for jj in range(8):
                            mm(p_[:, 0:nn], wh[:, jj, m * 128:(m + 1) * 128], uT[:, jj, n0:n0 + nn], ['hgw%d' % (m // 2)] + ukeys, [pk_],
                               start=(jj == 0), stop=(jj == 7))
                        yield
                        bias = pv('bin', 4 + m)
                        if m < 2:
                            act(QS[bi][:, m, 0:nn], p_[:, 0:nn], AF.Silu, [pk_, 'pvt'], ['hgQS%d' % bi], bias=bias)
                            return
                        if m >= 8:
                            act(zs[:, m - 8, n0:n0 + nn], p_[:, 0:nn], AF.Silu, [pk_, 'pvt'], ['hgzs'], bias=bias)
                            return
                        d, j = (m - 2) // 2, (m - 2) % 2
                        Ts = T[ic % 2]
                        Tk = ['hgT%d_%d' % (ic % 2, k) for k in range(4)]
                        t1, t2, t3, t4 = [x[:, 0:nn] for x in Ts]
                        act(t1, p_[:, 0:nn], AF.Sigmoid, [pk_, 'pvt'], [Tk[0]], bias=bias)
                        yield
                        ts('dve', t1, t1, lbt[:, 1, d * 2 + j:d * 2 + j + 1], lbt[:, 0, d * 2 + j:d * 2 + j + 1], ALU.mult, ALU.add,
                           [Tk[0], 'hglbt'], [Tk[0]])
                        yield
                        act(t2, t1, AF.Ln, [Tk[0]], [Tk[1]])
                        yield
                        if d == 0:
                            S.op('dve', lambda e: e.tensor_tensor_scan(out=t3, data0=R32[:, 0:nn], data1=t2, initial=0.0,
                                                                       op0=ALU.mult, op1=ALU.add),
                                 reads=[Tk[1], 'hgR32'], writes=[Tk[2]])
                        else:
                            S.op('dve', lambda e: e.tensor_tensor_scan(out=t3[:, ::-1],
                                                                       data0=R32[:, 0:nn], data1=t2[:, ::-1], initial=0.0,
                                                                       op0=ALU.mult, op1=ALU.add),
                                 reads=[Tk[1], 'hgR32'], writes=[Tk[2]])
                        yield
                        ts('dve', t3, t3, -80.0, None, ALU.max, None, [Tk[2]], [Tk[2]])
                        act(t1, t1, AF.Identity, [Tk[0]], [Tk[0]], bias=1.0, scale=-1.0)
                        yield
                        act(t4, t3, AF.Exp, [Tk[2]], [Tk[3]])
                        act(t2, t3, AF.Exp, [Tk[2]], [Tk[1]], scale=-1.0)
                        yield
                        tt('pool', KP[d][:, j, n0:n0 + nn], t1, t2, ALU.mult, [Tk[0], Tk[1]], ['hgKP%d' % d])
                        tt('pool', QP[d][:, j, n0:n0 + nn], QS[bi][:, j, 0:nn], t4, ALU.mult, ['hgQS%d' % bi, Tk[3]], ['hgQP%d' % d])
                        c0 = n0 // 32
                        gsrc = t4[:, 31::32] if d == 0 else t4[:, 0::32]
                        cp('act', G[:, d, c0:c0 + nn // 32, j], gsrc, [Tk[3]], ['hgG'])

                    plist = []
                    cnt = 0
                    ic = 0
                    for (n0, nn) in BLOCKS:
                        for m in (0, 1, 8, 9, 2, 3, 4, 5):
                            plist.append((cnt, ic, m, n0, nn))
                            cnt += 1
                            if 2 <= m < 8:
                                ic += 1
                    run_pipelined((hgproj(*p) for p in plist), STG['hgproj'])
                    for t in range(NTL):
                        p_, pk_ = pt[t % 2], 'hgpt%d' % (t % 2)
                        for jj in range(8):
                            mm(p_[:, 0:256], uT[:, jj, t * 128:(t + 1) * 128], wh[:, jj, 768:1024], ['hgw3', uTk[t]], [pk_],
                               start=(jj == 0), stop=(jj == 7))
                        tt('dve', VT[:, t, :], p_[:, 0:256], brow[:], ALU.add, [pk_, 'hgbrow'], ['hgVT'])
                    S.barrier()
                Sall = [sb(st, "hgSall%d" % d, [128, 2, 72, 64], BF16) for d in range(2)]
                with contextlib.ExitStack() as st2:
                    Sst = [sb(st2, "hgS%d" % d, [128, 2, 64], F32) for d in range(2)]
                    kTm = [sb(st2, "hgkTm%d" % i, [128, 4, 256], BF16) for i in range(3)]
                    Ug = [sb(st2, "hgUg%d" % i, [128, 4, 2, 64], F32) for i in range(3)]
                    ptr = [ps(st2, "hgptr%d" % i, [128, 8, 128], BF16) for i in range(2)]
                    pU = [ps(st2, "hgpU%d" % i, [128, 4, 2, 64], F32) for i in range(3)]
                    orders = [list(range(NTL)), [1, 0] + list(range(NTL - 1, 1, -1))]
                    for d in range(2):
                        memset('pool', Sst[d][:], 0.0, ['hgS%d' % d])
                    def hgchain(it, step, d):
                        t = orders[d][step]
                        pr, prk = ptr[it % 2], 'hgptr%d' % (it % 2)
                        km, kmk = kTm[it % 3], 'hgkTm%d' % (it % 3)
                        pu, puk = pU[it % 3], 'hgpU%d' % (it % 3)
                        ug, ugk = Ug[it % 3], 'hgUg%d' % (it % 3)
                        for j in range(2):
                            tr(pr[:, j, :], KP[d][:, j, t * 128:(t + 1) * 128], identb, ['hgKP%d' % d, 'cstb'], [prk])
                        yield
                        for cc in range(4):
                            prf = pr[:, 0:2, :].rearrange("p a b -> p (a b)")
                            if cc % 2 == 0:
                                ts('dve', km[:, cc, :], prf, cstf[:, 4, 64 + cc:64 + cc + 1], None, ALU.mult, None, [prk, 'cstf'], [kmk])
                            else:
                                act(km[:, cc, :], prf, AF.Identity, [prk, 'cstf'], [kmk], scale=cstf[:, 4, 64 + cc:64 + cc + 1])
                        yield
                        for cc in range(4):
                            for h in range(4):
                                hp = (h % 2) * 64
                                mm(pu[hp:hp + 64, cc, h // 2, :], km[:, cc, h * 64:(h + 1) * 64], VT[:, t, h * 64:(h + 1) * 64],
                                   [kmk, 'hgVT'], [puk])
                        yield
                        tt('dve', ug[:], pu[:], G[:, d, t * 4:(t + 1) * 4, :].unsqueeze(3).broadcast_to([128, 4, 2, 64]), ALU.mult,
                           [puk, 'hgG'], [ugk])
                        yield
                        ccs = range(4) if d == 0 else range(3, -1, -1)
                        for cc in ccs:
                            c = t * 4 + cc
                            cp('act', Sall[d][:, :, c, :], Sst[d][:], ['hgS%d' % d], ['hgSall%d_%d' % (d, t)])
                            for j in range(2):
                                stt(Sst[d][:, j, :], Sst[d][:, j, :], G[:, d, c, j:j + 1], ug[:, cc, j, :], ALU.mult, ALU.add,
                                    ['hgS%d' % d, 'hgG', ugk], ['hgS%d' % d])
                            yield

                    gbank = mkbanks(st2, 3, "hggk") if GJ_SPLIT[0] else None
                    gj = gate_jobs(l, last, st2, gbank, GJ_SPLIT[0]) if (GATE_PRE and GJ_SPLIT[0]) else []
                    run_pipelined(interleave([hgchain(i_, sd[0], sd[1]) for i_, sd in enumerate([(s_, d_) for s_ in range(NTL) for d_ in range(2)])], gj, 3), STG['hgchain'])
                    S.barrier()
                with contextlib.ExitStack() as st2:
                    if ('yb%d' % l) in debug:
                        dbgbuf = sb(st2, "dbgbuf", [128, 2, NT], F32)
                    AT = [[sb(st2, "hgAT%d_%d" % (i, d), [128, 4, 128], BF16) for d in range(2)] for i in range(3)]
                    sq = [sb(st2, "hgsq%d" % i, [128, 2, 128], BF16) for i in range(3)]
                    rr = [sb(st2, "hgrr%d" % i, [128, 2, 128], F32) for i in range(3)]
                    ob = [sb(st2, "hgob%d" % i, [128, 2, 128], F32) for i in range(3)]
                    bank = mkbanks(st2, 8, "hgbk")

                    def hgout(t):
                        i2 = t % 3
                        tsl = slice(t * 128, (t + 1) * 128)
                        pas = {}
                        for d in range(2):
                            for par in range(2):
                                pas[(d, par)] = bank()
                            for h in range(4):
                                hp = (h % 2) * 64
                                pa, pak = pas[(d, h % 2)]
                                pav = pa[:, 0:256].rearrange("p (a b) -> p a b", a=2)
                                mm(pav[:, h // 2, :], KP[d][hp:hp + 64, h // 2, tsl], QP[d][hp:hp + 64, h // 2, tsl],
                                   ['hgKP%d' % d, 'hgQP%d' % d], [pak])
                        yield
                        for d in range(2):
                            for par in range(2):
                                pa, pak = pas[(d, par)]
                                pav = pa[:, 0:256].rearrange("p (a b) -> p a b", a=2)
                                tt('dve', AT[i2][d][:, par::2, :], pav, maskb[:, d, :].unsqueeze(1).broadcast_to([128, 2, 128]), ALU.mult,
                                   [pak, 'maskb'], ['hgAT%d_%d' % (i2, d)])
                        yield
                        pos = [bank() for _ in range(2)]
                        povs = [pos[par][0][:, 0:256].rearrange("p (a b) -> p a b", a=2) for par in range(2)]
                        for h in range(4):
                            hp = (h % 2) * 64
                            pok = pos[h % 2][1]
                            reg = povs[h % 2][hp:hp + 64, h // 2, :]
                            first = True
                            for d in range(2):
                                mm(reg, VT[:, t, h * 64:(h + 1) * 64], AT[i2][d][:, h, :], ['hgVT', 'hgAT%d_%d' % (i2, d)], [pok],
                                   start=first, stop=False)
                                first = False
                                for cc in range(4):
                                    c = t * 4 + cc
                                    mm(reg[:, cc * 32:(cc + 1) * 32], Sall[d][hp:hp + 64, h // 2, c, :],
                                       QP[d][hp:hp + 64, h // 2, t * 128 + cc * 32:t * 128 + (cc + 1) * 32],
                                       ['hgSall%d_%d' % (d, t), 'hgQP%d' % d], [pok], start=False, stop=(d == 1 and cc == 3))
                        yield
                        obk = 'hgob%d' % i2
                        cp('act', ob[i2][0:64], povs[0][0:64], [pos[0][1]], [obk])
                        cp('dve', ob[i2][64:128], povs[1][64:128], [pos[1][1]], [obk])
                        yield
                        pov = ob[i2][:]
                        pok = obk
                        if ('yb%d' % l) in debug:
                            cp('pool', dbgbuf[:, :, tsl], pov, [pok], ['dbgbuf'])
                        act(sq[i2][:], pov, AF.Square, [pok], ['hgsq%d' % i2])
                        yield
                        pss_, psk = bank()
                        psv = pss_[:, 0:256].rearrange("p (a b) -> p a b", a=2)
                        for j in range(2):
                            mm(psv[:, j, :], bonesb, sq[i2][:, j, :], ['cstb', 'hgsq%d' % i2], [psk])
                        yield
                        act(rr[i2][:], psv, AF.Ln, [psk], ['hgrr%d' % i2], bias=RMS_EPS, scale=1.0 / 64)
                        yield
                        act(rr[i2][:], rr[i2][:], AF.Exp, ['hgrr%d' % i2], ['hgrr%d' % i2], scale=-0.5)
                        yield
                        tt('dve', rr[i2][:], pov, rr[i2][:], ALU.mult, [pok, 'hgrr%d' % i2], ['hgrr%d' % i2])
                        yield
                        for j in range(2):
                            stt(Y[:, 1, j, tsl], rr[i2][:, j, :], pv('hgnw', j), zs[:, j, tsl], ALU.mult, ALU.mult,
                                ['hgrr%d' % i2, 'pvt', 'hgzs'], ['Y1'])

                    gj = gate_jobs(l, last, st2, bank, GJ_SPLIT[1]) if (GATE_PRE and GJ_SPLIT[1]) else []
                    run_pipelined(interleave([hgout(t) for t in range(NTL) if not (last and t < 2 and not debug)], gj, 3), STG['out'])
                    if ('yb%d' % l) in debug:
                        dbg_dump('yb%d' % l, dbgbuf[:], [128, 2, NT], ['dbgbuf'])
                    S.barrier()
                S.barrier()
        PHASES['hg'] = phase_hg
        def phase_ret(l, h_src, last):
            with contextlib.ExitStack() as st:
                QR = sb(st, "rtQR", [128, 2, NT], BF16)
                KR = sb(st, "rtKR", [128, 2, NT], BF16)
                VT = sb(st, "rtVT", [128, NTL, 256], BF16)
                zs = sb(st, "rtzs", [128, 2, NT], BF16)
                Sall = [sb(st, "rtSall%d" % d, [128, 2, NTL, 64], BF16) for d in range(2)]
                LG = sb(st, "rtLG", [128, 4], F32)
                GL = sb(st, "rtGL", [128, 4], F32)
                LGH = sb(st, "rtLGH", [128, 8], F32)
                QDEC = sb(st, "rtQDEC", [128, 2, 2, 128], F32)
                KDEC = sb(st, "rtKDEC", [128, 2, 4], F32)
                DS = sb(st, "rtDS", [128, 4, 128], F32)
                tb8 = sb(st, "rtb8", [128, 2], F32)
                K_ = 'rttab'
                act(LG[:], pv('rdec'), AF.Exp, ['pvt'], [K_])
                ts('dve', LG[:], LG[:], -1.0, None, ALU.mult, None, [K_], [K_])
                act(GL[:], LG[:], AF.Exp, [K_], [K_], scale=128.0)
                act(LGH[:], pv('rdech'), AF.Exp, ['pvt'], [K_])
                ts('dve', LGH[:], LGH[:], -1.0, None, ALU.mult, None, [K_], [K_])
                for d in range(2):
                    for j in range(2):
                        act(QDEC[:, d, j, :], cstf[:, 5 + d, :], AF.Exp, ['cstf', K_], [K_], scale=LG[:, d * 2 + j:d * 2 + j + 1])
                    act(KDEC[:, d, :], LGH[:, d * 4:(d + 1) * 4], AF.Exp, ['cstf', K_], [K_], scale=cstf[:, 4, 68 + d:69 + d])
                with contextlib.ExitStack() as st2:
                    ta = sb(st2, "rtta", [128, 128], F32)
                    tb = sb(st2, "rttb", [128, 128], F32)
                    for h in range(4):
                        act(ta[:], cstf[:, 0, :], AF.Exp, ['cstf', K_], ['rtta'], scale=LGH[:, h:h + 1])
                        tt('dve', ta[:], ta[:], cstf[:, 2, :], ALU.mult, ['rtta', 'cstf'], ['rtta'])
                        act(tb[:], cstf[:, 1, :], AF.Exp, ['cstf', K_], ['rttb'], scale=LGH[:, 4 + h:5 + h])
                        tt('dve', tb[:], tb[:], cstf[:, 3, :], ALU.mult, ['rttb', 'cstf'], ['rttb'])
                        tt('dve', DS[:, h, :], ta[:], tb[:], ALU.add, ['rtta', 'rttb'], [K_])
                    ts('dve', tb8[:], pv('bin', 16, 2), 0.125, None, ALU.mult, None, ['pvt'], [K_])
                    S.barrier()
                if stop == 'ret_tab':
                    return
                with contextlib.ExitStack() as st2:
                    wr = sb(st2, "rtw", [128, 8, 1024], BF16)
                    for pc_ in range(4):
                        S.dma('pool', wr[:, :, pc_ * 256:(pc_ + 1) * 256], dr['w_in'][l][:, :, 1792 + pc_ * 256:1792 + (pc_ + 1) * 256], writes=['rtw%d' % pc_])
                    brow = sb(st2, "rtbrow", [128, 256], F32)
                    S.dma('sp', brow[:], dr['rows'][l][:, 4352:4608], writes=['rtbrow'])
                    COS = sb(st2, "rtcos", [128, 2048], F32)
                    SIN = sb(st2, "rtsin", [128, 2048], F32)
                    permf = sb(st2, "rtperm", [128, 128], F32)
                    S.dma('sp', COS[:], dr['rcos'], writes=['rtcos'])
                    S.dma('act', SIN[:], dr['rsin'], writes=['rtsin'])
                    S.dma('sp', permf[:], dr['cst'][:, 0, :], writes=['rtperm'])
                    qf = [sb(st2, "rtqf%d" % i, [128, 512], F32) for i in range(2)]
                    t1 = [sb(st2, "rtt1_%d" % i, [128, 512], F32) for i in range(2)]
                    pp = [ps(st2, "rtpp%d" % i, [128, 512], F32) for i in range(2)]
                    pq = [ps(st2, "rtpq%d" % i, [128, 512], F32) for i in range(2)]
                    pt = [ps(st2, "rtpt%d" % i, [128, 512], F32) for i in range(2)]
                    def rtproj(cnt, rc, m, n0, nn):
                        ukeys = uTk[n0 // 128:(n0 + nn) // 128]
                        p_, pk_ = pp[cnt % 2], 'rtpp%d' % (cnt % 2)
                        for jj in range(8):
                            mm(p_[:, 0:nn], wr[:, jj, m * 128:(m + 1) * 128], uT[:, jj, n0:n0 + nn], ['rtw%d' % (m // 2)] + ukeys, [pk_],
                               start=(jj == 0), stop=(jj == 7))
                        yield
                        if m >= 6:
                            act(zs[:, m - 6, n0:n0 + nn], p_[:, 0:nn], AF.Silu, [pk_, 'pvt'], ['rtzs'], bias=pv('bin', 14 + m))
                            return
                        isk = m >= 2
                        j = m % 2
                        dst = (KR if isk else QR)[:, j, n0:n0 + nn]
                        dk = 'rtKR' if isk else 'rtQR'
                        if n0 < 256:
                            if isk:
                                act(dst, p_[:, 0:nn], AF.Identity, [pk_, K_], [dk], bias=tb8[:, j:j + 1], scale=0.125)
                            else:
                                act(dst, p_[:, 0:nn], AF.Identity, [pk_, 'pvt'], [dk], bias=pv('bin', 14 + m))
                            return
                        q_, qk_ = qf[rc % 2], 'rtqf%d' % (rc % 2)
                        a_, ak_ = t1[rc % 2], 'rtt1_%d' % (rc % 2)
                        r_, rk_ = pq[rc % 2], 'rtpq%d' % (rc % 2)
                        if isk:
                            act(q_[:, 0:nn], p_[:, 0:nn], AF.Identity, [pk_, K_], [qk_], bias=tb8[:, j:j + 1], scale=0.125)
                        else:
                            act(q_[:, 0:nn], p_[:, 0:nn], AF.Identity, [pk_, 'pvt'], [qk_], bias=pv('bin', 14 + m))
                        yield
                        mm(r_[:, 0:nn], permf[:], q_[:, 0:nn], ['rtperm', qk_], [rk_])
                        yield
                        tsl = slice(n0 - 256, n0 - 256 + nn)
                        tt('dve', a_[:, 0:nn], r_[:, 0:nn], SIN[:, tsl], ALU.mult, [rk_, 'rtsin'], [ak_])
                        tt('pool', q_[:, 0:nn], q_[:, 0:nn], COS[:, tsl], ALU.mult, [qk_, 'rtcos'], [qk_])
                        yield
                        tt('dve', dst, a_[:, 0:nn], q_[:, 0:nn], ALU.add, [ak_, qk_], [dk])

                    plist = []
                    cnt = 0
                    rc = 0
                    for (n0, nn) in BLOCKS:
                        for m in (0, 1, 2, 3, 6, 7):
                            plist.append((cnt, rc, m, n0, nn))
                            cnt += 1
                            if m < 6 and n0 >= 256:
                                rc += 1
                    run_pipelined((rtproj(*p) for p in plist), 2)
                    for t in range(NTL):
                        p_, pk_ = pt[t % 2], 'rtpt%d' % (t % 2)
                        for jj in range(8):
                            mm(p_[:, 0:256], uT[:, jj, t * 128:(t + 1) * 128], wr[:, jj, 512:768], ['rtw2', uTk[t]], [pk_],
                               start=(jj == 0), stop=(jj == 7))
                        tt('dve', VT[:, t, :], p_[:, 0:256], brow[:], ALU.add, [pk_, 'rtbrow'], ['rtVT'])
                    S.barrier()
                if stop == 'ret_proj':
                    return
                with contextlib.ExitStack() as st2:
                    Sst = [sb(st2, "rtS%d" % d, [128, 2, 64], F32) for d in range(2)]
                    kT = [sb(st2, "rtkT%d" % i, [128, 256], BF16) for i in range(3)]
                    ptr = [ps(st2, "rtptr%d" % i, [128, 8, 128], BF16) for i in range(2)]
                    pU = [ps(st2, "rtpU%d" % i, [128, 512], F32) for i in range(3)]
                    orders = [list(range(NTL)), [1, 0] + list(range(NTL - 1, 1, -1))]
                    for d in range(2):
                        memset('pool', Sst[d][:], 0.0, ['rtS%d' % d])
                    def rtchain(it, step, d):
                        t = orders[d][step]
                        pr, prk = ptr[it % 2], 'rtptr%d' % (it % 2)
                        kt, ktk = kT[it % 3], 'rtkT%d' % (it % 3)
                        pu, puk = pU[it % 3], 'rtpU%d' % (it % 3)
                        puv = pu[:, 0:128].rearrange("p (a b) -> p a b", a=2)
                        for j in range(2):
                            tr(pr[:, j, :], KR[:, j, t * 128:(t + 1) * 128], identb, ['rtKR', 'cstb'], [prk])
                        yield
                        tt('dve', kt[:].rearrange("p (h k) -> p h k", h=4), pr[:, 0:2, :].rearrange("p a (b k) -> p (a b) k", b=2),
                           KDEC[:, d, :].unsqueeze(2).broadcast_to([128, 4, 64]), ALU.mult, [prk, K_], [ktk])
                        yield
                        for h in range(4):
                            hp = (h % 2) * 64
                            mm(puv[hp:hp + 64, h // 2, :], kt[:, h * 64:(h + 1) * 64], VT[:, t, h * 64:(h + 1) * 64], [ktk, 'rtVT'], [puk])
                        yield
                        cp('act', Sall[d][:, :, t, :], Sst[d][:], ['rtS%d' % d], ['rtSall%d_%d' % (d, t)])
                        for j in range(2):
                            stt(Sst[d][:, j, :], Sst[d][:, j, :], GL[:, d * 2 + j:d * 2 + j + 1], puv[:, j, :], ALU.mult, ALU.add,
                                ['rtS%d' % d, K_, puk], ['rtS%d' % d])

                    gbank = mkbanks(st2, 3, "rtgk") if GJ_SPLIT[2] else None
                    gj = gate_jobs(l, last, st2, gbank, GJ_SPLIT[2]) if (GATE_PRE and GJ_SPLIT[2]) else []
                    run_pipelined(interleave([rtchain(i_, sd[0], sd[1]) for i_, sd in enumerate([(s_, d_) for s_ in range(NTL) for d_ in range(2)])], gj, 4), STG['rtchain'])
                    S.barrier()
                if stop == 'ret_chain':
                    return
                with contextlib.ExitStack() as st2:
                    if ('yc%d' % l) in debug:
                        dbgbuf = sb(st2, "dbgbuf", [128, 2, NT], F32)
                    AT = [sb(st2, "rtAT%d" % i, [128, 4, 128], BF16) for i in range(3)]
                    qd = [[sb(st2, "rtqd%d_%d" % (i, d), [128, 2, 128], BF16) for d in range(2)] for i in range(3)]
                    sq = [sb(st2, "rtsq%d" % i, [128, 2, 128], BF16) for i in range(3)]
                    rr = [sb(st2, "rtrr%d" % i, [128, 2, 128], F32) for i in range(3)]
                    ob = [sb(st2, "rtob%d" % i, [128, 2, 128], F32) for i in range(3)]
                    bank = mkbanks(st2, 8, "rtbk")

                    def rtout(t):
                        i2 = t % 3
                        tsl = slice(t * 128, (t + 1) * 128)
                        pas = [bank() for _ in range(2)]
                        for h in range(4):
                            hp = (h % 2) * 64
                            pav = pas[h % 2][0][:, 0:256].rearrange("p (a b) -> p a b", a=2)
                            mm(pav[:, h // 2, :], KR[hp:hp + 64, h // 2, tsl], QR[hp:hp + 64, h // 2, tsl], ['rtKR', 'rtQR'], [pas[h % 2][1]])
                        for d in range(2):
                            tt('pool', qd[i2][d][:], QR[:, :, tsl], QDEC[:, d, :, :], ALU.mult, ['rtQR', K_], ['rtqd%d_%d' % (i2, d)])
                        yield
                        for par in range(2):
                            pav = pas[par][0][:, 0:256].rearrange("p (a b) -> p a b", a=2)
                            tt('dve', AT[i2][:, par::2, :], pav, DS[:, par::2, :], ALU.mult, [pas[par][1], K_], ['rtAT%d' % i2])
                        yield
                        pos = [bank() for _ in range(2)]
                        povs = [pos[par][0][:, 0:256].rearrange("p (a b) -> p a b", a=2) for par in range(2)]
                        for h in range(4):
                            hp = (h % 2) * 64
                            pok = pos[h % 2][1]
                            reg = povs[h % 2][hp:hp + 64, h // 2, :]
                            mm(reg, VT[:, t, h * 64:(h + 1) * 64], AT[i2][:, h, :], ['rtVT', 'rtAT%d' % i2], [pok], start=True, stop=False)
                            for d in range(2):
                                mm(reg, Sall[d][hp:hp + 64, h // 2, t, :], qd[i2][d][hp:hp + 64, h // 2, :],
                                   ['rtSall%d_%d' % (d, t), 'rtqd%d_%d' % (i2, d)], [pok], start=False, stop=(d == 1))
                        yield
                        obk = 'rtob%d' % i2
                        cp('act', ob[i2][0:64], povs[0][0:64], [pos[0][1]], [obk])
                        cp('dve', ob[i2][64:128], povs[1][64:128], [pos[1][1]], [obk])
                        yield
                        pov = ob[i2][:]
                        pok = obk
                        if ('yc%d' % l) in debug:
                            cp('pool', dbgbuf[:, :, tsl], pov, [pok], ['dbgbuf'])
                        act(sq[i2][:], pov, AF.Square, [pok], ['rtsq%d' % i2])
                        yield
                        pss_, psk = bank()
                        psv = pss_[:, 0:256].rearrange("p (a b) -> p a b", a=2)
                        for j in range(2):
                            mm(psv[:, j, :], bonesb, sq[i2][:, j, :], ['cstb', 'rtsq%d' % i2], [psk])
                        yield
                        act(rr[i2][:], psv, AF.Ln, [psk], ['rtrr%d' % i2], bias=RMS_EPS, scale=1.0 / 64)
                        yield
                        act(rr[i2][:], rr[i2][:], AF.Exp, ['rtrr%d' % i2], ['rtrr%d' % i2], scale=-0.5)
                        yield
                        tt('dve', rr[i2][:], pov, rr[i2][:], ALU.mult, [pok, 'rtrr%d' % i2], ['rtrr%d' % i2])
                        yield
                        tt('pool', Y[:, 2, :, tsl], rr[i2][:], zs[:, :, tsl], ALU.mult, ['rtrr%d' % i2, 'rtzs'], ['Y2'])

                    gj = gate_jobs(l, last, st2, bank, GJ_SPLIT[3]) if (GATE_PRE and GJ_SPLIT[3]) else []
                    run_pipelined(interleave([rtout(t) for t in range(NTL) if not (last and t < 2 and not debug)], gj, 3), STG['out'])
                    if ('yc%d' % l) in debug:
                        dbg_dump('yc%d' % l, dbgbuf[:], [128, 2, NT], ['dbgbuf'])
                    S.barrier()
                S.barrier()
        PHASES['ret'] = phase_ret
        def phase_rw(l, h_src, last):
            with contextlib.ExitStack() as st:
                RB = sb(st, "rwRB", [128, 2, NT], BF16)
                KB = sb(st, "rwKB", [128, 2, NT], BF16)
                VB = sb(st, "rwVB", [128, 2, NT], BF16)
                LB = sb(st, "rwLB", [128, NT], BF16)
                zs = sb(st, "rwzs", [128, 2, NT], BF16)
                vT = sb(st, "rwvT", [128, NTL, 256], BF16)
                lw2b = sb(st, "rwlw2", [128, 2, 256], BF16)
                S.dma('pool', lw2b[:], dr['lw2'][l], writes=['rwlw2'])
                oka = sb(st, "rwoka", [128, 2], F32)
                ts('dve', oka[:], pv('ka'), -1.0, 1.0, ALU.mult, ALU.add, ['pvt'], ['rwoka'])
                seen_b, seen_o = set(), set()
                with contextlib.ExitStack() as st2:
                    ww = sb(st2, "rww", [128, 8, 1152], BF16)
                    for pc_ in range(9):
                        S.dma('pool', ww[:, :, pc_ * 128:(pc_ + 1) * 128], dr['w_in'][l][:, :, 2816 + pc_ * 128:2816 + (pc_ + 1) * 128], writes=['rww%d' % pc_])
                    XRs = [sb(st2, "rwXR%d" % i, [128, NT + 4], F32) for i in range(2)]
                    XSs = [sb(st2, "rwXS%d" % i, [128, NT], F32) for i in range(2)]
                    c0 = sb(st2, "rwc0", [128, 7], F32)
                    pp = [ps(st2, "rwpp%d" % i, [128, 512], F32) for i in range(3)]
                    ptr = [ps(st2, "rwptr%d" % i, [128, 8, 128], BF16) for i in range(2)]
                    o_mu, _ = PV['mu']
                    mu0, mu1 = pvt[:, o_mu:o_mu + 7], pvt[:, o_mu + 7:o_mu + 14]
                    tt('dve', c0[:], mu0, mu1, ALU.add, ['pvt'], ['rwc0'])
                    ts('dve', c0[:], c0[:], -1.0, 1.0, ALU.mult, ALU.add, ['rwc0'], ['rwc0'])
                    for i in range(2):
                        memset('pool', XRs[i][:], 0.0, ['rwXR%d' % i])
                    cnt = 0
                    for m in (0, 1, 2, 3, 4, 5, 7, 6, 8):
                        XR, XRk = XRs[m % 2], 'rwXR%d' % (m % 2)
                        XS, XSk = XSs[m % 2], 'rwXS%d' % (m % 2)
                        for (n0, nn) in BLOCKS:
                            p_, pk_ = pp[cnt % 3], 'rwpp%d' % (cnt % 3)
                            cnt += 1
                            for jj in range(8):
                                mm(p_[:, 0:nn], ww[:, jj, m * 128:(m + 1) * 128], uT[:, jj, n0:n0 + nn],
                                   ['rww%d' % m] + uTk[n0 // 128:(n0 + nn) // 128], [pk_], start=(jj == 0), stop=(jj == 7))
                            if m >= 7:
                                act(zs[:, m - 7, n0:n0 + nn], p_[:, 0:nn], AF.Silu, [pk_, 'pvt'], ['rwzs'], bias=pv('bin', 22 + m))
                            else:
                                xo = 1 if n0 < 256 else 3
                                act(XR[:, n0 + xo:n0 + xo + nn], p_[:, 0:nn], AF.Identity, [pk_, 'pvt'], [XRk], bias=pv('bin', 22 + m))
                        if m >= 7:
                            continue
                        for (b0, ln, o0) in ((1, 256, 0), (259, 2048, 256)):
                            ts('dve', XS[:, o0:o0 + ln], XR[:, b0:b0 + ln], c0[:, m:m + 1], None, ALU.mult, None, [XRk, 'rwc0'], [XSk])
                            stt(XS[:, o0:o0 + ln], XR[:, b0 - 1:b0 - 1 + ln], mu0[:, m:m + 1], XS[:, o0:o0 + ln], ALU.mult, ALU.add,
                                [XRk, 'pvt', XSk], [XSk])
                            if m < 6:
                                dstT, dk = [(RB, 'rwRB'), (KB, 'rwKB'), (VB, 'rwVB')][m // 2]
                                stt(dstT[:, m % 2, o0:o0 + ln], XR[:, b0 + 1:b0 + 1 + ln], mu1[:, m:m + 1], XS[:, o0:o0 + ln], ALU.mult, ALU.add,
                                    [XRk, 'pvt', XSk], [dk])
                            else:
                                stt(XS[:, o0:o0 + ln], XR[:, b0 + 1:b0 + 1 + ln], mu1[:, m:m + 1], XS[:, o0:o0 + ln], ALU.mult, ALU.add,
                                    [XRk, 'pvt', XSk], [XSk])
                        if m == 6:
                            act(LB[0:64, :], XS[0:64, :], AF.Tanh, [XSk], ['rwLB'])
                            cp('pool', LB[64:128, :], XS[64:128, :], [XSk], ['rwLB'])
                    for t in range(NTL):
                        pr, prk = ptr[t % 2], 'rwptr%d' % (t % 2)
                        for j in range(2):
                            tr(pr[:, j, :], VB[:, j, t * 128:(t + 1) * 128], identb, ['rwVB', 'cstb'], [prk])
                        cp('dve' if t % 2 == 0 else 'act', vT[:, t, :], pr[:, 0:2, :].rearrange("p a b -> p (a b)"), [prk], ['rwvT'])
                    S.barrier()
                if stop == 'rw_proj':
                    return
                OS = sb(st, "rwOS", [128, 2, NT], F32)
                with contextlib.ExitStack() as st2:
                    def B(name, shape, dt=BF16):
                        return sb(st2, "rw_" + name, shape, dt), "rw_" + name
                    R64, R64k = B("R64", [128, 256], BF16)
                    memset('pool', R64[:], 1.0, [R64k])
                    memset('pool', R64[:, 0:256:64], 0.0, [R64k])
                    LW, LWk = B("LW", [128, 2, 128], F32)
                    SA, SAk = B("SA", [128, 2, 128], F32)
                    LGm, LGk = B("LG", [128, 2, 128], F32)
                    U0, U0k = LGm, LGk
                    EG, EGk = B("EG", [128, 2, 128], F32)
                    ENG, ENGk = B("ENG", [128, 2, 128], F32)
                    EGM, EGMk = B("EGM", [128, 2, 128], F32)
                    TA, TAk = B("TA", [128, 2, 128], F32)
                    TB_, TBk = B("TB", [128, 2, 128], F32)
                    SQ, SQk = B("SQ", [128, 2, 128])
                    RKD, RKDk = SQ, SQk
                    OBt = (None, None)
                    Zst = [B("Z%d" % d, [128, 2, 64], F32) for d in range(2)]
                    BUF = [dict() for _ in range(2)]
                    for d_ in range(2):
                        BUF[d_]['KKN'] = B("KKN_%d" % d_, [128, 2, 128])
                        BUF[d_]['KT'] = [B("KT_%d_%d" % (d_, s_), [128, 3, 2, 128]) for s_ in range(2)]
                        BUF[d_]['RT'] = [B("RT_%d_%d" % (d_, s_), [128, 2, 128]) for s_ in range(2)]
                        for j_ in range(2):
                            sfx = "_%d_%d" % (d_, j_)
                            SB = dict()
                            SB['TM'] = B("TM" + sfx, [128, 3, 128])
                            for nm_ in ('A1T', 'A2T', 'A3T', 'A4T', 'ALT', 'Tm', 'TTm', 'Xb', 'RHS', 'BYb'):
                                SB[nm_] = B(nm_ + sfx, [128, 2, 128])
                            SB['NY'] = B("NY" + sfx, [128, 2, 64])
                            SB['RH'] = B("RH" + sfx, [128, 128])
                            SB['GTb'] = B("GTb" + sfx, [128, 2, 128])
                            SB['ZLG'] = B("ZLG" + sfx, [128, 2, 64], F32)
                            SB['Z0b'] = B("Z0b" + sfx, [128, 2, 64])
                            BUF[d_][j_] = SB
                        BUF[d_]['GLt'] = [B("GLt_%d_%d" % (d_, s_), [128, 2, 2], F32) for s_ in range(2)]
                    banks = [ps(st2, "rwbank%d" % i, [128, 512], F32) for i in range(8)]
                    bcnt = [0]

                    def bank():
                        i = bcnt[0] % 8
                        bcnt[0] += 1
                        return banks[i], 'rwbank%d' % i
                    for d in range(2):
                        memset('pool', Zst[d][0][:], 0.0, [Zst[d][1], 'rw_Zs_%d_0' % d, 'rw_Zs_%d_1' % d])
                    for d_ in range(2):
                        for j_ in range(2):
                            memset('pool', BUF[d_][j_]['GTb'][0][:], 0.0, [BUF[d_][j_]['GTb'][1]])
                    orders = [list(range(NTL)), [1, 0] + list(range(NTL - 1, 1, -1))]
                    bc3 = lambda ap: ap.unsqueeze(2).broadcast_to([128, 2, 128])
                    def prep(d, t, slot):
                        KKN, KKNk = BUF[d]['KKN']
                        KT, KTk = BUF[d]['KT'][slot]
                        RTb, RTk = BUF[d]['RT'][slot]
                        GLt, GLk = BUF[d]['GLt'][slot]
                        tsl = slice(t * 128, (t + 1) * 128)
                        rev = (d == 1)
                        Z, Zk = Zst[d]
                        plw, plwk = bank()
                        pla, plak = bank()
                        plwv = plw[:, 0:256].rearrange("p (j t) -> p j t", j=2)
                        plav = pla[:, 0:256].rearrange("p (j t) -> p j t", j=2)
                        wb_ = 32 * d
                        for j in range(2):
                            mm(plwv[:, j, :], lw2b[wb_:wb_ + 16, d, j * 128:(j + 1) * 128], LB[wb_:wb_ + 16, tsl], ['rwlw2', 'rwLB'], [plwk])
                        for j in range(2):
                            mm(plav[:, j, :], lw2b[64:96, d, j * 128:(j + 1) * 128], LB[64:96, tsl], ['rwlw2', 'rwLB'], [plak])
                        for j in range(2):
                            act(LW[:, j, :], plwv[:, j, :], AF.Sigmoid, [plwk, 'pvt'], [LWk], bias=pv('w0', d * 2 + j))
                            act(SA[:, j, :], plav[:, j, :], AF.Sigmoid, [plak, 'pvt'], [SAk], bias=pv('a0', d * 2 + j))
                        ts('dve', LW[:], LW[:], -0.6065306597126334, None, ALU.mult, None, [LWk], [LWk])
                        yield
                        lwf = LW[:].rearrange("p a b -> p (a b)")
                        lgf = LGm[:].rearrange("p a b -> p (a b)")
                        if not rev:
                            S.op('dve', lambda e: e.tensor_tensor_scan(out=lgf, data0=R64[:], data1=lwf, initial=0.0, op0=ALU.mult, op1=ALU.add),
                                 reads=[LWk, R64k], writes=[LGk])
                        else:
                            S.op('dve', lambda e: e.tensor_tensor_scan(out=lgf[:, ::-1], data0=R64[:], data1=lwf[:, ::-1], initial=0.0,
                                                                       op0=ALU.mult, op1=ALU.add), reads=[LWk, R64k], writes=[LGk])
                        act(EG[:], LGm[:], AF.Exp, [LGk], [EGk])
                        yield
                        act(ENG[:], LGm[:], AF.Exp, [LGk], [ENGk], scale=-1.0)
                        yield
                        tt('pool', TA[:], LGm[:], LW[:], ALU.subtract, [LGk, LWk], [TAk])
                        yield
                        act(EGM[:], TA[:], AF.Exp, [TAk], [EGMk])
                        yield
                        gsrc = EG[:, :, 63::64] if not rev else EG[:, :, 0::64]
                        cp('pool', GLt[:], gsrc, [EGk], [GLk])
                        yield
                        tt('dve', TA[:], KB[:, :, tsl], bc3(pv('kk')), ALU.mult, ['rwKB', 'pvt', TAk], [TAk])
                        yield
                        act(SQ[:], TA[:], AF.Square, [TAk], [SQk])
                        yield
                        pss_, pssk = bank()
                        pssv = pss_[:, 0:256].rearrange("p (a b) -> p a b", a=2)
                        for j in range(2):
                            mm(pssv[:, j, :], bonesb, SQ[:, j, :], ['cstb', SQk], [pssk])
                        act(TB_[:], pssv, AF.Ln, [pssk], [TBk], bias=1e-24)
                        yield
                        act(TB_[:], TB_[:], AF.Exp, [TBk], [TBk], scale=-0.5)
                        yield
                        tt('dve', KKN[:], TA[:], TB_[:], ALU.mult, [TAk, TBk], [KKNk])
                        yield
                        tt('pool', KT[:, 0], KKN[:], EGM[:], ALU.mult, [KKNk, EGMk], [KTk])
                        yield
                        tt('dve', TA[:], SA[:], ENG[:], ALU.mult, [SAk, ENGk, TAk], [TAk])
                        yield
                        tt('pool', KT[:, 1], KKN[:], TA[:], ALU.mult, [KKNk, TAk], [KTk])
                        yield
                        tt('dve', U0[:], SA[:], bc3(pv('ka')), ALU.mult, [SAk, 'pvt'], [U0k])
                        yield
                        tt('dve', U0[:], U0[:], bc3(oka[:]), ALU.add, [U0k, 'rwoka'], [U0k])
                        yield
                        tt('pool', TB_[:], U0[:], ENG[:], ALU.mult, [U0k, ENGk, TBk], [TBk])
                        yield
                        tt('pool', KT[:, 2], KB[:, :, tsl], TB_[:], ALU.mult, ['rwKB', TBk], [KTk])
                        yield
                        tt('dve', RTb[:], RB[:, :, tsl], EG[:], ALU.mult, ['rwRB', EGk], [RTk])
                        yield
                        tt('dve', U0[:], U0[:], KB[:, :, tsl], ALU.mult, [U0k, 'rwKB'], [U0k])
                        yield
                        tt('dve', U0[:], U0[:], bc3(pv('rk')), ALU.mult, [U0k, 'pvt'], [U0k])
                        yield
                        tt('pool', RKD[:], U0[:], RB[:, :, tsl], ALU.mult, [U0k, 'rwRB'], [RKDk])
                        yield
                        pbn, pbnk = bank()
                        pbnv = pbn[:, 0:256].rearrange("p (a b) -> p a b", a=2)
                        for j in range(2):
                            mm(pbnv[:, j, :], bonesb, RKD[:, j, :], ['cstb', RKDk], [pbnk])
                        if t not in seen_b:
                            seen_b.add(t)
                            tt('dve', Y[:, 3, :, tsl], pbnv, VB[:, :, tsl], ALU.mult, [pbnk, 'rwVB'], ['Y3'])
                        else:
                            tt('dve', TA[:], pbnv, VB[:, :, tsl], ALU.mult, [pbnk, 'rwVB', TAk], [TAk])
                            tt('pool', Y[:, 3, :, tsl], Y[:, 3, :, tsl], TA[:], ALU.add, ['Y3', TAk], ['Y3'])

                    def prep_pair(step):
                        for d_ in range(2):
                            yield from prep(d_, orders[d_][step], step % 2)

                    def unit(d, t, slot):
                        KT, KTk = BUF[d]['KT'][slot]
                        RTb, RTk = BUF[d]['RT'][slot]
                        GLt, GLk = BUF[d]['GLt'][slot]
                        tsl = slice(t * 128, (t + 1) * 128)
                        rev = (d == 1)
                        subs = [stream(d, j, t, rev, tsl, KT, KTk, RTb, RTk, GLt, GLk) for j in range(2)]
                        while subs:
                            for g in list(subs):
                                try:
                                    next(g)
                                except StopIteration:
                                    subs.remove(g)
                                yield

                    def stream(d, j, t, rev, tsl, KT, KTk, RTb, RTk, GLt, GLk):
                        SB = BUF[d][j]
                        TM, TMk = SB['TM']
                        A1T, A1k = SB['A1T']
                        A2T, A2k = SB['A2T']
                        A3T, A3k = SB['A3T']
                        A4T, A4k = SB['A4T']
                        ALT, ALk = SB['ALT']
                        Tm, Tmk = SB['Tm']
                        TTm, TTk = SB['TTm']
                        Xb, Xbk = SB['Xb']
                        RHS, RHSk = SB['RHS']
                        BYb, BYk = SB['BYb']
                        NY, NYk = SB['NY']
                        RH, RHk = SB['RH']
                        GTb, GTk = SB['GTb']
                        ZLG, ZLGk = SB['ZLG']
                        Z0b, Z0k = SB['Z0b']
                        Z, _zk = Zst[d]
                        Zk = 'rw_Zs_%d_%d' % (d, j)
                        ptb, ptbk = bank()
                        ptv = ptb[:].bitcast(BF16).rearrange("p (a b) -> p a b", a=8)
                        for x in range(3):
                            tr(ptv[:, x, :], KT[:, x, j, :], identb, [KTk, 'cstb'], [ptbk])
                        cp('act', TM[:], ptv[:, 0:3, :], [ptbk], [TMk])
                        yield

                        def amat(dst, dstk, li, ri_src, ri_k, mslot):
                            pas = []
                            for par in range(2):
                                hp = par * 64
                                pa, pak = bank()
                                rhs = (RTb[hp:hp + 64, j, :] if ri_src is None else KT[hp:hp + 64, ri_src, j, :])
                                mm(pa[:, 0:128], KT[hp:hp + 64, li, j, :], rhs, [KTk, ri_k], [pak])
                                pas.append((pa, pak))
                            return pas

                        def aevac(pas, dst, dstk, mslot):
                            for par, (pa, pak) in enumerate(pas):
                                if mslot is None:
                                    cp('act', dst[:, par, :], pa[:, 0:128], [pak], [dstk])
                                else:
                                    tt('dve', dst[:, par, :], pa[:, 0:128], maskb[:, mslot, :], ALU.mult, [pak, 'maskb'], [dstk])
                        for (dst, dstk, li, rs, rk, ms) in ((A1T, A1k, 1, 0, KTk, None), (A2T, A2k, 2, 0, KTk, 2 + d),
                                                            (A3T, A3k, 1, None, RTk, 4 + d), (A4T, A4k, 2, None, RTk, 4 + d)):
                            pas = amat(dst, dstk, li, rs, rk, ms)
                            aevac(pas, dst, dstk, ms)
                            yield
                        idb2 = identb.unsqueeze(1).broadcast_to([128, 2, 128])
                        cp('pool', Tm[:], idb2, ['cstb'], [Tmk])
                        cp('pool', TTm[:], idb2, ['cstb'], [TTk])
                        for lv in range(6):
                            tt('pool', ALT[:], A1T[:], maskb[:, 6 + d * 6 + lv, :].unsqueeze(1).broadcast_to([128, 2, 128]), ALU.mult,
                               [A1k, 'maskb'], [ALk])
                            yield
                            px, pxk = bank()
                            pxv = px[:, 0:256].rearrange("p (h t) -> p h t", h=2)
                            for par in range(2):
                                mm(pxv[:, par, :], ALT[:, par, :], Tm[:, par, :], [ALk, Tmk], [pxk])
                            cp('act', Xb[:], pxv, [pxk], [Xbk])
                            yield
                            py_, pyk = bank()
                            pyv = py_[:].rearrange("p (x h t) -> p x h t", x=2, h=2)
                            for par in range(2):
                                mm(pyv[:, 0, par, :], Xb[:, par, :], TTm[:, par, :], [Xbk, TTk], [pyk])
                            if lv < 5:
                                for par in range(2):
                                    mm(pyv[:, 1, par, :], TTm[:, par, :], Xb[:, par, :], [Xbk, TTk], [pyk])
                            if lv < 5:
                                tt('dve', Tm[:], Tm[:], pyv[:, 1], ALU.subtract, [Tmk, pyk], [Tmk])
                            tt('dve', TTm[:], TTm[:], pyv[:, 0], ALU.subtract, [TTk, pyk], [TTk])
                            yield
                        pw, pwk = bank()
                        pwv = pw[:, 0:128].rearrange("p (h v) -> p h v", h=2)
                        for par in range(2):
                            h = 2 * j + par
                            mm(pwv[:, par, :], A2T[:, par, :], vT[:, t, h * 64:(h + 1) * 64], [A2k, 'rwvT'], [pwk])
                        cp('pool', RHS[:, :, 0:64], TM[:, 0, :].rearrange("p (h k) -> p h k", h=2), [TMk], [RHSk])
                        cp('act', RHS[:, :, 64:128], pwv, [pwk], [RHSk])
                        yield
                        pby, pbyk = bank()
                        pbyv = pby[:, 0:256].rearrange("p (h t) -> p h t", h=2)
                        for par in range(2):
                            mm(pbyv[:, par, :], TTm[:, par, :], RHS[:, par, :], [TTk, RHSk], [pbyk])
                        cp('act', BYb[:], pbyv, [pbyk], [BYk])
                        yield
                        ts('pool', NY[:], BYb[:, :, 64:128], -1.0, 0.0, ALU.mult, ALU.add, [BYk], [NYk])
                        pr_, prk = bank()
                        for par in range(2):
                            hp = par * 64
                            mm(pr_[hp:hp + 64, 0:128], BYb[:, par, 0:64], A3T[:, par, :], [BYk, A3k], [prk])
                        tt('dve', RH[:], RTb[:, j, :], pr_[:, 0:128], ALU.subtract, [RTk, prk], [RHk])
                        yield
                        for c in range(2):
                            cs = slice(c * 64, (c + 1) * 64)
                            pg_, pgk = bank()
                            pgv = pg_[:, 0:128].rearrange("p (x v) -> p x v", x=2)
                            for par in range(2):
                                hp = par * 64
                                h = 2 * j + par
                                hc = slice(h * 64, (h + 1) * 64)
                                pc = slice(par * 64, (par + 1) * 64)
                                mm(pgv[hp:hp + 64, 0, :], BYb[cs, par, 0:64], TM[cs, 1, pc], [BYk, TMk], [pgk])
                                mm(pgv[hp:hp + 64, 1, :], TM[cs, 2, pc], vT[cs, t, hc], [TMk, 'rwvT'], [pgk], start=True, stop=False)
                                mm(pgv[hp:hp + 64, 1, :], TM[cs, 1, pc], NY[cs, par, :], [TMk, NYk], [pgk], start=False, stop=True)
                            for par in range(2):
                                hp = par * 64
                                tt('dve', GTb[hp:hp + 64, c, hp:hp + 64], cstf[hp:hp + 64, 4, 0:64], pgv[hp:hp + 64, 0, :], ALU.subtract,
                                   ['cstf', pgk], [GTk])
                            act(ZLG[:, c, :], pgv[:, 1, :], AF.Identity, [pgk, GLk], [ZLGk], scale=GLt[:, j, c:c + 1])
                            yield
                        for c in ((0, 1) if not rev else (1, 0)):
                            cp('act', Z0b[:, c, :], Z[:, j, :], [Zk], [Z0k])
                            yield
                            pn, pnk = bank()
                            mm(pn[:, 0:64], GTb[:, c, :], Z0b[:, c, :], [GTk, Z0k], [pnk])
                            stt(Z[:, j, :], pn[:, 0:64], GLt[:, j, c:c + 1], ZLG[:, c, :], ALU.mult, ALU.add, [pnk, GLk, ZLGk, Zk], [Zk])
                            yield
                        for par in range(2):
                            hp = par * 64
                            h = 2 * j + par
                            hc = slice(h * 64, (h + 1) * 64)
                            po_, pok = bank()
                            reg = po_[hp:hp + 64, 0:128]
                            mm(reg, vT[:, t, hc], A4T[:, par, :], ['rwvT', A4k], [pok], start=True, stop=False)
                            mm(reg, NY[:, par, :], A3T[:, par, :], [NYk, A3k], [pok], start=False, stop=False)
                            for c in range(2):
                                mm(reg[:, c * 64:(c + 1) * 64], Z0b[hp:hp + 64, c, :], RH[hp:hp + 64, c * 64:(c + 1) * 64],
                                   [Z0k, RHk], [pok], start=False, stop=(c == 1))
                            osl = OS[hp:hp + 64, j, tsl]
                            osk = 'rwOS%d_%d' % (t, j)
                            if (t, j, par) not in seen_o:
                                seen_o.add((t, j, par))
                                cp('act', osl, reg, [pok], [osk])
                            else:
                                tt('dve', osl, osl, reg, ALU.add, [pok, osk], [osk])
                            yield

                    for _ in prep_pair(0):
                        pass
                    for step in range(NTL):
                        gens = [unit(d, orders[d][step], step % 2) for d in range(2)]
                        if step + 1 < NTL:
                            gens.append(prep_pair(step + 1))
                        while gens:
                            for g in list(gens):
                                try:
                                    next(g)
                                except StopIteration:
                                    gens.remove(g)
                    S.barrier()
                if stop is not None and stop.startswith('rw_'):
                    return
                with contextlib.ExitStack() as st2:
                    ob = [sb(st2, "rwob%d" % i, [128, 2, 128], BF16) for i in range(2)]
                    cen = [sb(st2, "rwcen%d" % i, [128, 2, 128], F32) for i in range(2)]
                    rs = [sb(st2, "rwrs%d" % i, [128, 2, 128], F32) for i in range(2)]
                    pm_ = [ps(st2, "rwpm%d" % i, [128, 512], F32) for i in range(2)]
                    pv_ = [ps(st2, "rwpv%d" % i, [128, 512], F32) for i in range(2)]
                    for t in range(NTL):
                        i2 = t % 2
                        tsl = slice(t * 128, (t + 1) * 128)
                        osk = 'rwOS%d_0' % t
                        osk1 = 'rwOS%d_1' % t
                        cp('act', ob[i2][:], OS[:, :, tsl], [osk, osk1], ['rwob%d' % i2])
                        pmv = pm_[i2][:, 0:256].rearrange("p (a b) -> p a b", a=2)
                        for j in range(2):
                            mm(pmv[:, j, :], bonesb, ob[i2][:, j, :], ['cstb', 'rwob%d' % i2], ['rwpm%d' % i2])
                        stt(cen[i2][:], pmv, -1.0 / 64, OS[:, :, tsl], ALU.mult, ALU.add, ['rwpm%d' % i2, osk, osk1], ['rwcen%d' % i2])
                        act(ob[i2][:], cen[i2][:], AF.Square, ['rwcen%d' % i2], ['rwob%d' % i2])
                        pvv = pv_[i2][:, 0:256].rearrange("p (a b) -> p a b", a=2)
                        for j in range(2):
                            mm(pvv[:, j, :], bonesb, ob[i2][:, j, :], ['cstb', 'rwob%d' % i2], ['rwpv%d' % i2])
                        act(rs[i2][:], pvv, AF.Ln, ['rwpv%d' % i2], ['rwrs%d' % i2], bias=RW_GN_EPS, scale=1.0 / 64)
                        act(rs[i2][:], rs[i2][:], AF.Exp, ['rwrs%d' % i2], ['rwrs%d' % i2], scale=-0.5)
                        tt('dve', cen[i2][:], cen[i2][:], rs[i2][:], ALU.mult, ['rwcen%d' % i2, 'rwrs%d' % i2], ['rwcen%d' % i2])
                        tt('pool', cen[i2][:], cen[i2][:], bc3(pv('gnw')), ALU.mult, ['rwcen%d' % i2, 'pvt'], ['rwcen%d' % i2])
                        tt('pool', cen[i2][:], cen[i2][:], bc3(pv('gnb')), ALU.add, ['rwcen%d' % i2, 'pvt'], ['rwcen%d' % i2])
                        tt('dve', cen[i2][:], cen[i2][:], Y[:, 3, :, tsl], ALU.add, ['rwcen%d' % i2, 'Y3'], ['rwcen%d' % i2])
                        if ('yd%d' % l) in debug:
                            cp('act', OS[:, :, tsl], cen[i2][:], ['rwcen%d' % i2], [osk, osk1])
                        tt('dve', Y[:, 3, :, tsl], cen[i2][:], zs[:, :, tsl], ALU.mult, ['rwcen%d' % i2, 'rwzs'], ['Y3'])
                    if ('yd%d' % l) in debug:
                        dbg_dump('yd%d' % l, OS[:], [128, 2, NT], ['rwOS%d_%d' % (t, j_) for t in range(NTL) for j_ in range(2)])
                    S.barrier()
                S.barrier()
        PHASES['rw'] = phase_rw
        def phase_merge(l, h_src, last):
            h_dst = out_d if last else h1_d
            with contextlib.ExitStack() as st:
                MG = sb(st, "mgMG", [128, 8, NT], BF16)
                wbr = sb(st, "mgwbr", [128, 4, 2, DM], BF16)
                S.dma('pool', wbr[:], dr['wbr'][l], writes=['mgwbr'])
                with contextlib.ExitStack() as st2:
                    wg = [sb(st2, "mgwg%d" % i, [128, 8, 4, 128], BF16) for i in range(2)]
                    sg = [sb(st2, "mgsg%d" % i, [128, 512], BF16) for i in range(3)]
                    ac = [sb(st2, "mgac%d" % i, [128, 512], F32) for i in range(2)]
                    tm = [sb(st2, "mgtm%d" % i, [128, 512], F32) for i in range(2)]
                    pgl = [ps(st2, "mgpg%d" % i, [128, 512], F32) for i in range(3)]
                    pbr = [ps(st2, "mgpb%d" % i, [128, 512], F32) for i in range(3)]
                    cg = 0
                    ca = 0
                    def load_wg(dt_):
                        for k in range(4):
                            if (l, k * 8 + dt_) in pre_sg:
                                continue
                            c0 = 3968 + k * 1024 + dt_ * 128
                            S.dma('pool', wg[dt_ % 2][:, :, k, :], dr['w_in'][l][:, :, c0:c0 + 128], writes=['mgwg%d' % (dt_ % 2)])
                    load_wg(0)
                    for dt_ in range(8):
                        w_, wk_ = wg[dt_ % 2], 'mgwg%d' % (dt_ % 2)
                        if dt_ + 1 < 8:
                            load_wg(dt_ + 1)
                        for (n0, nn) in BLOCKS:
                            if last and n0 < 256:
                                continue
                            a_, ak_ = ac[ca % 2], 'mgac%d' % (ca % 2)
                            t_, tk_ = tm[ca % 2], 'mgtm%d' % (ca % 2)
                            ca += 1
                            for k in range(4):
                                pg_, pgk_ = pgl[cg % 3], 'mgpg%d' % (cg % 3)
                                pb_, pbk_ = pbr[cg % 3], 'mgpb%d' % (cg % 3)
                                s_, sk_ = sg[cg % 3], 'mgsg%d' % (cg % 3)
                                cg += 1
                                if (l, k * 8 + dt_) in pre_sg:
                                    S.dma('sp' if cg % 2 == 0 else 'act', s_[:, 0:nn], sgd[k * 8 + dt_][:, n0:n0 + nn], reads=['sgd'], writes=[sk_])
                                else:
                                    for jj in range(8):
                                        mm(pg_[:, 0:nn], w_[:, jj, k, :], uT[:, jj, n0:n0 + nn], [wk_] + uTk[n0 // 128:(n0 + nn) // 128], [pgk_],
                                           start=(jj == 0), stop=(jj == 7))
                                    act(s_[:, 0:nn], pg_[:, 0:nn], AF.Sigmoid, [pgk_, 'pvt'], [sk_], bias=pv('bin', 31 + k * 8 + dt_))
                                for jc in range(2):
                                    mm(pb_[:, 0:nn], wbr[:, k, jc, dt_ * 128:(dt_ + 1) * 128], Y[:, k, jc, n0:n0 + nn], ['mgwbr', 'Y%d' % k], [pbk_],
                                       start=(jc == 0), stop=(jc == 1))
                                if k == 0:
                                    tt('dve', a_[:, 0:nn], pb_[:, 0:nn], s_[:, 0:nn], ALU.mult, [pbk_, sk_], [ak_])
                                else:
                                    tt('dve', t_[:, 0:nn], pb_[:, 0:nn], s_[:, 0:nn], ALU.mult, [pbk_, sk_], [tk_])
                                    if k < 3:
                                        tt('pool', a_[:, 0:nn], a_[:, 0:nn], t_[:, 0:nn], ALU.add, [ak_, tk_], [ak_])
                                    else:
                                        tt('pool', MG[:, dt_, n0:n0 + nn], a_[:, 0:nn], t_[:, 0:nn], ALU.add, [ak_, tk_], ['mgMG%d' % (n0 // 512 if n0 else 9)])
                    S.barrier()
                if ('merged%d' % l) in debug:
                    with contextlib.ExitStack() as st2:
                        mf = sb(st2, "mgf", [128, 8, NT], F32)
                        cp('dve', mf[:], MG[:], ['mgMG%d' % i for i in (9, 0, 1, 2, 3)], ['mgf'])
                        dbg_dump('merged%d' % l, mf[:], [128, 8, NT], ['mgf'])
                        S.barrier()
                with contextlib.ExitStack() as st2:
                    wo = sb(st2, "mgwo", [128, 8, DM], BF16)
                    S.dma('pool', wo[:], dr['wout'][l], writes=['mgwo'])
                    rows = sb(st2, "mgrows", [128, 3, DM], F32)
                    S.dma('sp', rows[:], dr['rows'][l][:, 0:3072].rearrange("p (a b) -> p a b", a=3), writes=['mgrows'])
                    hin_ = [sb(st2, "mghin%d" % i, [128, DM], F32) for i in range(2)]
                    ot = [sb(st2, "mgot%d" % i, [128, DM], F32) for i in range(2)]
                    stat = [sb(st2, "mgst%d" % i, [128, 16], F32) for i in range(2)]
                    po = [[ps(st2, "mgpo%d_%d" % (i, hh), [128, 512], F32) for hh in range(2)] for i in range(2)]
                    def mgout(it, t):
                        i2 = it % 2
                        ci = 1 if t < 2 else 0
                        tsl = slice(t * 128, (t + 1) * 128)
                        mgk = 'mgMG%d' % (9 if t < 2 else (t - 2) // 4)
                        hk_, ok_, sk_ = 'mghin%d' % i2, 'mgot%d' % i2, 'mgst%d' % i2
                        hi, o_, sti = hin_[i2], ot[i2], stat[i2]
                        S.dma('sp', hi[:], h_src[t * 128:(t + 1) * 128, :], writes=[hk_])
                        for hh in range(2):
                            pk_ = 'mgpo%d_%d' % (i2, hh)
                            for jj in range(8):
                                mm(po[i2][hh][:], MG[:, jj, tsl], wo[:, jj, hh * 512:(hh + 1) * 512], [mgk, 'mgwo'], [pk_], start=(jj == 0), stop=(jj == 7))
                        yield
                        for hh in range(2):
                            pk_ = 'mgpo%d_%d' % (i2, hh)
                            tt('dve', o_[:, hh * 512:(hh + 1) * 512], po[i2][hh][:], rows[:, 0, hh * 512:(hh + 1) * 512], ALU.add, [pk_, 'mgrows'], [ok_])
                        yield
                        tt('dve', o_[:], o_[:], gatebc[:, ci, :], ALU.mult, [ok_, 'gatebc'], [ok_])
                        yield
                        stt(o_[:], hi[:], ALPHA, o_[:], ALU.mult, ALU.add, [hk_, ok_], [ok_])
                        yield
                        S.op('dve', lambda e: e.bn_stats(out=sti[:, 0:6], in_=o_[:, 0:512]), reads=[ok_], writes=[sk_])
                        S.op('dve', lambda e: e.bn_stats(out=sti[:, 6:12], in_=o_[:, 512:1024]), reads=[ok_], writes=[sk_])
                        yield
                        S.op('dve', lambda e: e.bn_aggr(out=sti[:, 12:14], in_=sti[:, 0:12]), reads=[sk_], writes=[sk_])
                        yield
                        act(sti[:, 14:15], sti[:, 13:14], AF.Sqrt, [sk_], [sk_], bias=LN_EPS)
                        yield
                        S.op('dve', lambda e: e.reciprocal(out=sti[:, 14:15], in_=sti[:, 14:15]), reads=[sk_], writes=[sk_])
                        yield
                        stt(sti[:, 15:16], sti[:, 12:13], -1.0, sti[:, 14:15], ALU.mult, ALU.mult, [sk_], [sk_])
                        yield
                        act(o_[:], o_[:], AF.Identity, [ok_, sk_], [ok_], bias=sti[:, 15:16], scale=sti[:, 14:15])
                        yield
                        tt('pool', o_[:, 0:512], o_[:, 0:512], rows[:, 1, 0:512], ALU.mult, [ok_, 'mgrows'], [ok_])
                        tt('dve', o_[:, 512:1024], o_[:, 512:1024], rows[:, 1, 512:1024], ALU.mult, [ok_, 'mgrows'], [ok_])
                        yield
                        tt('pool', o_[:, 0:512], o_[:, 0:512], rows[:, 2, 0:512], ALU.add, [ok_, 'mgrows'], [ok_])
                        tt('dve', o_[:, 512:1024], o_[:, 512:1024], rows[:, 2, 512:1024], ALU.add, [ok_, 'mgrows'], [ok_])
                        yield
                        if last:
                            S.dma('sp', out_d[(t - 2) * 128:(t - 1) * 128, :], o_[:], reads=[ok_], writes=['outfinal'])
                        else:
                            S.dma('sp', h1_d[t * 128:(t + 1) * 128, :], o_[:], reads=[ok_], writes=['h1'])

                    tl = [t for t in range(NTL) if not (last and t < 2)]
                    run_pipelined((mgout(i_, t) for i_, t in enumerate(tl)), STG['mgout'])
                    S.barrier()
                S.barrier()
        PHASES['merge'] = phase_merge
        for l in range(nlayers):
            last = (l == nlayers - 1)
            h_src = dr['hin'] if l == 0 else h1_d
            S.dma('sp', pvt[:], dr['pv'][l], writes=['pvt'])
            with contextlib.ExitStack() as st:
                adw = [sb(st, "adw%d" % i, [128, 8, 512], F32) for i in range(2)]
                scb = sb(st, "scb", [128, 2, 8, 128], F32)
                grow = sb(st, "grow", [128, DM], F32)
                pm0 = ps(st, "pm0", [128, 16, 2], F32)
                pg = [ps(st, "pg%d" % i, [128, 512], F32) for i in range(2)]
                for i in range(2):
                    cp('dve', scb[:, i], silc[:, :, i:i + 1].broadcast_to([128, 8, 128]), ['silc'], ['scb'])
                S.dma('sp', grow[:], dr['rows'][l][:, 3072:4096], writes=['grow'])
                for ch in range(6):
                    buf = adw[ch % 2]
                    bk = 'adw%d' % (ch % 2)
                    S.dma('sp' if ch % 2 == 0 else 'act', buf[:], dr['ada_w'][l][:, :, ch * 512:(ch + 1) * 512], writes=[bk])
                    if ch < 4:
                        for mloc in range(4):
                            m = ch * 4 + mloc
                            for j in range(8):
                                mm(pm0[:, m, :], buf[:, j, mloc * 128:(mloc + 1) * 128], silc[:, j, :], [bk, 'silc'],
                                   ['pm0'], start=(j == 0), stop=(j == 7))
                    else:
                        for i in range(2):
                            for j in range(8):
                                mm(pg[i][:], scb[:, i, j, :], buf[:, j, :], [bk, 'scb'], ['pg%d' % i],
                                   start=(j == 0), stop=(j == 7))
                            tt('dve', gatebc[:, i, (ch - 4) * 512:(ch - 3) * 512], pg[i][:],
                               grow[:, (ch - 4) * 512:(ch - 3) * 512], ALU.add, ['pg%d' % i, 'grow'], ['gatebc'])
                tt('dve', modfm[:], pm0[:], pv('adab').unsqueeze(2).broadcast_to([128, 16, 2]), ALU.add,
                   ['pm0', 'pvt'], ['modfm'])
                ts('dve', modfm[:, 8:16, :], modfm[:, 8:16, :], 1.0, None, ALU.add, None, ['modfm'], ['modfm'])
                dbg_dump('modfm%d' % l, modfm[:], [128, 16, 2], ['modfm'])
                dbg_dump('gatebc%d' % l, gatebc[:], [128, 2, DM], ['gatebc'])
                S.barrier()
            with contextlib.ExitStack() as st:
                xin = [sb(st, "xin%d" % i, [128, DM], F32) for i in range(3)]
                xn = [sb(st, "xn%d" % i, [128, DM], BF16) for i in range(2)]
                stat = [sb(st, "stat%d" % i, [128, 16], F32) for i in range(3)]
                ptr = [ps(st, "ptr%d" % i, [128, 8, 128], BF16) for i in range(2)]
                def p1tile(t):
                    xi, xk = xin[t % 3], 'xin%d' % (t % 3)
                    sti, sk = stat[t % 3], 'stat%d' % (t % 3)
                    xo, xok = xn[t % 2], 'xn%d' % (t % 2)
                    pt, ptk = ptr[t % 2], 'ptr%d' % (t % 2)
                    ci = 1 if t < 2 else 0
                    S.dma('sp' if t % 2 == 0 else 'act', xi[:], h_src[t * 128:(t + 1) * 128, :], writes=[xk])
                    yield
                    S.op('dve', lambda e: e.bn_stats(out=sti[:, 0:6], in_=xi[:, 0:512]), reads=[xk], writes=[sk])
                    S.op('dve', lambda e: e.bn_stats(out=sti[:, 6:12], in_=xi[:, 512:1024]), reads=[xk], writes=[sk])
                    yield
                    S.op('dve', lambda e: e.bn_aggr(out=sti[:, 12:14], in_=sti[:, 0:12]), reads=[sk], writes=[sk])
                    yield
                    act(sti[:, 14:15], sti[:, 13:14], AF.Sqrt, [sk], [sk], bias=LN_EPS)
                    yield
                    S.op('dve', lambda e: e.reciprocal(out=sti[:, 14:15], in_=sti[:, 14:15]), reads=[sk], writes=[sk])
                    yield
                    stt(sti[:, 15:16], sti[:, 12:13], -1.0, sti[:, 14:15], ALU.mult, ALU.mult, [sk], [sk])
                    yield
                    act(xo[:], xi[:], AF.Identity, [xk, sk], [xok], bias=sti[:, 15:16], scale=sti[:, 14:15])
                    yield
                    for j in range(8):
                        tr(pt[:, j, :], xo[:, j * 128:(j + 1) * 128], identb, [xok, 'cstb'], [ptk])
                    yield
                    for j in range(8):
                        if j % 2 == 0:
                            act(uT[:, j, t * 128:(t + 1) * 128], pt[:, j, :], AF.Identity, [ptk, 'modfm'], ['uT%d' % t],
                                bias=modfm[:, j, ci:ci + 1], scale=modfm[:, 8 + j, ci:ci + 1])
                        else:
                            ts('dve', uT[:, j, t * 128:(t + 1) * 128], pt[:, j, :], modfm[:, 8 + j, ci:ci + 1],
                               modfm[:, j, ci:ci + 1], ALU.mult, ALU.add, [ptk, 'modfm'], ['uT%d' % t])

                run_pipelined((p1tile(t) for t in range(NTL)), STG['p1'])
                if ('uT%d' % l) in debug:
                    utf = sb(st, "utf", [128, 8, NT], F32)
                    cp('dve', utf[:], uT[:], ['uT%d' % t for t in range(NTL)], ['utf'])
                    dbg_dump('uT%d' % l, utf[:], [128, 8, NT], ['utf'])
                S.barrier()
            uTk = ['uT%d' % t for t in range(NTL)]

            for ph in list(PHASES):
                if ph in phases:
                    PHASES[ph](l, h_src, last)
            if ('h%d' % l) in debug and not last:
                d_ = dbg_out('h%d' % l, [NT, DM])
                S.dma('sp', d_, h1_d, writes=['dbgout_h%d' % l])
                S.barrier()
            if ('Y%d' % l) in debug:
                with contextlib.ExitStack() as st:
                    yf = sb(st, "yf", [128, 4, 2, NT], F32)
                    cp('dve', yf[:], Y[:], ['Y0', 'Y1', 'Y2', 'Y3'], ['yf'])
                    dbg_dump('Y%d' % l, yf[:], [128, 4, 2, NT], ['yf'])
                    S.barrier()

        S.final_wait('sp', ['outfinal'] + ['dbgout_' + n for n in dbg_d])
    if MEMDBG:
        print('SBUF min remaining by prefix:', minrem)
    return nc, dbg_d


def kernel(**inputs):
    inp = {k: np.asarray(v) for k, v in inputs.items()}
    sh = prep_shared(inp)
    nc, _ = build()
    in_maps = []
    for b in range(8):
        m = dict(sh)
        m.update(prep_core(inp, b))
        in_maps.append(m)
    res = run_bass_kernel_spmd(nc, in_maps, core_ids=list(range(8)))
    return np.stack([np.asarray(res.results[b]['out'], dtype=np.float32) for b in range(8)], 0)
```
